# Optimizing a Trainium2 kernel written in Bass

```python
import numpy as np
import jax
import jax.numpy as jnp
from jax import lax

D_MODEL = 1024
BATCH = 16
SEQ = 2048
DEPTH = 2

HEAD_DIM = 64
GLA_HEADS = 4
GLA_WIDTH = GLA_HEADS * HEAD_DIM
GLA_GATE_RANK = 16
GLA_TAU = 16.0
GLA_CHUNK = 64
RWKV_HEADS = 4
RWKV_WIDTH = RWKV_HEADS * HEAD_DIM
RWKV_DECAY_RANK = 64
RWKV_ICLR_RANK = 64
RWKV_VRES_RANK = 32
RWKV_GATE_RANK = 160
RWKV_GN_EPS = 64e-5
NSA_HEADS = 8
NSA_KV_HEADS = 2
NSA_WIDTH = NSA_HEADS * HEAD_DIM
NSA_KV_WIDTH = NSA_KV_HEADS * HEAD_DIM
NSA_CMP_BLOCK = 32
NSA_CMP_STRIDE = 16
NSA_CMP_HIDDEN = 256
NSA_SEL_BLOCK = 64
NSA_N_SELECT = 16
NSA_WINDOW = 512
NSA_SEL_QCHUNK = 32
NSA_WIN_QBLOCK = 128
ROPE_THETA = 10000.0
FFN_HIDDEN = ((8 * D_MODEL + 3 * 256 - 1) // (3 * 256)) * 256
DEEPNORM_ALPHA = (2 * DEPTH) ** 0.25
DEEPNORM_BETA = (8 * DEPTH) ** -0.25
LN_EPS = 1e-5
MASK_NEG = -1e30

GLA_COLS = (GLA_WIDTH, GLA_WIDTH, GLA_WIDTH, GLA_WIDTH, GLA_GATE_RANK)
RWKV_COLS = (RWKV_WIDTH, RWKV_WIDTH, RWKV_WIDTH,
             RWKV_DECAY_RANK, RWKV_ICLR_RANK, RWKV_GATE_RANK)
NSA_COLS = (NSA_WIDTH,) + (NSA_KV_WIDTH,) * 6 + (3 * NSA_HEADS,)
IN_COLS = sum(GLA_COLS) + sum(RWKV_COLS) + sum(NSA_COLS)

kernel_name = 'hybrid_gla_rwkv7_nsa_deepnorm'


def _split(h, widths):
    idx = np.cumsum(np.asarray(widths))[:-1].tolist()
    return jnp.split(h, idx, axis=-1)


def _layer_norm(x, w, b):
    xf = x.astype(jnp.float32)
    mu = xf.mean(-1, keepdims=True)
    var = jnp.square(xf - mu).mean(-1, keepdims=True)
    return ((xf - mu) * lax.rsqrt(var + LN_EPS) * w + b).astype(x.dtype)


def _head_norm(x, w, b, n_heads, eps):
    shp = x.shape
    xh = x.astype(jnp.float32).reshape(shp[:-1] + (n_heads, shp[-1] // n_heads))
    mu = xh.mean(-1, keepdims=True)
    var = jnp.square(xh - mu).mean(-1, keepdims=True)
    xh = (xh - mu) * lax.rsqrt(var + eps)
    return xh.reshape(shp) * w + b


def _rope(x, pos):
    half = x.shape[-1] // 2
    inv = ROPE_THETA ** (-jnp.arange(half, dtype=jnp.float32) / half)
    ang = pos.astype(jnp.float32)[:, None] * inv
    shape = (ang.shape[0],) + (1,) * (x.ndim - 3) + (half,)
    cos = jnp.cos(ang).reshape(shape).astype(x.dtype)
    sin = jnp.sin(ang).reshape(shape).astype(x.dtype)
    x1, x2 = x[..., :half], x[..., half:]
    return jnp.concatenate([x1 * cos - x2 * sin, x1 * sin + x2 * cos], axis=-1)


def _token_shift_mix(p, mu):
    prev = jnp.pad(p, ((0, 0), (1, 0), (0, 0)))[:, :-1]
    return p + (prev - p) * mu


def _gla_mixer(q, k, v, g, a_lr, w_a2, b_a, ln_w, ln_b):
    dt = q.dtype
    B, S, _ = q.shape
    H, dk, C = GLA_HEADS, HEAD_DIM, GLA_CHUNK
    n = S // C
    f32 = jnp.float32
    log_a = jax.nn.log_sigmoid((a_lr @ w_a2 + b_a).astype(f32)) / GLA_TAU

    def chunks(t):
        return t.astype(f32).reshape(B, n, C, H, dk).transpose(1, 0, 3, 2, 4)

    qc, kc, vc, ac = chunks(q * dk ** -0.5), chunks(k), chunks(v), chunks(log_a)
    causal = jnp.tril(jnp.ones((C, C), bool))[:, :, None]

    def step(state, inp):
        qi, ki, vi, ai = inp
        b = jnp.cumsum(ai, axis=2)
        inter = jnp.einsum('bhtk,bhkv->bhtv', qi * jnp.exp(b), state)
        diff = b[:, :, :, None, :] - b[:, :, None, :, :]
        decay = jnp.exp(jnp.where(causal, diff, -jnp.inf))
        scores = jnp.einsum('bhtk,bhsk,bhtsk->bhts', qi, ki, decay)
        intra = jnp.einsum('bhts,bhsv->bhtv', scores, vi)
        b_end = b[:, :, -1:, :]
        state = (jnp.exp(b_end[:, :, 0, :, None]) * state
                 + jnp.einsum('bhsk,bhsv->bhkv', ki * jnp.exp(b_end - b), vi))
        return state, inter + intra

    s0 = jnp.zeros((B, H, dk, dk), f32)
    _, o = lax.scan(step, s0, (qc, kc, vc, ac))
    o = o.transpose(1, 0, 3, 2, 4).reshape(B, S, H * dk)
    o = _head_norm(o, ln_w, ln_b, H, LN_EPS)
    return (o * jax.nn.silu(g.astype(f32))).astype(dt)


def _rwkv7_mixer(r, k, v, w_lr, a_lr, g_lr, w0, w2, a0, a2, g2, k_k, k_a, r_k, ln_w, ln_b):
    dt = r.dtype
    B, S, _ = r.shape
    H, N = RWKV_HEADS, HEAD_DIM
    f32 = jnp.float32
    r, k, v = r.astype(f32), k.astype(f32), v.astype(f32)
    w_log = -jax.nn.softplus(-(w0 + jnp.tanh(w_lr) @ w2).astype(f32)) - 0.5
    decay = jnp.exp(-jnp.exp(w_log))
    a = jax.nn.sigmoid((a0 + a_lr @ a2).astype(f32))
    g = (jax.nn.sigmoid(g_lr) @ g2).astype(f32)
    kk = (k * k_k).reshape(B, S, H, N)
    kk = kk / jnp.maximum(jnp.sqrt(jnp.sum(kk * kk, axis=-1, keepdims=True)), 1e-12)
    k = k * (1.0 + (a - 1.0) * k_a)

    def heads(t):
        return t.reshape(B, S, H, N).transpose(1, 0, 2, 3)

    a_h = a.reshape(B, S, H, N)
    xs = (heads(r), heads(decay), heads(k), heads(v),
          (-kk).transpose(1, 0, 2, 3), (kk * a_h).transpose(1, 0, 2, 3))

    def step(state, inp):
        r_t, w_t, k_t, v_t, a_t, b_t = inp
        sa = jnp.einsum('bhvk,bhk->bhv', state, a_t)
        state = (state * w_t[:, :, None, :] + sa[..., None] * b_t[:, :, None, :]
                 + v_t[..., None] * k_t[:, :, None, :])
        return state, jnp.einsum('bhvk,bhk->bhv', state, r_t)

    s0 = jnp.zeros((B, H, N, N), f32)
    _, y = lax.scan(step, s0, xs)
    y = y.transpose(1, 0, 2, 3).reshape(B, S, H * N)
    y = _head_norm(y, ln_w, ln_b, H, RWKV_GN_EPS)
    bonus = (jnp.sum((r * k).reshape(B, S, H, N) * r_k, axis=-1, keepdims=True)
             * v.reshape(B, S, H, N)).reshape(B, S, H * N)
    return ((y + bonus) * g).astype(dt)


def _compress(t, pos_emb, w1, w2):
    B, S, G, hd = t.shape
    n_sub = NSA_CMP_BLOCK // NSA_CMP_STRIDE
    ncb = S // NSA_CMP_STRIDE
    nc = ncb - n_sub + 1
    t_s = t.reshape(B, ncb, NSA_CMP_STRIDE, G, hd)
    blocks = jnp.concatenate([t_s[:, j:j + nc] for j in range(n_sub)], axis=2)
    blocks = blocks + pos_emb[None, None, :, None, :]
    flat = blocks.transpose(0, 1, 3, 2, 4).reshape(B, nc, G, NSA_CMP_BLOCK * hd)
    return jax.nn.gelu(flat @ w1) @ w2


def _overlap_matrix(seq):
    nc = seq // NSA_CMP_STRIDE - NSA_CMP_BLOCK // NSA_CMP_STRIDE + 1
    ns = seq // NSA_SEL_BLOCK
    c0 = np.arange(nc) * NSA_CMP_STRIDE
    s0 = np.arange(ns) * NSA_SEL_BLOCK
    lo = np.maximum(c0[:, None], s0[None, :])
    hi = np.minimum(c0[:, None] + NSA_CMP_BLOCK, s0[None, :] + NSA_SEL_BLOCK)
    return (np.maximum(hi - lo, 0) / NSA_CMP_STRIDE).astype(np.float32)


def _selected_attention(q, k, v, idx, pos):
    B, S, G, HPG, hd = q.shape
    n = idx.shape[-1]
    ns = S // NSA_SEL_BLOCK
    kb = k.reshape(B, ns, NSA_SEL_BLOCK, G, hd).transpose(0, 3, 1, 2, 4)
    vb = v.reshape(B, ns, NSA_SEL_BLOCK, G, hd).transpose(0, 3, 1, 2, 4)
    qcs = NSA_SEL_QCHUNK
    nq = S // qcs
    q_ch = q.reshape(B, nq, qcs, G, HPG, hd).transpose(1, 0, 2, 3, 4, 5)
    idx_ch = idx.reshape(B, G, nq, qcs, n).transpose(2, 0, 1, 3, 4)
    pos_ch = pos.reshape(nq, qcs)
    bi = jnp.arange(B)[:, None, None, None]
    gi = jnp.arange(G)[None, :, None, None]
    offs = jnp.arange(NSA_SEL_BLOCK)

    def one(args):
        qc, ic, tc = args
        kg = kb[bi, gi, ic]
        vg = vb[bi, gi, ic]
        s = jnp.einsum('bqghd,bgqnkd->bghqnk', qc, kg).astype(jnp.float32)
        kpos = ic[..., None] * NSA_SEL_BLOCK + offs
        valid = kpos <= tc[None, None, :, None, None]
        s = jnp.where(valid[:, :, None], s, MASK_NEG).reshape(B, G, HPG, qcs, n * NSA_SEL_BLOCK)
        pr = jax.nn.softmax(s, axis=-1).reshape(B, G, HPG, qcs, n, NSA_SEL_BLOCK)
        return jnp.einsum('bghqnk,bgqnkd->bqghd', pr.astype(vg.dtype), vg)

    o = lax.map(one, (q_ch, idx_ch, pos_ch))
    return o.transpose(1, 0, 2, 3, 4, 5).reshape(B, S, G, HPG, hd)


def _window_attention(q, k, v):
    B, S, G, HPG, hd = q.shape
    qb_sz, win = NSA_WIN_QBLOCK, NSA_WINDOW
    nb = S // qb_sz
    kp = jnp.pad(k, ((0, 0), (win, 0), (0, 0), (0, 0)))
    vp = jnp.pad(v, ((0, 0), (win, 0), (0, 0), (0, 0)))
    q_bl = q.reshape(B, nb, qb_sz, G, HPG, hd).transpose(1, 0, 2, 3, 4, 5)
    rel = jnp.arange(qb_sz)[:, None] + win - jnp.arange(qb_sz + win)[None, :]

    def one(args):
        qb, i = args
        start = i * qb_sz
        kb = lax.dynamic_slice_in_dim(kp, start, qb_sz + win, axis=1)
        vb = lax.dynamic_slice_in_dim(vp, start, qb_sz + win, axis=1)
        s = jnp.einsum('bqghd,bkgd->bghqk', qb, kb).astype(jnp.float32)
        kpos = start - win + jnp.arange(qb_sz + win)
        valid = (rel >= 0) & (rel < win) & (kpos[None, :] >= 0)
        pr = jax.nn.softmax(jnp.where(valid, s, MASK_NEG), axis=-1)
        return jnp.einsum('bghqk,bkgd->bqghd', pr.astype(vb.dtype), vb)

    o = lax.map(one, (q_bl, jnp.arange(nb)))
    return o.transpose(1, 0, 2, 3, 4, 5).reshape(B, S, G, HPG, hd)


def _nsa_mixer(h, pos, pos_k, pos_v, wk1, wk2, wv1, wv2):
    dt = h.dtype
    B, S, _ = h.shape
    G, HPG, hd = NSA_KV_HEADS, NSA_HEADS // NSA_KV_HEADS, HEAD_DIM
    q, kc, vc, ks, vs, kw, vw, gate = _split(h, NSA_COLS)
    q = _rope(q.reshape(B, S, G, HPG, hd), pos) * hd ** -0.5
    kc, ks, kw = (_rope(t.reshape(B, S, G, hd), pos) for t in (kc, ks, kw))
    vc, vs, vw = (t.reshape(B, S, G, hd) for t in (vc, vs, vw))

    k_cmp = _compress(kc, pos_k, wk1, wk2)
    v_cmp = _compress(vc, pos_v, wv1, wv2)
    nc = k_cmp.shape[1]
    s_cmp = jnp.einsum('bsghd,bcgd->bghsc', q, k_cmp).astype(jnp.float32)
    block_end = jnp.arange(nc) * NSA_CMP_STRIDE + NSA_CMP_BLOCK - 1
    valid_c = block_end[None, :] <= pos[:, None]
    p_cmp = jax.nn.softmax(jnp.where(valid_c, s_cmp, MASK_NEG), axis=-1) * valid_c
    o_cmp = jnp.einsum('bghsc,bcgd->bsghd', p_cmp.astype(v_cmp.dtype), v_cmp)

    ns = S // NSA_SEL_BLOCK
    overlap = jnp.asarray(_overlap_matrix(S))
    imp = jnp.einsum('bghsc,cj->bgsj', p_cmp, overlap)
    cur = pos // NSA_SEL_BLOCK
    j = jnp.arange(ns)[None, :]
    forced = (j == 0) | (j == cur[:, None]) | (j == cur[:, None] - 1)
    sel_score = jnp.where(forced, 1e9, jnp.where(j > cur[:, None], -1e9, imp))
    _, idx = lax.top_k(sel_score, min(NSA_N_SELECT, ns))
    o_slc = _selected_attention(q, ks, vs, idx, pos)

    o_win = _window_attention(q, kw, vw)

    gt = jax.nn.sigmoid(gate.astype(jnp.float32)).reshape(B, S, G, HPG, 3)
    o = gt[..., 0:1] * o_cmp + gt[..., 1:2] * o_slc + gt[..., 2:3] * o_win
    return o.reshape(B, S, NSA_WIDTH).astype(dt)


def setup_inputs(seed: int = 0) -> dict:
    key = jax.random.key(seed)
    ks = jax.random.split(key, 40)
    D, L = D_MODEL, DEPTH

    def nrm(k, shape, scale):
        return jax.random.normal(k, shape, jnp.float32) * scale

    def gain(k, shape):
        return 1.0 + 0.05 * jax.random.normal(k, shape, jnp.float32)

    rw_in = sum(RWKV_COLS)
    return {
        'x': nrm(ks[0], (BATCH, SEQ, D), 1.0),
        'w_in': nrm(ks[1], (L, D, IN_COLS), D ** -0.5),
        'w_in_vres': nrm(ks[2], (L - 1, D, RWKV_VRES_RANK), D ** -0.5),
        'gla_w_a2': nrm(ks[3], (L, GLA_GATE_RANK, GLA_WIDTH), GLA_GATE_RANK ** -0.5),
        'gla_b_a': nrm(ks[4], (L, GLA_WIDTH), 0.1),
        'gla_ln_w': gain(ks[5], (L, GLA_WIDTH)),
        'gla_ln_b': nrm(ks[6], (L, GLA_WIDTH), 0.02),
        'rwkv_mu': jax.random.uniform(ks[7], (L, rw_in), jnp.float32),
        'rwkv_mu_vres': jax.random.uniform(ks[8], (L - 1, RWKV_VRES_RANK), jnp.float32),
        'rwkv_w0': jax.random.uniform(ks[9], (L, RWKV_WIDTH), jnp.float32, -5.0, 1.0),
        'rwkv_w2': nrm(ks[10], (L, RWKV_DECAY_RANK, RWKV_WIDTH), 0.5 * RWKV_DECAY_RANK ** -0.5),
        'rwkv_a0': nrm(ks[11], (L, RWKV_WIDTH), 0.1),
        'rwkv_a2': nrm(ks[12], (L, RWKV_ICLR_RANK, RWKV_WIDTH), RWKV_ICLR_RANK ** -0.5),
        'rwkv_v0': nrm(ks[13], (L - 1, RWKV_WIDTH), 0.1),
        'rwkv_v2': nrm(ks[14], (L - 1, RWKV_VRES_RANK, RWKV_WIDTH), RWKV_VRES_RANK ** -0.5),
        'rwkv_g2': nrm(ks[15], (L, RWKV_GATE_RANK, RWKV_WIDTH), RWKV_GATE_RANK ** -0.5),
        'rwkv_k_k': 0.85 + 0.05 * jax.random.normal(ks[16], (L, RWKV_WIDTH), jnp.float32),
        'rwkv_k_a': gain(ks[17], (L, RWKV_WIDTH)),
        'rwkv_r_k': nrm(ks[18], (L, RWKV_HEADS, HEAD_DIM), 0.1),
        'rwkv_ln_w': gain(ks[19], (L, RWKV_WIDTH)),
        'rwkv_ln_b': nrm(ks[20], (L, RWKV_WIDTH), 0.02),
        'nsa_pos_k': nrm(ks[21], (L, NSA_CMP_BLOCK, HEAD_DIM), 0.02),
        'nsa_pos_v': nrm(ks[22], (L, NSA_CMP_BLOCK, HEAD_DIM), 0.02),
        'nsa_wk1': nrm(ks[23], (L, NSA_CMP_BLOCK * HEAD_DIM, NSA_CMP_HIDDEN), (NSA_CMP_BLOCK * HEAD_DIM) ** -0.5),
        'nsa_wk2': nrm(ks[24], (L, NSA_CMP_HIDDEN, HEAD_DIM), NSA_CMP_HIDDEN ** -0.5),
        'nsa_wv1': nrm(ks[25], (L, NSA_CMP_BLOCK * HEAD_DIM, NSA_CMP_HIDDEN), (NSA_CMP_BLOCK * HEAD_DIM) ** -0.5),
        'nsa_wv2': nrm(ks[26], (L, NSA_CMP_HIDDEN, HEAD_DIM), NSA_CMP_HIDDEN ** -0.5),
        'w_out': nrm(ks[27], (L, D, D), DEEPNORM_BETA * D ** -0.5),
        'ln1_w': gain(ks[28], (L, D)),
        'ln1_b': nrm(ks[29], (L, D), 0.02),
        'ffn_w_gate': nrm(ks[30], (L, D, FFN_HIDDEN), D ** -0.5),
        'ffn_w_up': nrm(ks[31], (L, D, FFN_HIDDEN), D ** -0.5),
        'ffn_w_down': nrm(ks[32], (L, FFN_HIDDEN, D), DEEPNORM_BETA * FFN_HIDDEN ** -0.5),
        'ln2_w': gain(ks[33], (L, D)),
        'ln2_b': nrm(ks[34], (L, D), 0.02),
    }


def reference(x, w_in, w_in_vres, gla_w_a2, gla_b_a, gla_ln_w, gla_ln_b,
              rwkv_mu, rwkv_mu_vres, rwkv_w0, rwkv_w2, rwkv_a0, rwkv_a2, rwkv_v0, rwkv_v2,
              rwkv_g2, rwkv_k_k, rwkv_k_a, rwkv_r_k, rwkv_ln_w, rwkv_ln_b,
              nsa_pos_k, nsa_pos_v, nsa_wk1, nsa_wk2, nsa_wv1, nsa_wv2,
              w_out, ln1_w, ln1_b, ffn_w_gate, ffn_w_up, ffn_w_down, ln2_w, ln2_b):
    pos = jnp.arange(x.shape[1], dtype=jnp.int32)
    v_first = None
    for l in range(DEPTH):
        w_cols = w_in[l] if l == 0 else jnp.concatenate([w_in[l], w_in_vres[l - 1]], axis=1)
        h = x @ w_cols
        h_gla, h_rwkv, h_nsa, h_vres = _split(
            h, (sum(GLA_COLS), sum(RWKV_COLS), sum(NSA_COLS), h.shape[-1] - IN_COLS))

        gq, gk, gv, gg, galr = _split(h_gla, GLA_COLS)
        o_gla = _gla_mixer(gq, gk, gv, gg, galr, gla_w_a2[l], gla_b_a[l], gla_ln_w[l], gla_ln_b[l])

        rr, rk, rv, rwlr, ralr, rglr = _split(_token_shift_mix(h_rwkv, rwkv_mu[l]), RWKV_COLS)
        if l == 0:
            v_first = rv
        else:
            vlr = _token_shift_mix(h_vres, rwkv_mu_vres[l - 1])
            rv = rv + (v_first - rv) * jax.nn.sigmoid(rwkv_v0[l - 1] + vlr @ rwkv_v2[l - 1])
        o_rwkv = _rwkv7_mixer(rr, rk, rv, rwlr, ralr, rglr, rwkv_w0[l], rwkv_w2[l], rwkv_a0[l],
                              rwkv_a2[l], rwkv_g2[l], rwkv_k_k[l], rwkv_k_a[l], rwkv_r_k[l],
                              rwkv_ln_w[l], rwkv_ln_b[l])

        o_nsa = _nsa_mixer(h_nsa, pos, nsa_pos_k[l], nsa_pos_v[l], nsa_wk1[l], nsa_wk2[l],
                           nsa_wv1[l], nsa_wv2[l])

        mix = jnp.concatenate([o_gla, o_rwkv, o_nsa], axis=-1) @ w_out[l]
        x = _layer_norm(DEEPNORM_ALPHA * x + mix, ln1_w[l], ln1_b[l])

        ffn = (jax.nn.silu(x @ ffn_w_gate[l]) * (x @ ffn_w_up[l])) @ ffn_w_down[l]
        x = _layer_norm(DEEPNORM_ALPHA * x + ffn, ln2_w[l], ln2_b[l])
    return x
```

```python
import math
from contextlib import ExitStack
import numpy as np
import concourse.bass as bass
import concourse.mybir as mybir
from concourse.bass_utils import run_bass_kernel_spmd

F32 = mybir.dt.float32
BF16 = mybir.dt.bfloat16
AF = mybir.ActivationFunctionType
ALU = mybir.AluOpType
AX = mybir.AxisListType

S = 2048
D = 1024
NT = S // 128
L = 2
FF = 2816
NFC = FF // 128
ALPHA = float((2 * L) ** 0.25)
LN_EPS = 1e-5
RW_EPS = 64e-5
NDS = 6


class Reg:
    __slots__ = ("lw", "rd")

    def __init__(self):
        self.lw = None
        self.rd = {}


def regs(n):
    return [Reg() for _ in range(n)]


class KB:
    def __init__(self):
        nc = bass.Bass("TRN2", target_bir_lowering=False)
        self.nc = nc
        self.E = {"pe": nc.tensor, "act": nc.scalar, "dve": nc.vector, "pool": nc.gpsimd, "sp": nc.sync}
        self.sems = {}
        self.cnt = {}
        for e in ("pe", "act", "dve", "pool"):
            self.sems[e] = nc.alloc_semaphore("s_" + e)
            self.cnt[e] = 0
        self.dq = {}
        for q in ("sp", "pool"):
            keys = []
            for i in range(NDS):
                k = "d_%s%d" % (q, i)
                self.sems[k] = nc.alloc_semaphore(k)
                self.cnt[k] = 0
                keys.append(k)
            self.dq[q] = [keys, 0]
        self.seen = {e: {} for e in self.E}
        self.nops = 0

    def _waits(self, e, reads, writes, extra=()):
        need = {}

        def add(rec):
            if rec is None:
                return
            k, c = rec
            if need.get(k, 0) < c:
                need[k] = c

        for r in reads:
            add(r.lw)
        for w in writes:
            add(w.lw)
            for k, c in w.rd.items():
                add((k, c))
        for rec in extra:
            add(rec)
        eng = self.E[e]
        seen = self.seen[e]
        for k, c in need.items():
            if e == "pe" and k == "pe":
                continue
            if seen.get(k, 0) >= c:
                continue
            eng.wait_ge(self.sems[k], c)
            seen[k] = c

    def _mark(self, rec, reads, writes):
        k, c = rec
        for r in reads:
            if r.rd.get(k, 0) < c:
                r.rd[k] = c
        for w in writes:
            w.lw = rec
            w.rd = {}

    def op(self, e, fn, reads=(), writes=()):
        self._waits(e, reads, writes)
        inst = fn(self.E[e])
        self.cnt[e] += 1
        inst.then_inc(self.sems[e], 1)
        self._mark((e, self.cnt[e]), reads, writes)
        self.nops += 1

    def dma(self, q, out, in_, reads=(), writes=(), **kw):
        keys, i = self.dq[q]
        k = keys[i]
        self.dq[q][1] = (i + 1) % len(keys)
        extra = [(k, self.cnt[k])] if self.cnt[k] else []
        self._waits(q, reads, writes, extra)
        self.E[q].dma_start(out=out, in_=in_, **kw).then_inc(self.sems[k], 16)
        self.cnt[k] += 16
        self._mark((k, self.cnt[k]), reads, writes)
        self.nops += 1

    def barrier(self):
        for e in self.E:
            for k, c in self.cnt.items():
                if c == 0 or (e == "pe" and k == "pe"):
                    continue
                if self.seen[e].get(k, 0) >= c:
                    continue
                self.E[e].wait_ge(self.sems[k], c)
                self.seen[e][k] = c

    def finish(self):
        self.barrier()


def bcast_rows(t, off, n, parts=128):
    return bass.AP(t, off, [[0, parts], [1, n]])


def build(cfg):
    kb = KB()
    nc = kb.nc
    NSEQ = cfg.get("nseq", 2)
    NLAY = cfg.get("nlay", 2)
    mixers = cfg.get("mixers", ("gla", "rwkv", "nsa"))
    dumps = cfg.get("dumps", ())
    inject_O = cfg.get("inject_O", False)

    def dram_in(name, shape):
        return nc.dram_tensor(name, list(shape), F32, kind="ExternalInput")

    x_d = dram_in("x", [2, S, D])
    w_in = dram_in("w_in", [L, D, 3400])
    w_vres = dram_in("w_in_vres", [1, D, 32])
    w_nsa = dram_in("w_nsa", [L, D, 2200])
    gla_w_a2 = dram_in("gla_w_a2", [L, 16, 256])
    rwkv_w2 = dram_in("rwkv_w2", [L, 64, 256])
    rwkv_a2 = dram_in("rwkv_a2", [L, 64, 256])
    rwkv_v2 = dram_in("rwkv_v2", [1, 32, 256])
    rwkv_g2 = dram_in("rwkv_g2", [L, 160, 256])
    nsa_wk1 = dram_in("nsa_wk1", [L, 2048, 256])
    nsa_wk2 = dram_in("nsa_wk2", [L, 256, 64])
    nsa_wv1 = dram_in("nsa_wv1", [L, 2048, 256])
    nsa_wv2 = dram_in("nsa_wv2", [L, 256, 64])
    nsa_posT = dram_in("nsa_posT", [L, 2, 64, 32])
    w_out = dram_in("w_out", [L, D, D])
    w_gate = dram_in("ffn_w_gate", [L, D, FF])
    w_up = dram_in("ffn_w_up", [L, D, FF])
    w_down = dram_in("ffn_w_down", [L, FF, D])
    ptab = dram_in("ptab", [L, 128, 32])
    rowtab = dram_in("rowtab", [L, 5120])
    c_ident = dram_in("c_ident", [128, 128])
    c_caus = dram_in("c_caus", [128, 128])
    c_rope = dram_in("c_rope", [2, 128, S])
    c_cmpmask = dram_in("c_cmpmask", [128, S])
    c_onehot = dram_in("c_onehot", [32, S])
    c_selm = dram_in("c_selm", [2, 128, NT, 32])
    c_ovl = dram_in("c_ovl", [128, 32])
    c_blk = dram_in("c_blk", [128, 128])
    c_hsel = dram_in("c_hsel", [128, 2])
    c_rwmask = dram_in("c_rwmask", [64, 5, 64])
    if inject_O:
        dbg_OT = dram_in("dbg_OT", [128, 8, S])
    out_d = nc.dram_tensor("out", [2, S, D], F32, kind="ExternalOutput")
    xres = [nc.dram_tensor("xres%d" % i, [S, D], F32, kind="Internal") for i in range(2)]
    xres_r = [regs(NT) for _ in range(2)]
    vfirst_d = nc.dram_tensor("vfirst", [2, 128, S], F32, kind="Internal")
    vfirst_r = regs(2)
    dump_d = {}
    for name, shape in dumps:
        dump_d[name] = nc.dram_tensor("dump_" + name, list(shape), F32, kind="ExternalOutput")

    XT = nc.alloc_sbuf_tensor("XT", [128, 8, S], BF16)
    XT_r = regs(NT)
    ident = nc.alloc_sbuf_tensor("ident", [128, 128], BF16)
    identf = nc.alloc_sbuf_tensor("identf", [128, 128], F32)
    caus = nc.alloc_sbuf_tensor("caus", [128, 4, 128], F32)
    ptb = nc.alloc_sbuf_tensor("ptb", [128, L, 32], F32)
    ptn = nc.alloc_sbuf_tensor("ptn", [128, L, 32], F32)
    c_r = Reg()
    for l in range(L):
        kb.dma("sp", ptb[:, l, :], ptab.ap()[l], writes=[c_r])
    kb.dma("pool", ident[:], c_ident.ap(), writes=[c_r])
    kb.dma("sp", identf[:], c_ident.ap(), writes=[c_r])
    for i in range(4):
        kb.dma("sp", caus[:, i, :], c_caus.ap(), writes=[c_r])

    PS = [nc.alloc_psum_tensor("ps%d" % i, [128, 512], F32) for i in range(6)]
    PS_r = regs(6)
    PB = [nc.alloc_psum_tensor("pb%d" % i, [128, 1024], BF16) for i in range(2)]
    PB_r = regs(2)

    _uid = [0]

    def sb(name, shape, dt):
        _uid[0] += 1
        return nc.sbuf_tensor("%s_%d" % (name, _uid[0]), list(shape), dt)

    def proj_fm(ps_ap, Wt, c0, M, tok0, N, wreads, treads, pw, kparts=8):
        def fn(pe):
            inst = None
            for k in range(kparts):
                inst = pe.matmul(ps_ap, lhsT=Wt[:, k, c0:c0 + M], rhs=XT[:, k, tok0:tok0 + N],
                                 start=(k == 0), stop=(k == kparts - 1))
            return inst
        kb.op("pe", fn, reads=list(wreads) + list(treads), writes=[pw])

    def proj_tm(ps_ap, Wt, c0, N, tt, wreads, pw):
        def fn(pe):
            inst = None
            for k in range(8):
                inst = pe.matmul(ps_ap, lhsT=XT[:, k, tt * 128:(tt + 1) * 128], rhs=Wt[:, k, c0:c0 + N],
                                 start=(k == 0), stop=(k == 7))
            return inst
        kb.op("pe", fn, reads=list(wreads) + [XT_r[tt]], writes=[pw])

    def load_w(Wt, src3, ncols, wreg, q="pool", chunk=512):
        for k in range(8):
            for c0 in range(0, ncols, chunk):
                c1 = min(ncols, c0 + chunk)
                kb.dma(q, Wt[:, k, c0:c1], src3[k * 128:(k + 1) * 128, c0:c1], writes=[wreg])

    xb = [nc.alloc_sbuf_tensor("xb%d" % i, [128, D], BF16) for i in range(2)]
    xb_r = regs(2)
    xb_i = [0]

    def make_xT(src_ap, src_reg, tt):
        i = xb_i[0]
        xb_i[0] ^= 1
        kb.op("act", lambda a: a.activation(out=xb[i][:], in_=src_ap, func=AF.Copy),
              reads=[src_reg], writes=[xb_r[i]])
        pb = PB[i]

        def fn(pe):
            inst = None
            for k in range(8):
                inst = pe.transpose(out=pb[:, k * 128:(k + 1) * 128], in_=xb[i][:, k * 128:(k + 1) * 128],
                                    identity=ident[:])
            return inst
        kb.op("pe", fn, reads=[xb_r[i], c_r], writes=[PB_r[i]])
        kb.op("dve", lambda v: v.tensor_copy(out=XT[:, :, tt * 128:(tt + 1) * 128],
                                             in_=pb[:, :].rearrange("p (k t) -> p k t", k=8)),
              reads=[PB_r[i]], writes=[XT_r[tt]])

    def layer_norm(xt_ap, xreg, lnw_ap, lnb_ap, lnreg, st, st_r):
        kb.op("dve", lambda v: v.bn_stats(out=st[:, 0:6], in_=xt_ap[:, 0:512]), reads=[xreg], writes=[st_r])
        kb.op("dve", lambda v: v.bn_stats(out=st[:, 6:12], in_=xt_ap[:, 512:1024]), reads=[xreg, st_r], writes=[st_r])
        kb.op("dve", lambda v: v.bn_aggr(out=st[:, 12:14], in_=st[:, 0:12]), reads=[st_r], writes=[st_r])
        kb.op("act", lambda a: a.activation(out=st[:, 14:15], in_=st[:, 13:14], func=AF.Sqrt, bias=LN_EPS, scale=1.0),
              reads=[st_r], writes=[st_r])
        kb.op("dve", lambda v: v.reciprocal(out=st[:, 15:16], in_=st[:, 14:15]), reads=[st_r], writes=[st_r])
        kb.op("dve", lambda v: v.scalar_tensor_tensor(out=st[:, 16:17], in0=st[:, 12:13], scalar=-1.0, in1=st[:, 15:16],
                                                      op0=ALU.mult, op1=ALU.mult), reads=[st_r], writes=[st_r])
        kb.op("act", lambda a: a.activation(out=xt_ap, in_=xt_ap, func=AF.Identity, bias=st[:, 16:17], scale=st[:, 15:16]),
              reads=[st_r, xreg], writes=[xreg])
        kb.op("dve", lambda v: v.tensor_tensor(out=xt_ap, in0=xt_ap, in1=lnw_ap, op=ALU.mult), reads=[xreg, lnreg], writes=[xreg])
        kb.op("dve", lambda v: v.tensor_tensor(out=xt_ap, in0=xt_ap, in1=lnb_ap, op=ALU.add), reads=[xreg, lnreg], writes=[xreg])

    def dump(name, sb_ap, reg, idx=None):
        if name in dump_d:
            dst = dump_d[name].ap() if idx is None else dump_d[name].ap()[idx]
            kb.dma("sp", dst, sb_ap, reads=[reg])

    kb.op("dve", lambda v: v.tensor_scalar(out=ptn[:, :, :], in0=ptb[:, :, :], scalar1=-1.0, scalar2=None, op0=ALU.mult),
          reads=[c_r], writes=[c_r])
    kb.op("dve", lambda v: v.tensor_scalar(out=ptn[:, :, 2:13], in0=ptb[:, :, 2:13], scalar1=-1.0, scalar2=1.0,
                                           op0=ALU.mult, op1=ALU.add), reads=[c_r], writes=[c_r])

    def gla_phase(sq, l, OT, OT_r):
        with ExitStack() as es:
            A = lambda n_, s_, d_: es.enter_context(sb(n_, s_, d_))
            Wg = A("Wg", [128, 8, 1152], BF16)
            wa2 = A("wa2", [16, 256], BF16)
            gln = A("gln", [128, 2, 256], F32)
            alrT = A("alrT", [16, 512], BF16)
            t1 = A("gt1", [128, 512], F32)
            t2 = A("gt2", [128, 512], F32)
            EB = A("EB", [128, 2, 512], F32)
            QTl = A("gQT", [128, 2, 2, 512], BF16)
            KTl = A("gKT", [128, 2, 512], BF16)
            Ktok = A("gKtok", [128, 4, 256], BF16)
            V = A("gV", [128, 4, 256], BF16)
            SG = A("gSG", [128, 4, 256], F32)
            AT = A("gAT", [128, 4, 128], BF16)
            St = A("gSt", [128, 2, 128], F32)
            tmpS = A("gtmpS", [128, 128], F32)
            blkm = A("gblk", [128, 128], F32)
            Sb = A("gSb", [128, 2, 128], BF16)
            rmask = A("grm", [128, 512], F32)
            ow = A("gow", [128, 256], F32)
            ow2 = A("gow2", [128, 256], F32)
            ob = A("gob", [128, 256], BF16)
            st = A("gst", [128, 32], F32)
            W_r, p_r, alr_r, t1_r, t2_r = Reg(), Reg(), Reg(), Reg(), Reg()
            EB_r, QT_r, KT_r = regs(2), regs(2), regs(2)
            Ktok_r, V_r, SG_r, AT_r, St_r, Sb_r = Reg(), regs(4), regs(4), Reg(), Reg(), Reg()
            ow_r, ow2_r, ob_r, st_r = Reg(), Reg(), Reg(), Reg()
            gs = cfg.get("gla_stop", 99)
            load_w(Wg, w_in.ap()[l][:, 0:1040], 1040, W_r, chunk=520)
            kb.dma("pool", wa2[:, :], gla_w_a2.ap()[l], writes=[p_r])
            kb.dma("sp", gln[:, 0, :], bcast_rows(rowtab, l * 5120 + 0, 256), writes=[p_r])
            kb.dma("sp", gln[:, 1, :], bcast_rows(rowtab, l * 5120 + 256, 256), writes=[p_r])
            kb.op("pool", lambda g: g.memset(rmask[:, :], 1.0), writes=[p_r])
            kb.op("pool", lambda g: g.memset(rmask[:, :].rearrange("p (c t) -> p c t", c=4)[:, :, 0:1], 0.0), writes=[p_r])
            kb.op("pool", lambda g: g.memset(St[:, :, :], 0.0), writes=[St_r])
            kb.op("pool", lambda g: g.memset(QTl[:, :, :, :], 0.0), writes=QT_r)
            kb.dma("sp", blkm[:, :], c_blk.ap(), writes=[p_r])
            tmpS_r = Reg()
            kb.op("pool", lambda g: g.memset(Sb[:, :, :], 0.0), writes=[Sb_r])
            for mt in range(4 if gs > 0 else 0):
                tok0 = mt * 512
                xr = XT_r[mt * 4:mt * 4 + 4]
                proj_fm(PS[0][0:16, :], Wg, 1024, 16, tok0, 512, [W_r], xr, PS_r[0])
                kb.op("act", lambda a: a.activation(out=alrT[:, :], in_=PS[0][0:16, :], func=AF.Copy),
                      reads=[PS_r[0]], writes=[alr_r])
                for hp in range(2):
                    kb.op("pe", lambda pe: pe.matmul(PS[1][:, :], lhsT=wa2[0:16, hp * 128:(hp + 1) * 128], rhs=alrT[0:16, :],
                                                     start=True, stop=True), reads=[p_r, alr_r], writes=[PS_r[1]])
                    kb.op("act", lambda a: a.activation(out=t1[:, :], in_=PS[1][:, :], func=AF.Exp, scale=-1.0,
                                                        bias=ptn[:, l, hp:hp + 1]), reads=[PS_r[1], c_r], writes=[t1_r])
                    kb.op("act", lambda a: a.activation(out=t1[:, :], in_=t1[:, :], func=AF.Ln, bias=1.0, scale=1.0),
                          reads=[t1_r], writes=[t1_r])
                    kb.op("dve", lambda v: v.tensor_tensor_scan(out=t2[:, :], data0=rmask[:, :], data1=t1[:, :], initial=0.0,
                                                                op0=ALU.mult, op1=ALU.add), reads=[t1_r, p_r], writes=[t2_r])
                    kb.op("act", lambda a: a.activation(out=EB[:, hp, :], in_=t2[:, :], func=AF.Exp, scale=-1.0 / 16.0),
                          reads=[t2_r], writes=[EB_r[hp]])
                    kb.op("act", lambda a: a.activation(out=t1[:, :], in_=t2[:, :], func=AF.Exp, scale=1.0 / 16.0),
                          reads=[t2_r], writes=[t1_r])
                    proj_fm(PS[2][:, :], Wg, hp * 128, 128, tok0, 512, [W_r], xr, PS_r[2])
                    for hh in range(2):
                        pr = slice(hh * 64, hh * 64 + 64)
                        kb.op("dve", lambda v: v.scalar_tensor_tensor(out=QTl[pr, hp, hh, :], in0=PS[2][pr, :], scalar=0.125,
                                                                      in1=EB[pr, hp, :], op0=ALU.mult, op1=ALU.mult),
                              reads=[PS_r[2], EB_r[hp]], writes=[QT_r[hp]])
                    proj_fm(PS[3][:, :], Wg, 256 + hp * 128, 128, tok0, 512, [W_r], xr, PS_r[3])
                    kb.op("dve", lambda v: v.tensor_tensor(out=KTl[:, hp, :], in0=PS[3][:, :], in1=t1[:, :], op=ALU.mult),
                          reads=[PS_r[3], t1_r], writes=[KT_r[hp]])

                    def fn(pe):
                        inst = None
                        for j in range(4):
                            inst = pe.transpose(out=PB[0][:, j * 128:(j + 1) * 128], in_=KTl[:, hp, j * 128:(j + 1) * 128],
                                                identity=ident[:])
                        return inst
                    kb.op("pe", fn, reads=[KT_r[hp], c_r], writes=[PB_r[0]])
                    kb.op("dve", lambda v: v.tensor_copy(out=Ktok[:, :, hp * 128:(hp + 1) * 128],
                                                         in_=PB[0][:, 0:512].rearrange("p (j c) -> p j c", j=4)),
                          reads=[PB_r[0]], writes=[Ktok_r])
                for j in range(4 if gs > 1 else 0):
                    tt = mt * 4 + j
                    proj_tm(PS[4][:, 0:256], Wg, 512, 256, tt, [W_r], PS_r[4])
                    proj_tm(PS[5][:, 0:256], Wg, 768, 256, tt, [W_r], PS_r[5])
                    if cfg.get("gv", 3) >= 2:
                        kb.op("dve", lambda v: v.tensor_scalar(out=V[:, j, :], in0=PS[4][:, 0:256], scalar1=1.0, scalar2=None, op0=ALU.mult),
                              reads=[PS_r[4]], writes=[V_r[j]])
                    if cfg.get("gv", 3) >= 3:
                        kb.op("act", lambda a: a.activation(out=SG[:, j, :], in_=PS[5][:, 0:256], func=AF.Silu),
                              reads=[PS_r[5]], writes=[SG_r[j]])
                for j in range(4 if gs > 2 else 0):
                    tt = mt * 4 + j
                    cc = slice(j * 128, (j + 1) * 128)

                    def fn(pe):
                        inst = None
                        for h in range(4):
                            hp, hh = h // 2, h % 2
                            inst = pe.matmul(PS[0][:, h * 128:(h + 1) * 128], lhsT=KTl[:, hp, cc], rhs=QTl[:, hp, hh, cc],
                                             start=True, stop=True)
                        return inst
                    kb.op("pe", fn, reads=QT_r + KT_r, writes=[PS_r[0]])
                    kb.op("dve", lambda v: v.tensor_tensor(out=AT[:, :, :], in0=PS[0][:, :].rearrange("p (h t) -> p h t", h=4),
                                                          in1=caus[:, :, :], op=ALU.mult), reads=[PS_r[0], c_r], writes=[AT_r])

                    def fn(pe):
                        inst = None
                        for h in range(4):
                            hp, hh = h // 2, h % 2
                            pe.matmul(PS[1][:, h * 64:(h + 1) * 64], lhsT=AT[:, h, :], rhs=V[:, j, h * 64:(h + 1) * 64],
                                      start=True, stop=False)
                            inst = pe.matmul(PS[1][:, h * 64:(h + 1) * 64], lhsT=QTl[:, hp, hh, cc], rhs=Sb[:, hp, hh * 64:(hh + 1) * 64],
                                             start=False, stop=True)
                        return inst
                    if gs <= 3:
                        continue
                    kb.op("pe", fn, reads=[AT_r, V_r[j], Sb_r] + QT_r, writes=[PS_r[1]])

                    def fn(pe):
                        inst = None
                        for hp in range(2):
                            inst = pe.matmul(PS[2][:, hp * 128:(hp + 1) * 128], lhsT=Ktok[:, j, hp * 128:(hp + 1) * 128],
                                             rhs=V[:, j, hp * 128:(hp + 1) * 128], start=True, stop=True)
                        return inst
                    if gs <= 4:
                        continue
                    kb.op("pe", fn, reads=[Ktok_r, V_r[j]], writes=[PS_r[2]])
                    for hp in range(2):
                        ee = EB[:, hp, j * 128 + 127:j * 128 + 128]
                        kb.op("dve", lambda v: v.scalar_tensor_tensor(out=tmpS[:, :], in0=PS[2][:, hp * 128:(hp + 1) * 128], scalar=ee,
                                                                      in1=blkm[:, :], op0=ALU.mult, op1=ALU.mult),
                              reads=[PS_r[2], EB_r[hp], p_r], writes=[tmpS_r])
                        kb.op("dve", lambda v: v.scalar_tensor_tensor(out=St[:, hp, :], in0=St[:, hp, :], scalar=ee, in1=tmpS[:, :],
                                                                      op0=ALU.mult, op1=ALU.add),
                              reads=[St_r, tmpS_r, EB_r[hp]], writes=[St_r])
                    kb.op("act", lambda a: a.activation(out=Sb[:, :, :], in_=St[:, :, :], func=AF.Copy), reads=[St_r], writes=[Sb_r])
                    if gs <= 5:
                        continue
                    head_norm_gate(PS[1][:, 0:256], PS_r[1], ow, ow_r, ow2, ow2_r, st, st_r, gln, p_r, LN_EPS)
                    kb.op("dve", lambda v: v.tensor_tensor(out=ob[:, :], in0=ow[:, :], in1=SG[:, j, :], op=ALU.mult),
                          reads=[ow_r, SG_r[j]], writes=[ob_r])
                    if gs <= 6:
                        continue
                    out_to_OT(ob, ob_r, 128, OT, OT_r, 0, tt * 128)
            kb.barrier()

    def head_norm_gate(ps_ap, ps_r, ow, ow_r, ow2, ow2_r, st, st_r, gln, gln_r, eps, P=128):
        v4 = lambda ap: ap.rearrange("p (h d) -> p h d", h=4)
        kb.op("act", lambda a: a.activation(out=ow[0:P, :], in_=ps_ap, func=AF.Copy), reads=[ps_r], writes=[ow_r])
        kb.op("act", lambda a: a.activation(out=ow2[0:P, :], in_=ow[0:P, :], func=AF.Square), reads=[ow_r], writes=[ow2_r])
        kb.op("dve", lambda v: v.reduce_sum(out=st[0:P, 0:4], in_=v4(ow[0:P, :]), axis=AX.X), reads=[ow_r], writes=[st_r])
        kb.op("dve", lambda v: v.reduce_sum(out=st[0:P, 4:8], in_=v4(ow2[0:P, :]), axis=AX.X), reads=[ow2_r, st_r], writes=[st_r])
        kb.op("dve", lambda v: v.tensor_scalar(out=st[0:P, 8:16], in0=st[0:P, 0:8], scalar1=1.0 / 64.0, scalar2=None, op0=ALU.mult),
              reads=[st_r], writes=[st_r])
        kb.op("dve", lambda v: v.tensor_tensor(out=st[0:P, 16:20], in0=st[0:P, 8:12], in1=st[0:P, 8:12], op=ALU.mult),
              reads=[st_r], writes=[st_r])
        kb.op("dve", lambda v: v.tensor_tensor(out=st[0:P, 20:24], in0=st[0:P, 12:16], in1=st[0:P, 16:20], op=ALU.subtract),
              reads=[st_r], writes=[st_r])
        kb.op("act", lambda a: a.activation(out=st[0:P, 24:28], in_=st[0:P, 20:24], func=AF.Sqrt, bias=eps, scale=1.0),
              reads=[st_r], writes=[st_r])
        kb.op("dve", lambda v: v.reciprocal(out=st[0:P, 28:32], in_=st[0:P, 24:28]), reads=[st_r], writes=[st_r])
        for h in range(4):
            kb.op("dve", lambda v: v.tensor_scalar(out=ow[0:P, h * 64:(h + 1) * 64], in0=ow[0:P, h * 64:(h + 1) * 64],
                                                   scalar1=st[0:P, 8 + h:9 + h], scalar2=st[0:P, 28 + h:29 + h],
                                                   op0=ALU.subtract, op1=ALU.mult), reads=[ow_r, st_r], writes=[ow_r])
        kb.op("dve", lambda v: v.tensor_tensor(out=ow[0:P, :], in0=ow[0:P, :], in1=gln[0:P, 0, :], op=ALU.mult),
              reads=[ow_r, gln_r], writes=[ow_r])
        kb.op("dve", lambda v: v.tensor_tensor(out=ow[0:P, :], in0=ow[0:P, :], in1=gln[0:P, 1, :], op=ALU.add),
              reads=[ow_r, gln_r], writes=[ow_r])

    def out_to_OT(ob, ob_r, P, OT, OT_r, k0, tokc0, ncols=256):
        nk = ncols // 128

        def fn(pe):
            inst = None
            for kk in range(nk):
                inst = pe.transpose(out=PB[1][:, kk * 128:kk * 128 + P], in_=ob[0:P, kk * 128:(kk + 1) * 128],
                                    identity=ident[0:P, 0:P])
            return inst
        kb.op("pe", fn, reads=[ob_r, c_r], writes=[PB_r[1]])
        tr = OT_r[tokc0 // 128]
        kb.op("act", lambda a: a.activation(out=OT[:, k0:k0 + nk, tokc0:tokc0 + P],
                                            in_=PB[1][:, 0:nk * 128].rearrange("p (k t) -> p k t", k=nk)[:, :, 0:P], func=AF.Copy),
              reads=[PB_r[1]], writes=[tr])

    RWT = [(0, 128), (128, 128), (256, 128), (384, 128), (512, 128), (640, 128), (768, 64), (832, 64),
           (896, 128), (1024, 32), (1056, 32)]
    C0 = float(math.exp(-0.5))

    def rwkv_phase(sq, l, OT, OT_r):
        MT = 256
        NCH = MT // 64
        NU = NCH * 4
        with ExitStack() as es:
            A = lambda n_, s_, d_: es.enter_context(sb(n_, s_, d_))
            Wr = A("Wr", [128, 8, 1088], BF16)
            w2 = A("rw2", [64, 256], BF16)
            a2 = A("ra2", [64, 256], BF16)
            v2 = A("rv2", [32, 256], BF16)
            g2a = A("rg2a", [128, 256], BF16)
            g2b = A("rg2b", [128, 256], BF16)
            rln = A("rln", [128, 2, 256], F32)
            blkm = A("rblk", [128, 128], F32)
            blkf = A("rblkf", [128, 128], F32)
            hselb = A("rhsel", [128, 2], BF16)
            rmask = A("rrm", [128, MT], F32)
            amask = A("ramask", [64, 5, 64], F32)
            xs = [A("rxs%d" % i, [128, MT], F32) for i in range(6)]
            tt_ = [A("rt%d" % i, [128, MT], F32) for i in range(8)]
            Gam = A("rGam", [128, 2, MT], F32)
            ATz = A("rATz", [128, 2, 2, MT], BF16)
            RTz = A("rRTz", [128, 2, 2, MT], BF16)
            BTt = A("rBT", [128, 2, MT], BF16)
            KTt = A("rKT", [128, 2, MT], BF16)
            rkr = A("rrkr", [128, 2, MT], BF16)
            TW = A("rTW", [64, MT], BF16)
            AL = A("rAL", [64, MT], BF16)
            SGL = A("rSGL", [128, MT], BF16)
            SGL2 = A("rSGL2", [128, MT], BF16)
            VR = A("rVR", [32, MT], BF16)
            vb = A("rvb", [128, MT], BF16)
            Vtok = A("rVtok", [128, NCH, 256], BF16)
            Btok = A("rBtok", [128, NCH, 256], BF16)
            Ktok = A("rKtok", [128, NCH, 256], BF16)
            gtok = A("rgtok", [64, NCH, 256], F32)
            cb = A("rcb", [64, NCH, 4], F32)
            MN = [A("rMN%d" % i, [64, NU, 2, 64], F32) for i in range(2)]
            Pm = A("rP", [64, NU, 64], F32)
            A3 = A("rA3", [128, NU, 3, 64], BF16)
            Rs = A("rRs", [64, 256], F32)
            Ub = A("rUb", [128, 256], BF16)
            Tblk = A("rTblk", [128, 2, 128], F32)
            Tb = A("rTb", [128, 2, 128], BF16)
            tmpS = A("rtmpS", [128, 128], F32)
            ow = A("row", [128, 256], F32)
            ow2 = A("row2", [128, 256], F32)
            ob = A("rob", [128, 256], BF16)
            st = A("rst", [128, 32], F32)
            W_r, p_r = Reg(), Reg()
            xs_r, t_r = regs(6), regs(8)
            Gam_r, ATz_r, RTz_r, BT_r, KT_r, rkr_r = regs(2), regs(2), regs(2), regs(2), regs(2), regs(2)
            TW_r, AL_r, SGL_r, SGL2_r, VR_r, vb_r = Reg(), Reg(), Reg(), Reg(), Reg(), Reg()
            Vtok_r, Btok_r, Ktok_r, gtok_r, cb_r = Reg(), Reg(), Reg(), Reg(), Reg()
            MN_r, P_r, A3_r, Rs_r, Ub_r, T_r, Tb_r, tmpS_r = regs(2), Reg(), Reg(), Reg(), Reg(), Reg(), Reg(), Reg()
            ow_r, ow2_r, ob_r, st_r = Reg(), Reg(), Reg(), Reg()
            rs_ = cfg.get("rw_stop", 99)
            load_w(Wr, w_in.ap()[l][:, 1040:2096], 1056, W_r, chunk=528)
            if l >= 1:
                for k in range(8):
                    kb.dma("pool", Wr[:, k, 1056:1088], w_vres.ap()[l - 1][k * 128:(k + 1) * 128, :], writes=[W_r])
            kb.dma("pool", w2[:, :], rwkv_w2.ap()[l], writes=[p_r])
            kb.dma("pool", a2[:, :], rwkv_a2.ap()[l], writes=[p_r])
            if l >= 1:
                kb.dma("pool", v2[:, :], rwkv_v2.ap()[l - 1], writes=[p_r])
            kb.op("pool", lambda g: g.memset(g2b[:, :], 0.0), writes=[p_r])
            kb.dma("pool", g2a[:, :], rwkv_g2.ap()[l][0:128, :], writes=[p_r])
            kb.dma("pool", g2b[0:32, :], rwkv_g2.ap()[l][128:160, :], reads=[p_r], writes=[p_r])
            kb.dma("sp", rln[:, 0, :], bcast_rows(rowtab, l * 5120 + 512, 256), writes=[p_r])
            kb.dma("sp", rln[:, 1, :], bcast_rows(rowtab, l * 5120 + 768, 256), writes=[p_r])
            kb.dma("sp", blkm[:, :], c_blk.ap(), writes=[p_r])
            kb.dma("sp", blkf[:, :], c_blk.ap(), writes=[p_r])
            kb.dma("pool", hselb[:, :], c_hsel.ap(), writes=[p_r])
            kb.dma("sp", amask[:, :, :], c_rwmask.ap(), writes=[p_r])
            kb.op("pool", lambda g: g.memset(rmask[:, :], 1.0), writes=[p_r])
            kb.op("pool", lambda g: g.memset(rmask[:, :].rearrange("p (c t) -> p c t", c=NCH)[:, :, 0:1], 0.0), writes=[p_r])
            for tz, rr in ((Tblk, [T_r]), (Tb, [Tb_r]), (ATz, ATz_r), (RTz, RTz_r), (SGL2, [SGL2_r]), (Vtok, [Vtok_r]),
                           (Btok, [Btok_r]), (Ktok, [Ktok_r]), (A3, [A3_r]), (Ub, [Ub_r])):
                kb.op("pool", lambda g, tz=tz: g.memset(tz[:], 0.0), writes=rr)

            def shift_proj(i, tok0, dst_fn):
                c0, n = RWT[i]
                mucol = 2 + i
                proj_fm(PS[0][0:n, 0:MT], Wr, c0, n, tok0, MT, [W_r], XT_r[tok0 // 128:tok0 // 128 + MT // 128], PS_r[0])
                tp = tt_[7]
                if tok0 == 0:
                    proj_fm(PS[1][0:n, 1:MT], Wr, c0, n, 0, MT - 1, [W_r], XT_r[0:MT // 128], PS_r[1])
                    kb.op("pool", lambda g: g.memset(tp[0:n, 0:1], 0.0), writes=[t_r[7]])
                    kb.op("act", lambda a: a.activation(out=tp[0:n, 1:MT], in_=PS[1][0:n, 1:MT], func=AF.Copy,
                                                        scale=ptb[0:n, l, mucol:mucol + 1]), reads=[PS_r[1], c_r, t_r[7]], writes=[t_r[7]])
                else:
                    proj_fm(PS[1][0:n, 0:MT], Wr, c0, n, tok0 - 1, MT, [W_r],
                            XT_r[(tok0 - 1) // 128:(tok0 - 1) // 128 + MT // 128 + 1], PS_r[1])
                    kb.op("act", lambda a: a.activation(out=tp[0:n, 0:MT], in_=PS[1][0:n, 0:MT], func=AF.Copy,
                                                        scale=ptb[0:n, l, mucol:mucol + 1]), reads=[PS_r[1], c_r], writes=[t_r[7]])
                dst_ap, dst_regs = dst_fn()
                kb.op("dve", lambda v: v.scalar_tensor_tensor(out=dst_ap, in0=PS[0][0:n, 0:MT], scalar=ptn[0:n, l, mucol:mucol + 1],
                                                              in1=tp[0:n, 0:MT], op0=ALU.mult, op1=ALU.add),
                      reads=[PS_r[0], t_r[7], c_r], writes=dst_regs)

            for mt in range(S // MT if rs_ > 0 else 0):
                tok0 = mt * MT
                for i in range(6):
                    shift_proj(i, tok0, lambda i=i: (xs[i][:, :], [xs_r[i]]))
                shift_proj(6, tok0, lambda: (tt_[0][0:64, :], [t_r[0]]))
                kb.op("act", lambda a: a.activation(out=TW[:, :], in_=tt_[0][0:64, :], func=AF.Tanh), reads=[t_r[0]], writes=[TW_r])
                shift_proj(7, tok0, lambda: (AL[:, :], [AL_r]))
                shift_proj(8, tok0, lambda: (tt_[0][:, :], [t_r[0]]))
                kb.op("act", lambda a: a.activation(out=SGL[:, :], in_=tt_[0][:, :], func=AF.Sigmoid), reads=[t_r[0]], writes=[SGL_r])
                shift_proj(9, tok0, lambda: (tt_[0][0:32, :], [t_r[0]]))
                kb.op("act", lambda a: a.activation(out=SGL2[0:32, :], in_=tt_[0][0:32, :], func=AF.Sigmoid), reads=[t_r[0]], writes=[SGL2_r])
                if l >= 1:
                    shift_proj(10, tok0, lambda: (VR[:, :], [VR_r]))
                if rs_ <= 1:
                    continue
                for hp in range(2):
                    rT, kT, vT = xs[hp], xs[2 + hp], xs[4 + hp]
                    rT_r, kT_r, vT_r = xs_r[hp], xs_r[2 + hp], xs_r[4 + hp]
                    t1, t2, t3, t4, t5, t6, t7 = tt_[0:7]
                    kb.op("pe", lambda pe: pe.matmul(PS[2][:, 0:MT], lhsT=w2[:, hp * 128:(hp + 1) * 128], rhs=TW[:, :], start=True, stop=True),
                          reads=[p_r, TW_r], writes=[PS_r[2]])
                    kb.op("act", lambda a: a.activation(out=t1[:, :], in_=PS[2][:, 0:MT], func=AF.Sigmoid, bias=ptb[:, l, 13 + hp:14 + hp]),
                          reads=[PS_r[2], c_r], writes=[t_r[0]])
                    kb.op("dve", lambda v: v.tensor_tensor_scan(out=t2[:, :], data0=rmask[:, :], data1=t1[:, :], initial=0.0,
                                                                op0=ALU.mult, op1=ALU.add), reads=[t_r[0], p_r], writes=[t_r[1]])
                    kb.op("act", lambda a: a.activation(out=Gam[:, hp, :], in_=t2[:, :], func=AF.Exp, scale=-C0), reads=[t_r[1]], writes=[Gam_r[hp]])
                    kb.op("act", lambda a: a.activation(out=t3[:, :], in_=t2[:, :], func=AF.Exp, scale=C0), reads=[t_r[1]], writes=[t_r[2]])
                    kb.op("dve", lambda v: v.tensor_tensor(out=t4[:, :], in0=t2[:, :], in1=t1[:, :], op=ALU.subtract),
                          reads=[t_r[0], t_r[1]], writes=[t_r[3]])
                    kb.op("act", lambda a: a.activation(out=t4[:, :], in_=t4[:, :], func=AF.Exp, scale=-C0), reads=[t_r[3]], writes=[t_r[3]])
                    kb.op("pe", lambda pe: pe.matmul(PS[3][:, 0:MT], lhsT=a2[:, hp * 128:(hp + 1) * 128], rhs=AL[:, :], start=True, stop=True),
                          reads=[p_r, AL_r], writes=[PS_r[3]])
                    kb.op("act", lambda a: a.activation(out=t5[:, :], in_=PS[3][:, 0:MT], func=AF.Sigmoid, bias=ptb[:, l, 15 + hp:16 + hp]),
                          reads=[PS_r[3], c_r], writes=[t_r[4]])
                    kb.op("dve", lambda v: v.tensor_scalar(out=t6[:, :], in0=kT[:, :], scalar1=ptb[:, l, 19 + hp:20 + hp], scalar2=None, op0=ALU.mult),
                          reads=[kT_r, c_r], writes=[t_r[5]])
                    kb.op("act", lambda a: a.activation(out=t7[:, :], in_=t6[:, :], func=AF.Square), reads=[t_r[5]], writes=[t_r[6]])
                    kb.op("pe", lambda pe: pe.matmul(PS[4][:, 0:MT], lhsT=blkf[:, :], rhs=t7[:, :], start=True, stop=True),
                          reads=[p_r, t_r[6]], writes=[PS_r[4]])
                    kb.op("act", lambda a: a.activation(out=t7[:, :], in_=PS[4][:, 0:MT], func=AF.Sqrt), reads=[PS_r[4], t_r[6]], writes=[t_r[6]])
                    kb.op("dve", lambda v: v.tensor_scalar(out=t7[:, :], in0=t7[:, :], scalar1=1e-12, scalar2=None, op0=ALU.max),
                          reads=[t_r[6]], writes=[t_r[6]])
                    kb.op("dve", lambda v: v.reciprocal(out=t7[:, :], in_=t7[:, :]), reads=[t_r[6]], writes=[t_r[6]])
                    kb.op("dve", lambda v: v.tensor_tensor(out=t6[:, :], in0=t6[:, :], in1=t7[:, :], op=ALU.mult),
                          reads=[t_r[5], t_r[6]], writes=[t_r[5]])
                    kb.op("dve", lambda v: v.tensor_scalar(out=t7[:, :], in0=t5[:, :], scalar1=-1.0, scalar2=ptb[:, l, 21 + hp:22 + hp],
                                                           op0=ALU.add, op1=ALU.mult), reads=[t_r[4], t_r[6], c_r], writes=[t_r[6]])
                    kb.op("dve", lambda v: v.scalar_tensor_tensor(out=t7[:, :], in0=t7[:, :], scalar=1.0, in1=kT[:, :], op0=ALU.add, op1=ALU.mult),
                          reads=[t_r[6], kT_r], writes=[t_r[6]])
                    for hh in range(2):
                        pr = slice(hh * 64, hh * 64 + 64)
                        kb.op("dve", lambda v: v.scalar_tensor_tensor(out=ATz[pr, hp, hh, :], in0=t6[pr, :], scalar=-1.0, in1=t4[pr, :],
                                                                      op0=ALU.mult, op1=ALU.mult), reads=[t_r[5], t_r[3]], writes=[ATz_r[hp]])
                        kb.op("dve", lambda v: v.tensor_tensor(out=RTz[pr, hp, hh, :], in0=rT[pr, :], in1=Gam[pr, hp, :], op=ALU.mult),
                              reads=[rT_r, Gam_r[hp]], writes=[RTz_r[hp]])
                    kb.op("dve", lambda v: v.tensor_tensor(out=t1[:, :], in0=t6[:, :], in1=t5[:, :], op=ALU.mult),
                          reads=[t_r[5], t_r[4], t_r[0]], writes=[t_r[0]])
                    kb.op("dve", lambda v: v.tensor_tensor(out=BTt[:, hp, :], in0=t1[:, :], in1=t3[:, :], op=ALU.mult),
                          reads=[t_r[0], t_r[2]], writes=[BT_r[hp]])
                    kb.op("dve", lambda v: v.tensor_tensor(out=KTt[:, hp, :], in0=t7[:, :], in1=t3[:, :], op=ALU.mult),
                          reads=[t_r[6], t_r[2]], writes=[KT_r[hp]])
                    kb.op("dve", lambda v: v.scalar_tensor_tensor(out=rkr[:, hp, :], in0=rT[:, :], scalar=ptb[:, l, 23 + hp:24 + hp], in1=t7[:, :],
                                                                  op0=ALU.mult, op1=ALU.mult), reads=[rT_r, t_r[6], c_r], writes=[rkr_r[hp]])
                    if l == 0:
                        kb.dma("sp", vfirst_d.ap()[hp, :, sq * 0 + tok0:tok0 + MT], vT[:, :], reads=[vT_r], writes=[vfirst_r[hp]])
                    else:
                        kb.op("pe", lambda pe: pe.matmul(PS[5][:, 0:MT], lhsT=v2[:, hp * 128:(hp + 1) * 128], rhs=VR[:, :], start=True, stop=True),
                              reads=[p_r, VR_r], writes=[PS_r[5]])
                        kb.op("act", lambda a: a.activation(out=t1[:, :], in_=PS[5][:, 0:MT], func=AF.Sigmoid, bias=ptb[:, l, 17 + hp:18 + hp]),
                              reads=[PS_r[5], c_r, t_r[0]], writes=[t_r[0]])
                        kb.dma("sp", t2[:, :], vfirst_d.ap()[hp, :, tok0:tok0 + MT], reads=[vfirst_r[hp], t_r[1]], writes=[t_r[1]])
                        kb.op("dve", lambda v: v.tensor_tensor(out=t2[:, :], in0=t2[:, :], in1=vT[:, :], op=ALU.subtract),
                              reads=[t_r[1], vT_r], writes=[t_r[1]])
                        kb.op("dve", lambda v: v.tensor_tensor(out=t2[:, :], in0=t2[:, :], in1=t1[:, :], op=ALU.mult),
                              reads=[t_r[1], t_r[0]], writes=[t_r[1]])
                        kb.op("dve", lambda v: v.tensor_tensor(out=vT[:, :], in0=vT[:, :], in1=t2[:, :], op=ALU.add),
                              reads=[t_r[1], vT_r], writes=[vT_r])
                    kb.op("act", lambda a: a.activation(out=vb[:, :], in_=vT[:, :], func=AF.Copy), reads=[vT_r], writes=[vb_r])
                    for src, src_r, dst, dst_r, pbi in ((vb, vb_r, Vtok, Vtok_r, 0), (None, BT_r[hp], Btok, Btok_r, 1), (None, KT_r[hp], Ktok, Ktok_r, 0)):
                        sap = (lambda c: vb[:, c * 64:(c + 1) * 64]) if src is vb else \
                              ((lambda c: BTt[:, hp, c * 64:(c + 1) * 64]) if dst is Btok else (lambda c: KTt[:, hp, c * 64:(c + 1) * 64]))

                        def fn(pe, sap=sap, pbi=pbi):
                            inst = None
                            for c in range(NCH):
                                inst = pe.transpose(out=PB[pbi][0:64, c * 128:(c + 1) * 128], in_=sap(c), identity=ident[:, :])
                            return inst
                        kb.op("pe", fn, reads=[src_r, c_r], writes=[PB_r[pbi]])
                        kb.op("dve", lambda v: v.tensor_copy(out=dst[0:64, :, hp * 128:(hp + 1) * 128],
                                                             in_=PB[pbi][0:64, 0:NCH * 128].rearrange("p (c d) -> p c d", c=NCH)),
                              reads=[PB_r[pbi]], writes=[dst_r])
                if rs_ <= 2:
                    continue
                def fn(pe):
                    inst = None
                    for c in range(NCH):
                        for hp in range(2):
                            inst = pe.matmul(PS[2][0:64, c * 4 + hp * 2:c * 4 + hp * 2 + 2], lhsT=rkr[:, hp, c * 64:(c + 1) * 64], rhs=hselb[:, :],
                                             start=True, stop=True)
                    return inst
                kb.op("pe", fn, reads=rkr_r + [p_r], writes=[PS_r[2]])
                kb.op("act", lambda a: a.activation(out=cb[:, :, :], in_=PS[2][0:64, 0:NCH * 4].rearrange("p (c h) -> p c h", c=NCH), func=AF.Copy),
                      reads=[PS_r[2]], writes=[cb_r])
                for c2 in range(NCH // 2):
                    def fn(pe):
                        inst = None
                        for cc_ in range(2):
                            c = c2 * 2 + cc_
                            pe.matmul(PS[3][0:64, cc_ * 256:(cc_ + 1) * 256], lhsT=SGL[:, c * 64:(c + 1) * 64], rhs=g2a[:, :], start=True, stop=False)
                            inst = pe.matmul(PS[3][0:64, cc_ * 256:(cc_ + 1) * 256], lhsT=SGL2[:, c * 64:(c + 1) * 64], rhs=g2b[:, :], start=False, stop=True)
                        return inst
                    kb.op("pe", fn, reads=[SGL_r, SGL2_r, p_r], writes=[PS_r[3]])
                    kb.op("act", lambda a: a.activation(out=gtok[:, c2 * 2:c2 * 2 + 2, :], in_=PS[3][0:64, :].rearrange("p (c d) -> p c d", c=2), func=AF.Copy),
                          reads=[PS_r[3]], writes=[gtok_r])
                for c in range(NCH):
                    cc = slice(c * 64, (c + 1) * 64)
                    for h in range(4):
                        hp, hh = h // 2, h % 2
                        u = c * 4 + h
                        pi = 4 + (u % 2)

                        def fn(pe, pi=pi):
                            pe.matmul(PS[pi][0:64, 0:64], lhsT=ATz[:, hp, hh, cc], rhs=BTt[:, hp, cc], start=True, stop=True)
                            pe.matmul(PS[pi][0:64, 64:128], lhsT=BTt[:, hp, cc], rhs=ATz[:, hp, hh, cc], start=True, stop=True)
                            pe.matmul(PS[pi][0:64, 128:192], lhsT=KTt[:, hp, cc], rhs=ATz[:, hp, hh, cc], start=True, stop=True)
                            pe.matmul(PS[pi][0:64, 192:256], lhsT=BTt[:, hp, cc], rhs=RTz[:, hp, hh, cc], start=True, stop=True)
                            return pe.matmul(PS[pi][0:64, 256:320], lhsT=KTt[:, hp, cc], rhs=RTz[:, hp, hh, cc], start=True, stop=True)
                        kb.op("pe", fn, reads=[ATz_r[hp], RTz_r[hp], BT_r[hp], KT_r[hp]], writes=[PS_r[pi]])
                        kb.op("dve", lambda v, pi=pi: v.tensor_tensor(out=MN[0][:, u, :, :], in0=PS[pi][0:64, 0:128].rearrange("p (a b) -> p a b", a=2),
                                                                      in1=amask[:, 0:2, :], op=ALU.mult), reads=[PS_r[pi], p_r], writes=[MN_r[0]])
                        kb.op("dve", lambda v, pi=pi: v.tensor_tensor(out=A3[0:64, u, :, :], in0=PS[pi][0:64, 128:320].rearrange("p (a b) -> p a b", a=3),
                                                                      in1=amask[:, 2:5, :], op=ALU.mult), reads=[PS_r[pi], p_r], writes=[A3_r])
                if rs_ <= 3:
                    continue
                kb.op("dve", lambda v: v.tensor_tensor(out=Pm[:, :, :], in0=MN[0][:, :, 1, :],
                                                      in1=bass.AP(identf, 0, [[128, 64], [0, NU], [1, 64]]), op=ALU.add),
                      reads=[MN_r[0], c_r], writes=[P_r])
                cur = 0
                for lev in range(5):
                    nxt = 1 - cur
                    lastlev = (lev == 4)
                    for g4 in range(NU // 4):
                        pi = 4 + (g4 % 2)

                        def fn(pe, pi=pi):
                            inst = None
                            for q in range(4):
                                u = g4 * 4 + q
                                inst = pe.matmul(PS[pi][0:64, q * 128:q * 128 + 64], lhsT=MN[cur][:, u, 1, :], rhs=MN[cur][:, u, 0, :], start=True, stop=True)
                                if not lastlev:
                                    inst = pe.matmul(PS[pi][0:64, q * 128 + 64:q * 128 + 128], lhsT=MN[cur][:, u, 0, :], rhs=MN[cur][:, u, 1, :],
                                                     start=True, stop=True)
                            return inst
                        kb.op("pe", fn, reads=[MN_r[cur]], writes=[PS_r[pi]])
                        kb.op("act", lambda a, pi=pi: a.activation(out=MN[nxt][:, g4 * 4:g4 * 4 + 4, :, :],
                                                                   in_=PS[pi][0:64, :].rearrange("p (u a b) -> p u a b", u=4, a=2), func=AF.Copy),
                              reads=[PS_r[pi]], writes=[MN_r[nxt]])
                    for g8 in range(NU // 8):
                        pi = 2 + (g8 % 2)

                        def fn(pe, pi=pi):
                            inst = None
                            for q in range(8):
                                u = g8 * 8 + q
                                inst = pe.matmul(PS[pi][0:64, q * 64:(q + 1) * 64], lhsT=MN[nxt][:, u, 0, :], rhs=Pm[:, u, :], start=True, stop=True)
                            return inst
                        kb.op("pe", fn, reads=[MN_r[nxt], P_r], writes=[PS_r[pi]])
                        kb.op("dve", lambda v, pi=pi: v.tensor_tensor(out=Pm[:, g8 * 8:g8 * 8 + 8, :], in0=PS[pi][0:64, :].rearrange("p (u b) -> p u b", u=8),
                                                                      in1=Pm[:, g8 * 8:g8 * 8 + 8, :], op=ALU.add), reads=[PS_r[pi], P_r], writes=[P_r])
                    cur = nxt
                if rs_ <= 4:
                    continue
                for c in range(NCH):
                    cc = slice(c * 64, (c + 1) * 64)

                    def fn(pe):
                        inst = None
                        for h in range(4):
                            hp, hh = h // 2, h % 2
                            u = c * 4 + h
                            pe.matmul(PS[0][0:64, h * 64:(h + 1) * 64], lhsT=A3[:, u, 0, :], rhs=Vtok[:, c, h * 64:(h + 1) * 64], start=True, stop=False)
                            inst = pe.matmul(PS[0][0:64, h * 64:(h + 1) * 64], lhsT=ATz[:, hp, hh, cc], rhs=Tb[:, hp, hh * 64:(hh + 1) * 64],
                                             start=False, stop=True)
                        return inst
                    kb.op("pe", fn, reads=[A3_r, Vtok_r, Tb_r] + ATz_r, writes=[PS_r[0]])
                    kb.op("act", lambda a: a.activation(out=Rs[:, :], in_=PS[0][0:64, 0:256], func=AF.Copy), reads=[PS_r[0]], writes=[Rs_r])

                    def fn(pe):
                        inst = None
                        for h in range(4):
                            u = c * 4 + h
                            inst = pe.matmul(PS[1][0:64, h * 64:(h + 1) * 64], lhsT=Pm[:, u, :], rhs=Rs[:, h * 64:(h + 1) * 64], start=True, stop=True)
                        return inst
                    kb.op("pe", fn, reads=[P_r, Rs_r], writes=[PS_r[1]])
                    kb.op("act", lambda a: a.activation(out=Ub[0:64, :], in_=PS[1][0:64, 0:256], func=AF.Copy), reads=[PS_r[1]], writes=[Ub_r])

                    def fn(pe):
                        inst = None
                        for h in range(4):
                            hp, hh = h // 2, h % 2
                            u = c * 4 + h
                            pe.matmul(PS[0][0:64, h * 64:(h + 1) * 64], lhsT=RTz[:, hp, hh, cc], rhs=Tb[:, hp, hh * 64:(hh + 1) * 64], start=True, stop=False)
                            pe.matmul(PS[0][0:64, h * 64:(h + 1) * 64], lhsT=A3[:, u, 1, :], rhs=Ub[:, h * 64:(h + 1) * 64], start=False, stop=False)
                            inst = pe.matmul(PS[0][0:64, h * 64:(h + 1) * 64], lhsT=A3[:, u, 2, :], rhs=Vtok[:, c, h * 64:(h + 1) * 64], start=False, stop=True)
                        return inst
                    kb.op("pe", fn, reads=[A3_r, Vtok_r, Tb_r, Ub_r] + RTz_r, writes=[PS_r[0]])

                    def fn(pe):
                        inst = None
                        for hp in range(2):
                            pe.matmul(PS[1][:, hp * 128:(hp + 1) * 128], lhsT=Btok[:, c, hp * 128:(hp + 1) * 128], rhs=Ub[:, hp * 128:(hp + 1) * 128],
                                      start=True, stop=False)
                            inst = pe.matmul(PS[1][:, hp * 128:(hp + 1) * 128], lhsT=Ktok[:, c, hp * 128:(hp + 1) * 128], rhs=Vtok[:, c, hp * 128:(hp + 1) * 128],
                                             start=False, stop=True)
                        return inst
                    kb.op("pe", fn, reads=[Btok_r, Ktok_r, Vtok_r, Ub_r], writes=[PS_r[1]])
                    for hp in range(2):
                        ee = Gam[:, hp, c * 64 + 63:c * 64 + 64]
                        kb.op("dve", lambda v: v.scalar_tensor_tensor(out=tmpS[:, :], in0=PS[1][:, hp * 128:(hp + 1) * 128], scalar=ee, in1=blkm[:, :],
                                                                      op0=ALU.mult, op1=ALU.mult), reads=[PS_r[1], Gam_r[hp], p_r], writes=[tmpS_r])
                        kb.op("dve", lambda v: v.scalar_tensor_tensor(out=Tblk[:, hp, :], in0=Tblk[:, hp, :], scalar=ee, in1=tmpS[:, :],
                                                                      op0=ALU.mult, op1=ALU.add), reads=[T_r, tmpS_r, Gam_r[hp]], writes=[T_r])
                    kb.op("act", lambda a: a.activation(out=Tb[:, :, :], in_=Tblk[:, :, :], func=AF.Copy), reads=[T_r], writes=[Tb_r])
                    if rs_ <= 5:
                        continue
                    head_norm_gate(PS[0][0:64, 0:256], PS_r[0], ow, ow_r, ow2, ow2_r, st, st_r, rln, p_r, RW_EPS, P=64)
                    for h in range(4):
                        kb.op("dve", lambda v: v.scalar_tensor_tensor(out=ow[0:64, h * 64:(h + 1) * 64], in0=Vtok[0:64, c, h * 64:(h + 1) * 64],
                                                                      scalar=cb[:, c, h:h + 1], in1=ow[0:64, h * 64:(h + 1) * 64],
                                                                      op0=ALU.mult, op1=ALU.add), reads=[Vtok_r, cb_r, ow_r], writes=[ow_r])
                    kb.op("dve", lambda v: v.tensor_tensor(out=ob[0:64, :], in0=ow[0:64, :], in1=gtok[:, c, :], op=ALU.mult),
                          reads=[ow_r, gtok_r], writes=[ob_r])
                    out_to_OT(ob, ob_r, 64, OT, OT_r, 2, tok0 + c * 64)
            kb.barrier()

    NQ, NQS, NKC, NKS, NKW, NVC, NVS, NGT = 0, 512, 1024, 1280, 1536, 1792, 1920, 2176
    GK = 1.5957691216057308

    def nsa_phase(sq, l, OT, OT_r):
        ns_ = cfg.get("nsa_stop", 99)
        with ExitStack() as es:
            A = lambda n_, s_, d_: es.enter_context(sb(n_, s_, d_))
            QT = A("nQT", [128, 4, S], BF16)
            KTz = A("nKTz", [128, 3, 2, S], BF16)
            VCz = A("nVCz", [128, 2, S], BF16)
            VS = A("nVS", [128, NT, 2, 65], BF16)
            VW = A("nVW", [128, NT, 2, 65], BF16)
            gsig = A("ngsig", [128, NT, 24], F32)
            kcmpTz = A("nkcmp", [128, 2, 128], BF16)
            vcmp = A("nvcmp", [128, 2, 97], BF16)
            QT_r, KT_r, VC_r, VS_r, VW_r, gs_r = regs(4), regs(4), regs(4), regs(NT), regs(NT), regs(NT)
            kc_r, vcm_r = Reg(), Reg()
            kb.op("pool", lambda g: g.memset(KTz[:], 0.0), writes=KT_r)
            kb.op("pool", lambda g: g.memset(VCz[:], 0.0), writes=VC_r)
            kb.op("pool", lambda g: g.memset(VS[:, :, :, 64:65], 1.0), writes=VS_r)
            kb.op("pool", lambda g: g.memset(VW[:, :, :, 64:65], 1.0), writes=VW_r)
            kb.op("pool", lambda g: g.memset(kcmpTz[:], 0.0), writes=[kc_r])
            kb.op("pool", lambda g: g.memset(vcmp[:], 0.0), writes=[vcm_r])
            with ExitStack() as es1:
                A1 = lambda n_, s_, d_: es1.enter_context(sb(n_, s_, d_))
                Wn = A1("nWn", [128, 8, 2200], BF16)
                rp = A1("nrp", [128, 2, 512], F32)
                t1 = A1("nt1", [128, 512], F32)
                t2 = A1("nt2", [128, 512], F32)
                W_r, rp_r, t1_r, t2_r = Reg(), Reg(), Reg(), Reg()
                load_w(Wn, w_nsa.ap()[l], 2200, W_r, chunk=440)
                for mt in range(4):
                    tok0 = mt * 512
                    bl = slice(tok0, tok0 + 512)
                    xr = XT_r[mt * 4:mt * 4 + 4]
                    kb.dma("sp", rp[:, 0, :], c_rope.ap()[0][:, bl], writes=[rp_r])
                    kb.dma("sp", rp[:, 1, :], c_rope.ap()[1][:, bl], writes=[rp_r])
                    for i in range(4):
                        proj_fm(PS[0][:, :], Wn, NQ + i * 128, 128, tok0, 512, [W_r], xr, PS_r[0])
                        proj_fm(PS[1][:, :], Wn, NQS + i * 128, 128, tok0, 512, [W_r], xr, PS_r[1])
                        kb.op("dve", lambda v: v.tensor_tensor(out=t1[:, :], in0=PS[0][:, :], in1=rp[:, 0, :], op=ALU.mult),
                              reads=[PS_r[0], rp_r], writes=[t1_r])
                        kb.op("dve", lambda v: v.scalar_tensor_tensor(out=t2[:, :], in0=PS[1][:, :], scalar=0.125, in1=rp[:, 1, :],
                                                                      op0=ALU.mult, op1=ALU.mult), reads=[PS_r[1], rp_r], writes=[t2_r])
                        kb.op("dve", lambda v: v.scalar_tensor_tensor(out=QT[:, i, bl], in0=t1[:, :], scalar=0.125, in1=t2[:, :],
                                                                      op0=ALU.mult, op1=ALU.add), reads=[t1_r, t2_r], writes=[QT_r[mt]])
                    for ty, c0 in ((0, NKC), (1, NKS), (2, NKW)):
                        proj_fm(PS[0][:, :], Wn, c0, 128, tok0, 512, [W_r], xr, PS_r[0])
                        proj_fm(PS[1][:, :], Wn, c0 + 128, 128, tok0, 512, [W_r], xr, PS_r[1])
                        kb.op("dve", lambda v: v.tensor_tensor(out=t1[:, :], in0=PS[0][:, :], in1=rp[:, 0, :], op=ALU.mult),
                              reads=[PS_r[0], rp_r], writes=[t1_r])
                        kb.op("dve", lambda v: v.tensor_tensor(out=t2[:, :], in0=PS[1][:, :], in1=rp[:, 1, :], op=ALU.mult),
                              reads=[PS_r[1], rp_r], writes=[t2_r])
                        for g in range(2):
                            pr = slice(g * 64, g * 64 + 64)
                            kb.op("dve", lambda v: v.tensor_tensor(out=KTz[pr, ty, g, bl], in0=t1[pr, :], in1=t2[pr, :], op=ALU.add),
                                  reads=[t1_r, t2_r], writes=[KT_r[mt]])
                    proj_fm(PS[2][:, :], Wn, NVC, 128, tok0, 512, [W_r], xr, PS_r[2])
                    for g in range(2):
                        pr = slice(g * 64, g * 64 + 64)
                        kb.op("act", lambda a: a.activation(out=VCz[pr, g, bl], in_=PS[2][pr, :], func=AF.Copy),
                              reads=[PS_r[2]], writes=[VC_r[mt]])
                    for j in range(4):
                        tt = mt * 4 + j
                        proj_tm(PS[3][:, 0:256], Wn, NVS, 256, tt, [W_r], PS_r[3])
                        kb.op("act", lambda a: a.activation(out=VS[:, tt, :, 0:64], in_=PS[3][:, 0:128].rearrange("p (g d) -> p g d", g=2), func=AF.Copy),
                              reads=[PS_r[3]], writes=[VS_r[tt], PS_r[3]])
                        kb.op("act", lambda a: a.activation(out=VW[:, tt, :, 0:64], in_=PS[3][:, 128:256].rearrange("p (g d) -> p g d", g=2), func=AF.Copy),
                              reads=[PS_r[3]], writes=[VW_r[tt], PS_r[3]])
                        proj_tm(PS[4][:, 0:24], Wn, NGT, 24, tt, [W_r], PS_r[4])
                        kb.op("act", lambda a: a.activation(out=gsig[:, tt, :], in_=PS[4][:, 0:24], func=AF.Sigmoid),
                              reads=[PS_r[4]], writes=[gs_r[tt]])
                kb.barrier()
            if ns_ <= 1:
                kb.barrier()
                return
            with ExitStack() as es2:
                A2 = lambda n_, s_, d_: es2.enter_context(sb(n_, s_, d_))
                w1d = A2("nw1d", [128, 32, 256], BF16)
                w2d = A2("nw2d", [128, 2, 128], BF16)
                wv2 = A2("nwv2", [128, 2, 64], BF16)
                posz = A2("nposz", [128, 2, 32], BF16)
                hc = A2("nhc", [128, 2], F32)
                gx = A2("ngx", [128, 128], F32)
                gw = A2("ngw", [128, 128], F32)
                gh = A2("ngh", [128, 2, 128], BF16)
                w1_r, w2_r, hc_r, gx_r, gw_r, gh_r = Reg(), Reg(), Reg(), Reg(), Reg(), Reg()
                kb.op("pool", lambda g: g.memset(posz[:], 0.0), writes=[w2_r])
                kb.op("pool", lambda g: g.memset(gh[:], 0.0), writes=[gh_r])
                for kv in range(2):
                    kb.dma("pool", posz[0:64, kv, :], nsa_posT.ap()[l, kv], reads=[w2_r], writes=[w2_r])
                for half in range(2):
                    kb.dma("pool", w2d[:, :, half * 64:(half + 1) * 64], nsa_wk2.ap()[l].rearrange("(t p) n -> p t n", p=128), writes=[w2_r])
                kb.dma("pool", wv2[:, :, :], nsa_wv2.ap()[l].rearrange("(t p) n -> p t n", p=128), writes=[w2_r])
                kb.dma("pool", vcmp[:, 0, 65:97], c_ovl.ap(), reads=[vcm_r], writes=[vcm_r])
                kb.dma("pool", vcmp[:, 1, 65:97], c_ovl.ap(), reads=[vcm_r], writes=[vcm_r])
                kb.op("pool", lambda g: g.memset(vcmp[0:127, :, 64:65], 1.0), reads=[vcm_r], writes=[vcm_r])
                for kv, w1src in ((0, nsa_wk1), (1, nsa_wv1)):
                    src3 = w1src.ap()[l].rearrange("(l d) n -> d l n", d=64)
                    for half in range(2):
                        for l4 in range(8):
                            kb.dma("pool", w1d[half * 64:(half + 1) * 64, l4 * 4:(l4 + 1) * 4, :], src3[:, l4 * 4:(l4 + 1) * 4, :], writes=[w1_r])
                    srcz = (lambda g: KTz[:, 0, g, :]) if kv == 0 else (lambda g: VCz[:, g, :])
                    src_regs = KT_r if kv == 0 else VC_r
                    for hf in range(2):
                        def fn(pe):
                            inst = None
                            for ll in range(32):
                                inst = pe.matmul(PS[0][:, 0:1], lhsT=w1d[:, ll, hf * 128:(hf + 1) * 128], rhs=posz[:, kv, ll:ll + 1],
                                                 start=(ll == 0), stop=(ll == 31))
                            return inst
                        kb.op("pe", fn, reads=[w1_r, w2_r], writes=[PS_r[0]])
                        kb.op("act", lambda a: a.activation(out=hc[:, hf:hf + 1], in_=PS[0][:, 0:1], func=AF.Copy), reads=[PS_r[0]], writes=[hc_r])
                    for g in range(2):
                        for hf in range(2):
                            def fn(pe):
                                inst = None
                                for ll in range(32):
                                    rhs = bass.AP(srcz(g).tensor, srcz(g).offset + ll, [srcz(g).ap[0], [16, 127]])
                                    inst = pe.matmul(PS[1][:, 0:127], lhsT=w1d[:, ll, hf * 128:(hf + 1) * 128], rhs=rhs,
                                                     start=(ll == 0), stop=(ll == 31))
                                return inst
                            kb.op("pe", fn, reads=[w1_r] + src_regs, writes=[PS_r[1]])
                            kb.op("act", lambda a: a.activation(out=gx[:, 0:127], in_=PS[1][:, 0:127], func=AF.Identity, bias=hc[:, hf:hf + 1], scale=1.0),
                                  reads=[PS_r[1], hc_r], writes=[gx_r])
                            kb.op("dve", lambda v: v.tensor_tensor(out=gw[:, 0:127], in0=gx[:, 0:127], in1=gx[:, 0:127], op=ALU.mult),
                                  reads=[gx_r], writes=[gw_r])
                            kb.op("dve", lambda v: v.tensor_scalar(out=gw[:, 0:127], in0=gw[:, 0:127], scalar1=0.044715, scalar2=1.0,
                                                                   op0=ALU.mult, op1=ALU.add), reads=[gw_r], writes=[gw_r])
                            kb.op("dve", lambda v: v.tensor_tensor(out=gw[:, 0:127], in0=gw[:, 0:127], in1=gx[:, 0:127], op=ALU.mult),
                                  reads=[gw_r, gx_r], writes=[gw_r])
                            kb.op("act", lambda a: a.activation(out=gw[:, 0:127], in_=gw[:, 0:127], func=AF.Sigmoid, scale=GK), reads=[gw_r], writes=[gw_r])
                            kb.op("dve", lambda v: v.tensor_tensor(out=gh[:, hf, 0:127], in0=gx[:, 0:127], in1=gw[:, 0:127], op=ALU.mult),
                                  reads=[gw_r, gx_r], writes=[gh_r])
                        if kv == 0:
                            def fn(pe):
                                pe.matmul(PS[2][:, 0:127], lhsT=w2d[:, 0, :], rhs=gh[:, 0, 0:127], start=True, stop=False)
                                return pe.matmul(PS[2][:, 0:127], lhsT=w2d[:, 1, :], rhs=gh[:, 1, 0:127], start=False, stop=True)
                            kb.op("pe", fn, reads=[w2_r, gh_r], writes=[PS_r[2]])
                            pr = slice(g * 64, g * 64 + 64)
                            kb.op("act", lambda a: a.activation(out=kcmpTz[pr, g, 0:127], in_=PS[2][pr, 0:127], func=AF.Copy),
                                  reads=[PS_r[2], kc_r], writes=[kc_r])
                        else:
                            def fn(pe):
                                pe.matmul(PS[2][:, 0:64], lhsT=gh[:, 0, :], rhs=wv2[:, 0, :], start=True, stop=False)
                                return pe.matmul(PS[2][:, 0:64], lhsT=gh[:, 1, :], rhs=wv2[:, 1, :], start=False, stop=True)
                            kb.op("pe", fn, reads=[w2_r, gh_r], writes=[PS_r[2]])
                            kb.op("act", lambda a: a.activation(out=vcmp[0:127, g, 0:64], in_=PS[2][0:127, 0:64], func=AF.Copy),
                                  reads=[PS_r[2], vcm_r], writes=[vcm_r])
                kb.barrier()
            if ns_ <= 2:
                kb.barrier()
                return
            selbT = A("nselbT", [128, 2, S], BF16)
            onehot = A("nonehot", [128, S], BF16)
            cmask = A("ncmask", [128, S], BF16)
            selm = A("nselm", [128, 2, NT, 32], F32)
            wmask = A("nwmask", [128, 128], F32)
            PT = [A("nPT%d" % i, [128, 640], BF16) for i in range(2)]
            ex = A("nex", [128, 512], F32)
            ocs = A("nocs", [128, 4, 97], F32)
            ONSA = A("nONSA", [128, 4, 512], F32)
            impb = A("nimp", [128, 4, 2, 32], F32)
            sc = A("nsc", [128, 32], F32)
            cm3 = A("ncm3", [128, 32, 32], F32)
            sb16 = A("nsb16", [128, 32], BF16)
            dn = A("ndn", [128, 16], F32)
            obf = A("nobf", [128, 512], BF16)
            k_r, sel_r, PT_r, ex_r, ocs_r, on_r, imp_r, sc_r, cm3_r, sb16_r, dn_r, obf_r = \
                Reg(), regs(4), regs(2), Reg(), Reg(), regs(4), Reg(), Reg(), Reg(), Reg(), Reg(), Reg()
            kb.op("pool", lambda g: g.memset(selbT[:], 0.0), writes=sel_r)
            kb.op("pool", lambda g: g.memset(onehot[:], 0.0), writes=[k_r])
            kb.dma("pool", onehot[0:32, :], c_onehot.ap(), reads=[k_r], writes=[k_r])
            kb.dma("pool", cmask[:, :], c_cmpmask.ap(), writes=[k_r])
            for m_ in range(2):
                kb.dma("sp", selm[:, m_, :, :], c_selm.ap()[m_], writes=[k_r])
            kb.op("dve", lambda v: v.tensor_scalar(out=wmask[:, :], in0=caus[:, 0, :], scalar1=-1.0, scalar2=1.0, op0=ALU.mult, op1=ALU.add),
                  reads=[c_r], writes=[k_r])

            def branch_finish(acc_ps, acc_r, W, nq, h, b, tts, first):
                kb.op("act", lambda a: a.activation(out=ocs[:, 0:nq, 0:W], in_=acc_ps.rearrange("p (q w) -> p q w", q=nq), func=AF.Copy),
                      reads=[acc_r], writes=[ocs_r, acc_r])
                kb.op("dve", lambda v: v.tensor_scalar(out=dn[:, 0:nq], in0=ocs[:, 0:nq, 64], scalar1=1e-30, scalar2=None, op0=ALU.max),
                      reads=[ocs_r], writes=[dn_r])
                kb.op("dve", lambda v: v.reciprocal(out=dn[:, 0:nq], in_=dn[:, 0:nq]), reads=[dn_r], writes=[dn_r])
                kb.op("dve", lambda v: v.tensor_tensor(out=dn[:, 8:8 + nq], in0=dn[:, 0:nq], in1=gsig[:, tts[0]:tts[0] + nq, h * 3 + b], op=ALU.mult),
                      reads=[dn_r] + gs_r[tts[0]:tts[0] + nq], writes=[dn_r])
                for qi in range(nq):
                    tl = tts[qi] % 4
                    if first:
                        kb.op("dve", lambda v: v.tensor_scalar(out=ONSA[:, tl, h * 64:(h + 1) * 64], in0=ocs[:, qi, 0:64], scalar1=dn[:, 8 + qi:9 + qi],
                                                               scalar2=None, op0=ALU.mult), reads=[ocs_r, dn_r], writes=[on_r[tl]])
                    else:
                        kb.op("dve", lambda v: v.scalar_tensor_tensor(out=ONSA[:, tl, h * 64:(h + 1) * 64], in0=ocs[:, qi, 0:64], scalar=dn[:, 8 + qi:9 + qi],
                                                                      in1=ONSA[:, tl, h * 64:(h + 1) * 64], op0=ALU.mult, op1=ALU.add),
                              reads=[ocs_r, dn_r, on_r[tl]], writes=[on_r[tl]])

            for qb in range(4):
                qc = slice(qb * 512, (qb + 1) * 512)
                tts = list(range(qb * 4, qb * 4 + 4))
                for h in range(8):
                    g, i = h // 4, h % 4
                    pi = h % 2
                    kb.op("pe", lambda pe: pe.matmul(PS[pi][:, :], lhsT=kcmpTz[:, g, :], rhs=QT[:, i, qc], start=True, stop=True),
                          reads=[kc_r, QT_r[qb]], writes=[PS_r[pi]])
                    kb.op("act", lambda a: a.activation(out=ex[:, :], in_=PS[pi][:, :], func=AF.Exp), reads=[PS_r[pi]], writes=[ex_r])
                    kb.op("dve", lambda v: v.tensor_tensor(out=PT[pi][:, 0:512], in0=ex[:, :], in1=cmask[:, qc], op=ALU.mult),
                          reads=[ex_r, k_r], writes=[PT_r[pi]])
                    ai = 2 + (h % 2)

                    def fn(pe):
                        inst = None
                        for q in range(4):
                            inst = pe.matmul(PS[ai][:, q * 97:(q + 1) * 97], lhsT=PT[pi][:, q * 128:(q + 1) * 128], rhs=vcmp[:, g, :], start=True, stop=True)
                        return inst
                    kb.op("pe", fn, reads=[PT_r[pi], vcm_r], writes=[PS_r[ai]])
                    branch_finish(PS[ai][:, 0:388], PS_r[ai], 97, 4, h, 0, tts, True)
                    for q in range(4):
                        if i == 0:
                            kb.op("dve", lambda v: v.tensor_scalar(out=impb[:, q, g, :], in0=ocs[:, q, 65:97], scalar1=dn[:, q:q + 1], scalar2=None, op0=ALU.mult),
                                  reads=[ocs_r, dn_r], writes=[imp_r])
                        else:
                            kb.op("dve", lambda v: v.scalar_tensor_tensor(out=impb[:, q, g, :], in0=ocs[:, q, 65:97], scalar=dn[:, q:q + 1], in1=impb[:, q, g, :],
                                                                          op0=ALU.mult, op1=ALU.add), reads=[ocs_r, dn_r, imp_r], writes=[imp_r])
                if ns_ <= 3:
                    continue
                for q in range(4):
                    tt = tts[q]
                    for g in range(2):
                        kb.op("dve", lambda v: v.tensor_tensor(out=sc[:, :], in0=impb[:, q, g, :], in1=selm[:, 0, tt, :], op=ALU.mult),
                              reads=[imp_r, k_r], writes=[sc_r])
                        kb.op("dve", lambda v: v.tensor_tensor(out=sc[:, :], in0=sc[:, :], in1=selm[:, 1, tt, :], op=ALU.add),
                              reads=[sc_r, k_r], writes=[sc_r])
                        kb.op("dve", lambda v: v.tensor_tensor(out=cm3[:, :, :], in0=bass.AP(sc, 0, [[32, 128], [0, 32], [1, 32]]),
                                                              in1=bass.AP(sc, 0, [[32, 128], [1, 32], [0, 32]]), op=ALU.is_gt),
                              reads=[sc_r], writes=[cm3_r])
                        kb.op("dve", lambda v: v.reduce_sum(out=sc[:, :], in_=cm3[:, :, :], axis=AX.X), reads=[cm3_r, sc_r], writes=[sc_r])
                        kb.op("dve", lambda v: v.tensor_scalar(out=sb16[:, :], in0=sc[:, :], scalar1=15.5, scalar2=-30000.0, op0=ALU.is_gt, op1=ALU.mult),
                              reads=[sc_r], writes=[sb16_r])
                        kb.op("pe", lambda pe: pe.transpose(out=PB[0][0:32, 0:128], in_=sb16[:, :], identity=ident[:, :]),
                              reads=[sb16_r, c_r], writes=[PB_r[0]])
                        kb.op("act", lambda a: a.activation(out=selbT[0:32, g, tt * 128:(tt + 1) * 128], in_=PB[0][0:32, 0:128], func=AF.Copy),
                              reads=[PB_r[0]], writes=[sel_r[qb]])
                if ns_ <= 4:
                    continue
                for h in range(8):
                    g, i = h // 4, h % 4
                    nkt = 4 * qb + 4
                    for kt in range(nkt):
                        kc_ = slice(kt * 128, (kt + 1) * 128)
                        pi = kt % 2

                        def fn(pe):
                            pe.matmul(PS[pi][:, :], lhsT=KTz[:, 1, g, kc_], rhs=QT[:, i, qc], start=True, stop=False)
                            return pe.matmul(PS[pi][:, :], lhsT=onehot[:, kc_], rhs=selbT[:, g, qc], start=False, stop=True)
                        kb.op("pe", fn, reads=[KT_r[kt // 4], QT_r[qb], k_r, sel_r[qb]], writes=[PS_r[pi]])
                        kb.op("act", lambda a: a.activation(out=PT[pi][:, 0:512], in_=PS[pi][:, :], func=AF.Exp), reads=[PS_r[pi]], writes=[PT_r[pi]])
                        if kt >= 4 * qb:
                            ql = kt - 4 * qb
                            kb.op("dve", lambda v: v.tensor_tensor(out=PT[pi][:, ql * 128:(ql + 1) * 128], in0=PT[pi][:, ql * 128:(ql + 1) * 128],
                                                                  in1=caus[:, 0, :], op=ALU.mult), reads=[PT_r[pi], c_r], writes=[PT_r[pi]])
                        for q in range(4):
                            qt = 4 * qb + q
                            if qt < kt:
                                continue
                            kb.op("pe", lambda pe: pe.matmul(PS[2 + q][:, 0:65], lhsT=PT[pi][:, q * 128:(q + 1) * 128], rhs=VS[:, kt, g, :],
                                                             start=(kt == 0), stop=(kt == qt)), reads=[PT_r[pi], VS_r[kt]], writes=[PS_r[2 + q]])
                    for q in range(4):
                        branch_finish(PS[2 + q][:, 0:65], PS_r[2 + q], 65, 1, h, 1, [tts[q]], False)
                if ns_ <= 5:
                    continue
                for h in range(8):
                    g, i = h // 4, h % 4
                    for q in range(4):
                        qt = 4 * qb + q
                        qcs = slice(qt * 128, (qt + 1) * 128)
                        kts = [kt for kt in range(qt - 4, qt + 1) if kt >= 0]
                        pi = q % 2
                        main = [kt for kt in kts if kt >= qt - 3]

                        def fn(pe):
                            inst = None
                            for n_, kt in enumerate(main):
                                inst = pe.matmul(PS[pi][:, n_ * 128:(n_ + 1) * 128], lhsT=KTz[:, 2, g, kt * 128:(kt + 1) * 128], rhs=QT[:, i, qcs],
                                                 start=True, stop=True)
                            return inst
                        kb.op("pe", fn, reads=KT_r + [QT_r[qb]], writes=[PS_r[pi]])
                        nm = len(main)
                        kb.op("act", lambda a: a.activation(out=PT[pi][:, 0:nm * 128], in_=PS[pi][:, 0:nm * 128], func=AF.Exp), reads=[PS_r[pi]], writes=[PT_r[pi]])
                        kb.op("dve", lambda v: v.tensor_tensor(out=PT[pi][:, (nm - 1) * 128:nm * 128], in0=PT[pi][:, (nm - 1) * 128:nm * 128],
                                                              in1=caus[:, 0, :], op=ALU.mult), reads=[PT_r[pi], c_r], writes=[PT_r[pi]])
                        tail = (qt - 4 >= 0)
                        if tail:
                            kt = qt - 4
                            kb.op("pe", lambda pe: pe.matmul(PS[2 + pi][:, 0:128], lhsT=KTz[:, 2, g, kt * 128:(kt + 1) * 128], rhs=QT[:, i, qcs],
                                                             start=True, stop=True), reads=KT_r + [QT_r[qb]], writes=[PS_r[2 + pi]])
                            kb.op("act", lambda a: a.activation(out=ex[:, 0:128], in_=PS[2 + pi][:, 0:128], func=AF.Exp), reads=[PS_r[2 + pi]], writes=[ex_r])
                            kb.op("dve", lambda v: v.tensor_tensor(out=PT[pi][:, 512:640], in0=ex[:, 0:128], in1=wmask[:, :], op=ALU.mult),
                                  reads=[ex_r, k_r, PT_r[pi]], writes=[PT_r[pi]])

                        def fn(pe):
                            inst = None
                            seq_ = [(n_, kt) for n_, kt in enumerate(main)] + ([(4, qt - 4)] if tail else [])
                            for idx, (n_, kt) in enumerate(seq_):
                                inst = pe.matmul(PS[4][:, q * 65:(q + 1) * 65], lhsT=PT[pi][:, n_ * 128:(n_ + 1) * 128], rhs=VW[:, kt, g, :],
                                                 start=(idx == 0), stop=(idx == len(seq_) - 1))
                            return inst
                        kb.op("pe", fn, reads=[PT_r[pi]] + VW_r[max(0, qt - 4):qt + 1], writes=[PS_r[4]])
                    branch_finish(PS[4][:, 0:260], PS_r[4], 65, 4, h, 2, tts, False)
                if ns_ <= 6:
                    continue
                for q in range(4):
                    tt = tts[q]
                    kb.op("act", lambda a: a.activation(out=obf[:, :], in_=ONSA[:, q, :], func=AF.Copy), reads=[on_r[q]], writes=[obf_r])
                    for half in range(2):
                        out_to_OT(obf[:, half * 256:(half + 1) * 256], obf_r, 128, OT, OT_r, 4 + 2 * half, tt * 128)
            kb.barrier()

    for sq in range(NSEQ):
        for l in range(NLAY):
            if l == 0:
                with sb("xstage", [128, 2, D], F32) as xst:
                    xst_r = regs(2)
                    for tt in range(NT):
                        i = tt % 2
                        kb.dma("sp", xst[:, i, :], x_d.ap()[sq, tt * 128:(tt + 1) * 128, :], writes=[xst_r[i]])
                        make_xT(xst[:, i, :], xst_r[i], tt)
                    kb.barrier()
            if cfg.get("stop") == "xt":
                continue
            res_src = (lambda tt: x_d.ap()[sq, tt * 128:(tt + 1) * 128, :]) if l == 0 else \
                      (lambda tt: xres[1].ap()[tt * 128:(tt + 1) * 128, :])
            res_regs = None if l == 0 else xres_r[1]

            with sb("OT", [128, 8, S], BF16) as OT:
                OT_r = regs(NT)
                if inject_O:
                    for k in range(8):
                        kb.dma("pool", OT[:, k, :], dbg_OT.ap()[:, k, :], writes=OT_r)
                if "gla" in mixers:
                    gla_phase(sq, l, OT, OT_r)
                if "rwkv" in mixers:
                    rwkv_phase(sq, l, OT, OT_r)
                if "nsa" in mixers:
                    nsa_phase(sq, l, OT, OT_r)
                if "OT" in dump_d:
                    for k in range(8):
                        with sb("otd", [128, S], F32) as otd:
                            r_ = Reg()
                            kb.op("act", lambda a: a.activation(out=otd[:], in_=OT[:, k, :], func=AF.Copy),
                                  reads=OT_r, writes=[r_])
                            dump("OT", otd[:], r_, idx=k)
                            kb.barrier()
                if cfg.get("stop") == "outproj0":
                    continue
                with sb("Wo", [128, 8, D], BF16) as Wo, \
                        sb("ln1", [128, 2, D], F32) as ln1, \
                        sb("xs1", [128, 2, D], F32) as xs1, \
                        sb("st1", [128, 2, 32], F32) as st1:
                    Wo_r = Reg()
                    ln_r = Reg()
                    xs_r = regs(2)
                    st_r = regs(2)
                    load_w(Wo, w_out.ap()[l], D, Wo_r)
                    kb.dma("sp", ln1[:, 0, :], bcast_rows(rowtab, l * 5120 + 1024, D), writes=[ln_r])
                    kb.dma("sp", ln1[:, 1, :], bcast_rows(rowtab, l * 5120 + 2048, D), writes=[ln_r])
                    for tt in range(NT):
                        i = tt % 2
                        kb.dma("sp", xs1[:, i, :], res_src(tt), reads=([res_regs[tt]] if res_regs else []), writes=[xs_r[i]])
                        for hf in range(2):
                            def fn(pe, hf=hf):
                                inst = None
                                for k in range(8):
                                    inst = pe.matmul(PS[hf][:, :], lhsT=OT[:, k, tt * 128:(tt + 1) * 128],
                                                     rhs=Wo[:, k, hf * 512:(hf + 1) * 512], start=(k == 0), stop=(k == 7))
                                return inst
                            kb.op("pe", fn, reads=[OT_r[tt], Wo_r], writes=[PS_r[hf]])
                            kb.op("dve", lambda v, hf=hf: v.scalar_tensor_tensor(
                                out=xs1[:, i, hf * 512:(hf + 1) * 512], in0=xs1[:, i, hf * 512:(hf + 1) * 512], scalar=ALPHA,
                                in1=PS[hf][:, :], op0=ALU.mult, op1=ALU.add), reads=[PS_r[hf], xs_r[i]], writes=[xs_r[i]])
                        layer_norm(xs1[:, i, :], xs_r[i], ln1[:, 0, :], ln1[:, 1, :], ln_r, st1[:, i, :], st_r[i])
                        kb.dma("sp", xres[0].ap()[tt * 128:(tt + 1) * 128, :], xs1[:, i, :], reads=[xs_r[i]], writes=[xres_r[0][tt]])
                        make_xT(xs1[:, i, :], xs_r[i], tt)
                        if "x1" in dump_d and sq == 0 and l == cfg.get("dump_layer", 0):
                            dump("x1", xs1[:, i, :], xs_r[i], idx=tt)
                    kb.barrier()
            if cfg.get("stop") in ("outproj", "outproj0"):
                continue
            with sb("aT", [128, NFC, 1024], BF16) as aT, \
                    sb("Wd", [128, NFC, D], BF16) as Wd, \
                    sb("Wgu", [128, 2, 2, 8, 512], BF16) as Wgu, \
                    sb("sg", [128, 2, 512], F32) as sg, \
                    sb("ln2", [128, 2, D], F32) as ln2, \
                    sb("xs2", [128, 2, D], F32) as xs2, \
                    sb("st2", [128, 2, 32], F32) as st2:
                Wd_r = Reg()
                ln_r = Reg()
                aT_r = regs(NFC)
                Wgu_r = regs(2)
                sg_r = regs(2)
                xs_r = regs(2)
                st_r = regs(2)
                kb.dma("sp", ln2[:, 0, :], bcast_rows(rowtab, l * 5120 + 3072, D), writes=[ln_r])
                kb.dma("sp", ln2[:, 1, :], bcast_rows(rowtab, l * 5120 + 4096, D), writes=[ln_r])
                for k in range(NFC):
                    for c0 in (0, 512):
                        kb.dma("pool", Wd[:, k, c0:c0 + 512], w_down.ap()[l, k * 128:(k + 1) * 128, c0:c0 + 512], writes=[Wd_r])
                last = (l == NLAY - 1)
                for mt in range(2):
                    tok0 = mt * 1024
                    for hc in range(NFC):
                        cg, ci = hc // 4, hc % 4
                        wi = cg % 2
                        if ci == 0:
                            ncol = min(512, FF - cg * 512)
                            for gu, wsrc in ((0, w_gate), (1, w_up)):
                                for k in range(8):
                                    kb.dma("pool", Wgu[:, wi, gu, k, 0:ncol],
                                           wsrc.ap()[l, k * 128:(k + 1) * 128, cg * 512:cg * 512 + ncol],
                                           writes=[Wgu_r[wi]])
                        for blk in range(2):
                            t0 = tok0 + blk * 512
                            pg, pu = (0, 1) if blk == 0 else (2, 3)
                            for gu, pi in ((0, pg), (1, pu)):
                                def fn(pe, gu=gu, pi=pi):
                                    inst = None
                                    for k in range(8):
                                        inst = pe.matmul(PS[pi][:, :], lhsT=Wgu[:, wi, gu, k, ci * 128:(ci + 1) * 128],
                                                         rhs=XT[:, k, t0:t0 + 512], start=(k == 0), stop=(k == 7))
                                    return inst
                                kb.op("pe", fn, reads=[Wgu_r[wi]] + XT_r[t0 // 128:t0 // 128 + 4], writes=[PS_r[pi]])
                            kb.op("act", lambda a: a.activation(out=sg[:, blk, :], in_=PS[pg][:, :], func=AF.Silu),
                                  reads=[PS_r[pg]], writes=[sg_r[blk]])
                            kb.op("dve", lambda v: v.tensor_tensor(out=aT[:, hc, blk * 512:(blk + 1) * 512], in0=sg[:, blk, :],
                                                                  in1=PS[pu][:, :], op=ALU.mult),
                                  reads=[sg_r[blk], PS_r[pu]], writes=[aT_r[hc]])
                    for t8 in range(8):
                        tt = mt * 8 + t8
                        i = tt % 2
                        kb.dma("sp", xs2[:, i, :], xres[0].ap()[tt * 128:(tt + 1) * 128, :], reads=[xres_r[0][tt]], writes=[xs_r[i]])
                        for hf in range(2):
                            pi = 4 + hf

                            def fn(pe, hf=hf, pi=pi):
                                inst = None
                                for k in range(NFC):
                                    inst = pe.matmul(PS[pi][:, :], lhsT=aT[:, k, t8 * 128:(t8 + 1) * 128],
                                                     rhs=Wd[:, k, hf * 512:(hf + 1) * 512], start=(k == 0), stop=(k == NFC - 1))
                                return inst
                            kb.op("pe", fn, reads=aT_r + [Wd_r], writes=[PS_r[pi]])
                            kb.op("dve", lambda v, hf=hf, pi=pi: v.scalar_tensor_tensor(
                                out=xs2[:, i, hf * 512:(hf + 1) * 512], in0=xs2[:, i, hf * 512:(hf + 1) * 512], scalar=ALPHA,
                                in1=PS[pi][:, :], op0=ALU.mult, op1=ALU.add), reads=[PS_r[pi], xs_r[i]], writes=[xs_r[i]])
                        layer_norm(xs2[:, i, :], xs_r[i], ln2[:, 0, :], ln2[:, 1, :], ln_r, st2[:, i, :], st_r[i])
                        if last:
                            kb.dma("sp", out_d.ap()[sq, tt * 128:(tt + 1) * 128, :], xs2[:, i, :], reads=[xs_r[i]])
                        else:
                            kb.dma("sp", xres[1].ap()[tt * 128:(tt + 1) * 128, :], xs2[:, i, :], reads=[xs_r[i]], writes=[xres_r[1][tt]])
                        if "x2" in dump_d and sq == 0 and l == cfg.get("dump_layer", 0):
                            dump("x2", xs2[:, i, :], xs_r[i], idx=tt)
                    if not last:
                        pass
                kb.barrier()
                if not last:
                    for tt in range(NT):
                        i = tt % 2
                        kb.dma("sp", xs2[:, i, :], xres[1].ap()[tt * 128:(tt + 1) * 128, :], reads=[xres_r[1][tt]], writes=[xs_r[i]])
                        make_xT(xs2[:, i, :], xs_r[i], tt)
                    kb.barrier()
    kb.finish()
    return kb


def host_consts():
    c = {}
    c["c_ident"] = np.eye(128, dtype=np.float32)
    s = np.arange(128)
    c["c_caus"] = (s[:, None] <= s[None, :]).astype(np.float32)
    half = 32
    inv = (10000.0 ** (-np.arange(half, dtype=np.float32) / half)).astype(np.float32)
    ang = (np.arange(S, dtype=np.float32)[:, None] * inv[None, :]).astype(np.float32)
    cos = np.cos(ang).astype(np.float32).T
    sin = np.sin(ang).astype(np.float32).T
    cosT = np.concatenate([cos, cos, cos, cos], 0)
    sinT = np.concatenate([-sin, sin, -sin, sin], 0)
    c["c_rope"] = np.stack([cosT, sinT]).astype(np.float32)
    cc = np.arange(128)
    t = np.arange(S)
    c["c_cmpmask"] = ((16 * cc[:, None] + 31 <= t[None, :]) & (cc[:, None] < 127)).astype(np.float32)
    j = np.arange(32)
    c["c_onehot"] = ((t[None, :] // 64) == j[:, None]).astype(np.float32)
    cur = t // 64
    forced = (j[None, :] == 0) | (j[None, :] == cur[:, None]) | (j[None, :] == cur[:, None] - 1)
    future = j[None, :] > cur[:, None]
    m1 = (~forced & ~future).astype(np.float32)
    m2 = np.where(forced, 1e9, np.where(future, -1e9, 0.0)).astype(np.float32)
    selm = np.stack([m1, m2])
    c["c_selm"] = np.ascontiguousarray(selm.reshape(2, NT, 128, 32).transpose(0, 2, 1, 3))
    c0 = np.arange(127) * 16
    s0 = np.arange(32) * 64
    lo = np.maximum(c0[:, None], s0[None, :])
    hi = np.minimum(c0[:, None] + 32, s0[None, :] + 64)
    ov = np.zeros((128, 32), np.float32)
    ov[:127] = np.maximum(hi - lo, 0) / 16
    c["c_ovl"] = ov
    blk = np.zeros((128, 128), np.float32)
    blk[:64, :64] = 1
    blk[64:, 64:] = 1
    c["c_blk"] = blk
    hs = np.zeros((128, 2), np.float32)
    hs[:64, 0] = 1
    hs[64:, 1] = 1
    c["c_hsel"] = hs
    i64 = np.arange(64)
    lo_strict = (i64[None, :] < i64[:, None]).astype(np.float32)
    up_strict = (i64[:, None] < i64[None, :]).astype(np.float32)
    up_incl = (i64[:, None] <= i64[None, :]).astype(np.float32)
    c["c_rwmask"] = np.ascontiguousarray(np.stack([lo_strict, up_strict, up_strict, up_incl, up_incl], axis=1))
    return c


def host_layout(inp):
    d = {}
    f = lambda a: np.ascontiguousarray(np.asarray(a, dtype=np.float32))
    for k in ("w_in", "w_in_vres", "gla_w_a2", "rwkv_w2", "rwkv_a2", "rwkv_v2", "rwkv_g2", "nsa_wk1", "nsa_wk2",
              "nsa_wv1", "nsa_wv2", "w_out", "ffn_w_gate", "ffn_w_up", "ffn_w_down"):
        d[k] = f(inp[k])
    base = 2096
    sw = lambda b: list(range(b + 32, b + 64)) + list(range(b, b + 32))
    pl = lambda b: list(range(b, b + 64))
    cols = []
    for i in range(4):
        cols += pl(base + i * 64) + pl(base + (4 + i) * 64)
    for i in range(4):
        cols += sw(base + i * 64) + sw(base + (4 + i) * 64)
    for c0 in (512, 768, 1024):
        cols += pl(base + c0) + pl(base + c0 + 64)
        cols += sw(base + c0) + sw(base + c0 + 64)
    cols += list(range(base + 640, base + 768)) + list(range(base + 896, base + 1024)) + list(range(base + 1152, base + 1280))
    cols += list(range(base + 1280, base + 1304))
    assert len(cols) == 2200
    d["w_nsa"] = f(np.asarray(inp["w_in"])[:, :, cols])
    d["nsa_posT"] = f(np.stack([np.asarray(inp["nsa_pos_k"]).transpose(0, 2, 1),
                                np.asarray(inp["nsa_pos_v"]).transpose(0, 2, 1)], axis=1))
    pt = np.zeros((L, 128, 32), np.float32)
    mu = np.asarray(inp["rwkv_mu"])
    rwt = [(0, 128), (128, 128), (256, 128), (384, 128), (512, 128), (640, 128), (768, 64), (832, 64), (896, 128), (1024, 32)]
    for l in range(L):
        pt[l, :, 0:2] = np.asarray(inp["gla_b_a"])[l].reshape(2, 128).T
        for i, (c0, n) in enumerate(rwt):
            pt[l, :n, 2 + i] = mu[l, c0:c0 + n]
        if l >= 1:
            pt[l, :32, 12] = np.asarray(inp["rwkv_mu_vres"])[l - 1]
            pt[l, :, 17:19] = np.asarray(inp["rwkv_v0"])[l - 1].reshape(2, 128).T
        pt[l, :, 13:15] = np.asarray(inp["rwkv_w0"])[l].reshape(2, 128).T
        pt[l, :, 15:17] = np.asarray(inp["rwkv_a0"])[l].reshape(2, 128).T
        pt[l, :, 19:21] = np.asarray(inp["rwkv_k_k"])[l].reshape(2, 128).T
        pt[l, :, 21:23] = np.asarray(inp["rwkv_k_a"])[l].reshape(2, 128).T
        pt[l, :, 23:25] = np.asarray(inp["rwkv_r_k"])[l].reshape(2, 128).T
    d["ptab"] = pt
    rt = np.zeros((L, 5120), np.float32)
    for l in range(L):
        rt[l, 0:256] = np.asarray(inp["gla_ln_w"])[l]
        rt[l, 256:512] = np.asarray(inp["gla_ln_b"])[l]
        rt[l, 512:768] = np.asarray(inp["rwkv_ln_w"])[l]
        rt[l, 768:1024] = np.asarray(inp["rwkv_ln_b"])[l]
        rt[l, 1024:2048] = np.asarray(inp["ln1_w"])[l]
        rt[l, 2048:3072] = np.asarray(inp["ln1_b"])[l]
        rt[l, 3072:4096] = np.asarray(inp["ln2_w"])[l]
        rt[l, 4096:5120] = np.asarray(inp["ln2_b"])[l]
    d["rowtab"] = rt
    return d


_CACHE = {}


def kernel(**inputs):
    cfg = {}
    if "full" not in _CACHE:
        _CACHE["full"] = build(cfg)
    kb = _CACHE["full"]
    shared = host_layout(inputs)
    shared.update(host_consts())
    x = np.ascontiguousarray(np.asarray(inputs["x"], dtype=np.float32))
    in_maps = []
    for c in range(8):
        m = dict(shared)
        m["x"] = x[2 * c:2 * c + 2]
        in_maps.append(m)
    res = run_bass_kernel_spmd(kb.nc, in_maps, core_ids=list(range(8)))
    return np.concatenate([r["out"] for r in res.results], axis=0).astype(np.float32)
```

```python
import math
from contextlib import ExitStack
import numpy as np
import concourse.bass as bass
import concourse.mybir as mybir
from concourse.bass_utils import run_bass_kernel_spmd

F32 = mybir.dt.float32
BF16 = mybir.dt.bfloat16
AF = mybir.ActivationFunctionType
ALU = mybir.AluOpType
AX = mybir.AxisListType

S = 2048
D = 1024
NT = S // 128
L = 2
FF = 2816
NFC = FF // 128
ALPHA = float((2 * L) ** 0.25)
LN_EPS = 1e-5
RW_EPS = 64e-5
NDS = 6


class Reg:
    __slots__ = ("lw", "rd")

    def __init__(self):
        self.lw = None
        self.rd = {}


def regs(n):
    return [Reg() for _ in range(n)]


class KB:
    def __init__(self):
        nc = bass.Bass("TRN2", target_bir_lowering=False)
        self.nc = nc
        self.E = {"pe": nc.tensor, "act": nc.scalar, "dve": nc.vector, "pool": nc.gpsimd, "sp": nc.sync}
        self.sems = {}
        self.cnt = {}
        for e in ("pe", "act", "dve", "pool"):
            self.sems[e] = nc.alloc_semaphore("s_" + e)
            self.cnt[e] = 0
        self.dq = {}
        for q, nds in (("sp", 12), ("pool", NDS), ("act", 6)):
            keys = []
            for i in range(nds):
                k = "d_%s%d" % (q, i)
                self.sems[k] = nc.alloc_semaphore(k)
                self.cnt[k] = 0
                keys.append(k)
            self.dq[q] = [keys, 0]
        self.seen = {e: {} for e in self.E}
        self.nops = 0

    def _waits(self, e, reads, writes, extra=()):
        need = {}

        def add(rec):
            if rec is None:
                return
            k, c = rec
            if need.get(k, 0) < c:
                need[k] = c

        for r in reads:
            add(r.lw)
        for w in writes:
            add(w.lw)
            for k, c in w.rd.items():
                add((k, c))
        for rec in extra:
            add(rec)
        eng = self.E[e]
        seen = self.seen[e]
        for k, c in need.items():
            if e == "pe" and k == "pe":
                continue
            if seen.get(k, 0) >= c:
                continue
            eng.wait_ge(self.sems[k], c)
            seen[k] = c

    def _mark(self, rec, reads, writes):
        k, c = rec
        for r in reads:
            if r.rd.get(k, 0) < c:
                r.rd[k] = c
        for w in writes:
            w.lw = rec
            w.rd = {}

    def op(self, e, fn, reads=(), writes=()):
        self._waits(e, reads, writes)
        inst = fn(self.E[e])
        self.cnt[e] += 1
        inst.then_inc(self.sems[e], 1)
        self._mark((e, self.cnt[e]), reads, writes)
        self.nops += 1

    def dma(self, q, out, in_, reads=(), writes=(), **kw):
        keys, i = self.dq[q]
        k = keys[i]
        self.dq[q][1] = (i + 1) % len(keys)
        extra = [(k, self.cnt[k])] if self.cnt[k] else []
        self._waits(q, reads, writes, extra)
        self.E[q].dma_start(out=out, in_=in_, **kw).then_inc(self.sems[k], 16)
        self.cnt[k] += 16
        self._mark((k, self.cnt[k]), reads, writes)
        self.nops += 1

    def barrier(self):
        for e in self.E:
            for k, c in self.cnt.items():
                if c == 0 or (e == "pe" and k == "pe"):
                    continue
                if self.seen[e].get(k, 0) >= c:
                    continue
                self.E[e].wait_ge(self.sems[k], c)
                self.seen[e][k] = c

    def finish(self):
        self.barrier()


def bcast_rows(t, off, n, parts=128):
    return bass.AP(t, off, [[0, parts], [1, n]])


def build(cfg):
    kb = KB()
    nc = kb.nc
    NSEQ = cfg.get("nseq", 2)
    NLAY = cfg.get("nlay", 2)
    mixers = cfg.get("mixers", ("gla", "rwkv", "nsa"))
    dumps = cfg.get("dumps", ())
    inject_O = cfg.get("inject_O", False)

    def dram_in(name, shape):
        return nc.dram_tensor(name, list(shape), F32, kind="ExternalInput")

    x_d = dram_in("x", [2, S, D])
    w_in = dram_in("w_in", [L, D, 3400])
    w_vres = dram_in("w_in_vres", [1, D, 32])
    w_nsa = dram_in("w_nsa", [L, D, 2200])
    gla_w_a2 = dram_in("gla_w_a2", [L, 16, 256])
    rwkv_w2 = dram_in("rwkv_w2", [L, 64, 256])
    rwkv_a2 = dram_in("rwkv_a2", [L, 64, 256])
    rwkv_v2 = dram_in("rwkv_v2", [1, 32, 256])
    rwkv_g2 = dram_in("rwkv_g2", [L, 160, 256])
    nsa_wk1 = dram_in("nsa_wk1", [L, 2048, 256])
    nsa_wk2 = dram_in("nsa_wk2", [L, 256, 64])
    nsa_wv1 = dram_in("nsa_wv1", [L, 2048, 256])
    nsa_wv2 = dram_in("nsa_wv2", [L, 256, 64])
    nsa_posT = dram_in("nsa_posT", [L, 2, 64, 32])
    w_out = dram_in("w_out", [L, D, D])
    w_gate = dram_in("ffn_w_gate", [L, D, FF])
    w_up = dram_in("ffn_w_up", [L, D, FF])
    w_down = dram_in("ffn_w_down", [L, FF, D])
    ptab = dram_in("ptab", [L, 128, 32])
    rowtab = dram_in("rowtab", [L, 5120])
    c_ident = dram_in("c_ident", [128, 128])
    c_caus = dram_in("c_caus", [128, 128])
    c_rope = dram_in("c_rope", [2, 128, S])
    c_cmpmask = dram_in("c_cmpmask", [128, S])
    c_onehot = dram_in("c_onehot", [32, S])
    c_selm = dram_in("c_selm", [2, 128, NT, 32])
    c_ovl = dram_in("c_ovl", [128, 32])
    c_blk = dram_in("c_blk", [128, 128])
    c_hsel = dram_in("c_hsel", [128, 2])
    c_rwmask = dram_in("c_rwmask", [64, 5, 64])
    if inject_O:
        dbg_OT = dram_in("dbg_OT", [128, 8, S])
    out_d = nc.dram_tensor("out", [2, S, D], F32, kind="ExternalOutput")
    xres = [nc.dram_tensor("xres%d" % i, [S, D], F32, kind="Internal") for i in range(2)]
    xres_r = [regs(NT) for _ in range(2)]
    vfirst_d = nc.dram_tensor("vfirst", [2, 128, S], F32, kind="Internal")
    vfirst_r = regs(2)
    dump_d = {}
    for name, shape in dumps:
        dump_d[name] = nc.dram_tensor("dump_" + name, list(shape), F32, kind="ExternalOutput")

    XT = nc.alloc_sbuf_tensor("XT", [128, 8, S], BF16)
    XT_r = regs(NT)
    ident = nc.alloc_sbuf_tensor("ident", [128, 128], BF16)
    identf = nc.alloc_sbuf_tensor("identf", [128, 128], F32)
    caus = nc.alloc_sbuf_tensor("caus", [128, 4, 128], F32)
    ptb = nc.alloc_sbuf_tensor("ptb", [128, L, 32], F32)
    ptn = nc.alloc_sbuf_tensor("ptn", [128, L, 32], F32)
    c_r = Reg()
    for l in range(L):
        kb.dma("sp", ptb[:, l, :], ptab.ap()[l], writes=[c_r])
    kb.dma("pool", ident[:], c_ident.ap(), writes=[c_r])
    kb.dma("sp", identf[:], c_ident.ap(), writes=[c_r])
    for i in range(4):
        kb.dma("sp", caus[:, i, :], c_caus.ap(), writes=[c_r])

    def dram16(name, shape):
        return nc.dram_tensor(name, list(shape), BF16, kind="Internal")

    conv_list = [(w_in, [L, D, 3400]), (w_nsa, [L, D, 2200]), (w_out, [L, D, D]), (w_gate, [L, D, FF]), (w_up, [L, D, FF]),
                 (w_down, [L, FF, D]), (nsa_wk1, [L, 2048, 256]), (nsa_wv1, [L, 2048, 256])]
    w16 = {}
    w16_r = Reg()
    CH = 2816
    with nc.sbuf_tensor("cv_f", [128, 3, CH], F32) as cvf, nc.sbuf_tensor("cv_b", [128, 3, CH], BF16) as cvb:
        cvf_r, cvb_r = regs(3), regs(3)
        job = 0
        for src, shape in conv_list:
            dst = dram16(src.name + "_16", shape)
            w16[src.name] = dst
            tot = 1
            for d_ in shape:
                tot *= d_
            per = tot // 128
            assert per * 128 == tot
            for c0 in range(0, per, CH):
                n = min(CH, per - c0)
                i = job % 3
                kb.dma("sp", cvf[:, i, 0:n], bass.AP(src, c0, [[per, 128], [1, n]]), writes=[cvf_r[i]])
                eng = "act"
                if eng == "act":
                    kb.op("act", lambda a: a.activation(out=cvb[:, i, 0:n], in_=cvf[:, i, 0:n], func=AF.Copy),
                          reads=[cvf_r[i]], writes=[cvb_r[i]])
                else:
                    kb.op(eng, lambda v: v.tensor_scalar(out=cvb[:, i, 0:n], in0=cvf[:, i, 0:n], scalar1=1.0, scalar2=None, op0=ALU.mult),
                          reads=[cvf_r[i]], writes=[cvb_r[i]])
                kb.dma("act", bass.AP(dst, c0, [[per, 128], [1, n]]), cvb[:, i, 0:n], reads=[cvb_r[i]], writes=[w16_r])
                job += 1
        kb.barrier()
    w_in16, w_nsa16, w_out16 = w16["w_in"], w16["w_nsa"], w16["w_out"]
    w_gate16, w_up16, w_down16 = w16["ffn_w_gate"], w16["ffn_w_up"], w16["ffn_w_down"]
    wk1_16, wv1_16 = w16["nsa_wk1"], w16["nsa_wv1"]

    PS = [nc.alloc_psum_tensor("ps%d" % i, [128, 512], F32) for i in range(6)]
    PS_r = regs(6)
    PB = [nc.alloc_psum_tensor("pb%d" % i, [128, 1024], BF16) for i in range(2)]
    PB_r = regs(2)

    _uid = [0]

    def sb(name, shape, dt):
        _uid[0] += 1
        return nc.sbuf_tensor("%s_%d" % (name, _uid[0]), list(shape), dt)

    def proj_fm(ps_ap, Wt, c0, M, tok0, N, wreads, treads, pw, kparts=8):
        def fn(pe):
            inst = None
            for k in range(kparts):
                inst = pe.matmul(ps_ap, lhsT=Wt[:, k, c0:c0 + M], rhs=XT[:, k, tok0:tok0 + N],
                                 start=(k == 0), stop=(k == kparts - 1))
            return inst
        kb.op("pe", fn, reads=list(wreads) + list(treads), writes=[pw])

    def proj_tm(ps_ap, Wt, c0, N, tt, wreads, pw):
        def fn(pe):
            inst = None
            for k in range(8):
                inst = pe.matmul(ps_ap, lhsT=XT[:, k, tt * 128:(tt + 1) * 128], rhs=Wt[:, k, c0:c0 + N],
                                 start=(k == 0), stop=(k == 7))
            return inst
        kb.op("pe", fn, reads=list(wreads) + [XT_r[tt]], writes=[pw])

    def load_w(Wt, src3, ncols, wreg, q="sp", chunk=None):
        for k in range(8):
            kb.dma(q, Wt[:, k, 0:ncols], src3[k * 128:(k + 1) * 128, 0:ncols], reads=[w16_r], writes=[wreg])

    xb = [nc.alloc_sbuf_tensor("xb%d" % i, [128, D], BF16) for i in range(2)]
    xb_r = regs(2)
    xb_i = [0]

    def make_xT(src_ap, src_reg, tt):
        i = xb_i[0]
        xb_i[0] ^= 1
        kb.op("act", lambda a: a.activation(out=xb[i][:], in_=src_ap, func=AF.Copy),
              reads=[src_reg], writes=[xb_r[i]])
        pb = PB[i]

        def fn(pe):
            inst = None
            for k in range(8):
                inst = pe.transpose(out=pb[:, k * 128:(k + 1) * 128], in_=xb[i][:, k * 128:(k + 1) * 128],
                                    identity=ident[:])
            return inst
        kb.op("pe", fn, reads=[xb_r[i], c_r], writes=[PB_r[i]])
        kb.op("dve", lambda v: v.tensor_copy(out=XT[:, :, tt * 128:(tt + 1) * 128],
                                             in_=pb[:, :].rearrange("p (k t) -> p k t", k=8)),
              reads=[PB_r[i]], writes=[XT_r[tt]])

    def layer_norm(xt_ap, xreg, lnw_ap, lnb_ap, lnreg, st, st_r):
        kb.op("dve", lambda v: v.bn_stats(out=st[:, 0:6], in_=xt_ap[:, 0:512]), reads=[xreg], writes=[st_r])
        kb.op("dve", lambda v: v.bn_stats(out=st[:, 6:12], in_=xt_ap[:, 512:1024]), reads=[xreg, st_r], writes=[st_r])
        kb.op("dve", lambda v: v.bn_aggr(out=st[:, 12:14], in_=st[:, 0:12]), reads=[st_r], writes=[st_r])
        kb.op("act", lambda a: a.activation(out=st[:, 14:15], in_=st[:, 13:14], func=AF.Sqrt, bias=LN_EPS, scale=1.0),
              reads=[st_r], writes=[st_r])
        kb.op("dve", lambda v: v.reciprocal(out=st[:, 15:16], in_=st[:, 14:15]), reads=[st_r], writes=[st_r])
        kb.op("dve", lambda v: v.scalar_tensor_tensor(out=st[:, 16:17], in0=st[:, 12:13], scalar=-1.0, in1=st[:, 15:16],
                                                      op0=ALU.mult, op1=ALU.mult), reads=[st_r], writes=[st_r])
        kb.op("act", lambda a: a.activation(out=xt_ap, in_=xt_ap, func=AF.Identity, bias=st[:, 16:17], scale=st[:, 15:16]),
              reads=[st_r, xreg], writes=[xreg])
        kb.op("dve", lambda v: v.tensor_tensor(out=xt_ap, in0=xt_ap, in1=lnw_ap, op=ALU.mult), reads=[xreg, lnreg], writes=[xreg])
        kb.op("dve", lambda v: v.tensor_tensor(out=xt_ap, in0=xt_ap, in1=lnb_ap, op=ALU.add), reads=[xreg, lnreg], writes=[xreg])

    def dump(name, sb_ap, reg, idx=None):
        if name in dump_d:
            dst = dump_d[name].ap() if idx is None else dump_d[name].ap()[idx]
            kb.dma("sp", dst, sb_ap, reads=[reg])

    kb.op("dve", lambda v: v.tensor_scalar(out=ptn[:, :, :], in0=ptb[:, :, :], scalar1=-1.0, scalar2=None, op0=ALU.mult),
          reads=[c_r], writes=[c_r])
    kb.op("dve", lambda v: v.tensor_scalar(out=ptn[:, :, 2:13], in0=ptb[:, :, 2:13], scalar1=-1.0, scalar2=1.0,
                                           op0=ALU.mult, op1=ALU.add), reads=[c_r], writes=[c_r])

    def gla_phase(sq, l, OT, OT_r):
        with ExitStack() as es:
            A = lambda n_, s_, d_: es.enter_context(sb(n_, s_, d_))
            Wg = A("Wg", [128, 8, 1152], BF16)
            wa2 = A("wa2", [16, 256], BF16)
            gln = A("gln", [128, 2, 256], F32)
            alrT = A("alrT", [16, 512], BF16)
            t1 = A("gt1", [128, 512], F32)
            t2 = A("gt2", [128, 512], F32)
            EB = A("EB", [128, 2, 512], F32)
            QTl = A("gQT", [128, 2, 2, 512], BF16)
            KTl = A("gKT", [128, 2, 512], BF16)
            Ktok = A("gKtok", [128, 4, 256], BF16)
            V = A("gV", [128, 4, 256], BF16)
            SG = A("gSG", [128, 4, 256], F32)
            AT = A("gAT", [128, 4, 128], BF16)
            St = A("gSt", [128, 2, 128], F32)
            tmpS = A("gtmpS", [128, 128], F32)
            blkm = A("gblk", [128, 128], F32)
            Sb = A("gSb", [128, 2, 128], BF16)
            rmask = A("grm", [128, 512], F32)
            ow = A("gow", [128, 256], F32)
            ow2 = A("gow2", [128, 256], F32)
            ob = A("gob", [128, 256], BF16)
            st = A("gst", [128, 32], F32)
            W_r, p_r, alr_r, t1_r, t2_r = Reg(), Reg(), Reg(), Reg(), Reg()
            EB_r, QT_r, KT_r = regs(2), regs(2), regs(2)
            Ktok_r, V_r, SG_r, AT_r, St_r, Sb_r = Reg(), regs(4), regs(4), Reg(), Reg(), Reg()
            ow_r, ow2_r, ob_r, st_r = Reg(), Reg(), Reg(), Reg()
            gs = cfg.get("gla_stop", 99)
            load_w(Wg, w_in16.ap()[l][:, 0:1040], 1040, W_r)
            kb.dma("pool", wa2[:, :], gla_w_a2.ap()[l], writes=[p_r])
            kb.dma("sp", gln[:, 0, :], bcast_rows(rowtab, l * 5120 + 0, 256), writes=[p_r])
            kb.dma("sp", gln[:, 1, :], bcast_rows(rowtab, l * 5120 + 256, 256), writes=[p_r])
            kb.op("pool", lambda g: g.memset(rmask[:, :], 1.0), writes=[p_r])
            kb.op("pool", lambda g: g.memset(rmask[:, :].rearrange("p (c t) -> p c t", c=4)[:, :, 0:1], 0.0), writes=[p_r])
            kb.op("pool", lambda g: g.memset(St[:, :, :], 0.0), writes=[St_r])
            kb.op("pool", lambda g: g.memset(QTl[:, :, :, :], 0.0), writes=QT_r)
            kb.dma("sp", blkm[:, :], c_blk.ap(), writes=[p_r])
            tmpS_r = Reg()
            kb.op("pool", lambda g: g.memset(Sb[:, :, :], 0.0), writes=[Sb_r])
            for mt in range(4 if gs > 0 else 0):
                tok0 = mt * 512
                xr = XT_r[mt * 4:mt * 4 + 4]
                proj_fm(PS[0][0:16, :], Wg, 1024, 16, tok0, 512, [W_r], xr, PS_r[0])
                kb.op("act", lambda a: a.activation(out=alrT[:, :], in_=PS[0][0:16, :], func=AF.Copy),
                      reads=[PS_r[0]], writes=[alr_r])
                for hp in range(2):
                    kb.op("pe", lambda pe: pe.matmul(PS[1][:, :], lhsT=wa2[0:16, hp * 128:(hp + 1) * 128], rhs=alrT[0:16, :],
                                                     start=True, stop=True), reads=[p_r, alr_r], writes=[PS_r[1]])
                    kb.op("act", lambda a: a.activation(out=t1[:, :], in_=PS[1][:, :], func=AF.Exp, scale=-1.0,
                                                        bias=ptn[:, l, hp:hp + 1]), reads=[PS_r[1], c_r], writes=[t1_r])
                    kb.op("act", lambda a: a.activation(out=t1[:, :], in_=t1[:, :], func=AF.Ln, bias=1.0, scale=1.0),
                          reads=[t1_r], writes=[t1_r])
                    kb.op("dve", lambda v: v.tensor_tensor_scan(out=t2[:, :], data0=rmask[:, :], data1=t1[:, :], initial=0.0,
                                                                op0=ALU.mult, op1=ALU.add), reads=[t1_r, p_r], writes=[t2_r])
                    kb.op("act", lambda a: a.activation(out=EB[:, hp, :], in_=t2[:, :], func=AF.Exp, scale=-1.0 / 16.0),
                          reads=[t2_r], writes=[EB_r[hp]])
                    kb.op("act", lambda a: a.activation(out=t1[:, :], in_=t2[:, :], func=AF.Exp, scale=1.0 / 16.0),
                          reads=[t2_r], writes=[t1_r])
                    proj_fm(PS[2][:, :], Wg, hp * 128, 128, tok0, 512, [W_r], xr, PS_r[2])
                    for hh in range(2):
                        pr = slice(hh * 64, hh * 64 + 64)
                        kb.op("dve", lambda v: v.scalar_tensor_tensor(out=QTl[pr, hp, hh, :], in0=PS[2][pr, :], scalar=0.125,
                                                                      in1=EB[pr, hp, :], op0=ALU.mult, op1=ALU.mult),
                              reads=[PS_r[2], EB_r[hp]], writes=[QT_r[hp]])
                    proj_fm(PS[3][:, :], Wg, 256 + hp * 128, 128, tok0, 512, [W_r], xr, PS_r[3])
                    kb.op("dve", lambda v: v.tensor_tensor(out=KTl[:, hp, :], in0=PS[3][:, :], in1=t1[:, :], op=ALU.mult),
                          reads=[PS_r[3], t1_r], writes=[KT_r[hp]])

                    def fn(pe):
                        inst = None
                        for j in range(4):
                            inst = pe.transpose(out=PB[0][:, j * 128:(j + 1) * 128], in_=KTl[:, hp, j * 128:(j + 1) * 128],
                                                identity=ident[:])
                        return inst
                    kb.op("pe", fn, reads=[KT_r[hp], c_r], writes=[PB_r[0]])
                    kb.op("dve", lambda v: v.tensor_copy(out=Ktok[:, :, hp * 128:(hp + 1) * 128],
                                                         in_=PB[0][:, 0:512].rearrange("p (j c) -> p j c", j=4)),
                          reads=[PB_r[0]], writes=[Ktok_r])
                for j in range(4 if gs > 1 else 0):
                    tt = mt * 4 + j
                    proj_tm(PS[4][:, 0:256], Wg, 512, 256, tt, [W_r], PS_r[4])
                    proj_tm(PS[5][:, 0:256], Wg, 768, 256, tt, [W_r], PS_r[5])
                    if cfg.get("gv", 3) >= 2:
                        kb.op("dve", lambda v: v.tensor_scalar(out=V[:, j, :], in0=PS[4][:, 0:256], scalar1=1.0, scalar2=None, op0=ALU.mult),
                              reads=[PS_r[4]], writes=[V_r[j]])
                    if cfg.get("gv", 3) >= 3:
                        kb.op("act", lambda a: a.activation(out=SG[:, j, :], in_=PS[5][:, 0:256], func=AF.Silu),
                              reads=[PS_r[5]], writes=[SG_r[j]])
                for j in range(4 if gs > 2 else 0):
                    tt = mt * 4 + j
                    cc = slice(j * 128, (j + 1) * 128)

                    def fn(pe):
                        inst = None
                        for h in range(4):
                            hp, hh = h // 2, h % 2
                            inst = pe.matmul(PS[0][:, h * 128:(h + 1) * 128], lhsT=KTl[:, hp, cc], rhs=QTl[:, hp, hh, cc],
                                             start=True, stop=True)
                        return inst
                    kb.op("pe", fn, reads=QT_r + KT_r, writes=[PS_r[0]])
                    kb.op("dve", lambda v: v.tensor_tensor(out=AT[:, :, :], in0=PS[0][:, :].rearrange("p (h t) -> p h t", h=4),
                                                          in1=caus[:, :, :], op=ALU.mult), reads=[PS_r[0], c_r], writes=[AT_r])

                    def fn(pe):
                        inst = None
                        for h in range(4):
                            hp, hh = h // 2, h % 2
                            pe.matmul(PS[1][:, h * 64:(h + 1) * 64], lhsT=AT[:, h, :], rhs=V[:, j, h * 64:(h + 1) * 64],
                                      start=True, stop=False)
                            inst = pe.matmul(PS[1][:, h * 64:(h + 1) * 64], lhsT=QTl[:, hp, hh, cc], rhs=Sb[:, hp, hh * 64:(hh + 1) * 64],
                                             start=False, stop=True)
                        return inst
                    if gs <= 3:
                        continue
                    kb.op("pe", fn, reads=[AT_r, V_r[j], Sb_r] + QT_r, writes=[PS_r[1]])

                    def fn(pe):
                        inst = None
                        for hp in range(2):
                            inst = pe.matmul(PS[2][:, hp * 128:(hp + 1) * 128], lhsT=Ktok[:, j, hp * 128:(hp + 1) * 128],
                                             rhs=V[:, j, hp * 128:(hp + 1) * 128], start=True, stop=True)
                        return inst
                    if gs <= 4:
                        continue
                    kb.op("pe", fn, reads=[Ktok_r, V_r[j]], writes=[PS_r[2]])
                    for hp in range(2):
                        ee = EB[:, hp, j * 128 + 127:j * 128 + 128]
                        kb.op("dve", lambda v: v.scalar_tensor_tensor(out=tmpS[:, :], in0=PS[2][:, hp * 128:(hp + 1) * 128], scalar=ee,
                                                                      in1=blkm[:, :], op0=ALU.mult, op1=ALU.mult),
                              reads=[PS_r[2], EB_r[hp], p_r], writes=[tmpS_r])
                        kb.op("dve", lambda v: v.scalar_tensor_tensor(out=St[:, hp, :], in0=St[:, hp, :], scalar=ee, in1=tmpS[:, :],
                                                                      op0=ALU.mult, op1=ALU.add),
                              reads=[St_r, tmpS_r, EB_r[hp]], writes=[St_r])
                    kb.op("act", lambda a: a.activation(out=Sb[:, :, :], in_=St[:, :, :], func=AF.Copy), reads=[St_r], writes=[Sb_r])
                    if gs <= 5:
                        continue
                    head_norm_gate(PS[1][:, 0:256], PS_r[1], ow, ow_r, ow2, ow2_r, st, st_r, gln, p_r, LN_EPS)
                    kb.op("dve", lambda v: v.tensor_tensor(out=ob[:, :], in0=ow[:, :], in1=SG[:, j, :], op=ALU.mult),
                          reads=[ow_r, SG_r[j]], writes=[ob_r])
                    if gs <= 6:
                        continue
                    out_to_OT(ob, ob_r, 128, OT, OT_r, 0, tt * 128)
            kb.barrier()

    def head_norm_gate(ps_ap, ps_r, ow, ow_r, ow2, ow2_r, st, st_r, gln, gln_r, eps, P=128):
        v4 = lambda ap: ap.rearrange("p (h d) -> p h d", h=4)
        kb.op("act", lambda a: a.activation(out=ow[0:P, :], in_=ps_ap, func=AF.Copy), reads=[ps_r], writes=[ow_r])
        kb.op("act", lambda a: a.activation(out=ow2[0:P, :], in_=ow[0:P, :], func=AF.Square), reads=[ow_r], writes=[ow2_r])
        kb.op("dve", lambda v: v.reduce_sum(out=st[0:P, 0:4], in_=v4(ow[0:P, :]), axis=AX.X), reads=[ow_r], writes=[st_r])
        kb.op("dve", lambda v: v.reduce_sum(out=st[0:P, 4:8], in_=v4(ow2[0:P, :]), axis=AX.X), reads=[ow2_r, st_r], writes=[st_r])
        kb.op("dve", lambda v: v.tensor_scalar(out=st[0:P, 8:16], in0=st[0:P, 0:8], scalar1=1.0 / 64.0, scalar2=None, op0=ALU.mult),
              reads=[st_r], writes=[st_r])
        kb.op("dve", lambda v: v.tensor_tensor(out=st[0:P, 16:20], in0=st[0:P, 8:12], in1=st[0:P, 8:12], op=ALU.mult),
              reads=[st_r], writes=[st_r])
        kb.op("dve", lambda v: v.tensor_tensor(out=st[0:P, 20:24], in0=st[0:P, 12:16], in1=st[0:P, 16:20], op=ALU.subtract),
              reads=[st_r], writes=[st_r])
        kb.op("act", lambda a: a.activation(out=st[0:P, 24:28], in_=st[0:P, 20:24], func=AF.Sqrt, bias=eps, scale=1.0),
              reads=[st_r], writes=[st_r])
        kb.op("dve", lambda v: v.reciprocal(out=st[0:P, 28:32], in_=st[0:P, 24:28]), reads=[st_r], writes=[st_r])
        for h in range(4):
            kb.op("dve", lambda v: v.tensor_scalar(out=ow[0:P, h * 64:(h + 1) * 64], in0=ow[0:P, h * 64:(h + 1) * 64],
                                                   scalar1=st[0:P, 8 + h:9 + h], scalar2=st[0:P, 28 + h:29 + h],
                                                   op0=ALU.subtract, op1=ALU.mult), reads=[ow_r, st_r], writes=[ow_r])
        kb.op("dve", lambda v: v.tensor_tensor(out=ow[0:P, :], in0=ow[0:P, :], in1=gln[0:P, 0, :], op=ALU.mult),
              reads=[ow_r, gln_r], writes=[ow_r])
        kb.op("dve", lambda v: v.tensor_tensor(out=ow[0:P, :], in0=ow[0:P, :], in1=gln[0:P, 1, :], op=ALU.add),
              reads=[ow_r, gln_r], writes=[ow_r])

    def out_to_OT(ob, ob_r, P, OT, OT_r, k0, tokc0, ncols=256):
        nk = ncols // 128

        def fn(pe):
            inst = None
            for kk in range(nk):
                inst = pe.transpose(out=PB[1][:, kk * 128:kk * 128 + P], in_=ob[0:P, kk * 128:(kk + 1) * 128],
                                    identity=ident[0:P, 0:P])
            return inst
        kb.op("pe", fn, reads=[ob_r, c_r], writes=[PB_r[1]])
        tr = OT_r[tokc0 // 128]
        kb.op("act", lambda a: a.activation(out=OT[:, k0:k0 + nk, tokc0:tokc0 + P],
                                            in_=PB[1][:, 0:nk * 128].rearrange("p (k t) -> p k t", k=nk)[:, :, 0:P], func=AF.Copy),
              reads=[PB_r[1]], writes=[tr])

    RWT = [(0, 128), (128, 128), (256, 128), (384, 128), (512, 128), (640, 128), (768, 64), (832, 64),
           (896, 128), (1024, 32), (1056, 32)]
    C0 = float(math.exp(-0.5))

    def rwkv_phase(sq, l, OT, OT_r):
        MT = 256
        NCH = MT // 64
        NU = NCH * 4
        with ExitStack() as es:
            A = lambda n_, s_, d_: es.enter_context(sb(n_, s_, d_))
            Wr = A("Wr", [128, 8, 1088], BF16)
            w2 = A("rw2", [64, 256], BF16)
            a2 = A("ra2", [64, 256], BF16)
            v2 = A("rv2", [32, 256], BF16)
            g2a = A("rg2a", [128, 256], BF16)
            g2b = A("rg2b", [128, 256], BF16)
            rln = A("rln", [128, 2, 256], F32)
            blkm = A("rblk", [128, 128], F32)
            blkf = A("rblkf", [128, 128], F32)
            hselb = A("rhsel", [128, 2], BF16)
            rmask = A("rrm", [128, MT], F32)
            amask = A("ramask", [64, 5, 64], F32)
            xs = [A("rxs%d" % i, [128, MT], F32) for i in range(6)]
            tt_ = [A("rt%d" % i, [128, MT], F32) for i in range(8)]
            Gam = A("rGam", [128, 2, MT], F32)
            ATz = A("rATz", [128, 2, 2, MT], BF16)
            RTz = A("rRTz", [128, 2, 2, MT], BF16)
            BTt = A("rBT", [128, 2, MT], BF16)
            KTt = A("rKT", [128, 2, MT], BF16)
            rkr = A("rrkr", [128, 2, MT], BF16)
            TW = A("rTW", [64, MT], BF16)
            AL = A("rAL", [64, MT], BF16)
            SGL = A("rSGL", [128, MT], BF16)
            SGL2 = A("rSGL2", [128, MT], BF16)
            VR = A("rVR", [32, MT], BF16)
            vb = A("rvb", [128, MT], BF16)
            Vtok = A("rVtok", [128, NCH, 256], BF16)
            Btok = A("rBtok", [128, NCH, 256], BF16)
            Ktok = A("rKtok", [128, NCH, 256], BF16)
            gtok = A("rgtok", [64, NCH, 256], F32)
            cb = A("rcb", [64, NCH, 4], F32)
            MN = [A("rMN%d" % i, [64, NU, 2, 64], F32) for i in range(2)]
            Pm = A("rP", [64, NU, 64], F32)
            A3 = A("rA3", [128, NU, 3, 64], BF16)
            Rs = A("rRs", [64, 256], F32)
            Ub = A("rUb", [128, 256], BF16)
            Tblk = A("rTblk", [128, 2, 128], F32)
            Tb = A("rTb", [128, 2, 128], BF16)
            tmpS = A("rtmpS", [128, 128], F32)
            ow = A("row", [128, 256], F32)
            ow2 = A("row2", [128, 256], F32)
            ob = A("rob", [128, 256], BF16)
            st = A("rst", [128, 32], F32)
            W_r, p_r = Reg(), Reg()
            xs_r, t_r = regs(6), regs(8)
            Gam_r, ATz_r, RTz_r, BT_r, KT_r, rkr_r = regs(2), regs(2), regs(2), regs(2), regs(2), regs(2)
            TW_r, AL_r, SGL_r, SGL2_r, VR_r, vb_r = Reg(), Reg(), Reg(), Reg(), Reg(), Reg()
            Vtok_r, Btok_r, Ktok_r, gtok_r, cb_r = Reg(), Reg(), Reg(), Reg(), Reg()
            MN_r, P_r, A3_r, Rs_r, Ub_r, T_r, Tb_r, tmpS_r = regs(2), Reg(), Reg(), Reg(), Reg(), Reg(), Reg(), Reg()
            ow_r, ow2_r, ob_r, st_r = Reg(), Reg(), Reg(), Reg()
            rs_ = cfg.get("rw_stop", 99)
            load_w(Wr, w_in16.ap()[l][:, 1040:2096], 1056, W_r)
            if l >= 1:
                for k in range(8):
                    kb.dma("pool", Wr[:, k, 1056:1088], w_vres.ap()[l - 1][k * 128:(k + 1) * 128, :], writes=[W_r])
            kb.dma("pool", w2[:, :], rwkv_w2.ap()[l], writes=[p_r])
            kb.dma("pool", a2[:, :], rwkv_a2.ap()[l], writes=[p_r])
            if l >= 1:
                kb.dma("pool", v2[:, :], rwkv_v2.ap()[l - 1], writes=[p_r])
            kb.op("pool", lambda g: g.memset(g2b[:, :], 0.0), writes=[p_r])
            kb.dma("pool", g2a[:, :], rwkv_g2.ap()[l][0:128, :], writes=[p_r])
            kb.dma("pool", g2b[0:32, :], rwkv_g2.ap()[l][128:160, :], reads=[p_r], writes=[p_r])
            kb.dma("sp", rln[:, 0, :], bcast_rows(rowtab, l * 5120 + 512, 256), writes=[p_r])
            kb.dma("sp", rln[:, 1, :], bcast_rows(rowtab, l * 5120 + 768, 256), writes=[p_r])
            kb.dma("sp", blkm[:, :], c_blk.ap(), writes=[p_r])
            kb.dma("sp", blkf[:, :], c_blk.ap(), writes=[p_r])
            kb.dma("pool", hselb[:, :], c_hsel.ap(), writes=[p_r])
            kb.dma("sp", amask[:, :, :], c_rwmask.ap(), writes=[p_r])
            kb.op("pool", lambda g: g.memset(rmask[:, :], 1.0), writes=[p_r])
            kb.op("pool", lambda g: g.memset(rmask[:, :].rearrange("p (c t) -> p c t", c=NCH)[:, :, 0:1], 0.0), writes=[p_r])
            for tz, rr in ((Tblk, [T_r]), (Tb, [Tb_r]), (ATz, ATz_r), (RTz, RTz_r), (SGL2, [SGL2_r]), (Vtok, [Vtok_r]),
                           (Btok, [Btok_r]), (Ktok, [Ktok_r]), (A3, [A3_r]), (Ub, [Ub_r])):
                kb.op("pool", lambda g, tz=tz: g.memset(tz[:], 0.0), writes=rr)

            def shift_proj(i, tok0, dst_fn):
                c0, n = RWT[i]
                mucol = 2 + i
                proj_fm(PS[0][0:n, 0:MT], Wr, c0, n, tok0, MT, [W_r], XT_r[tok0 // 128:tok0 // 128 + MT // 128], PS_r[0])
                tp = tt_[7]
                if tok0 == 0:
                    proj_fm(PS[1][0:n, 1:MT], Wr, c0, n, 0, MT - 1, [W_r], XT_r[0:MT // 128], PS_r[1])
                    kb.op("pool", lambda g: g.memset(tp[0:n, 0:1], 0.0), writes=[t_r[7]])
                    kb.op("act", lambda a: a.activation(out=tp[0:n, 1:MT], in_=PS[1][0:n, 1:MT], func=AF.Copy,
                                                        scale=ptb[0:n, l, mucol:mucol + 1]), reads=[PS_r[1], c_r, t_r[7]], writes=[t_r[7]])
                else:
                    proj_fm(PS[1][0:n, 0:MT], Wr, c0, n, tok0 - 1, MT, [W_r],
                            XT_r[(tok0 - 1) // 128:(tok0 - 1) // 128 + MT // 128 + 1], PS_r[1])
                    kb.op("act", lambda a: a.activation(out=tp[0:n, 0:MT], in_=PS[1][0:n, 0:MT], func=AF.Copy,
                                                        scale=ptb[0:n, l, mucol:mucol + 1]), reads=[PS_r[1], c_r], writes=[t_r[7]])
                dst_ap, dst_regs = dst_fn()
                kb.op("dve", lambda v: v.scalar_tensor_tensor(out=dst_ap, in0=PS[0][0:n, 0:MT], scalar=ptn[0:n, l, mucol:mucol + 1],
                                                              in1=tp[0:n, 0:MT], op0=ALU.mult, op1=ALU.add),
                      reads=[PS_r[0], t_r[7], c_r], writes=dst_regs)

            for mt in range(S // MT if rs_ > 0 else 0):
                tok0 = mt * MT
                for i in range(6):
                    shift_proj(i, tok0, lambda i=i: (xs[i][:, :], [xs_r[i]]))
                shift_proj(6, tok0, lambda: (tt_[0][0:64, :], [t_r[0]]))
                kb.op("act", lambda a: a.activation(out=TW[:, :], in_=tt_[0][0:64, :], func=AF.Tanh), reads=[t_r[0]], writes=[TW_r])
                shift_proj(7, tok0, lambda: (AL[:, :], [AL_r]))
                shift_proj(8, tok0, lambda: (tt_[0][:, :], [t_r[0]]))
                kb.op("act", lambda a: a.activation(out=SGL[:, :], in_=tt_[0][:, :], func=AF.Sigmoid), reads=[t_r[0]], writes=[SGL_r])
                shift_proj(9, tok0, lambda: (tt_[0][0:32, :], [t_r[0]]))
                kb.op("act", lambda a: a.activation(out=SGL2[0:32, :], in_=tt_[0][0:32, :], func=AF.Sigmoid), reads=[t_r[0]], writes=[SGL2_r])
                if l >= 1:
                    shift_proj(10, tok0, lambda: (VR[:, :], [VR_r]))
                if rs_ <= 1:
                    continue
                for hp in range(2):
                    rT, kT, vT = xs[hp], xs[2 + hp], xs[4 + hp]
                    rT_r, kT_r, vT_r = xs_r[hp], xs_r[2 + hp], xs_r[4 + hp]
                    t1, t2, t3, t4, t5, t6, t7 = tt_[0:7]
                    kb.op("pe", lambda pe: pe.matmul(PS[2][:, 0:MT], lhsT=w2[:, hp * 128:(hp + 1) * 128], rhs=TW[:, :], start=True, stop=True),
                          reads=[p_r, TW_r], writes=[PS_r[2]])
                    kb.op("act", lambda a: a.activation(out=t1[:, :], in_=PS[2][:, 0:MT], func=AF.Sigmoid, bias=ptb[:, l, 13 + hp:14 + hp]),
                          reads=[PS_r[2], c_r], writes=[t_r[0]])
                    kb.op("dve", lambda v: v.tensor_tensor_scan(out=t2[:, :], data0=rmask[:, :], data1=t1[:, :], initial=0.0,
                                                                op0=ALU.mult, op1=ALU.add), reads=[t_r[0], p_r], writes=[t_r[1]])
                    kb.op("act", lambda a: a.activation(out=Gam[:, hp, :], in_=t2[:, :], func=AF.Exp, scale=-C0), reads=[t_r[1]], writes=[Gam_r[hp]])
                    kb.op("act", lambda a: a.activation(out=t3[:, :], in_=t2[:, :], func=AF.Exp, scale=C0), reads=[t_r[1]], writes=[t_r[2]])
                    kb.op("dve", lambda v: v.tensor_tensor(out=t4[:, :], in0=t2[:, :], in1=t1[:, :], op=ALU.subtract),
                          reads=[t_r[0], t_r[1]], writes=[t_r[3]])
                    kb.op("act", lambda a: a.activation(out=t4[:, :], in_=t4[:, :], func=AF.Exp, scale=-C0), reads=[t_r[3]], writes=[t_r[3]])
                    kb.op("pe", lambda pe: pe.matmul(PS[3][:, 0:MT], lhsT=a2[:, hp * 128:(hp + 1) * 128], rhs=AL[:, :], start=True, stop=True),
                          reads=[p_r, AL_r], writes=[PS_r[3]])
                    kb.op("act", lambda a: a.activation(out=t5[:, :], in_=PS[3][:, 0:MT], func=AF.Sigmoid, bias=ptb[:, l, 15 + hp:16 + hp]),
                          reads=[PS_r[3], c_r], writes=[t_r[4]])
                    kb.op("dve", lambda v: v.tensor_scalar(out=t6[:, :], in0=kT[:, :], scalar1=ptb[:, l, 19 + hp:20 + hp], scalar2=None, op0=ALU.mult),
                          reads=[kT_r, c_r], writes=[t_r[5]])
                    kb.op("act", lambda a: a.activation(out=t7[:, :], in_=t6[:, :], func=AF.Square), reads=[t_r[5]], writes=[t_r[6]])
                    kb.op("pe", lambda pe: pe.matmul(PS[4][:, 0:MT], lhsT=blkf[:, :], rhs=t7[:, :], start=True, stop=True),
                          reads=[p_r, t_r[6]], writes=[PS_r[4]])
                    kb.op("act", lambda a: a.activation(out=t7[:, :], in_=PS[4][:, 0:MT], func=AF.Sqrt), reads=[PS_r[4], t_r[6]], writes=[t_r[6]])
                    kb.op("dve", lambda v: v.tensor_scalar(out=t7[:, :], in0=t7[:, :], scalar1=1e-12, scalar2=None, op0=ALU.max),
                          reads=[t_r[6]], writes=[t_r[6]])
                    kb.op("dve", lambda v: v.reciprocal(out=t7[:, :], in_=t7[:, :]), reads=[t_r[6]], writes=[t_r[6]])
                    kb.op("dve", lambda v: v.tensor_tensor(out=t6[:, :], in0=t6[:, :], in1=t7[:, :], op=ALU.mult),
                          reads=[t_r[5], t_r[6]], writes=[t_r[5]])
                    kb.op("dve", lambda v: v.tensor_scalar(out=t7[:, :], in0=t5[:, :], scalar1=-1.0, scalar2=ptb[:, l, 21 + hp:22 + hp],
                                                           op0=ALU.add, op1=ALU.mult), reads=[t_r[4], t_r[6], c_r], writes=[t_r[6]])
                    kb.op("dve", lambda v: v.scalar_tensor_tensor(out=t7[:, :], in0=t7[:, :], scalar=1.0, in1=kT[:, :], op0=ALU.add, op1=ALU.mult),
                          reads=[t_r[6], kT_r], writes=[t_r[6]])
                    for hh in range(2):
                        pr = slice(hh * 64, hh * 64 + 64)
                        kb.op("dve", lambda v: v.scalar_tensor_tensor(out=ATz[pr, hp, hh, :], in0=t6[pr, :], scalar=-1.0, in1=t4[pr, :],
                                                                      op0=ALU.mult, op1=ALU.mult), reads=[t_r[5], t_r[3]], writes=[ATz_r[hp]])
                        kb.op("dve", lambda v: v.tensor_tensor(out=RTz[pr, hp, hh, :], in0=rT[pr, :], in1=Gam[pr, hp, :], op=ALU.mult),
                              reads=[rT_r, Gam_r[hp]], writes=[RTz_r[hp]])
                    kb.op("dve", lambda v: v.tensor_tensor(out=t1[:, :], in0=t6[:, :], in1=t5[:, :], op=ALU.mult),
                          reads=[t_r[5], t_r[4], t_r[0]], writes=[t_r[0]])
                    kb.op("dve", lambda v: v.tensor_tensor(out=BTt[:, hp, :], in0=t1[:, :], in1=t3[:, :], op=ALU.mult),
                          reads=[t_r[0], t_r[2]], writes=[BT_r[hp]])
                    kb.op("dve", lambda v: v.tensor_tensor(out=KTt[:, hp, :], in0=t7[:, :], in1=t3[:, :], op=ALU.mult),
                          reads=[t_r[6], t_r[2]], writes=[KT_r[hp]])
                    kb.op("dve", lambda v: v.scalar_tensor_tensor(out=rkr[:, hp, :], in0=rT[:, :], scalar=ptb[:, l, 23 + hp:24 + hp], in1=t7[:, :],
                                                                  op0=ALU.mult, op1=ALU.mult), reads=[rT_r, t_r[6], c_r], writes=[rkr_r[hp]])
                    if l == 0:
                        kb.dma("sp", vfirst_d.ap()[hp, :, sq * 0 + tok0:tok0 + MT], vT[:, :], reads=[vT_r], writes=[vfirst_r[hp]])
                    else:
                        kb.op("pe", lambda pe: pe.matmul(PS[5][:, 0:MT], lhsT=v2[:, hp * 128:(hp + 1) * 128], rhs=VR[:, :], start=True, stop=True),
                              reads=[p_r, VR_r], writes=[PS_r[5]])
                        kb.op("act", lambda a: a.activation(out=t1[:, :], in_=PS[5][:, 0:MT], func=AF.Sigmoid, bias=ptb[:, l, 17 + hp:18 + hp]),
                              reads=[PS_r[5], c_r, t_r[0]], writes=[t_r[0]])
                        kb.dma("sp", t2[:, :], vfirst_d.ap()[hp, :, tok0:tok0 + MT], reads=[vfirst_r[hp], t_r[1]], writes=[t_r[1]])
                        kb.op("dve", lambda v: v.tensor_tensor(out=t2[:, :], in0=t2[:, :], in1=vT[:, :], op=ALU.subtract),
                              reads=[t_r[1], vT_r], writes=[t_r[1]])
                        kb.op("dve", lambda v: v.tensor_tensor(out=t2[:, :], in0=t2[:, :], in1=t1[:, :], op=ALU.mult),
                              reads=[t_r[1], t_r[0]], writes=[t_r[1]])
                        kb.op("dve", lambda v: v.tensor_tensor(out=vT[:, :], in0=vT[:, :], in1=t2[:, :], op=ALU.add),
                              reads=[t_r[1], vT_r], writes=[vT_r])
                    kb.op("act", lambda a: a.activation(out=vb[:, :], in_=vT[:, :], func=AF.Copy), reads=[vT_r], writes=[vb_r])
                    for src, src_r, dst, dst_r, pbi in ((vb, vb_r, Vtok, Vtok_r, 0), (None, BT_r[hp], Btok, Btok_r, 1), (None, KT_r[hp], Ktok, Ktok_r, 0)):
                        sap = (lambda c: vb[:, c * 64:(c + 1) * 64]) if src is vb else \
                              ((lambda c: BTt[:, hp, c * 64:(c + 1) * 64]) if dst is Btok else (lambda c: KTt[:, hp, c * 64:(c + 1) * 64]))

                        def fn(pe, sap=sap, pbi=pbi):
                            inst = None
                            for c in range(NCH):
                                inst = pe.transpose(out=PB[pbi][0:64, c * 128:(c + 1) * 128], in_=sap(c), identity=ident[:, :])
                            return inst
                        kb.op("pe", fn, reads=[src_r, c_r], writes=[PB_r[pbi]])
                        kb.op("dve", lambda v: v.tensor_copy(out=dst[0:64, :, hp * 128:(hp + 1) * 128],
                                                             in_=PB[pbi][0:64, 0:NCH * 128].rearrange("p (c d) -> p c d", c=NCH)),
                              reads=[PB_r[pbi]], writes=[dst_r])
                if rs_ <= 2:
                    continue
                def fn(pe):
                    inst = None
                    for c in range(NCH):
                        for hp in range(2):
                            inst = pe.matmul(PS[2][0:64, c * 4 + hp * 2:c * 4 + hp * 2 + 2], lhsT=rkr[:, hp, c * 64:(c + 1) * 64], rhs=hselb[:, :],
                                             start=True, stop=True)
                    return inst
                kb.op("pe", fn, reads=rkr_r + [p_r], writes=[PS_r[2]])
                kb.op("act", lambda a: a.activation(out=cb[:, :, :], in_=PS[2][0:64, 0:NCH * 4].rearrange("p (c h) -> p c h", c=NCH), func=AF.Copy),
                      reads=[PS_r[2]], writes=[cb_r])
                for c2 in range(NCH // 2):
                    def fn(pe):
                        inst = None
                        for cc_ in range(2):
                            c = c2 * 2 + cc_
                            pe.matmul(PS[3][0:64, cc_ * 256:(cc_ + 1) * 256], lhsT=SGL[:, c * 64:(c + 1) * 64], rhs=g2a[:, :], start=True, stop=False)
                            inst = pe.matmul(PS[3][0:64, cc_ * 256:(cc_ + 1) * 256], lhsT=SGL2[:, c * 64:(c + 1) * 64], rhs=g2b[:, :], start=False, stop=True)
                        return inst
                    kb.op("pe", fn, reads=[SGL_r, SGL2_r, p_r], writes=[PS_r[3]])
                    kb.op("act", lambda a: a.activation(out=gtok[:, c2 * 2:c2 * 2 + 2, :], in_=PS[3][0:64, :].rearrange("p (c d) -> p c d", c=2), func=AF.Copy),
                          reads=[PS_r[3]], writes=[gtok_r])
                for c in range(NCH):
                    cc = slice(c * 64, (c + 1) * 64)
                    for h in range(4):
                        hp, hh = h // 2, h % 2
                        u = c * 4 + h
                        pi = 4 + (u % 2)

                        def fn(pe, pi=pi):
                            pe.matmul(PS[pi][0:64, 0:64], lhsT=ATz[:, hp, hh, cc], rhs=BTt[:, hp, cc], start=True, stop=True)
                            pe.matmul(PS[pi][0:64, 64:128], lhsT=BTt[:, hp, cc], rhs=ATz[:, hp, hh, cc], start=True, stop=True)
                            pe.matmul(PS[pi][0:64, 128:192], lhsT=KTt[:, hp, cc], rhs=ATz[:, hp, hh, cc], start=True, stop=True)
                            pe.matmul(PS[pi][0:64, 192:256], lhsT=BTt[:, hp, cc], rhs=RTz[:, hp, hh, cc], start=True, stop=True)
                            return pe.matmul(PS[pi][0:64, 256:320], lhsT=KTt[:, hp, cc], rhs=RTz[:, hp, hh, cc], start=True, stop=True)
                        kb.op("pe", fn, reads=[ATz_r[hp], RTz_r[hp], BT_r[hp], KT_r[hp]], writes=[PS_r[pi]])
                        kb.op("dve", lambda v, pi=pi: v.tensor_tensor(out=MN[0][:, u, :, :], in0=PS[pi][0:64, 0:128].rearrange("p (a b) -> p a b", a=2),
                                                                      in1=amask[:, 0:2, :], op=ALU.mult), reads=[PS_r[pi], p_r], writes=[MN_r[0]])
                        kb.op("dve", lambda v, pi=pi: v.tensor_tensor(out=A3[0:64, u, :, :], in0=PS[pi][0:64, 128:320].rearrange("p (a b) -> p a b", a=3),
                                                                      in1=amask[:, 2:5, :], op=ALU.mult), reads=[PS_r[pi], p_r], writes=[A3_r])
                if rs_ <= 3:
                    continue
                kb.op("dve", lambda v: v.tensor_tensor(out=Pm[:, :, :], in0=MN[0][:, :, 1, :],
                                                      in1=bass.AP(identf, 0, [[128, 64], [0, NU], [1, 64]]), op=ALU.add),
                      reads=[MN_r[0], c_r], writes=[P_r])
                cur = 0
                for lev in range(5):
                    nxt = 1 - cur
                    lastlev = (lev == 4)
                    for g4 in range(NU // 4):
                        pi = 4 + (g4 % 2)

                        def fn(pe, pi=pi):
                            inst = None
                            for q in range(4):
                                u = g4 * 4 + q
                                inst = pe.matmul(PS[pi][0:64, q * 128:q * 128 + 64], lhsT=MN[cur][:, u, 1, :], rhs=MN[cur][:, u, 0, :], start=True, stop=True)
                                if not lastlev:
                                    inst = pe.matmul(PS[pi][0:64, q * 128 + 64:q * 128 + 128], lhsT=MN[cur][:, u, 0, :], rhs=MN[cur][:, u, 1, :],
                                                     start=True, stop=True)
                            return inst
                        kb.op("pe", fn, reads=[MN_r[cur]], writes=[PS_r[pi]])
                        kb.op("act", lambda a, pi=pi: a.activation(out=MN[nxt][:, g4 * 4:g4 * 4 + 4, :, :],
                                                                   in_=PS[pi][0:64, :].rearrange("p (u a b) -> p u a b", u=4, a=2), func=AF.Copy),
                              reads=[PS_r[pi]], writes=[MN_r[nxt]])
                    for g8 in range(NU // 8):
                        pi = 2 + (g8 % 2)

                        def fn(pe, pi=pi):
                            inst = None
                            for q in range(8):
                                u = g8 * 8 + q
                                inst = pe.matmul(PS[pi][0:64, q * 64:(q + 1) * 64], lhsT=MN[nxt][:, u, 0, :], rhs=Pm[:, u, :], start=True, stop=True)
                            return inst
                        kb.op("pe", fn, reads=[MN_r[nxt], P_r], writes=[PS_r[pi]])
                        kb.op("dve", lambda v, pi=pi: v.tensor_tensor(out=Pm[:, g8 * 8:g8 * 8 + 8, :], in0=PS[pi][0:64, :].rearrange("p (u b) -> p u b", u=8),
                                                                      in1=Pm[:, g8 * 8:g8 * 8 + 8, :], op=ALU.add), reads=[PS_r[pi], P_r], writes=[P_r])
                    cur = nxt
                if rs_ <= 4:
                    continue
                for c in range(NCH):
                    cc = slice(c * 64, (c + 1) * 64)

                    def fn(pe):
                        inst = None
                        for h in range(4):
                            hp, hh = h // 2, h % 2
                            u = c * 4 + h
                            pe.matmul(PS[0][0:64, h * 64:(h + 1) * 64], lhsT=A3[:, u, 0, :], rhs=Vtok[:, c, h * 64:(h + 1) * 64], start=True, stop=False)
                            inst = pe.matmul(PS[0][0:64, h * 64:(h + 1) * 64], lhsT=ATz[:, hp, hh, cc], rhs=Tb[:, hp, hh * 64:(hh + 1) * 64],
                                             start=False, stop=True)
                        return inst
                    kb.op("pe", fn, reads=[A3_r, Vtok_r, Tb_r] + ATz_r, writes=[PS_r[0]])
                    kb.op("act", lambda a: a.activation(out=Rs[:, :], in_=PS[0][0:64, 0:256], func=AF.Copy), reads=[PS_r[0]], writes=[Rs_r])

                    def fn(pe):
                        inst = None
                        for h in range(4):
                            u = c * 4 + h
                            inst = pe.matmul(PS[1][0:64, h * 64:(h + 1) * 64], lhsT=Pm[:, u, :], rhs=Rs[:, h * 64:(h + 1) * 64], start=True, stop=True)
                        return inst
                    kb.op("pe", fn, reads=[P_r, Rs_r], writes=[PS_r[1]])
                    kb.op("act", lambda a: a.activation(out=Ub[0:64, :], in_=PS[1][0:64, 0:256], func=AF.Copy), reads=[PS_r[1]], writes=[Ub_r])

                    def fn(pe):
                        inst = None
                        for h in range(4):
                            hp, hh = h // 2, h % 2
                            u = c * 4 + h
                            pe.matmul(PS[0][0:64, h * 64:(h + 1) * 64], lhsT=RTz[:, hp, hh, cc], rhs=Tb[:, hp, hh * 64:(hh + 1) * 64], start=True, stop=False)
                            pe.matmul(PS[0][0:64, h * 64:(h + 1) * 64], lhsT=A3[:, u, 1, :], rhs=Ub[:, h * 64:(h + 1) * 64], start=False, stop=False)
                            inst = pe.matmul(PS[0][0:64, h * 64:(h + 1) * 64], lhsT=A3[:, u, 2, :], rhs=Vtok[:, c, h * 64:(h + 1) * 64], start=False, stop=True)
                        return inst
                    kb.op("pe", fn, reads=[A3_r, Vtok_r, Tb_r, Ub_r] + RTz_r, writes=[PS_r[0]])

                    def fn(pe):
                        inst = None
                        for hp in range(2):
                            pe.matmul(PS[1][:, hp * 128:(hp + 1) * 128], lhsT=Btok[:, c, hp * 128:(hp + 1) * 128], rhs=Ub[:, hp * 128:(hp + 1) * 128],
                                      start=True, stop=False)
                            inst = pe.matmul(PS[1][:, hp * 128:(hp + 1) * 128], lhsT=Ktok[:, c, hp * 128:(hp + 1) * 128], rhs=Vtok[:, c, hp * 128:(hp + 1) * 128],
                                             start=False, stop=True)
                        return inst
                    kb.op("pe", fn, reads=[Btok_r, Ktok_r, Vtok_r, Ub_r], writes=[PS_r[1]])
                    for hp in range(2):
                        ee = Gam[:, hp, c * 64 + 63:c * 64 + 64]
                        kb.op("dve", lambda v: v.scalar_tensor_tensor(out=tmpS[:, :], in0=PS[1][:, hp * 128:(hp + 1) * 128], scalar=ee, in1=blkm[:, :],
                                                                      op0=ALU.mult, op1=ALU.mult), reads=[PS_r[1], Gam_r[hp], p_r], writes=[tmpS_r])
                        kb.op("dve", lambda v: v.scalar_tensor_tensor(out=Tblk[:, hp, :], in0=Tblk[:, hp, :], scalar=ee, in1=tmpS[:, :],
                                                                      op0=ALU.mult, op1=ALU.add), reads=[T_r, tmpS_r, Gam_r[hp]], writes=[T_r])
                    kb.op("act", lambda a: a.activation(out=Tb[:, :, :], in_=Tblk[:, :, :], func=AF.Copy), reads=[T_r], writes=[Tb_r])
                    if rs_ <= 5:
                        continue
                    head_norm_gate(PS[0][0:64, 0:256], PS_r[0], ow, ow_r, ow2, ow2_r, st, st_r, rln, p_r, RW_EPS, P=64)
                    for h in range(4):
                        kb.op("dve", lambda v: v.scalar_tensor_tensor(out=ow[0:64, h * 64:(h + 1) * 64], in0=Vtok[0:64, c, h * 64:(h + 1) * 64],
                                                                      scalar=cb[:, c, h:h + 1], in1=ow[0:64, h * 64:(h + 1) * 64],
                                                                      op0=ALU.mult, op1=ALU.add), reads=[Vtok_r, cb_r, ow_r], writes=[ow_r])
                    kb.op("dve", lambda v: v.tensor_tensor(out=ob[0:64, :], in0=ow[0:64, :], in1=gtok[:, c, :], op=ALU.mult),
                          reads=[ow_r, gtok_r], writes=[ob_r])
                    out_to_OT(ob, ob_r, 64, OT, OT_r, 2, tok0 + c * 64)
            kb.barrier()

    NQ, NQS, NKC, NKS, NKW, NVC, NVS, NGT = 0, 512, 1024, 1280, 1536, 1792, 1920, 2176
    GK = 1.5957691216057308

    def nsa_phase(sq, l, OT, OT_r):
        ns_ = cfg.get("nsa_stop", 99)
        with ExitStack() as es:
            A = lambda n_, s_, d_: es.enter_context(sb(n_, s_, d_))
            QT = A("nQT", [128, 4, S], BF16)
            KTz = A("nKTz", [128, 3, 2, S], BF16)
            VCz = A("nVCz", [128, 2, S], BF16)
            VS = A("nVS", [128, NT, 2, 65], BF16)
            VW = A("nVW", [128, NT, 2, 65], BF16)
            gsig = A("ngsig", [128, NT, 24], F32)
            kcmpTz = A("nkcmp", [128, 2, 128], BF16)
            vcmp = A("nvcmp", [128, 2, 97], BF16)
            QT_r, KT_r, VC_r, VS_r, VW_r, gs_r = regs(4), regs(4), regs(4), regs(NT), regs(NT), regs(NT)
            kc_r, vcm_r = Reg(), Reg()
            kb.op("pool", lambda g: g.memset(KTz[:], 0.0), writes=KT_r)
            kb.op("pool", lambda g: g.memset(VCz[:], 0.0), writes=VC_r)
            kb.op("pool", lambda g: g.memset(VS[:, :, :, 64:65], 1.0), writes=VS_r)
            kb.op("pool", lambda g: g.memset(VW[:, :, :, 64:65], 1.0), writes=VW_r)
            kb.op("pool", lambda g: g.memset(kcmpTz[:], 0.0), writes=[kc_r])
            kb.op("pool", lambda g: g.memset(vcmp[:], 0.0), writes=[vcm_r])
            with ExitStack() as es1:
                A1 = lambda n_, s_, d_: es1.enter_context(sb(n_, s_, d_))
                Wn = A1("nWn", [128, 8, 2200], BF16)
                rp = A1("nrp", [128, 2, 512], F32)
                t1 = A1("nt1", [128, 512], F32)
                t2 = A1("nt2", [128, 512], F32)
                W_r, rp_r, t1_r, t2_r = Reg(), Reg(), Reg(), Reg()
                load_w(Wn, w_nsa16.ap()[l], 2200, W_r)
                for mt in range(4):
                    tok0 = mt * 512
                    bl = slice(tok0, tok0 + 512)
                    xr = XT_r[mt * 4:mt * 4 + 4]
                    kb.dma("sp", rp[:, 0, :], c_rope.ap()[0][:, bl], writes=[rp_r])
                    kb.dma("sp", rp[:, 1, :], c_rope.ap()[1][:, bl], writes=[rp_r])
                    for i in range(4):
                        proj_fm(PS[0][:, :], Wn, NQ + i * 128, 128, tok0, 512, [W_r], xr, PS_r[0])
                        proj_fm(PS[1][:, :], Wn, NQS + i * 128, 128, tok0, 512, [W_r], xr, PS_r[1])
                        kb.op("dve", lambda v: v.tensor_tensor(out=t1[:, :], in0=PS[0][:, :], in1=rp[:, 0, :], op=ALU.mult),
                              reads=[PS_r[0], rp_r], writes=[t1_r])
                        kb.op("dve", lambda v: v.scalar_tensor_tensor(out=t2[:, :], in0=PS[1][:, :], scalar=0.125, in1=rp[:, 1, :],
                                                                      op0=ALU.mult, op1=ALU.mult), reads=[PS_r[1], rp_r], writes=[t2_r])
                        kb.op("dve", lambda v: v.scalar_tensor_tensor(out=QT[:, i, bl], in0=t1[:, :], scalar=0.125, in1=t2[:, :],
                                                                      op0=ALU.mult, op1=ALU.add), reads=[t1_r, t2_r], writes=[QT_r[mt]])
                    for ty, c0 in ((0, NKC), (1, NKS), (2, NKW)):
                        proj_fm(PS[0][:, :], Wn, c0, 128, tok0, 512, [W_r], xr, PS_r[0])
                        proj_fm(PS[1][:, :], Wn, c0 + 128, 128, tok0, 512, [W_r], xr, PS_r[1])
                        kb.op("dve", lambda v: v.tensor_tensor(out=t1[:, :], in0=PS[0][:, :], in1=rp[:, 0, :], op=ALU.mult),
                              reads=[PS_r[0], rp_r], writes=[t1_r])
                        kb.op("dve", lambda v: v.tensor_tensor(out=t2[:, :], in0=PS[1][:, :], in1=rp[:, 1, :], op=ALU.mult),
                              reads=[PS_r[1], rp_r], writes=[t2_r])
                        for g in range(2):
                            pr = slice(g * 64, g * 64 + 64)
                            kb.op("dve", lambda v: v.tensor_tensor(out=KTz[pr, ty, g, bl], in0=t1[pr, :], in1=t2[pr, :], op=ALU.add),
                                  reads=[t1_r, t2_r], writes=[KT_r[mt]])
                    proj_fm(PS[2][:, :], Wn, NVC, 128, tok0, 512, [W_r], xr, PS_r[2])
                    for g in range(2):
                        pr = slice(g * 64, g * 64 + 64)
                        kb.op("act", lambda a: a.activation(out=VCz[pr, g, bl], in_=PS[2][pr, :], func=AF.Copy),
                              reads=[PS_r[2]], writes=[VC_r[mt]])
                    for j in range(4):
                        tt = mt * 4 + j
                        proj_tm(PS[3][:, 0:256], Wn, NVS, 256, tt, [W_r], PS_r[3])
                        kb.op("act", lambda a: a.activation(out=VS[:, tt, :, 0:64], in_=PS[3][:, 0:128].rearrange("p (g d) -> p g d", g=2), func=AF.Copy),
                              reads=[PS_r[3]], writes=[VS_r[tt], PS_r[3]])
                        kb.op("act", lambda a: a.activation(out=VW[:, tt, :, 0:64], in_=PS[3][:, 128:256].rearrange("p (g d) -> p g d", g=2), func=AF.Copy),
                              reads=[PS_r[3]], writes=[VW_r[tt], PS_r[3]])
                        proj_tm(PS[4][:, 0:24], Wn, NGT, 24, tt, [W_r], PS_r[4])
                        kb.op("act", lambda a: a.activation(out=gsig[:, tt, :], in_=PS[4][:, 0:24], func=AF.Sigmoid),
                              reads=[PS_r[4]], writes=[gs_r[tt]])
                kb.barrier()
            if ns_ <= 1:
                kb.barrier()
                return
            with ExitStack() as es2:
                A2 = lambda n_, s_, d_: es2.enter_context(sb(n_, s_, d_))
                w1d = A2("nw1d", [128, 32, 256], BF16)
                w2d = A2("nw2d", [128, 2, 128], BF16)
                wv2 = A2("nwv2", [128, 2, 64], BF16)
                posz = A2("nposz", [128, 2, 32], BF16)
                hc = A2("nhc", [128, 2], F32)
                gx = A2("ngx", [128, 128], F32)
                gw = A2("ngw", [128, 128], F32)
                gh = A2("ngh", [128, 2, 128], BF16)
                w1_r, w2_r, hc_r, gx_r, gw_r, gh_r = Reg(), Reg(), Reg(), Reg(), Reg(), Reg()
                kb.op("pool", lambda g: g.memset(posz[:], 0.0), writes=[w2_r])
                kb.op("pool", lambda g: g.memset(gh[:], 0.0), writes=[gh_r])
                for kv in range(2):
                    kb.dma("pool", posz[0:64, kv, :], nsa_posT.ap()[l, kv], reads=[w2_r], writes=[w2_r])
                for half in range(2):
                    kb.dma("pool", w2d[:, :, half * 64:(half + 1) * 64], nsa_wk2.ap()[l].rearrange("(t p) n -> p t n", p=128), writes=[w2_r])
                kb.dma("pool", wv2[:, :, :], nsa_wv2.ap()[l].rearrange("(t p) n -> p t n", p=128), writes=[w2_r])
                kb.dma("pool", vcmp[:, 0, 65:97], c_ovl.ap(), reads=[vcm_r], writes=[vcm_r])
                kb.dma("pool", vcmp[:, 1, 65:97], c_ovl.ap(), reads=[vcm_r], writes=[vcm_r])
                kb.op("pool", lambda g: g.memset(vcmp[0:127, :, 64:65], 1.0), reads=[vcm_r], writes=[vcm_r])
                for kv, w1src in ((0, wk1_16), (1, wv1_16)):
                    src3 = w1src.ap()[l].rearrange("(l d) n -> d l n", d=64)
                    for half in range(2):
                        for l4 in range(8):
                            kb.dma("sp", w1d[half * 64:(half + 1) * 64, l4 * 4:(l4 + 1) * 4, :], src3[:, l4 * 4:(l4 + 1) * 4, :],
                                   reads=[w16_r], writes=[w1_r])
                    srcz = (lambda g: KTz[:, 0, g, :]) if kv == 0 else (lambda g: VCz[:, g, :])
                    src_regs = KT_r if kv == 0 else VC_r
                    for hf in range(2):
                        def fn(pe):
                            inst = None
                            for ll in range(32):
                                inst = pe.matmul(PS[0][:, 0:1], lhsT=w1d[:, ll, hf * 128:(hf + 1) * 128], rhs=posz[:, kv, ll:ll + 1],
                                                 start=(ll == 0), stop=(ll == 31))
                            return inst
                        kb.op("pe", fn, reads=[w1_r, w2_r], writes=[PS_r[0]])
                        kb.op("act", lambda a: a.activation(out=hc[:, hf:hf + 1], in_=PS[0][:, 0:1], func=AF.Copy), reads=[PS_r[0]], writes=[hc_r])
                    for g in range(2):
                        for hf in range(2):
                            def fn(pe):
                                inst = None
                                for ll in range(32):
                                    rhs = bass.AP(srcz(g).tensor, srcz(g).offset + ll, [srcz(g).ap[0], [16, 127]])
                                    inst = pe.matmul(PS[1][:, 0:127], lhsT=w1d[:, ll, hf * 128:(hf + 1) * 128], rhs=rhs,
                                                     start=(ll == 0), stop=(ll == 31))
                                return inst
                            kb.op("pe", fn, reads=[w1_r] + src_regs, writes=[PS_r[1]])
                            kb.op("act", lambda a: a.activation(out=gx[:, 0:127], in_=PS[1][:, 0:127], func=AF.Identity, bias=hc[:, hf:hf + 1], scale=1.0),
                                  reads=[PS_r[1], hc_r], writes=[gx_r])
                            kb.op("dve", lambda v: v.tensor_tensor(out=gw[:, 0:127], in0=gx[:, 0:127], in1=gx[:, 0:127], op=ALU.mult),
                                  reads=[gx_r], writes=[gw_r])
                            kb.op("dve", lambda v: v.tensor_scalar(out=gw[:, 0:127], in0=gw[:, 0:127], scalar1=0.044715, scalar2=1.0,
                                                                   op0=ALU.mult, op1=ALU.add), reads=[gw_r], writes=[gw_r])
                            kb.op("dve", lambda v: v.tensor_tensor(out=gw[:, 0:127], in0=gw[:, 0:127], in1=gx[:, 0:127], op=ALU.mult),
                                  reads=[gw_r, gx_r], writes=[gw_r])
                            kb.op("act", lambda a: a.activation(out=gw[:, 0:127], in_=gw[:, 0:127], func=AF.Sigmoid, scale=GK), reads=[gw_r], writes=[gw_r])
                            kb.op("dve", lambda v: v.tensor_tensor(out=gh[:, hf, 0:127], in0=gx[:, 0:127], in1=gw[:, 0:127], op=ALU.mult),
                                  reads=[gw_r, gx_r], writes=[gh_r])
                        if kv == 0:
                            def fn(pe):
                                pe.matmul(PS[2][:, 0:127], lhsT=w2d[:, 0, :], rhs=gh[:, 0, 0:127], start=True, stop=False)
                                return pe.matmul(PS[2][:, 0:127], lhsT=w2d[:, 1, :], rhs=gh[:, 1, 0:127], start=False, stop=True)
                            kb.op("pe", fn, reads=[w2_r, gh_r], writes=[PS_r[2]])
                            pr = slice(g * 64, g * 64 + 64)
                            kb.op("act", lambda a: a.activation(out=kcmpTz[pr, g, 0:127], in_=PS[2][pr, 0:127], func=AF.Copy),
                                  reads=[PS_r[2], kc_r], writes=[kc_r])
                        else:
                            def fn(pe):
                                pe.matmul(PS[2][:, 0:64], lhsT=gh[:, 0, :], rhs=wv2[:, 0, :], start=True, stop=False)
                                return pe.matmul(PS[2][:, 0:64], lhsT=gh[:, 1, :], rhs=wv2[:, 1, :], start=False, stop=True)
                            kb.op("pe", fn, reads=[w2_r, gh_r], writes=[PS_r[2]])
                            kb.op("act", lambda a: a.activation(out=vcmp[0:127, g, 0:64], in_=PS[2][0:127, 0:64], func=AF.Copy),
                                  reads=[PS_r[2], vcm_r], writes=[vcm_r])
                kb.barrier()
            if ns_ <= 2:
                kb.barrier()
                return
            selbT = A("nselbT", [128, 2, S], BF16)
            onehot = A("nonehot", [128, S], BF16)
            cmask = A("ncmask", [128, S], BF16)
            selm = A("nselm", [128, 2, NT, 32], F32)
            wmask = A("nwmask", [128, 128], F32)
            PT = [A("nPT%d" % i, [128, 640], BF16) for i in range(2)]
            ex = A("nex", [128, 512], F32)
            ocs = A("nocs", [128, 4, 97], F32)
            ONSA = A("nONSA", [128, 4, 512], F32)
            impb = A("nimp", [128, 4, 2, 32], F32)
            sc = A("nsc", [128, 32], F32)
            cm3 = A("ncm3", [128, 32, 32], F32)
            sb16 = A("nsb16", [128, 32], BF16)
            dn = A("ndn", [128, 16], F32)
            obf = A("nobf", [128, 512], BF16)
            k_r, sel_r, PT_r, ex_r, ocs_r, on_r, imp_r, sc_r, cm3_r, sb16_r, dn_r, obf_r = \
                Reg(), regs(4), regs(2), Reg(), Reg(), regs(4), Reg(), Reg(), Reg(), Reg(), Reg(), Reg()
            kb.op("pool", lambda g: g.memset(selbT[:], 0.0), writes=sel_r)
            kb.op("pool", lambda g: g.memset(onehot[:], 0.0), writes=[k_r])
            kb.dma("pool", onehot[0:32, :], c_onehot.ap(), reads=[k_r], writes=[k_r])
            kb.dma("pool", cmask[:, :], c_cmpmask.ap(), writes=[k_r])
            for m_ in range(2):
                kb.dma("sp", selm[:, m_, :, :], c_selm.ap()[m_], writes=[k_r])
            kb.op("dve", lambda v: v.tensor_scalar(out=wmask[:, :], in0=caus[:, 0, :], scalar1=-1.0, scalar2=1.0, op0=ALU.mult, op1=ALU.add),
                  reads=[c_r], writes=[k_r])

            def branch_finish(acc_ps, acc_r, W, nq, h, b, tts, first):
                kb.op("act", lambda a: a.activation(out=ocs[:, 0:nq, 0:W], in_=acc_ps.rearrange("p (q w) -> p q w", q=nq), func=AF.Copy),
                      reads=[acc_r], writes=[ocs_r, acc_r])
                kb.op("dve", lambda v: v.tensor_scalar(out=dn[:, 0:nq], in0=ocs[:, 0:nq, 64], scalar1=1e-30, scalar2=None, op0=ALU.max),
                      reads=[ocs_r], writes=[dn_r])
                kb.op("dve", lambda v: v.reciprocal(out=dn[:, 0:nq], in_=dn[:, 0:nq]), reads=[dn_r], writes=[dn_r])
                kb.op("dve", lambda v: v.tensor_tensor(out=dn[:, 8:8 + nq], in0=dn[:, 0:nq], in1=gsig[:, tts[0]:tts[0] + nq, h * 3 + b], op=ALU.mult),
                      reads=[dn_r] + gs_r[tts[0]:tts[0] + nq], writes=[dn_r])
                for qi in range(nq):
                    tl = tts[qi] % 4
                    if first:
                        kb.op("dve", lambda v: v.tensor_scalar(out=ONSA[:, tl, h * 64:(h + 1) * 64], in0=ocs[:, qi, 0:64], scalar1=dn[:, 8 + qi:9 + qi],
                                                               scalar2=None, op0=ALU.mult), reads=[ocs_r, dn_r], writes=[on_r[tl]])
                    else:
                        kb.op("dve", lambda v: v.scalar_tensor_tensor(out=ONSA[:, tl, h * 64:(h + 1) * 64], in0=ocs[:, qi, 0:64], scalar=dn[:, 8 + qi:9 + qi],
                                                                      in1=ONSA[:, tl, h * 64:(h + 1) * 64], op0=ALU.mult, op1=ALU.add),
                              reads=[ocs_r, dn_r, on_r[tl]], writes=[on_r[tl]])

            for qb in range(4):
                qc = slice(qb * 512, (qb + 1) * 512)
                tts = list(range(qb * 4, qb * 4 + 4))
                for h in range(8):
                    g, i = h // 4, h % 4
                    pi = h % 2
                    kb.op("pe", lambda pe: pe.matmul(PS[pi][:, :], lhsT=kcmpTz[:, g, :], rhs=QT[:, i, qc], start=True, stop=True),
                          reads=[kc_r, QT_r[qb]], writes=[PS_r[pi]])
                    kb.op("act", lambda a: a.activation(out=ex[:, :], in_=PS[pi][:, :], func=AF.Exp), reads=[PS_r[pi]], writes=[ex_r])
                    kb.op("dve", lambda v: v.tensor_tensor(out=PT[pi][:, 0:512], in0=ex[:, :], in1=cmask[:, qc], op=ALU.mult),
                          reads=[ex_r, k_r], writes=[PT_r[pi]])
                    ai = 2 + (h % 2)

                    def fn(pe):
                        inst = None
                        for q in range(4):
                            inst = pe.matmul(PS[ai][:, q * 97:(q + 1) * 97], lhsT=PT[pi][:, q * 128:(q + 1) * 128], rhs=vcmp[:, g, :], start=True, stop=True)
                        return inst
                    kb.op("pe", fn, reads=[PT_r[pi], vcm_r], writes=[PS_r[ai]])
                    branch_finish(PS[ai][:, 0:388], PS_r[ai], 97, 4, h, 0, tts, True)
                    for q in range(4):
                        if i == 0:
                            kb.op("dve", lambda v: v.tensor_scalar(out=impb[:, q, g, :], in0=ocs[:, q, 65:97], scalar1=dn[:, q:q + 1], scalar2=None, op0=ALU.mult),
                                  reads=[ocs_r, dn_r], writes=[imp_r])
                        else:
                            kb.op("dve", lambda v: v.scalar_tensor_tensor(out=impb[:, q, g, :], in0=ocs[:, q, 65:97], scalar=dn[:, q:q + 1], in1=impb[:, q, g, :],
                                                                          op0=ALU.mult, op1=ALU.add), reads=[ocs_r, dn_r, imp_r], writes=[imp_r])
                if ns_ <= 3:
                    continue
                for q in range(4):
                    tt = tts[q]
                    for g in range(2):
                        kb.op("dve", lambda v: v.tensor_tensor(out=sc[:, :], in0=impb[:, q, g, :], in1=selm[:, 0, tt, :], op=ALU.mult),
                              reads=[imp_r, k_r], writes=[sc_r])
                        kb.op("dve", lambda v: v.tensor_tensor(out=sc[:, :], in0=sc[:, :], in1=selm[:, 1, tt, :], op=ALU.add),
                              reads=[sc_r, k_r], writes=[sc_r])
                        kb.op("dve", lambda v: v.tensor_tensor(out=cm3[:, :, :], in0=bass.AP(sc, 0, [[32, 128], [0, 32], [1, 32]]),
                                                              in1=bass.AP(sc, 0, [[32, 128], [1, 32], [0, 32]]), op=ALU.is_gt),
                              reads=[sc_r], writes=[cm3_r])
                        kb.op("dve", lambda v: v.reduce_sum(out=sc[:, :], in_=cm3[:, :, :], axis=AX.X), reads=[cm3_r, sc_r], writes=[sc_r])
                        kb.op("dve", lambda v: v.tensor_scalar(out=sb16[:, :], in0=sc[:, :], scalar1=15.5, scalar2=-30000.0, op0=ALU.is_gt, op1=ALU.mult),
                              reads=[sc_r], writes=[sb16_r])
                        kb.op("pe", lambda pe: pe.transpose(out=PB[0][0:32, 0:128], in_=sb16[:, :], identity=ident[:, :]),
                              reads=[sb16_r, c_r], writes=[PB_r[0]])
                        kb.op("act", lambda a: a.activation(out=selbT[0:32, g, tt * 128:(tt + 1) * 128], in_=PB[0][0:32, 0:128], func=AF.Copy),
                              reads=[PB_r[0]], writes=[sel_r[qb]])
                if ns_ <= 4:
                    continue
                for h in range(8):
                    g, i = h // 4, h % 4
                    nkt = 4 * qb + 4
                    for kt in range(nkt):
                        kc_ = slice(kt * 128, (kt + 1) * 128)
                        pi = kt % 2

                        def fn(pe):
                            pe.matmul(PS[pi][:, :], lhsT=KTz[:, 1, g, kc_], rhs=QT[:, i, qc], start=True, stop=False)
                            return pe.matmul(PS[pi][:, :], lhsT=onehot[:, kc_], rhs=selbT[:, g, qc], start=False, stop=True)
                        kb.op("pe", fn, reads=[KT_r[kt // 4], QT_r[qb], k_r, sel_r[qb]], writes=[PS_r[pi]])
                        kb.op("act", lambda a: a.activation(out=PT[pi][:, 0:512], in_=PS[pi][:, :], func=AF.Exp), reads=[PS_r[pi]], writes=[PT_r[pi]])
                        if kt >= 4 * qb:
                            ql = kt - 4 * qb
                            kb.op("dve", lambda v: v.tensor_tensor(out=PT[pi][:, ql * 128:(ql + 1) * 128], in0=PT[pi][:, ql * 128:(ql + 1) * 128],
                                                                  in1=caus[:, 0, :], op=ALU.mult), reads=[PT_r[pi], c_r], writes=[PT_r[pi]])
                        for q in range(4):
                            qt = 4 * qb + q
                            if qt < kt:
                                continue
                            kb.op("pe", lambda pe: pe.matmul(PS[2 + q][:, 0:65], lhsT=PT[pi][:, q * 128:(q + 1) * 128], rhs=VS[:, kt, g, :],
                                                             start=(kt == 0), stop=(kt == qt)), reads=[PT_r[pi], VS_r[kt]], writes=[PS_r[2 + q]])
                    for q in range(4):
                        branch_finish(PS[2 + q][:, 0:65], PS_r[2 + q], 65, 1, h, 1, [tts[q]], False)
                if ns_ <= 5:
                    continue
                for h in range(8):
                    g, i = h // 4, h % 4
                    for q in range(4):
                        qt = 4 * qb + q
                        qcs = slice(qt * 128, (qt + 1) * 128)
                        kts = [kt for kt in range(qt - 4, qt + 1) if kt >= 0]
                        pi = q % 2
                        main = [kt for kt in kts if kt >= qt - 3]

                        def fn(pe):
                            inst = None
                            for n_, kt in enumerate(main):
                                inst = pe.matmul(PS[pi][:, n_ * 128:(n_ + 1) * 128], lhsT=KTz[:, 2, g, kt * 128:(kt + 1) * 128], rhs=QT[:, i, qcs],
                                                 start=True, stop=True)
                            return inst
                        kb.op("pe", fn, reads=KT_r + [QT_r[qb]], writes=[PS_r[pi]])
                        nm = len(main)
                        kb.op("act", lambda a: a.activation(out=PT[pi][:, 0:nm * 128], in_=PS[pi][:, 0:nm * 128], func=AF.Exp), reads=[PS_r[pi]], writes=[PT_r[pi]])
                        kb.op("dve", lambda v: v.tensor_tensor(out=PT[pi][:, (nm - 1) * 128:nm * 128], in0=PT[pi][:, (nm - 1) * 128:nm * 128],
                                                              in1=caus[:, 0, :], op=ALU.mult), reads=[PT_r[pi], c_r], writes=[PT_r[pi]])
                        tail = (qt - 4 >= 0)
                        if tail:
                            kt = qt - 4
                            kb.op("pe", lambda pe: pe.matmul(PS[2 + pi][:, 0:128], lhsT=KTz[:, 2, g, kt * 128:(kt + 1) * 128], rhs=QT[:, i, qcs],
                                                             start=True, stop=True), reads=KT_r + [QT_r[qb]], writes=[PS_r[2 + pi]])
                            kb.op("act", lambda a: a.activation(out=ex[:, 0:128], in_=PS[2 + pi][:, 0:128], func=AF.Exp), reads=[PS_r[2 + pi]], writes=[ex_r])
                            kb.op("dve", lambda v: v.tensor_tensor(out=PT[pi][:, 512:640], in0=ex[:, 0:128], in1=wmask[:, :], op=ALU.mult),
                                  reads=[ex_r, k_r, PT_r[pi]], writes=[PT_r[pi]])

                        def fn(pe):
                            inst = None
                            seq_ = [(n_, kt) for n_, kt in enumerate(main)] + ([(4, qt - 4)] if tail else [])
                            for idx, (n_, kt) in enumerate(seq_):
                                inst = pe.matmul(PS[4][:, q * 65:(q + 1) * 65], lhsT=PT[pi][:, n_ * 128:(n_ + 1) * 128], rhs=VW[:, kt, g, :],
                                                 start=(idx == 0), stop=(idx == len(seq_) - 1))
                            return inst
                        kb.op("pe", fn, reads=[PT_r[pi]] + VW_r[max(0, qt - 4):qt + 1], writes=[PS_r[4]])
                    branch_finish(PS[4][:, 0:260], PS_r[4], 65, 4, h, 2, tts, False)
                if ns_ <= 6:
                    continue
                for q in range(4):
                    tt = tts[q]
                    kb.op("act", lambda a: a.activation(out=obf[:, :], in_=ONSA[:, q, :], func=AF.Copy), reads=[on_r[q]], writes=[obf_r])
                    for half in range(2):
                        out_to_OT(obf[:, half * 256:(half + 1) * 256], obf_r, 128, OT, OT_r, 4 + 2 * half, tt * 128)
            kb.barrier()

    for sq in range(NSEQ):
        for l in range(NLAY):
            if l == 0:
                with sb("xstage", [128, 2, D], F32) as xst:
                    xst_r = regs(2)
                    for tt in range(NT):
                        i = tt % 2
                        kb.dma("sp", xst[:, i, :], x_d.ap()[sq, tt * 128:(tt + 1) * 128, :], writes=[xst_r[i]])
                        make_xT(xst[:, i, :], xst_r[i], tt)
                    kb.barrier()
            if cfg.get("stop") == "xt":
                continue
            res_src = (lambda tt: x_d.ap()[sq, tt * 128:(tt + 1) * 128, :]) if l == 0 else \
                      (lambda tt: xres[1].ap()[tt * 128:(tt + 1) * 128, :])
            res_regs = None if l == 0 else xres_r[1]

            with sb("OT", [128, 8, S], BF16) as OT:
                OT_r = regs(NT)
                if inject_O:
                    for k in range(8):
                        kb.dma("pool", OT[:, k, :], dbg_OT.ap()[:, k, :], writes=OT_r)
                if "gla" in mixers:
                    gla_phase(sq, l, OT, OT_r)
                if "rwkv" in mixers:
                    rwkv_phase(sq, l, OT, OT_r)
                if "nsa" in mixers:
                    nsa_phase(sq, l, OT, OT_r)
                if "OT" in dump_d:
                    for k in range(8):
                        with sb("otd", [128, S], F32) as otd:
                            r_ = Reg()
                            kb.op("act", lambda a: a.activation(out=otd[:], in_=OT[:, k, :], func=AF.Copy),
                                  reads=OT_r, writes=[r_])
                            dump("OT", otd[:], r_, idx=k)
                            kb.barrier()
                if cfg.get("stop") == "outproj0":
                    continue
                with sb("Wo", [128, 8, D], BF16) as Wo, \
                        sb("ln1", [128, 2, D], F32) as ln1, \
                        sb("xs1", [128, 2, D], F32) as xs1, \
                        sb("st1", [128, 2, 32], F32) as st1:
                    Wo_r = Reg()
                    ln_r = Reg()
                    xs_r = regs(2)
                    st_r = regs(2)
                    load_w(Wo, w_out16.ap()[l], D, Wo_r)
                    kb.dma("sp", ln1[:, 0, :], bcast_rows(rowtab, l * 5120 + 1024, D), writes=[ln_r])
                    kb.dma("sp", ln1[:, 1, :], bcast_rows(rowtab, l * 5120 + 2048, D), writes=[ln_r])
                    for tt in range(NT):
                        i = tt % 2
                        kb.dma("sp", xs1[:, i, :], res_src(tt), reads=([res_regs[tt]] if res_regs else []), writes=[xs_r[i]])
                        for hf in range(2):
                            def fn(pe, hf=hf):
                                inst = None
                                for k in range(8):
                                    inst = pe.matmul(PS[hf][:, :], lhsT=OT[:, k, tt * 128:(tt + 1) * 128],
                                                     rhs=Wo[:, k, hf * 512:(hf + 1) * 512], start=(k == 0), stop=(k == 7))
                                return inst
                            kb.op("pe", fn, reads=[OT_r[tt], Wo_r], writes=[PS_r[hf]])
                            kb.op("dve", lambda v, hf=hf: v.scalar_tensor_tensor(
                                out=xs1[:, i, hf * 512:(hf + 1) * 512], in0=xs1[:, i, hf * 512:(hf + 1) * 512], scalar=ALPHA,
                                in1=PS[hf][:, :], op0=ALU.mult, op1=ALU.add), reads=[PS_r[hf], xs_r[i]], writes=[xs_r[i]])
                        layer_norm(xs1[:, i, :], xs_r[i], ln1[:, 0, :], ln1[:, 1, :], ln_r, st1[:, i, :], st_r[i])
                        kb.dma("sp", xres[0].ap()[tt * 128:(tt + 1) * 128, :], xs1[:, i, :], reads=[xs_r[i]], writes=[xres_r[0][tt]])
                        make_xT(xs1[:, i, :], xs_r[i], tt)
                        if "x1" in dump_d and sq == 0 and l == cfg.get("dump_layer", 0):
                            dump("x1", xs1[:, i, :], xs_r[i], idx=tt)
                    kb.barrier()
            if cfg.get("stop") in ("outproj", "outproj0"):
                continue
            with sb("aT", [128, NFC, 1024], BF16) as aT, \
                    sb("Wd", [128, NFC, D], BF16) as Wd, \
                    sb("Wgu", [128, 2, 2, 8, 512], BF16) as Wgu, \
                    sb("sg", [128, 2, 512], F32) as sg, \
                    sb("ln2", [128, 2, D], F32) as ln2, \
                    sb("xs2", [128, 2, D], F32) as xs2, \
                    sb("st2", [128, 2, 32], F32) as st2:
                Wd_r = Reg()
                ln_r = Reg()
                aT_r = regs(NFC)
                Wgu_r = regs(2)
                sg_r = regs(2)
                xs_r = regs(2)
                st_r = regs(2)
                kb.dma("sp", ln2[:, 0, :], bcast_rows(rowtab, l * 5120 + 3072, D), writes=[ln_r])
                kb.dma("sp", ln2[:, 1, :], bcast_rows(rowtab, l * 5120 + 4096, D), writes=[ln_r])
                for k in range(NFC):
                    for c0 in (0, 512):
                        kb.dma("sp", Wd[:, k, c0:c0 + 512], w_down16.ap()[l, k * 128:(k + 1) * 128, c0:c0 + 512], reads=[w16_r], writes=[Wd_r])
                last = (l == NLAY - 1)
                for mt in range(2):
                    tok0 = mt * 1024
                    for hc in range(NFC):
                        cg, ci = hc // 4, hc % 4
                        wi = cg % 2
                        if ci == 0:
                            ncol = min(512, FF - cg * 512)
                            for gu, wsrc in ((0, w_gate16), (1, w_up16)):
                                for k in range(8):
                                    kb.dma("sp", Wgu[:, wi, gu, k, 0:ncol],
                                           wsrc.ap()[l, k * 128:(k + 1) * 128, cg * 512:cg * 512 + ncol],
                                           reads=[w16_r], writes=[Wgu_r[wi]])
                        for blk in range(2):
                            t0 = tok0 + blk * 512
                            pg, pu = (0, 1) if blk == 0 else (2, 3)
                            for gu, pi in ((0, pg), (1, pu)):
                                def fn(pe, gu=gu, pi=pi):
                                    inst = None
                                    for k in range(8):
                                        inst = pe.matmul(PS[pi][:, :], lhsT=Wgu[:, wi, gu, k, ci * 128:(ci + 1) * 128],
                                                         rhs=XT[:, k, t0:t0 + 512], start=(k == 0), stop=(k == 7))
                                    return inst
                                kb.op("pe", fn, reads=[Wgu_r[wi]] + XT_r[t0 // 128:t0 // 128 + 4], writes=[PS_r[pi]])
                            kb.op("act", lambda a: a.activation(out=sg[:, blk, :], in_=PS[pg][:, :], func=AF.Silu),
                                  reads=[PS_r[pg]], writes=[sg_r[blk]])
                            kb.op("dve", lambda v: v.tensor_tensor(out=aT[:, hc, blk * 512:(blk + 1) * 512], in0=sg[:, blk, :],
                                                                  in1=PS[pu][:, :], op=ALU.mult),
                                  reads=[sg_r[blk], PS_r[pu]], writes=[aT_r[hc]])
                    for t8 in range(8):
                        tt = mt * 8 + t8
                        i = tt % 2
                        kb.dma("sp", xs2[:, i, :], xres[0].ap()[tt * 128:(tt + 1) * 128, :], reads=[xres_r[0][tt]], writes=[xs_r[i]])
                        for hf in range(2):
                            pi = 4 + hf

                            def fn(pe, hf=hf, pi=pi):
                                inst = None
                                for k in range(NFC):
                                    inst = pe.matmul(PS[pi][:, :], lhsT=aT[:, k, t8 * 128:(t8 + 1) * 128],
                                                     rhs=Wd[:, k, hf * 512:(hf + 1) * 512], start=(k == 0), stop=(k == NFC - 1))
                                return inst
                            kb.op("pe", fn, reads=aT_r + [Wd_r], writes=[PS_r[pi]])
                            kb.op("dve", lambda v, hf=hf, pi=pi: v.scalar_tensor_tensor(
                                out=xs2[:, i, hf * 512:(hf + 1) * 512], in0=xs2[:, i, hf * 512:(hf + 1) * 512], scalar=ALPHA,
                                in1=PS[pi][:, :], op0=ALU.mult, op1=ALU.add), reads=[PS_r[pi], xs_r[i]], writes=[xs_r[i]])
                        layer_norm(xs2[:, i, :], xs_r[i], ln2[:, 0, :], ln2[:, 1, :], ln_r, st2[:, i, :], st_r[i])
                        if last:
                            kb.dma("sp", out_d.ap()[sq, tt * 128:(tt + 1) * 128, :], xs2[:, i, :], reads=[xs_r[i]])
                        else:
                            kb.dma("sp", xres[1].ap()[tt * 128:(tt + 1) * 128, :], xs2[:, i, :], reads=[xs_r[i]], writes=[xres_r[1][tt]])
                        if "x2" in dump_d and sq == 0 and l == cfg.get("dump_layer", 0):
                            dump("x2", xs2[:, i, :], xs_r[i], idx=tt)
                    if not last:
                        pass
                kb.barrier()
                if not last:
                    for tt in range(NT):
                        i = tt % 2
                        kb.dma("sp", xs2[:, i, :], xres[1].ap()[tt * 128:(tt + 1) * 128, :], reads=[xres_r[1][tt]], writes=[xs_r[i]])
                        make_xT(xs2[:, i, :], xs_r[i], tt)
                    kb.barrier()
    kb.finish()
    return kb


def host_consts():
    c = {}
    c["c_ident"] = np.eye(128, dtype=np.float32)
    s = np.arange(128)
    c["c_caus"] = (s[:, None] <= s[None, :]).astype(np.float32)
    half = 32
    inv = (10000.0 ** (-np.arange(half, dtype=np.float32) / half)).astype(np.float32)
    ang = (np.arange(S, dtype=np.float32)[:, None] * inv[None, :]).astype(np.float32)
    cos = np.cos(ang).astype(np.float32).T
    sin = np.sin(ang).astype(np.float32).T
    cosT = np.concatenate([cos, cos, cos, cos], 0)
    sinT = np.concatenate([-sin, sin, -sin, sin], 0)
    c["c_rope"] = np.stack([cosT, sinT]).astype(np.float32)
    cc = np.arange(128)
    t = np.arange(S)
    c["c_cmpmask"] = ((16 * cc[:, None] + 31 <= t[None, :]) & (cc[:, None] < 127)).astype(np.float32)
    j = np.arange(32)
    c["c_onehot"] = ((t[None, :] // 64) == j[:, None]).astype(np.float32)
    cur = t // 64
    forced = (j[None, :] == 0) | (j[None, :] == cur[:, None]) | (j[None, :] == cur[:, None] - 1)
    future = j[None, :] > cur[:, None]
    m1 = (~forced & ~future).astype(np.float32)
    m2 = np.where(forced, 1e9, np.where(future, -1e9, 0.0)).astype(np.float32)
    selm = np.stack([m1, m2])
    c["c_selm"] = np.ascontiguousarray(selm.reshape(2, NT, 128, 32).transpose(0, 2, 1, 3))
    c0 = np.arange(127) * 16
    s0 = np.arange(32) * 64
    lo = np.maximum(c0[:, None], s0[None, :])
    hi = np.minimum(c0[:, None] + 32, s0[None, :] + 64)
    ov = np.zeros((128, 32), np.float32)
    ov[:127] = np.maximum(hi - lo, 0) / 16
    c["c_ovl"] = ov
    blk = np.zeros((128, 128), np.float32)
    blk[:64, :64] = 1
    blk[64:, 64:] = 1
    c["c_blk"] = blk
    hs = np.zeros((128, 2), np.float32)
    hs[:64, 0] = 1
    hs[64:, 1] = 1
    c["c_hsel"] = hs
    i64 = np.arange(64)
    lo_strict = (i64[None, :] < i64[:, None]).astype(np.float32)
    up_strict = (i64[:, None] < i64[None, :]).astype(np.float32)
    up_incl = (i64[:, None] <= i64[None, :]).astype(np.float32)
    c["c_rwmask"] = np.ascontiguousarray(np.stack([lo_strict, up_strict, up_strict, up_incl, up_incl], axis=1))
    return c


def host_layout(inp):
    d = {}
    f = lambda a: np.ascontiguousarray(np.asarray(a, dtype=np.float32))
    for k in ("w_in", "w_in_vres", "gla_w_a2", "rwkv_w2", "rwkv_a2", "rwkv_v2", "rwkv_g2", "nsa_wk1", "nsa_wk2",
              "nsa_wv1", "nsa_wv2", "w_out", "ffn_w_gate", "ffn_w_up", "ffn_w_down"):
        d[k] = f(inp[k])
    base = 2096
    sw = lambda b: list(range(b + 32, b + 64)) + list(range(b, b + 32))
    pl = lambda b: list(range(b, b + 64))
    cols = []
    for i in range(4):
        cols += pl(base + i * 64) + pl(base + (4 + i) * 64)
    for i in range(4):
        cols += sw(base + i * 64) + sw(base + (4 + i) * 64)
    for c0 in (512, 768, 1024):
        cols += pl(base + c0) + pl(base + c0 + 64)
        cols += sw(base + c0) + sw(base + c0 + 64)
    cols += list(range(base + 640, base + 768)) + list(range(base + 896, base + 1024)) + list(range(base + 1152, base + 1280))
    cols += list(range(base + 1280, base + 1304))
    assert len(cols) == 2200
    d["w_nsa"] = f(np.asarray(inp["w_in"])[:, :, cols])
    d["nsa_posT"] = f(np.stack([np.asarray(inp["nsa_pos_k"]).transpose(0, 2, 1),
                                np.asarray(inp["nsa_pos_v"]).transpose(0, 2, 1)], axis=1))
    pt = np.zeros((L, 128, 32), np.float32)
    mu = np.asarray(inp["rwkv_mu"])
    rwt = [(0, 128), (128, 128), (256, 128), (384, 128), (512, 128), (640, 128), (768, 64), (832, 64), (896, 128), (1024, 32)]
    for l in range(L):
        pt[l, :, 0:2] = np.asarray(inp["gla_b_a"])[l].reshape(2, 128).T
        for i, (c0, n) in enumerate(rwt):
            pt[l, :n, 2 + i] = mu[l, c0:c0 + n]
        if l >= 1:
            pt[l, :32, 12] = np.asarray(inp["rwkv_mu_vres"])[l - 1]
            pt[l, :, 17:19] = np.asarray(inp["rwkv_v0"])[l - 1].reshape(2, 128).T
        pt[l, :, 13:15] = np.asarray(inp["rwkv_w0"])[l].reshape(2, 128).T
        pt[l, :, 15:17] = np.asarray(inp["rwkv_a0"])[l].reshape(2, 128).T
        pt[l, :, 19:21] = np.asarray(inp["rwkv_k_k"])[l].reshape(2, 128).T
        pt[l, :, 21:23] = np.asarray(inp["rwkv_k_a"])[l].reshape(2, 128).T
        pt[l, :, 23:25] = np.asarray(inp["rwkv_r_k"])[l].reshape(2, 128).T
    d["ptab"] = pt
    rt = np.zeros((L, 5120), np.float32)
    for l in range(L):
        rt[l, 0:256] = np.asarray(inp["gla_ln_w"])[l]
        rt[l, 256:512] = np.asarray(inp["gla_ln_b"])[l]
        rt[l, 512:768] = np.asarray(inp["rwkv_ln_w"])[l]
        rt[l, 768:1024] = np.asarray(inp["rwkv_ln_b"])[l]
        rt[l, 1024:2048] = np.asarray(inp["ln1_w"])[l]
        rt[l, 2048:3072] = np.asarray(inp["ln1_b"])[l]
        rt[l, 3072:4096] = np.asarray(inp["ln2_w"])[l]
        rt[l, 4096:5120] = np.asarray(inp["ln2_b"])[l]
    d["rowtab"] = rt
    return d


_CACHE = {}


def kernel(**inputs):
    cfg = {}
    if "full" not in _CACHE:
        _CACHE["full"] = build(cfg)
    kb = _CACHE["full"]
    shared = host_layout(inputs)
    shared.update(host_consts())
    x = np.ascontiguousarray(np.asarray(inputs["x"], dtype=np.float32))
    in_maps = []
    for c in range(8):
        m = dict(shared)
        m["x"] = x[2 * c:2 * c + 2]
        in_maps.append(m)
    res = run_bass_kernel_spmd(kb.nc, in_maps, core_ids=list(range(8)))
    return np.concatenate([r["out"] for r in res.results], axis=0).astype(np.float32)
```

```python
import math
from contextlib import ExitStack
import numpy as np
import concourse.bass as bass
import concourse.mybir as mybir
from concourse.bass_utils import run_bass_kernel_spmd

F32 = mybir.dt.float32
BF16 = mybir.dt.bfloat16
AF = mybir.ActivationFunctionType
ALU = mybir.AluOpType
AX = mybir.AxisListType

S = 2048
D = 1024
NT = S // 128
L = 2
FF = 2816
NFC = FF // 128
ALPHA = float((2 * L) ** 0.25)
LN_EPS = 1e-5
RW_EPS = 64e-5
NDS = 6


class Reg:
    __slots__ = ("lw", "rd")

    def __init__(self):
        self.lw = None
        self.rd = {}


def regs(n):
    return [Reg() for _ in range(n)]


class _Rec:
    def __init__(self):
        self.calls = []

    def __getattr__(self, name):
        def f(*a, **kw):
            self.calls.append((name, a, kw))
            return self
        return f


class _Node:
    __slots__ = ("e", "kind", "calls", "reads", "writes", "cost", "deps")

    def __init__(self, e, kind, calls, reads, writes, cost):
        self.e, self.kind, self.calls, self.reads, self.writes, self.cost = e, kind, calls, reads, writes, cost
        self.deps = ()


def _free_elems(ap):
    try:
        n = 1
        for s_ in list(ap.shape)[1:]:
            n *= int(s_)
        return n
    except Exception:
        return 256


def _ap_bytes(ap):
    try:
        n = 1
        for s_ in list(ap.shape):
            n *= int(s_)
        return n * 4
    except Exception:
        return 65536


def _est_cost(e, calls):
    t = 0.0
    for name, a, kw in calls:
        out = kw.get("out", a[0] if a else None)
        n = _free_elems(out) if out is not None else 256
        if e == "pe":
            mul = 4.0 if (name == "matmul" and getattr(kw.get("lhsT"), "dtype", None) == F32) else 1.0
            t += mul * max(n, 64) / 1.6 + 25.0
        elif e == "act":
            t += n / 1.0 + 220.0
        elif e == "dve":
            t += n / 0.9 + 80.0
        else:
            t += n * 2.0 + 300.0
    return t


class KB:
    def __init__(self, sched=True):
        nc = bass.Bass("TRN2", target_bir_lowering=False)
        self.nc = nc
        self.E = {"pe": nc.tensor, "act": nc.scalar, "dve": nc.vector, "pool": nc.gpsimd, "sp": nc.sync}
        self.sems = {}
        self.cnt = {}
        for e in ("pe", "act", "dve", "pool"):
            self.sems[e] = nc.alloc_semaphore("s_" + e)
            self.cnt[e] = 0
        self.dq = {}
        for q, nds in (("sp", 12), ("pool", NDS), ("act", 6)):
            keys = []
            for i in range(nds):
                k = "d_%s%d" % (q, i)
                self.sems[k] = nc.alloc_semaphore(k)
                self.cnt[k] = 0
                keys.append(k)
            self.dq[q] = [keys, 0]
        self.seen = {e: {} for e in self.E}
        self.nops = 0
        self.sched = sched
        self.pending = []

    def _waits(self, e, reads, writes, extra=()):
        need = {}

        def add(rec):
            if rec is None:
                return
            k, c = rec
            if need.get(k, 0) < c:
                need[k] = c

        for r in reads:
            add(r.lw)
        for w in writes:
            add(w.lw)
            for k, c in w.rd.items():
                add((k, c))
        for rec in extra:
            add(rec)
        eng = self.E[e]
        seen = self.seen[e]
        for k, c in need.items():
            if e == "pe" and k == "pe":
                continue
            if seen.get(k, 0) >= c:
                continue
            eng.wait_ge(self.sems[k], c)
            seen[k] = c

    def _mark(self, rec, reads, writes):
        k, c = rec
        for r in reads:
            if r.rd.get(k, 0) < c:
                r.rd[k] = c
        for w in writes:
            w.lw = rec
            w.rd = {}

    def op(self, e, fn, reads=(), writes=()):
        rec = _Rec()
        fn(rec)
        node = _Node(e, "op", rec.calls, list(reads), list(writes), _est_cost(e, rec.calls))
        if self.sched:
            self.pending.append(node)
        else:
            self._emit(node)

    def dma(self, q, out, in_, reads=(), writes=(), **kw):
        node = _Node(q, "dma", (out, in_, kw), list(reads), list(writes), 2000.0 + _ap_bytes(out) / 100.0)
        if self.sched:
            self.pending.append(node)
        else:
            self._emit(node)

    def _emit(self, n):
        e = n.e
        if n.kind == "op":
            self._waits(e, n.reads, n.writes)
            eng = self.E[e]
            inst = None
            for name, a, kw in n.calls:
                inst = getattr(eng, name)(*a, **kw)
            self.cnt[e] += 1
            inst.then_inc(self.sems[e], 1)
            self._mark((e, self.cnt[e]), n.reads, n.writes)
        else:
            out, in_, kw = n.calls
            keys, i = self.dq[e]
            k = keys[i]
            self.dq[e][1] = (i + 1) % len(keys)
            extra = [(k, self.cnt[k])] if self.cnt[k] else []
            self._waits(e, n.reads, n.writes, extra)
            self.E[e].dma_start(out=out, in_=in_, **kw).then_inc(self.sems[k], 16)
            self.cnt[k] += 16
            self._mark((k, self.cnt[k]), n.reads, n.writes)
        self.nops += 1

    def flush(self):
        nodes = self.pending
        self.pending = []
        if not nodes:
            return
        lastw, readers = {}, {}
        for i, n in enumerate(nodes):
            d = set()
            for r in n.reads:
                if id(r) in lastw:
                    d.add(lastw[id(r)])
            for w in n.writes:
                if id(w) in lastw:
                    d.add(lastw[id(w)])
                d.update(readers.get(id(w), ()))
            d.discard(i)
            n.deps = d
            for r in n.reads:
                readers.setdefault(id(r), []).append(i)
            for w in n.writes:
                lastw[id(w)] = i
                readers[id(w)] = []
        queues = {}
        for i, n in enumerate(nodes):
            queues.setdefault(n.e, []).append(i)
        fin = [None] * len(nodes)
        efree = {e: 0.0 for e in queues}
        W, LAT = 32, 120.0
        remaining = len(nodes)
        while remaining:
            best = None
            for e, q in queues.items():
                for i in q[:W]:
                    n = nodes[i]
                    ready = 0.0
                    ok = True
                    for d in n.deps:
                        f = fin[d]
                        if f is None:
                            ok = False
                            break
                        if f + LAT > ready:
                            ready = f + LAT
                    if not ok:
                        continue
                    start = ready if ready > efree[e] else efree[e]
                    key = (start, i)
                    if best is None or key < best[0]:
                        best = (key, e, i)
            assert best is not None
            (start, _), e, i = best
            n = nodes[i]
            fin[i] = start + n.cost
            efree[e] = (start + 60.0) if n.kind == "dma" else fin[i]
            queues[e].remove(i)
            self._emit(n)
            remaining -= 1

    def barrier(self):
        self.flush()
        for e in self.E:
            for k, c in self.cnt.items():
                if c == 0 or (e == "pe" and k == "pe"):
                    continue
                if self.seen[e].get(k, 0) >= c:
                    continue
                self.E[e].wait_ge(self.sems[k], c)
                self.seen[e][k] = c

    def finish(self):
        self.barrier()


def bcast_rows(t, off, n, parts=128):
    return bass.AP(t, off, [[0, parts], [1, n]])


def build(cfg):
    kb = KB(sched=cfg.get("sched", True))
    nc = kb.nc
    NSEQ = cfg.get("nseq", 2)
    NLAY = cfg.get("nlay", 2)
    mixers = cfg.get("mixers", ("gla", "rwkv", "nsa"))
    dumps = cfg.get("dumps", ())
    inject_O = cfg.get("inject_O", False)

    def dram_in(name, shape):
        return nc.dram_tensor(name, list(shape), F32, kind="ExternalInput")

    x_d = dram_in("x", [2, S, D])
    w_in = dram_in("w_in", [L, D, 3400])
    w_vres = dram_in("w_in_vres", [1, D, 32])
    w_nsa = dram_in("w_nsa", [L, D, 2200])
    gla_w_a2 = dram_in("gla_w_a2", [L, 16, 256])
    rwkv_w2 = dram_in("rwkv_w2", [L, 64, 256])
    rwkv_a2 = dram_in("rwkv_a2", [L, 64, 256])
    rwkv_v2 = dram_in("rwkv_v2", [1, 32, 256])
    rwkv_g2 = dram_in("rwkv_g2", [L, 160, 256])
    nsa_wk1 = dram_in("nsa_wk1", [L, 2048, 256])
    nsa_wk2 = dram_in("nsa_wk2", [L, 256, 64])
    nsa_wv1 = dram_in("nsa_wv1", [L, 2048, 256])
    nsa_wv2 = dram_in("nsa_wv2", [L, 256, 64])
    nsa_posT = dram_in("nsa_posT", [L, 2, 64, 32])
    w_out = dram_in("w_out", [L, D, D])
    w_gate = dram_in("ffn_w_gate", [L, D, FF])
    w_up = dram_in("ffn_w_up", [L, D, FF])
    w_down = dram_in("ffn_w_down", [L, FF, D])
    ptab = dram_in("ptab", [L, 128, 32])
    rowtab = dram_in("rowtab", [L, 5120])
    c_ident = dram_in("c_ident", [128, 128])
    c_caus = dram_in("c_caus", [128, 128])
    c_rope = dram_in("c_rope", [2, 128, S])
    c_cmpmask = dram_in("c_cmpmask", [128, S])
    c_onehot = dram_in("c_onehot", [32, S])
    c_selm = dram_in("c_selm", [2, 128, NT, 32])
    c_ovl = dram_in("c_ovl", [128, 32])
    c_blk = dram_in("c_blk", [128, 128])
    c_hsel = dram_in("c_hsel", [128, 2])
    c_rwmask = dram_in("c_rwmask", [64, 5, 64])
    if inject_O:
        dbg_OT = dram_in("dbg_OT", [128, 8, S])
    out_d = nc.dram_tensor("out", [2, S, D], F32, kind="ExternalOutput")
    xres = [nc.dram_tensor("xres%d" % i, [S, D], F32, kind="Internal") for i in range(2)]
    xres_r = [regs(NT) for _ in range(2)]
    vfirst_d = nc.dram_tensor("vfirst", [2, 128, S], F32, kind="Internal")
    vfirst_r = regs(2)
    dump_d = {}
    for name, shape in dumps:
        dump_d[name] = nc.dram_tensor("dump_" + name, list(shape), F32, kind="ExternalOutput")

    XT = nc.alloc_sbuf_tensor("XT", [128, 8, S], BF16)
    XT_r = regs(NT)
    ident = nc.alloc_sbuf_tensor("ident", [128, 128], BF16)
    identf = nc.alloc_sbuf_tensor("identf", [128, 128], F32)
    caus = nc.alloc_sbuf_tensor("caus", [128, 4, 128], F32)
    ptb = nc.alloc_sbuf_tensor("ptb", [128, L, 32], F32)
    ptn = nc.alloc_sbuf_tensor("ptn", [128, L, 32], F32)
    c_r = Reg()
    for l in range(L):
        kb.dma("sp", ptb[:, l, :], ptab.ap()[l], writes=[c_r])
    kb.dma("pool", ident[:], c_ident.ap(), writes=[c_r])
    kb.dma("sp", identf[:], c_ident.ap(), writes=[c_r])
    for i in range(4):
        kb.dma("sp", caus[:, i, :], c_caus.ap(), writes=[c_r])

    def dram16(name, shape):
        return nc.dram_tensor(name, list(shape), BF16, kind="Internal")

    conv_list = [(w_in, [L, D, 3400]), (w_nsa, [L, D, 2200]), (w_out, [L, D, D]), (w_gate, [L, D, FF]), (w_up, [L, D, FF]),
                 (w_down, [L, FF, D]), (nsa_wk1, [L, 2048, 256]), (nsa_wv1, [L, 2048, 256])]
    w16 = {}
    w16_r = Reg()
    CH = 2816
    with nc.sbuf_tensor("cv_f", [128, 3, CH], F32) as cvf, nc.sbuf_tensor("cv_b", [128, 3, CH], BF16) as cvb:
        cvf_r, cvb_r = regs(3), regs(3)
        job = 0
        for src, shape in conv_list:
            dst = dram16(src.name + "_16", shape)
            w16[src.name] = dst
            tot = 1
            for d_ in shape:
                tot *= d_
            per = tot // 128
            assert per * 128 == tot
            for c0 in range(0, per, CH):
                n = min(CH, per - c0)
                i = job % 3
                kb.dma("sp", cvf[:, i, 0:n], bass.AP(src, c0, [[per, 128], [1, n]]), writes=[cvf_r[i]])
                eng = "act"
                if eng == "act":
                    kb.op("act", lambda a: a.activation(out=cvb[:, i, 0:n], in_=cvf[:, i, 0:n], func=AF.Copy),
                          reads=[cvf_r[i]], writes=[cvb_r[i]])
                else:
                    kb.op(eng, lambda v: v.tensor_scalar(out=cvb[:, i, 0:n], in0=cvf[:, i, 0:n], scalar1=1.0, scalar2=None, op0=ALU.mult),
                          reads=[cvf_r[i]], writes=[cvb_r[i]])
                kb.dma("act", bass.AP(dst, c0, [[per, 128], [1, n]]), cvb[:, i, 0:n], reads=[cvb_r[i]], writes=[w16_r])
                job += 1
        kb.barrier()
    w_in16, w_nsa16, w_out16 = w16["w_in"], w16["w_nsa"], w16["w_out"]
    w_gate16, w_up16, w_down16 = w16["ffn_w_gate"], w16["ffn_w_up"], w16["ffn_w_down"]
    wk1_16, wv1_16 = w16["nsa_wk1"], w16["nsa_wv1"]

    PS = [nc.alloc_psum_tensor("ps%d" % i, [128, 512], F32) for i in range(6)]
    PS_r = regs(6)
    PB = [nc.alloc_psum_tensor("pb%d" % i, [128, 1024], BF16) for i in range(2)]
    PB_r = regs(2)

    _uid = [0]

    def sb(name, shape, dt):
        _uid[0] += 1
        return nc.sbuf_tensor("%s_%d" % (name, _uid[0]), list(shape), dt)

    def proj_fm(ps_ap, Wt, c0, M, tok0, N, wreads, treads, pw, kparts=8):
        def fn(pe):
            inst = None
            for k in range(kparts):
                inst = pe.matmul(ps_ap, lhsT=Wt[:, k, c0:c0 + M], rhs=XT[:, k, tok0:tok0 + N],
                                 start=(k == 0), stop=(k == kparts - 1))
            return inst
        kb.op("pe", fn, reads=list(wreads) + list(treads), writes=[pw])

    def proj_tm(ps_ap, Wt, c0, N, tt, wreads, pw):
        def fn(pe):
            inst = None
            for k in range(8):
                inst = pe.matmul(ps_ap, lhsT=XT[:, k, tt * 128:(tt + 1) * 128], rhs=Wt[:, k, c0:c0 + N],
                                 start=(k == 0), stop=(k == 7))
            return inst
        kb.op("pe", fn, reads=list(wreads) + [XT_r[tt]], writes=[pw])

    def load_w(Wt, src3, ncols, wreg, q="sp", chunk=None):
        for k in range(8):
            kb.dma(q, Wt[:, k, 0:ncols], src3[k * 128:(k + 1) * 128, 0:ncols], reads=[w16_r], writes=[wreg])

    xb = [nc.alloc_sbuf_tensor("xb%d" % i, [128, D], BF16) for i in range(2)]
    xb_r = regs(2)
    xb_i = [0]

    def make_xT(src_ap, src_reg, tt):
        i = xb_i[0]
        xb_i[0] ^= 1
        kb.op("act", lambda a: a.activation(out=xb[i][:], in_=src_ap, func=AF.Copy),
              reads=[src_reg], writes=[xb_r[i]])
        pb = PB[i]

        def fn(pe):
            inst = None
            for k in range(8):
                inst = pe.transpose(out=pb[:, k * 128:(k + 1) * 128], in_=xb[i][:, k * 128:(k + 1) * 128],
                                    identity=ident[:])
            return inst
        kb.op("pe", fn, reads=[xb_r[i], c_r], writes=[PB_r[i]])
        kb.op("dve", lambda v: v.tensor_copy(out=XT[:, :, tt * 128:(tt + 1) * 128],
                                             in_=pb[:, :].rearrange("p (k t) -> p k t", k=8)),
              reads=[PB_r[i]], writes=[XT_r[tt]])

    def layer_norm(xt_ap, xreg, lnw_ap, lnb_ap, lnreg, st, st_r):
        kb.op("dve", lambda v: v.bn_stats(out=st[:, 0:6], in_=xt_ap[:, 0:512]), reads=[xreg], writes=[st_r])
        kb.op("dve", lambda v: v.bn_stats(out=st[:, 6:12], in_=xt_ap[:, 512:1024]), reads=[xreg, st_r], writes=[st_r])
        kb.op("dve", lambda v: v.bn_aggr(out=st[:, 12:14], in_=st[:, 0:12]), reads=[st_r], writes=[st_r])
        kb.op("act", lambda a: a.activation(out=st[:, 14:15], in_=st[:, 13:14], func=AF.Sqrt, bias=LN_EPS, scale=1.0),
              reads=[st_r], writes=[st_r])
        kb.op("dve", lambda v: v.reciprocal(out=st[:, 15:16], in_=st[:, 14:15]), reads=[st_r], writes=[st_r])
        kb.op("dve", lambda v: v.scalar_tensor_tensor(out=st[:, 16:17], in0=st[:, 12:13], scalar=-1.0, in1=st[:, 15:16],
                                                      op0=ALU.mult, op1=ALU.mult), reads=[st_r], writes=[st_r])
        kb.op("act", lambda a: a.activation(out=xt_ap, in_=xt_ap, func=AF.Identity, bias=st[:, 16:17], scale=st[:, 15:16]),
              reads=[st_r, xreg], writes=[xreg])
        kb.op("dve", lambda v: v.tensor_tensor(out=xt_ap, in0=xt_ap, in1=lnw_ap, op=ALU.mult), reads=[xreg, lnreg], writes=[xreg])
        kb.op("dve", lambda v: v.tensor_tensor(out=xt_ap, in0=xt_ap, in1=lnb_ap, op=ALU.add), reads=[xreg, lnreg], writes=[xreg])

    def dump(name, sb_ap, reg, idx=None):
        if name in dump_d:
            dst = dump_d[name].ap() if idx is None else dump_d[name].ap()[idx]
            kb.dma("sp", dst, sb_ap, reads=[reg])

    kb.op("dve", lambda v: v.tensor_scalar(out=ptn[:, :, :], in0=ptb[:, :, :], scalar1=-1.0, scalar2=None, op0=ALU.mult),
          reads=[c_r], writes=[c_r])
    kb.op("dve", lambda v: v.tensor_scalar(out=ptn[:, :, 2:13], in0=ptb[:, :, 2:13], scalar1=-1.0, scalar2=1.0,
                                           op0=ALU.mult, op1=ALU.add), reads=[c_r], writes=[c_r])

    def gla_phase(sq, l, OT, OT_r):
        with ExitStack() as es:
            A = lambda n_, s_, d_: es.enter_context(sb(n_, s_, d_))
            Wg = A("Wg", [128, 8, 1152], BF16)
            wa2 = A("wa2", [16, 256], BF16)
            gln = A("gln", [128, 2, 256], F32)
            alrT = A("alrT", [16, 512], BF16)
            t1 = A("gt1", [128, 512], F32)
            t2 = A("gt2", [128, 512], F32)
            EB = A("EB", [128, 2, 512], F32)
            QTl = A("gQT", [128, 2, 2, 512], BF16)
            KTl = A("gKT", [128, 2, 512], BF16)
            Ktok = A("gKtok", [128, 4, 256], BF16)
            V = A("gV", [128, 4, 256], BF16)
            SG = A("gSG", [128, 4, 256], F32)
            AT = A("gAT", [128, 4, 128], BF16)
            St = A("gSt", [128, 2, 128], F32)
            tmpS = A("gtmpS", [128, 128], F32)
            blkm = A("gblk", [128, 128], F32)
            Sb = A("gSb", [128, 2, 128], BF16)
            rmask = A("grm", [128, 512], F32)
            ow = A("gow", [128, 256], F32)
            ow2 = A("gow2", [128, 256], F32)
            ob = A("gob", [128, 256], BF16)
            st = A("gst", [128, 32], F32)
            W_r, p_r, alr_r, t1_r, t2_r = Reg(), Reg(), Reg(), Reg(), Reg()
            EB_r, QT_r, KT_r = regs(2), regs(2), regs(2)
            Ktok_r, V_r, SG_r, AT_r, St_r, Sb_r = Reg(), regs(4), regs(4), Reg(), Reg(), Reg()
            ow_r, ow2_r, ob_r, st_r = Reg(), Reg(), Reg(), Reg()
            gs = cfg.get("gla_stop", 99)
            load_w(Wg, w_in16.ap()[l][:, 0:1040], 1040, W_r)
            kb.dma("pool", wa2[:, :], gla_w_a2.ap()[l], writes=[p_r])
            kb.dma("sp", gln[:, 0, :], bcast_rows(rowtab, l * 5120 + 0, 256), writes=[p_r])
            kb.dma("sp", gln[:, 1, :], bcast_rows(rowtab, l * 5120 + 256, 256), writes=[p_r])
            kb.op("pool", lambda g: g.memset(rmask[:, :], 1.0), writes=[p_r])
            kb.op("pool", lambda g: g.memset(rmask[:, :].rearrange("p (c t) -> p c t", c=4)[:, :, 0:1], 0.0), writes=[p_r])
            kb.op("pool", lambda g: g.memset(St[:, :, :], 0.0), writes=[St_r])
            kb.op("pool", lambda g: g.memset(QTl[:, :, :, :], 0.0), writes=QT_r)
            kb.dma("sp", blkm[:, :], c_blk.ap(), writes=[p_r])
            tmpS_r = Reg()
            kb.op("pool", lambda g: g.memset(Sb[:, :, :], 0.0), writes=[Sb_r])
            for mt in range(4 if gs > 0 else 0):
                tok0 = mt * 512
                xr = XT_r[mt * 4:mt * 4 + 4]
                proj_fm(PS[0][0:16, :], Wg, 1024, 16, tok0, 512, [W_r], xr, PS_r[0])
                kb.op("act", lambda a: a.activation(out=alrT[:, :], in_=PS[0][0:16, :], func=AF.Copy),
                      reads=[PS_r[0]], writes=[alr_r])
                for hp in range(2):
                    kb.op("pe", lambda pe: pe.matmul(PS[1][:, :], lhsT=wa2[0:16, hp * 128:(hp + 1) * 128], rhs=alrT[0:16, :],
                                                     start=True, stop=True), reads=[p_r, alr_r], writes=[PS_r[1]])
                    kb.op("act", lambda a: a.activation(out=t1[:, :], in_=PS[1][:, :], func=AF.Exp, scale=-1.0,
                                                        bias=ptn[:, l, hp:hp + 1]), reads=[PS_r[1], c_r], writes=[t1_r])
                    kb.op("act", lambda a: a.activation(out=t1[:, :], in_=t1[:, :], func=AF.Ln, bias=1.0, scale=1.0),
                          reads=[t1_r], writes=[t1_r])
                    kb.op("dve", lambda v: v.tensor_tensor_scan(out=t2[:, :], data0=rmask[:, :], data1=t1[:, :], initial=0.0,
                                                                op0=ALU.mult, op1=ALU.add), reads=[t1_r, p_r], writes=[t2_r])
                    kb.op("act", lambda a: a.activation(out=EB[:, hp, :], in_=t2[:, :], func=AF.Exp, scale=-1.0 / 16.0),
                          reads=[t2_r], writes=[EB_r[hp]])
                    kb.op("act", lambda a: a.activation(out=t1[:, :], in_=t2[:, :], func=AF.Exp, scale=1.0 / 16.0),
                          reads=[t2_r], writes=[t1_r])
                    proj_fm(PS[2][:, :], Wg, hp * 128, 128, tok0, 512, [W_r], xr, PS_r[2])
                    for hh in range(2):
                        pr = slice(hh * 64, hh * 64 + 64)
                        kb.op("dve", lambda v: v.scalar_tensor_tensor(out=QTl[pr, hp, hh, :], in0=PS[2][pr, :], scalar=0.125,
                                                                      in1=EB[pr, hp, :], op0=ALU.mult, op1=ALU.mult),
                              reads=[PS_r[2], EB_r[hp]], writes=[QT_r[hp]])
                    proj_fm(PS[3][:, :], Wg, 256 + hp * 128, 128, tok0, 512, [W_r], xr, PS_r[3])
                    kb.op("dve", lambda v: v.tensor_tensor(out=KTl[:, hp, :], in0=PS[3][:, :], in1=t1[:, :], op=ALU.mult),
                          reads=[PS_r[3], t1_r], writes=[KT_r[hp]])

                    def fn(pe):
                        inst = None
                        for j in range(4):
                            inst = pe.transpose(out=PB[0][:, j * 128:(j + 1) * 128], in_=KTl[:, hp, j * 128:(j + 1) * 128],
                                                identity=ident[:])
                        return inst
                    kb.op("pe", fn, reads=[KT_r[hp], c_r], writes=[PB_r[0]])
                    kb.op("dve", lambda v: v.tensor_copy(out=Ktok[:, :, hp * 128:(hp + 1) * 128],
                                                         in_=PB[0][:, 0:512].rearrange("p (j c) -> p j c", j=4)),
                          reads=[PB_r[0]], writes=[Ktok_r])
                for j in range(4 if gs > 1 else 0):
                    tt = mt * 4 + j
                    proj_tm(PS[4][:, 0:256], Wg, 512, 256, tt, [W_r], PS_r[4])
                    proj_tm(PS[5][:, 0:256], Wg, 768, 256, tt, [W_r], PS_r[5])
                    if cfg.get("gv", 3) >= 2:
                        kb.op("dve", lambda v: v.tensor_scalar(out=V[:, j, :], in0=PS[4][:, 0:256], scalar1=1.0, scalar2=None, op0=ALU.mult),
                              reads=[PS_r[4]], writes=[V_r[j]])
                    if cfg.get("gv", 3) >= 3:
                        kb.op("act", lambda a: a.activation(out=SG[:, j, :], in_=PS[5][:, 0:256], func=AF.Silu),
                              reads=[PS_r[5]], writes=[SG_r[j]])
                for j in range(4 if gs > 2 else 0):
                    tt = mt * 4 + j
                    cc = slice(j * 128, (j + 1) * 128)

                    def fn(pe):
                        inst = None
                        for h in range(4):
                            hp, hh = h // 2, h % 2
                            inst = pe.matmul(PS[0][:, h * 128:(h + 1) * 128], lhsT=KTl[:, hp, cc], rhs=QTl[:, hp, hh, cc],
                                             start=True, stop=True)
                        return inst
                    kb.op("pe", fn, reads=QT_r + KT_r, writes=[PS_r[0]])
                    kb.op("dve", lambda v: v.tensor_tensor(out=AT[:, :, :], in0=PS[0][:, :].rearrange("p (h t) -> p h t", h=4),
                                                          in1=caus[:, :, :], op=ALU.mult), reads=[PS_r[0], c_r], writes=[AT_r])

                    def fn(pe):
                        inst = None
                        for h in range(4):
                            hp, hh = h // 2, h % 2
                            pe.matmul(PS[1][:, h * 64:(h + 1) * 64], lhsT=AT[:, h, :], rhs=V[:, j, h * 64:(h + 1) * 64],
                                      start=True, stop=False)
                            inst = pe.matmul(PS[1][:, h * 64:(h + 1) * 64], lhsT=QTl[:, hp, hh, cc], rhs=Sb[:, hp, hh * 64:(hh + 1) * 64],
                                             start=False, stop=True)
                        return inst
                    if gs <= 3:
                        continue
                    kb.op("pe", fn, reads=[AT_r, V_r[j], Sb_r] + QT_r, writes=[PS_r[1]])

                    def fn(pe):
                        inst = None
                        for hp in range(2):
                            inst = pe.matmul(PS[2][:, hp * 128:(hp + 1) * 128], lhsT=Ktok[:, j, hp * 128:(hp + 1) * 128],
                                             rhs=V[:, j, hp * 128:(hp + 1) * 128], start=True, stop=True)
                        return inst
                    if gs <= 4:
                        continue
                    kb.op("pe", fn, reads=[Ktok_r, V_r[j]], writes=[PS_r[2]])
                    for hp in range(2):
                        ee = EB[:, hp, j * 128 + 127:j * 128 + 128]
                        kb.op("dve", lambda v: v.scalar_tensor_tensor(out=tmpS[:, :], in0=PS[2][:, hp * 128:(hp + 1) * 128], scalar=ee,
                                                                      in1=blkm[:, :], op0=ALU.mult, op1=ALU.mult),
                              reads=[PS_r[2], EB_r[hp], p_r], writes=[tmpS_r])
                        kb.op("dve", lambda v: v.scalar_tensor_tensor(out=St[:, hp, :], in0=St[:, hp, :], scalar=ee, in1=tmpS[:, :],
                                                                      op0=ALU.mult, op1=ALU.add),
                              reads=[St_r, tmpS_r, EB_r[hp]], writes=[St_r])
                    kb.op("act", lambda a: a.activation(out=Sb[:, :, :], in_=St[:, :, :], func=AF.Copy), reads=[St_r], writes=[Sb_r])
                    if gs <= 5:
                        continue
                    head_norm_gate(PS[1][:, 0:256], PS_r[1], ow, ow_r, ow2, ow2_r, st, st_r, gln, p_r, LN_EPS)
                    kb.op("dve", lambda v: v.tensor_tensor(out=ob[:, :], in0=ow[:, :], in1=SG[:, j, :], op=ALU.mult),
                          reads=[ow_r, SG_r[j]], writes=[ob_r])
                    if gs <= 6:
                        continue
                    out_to_OT(ob, ob_r, 128, OT, OT_r, 0, tt * 128)
            kb.barrier()

    def head_norm_gate(ps_ap, ps_r, ow, ow_r, ow2, ow2_r, st, st_r, gln, gln_r, eps, P=128):
        v4 = lambda ap: ap.rearrange("p (h d) -> p h d", h=4)
        kb.op("act", lambda a: a.activation(out=ow[0:P, :], in_=ps_ap, func=AF.Copy), reads=[ps_r], writes=[ow_r])
        kb.op("act", lambda a: a.activation(out=ow2[0:P, :], in_=ow[0:P, :], func=AF.Square), reads=[ow_r], writes=[ow2_r])
        kb.op("dve", lambda v: v.reduce_sum(out=st[0:P, 0:4], in_=v4(ow[0:P, :]), axis=AX.X), reads=[ow_r], writes=[st_r])
        kb.op("dve", lambda v: v.reduce_sum(out=st[0:P, 4:8], in_=v4(ow2[0:P, :]), axis=AX.X), reads=[ow2_r, st_r], writes=[st_r])
        kb.op("dve", lambda v: v.tensor_scalar(out=st[0:P, 8:16], in0=st[0:P, 0:8], scalar1=1.0 / 64.0, scalar2=None, op0=ALU.mult),
              reads=[st_r], writes=[st_r])
        kb.op("dve", lambda v: v.tensor_tensor(out=st[0:P, 16:20], in0=st[0:P, 8:12], in1=st[0:P, 8:12], op=ALU.mult),
              reads=[st_r], writes=[st_r])
        kb.op("dve", lambda v: v.tensor_tensor(out=st[0:P, 20:24], in0=st[0:P, 12:16], in1=st[0:P, 16:20], op=ALU.subtract),
              reads=[st_r], writes=[st_r])
        kb.op("act", lambda a: a.activation(out=st[0:P, 24:28], in_=st[0:P, 20:24], func=AF.Sqrt, bias=eps, scale=1.0),
              reads=[st_r], writes=[st_r])
        kb.op("dve", lambda v: v.reciprocal(out=st[0:P, 28:32], in_=st[0:P, 24:28]), reads=[st_r], writes=[st_r])
        for h in range(4):
            kb.op("dve", lambda v: v.tensor_scalar(out=ow[0:P, h * 64:(h + 1) * 64], in0=ow[0:P, h * 64:(h + 1) * 64],
                                                   scalar1=st[0:P, 8 + h:9 + h], scalar2=st[0:P, 28 + h:29 + h],
                                                   op0=ALU.subtract, op1=ALU.mult), reads=[ow_r, st_r], writes=[ow_r])
        kb.op("dve", lambda v: v.tensor_tensor(out=ow[0:P, :], in0=ow[0:P, :], in1=gln[0:P, 0, :], op=ALU.mult),
              reads=[ow_r, gln_r], writes=[ow_r])
        kb.op("dve", lambda v: v.tensor_tensor(out=ow[0:P, :], in0=ow[0:P, :], in1=gln[0:P, 1, :], op=ALU.add),
              reads=[ow_r, gln_r], writes=[ow_r])

    def out_to_OT(ob, ob_r, P, OT, OT_r, k0, tokc0, ncols=256):
        nk = ncols // 128

        def fn(pe):
            inst = None
            for kk in range(nk):
                inst = pe.transpose(out=PB[1][:, kk * 128:kk * 128 + P], in_=ob[0:P, kk * 128:(kk + 1) * 128],
                                    identity=ident[0:P, 0:P])
            return inst
        kb.op("pe", fn, reads=[ob_r, c_r], writes=[PB_r[1]])
        tr = OT_r[tokc0 // 128]
        kb.op("act", lambda a: a.activation(out=OT[:, k0:k0 + nk, tokc0:tokc0 + P],
                                            in_=PB[1][:, 0:nk * 128].rearrange("p (k t) -> p k t", k=nk)[:, :, 0:P], func=AF.Copy),
              reads=[PB_r[1]], writes=[tr])

    RWT = [(0, 128), (128, 128), (256, 128), (384, 128), (512, 128), (640, 128), (768, 64), (832, 64),
           (896, 128), (1024, 32), (1056, 32)]
    C0 = float(math.exp(-0.5))

    def rwkv_phase(sq, l, OT, OT_r):
        MT = 256
        NCH = MT // 64
        NU = NCH * 4
        with ExitStack() as es:
            A = lambda n_, s_, d_: es.enter_context(sb(n_, s_, d_))
            Wr = A("Wr", [128, 8, 1088], BF16)
            w2 = A("rw2", [64, 256], BF16)
            a2 = A("ra2", [64, 256], BF16)
            v2 = A("rv2", [32, 256], BF16)
            g2a = A("rg2a", [128, 256], BF16)
            g2b = A("rg2b", [128, 256], BF16)
            rln = A("rln", [128, 2, 256], F32)
            blkm = A("rblk", [128, 128], F32)
            blkf = A("rblkf", [128, 128], F32)
            hselb = A("rhsel", [128, 2], BF16)
            rmask = A("rrm", [128, MT], F32)
            amask = A("ramask", [64, 5, 64], F32)
            xs = [A("rxs%d" % i, [128, MT], F32) for i in range(6)]
            tt_ = [A("rt%d" % i, [128, MT], F32) for i in range(8)]
            Gam = A("rGam", [128, 2, MT], F32)
            ATz = A("rATz", [128, 2, 2, MT], BF16)
            RTz = A("rRTz", [128, 2, 2, MT], BF16)
            BTt = A("rBT", [128, 2, MT], BF16)
            KTt = A("rKT", [128, 2, MT], BF16)
            rkr = A("rrkr", [128, 2, MT], BF16)
            TW = A("rTW", [64, MT], BF16)
            AL = A("rAL", [64, MT], BF16)
            SGL = A("rSGL", [128, MT], BF16)
            SGL2 = A("rSGL2", [128, MT], BF16)
            VR = A("rVR", [32, MT], BF16)
            vb = A("rvb", [128, MT], BF16)
            Vtok = A("rVtok", [128, NCH, 256], BF16)
            Btok = A("rBtok", [128, NCH, 256], BF16)
            Ktok = A("rKtok", [128, NCH, 256], BF16)
            gtok = A("rgtok", [64, NCH, 256], F32)
            cb = A("rcb", [64, NCH, 4], F32)
            MN = [A("rMN%d" % i, [64, NU, 2, 64], F32) for i in range(2)]
            Pm = A("rP", [64, NU, 64], F32)
            A3 = A("rA3", [128, NU, 3, 64], BF16)
            Rs = A("rRs", [64, 256], F32)
            Ub = A("rUb", [128, 256], BF16)
            Tblk = A("rTblk", [128, 2, 128], F32)
            Tb = A("rTb", [128, 2, 128], BF16)
            tmpS = A("rtmpS", [128, 128], F32)
            ow = A("row", [128, 256], F32)
            ow2 = A("row2", [128, 256], F32)
            ob = A("rob", [128, 256], BF16)
            st = A("rst", [128, 32], F32)
            W_r, p_r = Reg(), Reg()
            xs_r, t_r = regs(6), regs(8)
            Gam_r, ATz_r, RTz_r, BT_r, KT_r, rkr_r = regs(2), regs(2), regs(2), regs(2), regs(2), regs(2)
            TW_r, AL_r, SGL_r, SGL2_r, VR_r, vb_r = Reg(), Reg(), Reg(), Reg(), Reg(), Reg()
            Vtok_r, Btok_r, Ktok_r, gtok_r, cb_r = Reg(), Reg(), Reg(), Reg(), Reg()
            MN_r, P_r, A3_r, Rs_r, Ub_r, T_r, Tb_r, tmpS_r = regs(2), Reg(), Reg(), Reg(), Reg(), Reg(), Reg(), Reg()
            ow_r, ow2_r, ob_r, st_r = Reg(), Reg(), Reg(), Reg()
            rs_ = cfg.get("rw_stop", 99)
            load_w(Wr, w_in16.ap()[l][:, 1040:2096], 1056, W_r)
            if l >= 1:
                for k in range(8):
                    kb.dma("pool", Wr[:, k, 1056:1088], w_vres.ap()[l - 1][k * 128:(k + 1) * 128, :], writes=[W_r])
            kb.dma("pool", w2[:, :], rwkv_w2.ap()[l], writes=[p_r])
            kb.dma("pool", a2[:, :], rwkv_a2.ap()[l], writes=[p_r])
            if l >= 1:
                kb.dma("pool", v2[:, :], rwkv_v2.ap()[l - 1], writes=[p_r])
            kb.op("pool", lambda g: g.memset(g2b[:, :], 0.0), writes=[p_r])
            kb.dma("pool", g2a[:, :], rwkv_g2.ap()[l][0:128, :], writes=[p_r])
            kb.dma("pool", g2b[0:32, :], rwkv_g2.ap()[l][128:160, :], reads=[p_r], writes=[p_r])
            kb.dma("sp", rln[:, 0, :], bcast_rows(rowtab, l * 5120 + 512, 256), writes=[p_r])
            kb.dma("sp", rln[:, 1, :], bcast_rows(rowtab, l * 5120 + 768, 256), writes=[p_r])
            kb.dma("sp", blkm[:, :], c_blk.ap(), writes=[p_r])
            kb.dma("sp", blkf[:, :], c_blk.ap(), writes=[p_r])
            kb.dma("pool", hselb[:, :], c_hsel.ap(), writes=[p_r])
            kb.dma("sp", amask[:, :, :], c_rwmask.ap(), writes=[p_r])
            kb.op("pool", lambda g: g.memset(rmask[:, :], 1.0), writes=[p_r])
            kb.op("pool", lambda g: g.memset(rmask[:, :].rearrange("p (c t) -> p c t", c=NCH)[:, :, 0:1], 0.0), writes=[p_r])
            for tz, rr in ((Tblk, [T_r]), (Tb, [Tb_r]), (ATz, ATz_r), (RTz, RTz_r), (SGL2, [SGL2_r]), (Vtok, [Vtok_r]),
                           (Btok, [Btok_r]), (Ktok, [Ktok_r]), (A3, [A3_r]), (Ub, [Ub_r])):
                kb.op("pool", lambda g, tz=tz: g.memset(tz[:], 0.0), writes=rr)

            def shift_proj(i, tok0, dst_fn):
                c0, n = RWT[i]
                mucol = 2 + i
                proj_fm(PS[0][0:n, 0:MT], Wr, c0, n, tok0, MT, [W_r], XT_r[tok0 // 128:tok0 // 128 + MT // 128], PS_r[0])
                tp = tt_[7]
                if tok0 == 0:
                    proj_fm(PS[1][0:n, 1:MT], Wr, c0, n, 0, MT - 1, [W_r], XT_r[0:MT // 128], PS_r[1])
                    kb.op("pool", lambda g: g.memset(tp[0:n, 0:1], 0.0), writes=[t_r[7]])
                    kb.op("act", lambda a: a.activation(out=tp[0:n, 1:MT], in_=PS[1][0:n, 1:MT], func=AF.Copy,
                                                        scale=ptb[0:n, l, mucol:mucol + 1]), reads=[PS_r[1], c_r, t_r[7]], writes=[t_r[7]])
                else:
                    proj_fm(PS[1][0:n, 0:MT], Wr, c0, n, tok0 - 1, MT, [W_r],
                            XT_r[(tok0 - 1) // 128:(tok0 - 1) // 128 + MT // 128 + 1], PS_r[1])
                    kb.op("act", lambda a: a.activation(out=tp[0:n, 0:MT], in_=PS[1][0:n, 0:MT], func=AF.Copy,
                                                        scale=ptb[0:n, l, mucol:mucol + 1]), reads=[PS_r[1], c_r], writes=[t_r[7]])
                dst_ap, dst_regs = dst_fn()
                kb.op("dve", lambda v: v.scalar_tensor_tensor(out=dst_ap, in0=PS[0][0:n, 0:MT], scalar=ptn[0:n, l, mucol:mucol + 1],
                                                              in1=tp[0:n, 0:MT], op0=ALU.mult, op1=ALU.add),
                      reads=[PS_r[0], t_r[7], c_r], writes=dst_regs)

            for mt in range(S // MT if rs_ > 0 else 0):
                tok0 = mt * MT
                for i in range(6):
                    shift_proj(i, tok0, lambda i=i: (xs[i][:, :], [xs_r[i]]))
                shift_proj(6, tok0, lambda: (tt_[0][0:64, :], [t_r[0]]))
                kb.op("act", lambda a: a.activation(out=TW[:, :], in_=tt_[0][0:64, :], func=AF.Tanh), reads=[t_r[0]], writes=[TW_r])
                shift_proj(7, tok0, lambda: (AL[:, :], [AL_r]))
                shift_proj(8, tok0, lambda: (tt_[0][:, :], [t_r[0]]))
                kb.op("act", lambda a: a.activation(out=SGL[:, :], in_=tt_[0][:, :], func=AF.Sigmoid), reads=[t_r[0]], writes=[SGL_r])
                shift_proj(9, tok0, lambda: (tt_[0][0:32, :], [t_r[0]]))
                kb.op("act", lambda a: a.activation(out=SGL2[0:32, :], in_=tt_[0][0:32, :], func=AF.Sigmoid), reads=[t_r[0]], writes=[SGL2_r])
                if l >= 1:
                    shift_proj(10, tok0, lambda: (VR[:, :], [VR_r]))
                if rs_ <= 1:
                    continue
                for hp in range(2):
                    rT, kT, vT = xs[hp], xs[2 + hp], xs[4 + hp]
                    rT_r, kT_r, vT_r = xs_r[hp], xs_r[2 + hp], xs_r[4 + hp]
                    t1, t2, t3, t4, t5, t6, t7 = tt_[0:7]
                    kb.op("pe", lambda pe: pe.matmul(PS[2][:, 0:MT], lhsT=w2[:, hp * 128:(hp + 1) * 128], rhs=TW[:, :], start=True, stop=True),
                          reads=[p_r, TW_r], writes=[PS_r[2]])
                    kb.op("act", lambda a: a.activation(out=t1[:, :], in_=PS[2][:, 0:MT], func=AF.Sigmoid, bias=ptb[:, l, 13 + hp:14 + hp]),
                          reads=[PS_r[2], c_r], writes=[t_r[0]])
                    kb.op("dve", lambda v: v.tensor_tensor_scan(out=t2[:, :], data0=rmask[:, :], data1=t1[:, :], initial=0.0,
                                                                op0=ALU.mult, op1=ALU.add), reads=[t_r[0], p_r], writes=[t_r[1]])
                    kb.op("act", lambda a: a.activation(out=Gam[:, hp, :], in_=t2[:, :], func=AF.Exp, scale=-C0), reads=[t_r[1]], writes=[Gam_r[hp]])
                    kb.op("act", lambda a: a.activation(out=t3[:, :], in_=t2[:, :], func=AF.Exp, scale=C0), reads=[t_r[1]], writes=[t_r[2]])
                    kb.op("dve", lambda v: v.tensor_tensor(out=t4[:, :], in0=t2[:, :], in1=t1[:, :], op=ALU.subtract),
                          reads=[t_r[0], t_r[1]], writes=[t_r[3]])
                    kb.op("act", lambda a: a.activation(out=t4[:, :], in_=t4[:, :], func=AF.Exp, scale=-C0), reads=[t_r[3]], writes=[t_r[3]])
                    kb.op("pe", lambda pe: pe.matmul(PS[3][:, 0:MT], lhsT=a2[:, hp * 128:(hp + 1) * 128], rhs=AL[:, :], start=True, stop=True),
                          reads=[p_r, AL_r], writes=[PS_r[3]])
                    kb.op("act", lambda a: a.activation(out=t5[:, :], in_=PS[3][:, 0:MT], func=AF.Sigmoid, bias=ptb[:, l, 15 + hp:16 + hp]),
                          reads=[PS_r[3], c_r], writes=[t_r[4]])
                    kb.op("dve", lambda v: v.tensor_scalar(out=t6[:, :], in0=kT[:, :], scalar1=ptb[:, l, 19 + hp:20 + hp], scalar2=None, op0=ALU.mult),
                          reads=[kT_r, c_r], writes=[t_r[5]])
                    kb.op("act", lambda a: a.activation(out=t7[:, :], in_=t6[:, :], func=AF.Square), reads=[t_r[5]], writes=[t_r[6]])
                    kb.op("pe", lambda pe: pe.matmul(PS[4][:, 0:MT], lhsT=blkf[:, :], rhs=t7[:, :], start=True, stop=True),
                          reads=[p_r, t_r[6]], writes=[PS_r[4]])
                    kb.op("act", lambda a: a.activation(out=t7[:, :], in_=PS[4][:, 0:MT], func=AF.Sqrt), reads=[PS_r[4], t_r[6]], writes=[t_r[6]])
                    kb.op("dve", lambda v: v.tensor_scalar(out=t7[:, :], in0=t7[:, :], scalar1=1e-12, scalar2=None, op0=ALU.max),
                          reads=[t_r[6]], writes=[t_r[6]])
                    kb.op("dve", lambda v: v.reciprocal(out=t7[:, :], in_=t7[:, :]), reads=[t_r[6]], writes=[t_r[6]])
                    kb.op("dve", lambda v: v.tensor_tensor(out=t6[:, :], in0=t6[:, :], in1=t7[:, :], op=ALU.mult),
                          reads=[t_r[5], t_r[6]], writes=[t_r[5]])
                    kb.op("dve", lambda v: v.tensor_scalar(out=t7[:, :], in0=t5[:, :], scalar1=-1.0, scalar2=ptb[:, l, 21 + hp:22 + hp],
                                                           op0=ALU.add, op1=ALU.mult), reads=[t_r[4], t_r[6], c_r], writes=[t_r[6]])
                    kb.op("dve", lambda v: v.scalar_tensor_tensor(out=t7[:, :], in0=t7[:, :], scalar=1.0, in1=kT[:, :], op0=ALU.add, op1=ALU.mult),
                          reads=[t_r[6], kT_r], writes=[t_r[6]])
                    for hh in range(2):
                        pr = slice(hh * 64, hh * 64 + 64)
                        kb.op("dve", lambda v: v.scalar_tensor_tensor(out=ATz[pr, hp, hh, :], in0=t6[pr, :], scalar=-1.0, in1=t4[pr, :],
                                                                      op0=ALU.mult, op1=ALU.mult), reads=[t_r[5], t_r[3]], writes=[ATz_r[hp]])
                        kb.op("dve", lambda v: v.tensor_tensor(out=RTz[pr, hp, hh, :], in0=rT[pr, :], in1=Gam[pr, hp, :], op=ALU.mult),
                              reads=[rT_r, Gam_r[hp]], writes=[RTz_r[hp]])
                    kb.op("dve", lambda v: v.tensor_tensor(out=t1[:, :], in0=t6[:, :], in1=t5[:, :], op=ALU.mult),
                          reads=[t_r[5], t_r[4], t_r[0]], writes=[t_r[0]])
                    kb.op("dve", lambda v: v.tensor_tensor(out=BTt[:, hp, :], in0=t1[:, :], in1=t3[:, :], op=ALU.mult),
                          reads=[t_r[0], t_r[2]], writes=[BT_r[hp]])
                    kb.op("dve", lambda v: v.tensor_tensor(out=KTt[:, hp, :], in0=t7[:, :], in1=t3[:, :], op=ALU.mult),
                          reads=[t_r[6], t_r[2]], writes=[KT_r[hp]])
                    kb.op("dve", lambda v: v.scalar_tensor_tensor(out=rkr[:, hp, :], in0=rT[:, :], scalar=ptb[:, l, 23 + hp:24 + hp], in1=t7[:, :],
                                                                  op0=ALU.mult, op1=ALU.mult), reads=[rT_r, t_r[6], c_r], writes=[rkr_r[hp]])
                    if l == 0:
                        kb.dma("sp", vfirst_d.ap()[hp, :, sq * 0 + tok0:tok0 + MT], vT[:, :], reads=[vT_r], writes=[vfirst_r[hp]])
                    else:
                        kb.op("pe", lambda pe: pe.matmul(PS[5][:, 0:MT], lhsT=v2[:, hp * 128:(hp + 1) * 128], rhs=VR[:, :], start=True, stop=True),
                              reads=[p_r, VR_r], writes=[PS_r[5]])
                        kb.op("act", lambda a: a.activation(out=t1[:, :], in_=PS[5][:, 0:MT], func=AF.Sigmoid, bias=ptb[:, l, 17 + hp:18 + hp]),
                              reads=[PS_r[5], c_r, t_r[0]], writes=[t_r[0]])
                        kb.dma("sp", t2[:, :], vfirst_d.ap()[hp, :, tok0:tok0 + MT], reads=[vfirst_r[hp], t_r[1]], writes=[t_r[1]])
                        kb.op("dve", lambda v: v.tensor_tensor(out=t2[:, :], in0=t2[:, :], in1=vT[:, :], op=ALU.subtract),
                              reads=[t_r[1], vT_r], writes=[t_r[1]])
                        kb.op("dve", lambda v: v.tensor_tensor(out=t2[:, :], in0=t2[:, :], in1=t1[:, :], op=ALU.mult),
                              reads=[t_r[1], t_r[0]], writes=[t_r[1]])
                        kb.op("dve", lambda v: v.tensor_tensor(out=vT[:, :], in0=vT[:, :], in1=t2[:, :], op=ALU.add),
                              reads=[t_r[1], vT_r], writes=[vT_r])
                    kb.op("act", lambda a: a.activation(out=vb[:, :], in_=vT[:, :], func=AF.Copy), reads=[vT_r], writes=[vb_r])
                    for src, src_r, dst, dst_r, pbi in ((vb, vb_r, Vtok, Vtok_r, 0), (None, BT_r[hp], Btok, Btok_r, 1), (None, KT_r[hp], Ktok, Ktok_r, 0)):
                        sap = (lambda c: vb[:, c * 64:(c + 1) * 64]) if src is vb else \
                              ((lambda c: BTt[:, hp, c * 64:(c + 1) * 64]) if dst is Btok else (lambda c: KTt[:, hp, c * 64:(c + 1) * 64]))

                        def fn(pe, sap=sap, pbi=pbi):
                            inst = None
                            for c in range(NCH):
                                inst = pe.transpose(out=PB[pbi][0:64, c * 128:(c + 1) * 128], in_=sap(c), identity=ident[:, :])
                            return inst
                        kb.op("pe", fn, reads=[src_r, c_r], writes=[PB_r[pbi]])
                        kb.op("dve", lambda v: v.tensor_copy(out=dst[0:64, :, hp * 128:(hp + 1) * 128],
                                                             in_=PB[pbi][0:64, 0:NCH * 128].rearrange("p (c d) -> p c d", c=NCH)),
                              reads=[PB_r[pbi]], writes=[dst_r])
                if rs_ <= 2:
                    continue
                def fn(pe):
                    inst = None
                    for c in range(NCH):
                        for hp in range(2):
                            inst = pe.matmul(PS[2][0:64, c * 4 + hp * 2:c * 4 + hp * 2 + 2], lhsT=rkr[:, hp, c * 64:(c + 1) * 64], rhs=hselb[:, :],
                                             start=True, stop=True)
                    return inst
                kb.op("pe", fn, reads=rkr_r + [p_r], writes=[PS_r[2]])
                kb.op("act", lambda a: a.activation(out=cb[:, :, :], in_=PS[2][0:64, 0:NCH * 4].rearrange("p (c h) -> p c h", c=NCH), func=AF.Copy),
                      reads=[PS_r[2]], writes=[cb_r])
                for c2 in range(NCH // 2):
                    def fn(pe):
                        inst = None
                        for cc_ in range(2):
                            c = c2 * 2 + cc_
                            pe.matmul(PS[3][0:64, cc_ * 256:(cc_ + 1) * 256], lhsT=SGL[:, c * 64:(c + 1) * 64], rhs=g2a[:, :], start=True, stop=False)
                            inst = pe.matmul(PS[3][0:64, cc_ * 256:(cc_ + 1) * 256], lhsT=SGL2[:, c * 64:(c + 1) * 64], rhs=g2b[:, :], start=False, stop=True)
                        return inst
                    kb.op("pe", fn, reads=[SGL_r, SGL2_r, p_r], writes=[PS_r[3]])
                    kb.op("act", lambda a: a.activation(out=gtok[:, c2 * 2:c2 * 2 + 2, :], in_=PS[3][0:64, :].rearrange("p (c d) -> p c d", c=2), func=AF.Copy),
                          reads=[PS_r[3]], writes=[gtok_r])
                for c in range(NCH):
                    cc = slice(c * 64, (c + 1) * 64)
                    for h in range(4):
                        hp, hh = h // 2, h % 2
                        u = c * 4 + h
                        pi = 4 + (u % 2)

                        def fn(pe, pi=pi):
                            pe.matmul(PS[pi][0:64, 0:64], lhsT=ATz[:, hp, hh, cc], rhs=BTt[:, hp, cc], start=True, stop=True)
                            pe.matmul(PS[pi][0:64, 64:128], lhsT=BTt[:, hp, cc], rhs=ATz[:, hp, hh, cc], start=True, stop=True)
                            pe.matmul(PS[pi][0:64, 128:192], lhsT=KTt[:, hp, cc], rhs=ATz[:, hp, hh, cc], start=True, stop=True)
                            pe.matmul(PS[pi][0:64, 192:256], lhsT=BTt[:, hp, cc], rhs=RTz[:, hp, hh, cc], start=True, stop=True)
                            return pe.matmul(PS[pi][0:64, 256:320], lhsT=KTt[:, hp, cc], rhs=RTz[:, hp, hh, cc], start=True, stop=True)
                        kb.op("pe", fn, reads=[ATz_r[hp], RTz_r[hp], BT_r[hp], KT_r[hp]], writes=[PS_r[pi]])
                        kb.op("dve", lambda v, pi=pi: v.tensor_tensor(out=MN[0][:, u, :, :], in0=PS[pi][0:64, 0:128].rearrange("p (a b) -> p a b", a=2),
                                                                      in1=amask[:, 0:2, :], op=ALU.mult), reads=[PS_r[pi], p_r], writes=[MN_r[0]])
                        kb.op("dve", lambda v, pi=pi: v.tensor_tensor(out=A3[0:64, u, :, :], in0=PS[pi][0:64, 128:320].rearrange("p (a b) -> p a b", a=3),
                                                                      in1=amask[:, 2:5, :], op=ALU.mult), reads=[PS_r[pi], p_r], writes=[A3_r])
                if rs_ <= 3:
                    continue
                kb.op("dve", lambda v: v.tensor_tensor(out=Pm[:, :, :], in0=MN[0][:, :, 1, :],
                                                      in1=bass.AP(identf, 0, [[128, 64], [0, NU], [1, 64]]), op=ALU.add),
                      reads=[MN_r[0], c_r], writes=[P_r])
                cur = 0
                for lev in range(5):
                    nxt = 1 - cur
                    lastlev = (lev == 4)
                    for g4 in range(NU // 4):
                        pi = 4 + (g4 % 2)

                        def fn(pe, pi=pi):
                            inst = None
                            for q in range(4):
                                u = g4 * 4 + q
                                inst = pe.matmul(PS[pi][0:64, q * 128:q * 128 + 64], lhsT=MN[cur][:, u, 1, :], rhs=MN[cur][:, u, 0, :], start=True, stop=True)
                                if not lastlev:
                                    inst = pe.matmul(PS[pi][0:64, q * 128 + 64:q * 128 + 128], lhsT=MN[cur][:, u, 0, :], rhs=MN[cur][:, u, 1, :],
                                                     start=True, stop=True)
                            return inst
                        kb.op("pe", fn, reads=[MN_r[cur]], writes=[PS_r[pi]])
                        kb.op("act", lambda a, pi=pi: a.activation(out=MN[nxt][:, g4 * 4:g4 * 4 + 4, :, :],
                                                                   in_=PS[pi][0:64, :].rearrange("p (u a b) -> p u a b", u=4, a=2), func=AF.Copy),
                              reads=[PS_r[pi]], writes=[MN_r[nxt]])
                    for g8 in range(NU // 8):
                        pi = 2 + (g8 % 2)

                        def fn(pe, pi=pi):
                            inst = None
                            for q in range(8):
                                u = g8 * 8 + q
                                inst = pe.matmul(PS[pi][0:64, q * 64:(q + 1) * 64], lhsT=MN[nxt][:, u, 0, :], rhs=Pm[:, u, :], start=True, stop=True)
                            return inst
                        kb.op("pe", fn, reads=[MN_r[nxt], P_r], writes=[PS_r[pi]])
                        kb.op("dve", lambda v, pi=pi: v.tensor_tensor(out=Pm[:, g8 * 8:g8 * 8 + 8, :], in0=PS[pi][0:64, :].rearrange("p (u b) -> p u b", u=8),
                                                                      in1=Pm[:, g8 * 8:g8 * 8 + 8, :], op=ALU.add), reads=[PS_r[pi], P_r], writes=[P_r])
                    cur = nxt
                if rs_ <= 4:
                    continue
                for c in range(NCH):
                    cc = slice(c * 64, (c + 1) * 64)

                    def fn(pe):
                        inst = None
                        for h in range(4):
                            hp, hh = h // 2, h % 2
                            u = c * 4 + h
                            pe.matmul(PS[0][0:64, h * 64:(h + 1) * 64], lhsT=A3[:, u, 0, :], rhs=Vtok[:, c, h * 64:(h + 1) * 64], start=True, stop=False)
                            inst = pe.matmul(PS[0][0:64, h * 64:(h + 1) * 64], lhsT=ATz[:, hp, hh, cc], rhs=Tb[:, hp, hh * 64:(hh + 1) * 64],
                                             start=False, stop=True)
                        return inst
                    kb.op("pe", fn, reads=[A3_r, Vtok_r, Tb_r] + ATz_r, writes=[PS_r[0]])
                    kb.op("act", lambda a: a.activation(out=Rs[:, :], in_=PS[0][0:64, 0:256], func=AF.Copy), reads=[PS_r[0]], writes=[Rs_r])

                    def fn(pe):
                        inst = None
                        for h in range(4):
                            u = c * 4 + h
                            inst = pe.matmul(PS[1][0:64, h * 64:(h + 1) * 64], lhsT=Pm[:, u, :], rhs=Rs[:, h * 64:(h + 1) * 64], start=True, stop=True)
                        return inst
                    kb.op("pe", fn, reads=[P_r, Rs_r], writes=[PS_r[1]])
                    kb.op("act", lambda a: a.activation(out=Ub[0:64, :], in_=PS[1][0:64, 0:256], func=AF.Copy), reads=[PS_r[1]], writes=[Ub_r])

                    def fn(pe):
                        inst = None
                        for h in range(4):
                            hp, hh = h // 2, h % 2
                            u = c * 4 + h
                            pe.matmul(PS[0][0:64, h * 64:(h + 1) * 64], lhsT=RTz[:, hp, hh, cc], rhs=Tb[:, hp, hh * 64:(hh + 1) * 64], start=True, stop=False)
                            pe.matmul(PS[0][0:64, h * 64:(h + 1) * 64], lhsT=A3[:, u, 1, :], rhs=Ub[:, h * 64:(h + 1) * 64], start=False, stop=False)
                            inst = pe.matmul(PS[0][0:64, h * 64:(h + 1) * 64], lhsT=A3[:, u, 2, :], rhs=Vtok[:, c, h * 64:(h + 1) * 64], start=False, stop=True)
                        return inst
                    kb.op("pe", fn, reads=[A3_r, Vtok_r, Tb_r, Ub_r] + RTz_r, writes=[PS_r[0]])

                    def fn(pe):
                        inst = None
                        for hp in range(2):
                            pe.matmul(PS[1][:, hp * 128:(hp + 1) * 128], lhsT=Btok[:, c, hp * 128:(hp + 1) * 128], rhs=Ub[:, hp * 128:(hp + 1) * 128],
                                      start=True, stop=False)
                            inst = pe.matmul(PS[1][:, hp * 128:(hp + 1) * 128], lhsT=Ktok[:, c, hp * 128:(hp + 1) * 128], rhs=Vtok[:, c, hp * 128:(hp + 1) * 128],
                                             start=False, stop=True)
                        return inst
                    kb.op("pe", fn, reads=[Btok_r, Ktok_r, Vtok_r, Ub_r], writes=[PS_r[1]])
                    for hp in range(2):
                        ee = Gam[:, hp, c * 64 + 63:c * 64 + 64]
                        kb.op("dve", lambda v: v.scalar_tensor_tensor(out=tmpS[:, :], in0=PS[1][:, hp * 128:(hp + 1) * 128], scalar=ee, in1=blkm[:, :],
                                                                      op0=ALU.mult, op1=ALU.mult), reads=[PS_r[1], Gam_r[hp], p_r], writes=[tmpS_r])
                        kb.op("dve", lambda v: v.scalar_tensor_tensor(out=Tblk[:, hp, :], in0=Tblk[:, hp, :], scalar=ee, in1=tmpS[:, :],
                                                                      op0=ALU.mult, op1=ALU.add), reads=[T_r, tmpS_r, Gam_r[hp]], writes=[T_r])
                    kb.op("act", lambda a: a.activation(out=Tb[:, :, :], in_=Tblk[:, :, :], func=AF.Copy), reads=[T_r], writes=[Tb_r])
                    if rs_ <= 5:
                        continue
                    head_norm_gate(PS[0][0:64, 0:256], PS_r[0], ow, ow_r, ow2, ow2_r, st, st_r, rln, p_r, RW_EPS, P=64)
                    for h in range(4):
                        kb.op("dve", lambda v: v.scalar_tensor_tensor(out=ow[0:64, h * 64:(h + 1) * 64], in0=Vtok[0:64, c, h * 64:(h + 1) * 64],
                                                                      scalar=cb[:, c, h:h + 1], in1=ow[0:64, h * 64:(h + 1) * 64],
                                                                      op0=ALU.mult, op1=ALU.add), reads=[Vtok_r, cb_r, ow_r], writes=[ow_r])
                    kb.op("dve", lambda v: v.tensor_tensor(out=ob[0:64, :], in0=ow[0:64, :], in1=gtok[:, c, :], op=ALU.mult),
                          reads=[ow_r, gtok_r], writes=[ob_r])
                    out_to_OT(ob, ob_r, 64, OT, OT_r, 2, tok0 + c * 64)
            kb.barrier()

    NQ, NQS, NKC, NKS, NKW, NVC, NVS, NGT = 0, 512, 1024, 1280, 1536, 1792, 1920, 2176
    GK = 1.5957691216057308

    def nsa_phase(sq, l, OT, OT_r):
        ns_ = cfg.get("nsa_stop", 99)
        with ExitStack() as es:
            A = lambda n_, s_, d_: es.enter_context(sb(n_, s_, d_))
            QT = A("nQT", [128, 4, S], BF16)
            KTz = A("nKTz", [128, 3, 2, S], BF16)
            VCz = A("nVCz", [128, 2, S], BF16)
            VS = A("nVS", [128, NT, 2, 65], BF16)
            VW = A("nVW", [128, NT, 2, 65], BF16)
            gsig = A("ngsig", [128, NT, 24], F32)
            kcmpTz = A("nkcmp", [128, 2, 128], BF16)
            vcmp = A("nvcmp", [128, 2, 97], BF16)
            QT_r, KT_r, VC_r, VS_r, VW_r, gs_r = regs(4), regs(4), regs(4), regs(NT), regs(NT), regs(NT)
            kc_r, vcm_r = Reg(), Reg()
            kb.op("pool", lambda g: g.memset(KTz[:], 0.0), writes=KT_r)
            kb.op("pool", lambda g: g.memset(VCz[:], 0.0), writes=VC_r)
            kb.op("pool", lambda g: g.memset(VS[:, :, :, 64:65], 1.0), writes=VS_r)
            kb.op("pool", lambda g: g.memset(VW[:, :, :, 64:65], 1.0), writes=VW_r)
            kb.op("pool", lambda g: g.memset(kcmpTz[:], 0.0), writes=[kc_r])
            kb.op("pool", lambda g: g.memset(vcmp[:], 0.0), writes=[vcm_r])
            with ExitStack() as es1:
                A1 = lambda n_, s_, d_: es1.enter_context(sb(n_, s_, d_))
                Wn = A1("nWn", [128, 8, 2200], BF16)
                rp = A1("nrp", [128, 2, 512], F32)
                t1 = A1("nt1", [128, 512], F32)
                t2 = A1("nt2", [128, 512], F32)
                W_r, rp_r, t1_r, t2_r = Reg(), Reg(), Reg(), Reg()
                load_w(Wn, w_nsa16.ap()[l], 2200, W_r)
                for mt in range(4):
                    tok0 = mt * 512
                    bl = slice(tok0, tok0 + 512)
                    xr = XT_r[mt * 4:mt * 4 + 4]
                    kb.dma("sp", rp[:, 0, :], c_rope.ap()[0][:, bl], writes=[rp_r])
                    kb.dma("sp", rp[:, 1, :], c_rope.ap()[1][:, bl], writes=[rp_r])
                    for i in range(4):
                        proj_fm(PS[0][:, :], Wn, NQ + i * 128, 128, tok0, 512, [W_r], xr, PS_r[0])
                        proj_fm(PS[1][:, :], Wn, NQS + i * 128, 128, tok0, 512, [W_r], xr, PS_r[1])
                        kb.op("dve", lambda v: v.tensor_tensor(out=t1[:, :], in0=PS[0][:, :], in1=rp[:, 0, :], op=ALU.mult),
                              reads=[PS_r[0], rp_r], writes=[t1_r])
                        kb.op("dve", lambda v: v.scalar_tensor_tensor(out=t2[:, :], in0=PS[1][:, :], scalar=0.125, in1=rp[:, 1, :],
                                                                      op0=ALU.mult, op1=ALU.mult), reads=[PS_r[1], rp_r], writes=[t2_r])
                        kb.op("dve", lambda v: v.scalar_tensor_tensor(out=QT[:, i, bl], in0=t1[:, :], scalar=0.125, in1=t2[:, :],
                                                                      op0=ALU.mult, op1=ALU.add), reads=[t1_r, t2_r], writes=[QT_r[mt]])
                    for ty, c0 in ((0, NKC), (1, NKS), (2, NKW)):
                        proj_fm(PS[0][:, :], Wn, c0, 128, tok0, 512, [W_r], xr, PS_r[0])
                        proj_fm(PS[1][:, :], Wn, c0 + 128, 128, tok0, 512, [W_r], xr, PS_r[1])
                        kb.op("dve", lambda v: v.tensor_tensor(out=t1[:, :], in0=PS[0][:, :], in1=rp[:, 0, :], op=ALU.mult),
                              reads=[PS_r[0], rp_r], writes=[t1_r])
                        kb.op("dve", lambda v: v.tensor_tensor(out=t2[:, :], in0=PS[1][:, :], in1=rp[:, 1, :], op=ALU.mult),
                              reads=[PS_r[1], rp_r], writes=[t2_r])
                        for g in range(2):
                            pr = slice(g * 64, g * 64 + 64)
                            kb.op("dve", lambda v: v.tensor_tensor(out=KTz[pr, ty, g, bl], in0=t1[pr, :], in1=t2[pr, :], op=ALU.add),
                                  reads=[t1_r, t2_r], writes=[KT_r[mt]])
                    proj_fm(PS[2][:, :], Wn, NVC, 128, tok0, 512, [W_r], xr, PS_r[2])
                    for g in range(2):
                        pr = slice(g * 64, g * 64 + 64)
                        kb.op("act", lambda a: a.activation(out=VCz[pr, g, bl], in_=PS[2][pr, :], func=AF.Copy),
                              reads=[PS_r[2]], writes=[VC_r[mt]])
                    for j in range(4):
                        tt = mt * 4 + j
                        proj_tm(PS[3][:, 0:256], Wn, NVS, 256, tt, [W_r], PS_r[3])
                        kb.op("act", lambda a: a.activation(out=VS[:, tt, :, 0:64], in_=PS[3][:, 0:128].rearrange("p (g d) -> p g d", g=2), func=AF.Copy),
                              reads=[PS_r[3]], writes=[VS_r[tt], PS_r[3]])
                        kb.op("act", lambda a: a.activation(out=VW[:, tt, :, 0:64], in_=PS[3][:, 128:256].rearrange("p (g d) -> p g d", g=2), func=AF.Copy),
                              reads=[PS_r[3]], writes=[VW_r[tt], PS_r[3]])
                        proj_tm(PS[4][:, 0:24], Wn, NGT, 24, tt, [W_r], PS_r[4])
                        kb.op("act", lambda a: a.activation(out=gsig[:, tt, :], in_=PS[4][:, 0:24], func=AF.Sigmoid),
                              reads=[PS_r[4]], writes=[gs_r[tt]])
                kb.barrier()
            if ns_ <= 1:
                kb.barrier()
                return
            with ExitStack() as es2:
                A2 = lambda n_, s_, d_: es2.enter_context(sb(n_, s_, d_))
                w1d = A2("nw1d", [128, 32, 256], BF16)
                w2d = A2("nw2d", [128, 2, 128], BF16)
                wv2 = A2("nwv2", [128, 2, 64], BF16)
                posz = A2("nposz", [128, 2, 32], BF16)
                hc = A2("nhc", [128, 2], F32)
                gx = A2("ngx", [128, 128], F32)
                gw = A2("ngw", [128, 128], F32)
                gh = A2("ngh", [128, 2, 128], BF16)
                w1_r, w2_r, hc_r, gx_r, gw_r, gh_r = Reg(), Reg(), Reg(), Reg(), Reg(), Reg()
                kb.op("pool", lambda g: g.memset(posz[:], 0.0), writes=[w2_r])
                kb.op("pool", lambda g: g.memset(gh[:], 0.0), writes=[gh_r])
                for kv in range(2):
                    kb.dma("pool", posz[0:64, kv, :], nsa_posT.ap()[l, kv], reads=[w2_r], writes=[w2_r])
                for half in range(2):
                    kb.dma("pool", w2d[:, :, half * 64:(half + 1) * 64], nsa_wk2.ap()[l].rearrange("(t p) n -> p t n", p=128), writes=[w2_r])
                kb.dma("pool", wv2[:, :, :], nsa_wv2.ap()[l].rearrange("(t p) n -> p t n", p=128), writes=[w2_r])
                kb.dma("pool", vcmp[:, 0, 65:97], c_ovl.ap(), reads=[vcm_r], writes=[vcm_r])
                kb.dma("pool", vcmp[:, 1, 65:97], c_ovl.ap(), reads=[vcm_r], writes=[vcm_r])
                kb.op("pool", lambda g: g.memset(vcmp[0:127, :, 64:65], 1.0), reads=[vcm_r], writes=[vcm_r])
                for kv, w1src in ((0, wk1_16), (1, wv1_16)):
                    src3 = w1src.ap()[l].rearrange("(l d) n -> d l n", d=64)
                    for half in range(2):
                        for l4 in range(8):
                            kb.dma("sp", w1d[half * 64:(half + 1) * 64, l4 * 4:(l4 + 1) * 4, :], src3[:, l4 * 4:(l4 + 1) * 4, :],
                                   reads=[w16_r], writes=[w1_r])
                    srcz = (lambda g: KTz[:, 0, g, :]) if kv == 0 else (lambda g: VCz[:, g, :])
                    src_regs = KT_r if kv == 0 else VC_r
                    for hf in range(2):
                        def fn(pe):
                            inst = None
                            for ll in range(32):
                                inst = pe.matmul(PS[0][:, 0:1], lhsT=w1d[:, ll, hf * 128:(hf + 1) * 128], rhs=posz[:, kv, ll:ll + 1],
                                                 start=(ll == 0), stop=(ll == 31))
                            return inst
                        kb.op("pe", fn, reads=[w1_r, w2_r], writes=[PS_r[0]])
                        kb.op("act", lambda a: a.activation(out=hc[:, hf:hf + 1], in_=PS[0][:, 0:1], func=AF.Copy), reads=[PS_r[0]], writes=[hc_r])
                    for g in range(2):
                        for hf in range(2):
                            def fn(pe):
                                inst = None
                                for ll in range(32):
                                    rhs = bass.AP(srcz(g).tensor, srcz(g).offset + ll, [srcz(g).ap[0], [16, 127]])
                                    inst = pe.matmul(PS[1][:, 0:127], lhsT=w1d[:, ll, hf * 128:(hf + 1) * 128], rhs=rhs,
                                                     start=(ll == 0), stop=(ll == 31))
                                return inst
                            kb.op("pe", fn, reads=[w1_r] + src_regs, writes=[PS_r[1]])
                            kb.op("act", lambda a: a.activation(out=gx[:, 0:127], in_=PS[1][:, 0:127], func=AF.Identity, bias=hc[:, hf:hf + 1], scale=1.0),
                                  reads=[PS_r[1], hc_r], writes=[gx_r])
                            kb.op("dve", lambda v: v.tensor_tensor(out=gw[:, 0:127], in0=gx[:, 0:127], in1=gx[:, 0:127], op=ALU.mult),
                                  reads=[gx_r], writes=[gw_r])
                            kb.op("dve", lambda v: v.tensor_scalar(out=gw[:, 0:127], in0=gw[:, 0:127], scalar1=0.044715, scalar2=1.0,
                                                                   op0=ALU.mult, op1=ALU.add), reads=[gw_r], writes=[gw_r])
                            kb.op("dve", lambda v: v.tensor_tensor(out=gw[:, 0:127], in0=gw[:, 0:127], in1=gx[:, 0:127], op=ALU.mult),
                                  reads=[gw_r, gx_r], writes=[gw_r])
                            kb.op("act", lambda a: a.activation(out=gw[:, 0:127], in_=gw[:, 0:127], func=AF.Sigmoid, scale=GK), reads=[gw_r], writes=[gw_r])
                            kb.op("dve", lambda v: v.tensor_tensor(out=gh[:, hf, 0:127], in0=gx[:, 0:127], in1=gw[:, 0:127], op=ALU.mult),
                                  reads=[gw_r, gx_r], writes=[gh_r])
                        if kv == 0:
                            def fn(pe):
                                pe.matmul(PS[2][:, 0:127], lhsT=w2d[:, 0, :], rhs=gh[:, 0, 0:127], start=True, stop=False)
                                return pe.matmul(PS[2][:, 0:127], lhsT=w2d[:, 1, :], rhs=gh[:, 1, 0:127], start=False, stop=True)
                            kb.op("pe", fn, reads=[w2_r, gh_r], writes=[PS_r[2]])
                            pr = slice(g * 64, g * 64 + 64)
                            kb.op("act", lambda a: a.activation(out=kcmpTz[pr, g, 0:127], in_=PS[2][pr, 0:127], func=AF.Copy),
                                  reads=[PS_r[2], kc_r], writes=[kc_r])
                        else:
                            def fn(pe):
                                pe.matmul(PS[2][:, 0:64], lhsT=gh[:, 0, :], rhs=wv2[:, 0, :], start=True, stop=False)
                                return pe.matmul(PS[2][:, 0:64], lhsT=gh[:, 1, :], rhs=wv2[:, 1, :], start=False, stop=True)
                            kb.op("pe", fn, reads=[w2_r, gh_r], writes=[PS_r[2]])
                            kb.op("act", lambda a: a.activation(out=vcmp[0:127, g, 0:64], in_=PS[2][0:127, 0:64], func=AF.Copy),
                                  reads=[PS_r[2], vcm_r], writes=[vcm_r])
                kb.barrier()
            if ns_ <= 2:
                kb.barrier()
                return
            selbT = A("nselbT", [128, 2, S], BF16)
            onehot = A("nonehot", [128, S], BF16)
            cmask = A("ncmask", [128, S], BF16)
            selm = A("nselm", [128, 2, NT, 32], F32)
            wmask = A("nwmask", [128, 128], F32)
            PT = [A("nPT%d" % i, [128, 640], BF16) for i in range(2)]
            ex = A("nex", [128, 512], F32)
            ocs = A("nocs", [128, 4, 97], F32)
            ONSA = A("nONSA", [128, 4, 512], F32)
            impb = A("nimp", [128, 4, 2, 32], F32)
            sc = A("nsc", [128, 32], F32)
            cm3 = A("ncm3", [128, 32, 32], F32)
            sb16 = A("nsb16", [128, 32], BF16)
            dn = A("ndn", [128, 16], F32)
            obf = A("nobf", [128, 512], BF16)
            k_r, sel_r, PT_r, ex_r, ocs_r, on_r, imp_r, sc_r, cm3_r, sb16_r, dn_r, obf_r = \
                Reg(), regs(4), regs(2), Reg(), Reg(), regs(4), Reg(), Reg(), Reg(), Reg(), Reg(), Reg()
            kb.op("pool", lambda g: g.memset(selbT[:], 0.0), writes=sel_r)
            kb.op("pool", lambda g: g.memset(onehot[:], 0.0), writes=[k_r])
            kb.dma("pool", onehot[0:32, :], c_onehot.ap(), reads=[k_r], writes=[k_r])
            kb.dma("pool", cmask[:, :], c_cmpmask.ap(), writes=[k_r])
            for m_ in range(2):
                kb.dma("sp", selm[:, m_, :, :], c_selm.ap()[m_], writes=[k_r])
            kb.op("dve", lambda v: v.tensor_scalar(out=wmask[:, :], in0=caus[:, 0, :], scalar1=-1.0, scalar2=1.0, op0=ALU.mult, op1=ALU.add),
                  reads=[c_r], writes=[k_r])

            def branch_finish(acc_ps, acc_r, W, nq, h, b, tts, first):
                kb.op("act", lambda a: a.activation(out=ocs[:, 0:nq, 0:W], in_=acc_ps.rearrange("p (q w) -> p q w", q=nq), func=AF.Copy),
                      reads=[acc_r], writes=[ocs_r, acc_r])
                kb.op("dve", lambda v: v.tensor_scalar(out=dn[:, 0:nq], in0=ocs[:, 0:nq, 64], scalar1=1e-30, scalar2=None, op0=ALU.max),
                      reads=[ocs_r], writes=[dn_r])
                kb.op("dve", lambda v: v.reciprocal(out=dn[:, 0:nq], in_=dn[:, 0:nq]), reads=[dn_r], writes=[dn_r])
                kb.op("dve", lambda v: v.tensor_tensor(out=dn[:, 8:8 + nq], in0=dn[:, 0:nq], in1=gsig[:, tts[0]:tts[0] + nq, h * 3 + b], op=ALU.mult),
                      reads=[dn_r] + gs_r[tts[0]:tts[0] + nq], writes=[dn_r])
                for qi in range(nq):
                    tl = tts[qi] % 4
                    if first:
                        kb.op("dve", lambda v: v.tensor_scalar(out=ONSA[:, tl, h * 64:(h + 1) * 64], in0=ocs[:, qi, 0:64], scalar1=dn[:, 8 + qi:9 + qi],
                                                               scalar2=None, op0=ALU.mult), reads=[ocs_r, dn_r], writes=[on_r[tl]])
                    else:
                        kb.op("dve", lambda v: v.scalar_tensor_tensor(out=ONSA[:, tl, h * 64:(h + 1) * 64], in0=ocs[:, qi, 0:64], scalar=dn[:, 8 + qi:9 + qi],
                                                                      in1=ONSA[:, tl, h * 64:(h + 1) * 64], op0=ALU.mult, op1=ALU.add),
                              reads=[ocs_r, dn_r, on_r[tl]], writes=[on_r[tl]])

            for qb in range(4):
                qc = slice(qb * 512, (qb + 1) * 512)
                tts = list(range(qb * 4, qb * 4 + 4))
                for h in range(8):
                    g, i = h // 4, h % 4
                    pi = h % 2
                    kb.op("pe", lambda pe: pe.matmul(PS[pi][:, :], lhsT=kcmpTz[:, g, :], rhs=QT[:, i, qc], start=True, stop=True),
                          reads=[kc_r, QT_r[qb]], writes=[PS_r[pi]])
                    kb.op("act", lambda a: a.activation(out=ex[:, :], in_=PS[pi][:, :], func=AF.Exp), reads=[PS_r[pi]], writes=[ex_r])
                    kb.op("dve", lambda v: v.tensor_tensor(out=PT[pi][:, 0:512], in0=ex[:, :], in1=cmask[:, qc], op=ALU.mult),
                          reads=[ex_r, k_r], writes=[PT_r[pi]])
                    ai = 2 + (h % 2)

                    def fn(pe):
                        inst = None
                        for q in range(4):
                            inst = pe.matmul(PS[ai][:, q * 97:(q + 1) * 97], lhsT=PT[pi][:, q * 128:(q + 1) * 128], rhs=vcmp[:, g, :], start=True, stop=True)
                        return inst
                    kb.op("pe", fn, reads=[PT_r[pi], vcm_r], writes=[PS_r[ai]])
                    branch_finish(PS[ai][:, 0:388], PS_r[ai], 97, 4, h, 0, tts, True)
                    for q in range(4):
                        if i == 0:
                            kb.op("dve", lambda v: v.tensor_scalar(out=impb[:, q, g, :], in0=ocs[:, q, 65:97], scalar1=dn[:, q:q + 1], scalar2=None, op0=ALU.mult),
                                  reads=[ocs_r, dn_r], writes=[imp_r])
                        else:
                            kb.op("dve", lambda v: v.scalar_tensor_tensor(out=impb[:, q, g, :], in0=ocs[:, q, 65:97], scalar=dn[:, q:q + 1], in1=impb[:, q, g, :],
                                                                          op0=ALU.mult, op1=ALU.add), reads=[ocs_r, dn_r, imp_r], writes=[imp_r])
                if ns_ <= 3:
                    continue
                for q in range(4):
                    tt = tts[q]
                    for g in range(2):
                        kb.op("dve", lambda v: v.tensor_tensor(out=sc[:, :], in0=impb[:, q, g, :], in1=selm[:, 0, tt, :], op=ALU.mult),
                              reads=[imp_r, k_r], writes=[sc_r])
                        kb.op("dve", lambda v: v.tensor_tensor(out=sc[:, :], in0=sc[:, :], in1=selm[:, 1, tt, :], op=ALU.add),
                              reads=[sc_r, k_r], writes=[sc_r])
                        kb.op("dve", lambda v: v.tensor_tensor(out=cm3[:, :, :], in0=bass.AP(sc, 0, [[32, 128], [0, 32], [1, 32]]),
                                                              in1=bass.AP(sc, 0, [[32, 128], [1, 32], [0, 32]]), op=ALU.is_gt),
                              reads=[sc_r], writes=[cm3_r])
                        kb.op("dve", lambda v: v.reduce_sum(out=sc[:, :], in_=cm3[:, :, :], axis=AX.X), reads=[cm3_r, sc_r], writes=[sc_r])
                        kb.op("dve", lambda v: v.tensor_scalar(out=sb16[:, :], in0=sc[:, :], scalar1=15.5, scalar2=-30000.0, op0=ALU.is_gt, op1=ALU.mult),
                              reads=[sc_r], writes=[sb16_r])
                        kb.op("pe", lambda pe: pe.transpose(out=PB[0][0:32, 0:128], in_=sb16[:, :], identity=ident[:, :]),
                              reads=[sb16_r, c_r], writes=[PB_r[0]])
                        kb.op("act", lambda a: a.activation(out=selbT[0:32, g, tt * 128:(tt + 1) * 128], in_=PB[0][0:32, 0:128], func=AF.Copy),
                              reads=[PB_r[0]], writes=[sel_r[qb]])
                if ns_ <= 4:
                    continue
                for h in range(8):
                    g, i = h // 4, h % 4
                    nkt = 4 * qb + 4
                    for kt in range(nkt):
                        kc_ = slice(kt * 128, (kt + 1) * 128)
                        pi = kt % 2

                        def fn(pe):
                            pe.matmul(PS[pi][:, :], lhsT=KTz[:, 1, g, kc_], rhs=QT[:, i, qc], start=True, stop=False)
                            return pe.matmul(PS[pi][:, :], lhsT=onehot[:, kc_], rhs=selbT[:, g, qc], start=False, stop=True)
                        kb.op("pe", fn, reads=[KT_r[kt // 4], QT_r[qb], k_r, sel_r[qb]], writes=[PS_r[pi]])
                        kb.op("act", lambda a: a.activation(out=PT[pi][:, 0:512], in_=PS[pi][:, :], func=AF.Exp), reads=[PS_r[pi]], writes=[PT_r[pi]])
                        if kt >= 4 * qb:
                            ql = kt - 4 * qb
                            kb.op("dve", lambda v: v.tensor_tensor(out=PT[pi][:, ql * 128:(ql + 1) * 128], in0=PT[pi][:, ql * 128:(ql + 1) * 128],
                                                                  in1=caus[:, 0, :], op=ALU.mult), reads=[PT_r[pi], c_r], writes=[PT_r[pi]])
                        for q in range(4):
                            qt = 4 * qb + q
                            if qt < kt:
                                continue
                            kb.op("pe", lambda pe: pe.matmul(PS[2 + q][:, 0:65], lhsT=PT[pi][:, q * 128:(q + 1) * 128], rhs=VS[:, kt, g, :],
                                                             start=(kt == 0), stop=(kt == qt)), reads=[PT_r[pi], VS_r[kt]], writes=[PS_r[2 + q]])
                    for q in range(4):
                        branch_finish(PS[2 + q][:, 0:65], PS_r[2 + q], 65, 1, h, 1, [tts[q]], False)
                if ns_ <= 5:
                    continue
                for h in range(8):
                    g, i = h // 4, h % 4
                    for q in range(4):
                        qt = 4 * qb + q
                        qcs = slice(qt * 128, (qt + 1) * 128)
                        kts = [kt for kt in range(qt - 4, qt + 1) if kt >= 0]
                        pi = q % 2
                        main = [kt for kt in kts if kt >= qt - 3]

                        def fn(pe):
                            inst = None
                            for n_, kt in enumerate(main):
                                inst = pe.matmul(PS[pi][:, n_ * 128:(n_ + 1) * 128], lhsT=KTz[:, 2, g, kt * 128:(kt + 1) * 128], rhs=QT[:, i, qcs],
                                                 start=True, stop=True)
                            return inst
                        kb.op("pe", fn, reads=KT_r + [QT_r[qb]], writes=[PS_r[pi]])
                        nm = len(main)
                        kb.op("act", lambda a: a.activation(out=PT[pi][:, 0:nm * 128], in_=PS[pi][:, 0:nm * 128], func=AF.Exp), reads=[PS_r[pi]], writes=[PT_r[pi]])
                        kb.op("dve", lambda v: v.tensor_tensor(out=PT[pi][:, (nm - 1) * 128:nm * 128], in0=PT[pi][:, (nm - 1) * 128:nm * 128],
                                                              in1=caus[:, 0, :], op=ALU.mult), reads=[PT_r[pi], c_r], writes=[PT_r[pi]])
                        tail = (qt - 4 >= 0)
                        if tail:
                            kt = qt - 4
                            kb.op("pe", lambda pe: pe.matmul(PS[2 + pi][:, 0:128], lhsT=KTz[:, 2, g, kt * 128:(kt + 1) * 128], rhs=QT[:, i, qcs],
                                                             start=True, stop=True), reads=KT_r + [QT_r[qb]], writes=[PS_r[2 + pi]])
                            kb.op("act", lambda a: a.activation(out=ex[:, 0:128], in_=PS[2 + pi][:, 0:128], func=AF.Exp), reads=[PS_r[2 + pi]], writes=[ex_r])
                            kb.op("dve", lambda v: v.tensor_tensor(out=PT[pi][:, 512:640], in0=ex[:, 0:128], in1=wmask[:, :], op=ALU.mult),
                                  reads=[ex_r, k_r, PT_r[pi]], writes=[PT_r[pi]])

                        def fn(pe):
                            inst = None
                            seq_ = [(n_, kt) for n_, kt in enumerate(main)] + ([(4, qt - 4)] if tail else [])
                            for idx, (n_, kt) in enumerate(seq_):
                                inst = pe.matmul(PS[4][:, q * 65:(q + 1) * 65], lhsT=PT[pi][:, n_ * 128:(n_ + 1) * 128], rhs=VW[:, kt, g, :],
                                                 start=(idx == 0), stop=(idx == len(seq_) - 1))
                            return inst
                        kb.op("pe", fn, reads=[PT_r[pi]] + VW_r[max(0, qt - 4):qt + 1], writes=[PS_r[4]])
                    branch_finish(PS[4][:, 0:260], PS_r[4], 65, 4, h, 2, tts, False)
                if ns_ <= 6:
                    continue
                for q in range(4):
                    tt = tts[q]
                    kb.op("act", lambda a: a.activation(out=obf[:, :], in_=ONSA[:, q, :], func=AF.Copy), reads=[on_r[q]], writes=[obf_r])
                    for half in range(2):
                        out_to_OT(obf[:, half * 256:(half + 1) * 256], obf_r, 128, OT, OT_r, 4 + 2 * half, tt * 128)
            kb.barrier()

    for sq in range(NSEQ):
        for l in range(NLAY):
            if l == 0:
                with sb("xstage", [128, 2, D], F32) as xst:
                    xst_r = regs(2)
                    for tt in range(NT):
                        i = tt % 2
                        kb.dma("sp", xst[:, i, :], x_d.ap()[sq, tt * 128:(tt + 1) * 128, :], writes=[xst_r[i]])
                        make_xT(xst[:, i, :], xst_r[i], tt)
                    kb.barrier()
            if cfg.get("stop") == "xt":
                continue
            res_src = (lambda tt: x_d.ap()[sq, tt * 128:(tt + 1) * 128, :]) if l == 0 else \
                      (lambda tt: xres[1].ap()[tt * 128:(tt + 1) * 128, :])
            res_regs = None if l == 0 else xres_r[1]

            with sb("OT", [128, 8, S], BF16) as OT:
                OT_r = regs(NT)
                if inject_O:
                    for k in range(8):
                        kb.dma("pool", OT[:, k, :], dbg_OT.ap()[:, k, :], writes=OT_r)
                if "gla" in mixers:
                    gla_phase(sq, l, OT, OT_r)
                if "rwkv" in mixers:
                    rwkv_phase(sq, l, OT, OT_r)
                if "nsa" in mixers:
                    nsa_phase(sq, l, OT, OT_r)
                if "OT" in dump_d:
                    for k in range(8):
                        with sb("otd", [128, S], F32) as otd:
                            r_ = Reg()
                            kb.op("act", lambda a: a.activation(out=otd[:], in_=OT[:, k, :], func=AF.Copy),
                                  reads=OT_r, writes=[r_])
                            dump("OT", otd[:], r_, idx=k)
                            kb.barrier()
                if cfg.get("stop") == "outproj0":
                    continue
                with sb("Wo", [128, 8, D], BF16) as Wo, \
                        sb("ln1", [128, 2, D], F32) as ln1, \
                        sb("xs1", [128, 2, D], F32) as xs1, \
                        sb("st1", [128, 2, 32], F32) as st1:
                    Wo_r = Reg()
                    ln_r = Reg()
                    xs_r = regs(2)
                    st_r = regs(2)
                    load_w(Wo, w_out16.ap()[l], D, Wo_r)
                    kb.dma("sp", ln1[:, 0, :], bcast_rows(rowtab, l * 5120 + 1024, D), writes=[ln_r])
                    kb.dma("sp", ln1[:, 1, :], bcast_rows(rowtab, l * 5120 + 2048, D), writes=[ln_r])
                    for tt in range(NT):
                        i = tt % 2
                        kb.dma("sp", xs1[:, i, :], res_src(tt), reads=([res_regs[tt]] if res_regs else []), writes=[xs_r[i]])
                        for hf in range(2):
                            def fn(pe, hf=hf):
                                inst = None
                                for k in range(8):
                                    inst = pe.matmul(PS[hf][:, :], lhsT=OT[:, k, tt * 128:(tt + 1) * 128],
                                                     rhs=Wo[:, k, hf * 512:(hf + 1) * 512], start=(k == 0), stop=(k == 7))
                                return inst
                            kb.op("pe", fn, reads=[OT_r[tt], Wo_r], writes=[PS_r[hf]])
                            kb.op("dve", lambda v, hf=hf: v.scalar_tensor_tensor(
                                out=xs1[:, i, hf * 512:(hf + 1) * 512], in0=xs1[:, i, hf * 512:(hf + 1) * 512], scalar=ALPHA,
                                in1=PS[hf][:, :], op0=ALU.mult, op1=ALU.add), reads=[PS_r[hf], xs_r[i]], writes=[xs_r[i]])
                        layer_norm(xs1[:, i, :], xs_r[i], ln1[:, 0, :], ln1[:, 1, :], ln_r, st1[:, i, :], st_r[i])
                        kb.dma("sp", xres[0].ap()[tt * 128:(tt + 1) * 128, :], xs1[:, i, :], reads=[xs_r[i]], writes=[xres_r[0][tt]])
                        make_xT(xs1[:, i, :], xs_r[i], tt)
                        if "x1" in dump_d and sq == 0 and l == cfg.get("dump_layer", 0):
                            dump("x1", xs1[:, i, :], xs_r[i], idx=tt)
                    kb.barrier()
            if cfg.get("stop") in ("outproj", "outproj0"):
                continue
            with sb("aT", [128, NFC, 1024], BF16) as aT, \
                    sb("Wd", [128, NFC, D], BF16) as Wd, \
                    sb("Wgu", [128, 2, 2, 8, 512], BF16) as Wgu, \
                    sb("sg", [128, 2, 512], F32) as sg, \
                    sb("ln2", [128, 2, D], F32) as ln2, \
                    sb("xs2", [128, 2, D], F32) as xs2, \
                    sb("st2", [128, 2, 32], F32) as st2:
                Wd_r = Reg()
                ln_r = Reg()
                aT_r = regs(NFC)
                Wgu_r = regs(2)
                sg_r = regs(2)
                xs_r = regs(2)
                st_r = regs(2)
                kb.dma("sp", ln2[:, 0, :], bcast_rows(rowtab, l * 5120 + 3072, D), writes=[ln_r])
                kb.dma("sp", ln2[:, 1, :], bcast_rows(rowtab, l * 5120 + 4096, D), writes=[ln_r])
                for k in range(NFC):
                    for c0 in (0, 512):
                        kb.dma("sp", Wd[:, k, c0:c0 + 512], w_down16.ap()[l, k * 128:(k + 1) * 128, c0:c0 + 512], reads=[w16_r], writes=[Wd_r])
                last = (l == NLAY - 1)
                for mt in range(2):
                    tok0 = mt * 1024
                    for hc in range(NFC):
                        cg, ci = hc // 4, hc % 4
                        wi = cg % 2
                        if ci == 0:
                            ncol = min(512, FF - cg * 512)
                            for gu, wsrc in ((0, w_gate16), (1, w_up16)):
                                for k in range(8):
                                    kb.dma("sp", Wgu[:, wi, gu, k, 0:ncol],
                                           wsrc.ap()[l, k * 128:(k + 1) * 128, cg * 512:cg * 512 + ncol],
                                           reads=[w16_r], writes=[Wgu_r[wi]])
                        for blk in range(2):
                            t0 = tok0 + blk * 512
                            pg, pu = (0, 1) if blk == 0 else (2, 3)
                            for gu, pi in ((0, pg), (1, pu)):
                                def fn(pe, gu=gu, pi=pi):
                                    inst = None
                                    for k in range(8):
                                        inst = pe.matmul(PS[pi][:, :], lhsT=Wgu[:, wi, gu, k, ci * 128:(ci + 1) * 128],
                                                         rhs=XT[:, k, t0:t0 + 512], start=(k == 0), stop=(k == 7))
                                    return inst
                                kb.op("pe", fn, reads=[Wgu_r[wi]] + XT_r[t0 // 128:t0 // 128 + 4], writes=[PS_r[pi]])
                            kb.op("act", lambda a: a.activation(out=sg[:, blk, :], in_=PS[pg][:, :], func=AF.Silu),
                                  reads=[PS_r[pg]], writes=[sg_r[blk]])
                            kb.op("dve", lambda v: v.tensor_tensor(out=aT[:, hc, blk * 512:(blk + 1) * 512], in0=sg[:, blk, :],
                                                                  in1=PS[pu][:, :], op=ALU.mult),
                                  reads=[sg_r[blk], PS_r[pu]], writes=[aT_r[hc]])
                    for t8 in range(8):
                        tt = mt * 8 + t8
                        i = tt % 2
                        kb.dma("sp", xs2[:, i, :], xres[0].ap()[tt * 128:(tt + 1) * 128, :], reads=[xres_r[0][tt]], writes=[xs_r[i]])
                        for hf in range(2):
                            pi = 4 + hf

                            def fn(pe, hf=hf, pi=pi):
                                inst = None
                                for k in range(NFC):
                                    inst = pe.matmul(PS[pi][:, :], lhsT=aT[:, k, t8 * 128:(t8 + 1) * 128],
                                                     rhs=Wd[:, k, hf * 512:(hf + 1) * 512], start=(k == 0), stop=(k == NFC - 1))
                                return inst
                            kb.op("pe", fn, reads=aT_r + [Wd_r], writes=[PS_r[pi]])
                            kb.op("dve", lambda v, hf=hf, pi=pi: v.scalar_tensor_tensor(
                                out=xs2[:, i, hf * 512:(hf + 1) * 512], in0=xs2[:, i, hf * 512:(hf + 1) * 512], scalar=ALPHA,
                                in1=PS[pi][:, :], op0=ALU.mult, op1=ALU.add), reads=[PS_r[pi], xs_r[i]], writes=[xs_r[i]])
                        layer_norm(xs2[:, i, :], xs_r[i], ln2[:, 0, :], ln2[:, 1, :], ln_r, st2[:, i, :], st_r[i])
                        if last:
                            kb.dma("sp", out_d.ap()[sq, tt * 128:(tt + 1) * 128, :], xs2[:, i, :], reads=[xs_r[i]])
                        else:
                            kb.dma("sp", xres[1].ap()[tt * 128:(tt + 1) * 128, :], xs2[:, i, :], reads=[xs_r[i]], writes=[xres_r[1][tt]])
                        if "x2" in dump_d and sq == 0 and l == cfg.get("dump_layer", 0):
                            dump("x2", xs2[:, i, :], xs_r[i], idx=tt)
                    if not last:
                        pass
                kb.barrier()
                if not last:
                    for tt in range(NT):
                        i = tt % 2
                        kb.dma("sp", xs2[:, i, :], xres[1].ap()[tt * 128:(tt + 1) * 128, :], reads=[xres_r[1][tt]], writes=[xs_r[i]])
                        make_xT(xs2[:, i, :], xs_r[i], tt)
                    kb.barrier()
    kb.finish()
    return kb


def host_consts():
    c = {}
    c["c_ident"] = np.eye(128, dtype=np.float32)
    s = np.arange(128)
    c["c_caus"] = (s[:, None] <= s[None, :]).astype(np.float32)
    half = 32
    inv = (10000.0 ** (-np.arange(half, dtype=np.float32) / half)).astype(np.float32)
    ang = (np.arange(S, dtype=np.float32)[:, None] * inv[None, :]).astype(np.float32)
    cos = np.cos(ang).astype(np.float32).T
    sin = np.sin(ang).astype(np.float32).T
    cosT = np.concatenate([cos, cos, cos, cos], 0)
    sinT = np.concatenate([-sin, sin, -sin, sin], 0)
    c["c_rope"] = np.stack([cosT, sinT]).astype(np.float32)
    cc = np.arange(128)
    t = np.arange(S)
    c["c_cmpmask"] = ((16 * cc[:, None] + 31 <= t[None, :]) & (cc[:, None] < 127)).astype(np.float32)
    j = np.arange(32)
    c["c_onehot"] = ((t[None, :] // 64) == j[:, None]).astype(np.float32)
    cur = t // 64
    forced = (j[None, :] == 0) | (j[None, :] == cur[:, None]) | (j[None, :] == cur[:, None] - 1)
    future = j[None, :] > cur[:, None]
    m1 = (~forced & ~future).astype(np.float32)
    m2 = np.where(forced, 1e9, np.where(future, -1e9, 0.0)).astype(np.float32)
    selm = np.stack([m1, m2])
    c["c_selm"] = np.ascontiguousarray(selm.reshape(2, NT, 128, 32).transpose(0, 2, 1, 3))
    c0 = np.arange(127) * 16
    s0 = np.arange(32) * 64
    lo = np.maximum(c0[:, None], s0[None, :])
    hi = np.minimum(c0[:, None] + 32, s0[None, :] + 64)
    ov = np.zeros((128, 32), np.float32)
    ov[:127] = np.maximum(hi - lo, 0) / 16
    c["c_ovl"] = ov
    blk = np.zeros((128, 128), np.float32)
    blk[:64, :64] = 1
    blk[64:, 64:] = 1
    c["c_blk"] = blk
    hs = np.zeros((128, 2), np.float32)
    hs[:64, 0] = 1
    hs[64:, 1] = 1
    c["c_hsel"] = hs
    i64 = np.arange(64)
    lo_strict = (i64[None, :] < i64[:, None]).astype(np.float32)
    up_strict = (i64[:, None] < i64[None, :]).astype(np.float32)
    up_incl = (i64[:, None] <= i64[None, :]).astype(np.float32)
    c["c_rwmask"] = np.ascontiguousarray(np.stack([lo_strict, up_strict, up_strict, up_incl, up_incl], axis=1))
    return c


def host_layout(inp):
    d = {}
    f = lambda a: np.ascontiguousarray(np.asarray(a, dtype=np.float32))
    for k in ("w_in", "w_in_vres", "gla_w_a2", "rwkv_w2", "rwkv_a2", "rwkv_v2", "rwkv_g2", "nsa_wk1", "nsa_wk2",
              "nsa_wv1", "nsa_wv2", "w_out", "ffn_w_gate", "ffn_w_up", "ffn_w_down"):
        d[k] = f(inp[k])
    base = 2096
    sw = lambda b: list(range(b + 32, b + 64)) + list(range(b, b + 32))
    pl = lambda b: list(range(b, b + 64))
    cols = []
    for i in range(4):
        cols += pl(base + i * 64) + pl(base + (4 + i) * 64)
    for i in range(4):
        cols += sw(base + i * 64) + sw(base + (4 + i) * 64)
    for c0 in (512, 768, 1024):
        cols += pl(base + c0) + pl(base + c0 + 64)
        cols += sw(base + c0) + sw(base + c0 + 64)
    cols += list(range(base + 640, base + 768)) + list(range(base + 896, base + 1024)) + list(range(base + 1152, base + 1280))
    cols += list(range(base + 1280, base + 1304))
    assert len(cols) == 2200
    d["w_nsa"] = f(np.asarray(inp["w_in"])[:, :, cols])
    d["nsa_posT"] = f(np.stack([np.asarray(inp["nsa_pos_k"]).transpose(0, 2, 1),
                                np.asarray(inp["nsa_pos_v"]).transpose(0, 2, 1)], axis=1))
    pt = np.zeros((L, 128, 32), np.float32)
    mu = np.asarray(inp["rwkv_mu"])
    rwt = [(0, 128), (128, 128), (256, 128), (384, 128), (512, 128), (640, 128), (768, 64), (832, 64), (896, 128), (1024, 32)]
    for l in range(L):
        pt[l, :, 0:2] = np.asarray(inp["gla_b_a"])[l].reshape(2, 128).T
        for i, (c0, n) in enumerate(rwt):
            pt[l, :n, 2 + i] = mu[l, c0:c0 + n]
        if l >= 1:
            pt[l, :32, 12] = np.asarray(inp["rwkv_mu_vres"])[l - 1]
            pt[l, :, 17:19] = np.asarray(inp["rwkv_v0"])[l - 1].reshape(2, 128).T
        pt[l, :, 13:15] = np.asarray(inp["rwkv_w0"])[l].reshape(2, 128).T
        pt[l, :, 15:17] = np.asarray(inp["rwkv_a0"])[l].reshape(2, 128).T
        pt[l, :, 19:21] = np.asarray(inp["rwkv_k_k"])[l].reshape(2, 128).T
        pt[l, :, 21:23] = np.asarray(inp["rwkv_k_a"])[l].reshape(2, 128).T
        pt[l, :, 23:25] = np.asarray(inp["rwkv_r_k"])[l].reshape(2, 128).T
    d["ptab"] = pt
    rt = np.zeros((L, 5120), np.float32)
    for l in range(L):
        rt[l, 0:256] = np.asarray(inp["gla_ln_w"])[l]
        rt[l, 256:512] = np.asarray(inp["gla_ln_b"])[l]
        rt[l, 512:768] = np.asarray(inp["rwkv_ln_w"])[l]
        rt[l, 768:1024] = np.asarray(inp["rwkv_ln_b"])[l]
        rt[l, 1024:2048] = np.asarray(inp["ln1_w"])[l]
        rt[l, 2048:3072] = np.asarray(inp["ln1_b"])[l]
        rt[l, 3072:4096] = np.asarray(inp["ln2_w"])[l]
        rt[l, 4096:5120] = np.asarray(inp["ln2_b"])[l]
    d["rowtab"] = rt
    return d


_CACHE = {}


def kernel(**inputs):
    cfg = {}
    if "full" not in _CACHE:
        _CACHE["full"] = build(cfg)
    kb = _CACHE["full"]
    shared = host_layout(inputs)
    shared.update(host_consts())
    x = np.ascontiguousarray(np.asarray(inputs["x"], dtype=np.float32))
    in_maps = []
    for c in range(8):
        m = dict(shared)
        m["x"] = x[2 * c:2 * c + 2]
        in_maps.append(m)
    res = run_bass_kernel_spmd(kb.nc, in_maps, core_ids=list(range(8)))
    return np.concatenate([r["out"] for r in res.results], axis=0).astype(np.float32)
```

```python
import math
from contextlib import ExitStack
import numpy as np
import concourse.bass as bass
import concourse.mybir as mybir
from concourse.bass_utils import run_bass_kernel_spmd

F32 = mybir.dt.float32
BF16 = mybir.dt.bfloat16
AF = mybir.ActivationFunctionType
ALU = mybir.AluOpType
AX = mybir.AxisListType

S = 2048
D = 1024
NT = S // 128
L = 2
FF = 2816
NFC = FF // 128
ALPHA = float((2 * L) ** 0.25)
LN_EPS = 1e-5
RW_EPS = 64e-5
NDS = 6


class Reg:
    __slots__ = ("lw", "rd")

    def __init__(self):
        self.lw = None
        self.rd = {}


def regs(n):
    return [Reg() for _ in range(n)]


class _Rec:
    def __init__(self):
        self.calls = []

    def __getattr__(self, name):
        def f(*a, **kw):
            self.calls.append((name, a, kw))
            return self
        return f


class _Node:
    __slots__ = ("e", "kind", "calls", "reads", "writes", "cost", "deps")

    def __init__(self, e, kind, calls, reads, writes, cost):
        self.e, self.kind, self.calls, self.reads, self.writes, self.cost = e, kind, calls, reads, writes, cost
        self.deps = ()


def _free_elems(ap):
    try:
        n = 1
        for s_ in list(ap.shape)[1:]:
            n *= int(s_)
        return n
    except Exception:
        return 256


def _ap_bytes(ap):
    try:
        n = 1
        for s_ in list(ap.shape):
            n *= int(s_)
        return n * 4
    except Exception:
        return 65536


def _est_cost(e, calls):
    t = 0.0
    for name, a, kw in calls:
        out = kw.get("out", a[0] if a else None)
        n = _free_elems(out) if out is not None else 256
        if e == "pe":
            mul = 4.0 if (name == "matmul" and getattr(kw.get("lhsT"), "dtype", None) == F32) else 1.0
            t += mul * max(n, 64) / 1.6 + 25.0
        elif e == "act":
            t += n / 1.0 + 220.0
        elif e == "dve":
            t += n / 0.9 + 80.0
        else:
            t += n * 2.0 + 300.0
    return t


class KB:
    def __init__(self, sched=True, W=32):
        nc = bass.Bass("TRN2", target_bir_lowering=False)
        self.nc = nc
        self.E = {"pe": nc.tensor, "act": nc.scalar, "dve": nc.vector, "pool": nc.gpsimd, "sp": nc.sync}
        self.sems = {}
        self.cnt = {}
        for e in ("pe", "act", "dve", "pool"):
            self.sems[e] = nc.alloc_semaphore("s_" + e)
            self.cnt[e] = 0
        self.dq = {}
        for q, nds in (("sp", 12), ("pool", NDS), ("act", 6)):
            keys = []
            for i in range(nds):
                k = "d_%s%d" % (q, i)
                self.sems[k] = nc.alloc_semaphore(k)
                self.cnt[k] = 0
                keys.append(k)
            self.dq[q] = [keys, 0]
        self.seen = {e: {} for e in self.E}
        self.nops = 0
        self.sched = sched
        self.W = W
        self.pending = []

    def _waits(self, e, reads, writes, extra=()):
        need = {}

        def add(rec):
            if rec is None:
                return
            k, c = rec
            if need.get(k, 0) < c:
                need[k] = c

        for r in reads:
            add(r.lw)
        for w in writes:
            add(w.lw)
            for k, c in w.rd.items():
                add((k, c))
        for rec in extra:
            add(rec)
        eng = self.E[e]
        seen = self.seen[e]
        for k, c in need.items():
            if e == "pe" and k == "pe":
                continue
            if seen.get(k, 0) >= c:
                continue
            eng.wait_ge(self.sems[k], c)
            seen[k] = c

    def _mark(self, rec, reads, writes):
        k, c = rec
        for r in reads:
            if r.rd.get(k, 0) < c:
                r.rd[k] = c
        for w in writes:
            w.lw = rec
            w.rd = {}

    def op(self, e, fn, reads=(), writes=()):
        rec = _Rec()
        fn(rec)
        node = _Node(e, "op", rec.calls, list(reads), list(writes), _est_cost(e, rec.calls))
        if self.sched:
            self.pending.append(node)
        else:
            self._emit(node)

    def dma(self, q, out, in_, reads=(), writes=(), **kw):
        node = _Node(q, "dma", (out, in_, kw), list(reads), list(writes), 2000.0 + _ap_bytes(out) / 100.0)
        if self.sched:
            self.pending.append(node)
        else:
            self._emit(node)

    def _emit(self, n):
        e = n.e
        if n.kind == "op":
            self._waits(e, n.reads, n.writes)
            eng = self.E[e]
            inst = None
            for name, a, kw in n.calls:
                inst = getattr(eng, name)(*a, **kw)
            self.cnt[e] += 1
            inst.then_inc(self.sems[e], 1)
            self._mark((e, self.cnt[e]), n.reads, n.writes)
        else:
            out, in_, kw = n.calls
            keys, i = self.dq[e]
            k = keys[i]
            self.dq[e][1] = (i + 1) % len(keys)
            extra = [(k, self.cnt[k])] if self.cnt[k] else []
            self._waits(e, n.reads, n.writes, extra)
            self.E[e].dma_start(out=out, in_=in_, **kw).then_inc(self.sems[k], 16)
            self.cnt[k] += 16
            self._mark((k, self.cnt[k]), n.reads, n.writes)
        self.nops += 1

    def flush(self):
        nodes = self.pending
        self.pending = []
        if not nodes:
            return
        lastw, readers = {}, {}
        for i, n in enumerate(nodes):
            d = set()
            for r in n.reads:
                if id(r) in lastw:
                    d.add(lastw[id(r)])
            for w in n.writes:
                if id(w) in lastw:
                    d.add(lastw[id(w)])
                d.update(readers.get(id(w), ()))
            d.discard(i)
            n.deps = d
            for r in n.reads:
                readers.setdefault(id(r), []).append(i)
            for w in n.writes:
                lastw[id(w)] = i
                readers[id(w)] = []
        queues = {}
        for i, n in enumerate(nodes):
            queues.setdefault(n.e, []).append(i)
        fin = [None] * len(nodes)
        efree = {e: 0.0 for e in queues}
        W, LAT = self.W, 120.0
        remaining = len(nodes)
        while remaining:
            best = None
            for e, q in queues.items():
                for i in q[:W]:
                    n = nodes[i]
                    ready = 0.0
                    ok = True
                    for d in n.deps:
                        f = fin[d]
                        if f is None:
                            ok = False
                            break
                        if f + LAT > ready:
                            ready = f + LAT
                    if not ok:
                        continue
                    start = ready if ready > efree[e] else efree[e]
                    key = (start, i)
                    if best is None or key < best[0]:
                        best = (key, e, i)
            assert best is not None
            (start, _), e, i = best
            n = nodes[i]
            fin[i] = start + n.cost
            efree[e] = (start + 60.0) if n.kind == "dma" else fin[i]
            queues[e].remove(i)
            self._emit(n)
            remaining -= 1

    def barrier(self):
        self.flush()
        for e in self.E:
            for k, c in self.cnt.items():
                if c == 0 or (e == "pe" and k == "pe"):
                    continue
                if self.seen[e].get(k, 0) >= c:
                    continue
                self.E[e].wait_ge(self.sems[k], c)
                self.seen[e][k] = c

    def finish(self):
        self.barrier()


def bcast_rows(t, off, n, parts=128):
    return bass.AP(t, off, [[0, parts], [1, n]])


def build(cfg):
    kb = KB(sched=cfg.get("sched", True), W=cfg.get("W", 128))
    nc = kb.nc
    NSEQ = cfg.get("nseq", 2)
    NLAY = cfg.get("nlay", 2)
    mixers = cfg.get("mixers", ("gla", "rwkv", "nsa"))
    dumps = cfg.get("dumps", ())
    inject_O = cfg.get("inject_O", False)

    def dram_in(name, shape):
        return nc.dram_tensor(name, list(shape), F32, kind="ExternalInput")

    x_d = dram_in("x", [2, S, D])
    w_in = dram_in("w_in", [L, D, 3400])
    w_vres = dram_in("w_in_vres", [1, D, 32])
    w_nsa = dram_in("w_nsa", [L, D, 2200])
    gla_w_a2 = dram_in("gla_w_a2", [L, 16, 256])
    rwkv_w2 = dram_in("rwkv_w2", [L, 64, 256])
    rwkv_a2 = dram_in("rwkv_a2", [L, 64, 256])
    rwkv_v2 = dram_in("rwkv_v2", [1, 32, 256])
    rwkv_g2 = dram_in("rwkv_g2", [L, 160, 256])
    nsa_wk1 = dram_in("nsa_wk1", [L, 2048, 256])
    nsa_wk2 = dram_in("nsa_wk2", [L, 256, 64])
    nsa_wv1 = dram_in("nsa_wv1", [L, 2048, 256])
    nsa_wv2 = dram_in("nsa_wv2", [L, 256, 64])
    nsa_posT = dram_in("nsa_posT", [L, 2, 64, 32])
    w_out = dram_in("w_out", [L, D, D])
    w_gate = dram_in("ffn_w_gate", [L, D, FF])
    w_up = dram_in("ffn_w_up", [L, D, FF])
    w_down = dram_in("ffn_w_down", [L, FF, D])
    ptab = dram_in("ptab", [L, 128, 32])
    rowtab = dram_in("rowtab", [L, 5120])
    c_ident = dram_in("c_ident", [128, 128])
    c_caus = dram_in("c_caus", [128, 128])
    c_rope = dram_in("c_rope", [2, 128, S])
    c_cmpmask = dram_in("c_cmpmask", [128, S])
    c_onehot = dram_in("c_onehot", [32, S])
    c_selm = dram_in("c_selm", [2, 128, NT, 32])
    c_ovl = dram_in("c_ovl", [128, 32])
    c_blk = dram_in("c_blk", [128, 128])
    c_hsel = dram_in("c_hsel", [128, 2])
    c_rwmask = dram_in("c_rwmask", [64, 5, 64])
    if inject_O:
        dbg_OT = dram_in("dbg_OT", [128, 8, S])
    out_d = nc.dram_tensor("out", [2, S, D], F32, kind="ExternalOutput")
    xres = [nc.dram_tensor("xres%d" % i, [S, D], F32, kind="Internal") for i in range(2)]
    xres_r = [regs(NT) for _ in range(2)]
    vfirst_d = nc.dram_tensor("vfirst", [2, 128, S], F32, kind="Internal")
    vfirst_r = regs(2)
    dump_d = {}
    for name, shape in dumps:
        dump_d[name] = nc.dram_tensor("dump_" + name, list(shape), F32, kind="ExternalOutput")

    XT = nc.alloc_sbuf_tensor("XT", [128, 8, S], BF16)
    XT_r = regs(NT)
    ident = nc.alloc_sbuf_tensor("ident", [128, 128], BF16)
    identf = nc.alloc_sbuf_tensor("identf", [128, 128], F32)
    caus = nc.alloc_sbuf_tensor("caus", [128, 4, 128], F32)
    ptb = nc.alloc_sbuf_tensor("ptb", [128, L, 32], F32)
    ptn = nc.alloc_sbuf_tensor("ptn", [128, L, 32], F32)
    c_r = Reg()
    for l in range(L):
        kb.dma("sp", ptb[:, l, :], ptab.ap()[l], writes=[c_r])
    kb.dma("pool", ident[:], c_ident.ap(), writes=[c_r])
    kb.dma("sp", identf[:], c_ident.ap(), writes=[c_r])
    for i in range(4):
        kb.dma("sp", caus[:, i, :], c_caus.ap(), writes=[c_r])

    def dram16(name, shape):
        return nc.dram_tensor(name, list(shape), BF16, kind="Internal")

    conv_list = [(w_in, [L, D, 3400]), (w_nsa, [L, D, 2200]), (w_out, [L, D, D]), (w_gate, [L, D, FF]), (w_up, [L, D, FF]),
                 (w_down, [L, FF, D]), (nsa_wk1, [L, 2048, 256]), (nsa_wv1, [L, 2048, 256])]
    w16 = {}
    w16_r = Reg()
    CH = 2816
    with nc.sbuf_tensor("cv_f", [128, 3, CH], F32) as cvf, nc.sbuf_tensor("cv_b", [128, 3, CH], BF16) as cvb:
        cvf_r, cvb_r = regs(3), regs(3)
        job = 0
        for src, shape in conv_list:
            dst = dram16(src.name + "_16", shape)
            w16[src.name] = dst
            tot = 1
            for d_ in shape:
                tot *= d_
            per = tot // 128
            assert per * 128 == tot
            for c0 in range(0, per, CH):
                n = min(CH, per - c0)
                i = job % 3
                kb.dma("sp", cvf[:, i, 0:n], bass.AP(src, c0, [[per, 128], [1, n]]), writes=[cvf_r[i]])
                eng = "act"
                if eng == "act":
                    kb.op("act", lambda a: a.activation(out=cvb[:, i, 0:n], in_=cvf[:, i, 0:n], func=AF.Copy),
                          reads=[cvf_r[i]], writes=[cvb_r[i]])
                else:
                    kb.op(eng, lambda v: v.tensor_scalar(out=cvb[:, i, 0:n], in0=cvf[:, i, 0:n], scalar1=1.0, scalar2=None, op0=ALU.mult),
                          reads=[cvf_r[i]], writes=[cvb_r[i]])
                kb.dma("act", bass.AP(dst, c0, [[per, 128], [1, n]]), cvb[:, i, 0:n], reads=[cvb_r[i]], writes=[w16_r])
                job += 1
        kb.barrier()
    w_in16, w_nsa16, w_out16 = w16["w_in"], w16["w_nsa"], w16["w_out"]
    w_gate16, w_up16, w_down16 = w16["ffn_w_gate"], w16["ffn_w_up"], w16["ffn_w_down"]
    wk1_16, wv1_16 = w16["nsa_wk1"], w16["nsa_wv1"]

    PS = [nc.alloc_psum_tensor("ps%d" % i, [128, 512], F32) for i in range(6)]
    PS_r = regs(6)
    PB = [nc.alloc_psum_tensor("pb%d" % i, [128, 1024], BF16) for i in range(2)]
    PB_r = regs(2)

    _uid = [0]

    def sb(name, shape, dt):
        _uid[0] += 1
        return nc.sbuf_tensor("%s_%d" % (name, _uid[0]), list(shape), dt)

    def proj_fm(ps_ap, Wt, c0, M, tok0, N, wreads, treads, pw, kparts=8):
        def fn(pe):
            inst = None
            for k in range(kparts):
                inst = pe.matmul(ps_ap, lhsT=Wt[:, k, c0:c0 + M], rhs=XT[:, k, tok0:tok0 + N],
                                 start=(k == 0), stop=(k == kparts - 1))
            return inst
        kb.op("pe", fn, reads=list(wreads) + list(treads), writes=[pw])

    def proj_tm(ps_ap, Wt, c0, N, tt, wreads, pw):
        def fn(pe):
            inst = None
            for k in range(8):
                inst = pe.matmul(ps_ap, lhsT=XT[:, k, tt * 128:(tt + 1) * 128], rhs=Wt[:, k, c0:c0 + N],
                                 start=(k == 0), stop=(k == 7))
            return inst
        kb.op("pe", fn, reads=list(wreads) + [XT_r[tt]], writes=[pw])

    def load_w(Wt, src3, ncols, wreg, q="sp", chunk=None):
        for k in range(8):
            kb.dma(q, Wt[:, k, 0:ncols], src3[k * 128:(k + 1) * 128, 0:ncols], reads=[w16_r], writes=[wreg])

    xb = [nc.alloc_sbuf_tensor("xb%d" % i, [128, D], BF16) for i in range(2)]
    xb_r = regs(2)
    xb_i = [0]

    def make_xT(src_ap, src_reg, tt):
        i = xb_i[0]
        xb_i[0] ^= 1
        kb.op("act", lambda a: a.activation(out=xb[i][:], in_=src_ap, func=AF.Copy),
              reads=[src_reg], writes=[xb_r[i]])
        pb = PB[i]

        def fn(pe):
            inst = None
            for k in range(8):
                inst = pe.transpose(out=pb[:, k * 128:(k + 1) * 128], in_=xb[i][:, k * 128:(k + 1) * 128],
                                    identity=ident[:])
            return inst
        kb.op("pe", fn, reads=[xb_r[i], c_r], writes=[PB_r[i]])
        kb.op("dve", lambda v: v.tensor_copy(out=XT[:, :, tt * 128:(tt + 1) * 128],
                                             in_=pb[:, :].rearrange("p (k t) -> p k t", k=8)),
              reads=[PB_r[i]], writes=[XT_r[tt]])

    def layer_norm(xt_ap, xreg, lnw_ap, lnb_ap, lnreg, st, st_r):
        kb.op("dve", lambda v: v.bn_stats(out=st[:, 0:6], in_=xt_ap[:, 0:512]), reads=[xreg], writes=[st_r])
        kb.op("dve", lambda v: v.bn_stats(out=st[:, 6:12], in_=xt_ap[:, 512:1024]), reads=[xreg, st_r], writes=[st_r])
        kb.op("dve", lambda v: v.bn_aggr(out=st[:, 12:14], in_=st[:, 0:12]), reads=[st_r], writes=[st_r])
        kb.op("act", lambda a: a.activation(out=st[:, 14:15], in_=st[:, 13:14], func=AF.Sqrt, bias=LN_EPS, scale=1.0),
              reads=[st_r], writes=[st_r])
        kb.op("dve", lambda v: v.reciprocal(out=st[:, 15:16], in_=st[:, 14:15]), reads=[st_r], writes=[st_r])
        kb.op("dve", lambda v: v.scalar_tensor_tensor(out=st[:, 16:17], in0=st[:, 12:13], scalar=-1.0, in1=st[:, 15:16],
                                                      op0=ALU.mult, op1=ALU.mult), reads=[st_r], writes=[st_r])
        kb.op("act", lambda a: a.activation(out=xt_ap, in_=xt_ap, func=AF.Identity, bias=st[:, 16:17], scale=st[:, 15:16]),
              reads=[st_r, xreg], writes=[xreg])
        kb.op("dve", lambda v: v.tensor_tensor(out=xt_ap, in0=xt_ap, in1=lnw_ap, op=ALU.mult), reads=[xreg, lnreg], writes=[xreg])
        kb.op("dve", lambda v: v.tensor_tensor(out=xt_ap, in0=xt_ap, in1=lnb_ap, op=ALU.add), reads=[xreg, lnreg], writes=[xreg])

    def dump(name, sb_ap, reg, idx=None):
        if name in dump_d:
            dst = dump_d[name].ap() if idx is None else dump_d[name].ap()[idx]
            kb.dma("sp", dst, sb_ap, reads=[reg])

    kb.op("dve", lambda v: v.tensor_scalar(out=ptn[:, :, :], in0=ptb[:, :, :], scalar1=-1.0, scalar2=None, op0=ALU.mult),
          reads=[c_r], writes=[c_r])
    kb.op("dve", lambda v: v.tensor_scalar(out=ptn[:, :, 2:13], in0=ptb[:, :, 2:13], scalar1=-1.0, scalar2=1.0,
                                           op0=ALU.mult, op1=ALU.add), reads=[c_r], writes=[c_r])

    def gla_phase(sq, l, OT, OT_r):
        with ExitStack() as es:
            A = lambda n_, s_, d_: es.enter_context(sb(n_, s_, d_))
            Wg = A("Wg", [128, 8, 1152], BF16)
            wa2 = A("wa2", [16, 256], BF16)
            gln = A("gln", [128, 2, 256], F32)
            alrT = A("alrT", [16, 512], BF16)
            t1 = A("gt1", [128, 512], F32)
            t2 = A("gt2", [128, 512], F32)
            EB = A("EB", [128, 2, 512], F32)
            QTl = A("gQT", [128, 2, 2, 512], BF16)
            KTl = A("gKT", [128, 2, 512], BF16)
            Ktok = A("gKtok", [128, 4, 256], BF16)
            V = A("gV", [128, 4, 256], BF16)
            SG = A("gSG", [128, 4, 256], F32)
            AT = A("gAT", [128, 4, 128], BF16)
            St = A("gSt", [128, 2, 128], F32)
            tmpS = A("gtmpS", [128, 128], F32)
            blkm = A("gblk", [128, 128], F32)
            Sb = A("gSb", [128, 2, 128], BF16)
            rmask = A("grm", [128, 512], F32)
            ow = A("gow", [128, 256], F32)
            ow2 = A("gow2", [128, 256], F32)
            ob = A("gob", [128, 256], BF16)
            st = A("gst", [128, 32], F32)
            W_r, p_r, alr_r, t1_r, t2_r = Reg(), Reg(), Reg(), Reg(), Reg()
            EB_r, QT_r, KT_r = regs(2), regs(2), regs(2)
            Ktok_r, V_r, SG_r, AT_r, St_r, Sb_r = Reg(), regs(4), regs(4), Reg(), Reg(), Reg()
            ow_r, ow2_r, ob_r, st_r = Reg(), Reg(), Reg(), Reg()
            gs = cfg.get("gla_stop", 99)
            load_w(Wg, w_in16.ap()[l][:, 0:1040], 1040, W_r)
            kb.dma("pool", wa2[:, :], gla_w_a2.ap()[l], writes=[p_r])
            kb.dma("sp", gln[:, 0, :], bcast_rows(rowtab, l * 5120 + 0, 256), writes=[p_r])
            kb.dma("sp", gln[:, 1, :], bcast_rows(rowtab, l * 5120 + 256, 256), writes=[p_r])
            kb.op("pool", lambda g: g.memset(rmask[:, :], 1.0), writes=[p_r])
            kb.op("pool", lambda g: g.memset(rmask[:, :].rearrange("p (c t) -> p c t", c=4)[:, :, 0:1], 0.0), writes=[p_r])
            kb.op("pool", lambda g: g.memset(St[:, :, :], 0.0), writes=[St_r])
            kb.op("pool", lambda g: g.memset(QTl[:, :, :, :], 0.0), writes=QT_r)
            kb.dma("sp", blkm[:, :], c_blk.ap(), writes=[p_r])
            tmpS_r = Reg()
            kb.op("pool", lambda g: g.memset(Sb[:, :, :], 0.0), writes=[Sb_r])
            for mt in range(4 if gs > 0 else 0):
                tok0 = mt * 512
                xr = XT_r[mt * 4:mt * 4 + 4]
                proj_fm(PS[0][0:16, :], Wg, 1024, 16, tok0, 512, [W_r], xr, PS_r[0])
                kb.op("act", lambda a: a.activation(out=alrT[:, :], in_=PS[0][0:16, :], func=AF.Copy),
                      reads=[PS_r[0]], writes=[alr_r])
                for hp in range(2):
                    kb.op("pe", lambda pe: pe.matmul(PS[1][:, :], lhsT=wa2[0:16, hp * 128:(hp + 1) * 128], rhs=alrT[0:16, :],
                                                     start=True, stop=True), reads=[p_r, alr_r], writes=[PS_r[1]])
                    kb.op("act", lambda a: a.activation(out=t1[:, :], in_=PS[1][:, :], func=AF.Exp, scale=-1.0,
                                                        bias=ptn[:, l, hp:hp + 1]), reads=[PS_r[1], c_r], writes=[t1_r])
                    kb.op("act", lambda a: a.activation(out=t1[:, :], in_=t1[:, :], func=AF.Ln, bias=1.0, scale=1.0),
                          reads=[t1_r], writes=[t1_r])
                    kb.op("dve", lambda v: v.tensor_tensor_scan(out=t2[:, :], data0=rmask[:, :], data1=t1[:, :], initial=0.0,
                                                                op0=ALU.mult, op1=ALU.add), reads=[t1_r, p_r], writes=[t2_r])
                    kb.op("act", lambda a: a.activation(out=EB[:, hp, :], in_=t2[:, :], func=AF.Exp, scale=-1.0 / 16.0),
                          reads=[t2_r], writes=[EB_r[hp]])
                    kb.op("act", lambda a: a.activation(out=t1[:, :], in_=t2[:, :], func=AF.Exp, scale=1.0 / 16.0),
                          reads=[t2_r], writes=[t1_r])
                    proj_fm(PS[2][:, :], Wg, hp * 128, 128, tok0, 512, [W_r], xr, PS_r[2])
                    for hh in range(2):
                        pr = slice(hh * 64, hh * 64 + 64)
                        kb.op("dve", lambda v: v.scalar_tensor_tensor(out=QTl[pr, hp, hh, :], in0=PS[2][pr, :], scalar=0.125,
                                                                      in1=EB[pr, hp, :], op0=ALU.mult, op1=ALU.mult),
                              reads=[PS_r[2], EB_r[hp]], writes=[QT_r[hp]])
                    proj_fm(PS[3][:, :], Wg, 256 + hp * 128, 128, tok0, 512, [W_r], xr, PS_r[3])
                    kb.op("dve", lambda v: v.tensor_tensor(out=KTl[:, hp, :], in0=PS[3][:, :], in1=t1[:, :], op=ALU.mult),
                          reads=[PS_r[3], t1_r], writes=[KT_r[hp]])

                    def fn(pe):
                        inst = None
                        for j in range(4):
                            inst = pe.transpose(out=PB[0][:, j * 128:(j + 1) * 128], in_=KTl[:, hp, j * 128:(j + 1) * 128],
                                                identity=ident[:])
                        return inst
                    kb.op("pe", fn, reads=[KT_r[hp], c_r], writes=[PB_r[0]])
                    kb.op("dve", lambda v: v.tensor_copy(out=Ktok[:, :, hp * 128:(hp + 1) * 128],
                                                         in_=PB[0][:, 0:512].rearrange("p (j c) -> p j c", j=4)),
                          reads=[PB_r[0]], writes=[Ktok_r])
                for j in range(4 if gs > 1 else 0):
                    tt = mt * 4 + j
                    proj_tm(PS[4][:, 0:256], Wg, 512, 256, tt, [W_r], PS_r[4])
                    proj_tm(PS[5][:, 0:256], Wg, 768, 256, tt, [W_r], PS_r[5])
                    if cfg.get("gv", 3) >= 2:
                        kb.op("dve", lambda v: v.tensor_scalar(out=V[:, j, :], in0=PS[4][:, 0:256], scalar1=1.0, scalar2=None, op0=ALU.mult),
                              reads=[PS_r[4]], writes=[V_r[j]])
                    if cfg.get("gv", 3) >= 3:
                        kb.op("act", lambda a: a.activation(out=SG[:, j, :], in_=PS[5][:, 0:256], func=AF.Silu),
                              reads=[PS_r[5]], writes=[SG_r[j]])
                for j in range(4 if gs > 2 else 0):
                    tt = mt * 4 + j
                    cc = slice(j * 128, (j + 1) * 128)

                    def fn(pe):
                        inst = None
                        for h in range(4):
                            hp, hh = h // 2, h % 2
                            inst = pe.matmul(PS[0][:, h * 128:(h + 1) * 128], lhsT=KTl[:, hp, cc], rhs=QTl[:, hp, hh, cc],
                                             start=True, stop=True)
                        return inst
                    kb.op("pe", fn, reads=QT_r + KT_r, writes=[PS_r[0]])
                    kb.op("dve", lambda v: v.tensor_tensor(out=AT[:, :, :], in0=PS[0][:, :].rearrange("p (h t) -> p h t", h=4),
                                                          in1=caus[:, :, :], op=ALU.mult), reads=[PS_r[0], c_r], writes=[AT_r])

                    def fn(pe):
                        inst = None
                        for h in range(4):
                            hp, hh = h // 2, h % 2
                            pe.matmul(PS[1][:, h * 64:(h + 1) * 64], lhsT=AT[:, h, :], rhs=V[:, j, h * 64:(h + 1) * 64],
                                      start=True, stop=False)
                            inst = pe.matmul(PS[1][:, h * 64:(h + 1) * 64], lhsT=QTl[:, hp, hh, cc], rhs=Sb[:, hp, hh * 64:(hh + 1) * 64],
                                             start=False, stop=True)
                        return inst
                    if gs <= 3:
                        continue
                    kb.op("pe", fn, reads=[AT_r, V_r[j], Sb_r] + QT_r, writes=[PS_r[1]])

                    def fn(pe):
                        inst = None
                        for hp in range(2):
                            inst = pe.matmul(PS[2][:, hp * 128:(hp + 1) * 128], lhsT=Ktok[:, j, hp * 128:(hp + 1) * 128],
                                             rhs=V[:, j, hp * 128:(hp + 1) * 128], start=True, stop=True)
                        return inst
                    if gs <= 4:
                        continue
                    kb.op("pe", fn, reads=[Ktok_r, V_r[j]], writes=[PS_r[2]])
                    for hp in range(2):
                        ee = EB[:, hp, j * 128 + 127:j * 128 + 128]
                        kb.op("dve", lambda v: v.scalar_tensor_tensor(out=tmpS[:, :], in0=PS[2][:, hp * 128:(hp + 1) * 128], scalar=ee,
                                                                      in1=blkm[:, :], op0=ALU.mult, op1=ALU.mult),
                              reads=[PS_r[2], EB_r[hp], p_r], writes=[tmpS_r])
                        kb.op("dve", lambda v: v.scalar_tensor_tensor(out=St[:, hp, :], in0=St[:, hp, :], scalar=ee, in1=tmpS[:, :],
                                                                      op0=ALU.mult, op1=ALU.add),
                              reads=[St_r, tmpS_r, EB_r[hp]], writes=[St_r])
                    kb.op("act", lambda a: a.activation(out=Sb[:, :, :], in_=St[:, :, :], func=AF.Copy), reads=[St_r], writes=[Sb_r])
                    if gs <= 5:
                        continue
                    head_norm_gate(PS[1][:, 0:256], PS_r[1], ow, ow_r, ow2, ow2_r, st, st_r, gln, p_r, LN_EPS)
                    kb.op("dve", lambda v: v.tensor_tensor(out=ob[:, :], in0=ow[:, :], in1=SG[:, j, :], op=ALU.mult),
                          reads=[ow_r, SG_r[j]], writes=[ob_r])
                    if gs <= 6:
                        continue
                    out_to_OT(ob, ob_r, 128, OT, OT_r, 0, tt * 128)
            kb.barrier()

    def head_norm_gate(ps_ap, ps_r, ow, ow_r, ow2, ow2_r, st, st_r, gln, gln_r, eps, P=128):
        v4 = lambda ap: ap.rearrange("p (h d) -> p h d", h=4)
        kb.op("act", lambda a: a.activation(out=ow[0:P, :], in_=ps_ap, func=AF.Copy), reads=[ps_r], writes=[ow_r])
        kb.op("act", lambda a: a.activation(out=ow2[0:P, :], in_=ow[0:P, :], func=AF.Square), reads=[ow_r], writes=[ow2_r])
        kb.op("dve", lambda v: v.reduce_sum(out=st[0:P, 0:4], in_=v4(ow[0:P, :]), axis=AX.X), reads=[ow_r], writes=[st_r])
        kb.op("dve", lambda v: v.reduce_sum(out=st[0:P, 4:8], in_=v4(ow2[0:P, :]), axis=AX.X), reads=[ow2_r, st_r], writes=[st_r])
        kb.op("dve", lambda v: v.tensor_scalar(out=st[0:P, 8:16], in0=st[0:P, 0:8], scalar1=1.0 / 64.0, scalar2=None, op0=ALU.mult),
              reads=[st_r], writes=[st_r])
        kb.op("dve", lambda v: v.tensor_tensor(out=st[0:P, 16:20], in0=st[0:P, 8:12], in1=st[0:P, 8:12], op=ALU.mult),
              reads=[st_r], writes=[st_r])
        kb.op("dve", lambda v: v.tensor_tensor(out=st[0:P, 20:24], in0=st[0:P, 12:16], in1=st[0:P, 16:20], op=ALU.subtract),
              reads=[st_r], writes=[st_r])
        kb.op("act", lambda a: a.activation(out=st[0:P, 24:28], in_=st[0:P, 20:24], func=AF.Sqrt, bias=eps, scale=1.0),
              reads=[st_r], writes=[st_r])
        kb.op("dve", lambda v: v.reciprocal(out=st[0:P, 28:32], in_=st[0:P, 24:28]), reads=[st_r], writes=[st_r])
        for h in range(4):
            kb.op("dve", lambda v: v.tensor_scalar(out=ow[0:P, h * 64:(h + 1) * 64], in0=ow[0:P, h * 64:(h + 1) * 64],
                                                   scalar1=st[0:P, 8 + h:9 + h], scalar2=st[0:P, 28 + h:29 + h],
                                                   op0=ALU.subtract, op1=ALU.mult), reads=[ow_r, st_r], writes=[ow_r])
        kb.op("dve", lambda v: v.tensor_tensor(out=ow[0:P, :], in0=ow[0:P, :], in1=gln[0:P, 0, :], op=ALU.mult),
              reads=[ow_r, gln_r], writes=[ow_r])
        kb.op("dve", lambda v: v.tensor_tensor(out=ow[0:P, :], in0=ow[0:P, :], in1=gln[0:P, 1, :], op=ALU.add),
              reads=[ow_r, gln_r], writes=[ow_r])

    def out_to_OT(ob, ob_r, P, OT, OT_r, k0, tokc0, ncols=256):
        nk = ncols // 128

        def fn(pe):
            inst = None
            for kk in range(nk):
                inst = pe.transpose(out=PB[1][:, kk * 128:kk * 128 + P], in_=ob[0:P, kk * 128:(kk + 1) * 128],
                                    identity=ident[0:P, 0:P])
            return inst
        kb.op("pe", fn, reads=[ob_r, c_r], writes=[PB_r[1]])
        tr = OT_r[tokc0 // 128]
        kb.op("act", lambda a: a.activation(out=OT[:, k0:k0 + nk, tokc0:tokc0 + P],
                                            in_=PB[1][:, 0:nk * 128].rearrange("p (k t) -> p k t", k=nk)[:, :, 0:P], func=AF.Copy),
              reads=[PB_r[1]], writes=[tr])

    RWT = [(0, 128), (128, 128), (256, 128), (384, 128), (512, 128), (640, 128), (768, 64), (832, 64),
           (896, 128), (1024, 32), (1056, 32)]
    C0 = float(math.exp(-0.5))

    def rwkv_phase(sq, l, OT, OT_r):
        MT = 256
        NCH = MT // 64
        NU = NCH * 4
        with ExitStack() as es:
            A = lambda n_, s_, d_: es.enter_context(sb(n_, s_, d_))
            Wr = A("Wr", [128, 8, 1088], BF16)
            w2 = A("rw2", [64, 256], BF16)
            a2 = A("ra2", [64, 256], BF16)
            v2 = A("rv2", [32, 256], BF16)
            g2a = A("rg2a", [128, 256], BF16)
            g2b = A("rg2b", [128, 256], BF16)
            rln = A("rln", [128, 2, 256], F32)
            blkm = A("rblk", [128, 128], F32)
            blkf = A("rblkf", [128, 128], F32)
            hselb = A("rhsel", [128, 2], BF16)
            rmask = A("rrm", [128, MT], F32)
            amask = A("ramask", [64, 5, 64], F32)
            xs = [A("rxs%d" % i, [128, MT], F32) for i in range(6)]
            tt_ = [A("rt%d" % i, [128, MT], F32) for i in range(8)]
            Gam2 = [A("rGam%d" % i_, [128, 2, MT], F32) for i_ in range(2)]
            ATz2 = [A("rATz%d" % i_, [128, 2, 2, MT], BF16) for i_ in range(2)]
            RTz2 = [A("rRTz%d" % i_, [128, 2, 2, MT], BF16) for i_ in range(2)]
            BTt = A("rBT", [128, 2, MT], BF16)
            KTt = A("rKT", [128, 2, MT], BF16)
            rkr = A("rrkr", [128, 2, MT], BF16)
            TW = A("rTW", [64, MT], BF16)
            AL = A("rAL", [64, MT], BF16)
            SGL = A("rSGL", [128, MT], BF16)
            SGL2 = A("rSGL2", [128, MT], BF16)
            VR = A("rVR", [32, MT], BF16)
            vb = A("rvb", [128, MT], BF16)
            Vtok2 = [A("rVtok%d" % i_, [128, NCH, 256], BF16) for i_ in range(2)]
            Btok2 = [A("rBtok%d" % i_, [128, NCH, 256], BF16) for i_ in range(2)]
            Ktok2 = [A("rKtok%d" % i_, [128, NCH, 256], BF16) for i_ in range(2)]
            gtok2 = [A("rgtok%d" % i_, [64, NCH, 256], F32) for i_ in range(2)]
            cb2 = [A("rcb%d" % i_, [64, NCH, 4], F32) for i_ in range(2)]
            MN = [A("rMN%d" % i, [64, NU, 2, 64], F32) for i in range(2)]
            Pm2 = [A("rP%d" % i_, [64, NU, 64], F32) for i_ in range(2)]
            A32 = [A("rA3%d" % i_, [128, NU, 3, 64], BF16) for i_ in range(2)]
            Rs = A("rRs", [64, 256], F32)
            Ub = A("rUb", [128, 256], BF16)
            Tblk = A("rTblk", [128, 2, 128], F32)
            Tb = A("rTb", [128, 2, 128], BF16)
            tmpS = A("rtmpS", [128, 128], F32)
            ow = A("row", [128, 256], F32)
            ow2 = A("row2", [128, 256], F32)
            ob = A("rob", [128, 256], BF16)
            st = A("rst", [128, 32], F32)
            W_r, p_r = Reg(), Reg()
            xs_r, t_r = regs(6), regs(8)
            BT_r, KT_r, rkr_r = regs(2), regs(2), regs(2)
            Gam_r2, ATz_r2, RTz_r2 = [regs(2), regs(2)], [regs(2), regs(2)], [regs(2), regs(2)]
            TW_r, AL_r, SGL_r, SGL2_r, VR_r, vb_r = Reg(), Reg(), Reg(), Reg(), Reg(), Reg()
            Vtok_r2, Btok_r2, Ktok_r2, gtok_r2, cb_r2 = regs(2), regs(2), regs(2), regs(2), regs(2)
            MN_r, Rs_r, Ub_r, T_r, Tb_r, tmpS_r = regs(2), Reg(), Reg(), Reg(), Reg(), Reg()
            P_r2, A3_r2 = regs(2), regs(2)
            ow_r, ow2_r, ob_r, st_r = Reg(), Reg(), Reg(), Reg()
            rs_ = cfg.get("rw_stop", 99)
            load_w(Wr, w_in16.ap()[l][:, 1040:2096], 1056, W_r)
            if l >= 1:
                for k in range(8):
                    kb.dma("pool", Wr[:, k, 1056:1088], w_vres.ap()[l - 1][k * 128:(k + 1) * 128, :], writes=[W_r])
            kb.dma("pool", w2[:, :], rwkv_w2.ap()[l], writes=[p_r])
            kb.dma("pool", a2[:, :], rwkv_a2.ap()[l], writes=[p_r])
            if l >= 1:
                kb.dma("pool", v2[:, :], rwkv_v2.ap()[l - 1], writes=[p_r])
            kb.op("pool", lambda g: g.memset(g2b[:, :], 0.0), writes=[p_r])
            kb.dma("pool", g2a[:, :], rwkv_g2.ap()[l][0:128, :], writes=[p_r])
            kb.dma("pool", g2b[0:32, :], rwkv_g2.ap()[l][128:160, :], reads=[p_r], writes=[p_r])
            kb.dma("sp", rln[:, 0, :], bcast_rows(rowtab, l * 5120 + 512, 256), writes=[p_r])
            kb.dma("sp", rln[:, 1, :], bcast_rows(rowtab, l * 5120 + 768, 256), writes=[p_r])
            kb.dma("sp", blkm[:, :], c_blk.ap(), writes=[p_r])
            kb.dma("sp", blkf[:, :], c_blk.ap(), writes=[p_r])
            kb.dma("pool", hselb[:, :], c_hsel.ap(), writes=[p_r])
            kb.dma("sp", amask[:, :, :], c_rwmask.ap(), writes=[p_r])
            kb.op("pool", lambda g: g.memset(rmask[:, :], 1.0), writes=[p_r])
            kb.op("pool", lambda g: g.memset(rmask[:, :].rearrange("p (c t) -> p c t", c=NCH)[:, :, 0:1], 0.0), writes=[p_r])
            zl = [(Tblk, [T_r]), (Tb, [Tb_r]), (SGL2, [SGL2_r]), (Ub, [Ub_r])]
            for i_ in range(2):
                zl += [(ATz2[i_], ATz_r2[i_]), (RTz2[i_], RTz_r2[i_]), (Vtok2[i_], [Vtok_r2[i_]]), (Btok2[i_], [Btok_r2[i_]]),
                       (Ktok2[i_], [Ktok_r2[i_]]), (A32[i_], [A3_r2[i_]])]
            for tz, rr in zl:
                kb.op("pool", lambda g, tz=tz: g.memset(tz[:], 0.0), writes=rr)

            carry = A("rcarry", [128, 12], F32)
            carry_r = regs(11)
            kb.op("pool", lambda g: g.memset(carry[:, :], 0.0), writes=carry_r)

            def shift_proj(i, tok0, dst_fn):
                c0, n = RWT[i]
                mucol = 2 + i
                pi = i % 2
                tp = tt_[7] if pi == 0 else tt_[6]
                tpr = t_r[7] if pi == 0 else t_r[6]
                proj_fm(PS[pi][0:n, 0:MT], Wr, c0, n, tok0, MT, [W_r], XT_r[tok0 // 128:tok0 // 128 + MT // 128], PS_r[pi])
                kb.op("act", lambda a: a.activation(out=tp[0:n, 1:MT], in_=PS[pi][0:n, 0:MT - 1], func=AF.Copy,
                                                    scale=ptb[0:n, l, mucol:mucol + 1]), reads=[PS_r[pi], c_r], writes=[tpr, PS_r[pi]])
                kb.op("act", lambda a: a.activation(out=tp[0:n, 0:1], in_=carry[0:n, i:i + 1], func=AF.Copy,
                                                    scale=ptb[0:n, l, mucol:mucol + 1]), reads=[carry_r[i], c_r, tpr], writes=[tpr])
                kb.op("act", lambda a: a.activation(out=carry[0:n, i:i + 1], in_=PS[pi][0:n, MT - 1:MT], func=AF.Copy),
                      reads=[PS_r[pi], carry_r[i]], writes=[carry_r[i], PS_r[pi]])
                dst_ap, dst_regs = dst_fn()
                kb.op("dve", lambda v: v.scalar_tensor_tensor(out=dst_ap, in0=PS[pi][0:n, 0:MT], scalar=ptn[0:n, l, mucol:mucol + 1],
                                                              in1=tp[0:n, 0:MT], op0=ALU.mult, op1=ALU.add),
                      reads=[PS_r[pi], tpr, c_r], writes=dst_regs + [PS_r[pi]])

            def prep(mt):
                tok0 = mt * MT
                par = mt % 2
                ATz, RTz, Vtok, Btok, Ktok, Gam, gtok, cb, Pm, A3 = ATz2[par], RTz2[par], Vtok2[par], Btok2[par], Ktok2[par], Gam2[par], gtok2[par], cb2[par], Pm2[par], A32[par]
                ATz_r, RTz_r, Vtok_r, Btok_r, Ktok_r, Gam_r, gtok_r, cb_r, P_r, A3_r = ATz_r2[par], RTz_r2[par], Vtok_r2[par], Btok_r2[par], Ktok_r2[par], Gam_r2[par], gtok_r2[par], cb_r2[par], P_r2[par], A3_r2[par]
                for i in range(6):
                    shift_proj(i, tok0, lambda i=i: (xs[i][:, :], [xs_r[i]]))
                shift_proj(6, tok0, lambda: (tt_[0][0:64, :], [t_r[0]]))
                kb.op("act", lambda a: a.activation(out=TW[:, :], in_=tt_[0][0:64, :], func=AF.Tanh), reads=[t_r[0]], writes=[TW_r])
                shift_proj(7, tok0, lambda: (AL[:, :], [AL_r]))
                shift_proj(8, tok0, lambda: (tt_[0][:, :], [t_r[0]]))
                kb.op("act", lambda a: a.activation(out=SGL[:, :], in_=tt_[0][:, :], func=AF.Sigmoid), reads=[t_r[0]], writes=[SGL_r])
                shift_proj(9, tok0, lambda: (tt_[0][0:32, :], [t_r[0]]))
                kb.op("act", lambda a: a.activation(out=SGL2[0:32, :], in_=tt_[0][0:32, :], func=AF.Sigmoid), reads=[t_r[0]], writes=[SGL2_r])
                if l >= 1:
                    shift_proj(10, tok0, lambda: (VR[:, :], [VR_r]))
                if rs_ <= 1:
                    return
                for hp in range(2):
                    rT, kT, vT = xs[hp], xs[2 + hp], xs[4 + hp]
                    rT_r, kT_r, vT_r = xs_r[hp], xs_r[2 + hp], xs_r[4 + hp]
                    t1, t2, t3, t4, t5, t6, t7 = tt_[0:7]
                    kb.op("pe", lambda pe: pe.matmul(PS[2][:, 0:MT], lhsT=w2[:, hp * 128:(hp + 1) * 128], rhs=TW[:, :], start=True, stop=True),
                          reads=[p_r, TW_r], writes=[PS_r[2]])
                    kb.op("act", lambda a: a.activation(out=t1[:, :], in_=PS[2][:, 0:MT], func=AF.Sigmoid, bias=ptb[:, l, 13 + hp:14 + hp]),
                          reads=[PS_r[2], c_r], writes=[t_r[0]])
                    kb.op("dve", lambda v: v.tensor_tensor_scan(out=t2[:, :], data0=rmask[:, :], data1=t1[:, :], initial=0.0,
                                                                op0=ALU.mult, op1=ALU.add), reads=[t_r[0], p_r], writes=[t_r[1]])
                    kb.op("act", lambda a: a.activation(out=Gam[:, hp, :], in_=t2[:, :], func=AF.Exp, scale=-C0), reads=[t_r[1]], writes=[Gam_r[hp]])
                    kb.op("act", lambda a: a.activation(out=t3[:, :], in_=t2[:, :], func=AF.Exp, scale=C0), reads=[t_r[1]], writes=[t_r[2]])
                    kb.op("dve", lambda v: v.tensor_tensor(out=t4[:, :], in0=t2[:, :], in1=t1[:, :], op=ALU.subtract),
                          reads=[t_r[0], t_r[1]], writes=[t_r[3]])
                    kb.op("act", lambda a: a.activation(out=t4[:, :], in_=t4[:, :], func=AF.Exp, scale=-C0), reads=[t_r[3]], writes=[t_r[3]])
                    kb.op("pe", lambda pe: pe.matmul(PS[3][:, 0:MT], lhsT=a2[:, hp * 128:(hp + 1) * 128], rhs=AL[:, :], start=True, stop=True),
                          reads=[p_r, AL_r], writes=[PS_r[3]])
                    kb.op("act", lambda a: a.activation(out=t5[:, :], in_=PS[3][:, 0:MT], func=AF.Sigmoid, bias=ptb[:, l, 15 + hp:16 + hp]),
                          reads=[PS_r[3], c_r], writes=[t_r[4]])
                    kb.op("dve", lambda v: v.tensor_scalar(out=t6[:, :], in0=kT[:, :], scalar1=ptb[:, l, 19 + hp:20 + hp], scalar2=None, op0=ALU.mult),
                          reads=[kT_r, c_r], writes=[t_r[5]])
                    kb.op("act", lambda a: a.activation(out=t7[:, :], in_=t6[:, :], func=AF.Square), reads=[t_r[5]], writes=[t_r[6]])
                    kb.op("pe", lambda pe: pe.matmul(PS[4][:, 0:MT], lhsT=blkf[:, :], rhs=t7[:, :], start=True, stop=True),
                          reads=[p_r, t_r[6]], writes=[PS_r[4]])
                    kb.op("act", lambda a: a.activation(out=t7[:, :], in_=PS[4][:, 0:MT], func=AF.Sqrt), reads=[PS_r[4], t_r[6]], writes=[t_r[6]])
                    kb.op("dve", lambda v: v.tensor_scalar(out=t7[:, :], in0=t7[:, :], scalar1=1e-12, scalar2=None, op0=ALU.max),
                          reads=[t_r[6]], writes=[t_r[6]])
                    kb.op("dve", lambda v: v.reciprocal(out=t7[:, :], in_=t7[:, :]), reads=[t_r[6]], writes=[t_r[6]])
                    kb.op("dve", lambda v: v.tensor_tensor(out=t6[:, :], in0=t6[:, :], in1=t7[:, :], op=ALU.mult),
                          reads=[t_r[5], t_r[6]], writes=[t_r[5]])
                    kb.op("dve", lambda v: v.tensor_scalar(out=t7[:, :], in0=t5[:, :], scalar1=-1.0, scalar2=ptb[:, l, 21 + hp:22 + hp],
                                                           op0=ALU.add, op1=ALU.mult), reads=[t_r[4], t_r[6], c_r], writes=[t_r[6]])
                    kb.op("dve", lambda v: v.scalar_tensor_tensor(out=t7[:, :], in0=t7[:, :], scalar=1.0, in1=kT[:, :], op0=ALU.add, op1=ALU.mult),
                          reads=[t_r[6], kT_r], writes=[t_r[6]])
                    for hh in range(2):
                        pr = slice(hh * 64, hh * 64 + 64)
                        kb.op("dve", lambda v: v.scalar_tensor_tensor(out=ATz[pr, hp, hh, :], in0=t6[pr, :], scalar=-1.0, in1=t4[pr, :],
                                                                      op0=ALU.mult, op1=ALU.mult), reads=[t_r[5], t_r[3]], writes=[ATz_r[hp]])
                        kb.op("dve", lambda v: v.tensor_tensor(out=RTz[pr, hp, hh, :], in0=rT[pr, :], in1=Gam[pr, hp, :], op=ALU.mult),
                              reads=[rT_r, Gam_r[hp]], writes=[RTz_r[hp]])
                    kb.op("dve", lambda v: v.tensor_tensor(out=t1[:, :], in0=t6[:, :], in1=t5[:, :], op=ALU.mult),
                          reads=[t_r[5], t_r[4], t_r[0]], writes=[t_r[0]])
                    kb.op("dve", lambda v: v.tensor_tensor(out=BTt[:, hp, :], in0=t1[:, :], in1=t3[:, :], op=ALU.mult),
                          reads=[t_r[0], t_r[2]], writes=[BT_r[hp]])
                    kb.op("dve", lambda v: v.tensor_tensor(out=KTt[:, hp, :], in0=t7[:, :], in1=t3[:, :], op=ALU.mult),
                          reads=[t_r[6], t_r[2]], writes=[KT_r[hp]])
                    kb.op("dve", lambda v: v.scalar_tensor_tensor(out=rkr[:, hp, :], in0=rT[:, :], scalar=ptb[:, l, 23 + hp:24 + hp], in1=t7[:, :],
                                                                  op0=ALU.mult, op1=ALU.mult), reads=[rT_r, t_r[6], c_r], writes=[rkr_r[hp]])
                    if l == 0:
                        kb.dma("sp", vfirst_d.ap()[hp, :, sq * 0 + tok0:tok0 + MT], vT[:, :], reads=[vT_r], writes=[vfirst_r[hp]])
                    else:
                        kb.op("pe", lambda pe: pe.matmul(PS[5][:, 0:MT], lhsT=v2[:, hp * 128:(hp + 1) * 128], rhs=VR[:, :], start=True, stop=True),
                              reads=[p_r, VR_r], writes=[PS_r[5]])
                        kb.op("act", lambda a: a.activation(out=t1[:, :], in_=PS[5][:, 0:MT], func=AF.Sigmoid, bias=ptb[:, l, 17 + hp:18 + hp]),
                              reads=[PS_r[5], c_r, t_r[0]], writes=[t_r[0]])
                        kb.dma("sp", t2[:, :], vfirst_d.ap()[hp, :, tok0:tok0 + MT], reads=[vfirst_r[hp], t_r[1]], writes=[t_r[1]])
                        kb.op("dve", lambda v: v.tensor_tensor(out=t2[:, :], in0=t2[:, :], in1=vT[:, :], op=ALU.subtract),
                              reads=[t_r[1], vT_r], writes=[t_r[1]])
                        kb.op("dve", lambda v: v.tensor_tensor(out=t2[:, :], in0=t2[:, :], in1=t1[:, :], op=ALU.mult),
                              reads=[t_r[1], t_r[0]], writes=[t_r[1]])
                        kb.op("dve", lambda v: v.tensor_tensor(out=vT[:, :], in0=vT[:, :], in1=t2[:, :], op=ALU.add),
                              reads=[t_r[1], vT_r], writes=[vT_r])
                    kb.op("act", lambda a: a.activation(out=vb[:, :], in_=vT[:, :], func=AF.Copy), reads=[vT_r], writes=[vb_r])
                    for src, src_r, dst, dst_r, pbi in ((vb, vb_r, Vtok, Vtok_r, 0), (None, BT_r[hp], Btok, Btok_r, 1), (None, KT_r[hp], Ktok, Ktok_r, 0)):
                        sap = (lambda c: vb[:, c * 64:(c + 1) * 64]) if src is vb else \
                              ((lambda c: BTt[:, hp, c * 64:(c + 1) * 64]) if dst is Btok else (lambda c: KTt[:, hp, c * 64:(c + 1) * 64]))

                        def fn(pe, sap=sap, pbi=pbi):
                            inst = None
                            for c in range(NCH):
                                inst = pe.transpose(out=PB[pbi][0:64, c * 128:(c + 1) * 128], in_=sap(c), identity=ident[:, :])
                            return inst
                        kb.op("pe", fn, reads=[src_r, c_r], writes=[PB_r[pbi]])
                        kb.op("dve", lambda v: v.tensor_copy(out=dst[0:64, :, hp * 128:(hp + 1) * 128],
                                                             in_=PB[pbi][0:64, 0:NCH * 128].rearrange("p (c d) -> p c d", c=NCH)),
                              reads=[PB_r[pbi]], writes=[dst_r])
                if rs_ <= 2:
                    return
                def fn(pe):
                    inst = None
                    for c in range(NCH):
                        for hp in range(2):
                            inst = pe.matmul(PS[2][0:64, c * 4 + hp * 2:c * 4 + hp * 2 + 2], lhsT=rkr[:, hp, c * 64:(c + 1) * 64], rhs=hselb[:, :],
                                             start=True, stop=True)
                    return inst
                kb.op("pe", fn, reads=rkr_r + [p_r], writes=[PS_r[2]])
                kb.op("act", lambda a: a.activation(out=cb[:, :, :], in_=PS[2][0:64, 0:NCH * 4].rearrange("p (c h) -> p c h", c=NCH), func=AF.Copy),
                      reads=[PS_r[2]], writes=[cb_r])
                for c2 in range(NCH // 2):
                    def fn(pe):
                        inst = None
                        for cc_ in range(2):
                            c = c2 * 2 + cc_
                            pe.matmul(PS[3][0:64, cc_ * 256:(cc_ + 1) * 256], lhsT=SGL[:, c * 64:(c + 1) * 64], rhs=g2a[:, :], start=True, stop=False)
                            inst = pe.matmul(PS[3][0:64, cc_ * 256:(cc_ + 1) * 256], lhsT=SGL2[:, c * 64:(c + 1) * 64], rhs=g2b[:, :], start=False, stop=True)
                        return inst
                    kb.op("pe", fn, reads=[SGL_r, SGL2_r, p_r], writes=[PS_r[3]])
                    kb.op("act", lambda a: a.activation(out=gtok[:, c2 * 2:c2 * 2 + 2, :], in_=PS[3][0:64, :].rearrange("p (c d) -> p c d", c=2), func=AF.Copy),
                          reads=[PS_r[3]], writes=[gtok_r])
                for c in range(NCH):
                    cc = slice(c * 64, (c + 1) * 64)
                    for h in range(4):
                        hp, hh = h // 2, h % 2
                        u = c * 4 + h
                        pi = 4 + (u % 2)

                        def fn(pe, pi=pi):
                            pe.matmul(PS[pi][0:64, 0:64], lhsT=ATz[:, hp, hh, cc], rhs=BTt[:, hp, cc], start=True, stop=True)
                            pe.matmul(PS[pi][0:64, 64:128], lhsT=BTt[:, hp, cc], rhs=ATz[:, hp, hh, cc], start=True, stop=True)
                            pe.matmul(PS[pi][0:64, 128:192], lhsT=KTt[:, hp, cc], rhs=ATz[:, hp, hh, cc], start=True, stop=True)
                            pe.matmul(PS[pi][0:64, 192:256], lhsT=BTt[:, hp, cc], rhs=RTz[:, hp, hh, cc], start=True, stop=True)
                            return pe.matmul(PS[pi][0:64, 256:320], lhsT=KTt[:, hp, cc], rhs=RTz[:, hp, hh, cc], start=True, stop=True)
                        kb.op("pe", fn, reads=[ATz_r[hp], RTz_r[hp], BT_r[hp], KT_r[hp]], writes=[PS_r[pi]])
                        kb.op("dve", lambda v, pi=pi: v.tensor_tensor(out=MN[0][:, u, :, :], in0=PS[pi][0:64, 0:128].rearrange("p (a b) -> p a b", a=2),
                                                                      in1=amask[:, 0:2, :], op=ALU.mult), reads=[PS_r[pi], p_r], writes=[MN_r[0]])
                        kb.op("dve", lambda v, pi=pi: v.tensor_tensor(out=A3[0:64, u, :, :], in0=PS[pi][0:64, 128:320].rearrange("p (a b) -> p a b", a=3),
                                                                      in1=amask[:, 2:5, :], op=ALU.mult), reads=[PS_r[pi], p_r], writes=[A3_r])
                if rs_ <= 3:
                    return
                kb.op("dve", lambda v: v.tensor_tensor(out=Pm[:, :, :], in0=MN[0][:, :, 1, :],
                                                      in1=bass.AP(identf, 0, [[128, 64], [0, NU], [1, 64]]), op=ALU.add),
                      reads=[MN_r[0], c_r], writes=[P_r])
                cur = 0
                for lev in range(5):
                    nxt = 1 - cur
                    lastlev = (lev == 4)
                    for g4 in range(NU // 4):
                        pi = 4 + (g4 % 2)

                        def fn(pe, pi=pi):
                            inst = None
                            for q in range(4):
                                u = g4 * 4 + q
                                inst = pe.matmul(PS[pi][0:64, q * 128:q * 128 + 64], lhsT=MN[cur][:, u, 1, :], rhs=MN[cur][:, u, 0, :], start=True, stop=True)
                                if not lastlev:
                                    inst = pe.matmul(PS[pi][0:64, q * 128 + 64:q * 128 + 128], lhsT=MN[cur][:, u, 0, :], rhs=MN[cur][:, u, 1, :],
                                                     start=True, stop=True)
                            return inst
                        kb.op("pe", fn, reads=[MN_r[cur]], writes=[PS_r[pi]])
                        kb.op("act", lambda a, pi=pi: a.activation(out=MN[nxt][:, g4 * 4:g4 * 4 + 4, :, :],
                                                                   in_=PS[pi][0:64, :].rearrange("p (u a b) -> p u a b", u=4, a=2), func=AF.Copy),
                              reads=[PS_r[pi]], writes=[MN_r[nxt]])
                    for g8 in range(NU // 8):
                        pi = 2 + (g8 % 2)

                        def fn(pe, pi=pi):
                            inst = None
                            for q in range(8):
                                u = g8 * 8 + q
                                inst = pe.matmul(PS[pi][0:64, q * 64:(q + 1) * 64], lhsT=MN[nxt][:, u, 0, :], rhs=Pm[:, u, :], start=True, stop=True)
                            return inst
                        kb.op("pe", fn, reads=[MN_r[nxt], P_r], writes=[PS_r[pi]])
                        kb.op("dve", lambda v, pi=pi: v.tensor_tensor(out=Pm[:, g8 * 8:g8 * 8 + 8, :], in0=PS[pi][0:64, :].rearrange("p (u b) -> p u b", u=8),
                                                                      in1=Pm[:, g8 * 8:g8 * 8 + 8, :], op=ALU.add), reads=[PS_r[pi], P_r], writes=[P_r])
                    cur = nxt
                if rs_ <= 4:
                    return
            def chain(mt):
                tok0 = mt * MT
                par = mt % 2
                ATz, RTz, Vtok, Btok, Ktok, Gam, gtok, cb, Pm, A3 = ATz2[par], RTz2[par], Vtok2[par], Btok2[par], Ktok2[par], Gam2[par], gtok2[par], cb2[par], Pm2[par], A32[par]
                ATz_r, RTz_r, Vtok_r, Btok_r, Ktok_r, Gam_r, gtok_r, cb_r, P_r, A3_r = ATz_r2[par], RTz_r2[par], Vtok_r2[par], Btok_r2[par], Ktok_r2[par], Gam_r2[par], gtok_r2[par], cb_r2[par], P_r2[par], A3_r2[par]
                if rs_ <= 4:
                    return
                for c in range(NCH):
                    cc = slice(c * 64, (c + 1) * 64)

                    def fn(pe):
                        inst = None
                        for h in range(4):
                            hp, hh = h // 2, h % 2
                            u = c * 4 + h
                            pe.matmul(PS[0][0:64, h * 64:(h + 1) * 64], lhsT=A3[:, u, 0, :], rhs=Vtok[:, c, h * 64:(h + 1) * 64], start=True, stop=False)
                            inst = pe.matmul(PS[0][0:64, h * 64:(h + 1) * 64], lhsT=ATz[:, hp, hh, cc], rhs=Tb[:, hp, hh * 64:(hh + 1) * 64],
                                             start=False, stop=True)
                        return inst
                    kb.op("pe", fn, reads=[A3_r, Vtok_r, Tb_r] + ATz_r, writes=[PS_r[0]])
                    kb.op("act", lambda a: a.activation(out=Rs[:, :], in_=PS[0][0:64, 0:256], func=AF.Copy), reads=[PS_r[0]], writes=[Rs_r])

                    def fn(pe):
                        inst = None
                        for h in range(4):
                            u = c * 4 + h
                            inst = pe.matmul(PS[1][0:64, h * 64:(h + 1) * 64], lhsT=Pm[:, u, :], rhs=Rs[:, h * 64:(h + 1) * 64], start=True, stop=True)
                        return inst
                    kb.op("pe", fn, reads=[P_r, Rs_r], writes=[PS_r[1]])
                    kb.op("act", lambda a: a.activation(out=Ub[0:64, :], in_=PS[1][0:64, 0:256], func=AF.Copy), reads=[PS_r[1]], writes=[Ub_r])

                    def fn(pe):
                        inst = None
                        for h in range(4):
                            hp, hh = h // 2, h % 2
                            u = c * 4 + h
                            pe.matmul(PS[0][0:64, h * 64:(h + 1) * 64], lhsT=RTz[:, hp, hh, cc], rhs=Tb[:, hp, hh * 64:(hh + 1) * 64], start=True, stop=False)
                            pe.matmul(PS[0][0:64, h * 64:(h + 1) * 64], lhsT=A3[:, u, 1, :], rhs=Ub[:, h * 64:(h + 1) * 64], start=False, stop=False)
                            inst = pe.matmul(PS[0][0:64, h * 64:(h + 1) * 64], lhsT=A3[:, u, 2, :], rhs=Vtok[:, c, h * 64:(h + 1) * 64], start=False, stop=True)
                        return inst
                    kb.op("pe", fn, reads=[A3_r, Vtok_r, Tb_r, Ub_r] + RTz_r, writes=[PS_r[0]])

                    def fn(pe):
                        inst = None
                        for hp in range(2):
                            pe.matmul(PS[1][:, hp * 128:(hp + 1) * 128], lhsT=Btok[:, c, hp * 128:(hp + 1) * 128], rhs=Ub[:, hp * 128:(hp + 1) * 128],
                                      start=True, stop=False)
                            inst = pe.matmul(PS[1][:, hp * 128:(hp + 1) * 128], lhsT=Ktok[:, c, hp * 128:(hp + 1) * 128], rhs=Vtok[:, c, hp * 128:(hp + 1) * 128],
                                             start=False, stop=True)
                        return inst
                    kb.op("pe", fn, reads=[Btok_r, Ktok_r, Vtok_r, Ub_r], writes=[PS_r[1]])
                    for hp in range(2):
                        ee = Gam[:, hp, c * 64 + 63:c * 64 + 64]
                        kb.op("dve", lambda v: v.scalar_tensor_tensor(out=tmpS[:, :], in0=PS[1][:, hp * 128:(hp + 1) * 128], scalar=ee, in1=blkm[:, :],
                                                                      op0=ALU.mult, op1=ALU.mult), reads=[PS_r[1], Gam_r[hp], p_r], writes=[tmpS_r])
                        kb.op("dve", lambda v: v.scalar_tensor_tensor(out=Tblk[:, hp, :], in0=Tblk[:, hp, :], scalar=ee, in1=tmpS[:, :],
                                                                      op0=ALU.mult, op1=ALU.add), reads=[T_r, tmpS_r, Gam_r[hp]], writes=[T_r])
                    kb.op("act", lambda a: a.activation(out=Tb[:, :, :], in_=Tblk[:, :, :], func=AF.Copy), reads=[T_r], writes=[Tb_r])
                    if rs_ <= 5:
                        continue
                    head_norm_gate(PS[0][0:64, 0:256], PS_r[0], ow, ow_r, ow2, ow2_r, st, st_r, rln, p_r, RW_EPS, P=64)
                    for h in range(4):
                        kb.op("dve", lambda v: v.scalar_tensor_tensor(out=ow[0:64, h * 64:(h + 1) * 64], in0=Vtok[0:64, c, h * 64:(h + 1) * 64],
                                                                      scalar=cb[:, c, h:h + 1], in1=ow[0:64, h * 64:(h + 1) * 64],
                                                                      op0=ALU.mult, op1=ALU.add), reads=[Vtok_r, cb_r, ow_r], writes=[ow_r])
                    kb.op("dve", lambda v: v.tensor_tensor(out=ob[0:64, :], in0=ow[0:64, :], in1=gtok[:, c, :], op=ALU.mult),
                          reads=[ow_r, gtok_r], writes=[ob_r])
                    out_to_OT(ob, ob_r, 64, OT, OT_r, 2, tok0 + c * 64)
            nmt = S // MT if rs_ > 0 else 0
            if nmt:
                prep(0)
            for mt in range(nmt):
                if mt + 1 < nmt:
                    prep(mt + 1)
                chain(mt)
            kb.barrier()

    NQ, NQS, NKC, NKS, NKW, NVC, NVS, NGT = 0, 512, 1024, 1280, 1536, 1792, 1920, 2176
    GK = 1.5957691216057308

    def nsa_phase(sq, l, OT, OT_r):
        ns_ = cfg.get("nsa_stop", 99)
        with ExitStack() as es:
            A = lambda n_, s_, d_: es.enter_context(sb(n_, s_, d_))
            QT = A("nQT", [128, 4, S], BF16)
            KTz = A("nKTz", [128, 3, 2, S], BF16)
            VCz = A("nVCz", [128, 2, S], BF16)
            VS = A("nVS", [128, NT, 2, 65], BF16)
            VW = A("nVW", [128, NT, 2, 65], BF16)
            gsig = A("ngsig", [128, NT, 24], F32)
            kcmpTz = A("nkcmp", [128, 2, 128], BF16)
            vcmp = A("nvcmp", [128, 2, 97], BF16)
            QT_r, KT_r, VC_r, VS_r, VW_r, gs_r = regs(4), regs(4), regs(4), regs(NT), regs(NT), regs(NT)
            kc_r, vcm_r = Reg(), Reg()
            kb.op("pool", lambda g: g.memset(KTz[:], 0.0), writes=KT_r)
            kb.op("pool", lambda g: g.memset(VCz[:], 0.0), writes=VC_r)
            kb.op("pool", lambda g: g.memset(VS[:, :, :, 64:65], 1.0), writes=VS_r)
            kb.op("pool", lambda g: g.memset(VW[:, :, :, 64:65], 1.0), writes=VW_r)
            kb.op("pool", lambda g: g.memset(kcmpTz[:], 0.0), writes=[kc_r])
            kb.op("pool", lambda g: g.memset(vcmp[:], 0.0), writes=[vcm_r])
            with ExitStack() as es1:
                A1 = lambda n_, s_, d_: es1.enter_context(sb(n_, s_, d_))
                Wn = A1("nWn", [128, 8, 2200], BF16)
                rp = A1("nrp", [128, 2, 512], F32)
                t1 = A1("nt1", [128, 512], F32)
                t2 = A1("nt2", [128, 512], F32)
                W_r, rp_r, t1_r, t2_r = Reg(), Reg(), Reg(), Reg()
                load_w(Wn, w_nsa16.ap()[l], 2200, W_r)
                for mt in range(4):
                    tok0 = mt * 512
                    bl = slice(tok0, tok0 + 512)
                    xr = XT_r[mt * 4:mt * 4 + 4]
                    kb.dma("sp", rp[:, 0, :], c_rope.ap()[0][:, bl], writes=[rp_r])
                    kb.dma("sp", rp[:, 1, :], c_rope.ap()[1][:, bl], writes=[rp_r])
                    for i in range(4):
                        proj_fm(PS[0][:, :], Wn, NQ + i * 128, 128, tok0, 512, [W_r], xr, PS_r[0])
                        proj_fm(PS[1][:, :], Wn, NQS + i * 128, 128, tok0, 512, [W_r], xr, PS_r[1])
                        kb.op("dve", lambda v: v.tensor_tensor(out=t1[:, :], in0=PS[0][:, :], in1=rp[:, 0, :], op=ALU.mult),
                              reads=[PS_r[0], rp_r], writes=[t1_r])
                        kb.op("dve", lambda v: v.scalar_tensor_tensor(out=t2[:, :], in0=PS[1][:, :], scalar=0.125, in1=rp[:, 1, :],
                                                                      op0=ALU.mult, op1=ALU.mult), reads=[PS_r[1], rp_r], writes=[t2_r])
                        kb.op("dve", lambda v: v.scalar_tensor_tensor(out=QT[:, i, bl], in0=t1[:, :], scalar=0.125, in1=t2[:, :],
                                                                      op0=ALU.mult, op1=ALU.add), reads=[t1_r, t2_r], writes=[QT_r[mt]])
                    for ty, c0 in ((0, NKC), (1, NKS), (2, NKW)):
                        proj_fm(PS[0][:, :], Wn, c0, 128, tok0, 512, [W_r], xr, PS_r[0])
                        proj_fm(PS[1][:, :], Wn, c0 + 128, 128, tok0, 512, [W_r], xr, PS_r[1])
                        kb.op("dve", lambda v: v.tensor_tensor(out=t1[:, :], in0=PS[0][:, :], in1=rp[:, 0, :], op=ALU.mult),
                              reads=[PS_r[0], rp_r], writes=[t1_r])
                        kb.op("dve", lambda v: v.tensor_tensor(out=t2[:, :], in0=PS[1][:, :], in1=rp[:, 1, :], op=ALU.mult),
                              reads=[PS_r[1], rp_r], writes=[t2_r])
                        for g in range(2):
                            pr = slice(g * 64, g * 64 + 64)
                            kb.op("dve", lambda v: v.tensor_tensor(out=KTz[pr, ty, g, bl], in0=t1[pr, :], in1=t2[pr, :], op=ALU.add),
                                  reads=[t1_r, t2_r], writes=[KT_r[mt]])
                    proj_fm(PS[2][:, :], Wn, NVC, 128, tok0, 512, [W_r], xr, PS_r[2])
                    for g in range(2):
                        pr = slice(g * 64, g * 64 + 64)
                        kb.op("act", lambda a: a.activation(out=VCz[pr, g, bl], in_=PS[2][pr, :], func=AF.Copy),
                              reads=[PS_r[2]], writes=[VC_r[mt]])
                    for j in range(4):
                        tt = mt * 4 + j
                        proj_tm(PS[3][:, 0:256], Wn, NVS, 256, tt, [W_r], PS_r[3])
                        kb.op("act", lambda a: a.activation(out=VS[:, tt, :, 0:64], in_=PS[3][:, 0:128].rearrange("p (g d) -> p g d", g=2), func=AF.Copy),
                              reads=[PS_r[3]], writes=[VS_r[tt], PS_r[3]])
                        kb.op("act", lambda a: a.activation(out=VW[:, tt, :, 0:64], in_=PS[3][:, 128:256].rearrange("p (g d) -> p g d", g=2), func=AF.Copy),
                              reads=[PS_r[3]], writes=[VW_r[tt], PS_r[3]])
                        proj_tm(PS[4][:, 0:24], Wn, NGT, 24, tt, [W_r], PS_r[4])
                        kb.op("act", lambda a: a.activation(out=gsig[:, tt, :], in_=PS[4][:, 0:24], func=AF.Sigmoid),
                              reads=[PS_r[4]], writes=[gs_r[tt]])
                kb.barrier()
            if ns_ <= 1:
                kb.barrier()
                return
            with ExitStack() as es2:
                A2 = lambda n_, s_, d_: es2.enter_context(sb(n_, s_, d_))
                w1d = A2("nw1d", [128, 32, 256], BF16)
                w2d = A2("nw2d", [128, 2, 128], BF16)
                wv2 = A2("nwv2", [128, 2, 64], BF16)
                posz = A2("nposz", [128, 2, 32], BF16)
                hc = A2("nhc", [128, 2], F32)
                gx = A2("ngx", [128, 128], F32)
                gw = A2("ngw", [128, 128], F32)
                gh = A2("ngh", [128, 2, 128], BF16)
                w1_r, w2_r, hc_r, gx_r, gw_r, gh_r = Reg(), Reg(), Reg(), Reg(), Reg(), Reg()
                kb.op("pool", lambda g: g.memset(posz[:], 0.0), writes=[w2_r])
                kb.op("pool", lambda g: g.memset(gh[:], 0.0), writes=[gh_r])
                for kv in range(2):
                    kb.dma("pool", posz[0:64, kv, :], nsa_posT.ap()[l, kv], reads=[w2_r], writes=[w2_r])
                for half in range(2):
                    kb.dma("pool", w2d[:, :, half * 64:(half + 1) * 64], nsa_wk2.ap()[l].rearrange("(t p) n -> p t n", p=128), writes=[w2_r])
                kb.dma("pool", wv2[:, :, :], nsa_wv2.ap()[l].rearrange("(t p) n -> p t n", p=128), writes=[w2_r])
                kb.dma("pool", vcmp[:, 0, 65:97], c_ovl.ap(), reads=[vcm_r], writes=[vcm_r])
                kb.dma("pool", vcmp[:, 1, 65:97], c_ovl.ap(), reads=[vcm_r], writes=[vcm_r])
                kb.op("pool", lambda g: g.memset(vcmp[0:127, :, 64:65], 1.0), reads=[vcm_r], writes=[vcm_r])
                for kv, w1src in ((0, wk1_16), (1, wv1_16)):
                    src3 = w1src.ap()[l].rearrange("(l d) n -> d l n", d=64)
                    for half in range(2):
                        for l4 in range(8):
                            kb.dma("sp", w1d[half * 64:(half + 1) * 64, l4 * 4:(l4 + 1) * 4, :], src3[:, l4 * 4:(l4 + 1) * 4, :],
                                   reads=[w16_r], writes=[w1_r])
                    srcz = (lambda g: KTz[:, 0, g, :]) if kv == 0 else (lambda g: VCz[:, g, :])
                    src_regs = KT_r if kv == 0 else VC_r
                    for hf in range(2):
                        def fn(pe):
                            inst = None
                            for ll in range(32):
                                inst = pe.matmul(PS[0][:, 0:1], lhsT=w1d[:, ll, hf * 128:(hf + 1) * 128], rhs=posz[:, kv, ll:ll + 1],
                                                 start=(ll == 0), stop=(ll == 31))
                            return inst
                        kb.op("pe", fn, reads=[w1_r, w2_r], writes=[PS_r[0]])
                        kb.op("act", lambda a: a.activation(out=hc[:, hf:hf + 1], in_=PS[0][:, 0:1], func=AF.Copy), reads=[PS_r[0]], writes=[hc_r])
                    for g in range(2):
                        for hf in range(2):
                            def fn(pe):
                                inst = None
                                for ll in range(32):
                                    rhs = bass.AP(srcz(g).tensor, srcz(g).offset + ll, [srcz(g).ap[0], [16, 127]])
                                    inst = pe.matmul(PS[1][:, 0:127], lhsT=w1d[:, ll, hf * 128:(hf + 1) * 128], rhs=rhs,
                                                     start=(ll == 0), stop=(ll == 31))
                                return inst
                            kb.op("pe", fn, reads=[w1_r] + src_regs, writes=[PS_r[1]])
                            kb.op("act", lambda a: a.activation(out=gx[:, 0:127], in_=PS[1][:, 0:127], func=AF.Identity, bias=hc[:, hf:hf + 1], scale=1.0),
                                  reads=[PS_r[1], hc_r], writes=[gx_r])
                            kb.op("dve", lambda v: v.tensor_tensor(out=gw[:, 0:127], in0=gx[:, 0:127], in1=gx[:, 0:127], op=ALU.mult),
                                  reads=[gx_r], writes=[gw_r])
                            kb.op("dve", lambda v: v.tensor_scalar(out=gw[:, 0:127], in0=gw[:, 0:127], scalar1=0.044715, scalar2=1.0,
                                                                   op0=ALU.mult, op1=ALU.add), reads=[gw_r], writes=[gw_r])
                            kb.op("dve", lambda v: v.tensor_tensor(out=gw[:, 0:127], in0=gw[:, 0:127], in1=gx[:, 0:127], op=ALU.mult),
                                  reads=[gw_r, gx_r], writes=[gw_r])
                            kb.op("act", lambda a: a.activation(out=gw[:, 0:127], in_=gw[:, 0:127], func=AF.Sigmoid, scale=GK), reads=[gw_r], writes=[gw_r])
                            kb.op("dve", lambda v: v.tensor_tensor(out=gh[:, hf, 0:127], in0=gx[:, 0:127], in1=gw[:, 0:127], op=ALU.mult),
                                  reads=[gw_r, gx_r], writes=[gh_r])
                        if kv == 0:
                            def fn(pe):
                                pe.matmul(PS[2][:, 0:127], lhsT=w2d[:, 0, :], rhs=gh[:, 0, 0:127], start=True, stop=False)
                                return pe.matmul(PS[2][:, 0:127], lhsT=w2d[:, 1, :], rhs=gh[:, 1, 0:127], start=False, stop=True)
                            kb.op("pe", fn, reads=[w2_r, gh_r], writes=[PS_r[2]])
                            pr = slice(g * 64, g * 64 + 64)
                            kb.op("act", lambda a: a.activation(out=kcmpTz[pr, g, 0:127], in_=PS[2][pr, 0:127], func=AF.Copy),
                                  reads=[PS_r[2], kc_r], writes=[kc_r])
                        else:
                            def fn(pe):
                                pe.matmul(PS[2][:, 0:64], lhsT=gh[:, 0, :], rhs=wv2[:, 0, :], start=True, stop=False)
                                return pe.matmul(PS[2][:, 0:64], lhsT=gh[:, 1, :], rhs=wv2[:, 1, :], start=False, stop=True)
                            kb.op("pe", fn, reads=[w2_r, gh_r], writes=[PS_r[2]])
                            kb.op("act", lambda a: a.activation(out=vcmp[0:127, g, 0:64], in_=PS[2][0:127, 0:64], func=AF.Copy),
                                  reads=[PS_r[2], vcm_r], writes=[vcm_r])
                kb.barrier()
            if ns_ <= 2:
                kb.barrier()
                return
            selbT = A("nselbT", [128, 2, S], BF16)
            onehot = A("nonehot", [128, S], BF16)
            cmask = A("ncmask", [128, S], BF16)
            selm = A("nselm", [128, 2, NT, 32], F32)
            wmask = A("nwmask", [128, 128], F32)
            PT = [A("nPT%d" % i, [128, 640], BF16) for i in range(2)]
            ex = A("nex", [128, 512], F32)
            ocs = A("nocs", [128, 4, 97], F32)
            ONSA2 = [A("nONSA%d" % i_, [128, 4, 512], F32) for i_ in range(2)]
            impb2 = [A("nimp%d" % i_, [128, 4, 2, 32], F32) for i_ in range(2)]
            sc = A("nsc", [128, 32], F32)
            cm3 = A("ncm3", [128, 32, 32], F32)
            sb16 = A("nsb16", [128, 32], BF16)
            dn = A("ndn", [128, 16], F32)
            obf = A("nobf", [128, 512], BF16)
            k_r, sel_r, PT_r, ex_r, ocs_r, sc_r, cm3_r, sb16_r, dn_r, obf_r = \
                Reg(), regs(4), regs(2), Reg(), Reg(), Reg(), Reg(), Reg(), Reg(), Reg()
            on_r2, imp_r2 = [regs(4), regs(4)], regs(2)
            kb.op("pool", lambda g: g.memset(selbT[:], 0.0), writes=sel_r)
            kb.op("pool", lambda g: g.memset(onehot[:], 0.0), writes=[k_r])
            kb.dma("pool", onehot[0:32, :], c_onehot.ap(), reads=[k_r], writes=[k_r])
            kb.dma("pool", cmask[:, :], c_cmpmask.ap(), writes=[k_r])
            for m_ in range(2):
                kb.dma("sp", selm[:, m_, :, :], c_selm.ap()[m_], writes=[k_r])
            kb.op("dve", lambda v: v.tensor_scalar(out=wmask[:, :], in0=caus[:, 0, :], scalar1=-1.0, scalar2=1.0, op0=ALU.mult, op1=ALU.add),
                  reads=[c_r], writes=[k_r])

            def branch_finish(acc_ps, acc_r, W, nq, h, b, tts, first, ONSA, on_r):
                kb.op("act", lambda a: a.activation(out=ocs[:, 0:nq, 0:W], in_=acc_ps.rearrange("p (q w) -> p q w", q=nq), func=AF.Copy),
                      reads=[acc_r], writes=[ocs_r, acc_r])
                kb.op("dve", lambda v: v.tensor_scalar(out=dn[:, 0:nq], in0=ocs[:, 0:nq, 64], scalar1=1e-30, scalar2=None, op0=ALU.max),
                      reads=[ocs_r], writes=[dn_r])
                kb.op("dve", lambda v: v.reciprocal(out=dn[:, 0:nq], in_=dn[:, 0:nq]), reads=[dn_r], writes=[dn_r])
                kb.op("dve", lambda v: v.tensor_tensor(out=dn[:, 8:8 + nq], in0=dn[:, 0:nq], in1=gsig[:, tts[0]:tts[0] + nq, h * 3 + b], op=ALU.mult),
                      reads=[dn_r] + gs_r[tts[0]:tts[0] + nq], writes=[dn_r])
                for qi in range(nq):
                    tl = tts[qi] % 4
                    if first:
                        kb.op("dve", lambda v: v.tensor_scalar(out=ONSA[:, tl, h * 64:(h + 1) * 64], in0=ocs[:, qi, 0:64], scalar1=dn[:, 8 + qi:9 + qi],
                                                               scalar2=None, op0=ALU.mult), reads=[ocs_r, dn_r], writes=[on_r[tl]])
                    else:
                        kb.op("dve", lambda v: v.scalar_tensor_tensor(out=ONSA[:, tl, h * 64:(h + 1) * 64], in0=ocs[:, qi, 0:64], scalar=dn[:, 8 + qi:9 + qi],
                                                                      in1=ONSA[:, tl, h * 64:(h + 1) * 64], op0=ALU.mult, op1=ALU.add),
                              reads=[ocs_r, dn_r, on_r[tl]], writes=[on_r[tl]])

            def nsa_a(qb):
                qc = slice(qb * 512, (qb + 1) * 512)
                tts = list(range(qb * 4, qb * 4 + 4))
                ONSA, on_r, impb, imp_r = ONSA2[qb % 2], on_r2[qb % 2], impb2[qb % 2], imp_r2[qb % 2]
                for h in range(8):
                    g, i = h // 4, h % 4
                    pi = h % 2
                    kb.op("pe", lambda pe: pe.matmul(PS[pi][:, :], lhsT=kcmpTz[:, g, :], rhs=QT[:, i, qc], start=True, stop=True),
                          reads=[kc_r, QT_r[qb]], writes=[PS_r[pi]])
                    kb.op("act", lambda a: a.activation(out=ex[:, :], in_=PS[pi][:, :], func=AF.Exp), reads=[PS_r[pi]], writes=[ex_r])
                    kb.op("dve", lambda v: v.tensor_tensor(out=PT[pi][:, 0:512], in0=ex[:, :], in1=cmask[:, qc], op=ALU.mult),
                          reads=[ex_r, k_r], writes=[PT_r[pi]])
                    ai = 2 + (h % 2)

                    def fn(pe):
                        inst = None
                        for q in range(4):
                            inst = pe.matmul(PS[ai][:, q * 97:(q + 1) * 97], lhsT=PT[pi][:, q * 128:(q + 1) * 128], rhs=vcmp[:, g, :], start=True, stop=True)
                        return inst
                    kb.op("pe", fn, reads=[PT_r[pi], vcm_r], writes=[PS_r[ai]])
                    branch_finish(PS[ai][:, 0:388], PS_r[ai], 97, 4, h, 0, tts, True, ONSA, on_r)
                    for q in range(4):
                        if i == 0:
                            kb.op("dve", lambda v: v.tensor_scalar(out=impb[:, q, g, :], in0=ocs[:, q, 65:97], scalar1=dn[:, q:q + 1], scalar2=None, op0=ALU.mult),
                                  reads=[ocs_r, dn_r], writes=[imp_r])
                        else:
                            kb.op("dve", lambda v: v.scalar_tensor_tensor(out=impb[:, q, g, :], in0=ocs[:, q, 65:97], scalar=dn[:, q:q + 1], in1=impb[:, q, g, :],
                                                                          op0=ALU.mult, op1=ALU.add), reads=[ocs_r, dn_r, imp_r], writes=[imp_r])
            def nsa_b(qb):
                qc = slice(qb * 512, (qb + 1) * 512)
                tts = list(range(qb * 4, qb * 4 + 4))
                ONSA, on_r, impb, imp_r = ONSA2[qb % 2], on_r2[qb % 2], impb2[qb % 2], imp_r2[qb % 2]
                for q in range(4):
                    tt = tts[q]
                    for g in range(2):
                        kb.op("dve", lambda v: v.tensor_tensor(out=sc[:, :], in0=impb[:, q, g, :], in1=selm[:, 0, tt, :], op=ALU.mult),
                              reads=[imp_r, k_r], writes=[sc_r])
                        kb.op("dve", lambda v: v.tensor_tensor(out=sc[:, :], in0=sc[:, :], in1=selm[:, 1, tt, :], op=ALU.add),
                              reads=[sc_r, k_r], writes=[sc_r])
                        kb.op("dve", lambda v: v.tensor_tensor(out=cm3[:, :, :], in0=bass.AP(sc, 0, [[32, 128], [0, 32], [1, 32]]),
                                                              in1=bass.AP(sc, 0, [[32, 128], [1, 32], [0, 32]]), op=ALU.is_gt),
                              reads=[sc_r], writes=[cm3_r])
                        kb.op("dve", lambda v: v.reduce_sum(out=sc[:, :], in_=cm3[:, :, :], axis=AX.X), reads=[cm3_r, sc_r], writes=[sc_r])
                        kb.op("dve", lambda v: v.tensor_scalar(out=sb16[:, :], in0=sc[:, :], scalar1=15.5, scalar2=-30000.0, op0=ALU.is_gt, op1=ALU.mult),
                              reads=[sc_r], writes=[sb16_r])
                        kb.op("pe", lambda pe: pe.transpose(out=PB[0][0:32, 0:128], in_=sb16[:, :], identity=ident[:, :]),
                              reads=[sb16_r, c_r], writes=[PB_r[0]])
                        kb.op("act", lambda a: a.activation(out=selbT[0:32, g, tt * 128:(tt + 1) * 128], in_=PB[0][0:32, 0:128], func=AF.Copy),
                              reads=[PB_r[0]], writes=[sel_r[qb]])
            def nsa_c(qb):
                qc = slice(qb * 512, (qb + 1) * 512)
                tts = list(range(qb * 4, qb * 4 + 4))
                ONSA, on_r, impb, imp_r = ONSA2[qb % 2], on_r2[qb % 2], impb2[qb % 2], imp_r2[qb % 2]
                for h in range(8):
                    g, i = h // 4, h % 4
                    nkt = 4 * qb + 4
                    for kt in range(nkt):
                        kc_ = slice(kt * 128, (kt + 1) * 128)
                        pi = kt % 2

                        def fn(pe):
                            pe.matmul(PS[pi][:, :], lhsT=KTz[:, 1, g, kc_], rhs=QT[:, i, qc], start=True, stop=False)
                            return pe.matmul(PS[pi][:, :], lhsT=onehot[:, kc_], rhs=selbT[:, g, qc], start=False, stop=True)
                        kb.op("pe", fn, reads=[KT_r[kt // 4], QT_r[qb], k_r, sel_r[qb]], writes=[PS_r[pi]])
                        kb.op("act", lambda a: a.activation(out=PT[pi][:, 0:512], in_=PS[pi][:, :], func=AF.Exp), reads=[PS_r[pi]], writes=[PT_r[pi]])
                        if kt >= 4 * qb:
                            ql = kt - 4 * qb
                            kb.op("dve", lambda v: v.tensor_tensor(out=PT[pi][:, ql * 128:(ql + 1) * 128], in0=PT[pi][:, ql * 128:(ql + 1) * 128],
                                                                  in1=caus[:, 0, :], op=ALU.mult), reads=[PT_r[pi], c_r], writes=[PT_r[pi]])
                        for q in range(4):
                            qt = 4 * qb + q
                            if qt < kt:
                                continue
                            kb.op("pe", lambda pe: pe.matmul(PS[2 + q][:, 0:65], lhsT=PT[pi][:, q * 128:(q + 1) * 128], rhs=VS[:, kt, g, :],
                                                             start=(kt == 0), stop=(kt == qt)), reads=[PT_r[pi], VS_r[kt]], writes=[PS_r[2 + q]])
                    for q in range(4):
                        branch_finish(PS[2 + q][:, 0:65], PS_r[2 + q], 65, 1, h, 1, [tts[q]], False, ONSA, on_r)
            def nsa_d(qb):
                qc = slice(qb * 512, (qb + 1) * 512)
                tts = list(range(qb * 4, qb * 4 + 4))
                ONSA, on_r, impb, imp_r = ONSA2[qb % 2], on_r2[qb % 2], impb2[qb % 2], imp_r2[qb % 2]
                for h in range(8):
                    g, i = h // 4, h % 4
                    for q in range(4):
                        qt = 4 * qb + q
                        qcs = slice(qt * 128, (qt + 1) * 128)
                        kts = [kt for kt in range(qt - 4, qt + 1) if kt >= 0]
                        pi = q % 2
                        main = [kt for kt in kts if kt >= qt - 3]

                        def fn(pe):
                            inst = None
                            for n_, kt in enumerate(main):
                                inst = pe.matmul(PS[pi][:, n_ * 128:(n_ + 1) * 128], lhsT=KTz[:, 2, g, kt * 128:(kt + 1) * 128], rhs=QT[:, i, qcs],
                                                 start=True, stop=True)
                            return inst
                        kb.op("pe", fn, reads=KT_r + [QT_r[qb]], writes=[PS_r[pi]])
                        nm = len(main)
                        kb.op("act", lambda a: a.activation(out=PT[pi][:, 0:nm * 128], in_=PS[pi][:, 0:nm * 128], func=AF.Exp), reads=[PS_r[pi]], writes=[PT_r[pi]])
                        kb.op("dve", lambda v: v.tensor_tensor(out=PT[pi][:, (nm - 1) * 128:nm * 128], in0=PT[pi][:, (nm - 1) * 128:nm * 128],
                                                              in1=caus[:, 0, :], op=ALU.mult), reads=[PT_r[pi], c_r], writes=[PT_r[pi]])
                        tail = (qt - 4 >= 0)
                        if tail:
                            kt = qt - 4
                            kb.op("pe", lambda pe: pe.matmul(PS[2 + pi][:, 0:128], lhsT=KTz[:, 2, g, kt * 128:(kt + 1) * 128], rhs=QT[:, i, qcs],
                                                             start=True, stop=True), reads=KT_r + [QT_r[qb]], writes=[PS_r[2 + pi]])
                            kb.op("act", lambda a: a.activation(out=ex[:, 0:128], in_=PS[2 + pi][:, 0:128], func=AF.Exp), reads=[PS_r[2 + pi]], writes=[ex_r])
                            kb.op("dve", lambda v: v.tensor_tensor(out=PT[pi][:, 512:640], in0=ex[:, 0:128], in1=wmask[:, :], op=ALU.mult),
                                  reads=[ex_r, k_r, PT_r[pi]], writes=[PT_r[pi]])

                        def fn(pe):
                            inst = None
                            seq_ = [(n_, kt) for n_, kt in enumerate(main)] + ([(4, qt - 4)] if tail else [])
                            for idx, (n_, kt) in enumerate(seq_):
                                inst = pe.matmul(PS[4][:, q * 65:(q + 1) * 65], lhsT=PT[pi][:, n_ * 128:(n_ + 1) * 128], rhs=VW[:, kt, g, :],
                                                 start=(idx == 0), stop=(idx == len(seq_) - 1))
                            return inst
                        kb.op("pe", fn, reads=[PT_r[pi]] + VW_r[max(0, qt - 4):qt + 1], writes=[PS_r[4]])
                    branch_finish(PS[4][:, 0:260], PS_r[4], 65, 4, h, 2, tts, False, ONSA, on_r)
            def nsa_e(qb):
                qc = slice(qb * 512, (qb + 1) * 512)
                tts = list(range(qb * 4, qb * 4 + 4))
                ONSA, on_r, impb, imp_r = ONSA2[qb % 2], on_r2[qb % 2], impb2[qb % 2], imp_r2[qb % 2]
                for q in range(4):
                    tt = tts[q]
                    kb.op("act", lambda a: a.activation(out=obf[:, :], in_=ONSA[:, q, :], func=AF.Copy), reads=[on_r[q]], writes=[obf_r])
                    for half in range(2):
                        out_to_OT(obf[:, half * 256:(half + 1) * 256], obf_r, 128, OT, OT_r, 4 + 2 * half, tt * 128)
            for qb in range(4):
                if qb == 0:
                    nsa_a(0)
                if qb + 1 < 4:
                    nsa_a(qb + 1)
                if ns_ > 3:
                    nsa_b(qb)
                if ns_ > 5:
                    nsa_d(qb)
                if ns_ > 4:
                    nsa_c(qb)
                if ns_ > 6:
                    nsa_e(qb)
            kb.barrier()

    for sq in range(NSEQ):
        for l in range(NLAY):
            if l == 0:
                with sb("xstage", [128, 2, D], F32) as xst:
                    xst_r = regs(2)
                    for tt in range(NT):
                        i = tt % 2
                        kb.dma("sp", xst[:, i, :], x_d.ap()[sq, tt * 128:(tt + 1) * 128, :], writes=[xst_r[i]])
                        make_xT(xst[:, i, :], xst_r[i], tt)
                    kb.barrier()
            if cfg.get("stop") == "xt":
                continue
            res_src = (lambda tt: x_d.ap()[sq, tt * 128:(tt + 1) * 128, :]) if l == 0 else \
                      (lambda tt: xres[1].ap()[tt * 128:(tt + 1) * 128, :])
            res_regs = None if l == 0 else xres_r[1]

            with sb("OT", [128, 8, S], BF16) as OT:
                OT_r = regs(NT)
                if inject_O:
                    for k in range(8):
                        kb.dma("pool", OT[:, k, :], dbg_OT.ap()[:, k, :], writes=OT_r)
                if "gla" in mixers:
                    gla_phase(sq, l, OT, OT_r)
                if "rwkv" in mixers:
                    rwkv_phase(sq, l, OT, OT_r)
                if "nsa" in mixers:
                    nsa_phase(sq, l, OT, OT_r)
                if "OT" in dump_d:
                    for k in range(8):
                        with sb("otd", [128, S], F32) as otd:
                            r_ = Reg()
                            kb.op("act", lambda a: a.activation(out=otd[:], in_=OT[:, k, :], func=AF.Copy),
                                  reads=OT_r, writes=[r_])
                            dump("OT", otd[:], r_, idx=k)
                            kb.barrier()
                if cfg.get("stop") == "outproj0":
                    continue
                with sb("Wo", [128, 8, D], BF16) as Wo, \
                        sb("ln1", [128, 2, D], F32) as ln1, \
                        sb("xs1", [128, 2, D], F32) as xs1, \
                        sb("st1", [128, 2, 32], F32) as st1:
                    Wo_r = Reg()
                    ln_r = Reg()
                    xs_r = regs(2)
                    st_r = regs(2)
                    load_w(Wo, w_out16.ap()[l], D, Wo_r)
                    kb.dma("sp", ln1[:, 0, :], bcast_rows(rowtab, l * 5120 + 1024, D), writes=[ln_r])
                    kb.dma("sp", ln1[:, 1, :], bcast_rows(rowtab, l * 5120 + 2048, D), writes=[ln_r])
                    for tt in range(NT):
                        i = tt % 2
                        kb.dma("sp", xs1[:, i, :], res_src(tt), reads=([res_regs[tt]] if res_regs else []), writes=[xs_r[i]])
                        for hf in range(2):
                            def fn(pe, hf=hf):
                                inst = None
                                for k in range(8):
                                    inst = pe.matmul(PS[hf][:, :], lhsT=OT[:, k, tt * 128:(tt + 1) * 128],
                                                     rhs=Wo[:, k, hf * 512:(hf + 1) * 512], start=(k == 0), stop=(k == 7))
                                return inst
                            kb.op("pe", fn, reads=[OT_r[tt], Wo_r], writes=[PS_r[hf]])
                            kb.op("dve", lambda v, hf=hf: v.scalar_tensor_tensor(
                                out=xs1[:, i, hf * 512:(hf + 1) * 512], in0=xs1[:, i, hf * 512:(hf + 1) * 512], scalar=ALPHA,
                                in1=PS[hf][:, :], op0=ALU.mult, op1=ALU.add), reads=[PS_r[hf], xs_r[i]], writes=[xs_r[i]])
                        layer_norm(xs1[:, i, :], xs_r[i], ln1[:, 0, :], ln1[:, 1, :], ln_r, st1[:, i, :], st_r[i])
                        kb.dma("sp", xres[0].ap()[tt * 128:(tt + 1) * 128, :], xs1[:, i, :], reads=[xs_r[i]], writes=[xres_r[0][tt]])
                        make_xT(xs1[:, i, :], xs_r[i], tt)
                        if "x1" in dump_d and sq == 0 and l == cfg.get("dump_layer", 0):
                            dump("x1", xs1[:, i, :], xs_r[i], idx=tt)
                    kb.barrier()
            if cfg.get("stop") in ("outproj", "outproj0"):
                continue
            with sb("aT", [128, NFC, 1024], BF16) as aT, \
                    sb("Wd", [128, NFC, D], BF16) as Wd, \
                    sb("Wgu", [128, 2, 2, 8, 512], BF16) as Wgu, \
                    sb("sg", [128, 2, 512], F32) as sg, \
                    sb("ln2", [128, 2, D], F32) as ln2, \
                    sb("xs2", [128, 2, D], F32) as xs2, \
                    sb("st2", [128, 2, 32], F32) as st2:
                Wd_r = Reg()
                ln_r = Reg()
                aT_r = regs(NFC)
                Wgu_r = regs(2)
                sg_r = regs(2)
                xs_r = regs(2)
                st_r = regs(2)
                kb.dma("sp", ln2[:, 0, :], bcast_rows(rowtab, l * 5120 + 3072, D), writes=[ln_r])
                kb.dma("sp", ln2[:, 1, :], bcast_rows(rowtab, l * 5120 + 4096, D), writes=[ln_r])
                for k in range(NFC):
                    for c0 in (0, 512):
                        kb.dma("sp", Wd[:, k, c0:c0 + 512], w_down16.ap()[l, k * 128:(k + 1) * 128, c0:c0 + 512], reads=[w16_r], writes=[Wd_r])
                last = (l == NLAY - 1)
                for mt in range(2):
                    tok0 = mt * 1024
                    for hc in range(NFC):
                        cg, ci = hc // 4, hc % 4
                        wi = cg % 2
                        if ci == 0:
                            ncol = min(512, FF - cg * 512)
                            for gu, wsrc in ((0, w_gate16), (1, w_up16)):
                                for k in range(8):
                                    kb.dma("sp", Wgu[:, wi, gu, k, 0:ncol],
                                           wsrc.ap()[l, k * 128:(k + 1) * 128, cg * 512:cg * 512 + ncol],
                                           reads=[w16_r], writes=[Wgu_r[wi]])
                        for blk in range(2):
                            t0 = tok0 + blk * 512
                            pg, pu = (0, 1) if blk == 0 else (2, 3)
                            for gu, pi in ((0, pg), (1, pu)):
                                def fn(pe, gu=gu, pi=pi):
                                    inst = None
                                    for k in range(8):
                                        inst = pe.matmul(PS[pi][:, :], lhsT=Wgu[:, wi, gu, k, ci * 128:(ci + 1) * 128],
                                                         rhs=XT[:, k, t0:t0 + 512], start=(k == 0), stop=(k == 7))
                                    return inst
                                kb.op("pe", fn, reads=[Wgu_r[wi]] + XT_r[t0 // 128:t0 // 128 + 4], writes=[PS_r[pi]])
                            kb.op("act", lambda a: a.activation(out=sg[:, blk, :], in_=PS[pg][:, :], func=AF.Silu),
                                  reads=[PS_r[pg]], writes=[sg_r[blk]])
                            kb.op("dve", lambda v: v.tensor_tensor(out=aT[:, hc, blk * 512:(blk + 1) * 512], in0=sg[:, blk, :],
                                                                  in1=PS[pu][:, :], op=ALU.mult),
                                  reads=[sg_r[blk], PS_r[pu]], writes=[aT_r[hc]])
                    for t8 in range(8):
                        tt = mt * 8 + t8
                        i = tt % 2
                        kb.dma("sp", xs2[:, i, :], xres[0].ap()[tt * 128:(tt + 1) * 128, :], reads=[xres_r[0][tt]], writes=[xs_r[i]])
                        for hf in range(2):
                            pi = 4 + hf

                            def fn(pe, hf=hf, pi=pi):
                                inst = None
                                for k in range(NFC):
                                    inst = pe.matmul(PS[pi][:, :], lhsT=aT[:, k, t8 * 128:(t8 + 1) * 128],
                                                     rhs=Wd[:, k, hf * 512:(hf + 1) * 512], start=(k == 0), stop=(k == NFC - 1))
                                return inst
                            kb.op("pe", fn, reads=aT_r + [Wd_r], writes=[PS_r[pi]])
                            kb.op("dve", lambda v, hf=hf, pi=pi: v.scalar_tensor_tensor(
                                out=xs2[:, i, hf * 512:(hf + 1) * 512], in0=xs2[:, i, hf * 512:(hf + 1) * 512], scalar=ALPHA,
                                in1=PS[pi][:, :], op0=ALU.mult, op1=ALU.add), reads=[PS_r[pi], xs_r[i]], writes=[xs_r[i]])
                        layer_norm(xs2[:, i, :], xs_r[i], ln2[:, 0, :], ln2[:, 1, :], ln_r, st2[:, i, :], st_r[i])
                        if last:
                            kb.dma("sp", out_d.ap()[sq, tt * 128:(tt + 1) * 128, :], xs2[:, i, :], reads=[xs_r[i]])
                        else:
                            kb.dma("sp", xres[1].ap()[tt * 128:(tt + 1) * 128, :], xs2[:, i, :], reads=[xs_r[i]], writes=[xres_r[1][tt]])
                        if "x2" in dump_d and sq == 0 and l == cfg.get("dump_layer", 0):
                            dump("x2", xs2[:, i, :], xs_r[i], idx=tt)
                    if not last:
                        pass
                kb.barrier()
                if not last:
                    for tt in range(NT):
                        i = tt % 2
                        kb.dma("sp", xs2[:, i, :], xres[1].ap()[tt * 128:(tt + 1) * 128, :], reads=[xres_r[1][tt]], writes=[xs_r[i]])
                        make_xT(xs2[:, i, :], xs_r[i], tt)
                    kb.barrier()
    kb.finish()
    return kb


def host_consts():
    c = {}
    c["c_ident"] = np.eye(128, dtype=np.float32)
    s = np.arange(128)
    c["c_caus"] = (s[:, None] <= s[None, :]).astype(np.float32)
    half = 32
    inv = (10000.0 ** (-np.arange(half, dtype=np.float32) / half)).astype(np.float32)
    ang = (np.arange(S, dtype=np.float32)[:, None] * inv[None, :]).astype(np.float32)
    cos = np.cos(ang).astype(np.float32).T
    sin = np.sin(ang).astype(np.float32).T
    cosT = np.concatenate([cos, cos, cos, cos], 0)
    sinT = np.concatenate([-sin, sin, -sin, sin], 0)
    c["c_rope"] = np.stack([cosT, sinT]).astype(np.float32)
    cc = np.arange(128)
    t = np.arange(S)
    c["c_cmpmask"] = ((16 * cc[:, None] + 31 <= t[None, :]) & (cc[:, None] < 127)).astype(np.float32)
    j = np.arange(32)
    c["c_onehot"] = ((t[None, :] // 64) == j[:, None]).astype(np.float32)
    cur = t // 64
    forced = (j[None, :] == 0) | (j[None, :] == cur[:, None]) | (j[None, :] == cur[:, None] - 1)
    future = j[None, :] > cur[:, None]
    m1 = (~forced & ~future).astype(np.float32)
    m2 = np.where(forced, 1e9, np.where(future, -1e9, 0.0)).astype(np.float32)
    selm = np.stack([m1, m2])
    c["c_selm"] = np.ascontiguousarray(selm.reshape(2, NT, 128, 32).transpose(0, 2, 1, 3))
    c0 = np.arange(127) * 16
    s0 = np.arange(32) * 64
    lo = np.maximum(c0[:, None], s0[None, :])
    hi = np.minimum(c0[:, None] + 32, s0[None, :] + 64)
    ov = np.zeros((128, 32), np.float32)
    ov[:127] = np.maximum(hi - lo, 0) / 16
    c["c_ovl"] = ov
    blk = np.zeros((128, 128), np.float32)
    blk[:64, :64] = 1
    blk[64:, 64:] = 1
    c["c_blk"] = blk
    hs = np.zeros((128, 2), np.float32)
    hs[:64, 0] = 1
    hs[64:, 1] = 1
    c["c_hsel"] = hs
    i64 = np.arange(64)
    lo_strict = (i64[None, :] < i64[:, None]).astype(np.float32)
    up_strict = (i64[:, None] < i64[None, :]).astype(np.float32)
    up_incl = (i64[:, None] <= i64[None, :]).astype(np.float32)
    c["c_rwmask"] = np.ascontiguousarray(np.stack([lo_strict, up_strict, up_strict, up_incl, up_incl], axis=1))
    return c


def host_layout(inp):
    d = {}
    f = lambda a: np.ascontiguousarray(np.asarray(a, dtype=np.float32))
    for k in ("w_in", "w_in_vres", "gla_w_a2", "rwkv_w2", "rwkv_a2", "rwkv_v2", "rwkv_g2", "nsa_wk1", "nsa_wk2",
              "nsa_wv1", "nsa_wv2", "w_out", "ffn_w_gate", "ffn_w_up", "ffn_w_down"):
        d[k] = f(inp[k])
    base = 2096
    sw = lambda b: list(range(b + 32, b + 64)) + list(range(b, b + 32))
    pl = lambda b: list(range(b, b + 64))
    cols = []
    for i in range(4):
        cols += pl(base + i * 64) + pl(base + (4 + i) * 64)
    for i in range(4):
        cols += sw(base + i * 64) + sw(base + (4 + i) * 64)
    for c0 in (512, 768, 1024):
        cols += pl(base + c0) + pl(base + c0 + 64)
        cols += sw(base + c0) + sw(base + c0 + 64)
    cols += list(range(base + 640, base + 768)) + list(range(base + 896, base + 1024)) + list(range(base + 1152, base + 1280))
    cols += list(range(base + 1280, base + 1304))
    assert len(cols) == 2200
    d["w_nsa"] = f(np.asarray(inp["w_in"])[:, :, cols])
    d["nsa_posT"] = f(np.stack([np.asarray(inp["nsa_pos_k"]).transpose(0, 2, 1),
                                np.asarray(inp["nsa_pos_v"]).transpose(0, 2, 1)], axis=1))
    pt = np.zeros((L, 128, 32), np.float32)
    mu = np.asarray(inp["rwkv_mu"])
    rwt = [(0, 128), (128, 128), (256, 128), (384, 128), (512, 128), (640, 128), (768, 64), (832, 64), (896, 128), (1024, 32)]
    for l in range(L):
        pt[l, :, 0:2] = np.asarray(inp["gla_b_a"])[l].reshape(2, 128).T
        for i, (c0, n) in enumerate(rwt):
            pt[l, :n, 2 + i] = mu[l, c0:c0 + n]
        if l >= 1:
            pt[l, :32, 12] = np.asarray(inp["rwkv_mu_vres"])[l - 1]
            pt[l, :, 17:19] = np.asarray(inp["rwkv_v0"])[l - 1].reshape(2, 128).T
        pt[l, :, 13:15] = np.asarray(inp["rwkv_w0"])[l].reshape(2, 128).T
        pt[l, :, 15:17] = np.asarray(inp["rwkv_a0"])[l].reshape(2, 128).T
        pt[l, :, 19:21] = np.asarray(inp["rwkv_k_k"])[l].reshape(2, 128).T
        pt[l, :, 21:23] = np.asarray(inp["rwkv_k_a"])[l].reshape(2, 128).T
        pt[l, :, 23:25] = np.asarray(inp["rwkv_r_k"])[l].reshape(2, 128).T
    d["ptab"] = pt
    rt = np.zeros((L, 5120), np.float32)
    for l in range(L):
        rt[l, 0:256] = np.asarray(inp["gla_ln_w"])[l]
        rt[l, 256:512] = np.asarray(inp["gla_ln_b"])[l]
        rt[l, 512:768] = np.asarray(inp["rwkv_ln_w"])[l]
        rt[l, 768:1024] = np.asarray(inp["rwkv_ln_b"])[l]
        rt[l, 1024:2048] = np.asarray(inp["ln1_w"])[l]
        rt[l, 2048:3072] = np.asarray(inp["ln1_b"])[l]
        rt[l, 3072:4096] = np.asarray(inp["ln2_w"])[l]
        rt[l, 4096:5120] = np.asarray(inp["ln2_b"])[l]
    d["rowtab"] = rt
    return d


_CACHE = {}


def kernel(**inputs):
    cfg = {}
    if "full" not in _CACHE:
        _CACHE["full"] = build(cfg)
    kb = _CACHE["full"]
    shared = host_layout(inputs)
    shared.update(host_consts())
    x = np.ascontiguousarray(np.asarray(inputs["x"], dtype=np.float32))
    in_maps = []
    for c in range(8):
        m = dict(shared)
        m["x"] = x[2 * c:2 * c + 2]
        in_maps.append(m)
    res = run_bass_kernel_spmd(kb.nc, in_maps, core_ids=list(range(8)))
    return np.concatenate([r["out"] for r in res.results], axis=0).astype(np.float32)
```

```python
import math
from contextlib import ExitStack
import numpy as np
import concourse.bass as bass
import concourse.mybir as mybir
from concourse.bass_utils import run_bass_kernel_spmd

F32 = mybir.dt.float32
BF16 = mybir.dt.bfloat16
AF = mybir.ActivationFunctionType
ALU = mybir.AluOpType
AX = mybir.AxisListType

S = 2048
D = 1024
NT = S // 128
L = 2
FF = 2816
NFC = FF // 128
ALPHA = float((2 * L) ** 0.25)
LN_EPS = 1e-5
RW_EPS = 64e-5
NDS = 6


class Reg:
    __slots__ = ("lw", "rd")

    def __init__(self):
        self.lw = None
        self.rd = {}


def regs(n):
    return [Reg() for _ in range(n)]


class _Rec:
    def __init__(self):
        self.calls = []

    def __getattr__(self, name):
        def f(*a, **kw):
            self.calls.append((name, a, kw))
            return self
        return f


class _Node:
    __slots__ = ("e", "kind", "calls", "reads", "writes", "cost", "deps")

    def __init__(self, e, kind, calls, reads, writes, cost):
        self.e, self.kind, self.calls, self.reads, self.writes, self.cost = e, kind, calls, reads, writes, cost
        self.deps = ()


def _free_elems(ap):
    try:
        n = 1
        for s_ in list(ap.shape)[1:]:
            n *= int(s_)
        return n
    except Exception:
        return 256


def _ap_bytes(ap):
    try:
        n = 1
        for s_ in list(ap.shape):
            n *= int(s_)
        return n * 4
    except Exception:
        return 65536


def _est_cost(e, calls):
    t = 0.0
    for name, a, kw in calls:
        out = kw.get("out", a[0] if a else None)
        n = _free_elems(out) if out is not None else 256
        if e == "pe":
            mul = 4.0 if (name == "matmul" and getattr(kw.get("lhsT"), "dtype", None) == F32) else 1.0
            t += mul * max(n, 64) / 1.6 + 25.0
        elif e == "act":
            t += n / 1.0 + 220.0
        elif e == "dve":
            t += n / 0.9 + 80.0
        else:
            t += n * 2.0 + 300.0
    return t


class KB:
    def __init__(self, sched=True, W=32):
        nc = bass.Bass("TRN2", target_bir_lowering=False)
        self.nc = nc
        self.E = {"pe": nc.tensor, "act": nc.scalar, "dve": nc.vector, "pool": nc.gpsimd, "sp": nc.sync}
        self.sems = {}
        self.cnt = {}
        for e in ("pe", "act", "dve", "pool"):
            self.sems[e] = nc.alloc_semaphore("s_" + e)
            self.cnt[e] = 0
        self.dq = {}
        for q, nds in (("sp", 12), ("pool", NDS), ("act", 6)):
            keys = []
            for i in range(nds):
                k = "d_%s%d" % (q, i)
                self.sems[k] = nc.alloc_semaphore(k)
                self.cnt[k] = 0
                keys.append(k)
            self.dq[q] = [keys, 0]
        self.seen = {e: {} for e in self.E}
        self.nops = 0
        self.sched = sched
        self.W = W
        self.pending = []

    def _waits(self, e, reads, writes, extra=()):
        need = {}

        def add(rec):
            if rec is None:
                return
            k, c = rec
            if need.get(k, 0) < c:
                need[k] = c

        for r in reads:
            add(r.lw)
        for w in writes:
            add(w.lw)
            for k, c in w.rd.items():
                add((k, c))
        for rec in extra:
            add(rec)
        eng = self.E[e]
        seen = self.seen[e]
        for k, c in need.items():
            if e == "pe" and k == "pe":
                continue
            if seen.get(k, 0) >= c:
                continue
            eng.wait_ge(self.sems[k], c)
            seen[k] = c

    def _mark(self, rec, reads, writes):
        k, c = rec
        for r in reads:
            if r.rd.get(k, 0) < c:
                r.rd[k] = c
        for w in writes:
            w.lw = rec
            w.rd = {}

    def op(self, e, fn, reads=(), writes=()):
        rec = _Rec()
        fn(rec)
        node = _Node(e, "op", rec.calls, list(reads), list(writes), _est_cost(e, rec.calls))
        if self.sched:
            self.pending.append(node)
        else:
            self._emit(node)

    def dma(self, q, out, in_, reads=(), writes=(), **kw):
        node = _Node(q, "dma", (out, in_, kw), list(reads), list(writes), 2000.0 + _ap_bytes(out) / 100.0)
        if self.sched:
            self.pending.append(node)
        else:
            self._emit(node)

    def _emit(self, n):
        e = n.e
        if n.kind == "op":
            self._waits(e, n.reads, n.writes)
            eng = self.E[e]
            inst = None
            for name, a, kw in n.calls:
                inst = getattr(eng, name)(*a, **kw)
            self.cnt[e] += 1
            inst.then_inc(self.sems[e], 1)
            self._mark((e, self.cnt[e]), n.reads, n.writes)
        else:
            out, in_, kw = n.calls
            keys, i = self.dq[e]
            k = keys[i]
            self.dq[e][1] = (i + 1) % len(keys)
            extra = [(k, self.cnt[k])] if self.cnt[k] else []
            self._waits(e, n.reads, n.writes, extra)
            self.E[e].dma_start(out=out, in_=in_, **kw).then_inc(self.sems[k], 16)
            self.cnt[k] += 16
            self._mark((k, self.cnt[k]), n.reads, n.writes)
        self.nops += 1

    def flush(self):
        nodes = self.pending
        self.pending = []
        if not nodes:
            return
        lastw, readers = {}, {}
        for i, n in enumerate(nodes):
            d = set()
            for r in n.reads:
                if id(r) in lastw:
                    d.add(lastw[id(r)])
            for w in n.writes:
                if id(w) in lastw:
                    d.add(lastw[id(w)])
                d.update(readers.get(id(w), ()))
            d.discard(i)
            n.deps = d
            for r in n.reads:
                readers.setdefault(id(r), []).append(i)
            for w in n.writes:
                lastw[id(w)] = i
                readers[id(w)] = []
        queues = {}
        for i, n in enumerate(nodes):
            queues.setdefault(n.e, []).append(i)
        fin = [None] * len(nodes)
        efree = {e: 0.0 for e in queues}
        W, LAT = self.W, 120.0
        remaining = len(nodes)
        while remaining:
            best = None
            for e, q in queues.items():
                for i in q[:W]:
                    n = nodes[i]
                    ready = 0.0
                    ok = True
                    for d in n.deps:
                        f = fin[d]
                        if f is None:
                            ok = False
                            break
                        if f + LAT > ready:
                            ready = f + LAT
                    if not ok:
                        continue
                    start = ready if ready > efree[e] else efree[e]
                    key = (start, i)
                    if best is None or key < best[0]:
                        best = (key, e, i)
            assert best is not None
            (start, _), e, i = best
            n = nodes[i]
            fin[i] = start + n.cost
            efree[e] = (start + 60.0) if n.kind == "dma" else fin[i]
            queues[e].remove(i)
            self._emit(n)
            remaining -= 1

    def barrier(self):
        self.flush()
        for e in self.E:
            for k, c in self.cnt.items():
                if c == 0 or (e == "pe" and k == "pe"):
                    continue
                if self.seen[e].get(k, 0) >= c:
                    continue
                self.E[e].wait_ge(self.sems[k], c)
                self.seen[e][k] = c

    def finish(self):
        self.barrier()


def bcast_rows(t, off, n, parts=128):
    return bass.AP(t, off, [[0, parts], [1, n]])


def build(cfg):
    kb = KB(sched=cfg.get("sched", True), W=cfg.get("W", 128))
    nc = kb.nc
    NSEQ = cfg.get("nseq", 2)
    NLAY = cfg.get("nlay", 2)
    mixers = cfg.get("mixers", ("gla", "rwkv", "nsa"))
    dumps = cfg.get("dumps", ())
    inject_O = cfg.get("inject_O", False)

    def dram_in(name, shape):
        return nc.dram_tensor(name, list(shape), F32, kind="ExternalInput")

    x_d = dram_in("x", [2, S, D])
    w_in = dram_in("w_in", [L, D, 3400])
    w_vres = dram_in("w_in_vres", [1, D, 32])
    w_nsa = dram_in("w_nsa", [L, D, 2200])
    gla_w_a2 = dram_in("gla_w_a2", [L, 16, 256])
    rwkv_w2 = dram_in("rwkv_w2", [L, 64, 256])
    rwkv_a2 = dram_in("rwkv_a2", [L, 64, 256])
    rwkv_v2 = dram_in("rwkv_v2", [1, 32, 256])
    rwkv_g2 = dram_in("rwkv_g2", [L, 160, 256])
    nsa_wk1 = dram_in("nsa_wk1", [L, 2048, 256])
    nsa_wk2 = dram_in("nsa_wk2", [L, 256, 64])
    nsa_wv1 = dram_in("nsa_wv1", [L, 2048, 256])
    nsa_wv2 = dram_in("nsa_wv2", [L, 256, 64])
    nsa_posT = dram_in("nsa_posT", [L, 2, 64, 32])
    w_out = dram_in("w_out", [L, D, D])
    w_gate = dram_in("ffn_w_gate", [L, D, FF])
    w_up = dram_in("ffn_w_up", [L, D, FF])
    w_down = dram_in("ffn_w_down", [L, FF, D])
    ptab = dram_in("ptab", [L, 128, 32])
    rowtab = dram_in("rowtab", [L, 5120])
    c_ident = dram_in("c_ident", [128, 128])
    c_caus = dram_in("c_caus", [128, 128])
    c_rope = dram_in("c_rope", [2, 128, S])
    c_cmpmask = dram_in("c_cmpmask", [128, S])
    c_onehot = dram_in("c_onehot", [32, S])
    c_selm = dram_in("c_selm", [2, 128, NT, 32])
    c_ovl = dram_in("c_ovl", [128, 32])
    c_blk = dram_in("c_blk", [128, 128])
    c_hsel = dram_in("c_hsel", [128, 2])
    c_rwmask = dram_in("c_rwmask", [64, 5, 64])
    if inject_O:
        dbg_OT = dram_in("dbg_OT", [128, 8, S])
    out_d = nc.dram_tensor("out", [2, S, D], F32, kind="ExternalOutput")
    xres = [nc.dram_tensor("xres%d" % i, [S, D], F32, kind="Internal") for i in range(2)]
    xres_r = [regs(NT) for _ in range(2)]
    vfirst_d = nc.dram_tensor("vfirst", [2, 128, S], F32, kind="Internal")
    vfirst_r = regs(2)
    dump_d = {}
    for name, shape in dumps:
        dump_d[name] = nc.dram_tensor("dump_" + name, list(shape), F32, kind="ExternalOutput")

    XT = nc.alloc_sbuf_tensor("XT", [128, 8, S], BF16)
    XT_r = regs(NT)
    ident = nc.alloc_sbuf_tensor("ident", [128, 128], BF16)
    identf = nc.alloc_sbuf_tensor("identf", [128, 128], F32)
    caus = nc.alloc_sbuf_tensor("caus", [128, 4, 128], F32)
    ptb = nc.alloc_sbuf_tensor("ptb", [128, L, 32], F32)
    ptn = nc.alloc_sbuf_tensor("ptn", [128, L, 32], F32)
    c_r = Reg()
    for l in range(L):
        kb.dma("sp", ptb[:, l, :], ptab.ap()[l], writes=[c_r])
    kb.dma("pool", ident[:], c_ident.ap(), writes=[c_r])
    kb.dma("sp", identf[:], c_ident.ap(), writes=[c_r])
    for i in range(4):
        kb.dma("sp", caus[:, i, :], c_caus.ap(), writes=[c_r])

    def dram16(name, shape):
        return nc.dram_tensor(name, list(shape), BF16, kind="Internal")

    conv_list = [(w_in, [L, D, 3400]), (w_nsa, [L, D, 2200]), (w_out, [L, D, D]), (w_gate, [L, D, FF]), (w_up, [L, D, FF]),
                 (w_down, [L, FF, D]), (nsa_wk1, [L, 2048, 256]), (nsa_wv1, [L, 2048, 256])]
    w16 = {}
    w16_r = Reg()
    CH = 2816
    with nc.sbuf_tensor("cv_f", [128, 3, CH], F32) as cvf, nc.sbuf_tensor("cv_b", [128, 3, CH], BF16) as cvb:
        cvf_r, cvb_r = regs(3), regs(3)
        job = 0
        for src, shape in conv_list:
            dst = dram16(src.name + "_16", shape)
            w16[src.name] = dst
            tot = 1
            for d_ in shape:
                tot *= d_
            per = tot // 128
            assert per * 128 == tot
            for c0 in range(0, per, CH):
                n = min(CH, per - c0)
                i = job % 3
                kb.dma("sp", cvf[:, i, 0:n], bass.AP(src, c0, [[per, 128], [1, n]]), writes=[cvf_r[i]])
                eng = "act"
                if eng == "act":
                    kb.op("act", lambda a: a.activation(out=cvb[:, i, 0:n], in_=cvf[:, i, 0:n], func=AF.Copy),
                          reads=[cvf_r[i]], writes=[cvb_r[i]])
                else:
                    kb.op(eng, lambda v: v.tensor_scalar(out=cvb[:, i, 0:n], in0=cvf[:, i, 0:n], scalar1=1.0, scalar2=None, op0=ALU.mult),
                          reads=[cvf_r[i]], writes=[cvb_r[i]])
                kb.dma("act", bass.AP(dst, c0, [[per, 128], [1, n]]), cvb[:, i, 0:n], reads=[cvb_r[i]], writes=[w16_r])
                job += 1
        kb.barrier()
    w_in16, w_nsa16, w_out16 = w16["w_in"], w16["w_nsa"], w16["w_out"]
    w_gate16, w_up16, w_down16 = w16["ffn_w_gate"], w16["ffn_w_up"], w16["ffn_w_down"]
    wk1_16, wv1_16 = w16["nsa_wk1"], w16["nsa_wv1"]

    PS = [nc.alloc_psum_tensor("ps%d" % i, [128, 512], F32) for i in range(6)]
    PS_r = regs(6)
    PB = [nc.alloc_psum_tensor("pb%d" % i, [128, 1024], BF16) for i in range(2)]
    PB_r = regs(2)

    _uid = [0]

    def sb(name, shape, dt):
        _uid[0] += 1
        return nc.sbuf_tensor("%s_%d" % (name, _uid[0]), list(shape), dt)

    def proj_fm(ps_ap, Wt, c0, M, tok0, N, wreads, treads, pw, kparts=8):
        def fn(pe):
            inst = None
            for k in range(kparts):
                inst = pe.matmul(ps_ap, lhsT=Wt[:, k, c0:c0 + M], rhs=XT[:, k, tok0:tok0 + N],
                                 start=(k == 0), stop=(k == kparts - 1))
            return inst
        kb.op("pe", fn, reads=list(wreads) + list(treads), writes=[pw])

    def proj_tm(ps_ap, Wt, c0, N, tt, wreads, pw):
        def fn(pe):
            inst = None
            for k in range(8):
                inst = pe.matmul(ps_ap, lhsT=XT[:, k, tt * 128:(tt + 1) * 128], rhs=Wt[:, k, c0:c0 + N],
                                 start=(k == 0), stop=(k == 7))
            return inst
        kb.op("pe", fn, reads=list(wreads) + [XT_r[tt]], writes=[pw])

    def load_w(Wt, src3, ncols, wreg, q="sp", chunk=None):
        for k in range(8):
            kb.dma(q, Wt[:, k, 0:ncols], src3[k * 128:(k + 1) * 128, 0:ncols], reads=[w16_r], writes=[wreg])

    xb = [nc.alloc_sbuf_tensor("xb%d" % i, [128, D], BF16) for i in range(2)]
    xb_r = regs(2)
    xb_i = [0]

    def make_xT(src_ap, src_reg, tt):
        i = xb_i[0]
        xb_i[0] ^= 1
        kb.op("act", lambda a: a.activation(out=xb[i][:], in_=src_ap, func=AF.Copy),
              reads=[src_reg], writes=[xb_r[i]])
        pb = PB[i]

        def fn(pe):
            inst = None
            for k in range(8):
                inst = pe.transpose(out=pb[:, k * 128:(k + 1) * 128], in_=xb[i][:, k * 128:(k + 1) * 128],
                                    identity=ident[:])
            return inst
        kb.op("pe", fn, reads=[xb_r[i], c_r], writes=[PB_r[i]])
        kb.op("dve", lambda v: v.tensor_copy(out=XT[:, :, tt * 128:(tt + 1) * 128],
                                             in_=pb[:, :].rearrange("p (k t) -> p k t", k=8)),
              reads=[PB_r[i]], writes=[XT_r[tt]])

    def layer_norm(xt_ap, xreg, lnw_ap, lnb_ap, lnreg, st, st_r):
        kb.op("dve", lambda v: v.bn_stats(out=st[:, 0:6], in_=xt_ap[:, 0:512]), reads=[xreg], writes=[st_r])
        kb.op("dve", lambda v: v.bn_stats(out=st[:, 6:12], in_=xt_ap[:, 512:1024]), reads=[xreg, st_r], writes=[st_r])
        kb.op("dve", lambda v: v.bn_aggr(out=st[:, 12:14], in_=st[:, 0:12]), reads=[st_r], writes=[st_r])
        kb.op("act", lambda a: a.activation(out=st[:, 14:15], in_=st[:, 13:14], func=AF.Sqrt, bias=LN_EPS, scale=1.0),
              reads=[st_r], writes=[st_r])
        kb.op("dve", lambda v: v.reciprocal(out=st[:, 15:16], in_=st[:, 14:15]), reads=[st_r], writes=[st_r])
        kb.op("dve", lambda v: v.scalar_tensor_tensor(out=st[:, 16:17], in0=st[:, 12:13], scalar=-1.0, in1=st[:, 15:16],
                                                      op0=ALU.mult, op1=ALU.mult), reads=[st_r], writes=[st_r])
        kb.op("act", lambda a: a.activation(out=xt_ap, in_=xt_ap, func=AF.Identity, bias=st[:, 16:17], scale=st[:, 15:16]),
              reads=[st_r, xreg], writes=[xreg])
        kb.op("dve", lambda v: v.tensor_tensor(out=xt_ap, in0=xt_ap, in1=lnw_ap, op=ALU.mult), reads=[xreg, lnreg], writes=[xreg])
        kb.op("dve", lambda v: v.tensor_tensor(out=xt_ap, in0=xt_ap, in1=lnb_ap, op=ALU.add), reads=[xreg, lnreg], writes=[xreg])

    def dump(name, sb_ap, reg, idx=None):
        if name in dump_d:
            dst = dump_d[name].ap() if idx is None else dump_d[name].ap()[idx]
            kb.dma("sp", dst, sb_ap, reads=[reg])

    kb.op("dve", lambda v: v.tensor_scalar(out=ptn[:, :, :], in0=ptb[:, :, :], scalar1=-1.0, scalar2=None, op0=ALU.mult),
          reads=[c_r], writes=[c_r])
    kb.op("dve", lambda v: v.tensor_scalar(out=ptn[:, :, 2:13], in0=ptb[:, :, 2:13], scalar1=-1.0, scalar2=1.0,
                                           op0=ALU.mult, op1=ALU.add), reads=[c_r], writes=[c_r])

    def gla_phase(sq, l, OT, OT_r):
        with ExitStack() as es:
            A = lambda n_, s_, d_: es.enter_context(sb(n_, s_, d_))
            Wg = A("Wg", [128, 8, 1152], BF16)
            wa2 = A("wa2", [16, 256], BF16)
            gln = A("gln", [128, 2, 256], F32)
            alrT = A("alrT", [16, 512], BF16)
            t1 = A("gt1", [128, 512], F32)
            t2 = A("gt2", [128, 512], F32)
            EB = A("EB", [128, 2, 512], F32)
            QTl = A("gQT", [128, 2, 2, 512], BF16)
            KTl = A("gKT", [128, 2, 512], BF16)
            Ktok = A("gKtok", [128, 4, 256], BF16)
            V = A("gV", [128, 4, 256], BF16)
            SG = A("gSG", [128, 4, 256], F32)
            AT = A("gAT", [128, 4, 128], BF16)
            St = A("gSt", [128, 2, 128], F32)
            tmpS = A("gtmpS", [128, 128], F32)
            blkm = A("gblk", [128, 128], F32)
            Sb = A("gSb", [128, 2, 128], BF16)
            rmask = A("grm", [128, 512], F32)
            ow = A("gow", [128, 256], F32)
            ow2 = A("gow2", [128, 256], F32)
            ob = A("gob", [128, 256], BF16)
            st = A("gst", [128, 32], F32)
            W_r, p_r, alr_r, t1_r, t2_r = Reg(), Reg(), Reg(), Reg(), Reg()
            EB_r, QT_r, KT_r = regs(2), regs(2), regs(2)
            Ktok_r, V_r, SG_r, AT_r, St_r, Sb_r = Reg(), regs(4), regs(4), Reg(), Reg(), Reg()
            ow_r, ow2_r, ob_r, st_r = Reg(), Reg(), Reg(), Reg()
            gs = cfg.get("gla_stop", 99)
            load_w(Wg, w_in16.ap()[l][:, 0:1040], 1040, W_r)
            kb.dma("pool", wa2[:, :], gla_w_a2.ap()[l], writes=[p_r])
            kb.dma("sp", gln[:, 0, :], bcast_rows(rowtab, l * 5120 + 0, 256), writes=[p_r])
            kb.dma("sp", gln[:, 1, :], bcast_rows(rowtab, l * 5120 + 256, 256), writes=[p_r])
            kb.op("pool", lambda g: g.memset(rmask[:, :], 1.0), writes=[p_r])
            kb.op("pool", lambda g: g.memset(rmask[:, :].rearrange("p (c t) -> p c t", c=4)[:, :, 0:1], 0.0), writes=[p_r])
            kb.op("pool", lambda g: g.memset(St[:, :, :], 0.0), writes=[St_r])
            kb.op("pool", lambda g: g.memset(QTl[:, :, :, :], 0.0), writes=QT_r)
            kb.dma("sp", blkm[:, :], c_blk.ap(), writes=[p_r])
            tmpS_r = Reg()
            kb.op("pool", lambda g: g.memset(Sb[:, :, :], 0.0), writes=[Sb_r])
            for mt in range(4 if gs > 0 else 0):
                tok0 = mt * 512
                xr = XT_r[mt * 4:mt * 4 + 4]
                proj_fm(PS[0][0:16, :], Wg, 1024, 16, tok0, 512, [W_r], xr, PS_r[0])
                kb.op("act", lambda a: a.activation(out=alrT[:, :], in_=PS[0][0:16, :], func=AF.Copy),
                      reads=[PS_r[0]], writes=[alr_r])
                for hp in range(2):
                    kb.op("pe", lambda pe: pe.matmul(PS[1][:, :], lhsT=wa2[0:16, hp * 128:(hp + 1) * 128], rhs=alrT[0:16, :],
                                                     start=True, stop=True), reads=[p_r, alr_r], writes=[PS_r[1]])
                    kb.op("act", lambda a: a.activation(out=t1[:, :], in_=PS[1][:, :], func=AF.Exp, scale=-1.0,
                                                        bias=ptn[:, l, hp:hp + 1]), reads=[PS_r[1], c_r], writes=[t1_r])
                    kb.op("act", lambda a: a.activation(out=t1[:, :], in_=t1[:, :], func=AF.Ln, bias=1.0, scale=1.0),
                          reads=[t1_r], writes=[t1_r])
                    kb.op("dve", lambda v: v.tensor_tensor_scan(out=t2[:, :], data0=rmask[:, :], data1=t1[:, :], initial=0.0,
                                                                op0=ALU.mult, op1=ALU.add), reads=[t1_r, p_r], writes=[t2_r])
                    kb.op("act", lambda a: a.activation(out=EB[:, hp, :], in_=t2[:, :], func=AF.Exp, scale=-1.0 / 16.0),
                          reads=[t2_r], writes=[EB_r[hp]])
                    kb.op("act", lambda a: a.activation(out=t1[:, :], in_=t2[:, :], func=AF.Exp, scale=1.0 / 16.0),
                          reads=[t2_r], writes=[t1_r])
                    proj_fm(PS[2][:, :], Wg, hp * 128, 128, tok0, 512, [W_r], xr, PS_r[2])
                    for hh in range(2):
                        pr = slice(hh * 64, hh * 64 + 64)
                        kb.op("dve", lambda v: v.scalar_tensor_tensor(out=QTl[pr, hp, hh, :], in0=PS[2][pr, :], scalar=0.125,
                                                                      in1=EB[pr, hp, :], op0=ALU.mult, op1=ALU.mult),
                              reads=[PS_r[2], EB_r[hp]], writes=[QT_r[hp]])
                    proj_fm(PS[3][:, :], Wg, 256 + hp * 128, 128, tok0, 512, [W_r], xr, PS_r[3])
                    kb.op("dve", lambda v: v.tensor_tensor(out=KTl[:, hp, :], in0=PS[3][:, :], in1=t1[:, :], op=ALU.mult),
                          reads=[PS_r[3], t1_r], writes=[KT_r[hp]])

                    def fn(pe):
                        inst = None
                        for j in range(4):
                            inst = pe.transpose(out=PB[0][:, j * 128:(j + 1) * 128], in_=KTl[:, hp, j * 128:(j + 1) * 128],
                                                identity=ident[:])
                        return inst
                    kb.op("pe", fn, reads=[KT_r[hp], c_r], writes=[PB_r[0]])
                    kb.op("dve", lambda v: v.tensor_copy(out=Ktok[:, :, hp * 128:(hp + 1) * 128],
                                                         in_=PB[0][:, 0:512].rearrange("p (j c) -> p j c", j=4)),
                          reads=[PB_r[0]], writes=[Ktok_r])
                for j in range(4 if gs > 1 else 0):
                    tt = mt * 4 + j
                    proj_tm(PS[4][:, 0:256], Wg, 512, 256, tt, [W_r], PS_r[4])
                    proj_tm(PS[5][:, 0:256], Wg, 768, 256, tt, [W_r], PS_r[5])
                    if cfg.get("gv", 3) >= 2:
                        kb.op("dve", lambda v: v.tensor_scalar(out=V[:, j, :], in0=PS[4][:, 0:256], scalar1=1.0, scalar2=None, op0=ALU.mult),
                              reads=[PS_r[4]], writes=[V_r[j]])
                    if cfg.get("gv", 3) >= 3:
                        kb.op("act", lambda a: a.activation(out=SG[:, j, :], in_=PS[5][:, 0:256], func=AF.Silu),
                              reads=[PS_r[5]], writes=[SG_r[j]])
                for j in range(4 if gs > 2 else 0):
                    tt = mt * 4 + j
                    cc = slice(j * 128, (j + 1) * 128)

                    def fn(pe):
                        inst = None
                        for h in range(4):
                            hp, hh = h // 2, h % 2
                            inst = pe.matmul(PS[0][:, h * 128:(h + 1) * 128], lhsT=KTl[:, hp, cc], rhs=QTl[:, hp, hh, cc],
                                             start=True, stop=True)
                        return inst
                    kb.op("pe", fn, reads=QT_r + KT_r, writes=[PS_r[0]])
                    kb.op("dve", lambda v: v.tensor_tensor(out=AT[:, :, :], in0=PS[0][:, :].rearrange("p (h t) -> p h t", h=4),
                                                          in1=caus[:, :, :], op=ALU.mult), reads=[PS_r[0], c_r], writes=[AT_r])

                    def fn(pe):
                        inst = None
                        for h in range(4):
                            hp, hh = h // 2, h % 2
                            pe.matmul(PS[1][:, h * 64:(h + 1) * 64], lhsT=AT[:, h, :], rhs=V[:, j, h * 64:(h + 1) * 64],
                                      start=True, stop=False)
                            inst = pe.matmul(PS[1][:, h * 64:(h + 1) * 64], lhsT=QTl[:, hp, hh, cc], rhs=Sb[:, hp, hh * 64:(hh + 1) * 64],
                                             start=False, stop=True)
                        return inst
                    if gs <= 3:
                        continue
                    kb.op("pe", fn, reads=[AT_r, V_r[j], Sb_r] + QT_r, writes=[PS_r[1]])

                    def fn(pe):
                        inst = None
                        for hp in range(2):
                            inst = pe.matmul(PS[2][:, hp * 128:(hp + 1) * 128], lhsT=Ktok[:, j, hp * 128:(hp + 1) * 128],
                                             rhs=V[:, j, hp * 128:(hp + 1) * 128], start=True, stop=True)
                        return inst
                    if gs <= 4:
                        continue
                    kb.op("pe", fn, reads=[Ktok_r, V_r[j]], writes=[PS_r[2]])
                    for hp in range(2):
                        ee = EB[:, hp, j * 128 + 127:j * 128 + 128]
                        kb.op("dve", lambda v: v.scalar_tensor_tensor(out=tmpS[:, :], in0=PS[2][:, hp * 128:(hp + 1) * 128], scalar=ee,
                                                                      in1=blkm[:, :], op0=ALU.mult, op1=ALU.mult),
                              reads=[PS_r[2], EB_r[hp], p_r], writes=[tmpS_r])
                        kb.op("dve", lambda v: v.scalar_tensor_tensor(out=St[:, hp, :], in0=St[:, hp, :], scalar=ee, in1=tmpS[:, :],
                                                                      op0=ALU.mult, op1=ALU.add),
                              reads=[St_r, tmpS_r, EB_r[hp]], writes=[St_r])
                    kb.op("act", lambda a: a.activation(out=Sb[:, :, :], in_=St[:, :, :], func=AF.Copy), reads=[St_r], writes=[Sb_r])
                    if gs <= 5:
                        continue
                    head_norm_gate(PS[1][:, 0:256], PS_r[1], ow, ow_r, ow2, ow2_r, st, st_r, gln, p_r, LN_EPS)
                    kb.op("dve", lambda v: v.tensor_tensor(out=ob[:, :], in0=ow[:, :], in1=SG[:, j, :], op=ALU.mult),
                          reads=[ow_r, SG_r[j]], writes=[ob_r])
                    if gs <= 6:
                        continue
                    out_to_OT(ob, ob_r, 128, OT, OT_r, 0, tt * 128)
            kb.barrier()

    def head_norm_gate(ps_ap, ps_r, ow, ow_r, ow2, ow2_r, st, st_r, gln, gln_r, eps, P=128):
        v4 = lambda ap: ap.rearrange("p (h d) -> p h d", h=4)
        kb.op("act", lambda a: a.activation(out=ow[0:P, :], in_=ps_ap, func=AF.Copy), reads=[ps_r], writes=[ow_r])
        kb.op("act", lambda a: a.activation(out=ow2[0:P, :], in_=ow[0:P, :], func=AF.Square), reads=[ow_r], writes=[ow2_r])
        kb.op("dve", lambda v: v.reduce_sum(out=st[0:P, 0:4], in_=v4(ow[0:P, :]), axis=AX.X), reads=[ow_r], writes=[st_r])
        kb.op("dve", lambda v: v.reduce_sum(out=st[0:P, 4:8], in_=v4(ow2[0:P, :]), axis=AX.X), reads=[ow2_r, st_r], writes=[st_r])
        kb.op("dve", lambda v: v.tensor_scalar(out=st[0:P, 8:16], in0=st[0:P, 0:8], scalar1=1.0 / 64.0, scalar2=None, op0=ALU.mult),
              reads=[st_r], writes=[st_r])
        kb.op("dve", lambda v: v.tensor_tensor(out=st[0:P, 16:20], in0=st[0:P, 8:12], in1=st[0:P, 8:12], op=ALU.mult),
              reads=[st_r], writes=[st_r])
        kb.op("dve", lambda v: v.tensor_tensor(out=st[0:P, 20:24], in0=st[0:P, 12:16], in1=st[0:P, 16:20], op=ALU.subtract),
              reads=[st_r], writes=[st_r])
        kb.op("act", lambda a: a.activation(out=st[0:P, 24:28], in_=st[0:P, 20:24], func=AF.Sqrt, bias=eps, scale=1.0),
              reads=[st_r], writes=[st_r])
        kb.op("dve", lambda v: v.reciprocal(out=st[0:P, 28:32], in_=st[0:P, 24:28]), reads=[st_r], writes=[st_r])
        for h in range(4):
            kb.op("dve", lambda v: v.tensor_scalar(out=ow[0:P, h * 64:(h + 1) * 64], in0=ow[0:P, h * 64:(h + 1) * 64],
                                                   scalar1=st[0:P, 8 + h:9 + h], scalar2=st[0:P, 28 + h:29 + h],
                                                   op0=ALU.subtract, op1=ALU.mult), reads=[ow_r, st_r], writes=[ow_r])
        kb.op("dve", lambda v: v.tensor_tensor(out=ow[0:P, :], in0=ow[0:P, :], in1=gln[0:P, 0, :], op=ALU.mult),
              reads=[ow_r, gln_r], writes=[ow_r])
        kb.op("dve", lambda v: v.tensor_tensor(out=ow[0:P, :], in0=ow[0:P, :], in1=gln[0:P, 1, :], op=ALU.add),
              reads=[ow_r, gln_r], writes=[ow_r])

    def out_to_OT(ob, ob_r, P, OT, OT_r, k0, tokc0, ncols=256):
        nk = ncols // 128

        def fn(pe):
            inst = None
            for kk in range(nk):
                inst = pe.transpose(out=PB[1][:, kk * 128:kk * 128 + P], in_=ob[0:P, kk * 128:(kk + 1) * 128],
                                    identity=ident[0:P, 0:P])
            return inst
        kb.op("pe", fn, reads=[ob_r, c_r], writes=[PB_r[1]])
        tr = OT_r[tokc0 // 128]
        kb.op("act", lambda a: a.activation(out=OT[:, k0:k0 + nk, tokc0:tokc0 + P],
                                            in_=PB[1][:, 0:nk * 128].rearrange("p (k t) -> p k t", k=nk)[:, :, 0:P], func=AF.Copy),
              reads=[PB_r[1]], writes=[tr])

    RWT = [(0, 128), (128, 128), (256, 128), (384, 128), (512, 128), (640, 128), (768, 64), (832, 64),
           (896, 128), (1024, 32), (1056, 32)]
    C0 = float(math.exp(-0.5))

    def rwkv_phase(sq, l, OT, OT_r):
        MT = 256
        NCH = MT // 64
        NU = NCH * 4
        with ExitStack() as es:
            A = lambda n_, s_, d_: es.enter_context(sb(n_, s_, d_))
            Wr = A("Wr", [128, 8, 1088], BF16)
            w2 = A("rw2", [64, 256], BF16)
            a2 = A("ra2", [64, 256], BF16)
            v2 = A("rv2", [32, 256], BF16)
            g2a = A("rg2a", [128, 256], BF16)
            g2b = A("rg2b", [128, 256], BF16)
            rln = A("rln", [128, 2, 256], F32)
            blkm = A("rblk", [128, 128], F32)
            blkf = A("rblkf", [128, 128], F32)
            hselb = A("rhsel", [128, 2], BF16)
            rmask = A("rrm", [128, MT], F32)
            amask = A("ramask", [64, 5, 64], F32)
            xs = [A("rxs%d" % i, [128, MT], F32) for i in range(6)]
            tt_ = [A("rt%d" % i, [128, MT], F32) for i in range(8)]
            ttb_ = [A("rtb%d" % i, [128, MT], F32) for i in range(9)]
            tb_r = regs(9)
            Gam2 = [A("rGam%d" % i_, [128, 2, MT], F32) for i_ in range(2)]
            ARz2 = [A("rARz%d" % i_, [128, 2, 2, 2, MT], BF16) for i_ in range(2)]
            ATz2 = [t_[:, :, :, 0, :] for t_ in ARz2]
            RTz2 = [t_[:, :, :, 1, :] for t_ in ARz2]
            BTt = A("rBT", [128, 2, MT], BF16)
            KTt = A("rKT", [128, 2, MT], BF16)
            rkr = A("rrkr", [128, 2, MT], BF16)
            TW = A("rTW", [64, MT], BF16)
            AL = A("rAL", [64, MT], BF16)
            SGL = A("rSGL", [128, MT], BF16)
            SGL2 = A("rSGL2", [128, MT], BF16)
            VR = A("rVR", [32, MT], BF16)
            vb = A("rvb", [128, MT], BF16)
            Vtok2 = [A("rVtok%d" % i_, [128, NCH, 256], BF16) for i_ in range(2)]
            Btok2 = [A("rBtok%d" % i_, [128, NCH, 256], BF16) for i_ in range(2)]
            Ktok2 = [A("rKtok%d" % i_, [128, NCH, 256], BF16) for i_ in range(2)]
            gtok2 = [A("rgtok%d" % i_, [64, NCH, 256], F32) for i_ in range(2)]
            cb2 = [A("rcb%d" % i_, [64, NCH, 4], F32) for i_ in range(2)]
            MN = [A("rMN%d" % i, [64, NU, 2, 64], F32) for i in range(2)]
            Pm2 = [A("rP%d" % i_, [64, NU, 64], F32) for i_ in range(2)]
            A32 = [A("rA3%d" % i_, [128, NU, 3, 64], BF16) for i_ in range(2)]
            Rs = A("rRs", [64, 256], F32)
            Ub = A("rUb", [128, 256], BF16)
            Tblk = A("rTblk", [128, 2, 128], F32)
            Tb = A("rTb", [128, 2, 128], BF16)
            tmpS = A("rtmpS", [128, 128], F32)
            ow = A("row", [128, 256], F32)
            ow2 = A("row2", [128, 256], F32)
            ob = A("rob", [128, 256], BF16)
            st = A("rst", [128, 32], F32)
            W_r, p_r = Reg(), Reg()
            xs_r, t_r = regs(6), regs(8)
            t_r0 = t_r
            BT_r, KT_r, rkr_r = regs(2), regs(2), regs(2)
            Gam_r2, ATz_r2, RTz_r2 = [regs(2), regs(2)], [regs(2), regs(2)], [regs(2), regs(2)]
            TW_r, AL_r, SGL_r, SGL2_r, VR_r, vb_r = Reg(), Reg(), Reg(), Reg(), Reg(), Reg()
            Vtok_r2, Btok_r2, Ktok_r2, gtok_r2, cb_r2 = regs(2), regs(2), regs(2), regs(2), regs(2)
            MN_r, Rs_r, Ub_r, T_r, Tb_r, tmpS_r = regs(2), Reg(), Reg(), Reg(), Reg(), Reg()
            P_r2, A3_r2 = regs(2), regs(2)
            ow_r, ow2_r, ob_r, st_r = Reg(), Reg(), Reg(), Reg()
            rs_ = cfg.get("rw_stop", 99)
            load_w(Wr, w_in16.ap()[l][:, 1040:2096], 1056, W_r)
            if l >= 1:
                for k in range(8):
                    kb.dma("pool", Wr[:, k, 1056:1088], w_vres.ap()[l - 1][k * 128:(k + 1) * 128, :], writes=[W_r])
            kb.dma("pool", w2[:, :], rwkv_w2.ap()[l], writes=[p_r])
            kb.dma("pool", a2[:, :], rwkv_a2.ap()[l], writes=[p_r])
            if l >= 1:
                kb.dma("pool", v2[:, :], rwkv_v2.ap()[l - 1], writes=[p_r])
            kb.op("pool", lambda g: g.memset(g2b[:, :], 0.0), writes=[p_r])
            kb.dma("pool", g2a[:, :], rwkv_g2.ap()[l][0:128, :], writes=[p_r])
            kb.dma("pool", g2b[0:32, :], rwkv_g2.ap()[l][128:160, :], reads=[p_r], writes=[p_r])
            kb.dma("sp", rln[:, 0, :], bcast_rows(rowtab, l * 5120 + 512, 256), writes=[p_r])
            kb.dma("sp", rln[:, 1, :], bcast_rows(rowtab, l * 5120 + 768, 256), writes=[p_r])
            kb.dma("sp", blkm[:, :], c_blk.ap(), writes=[p_r])
            kb.dma("sp", blkf[:, :], c_blk.ap(), writes=[p_r])
            kb.dma("pool", hselb[:, :], c_hsel.ap(), writes=[p_r])
            kb.dma("sp", amask[:, :, :], c_rwmask.ap(), writes=[p_r])
            kb.op("pool", lambda g: g.memset(rmask[:, :], 1.0), writes=[p_r])
            kb.op("pool", lambda g: g.memset(rmask[:, :].rearrange("p (c t) -> p c t", c=NCH)[:, :, 0:1], 0.0), writes=[p_r])
            zl = [(Tblk, [T_r]), (Tb, [Tb_r]), (SGL2, [SGL2_r]), (Ub, [Ub_r])]
            for i_ in range(2):
                zl += [(ARz2[i_], ATz_r2[i_] + RTz_r2[i_]), (Vtok2[i_], [Vtok_r2[i_]]), (Btok2[i_], [Btok_r2[i_]]),
                       (Ktok2[i_], [Ktok_r2[i_]]), (A32[i_], [A3_r2[i_]])]
            for tz, rr in zl:
                kb.op("pool", lambda g, tz=tz: g.memset(tz[:], 0.0), writes=rr)

            carry = A("rcarry", [128, 12], F32)
            carry_r = regs(11)
            kb.op("pool", lambda g: g.memset(carry[:, :], 0.0), writes=carry_r)

            def shift_proj(i, tok0, dst_fn):
                c0, n = RWT[i]
                mucol = 2 + i
                pi = i % 2
                tp = ttb_[7] if pi == 0 else ttb_[8]
                tpr = tb_r[7] if pi == 0 else tb_r[8]
                proj_fm(PS[pi][0:n, 0:MT], Wr, c0, n, tok0, MT, [W_r], XT_r[tok0 // 128:tok0 // 128 + MT // 128], PS_r[pi])
                kb.op("act", lambda a: a.activation(out=tp[0:n, 1:MT], in_=PS[pi][0:n, 0:MT - 1], func=AF.Copy,
                                                    scale=ptb[0:n, l, mucol:mucol + 1]), reads=[PS_r[pi], c_r], writes=[tpr, PS_r[pi]])
                kb.op("act", lambda a: a.activation(out=tp[0:n, 0:1], in_=carry[0:n, i:i + 1], func=AF.Copy,
                                                    scale=ptb[0:n, l, mucol:mucol + 1]), reads=[carry_r[i], c_r, tpr], writes=[tpr])
                kb.op("act", lambda a: a.activation(out=carry[0:n, i:i + 1], in_=PS[pi][0:n, MT - 1:MT], func=AF.Copy),
                      reads=[PS_r[pi], carry_r[i]], writes=[carry_r[i], PS_r[pi]])
                dst_ap, dst_regs = dst_fn()
                kb.op("dve", lambda v: v.scalar_tensor_tensor(out=dst_ap, in0=PS[pi][0:n, 0:MT], scalar=ptn[0:n, l, mucol:mucol + 1],
                                                              in1=tp[0:n, 0:MT], op0=ALU.mult, op1=ALU.add),
                      reads=[PS_r[pi], tpr, c_r], writes=dst_regs + [PS_r[pi]])

            def prep(mt):
                tok0 = mt * MT
                par = mt % 2
                t_r = t_r0
                ATz, RTz, Vtok, Btok, Ktok, Gam, gtok, cb, Pm, A3 = ATz2[par], RTz2[par], Vtok2[par], Btok2[par], Ktok2[par], Gam2[par], gtok2[par], cb2[par], Pm2[par], A32[par]
                ATz_r, RTz_r, Vtok_r, Btok_r, Ktok_r, Gam_r, gtok_r, cb_r, P_r, A3_r = ATz_r2[par], RTz_r2[par], Vtok_r2[par], Btok_r2[par], Ktok_r2[par], Gam_r2[par], gtok_r2[par], cb_r2[par], P_r2[par], A3_r2[par]
                for i in range(6):
                    shift_proj(i, tok0, lambda i=i: (xs[i][:, :], [xs_r[i]]))
                shift_proj(6, tok0, lambda: (tt_[0][0:64, :], [t_r[0]]))
                kb.op("act", lambda a: a.activation(out=TW[:, :], in_=tt_[0][0:64, :], func=AF.Tanh), reads=[t_r[0]], writes=[TW_r])
                shift_proj(7, tok0, lambda: (AL[:, :], [AL_r]))
                shift_proj(8, tok0, lambda: (tt_[0][:, :], [t_r[0]]))
                kb.op("act", lambda a: a.activation(out=SGL[:, :], in_=tt_[0][:, :], func=AF.Sigmoid), reads=[t_r[0]], writes=[SGL_r])
                shift_proj(9, tok0, lambda: (tt_[0][0:32, :], [t_r[0]]))
                kb.op("act", lambda a: a.activation(out=SGL2[0:32, :], in_=tt_[0][0:32, :], func=AF.Sigmoid), reads=[t_r[0]], writes=[SGL2_r])
                if l >= 1:
                    shift_proj(10, tok0, lambda: (VR[:, :], [VR_r]))
                if rs_ <= 1:
                    return
                for hp in range(2):
                    rT, kT, vT = xs[hp], xs[2 + hp], xs[4 + hp]
                    rT_r, kT_r, vT_r = xs_r[hp], xs_r[2 + hp], xs_r[4 + hp]
                    t1, t2, t3, t4, t5, t6, t7 = tt_[0:7] if hp == 0 else ttb_[0:7]
                    t_r = t_r0 if hp == 0 else tb_r
                    kb.op("pe", lambda pe: pe.matmul(PS[2][:, 0:MT], lhsT=w2[:, hp * 128:(hp + 1) * 128], rhs=TW[:, :], start=True, stop=True),
                          reads=[p_r, TW_r], writes=[PS_r[2]])
                    kb.op("act", lambda a: a.activation(out=t1[:, :], in_=PS[2][:, 0:MT], func=AF.Sigmoid, bias=ptb[:, l, 13 + hp:14 + hp]),
                          reads=[PS_r[2], c_r], writes=[t_r[0]])
                    kb.op("dve", lambda v: v.tensor_tensor_scan(out=t2[:, :], data0=rmask[:, :], data1=t1[:, :], initial=0.0,
                                                                op0=ALU.mult, op1=ALU.add), reads=[t_r[0], p_r], writes=[t_r[1]])
                    kb.op("act", lambda a: a.activation(out=Gam[:, hp, :], in_=t2[:, :], func=AF.Exp, scale=-C0), reads=[t_r[1]], writes=[Gam_r[hp]])
                    kb.op("act", lambda a: a.activation(out=t3[:, :], in_=t2[:, :], func=AF.Exp, scale=C0), reads=[t_r[1]], writes=[t_r[2]])
                    kb.op("dve", lambda v: v.tensor_tensor(out=t4[:, :], in0=t2[:, :], in1=t1[:, :], op=ALU.subtract),
                          reads=[t_r[0], t_r[1]], writes=[t_r[3]])
                    kb.op("act", lambda a: a.activation(out=t4[:, :], in_=t4[:, :], func=AF.Exp, scale=-C0), reads=[t_r[3]], writes=[t_r[3]])
                    kb.op("pe", lambda pe: pe.matmul(PS[3][:, 0:MT], lhsT=a2[:, hp * 128:(hp + 1) * 128], rhs=AL[:, :], start=True, stop=True),
                          reads=[p_r, AL_r], writes=[PS_r[3]])
                    kb.op("act", lambda a: a.activation(out=t5[:, :], in_=PS[3][:, 0:MT], func=AF.Sigmoid, bias=ptb[:, l, 15 + hp:16 + hp]),
                          reads=[PS_r[3], c_r], writes=[t_r[4]])
                    kb.op("dve", lambda v: v.tensor_scalar(out=t6[:, :], in0=kT[:, :], scalar1=ptb[:, l, 19 + hp:20 + hp], scalar2=None, op0=ALU.mult),
                          reads=[kT_r, c_r], writes=[t_r[5]])
                    kb.op("act", lambda a: a.activation(out=t7[:, :], in_=t6[:, :], func=AF.Square), reads=[t_r[5]], writes=[t_r[6]])
                    kb.op("pe", lambda pe: pe.matmul(PS[4][:, 0:MT], lhsT=blkf[:, :], rhs=t7[:, :], start=True, stop=True),
                          reads=[p_r, t_r[6]], writes=[PS_r[4]])
                    kb.op("act", lambda a: a.activation(out=t7[:, :], in_=PS[4][:, 0:MT], func=AF.Sqrt), reads=[PS_r[4], t_r[6]], writes=[t_r[6]])
                    kb.op("dve", lambda v: v.tensor_scalar(out=t7[:, :], in0=t7[:, :], scalar1=1e-12, scalar2=None, op0=ALU.max),
                          reads=[t_r[6]], writes=[t_r[6]])
                    kb.op("dve", lambda v: v.reciprocal(out=t7[:, :], in_=t7[:, :]), reads=[t_r[6]], writes=[t_r[6]])
                    kb.op("dve", lambda v: v.tensor_tensor(out=t6[:, :], in0=t6[:, :], in1=t7[:, :], op=ALU.mult),
                          reads=[t_r[5], t_r[6]], writes=[t_r[5]])
                    kb.op("dve", lambda v: v.tensor_scalar(out=t7[:, :], in0=t5[:, :], scalar1=-1.0, scalar2=ptb[:, l, 21 + hp:22 + hp],
                                                           op0=ALU.add, op1=ALU.mult), reads=[t_r[4], t_r[6], c_r], writes=[t_r[6]])
                    kb.op("dve", lambda v: v.scalar_tensor_tensor(out=t7[:, :], in0=t7[:, :], scalar=1.0, in1=kT[:, :], op0=ALU.add, op1=ALU.mult),
                          reads=[t_r[6], kT_r], writes=[t_r[6]])
                    for hh in range(2):
                        pr = slice(hh * 64, hh * 64 + 64)
                        kb.op("dve", lambda v: v.scalar_tensor_tensor(out=ATz[pr, hp, hh, :], in0=t6[pr, :], scalar=-1.0, in1=t4[pr, :],
                                                                      op0=ALU.mult, op1=ALU.mult), reads=[t_r[5], t_r[3]], writes=[ATz_r[hp]])
                        kb.op("dve", lambda v: v.tensor_tensor(out=RTz[pr, hp, hh, :], in0=rT[pr, :], in1=Gam[pr, hp, :], op=ALU.mult),
                              reads=[rT_r, Gam_r[hp]], writes=[RTz_r[hp]])
                    kb.op("dve", lambda v: v.tensor_tensor(out=t1[:, :], in0=t6[:, :], in1=t5[:, :], op=ALU.mult),
                          reads=[t_r[5], t_r[4], t_r[0]], writes=[t_r[0]])
                    kb.op("dve", lambda v: v.tensor_tensor(out=BTt[:, hp, :], in0=t1[:, :], in1=t3[:, :], op=ALU.mult),
                          reads=[t_r[0], t_r[2]], writes=[BT_r[hp]])
                    kb.op("dve", lambda v: v.tensor_tensor(out=KTt[:, hp, :], in0=t7[:, :], in1=t3[:, :], op=ALU.mult),
                          reads=[t_r[6], t_r[2]], writes=[KT_r[hp]])
                    kb.op("dve", lambda v: v.scalar_tensor_tensor(out=rkr[:, hp, :], in0=rT[:, :], scalar=ptb[:, l, 23 + hp:24 + hp], in1=t7[:, :],
                                                                  op0=ALU.mult, op1=ALU.mult), reads=[rT_r, t_r[6], c_r], writes=[rkr_r[hp]])
                    if l == 0:
                        kb.dma("sp", vfirst_d.ap()[hp, :, sq * 0 + tok0:tok0 + MT], vT[:, :], reads=[vT_r], writes=[vfirst_r[hp]])
                    else:
                        kb.op("pe", lambda pe: pe.matmul(PS[5][:, 0:MT], lhsT=v2[:, hp * 128:(hp + 1) * 128], rhs=VR[:, :], start=True, stop=True),
                              reads=[p_r, VR_r], writes=[PS_r[5]])
                        kb.op("act", lambda a: a.activation(out=t1[:, :], in_=PS[5][:, 0:MT], func=AF.Sigmoid, bias=ptb[:, l, 17 + hp:18 + hp]),
                              reads=[PS_r[5], c_r, t_r[0]], writes=[t_r[0]])
                        kb.dma("sp", t2[:, :], vfirst_d.ap()[hp, :, tok0:tok0 + MT], reads=[vfirst_r[hp], t_r[1]], writes=[t_r[1]])
                        kb.op("dve", lambda v: v.tensor_tensor(out=t2[:, :], in0=t2[:, :], in1=vT[:, :], op=ALU.subtract),
                              reads=[t_r[1], vT_r], writes=[t_r[1]])
                        kb.op("dve", lambda v: v.tensor_tensor(out=t2[:, :], in0=t2[:, :], in1=t1[:, :], op=ALU.mult),
                              reads=[t_r[1], t_r[0]], writes=[t_r[1]])
                        kb.op("dve", lambda v: v.tensor_tensor(out=vT[:, :], in0=vT[:, :], in1=t2[:, :], op=ALU.add),
                              reads=[t_r[1], vT_r], writes=[vT_r])
                    kb.op("act", lambda a: a.activation(out=vb[:, :], in_=vT[:, :], func=AF.Copy), reads=[vT_r], writes=[vb_r])
                    for src, src_r, dst, dst_r, pbi in ((vb, vb_r, Vtok, Vtok_r, 0), (None, BT_r[hp], Btok, Btok_r, 1), (None, KT_r[hp], Ktok, Ktok_r, 0)):
                        sap = (lambda c: vb[:, c * 64:(c + 1) * 64]) if src is vb else \
                              ((lambda c: BTt[:, hp, c * 64:(c + 1) * 64]) if dst is Btok else (lambda c: KTt[:, hp, c * 64:(c + 1) * 64]))

                        def fn(pe, sap=sap, pbi=pbi):
                            inst = None
                            for c in range(NCH):
                                inst = pe.transpose(out=PB[pbi][0:64, c * 128:(c + 1) * 128], in_=sap(c), identity=ident[:, :])
                            return inst
                        kb.op("pe", fn, reads=[src_r, c_r], writes=[PB_r[pbi]])
                        kb.op("dve", lambda v: v.tensor_copy(out=dst[0:64, :, hp * 128:(hp + 1) * 128],
                                                             in_=PB[pbi][0:64, 0:NCH * 128].rearrange("p (c d) -> p c d", c=NCH)),
                              reads=[PB_r[pbi]], writes=[dst_r])
                if rs_ <= 2:
                    return
                def fn(pe):
                    inst = None
                    for c in range(NCH):
                        for hp in range(2):
                            inst = pe.matmul(PS[2][0:64, c * 4 + hp * 2:c * 4 + hp * 2 + 2], lhsT=rkr[:, hp, c * 64:(c + 1) * 64], rhs=hselb[:, :],
                                             start=True, stop=True)
                    return inst
                kb.op("pe", fn, reads=rkr_r + [p_r], writes=[PS_r[2]])
                kb.op("act", lambda a: a.activation(out=cb[:, :, :], in_=PS[2][0:64, 0:NCH * 4].rearrange("p (c h) -> p c h", c=NCH), func=AF.Copy),
                      reads=[PS_r[2]], writes=[cb_r])
                for c2 in range(NCH // 2):
                    def fn(pe):
                        inst = None
                        for cc_ in range(2):
                            c = c2 * 2 + cc_
                            pe.matmul(PS[3][0:64, cc_ * 256:(cc_ + 1) * 256], lhsT=SGL[:, c * 64:(c + 1) * 64], rhs=g2a[:, :], start=True, stop=False)
                            inst = pe.matmul(PS[3][0:64, cc_ * 256:(cc_ + 1) * 256], lhsT=SGL2[:, c * 64:(c + 1) * 64], rhs=g2b[:, :], start=False, stop=True)
                        return inst
                    kb.op("pe", fn, reads=[SGL_r, SGL2_r, p_r], writes=[PS_r[3]])
                    kb.op("act", lambda a: a.activation(out=gtok[:, c2 * 2:c2 * 2 + 2, :], in_=PS[3][0:64, :].rearrange("p (c d) -> p c d", c=2), func=AF.Copy),
                          reads=[PS_r[3]], writes=[gtok_r])
                for c in range(NCH):
                    cc = slice(c * 64, (c + 1) * 64)
                    for h in range(4):
                        hp, hh = h // 2, h % 2
                        u = c * 4 + h
                        pi = 4 + (u % 2)

                        def fn(pe, pi=pi):
                            AR = ARz2[par]
                            pe.matmul(PS[pi][0:64, 0:64], lhsT=ATz[:, hp, hh, cc], rhs=BTt[:, hp, cc], start=True, stop=True)
                            pe.matmul(PS[pi][0:64, 64:192], lhsT=BTt[:, hp, cc], rhs=AR[:, hp, hh, :, cc], start=True, stop=True)
                            return pe.matmul(PS[pi][0:64, 192:320], lhsT=KTt[:, hp, cc], rhs=AR[:, hp, hh, :, cc], start=True, stop=True)
                        kb.op("pe", fn, reads=[ATz_r[hp], RTz_r[hp], BT_r[hp], KT_r[hp]], writes=[PS_r[pi]])
                        kb.op("dve", lambda v, pi=pi: v.tensor_tensor(out=MN[0][:, u, :, :], in0=PS[pi][0:64, 0:128].rearrange("p (a b) -> p a b", a=2),
                                                                      in1=amask[:, 0:2, :], op=ALU.mult), reads=[PS_r[pi], p_r], writes=[MN_r[0]])
                        kb.op("dve", lambda v, pi=pi: v.tensor_tensor(out=A3[0:64, u, :, :], in0=PS[pi][0:64, 128:320].rearrange("p (a b) -> p a b", a=3),
                                                                      in1=amask[:, 2:5, :], op=ALU.mult), reads=[PS_r[pi], p_r], writes=[A3_r])
                if rs_ <= 3:
                    return
                kb.op("dve", lambda v: v.tensor_tensor(out=Pm[:, :, :], in0=MN[0][:, :, 1, :],
                                                      in1=bass.AP(identf, 0, [[128, 64], [0, NU], [1, 64]]), op=ALU.add),
                      reads=[MN_r[0], c_r], writes=[P_r])
                cur = 0
                for lev in range(5):
                    nxt = 1 - cur
                    lastlev = (lev == 4)
                    for g4 in range(NU // 4):
                        pi = 4 + (g4 % 2)

                        def fn(pe, pi=pi):
                            inst = None
                            for q in range(4):
                                u = g4 * 4 + q
                                inst = pe.matmul(PS[pi][0:64, q * 128:q * 128 + 64], lhsT=MN[cur][:, u, 1, :], rhs=MN[cur][:, u, 0, :], start=True, stop=True)
                                if not lastlev:
                                    inst = pe.matmul(PS[pi][0:64, q * 128 + 64:q * 128 + 128], lhsT=MN[cur][:, u, 0, :], rhs=MN[cur][:, u, 1, :],
                                                     start=True, stop=True)
                            return inst
                        kb.op("pe", fn, reads=[MN_r[cur]], writes=[PS_r[pi]])
                        kb.op("act", lambda a, pi=pi: a.activation(out=MN[nxt][:, g4 * 4:g4 * 4 + 4, :, :],
                                                                   in_=PS[pi][0:64, :].rearrange("p (u a b) -> p u a b", u=4, a=2), func=AF.Copy),
                              reads=[PS_r[pi]], writes=[MN_r[nxt]])
                    for g8 in range(NU // 8):
                        pi = 2 + (g8 % 2)

                        def fn(pe, pi=pi):
                            inst = None
                            for q in range(8):
                                u = g8 * 8 + q
                                inst = pe.matmul(PS[pi][0:64, q * 64:(q + 1) * 64], lhsT=MN[nxt][:, u, 0, :], rhs=Pm[:, u, :], start=True, stop=True)
                            return inst
                        kb.op("pe", fn, reads=[MN_r[nxt], P_r], writes=[PS_r[pi]])
                        kb.op("dve", lambda v, pi=pi: v.tensor_tensor(out=Pm[:, g8 * 8:g8 * 8 + 8, :], in0=PS[pi][0:64, :].rearrange("p (u b) -> p u b", u=8),
                                                                      in1=Pm[:, g8 * 8:g8 * 8 + 8, :], op=ALU.add), reads=[PS_r[pi], P_r], writes=[P_r])
                    cur = nxt
                if rs_ <= 4:
                    return
            def chain(mt):
                tok0 = mt * MT
                par = mt % 2
                ATz, RTz, Vtok, Btok, Ktok, Gam, gtok, cb, Pm, A3 = ATz2[par], RTz2[par], Vtok2[par], Btok2[par], Ktok2[par], Gam2[par], gtok2[par], cb2[par], Pm2[par], A32[par]
                ATz_r, RTz_r, Vtok_r, Btok_r, Ktok_r, Gam_r, gtok_r, cb_r, P_r, A3_r = ATz_r2[par], RTz_r2[par], Vtok_r2[par], Btok_r2[par], Ktok_r2[par], Gam_r2[par], gtok_r2[par], cb_r2[par], P_r2[par], A3_r2[par]
                if rs_ <= 4:
                    return
                for c in range(NCH):
                    cc = slice(c * 64, (c + 1) * 64)

                    def fn(pe):
                        inst = None
                        for h in range(4):
                            hp, hh = h // 2, h % 2
                            u = c * 4 + h
                            pe.matmul(PS[0][0:64, h * 64:(h + 1) * 64], lhsT=A3[:, u, 1, :], rhs=Vtok[:, c, h * 64:(h + 1) * 64], start=True, stop=False)
                            inst = pe.matmul(PS[0][0:64, h * 64:(h + 1) * 64], lhsT=ATz[:, hp, hh, cc], rhs=Tb[:, hp, hh * 64:(hh + 1) * 64],
                                             start=False, stop=True)
                        return inst
                    kb.op("pe", fn, reads=[A3_r, Vtok_r, Tb_r] + ATz_r, writes=[PS_r[0]])
                    kb.op("act", lambda a: a.activation(out=Rs[:, :], in_=PS[0][0:64, 0:256], func=AF.Copy), reads=[PS_r[0]], writes=[Rs_r])

                    def fn(pe):
                        inst = None
                        for h in range(4):
                            u = c * 4 + h
                            inst = pe.matmul(PS[1][0:64, h * 64:(h + 1) * 64], lhsT=Pm[:, u, :], rhs=Rs[:, h * 64:(h + 1) * 64], start=True, stop=True)
                        return inst
                    kb.op("pe", fn, reads=[P_r, Rs_r], writes=[PS_r[1]])
                    kb.op("act", lambda a: a.activation(out=Ub[0:64, :], in_=PS[1][0:64, 0:256], func=AF.Copy), reads=[PS_r[1]], writes=[Ub_r])

                    def fn(pe):
                        inst = None
                        for h in range(4):
                            hp, hh = h // 2, h % 2
                            u = c * 4 + h
                            pe.matmul(PS[0][0:64, h * 64:(h + 1) * 64], lhsT=RTz[:, hp, hh, cc], rhs=Tb[:, hp, hh * 64:(hh + 1) * 64], start=True, stop=False)
                            pe.matmul(PS[0][0:64, h * 64:(h + 1) * 64], lhsT=A3[:, u, 0, :], rhs=Ub[:, h * 64:(h + 1) * 64], start=False, stop=False)
                            inst = pe.matmul(PS[0][0:64, h * 64:(h + 1) * 64], lhsT=A3[:, u, 2, :], rhs=Vtok[:, c, h * 64:(h + 1) * 64], start=False, stop=True)
                        return inst
                    kb.op("pe", fn, reads=[A3_r, Vtok_r, Tb_r, Ub_r] + RTz_r, writes=[PS_r[0]])

                    def fn(pe):
                        inst = None
                        for hp in range(2):
                            pe.matmul(PS[1][:, hp * 128:(hp + 1) * 128], lhsT=Btok[:, c, hp * 128:(hp + 1) * 128], rhs=Ub[:, hp * 128:(hp + 1) * 128],
                                      start=True, stop=False)
                            inst = pe.matmul(PS[1][:, hp * 128:(hp + 1) * 128], lhsT=Ktok[:, c, hp * 128:(hp + 1) * 128], rhs=Vtok[:, c, hp * 128:(hp + 1) * 128],
                                             start=False, stop=True)
                        return inst
                    kb.op("pe", fn, reads=[Btok_r, Ktok_r, Vtok_r, Ub_r], writes=[PS_r[1]])
                    for hp in range(2):
                        ee = Gam[:, hp, c * 64 + 63:c * 64 + 64]
                        kb.op("dve", lambda v: v.scalar_tensor_tensor(out=tmpS[:, :], in0=PS[1][:, hp * 128:(hp + 1) * 128], scalar=ee, in1=blkm[:, :],
                                                                      op0=ALU.mult, op1=ALU.mult), reads=[PS_r[1], Gam_r[hp], p_r], writes=[tmpS_r])
                        kb.op("dve", lambda v: v.scalar_tensor_tensor(out=Tblk[:, hp, :], in0=Tblk[:, hp, :], scalar=ee, in1=tmpS[:, :],
                                                                      op0=ALU.mult, op1=ALU.add), reads=[T_r, tmpS_r, Gam_r[hp]], writes=[T_r])
                    kb.op("act", lambda a: a.activation(out=Tb[:, :, :], in_=Tblk[:, :, :], func=AF.Copy), reads=[T_r], writes=[Tb_r])
                    if rs_ <= 5:
                        continue
                    head_norm_gate(PS[0][0:64, 0:256], PS_r[0], ow, ow_r, ow2, ow2_r, st, st_r, rln, p_r, RW_EPS, P=64)
                    for h in range(4):
                        kb.op("dve", lambda v: v.scalar_tensor_tensor(out=ow[0:64, h * 64:(h + 1) * 64], in0=Vtok[0:64, c, h * 64:(h + 1) * 64],
                                                                      scalar=cb[:, c, h:h + 1], in1=ow[0:64, h * 64:(h + 1) * 64],
                                                                      op0=ALU.mult, op1=ALU.add), reads=[Vtok_r, cb_r, ow_r], writes=[ow_r])
                    kb.op("dve", lambda v: v.tensor_tensor(out=ob[0:64, :], in0=ow[0:64, :], in1=gtok[:, c, :], op=ALU.mult),
                          reads=[ow_r, gtok_r], writes=[ob_r])
                    out_to_OT(ob, ob_r, 64, OT, OT_r, 2, tok0 + c * 64)
            nmt = S // MT if rs_ > 0 else 0
            if nmt:
                prep(0)
            for mt in range(nmt):
                if mt + 1 < nmt:
                    prep(mt + 1)
                chain(mt)
            kb.barrier()

    NQ, NQS, NKC, NKS, NKW, NVC, NVS, NGT = 0, 512, 1024, 1280, 1536, 1792, 1920, 2176
    GK = 1.5957691216057308

    def nsa_phase(sq, l, OT, OT_r):
        ns_ = cfg.get("nsa_stop", 99)
        with ExitStack() as es:
            A = lambda n_, s_, d_: es.enter_context(sb(n_, s_, d_))
            QT = A("nQT", [128, 4, S], BF16)
            KTz = A("nKTz", [128, 3, 2, S], BF16)
            VCz = A("nVCz", [128, 2, S], BF16)
            VS = A("nVS", [128, NT, 2, 65], BF16)
            VW = A("nVW", [128, NT, 2, 65], BF16)
            gsig = A("ngsig", [128, NT, 24], F32)
            kcmpTz = A("nkcmp", [128, 2, 128], BF16)
            vcmp = A("nvcmp", [128, 2, 97], BF16)
            QT_r, KT_r, VC_r, VS_r, VW_r, gs_r = regs(4), regs(4), regs(4), regs(NT), regs(NT), regs(NT)
            kc_r, vcm_r = Reg(), Reg()
            kb.op("pool", lambda g: g.memset(KTz[:], 0.0), writes=KT_r)
            kb.op("pool", lambda g: g.memset(VCz[:], 0.0), writes=VC_r)
            kb.op("pool", lambda g: g.memset(VS[:, :, :, 64:65], 1.0), writes=VS_r)
            kb.op("pool", lambda g: g.memset(VW[:, :, :, 64:65], 1.0), writes=VW_r)
            kb.op("pool", lambda g: g.memset(kcmpTz[:], 0.0), writes=[kc_r])
            kb.op("pool", lambda g: g.memset(vcmp[:], 0.0), writes=[vcm_r])
            with ExitStack() as es1:
                A1 = lambda n_, s_, d_: es1.enter_context(sb(n_, s_, d_))
                Wn = A1("nWn", [128, 8, 2200], BF16)
                rp = A1("nrp", [128, 2, 512], F32)
                t1 = A1("nt1", [128, 512], F32)
                t2 = A1("nt2", [128, 512], F32)
                W_r, rp_r, t1_r, t2_r = Reg(), Reg(), Reg(), Reg()
                load_w(Wn, w_nsa16.ap()[l], 2200, W_r)
                for mt in range(4):
                    tok0 = mt * 512
                    bl = slice(tok0, tok0 + 512)
                    xr = XT_r[mt * 4:mt * 4 + 4]
                    kb.dma("sp", rp[:, 0, :], c_rope.ap()[0][:, bl], writes=[rp_r])
                    kb.dma("sp", rp[:, 1, :], c_rope.ap()[1][:, bl], writes=[rp_r])
                    for i in range(4):
                        proj_fm(PS[0][:, :], Wn, NQ + i * 128, 128, tok0, 512, [W_r], xr, PS_r[0])
                        proj_fm(PS[1][:, :], Wn, NQS + i * 128, 128, tok0, 512, [W_r], xr, PS_r[1])
                        kb.op("dve", lambda v: v.tensor_tensor(out=t1[:, :], in0=PS[0][:, :], in1=rp[:, 0, :], op=ALU.mult),
                              reads=[PS_r[0], rp_r], writes=[t1_r])
                        kb.op("dve", lambda v: v.scalar_tensor_tensor(out=t2[:, :], in0=PS[1][:, :], scalar=0.125, in1=rp[:, 1, :],
                                                                      op0=ALU.mult, op1=ALU.mult), reads=[PS_r[1], rp_r], writes=[t2_r])
                        kb.op("dve", lambda v: v.scalar_tensor_tensor(out=QT[:, i, bl], in0=t1[:, :], scalar=0.125, in1=t2[:, :],
                                                                      op0=ALU.mult, op1=ALU.add), reads=[t1_r, t2_r], writes=[QT_r[mt]])
                    for ty, c0 in ((0, NKC), (1, NKS), (2, NKW)):
                        proj_fm(PS[0][:, :], Wn, c0, 128, tok0, 512, [W_r], xr, PS_r[0])
                        proj_fm(PS[1][:, :], Wn, c0 + 128, 128, tok0, 512, [W_r], xr, PS_r[1])
                        kb.op("dve", lambda v: v.tensor_tensor(out=t1[:, :], in0=PS[0][:, :], in1=rp[:, 0, :], op=ALU.mult),
                              reads=[PS_r[0], rp_r], writes=[t1_r])
                        kb.op("dve", lambda v: v.tensor_tensor(out=t2[:, :], in0=PS[1][:, :], in1=rp[:, 1, :], op=ALU.mult),
                              reads=[PS_r[1], rp_r], writes=[t2_r])
                        for g in range(2):
                            pr = slice(g * 64, g * 64 + 64)
                            kb.op("dve", lambda v: v.tensor_tensor(out=KTz[pr, ty, g, bl], in0=t1[pr, :], in1=t2[pr, :], op=ALU.add),
                                  reads=[t1_r, t2_r], writes=[KT_r[mt]])
                    proj_fm(PS[2][:, :], Wn, NVC, 128, tok0, 512, [W_r], xr, PS_r[2])
                    for g in range(2):
                        pr = slice(g * 64, g * 64 + 64)
                        kb.op("act", lambda a: a.activation(out=VCz[pr, g, bl], in_=PS[2][pr, :], func=AF.Copy),
                              reads=[PS_r[2]], writes=[VC_r[mt]])
                    for j in range(4):
                        tt = mt * 4 + j
                        proj_tm(PS[3][:, 0:256], Wn, NVS, 256, tt, [W_r], PS_r[3])
                        kb.op("act", lambda a: a.activation(out=VS[:, tt, :, 0:64], in_=PS[3][:, 0:128].rearrange("p (g d) -> p g d", g=2), func=AF.Copy),
                              reads=[PS_r[3]], writes=[VS_r[tt], PS_r[3]])
                        kb.op("act", lambda a: a.activation(out=VW[:, tt, :, 0:64], in_=PS[3][:, 128:256].rearrange("p (g d) -> p g d", g=2), func=AF.Copy),
                              reads=[PS_r[3]], writes=[VW_r[tt], PS_r[3]])
                        proj_tm(PS[4][:, 0:24], Wn, NGT, 24, tt, [W_r], PS_r[4])
                        kb.op("act", lambda a: a.activation(out=gsig[:, tt, :], in_=PS[4][:, 0:24], func=AF.Sigmoid),
                              reads=[PS_r[4]], writes=[gs_r[tt]])
                kb.barrier()
            if ns_ <= 1:
                kb.barrier()
                return
            with ExitStack() as es2:
                A2 = lambda n_, s_, d_: es2.enter_context(sb(n_, s_, d_))
                w1d = A2("nw1d", [128, 32, 256], BF16)
                w2d = A2("nw2d", [128, 2, 128], BF16)
                wv2 = A2("nwv2", [128, 2, 64], BF16)
                posz = A2("nposz", [128, 2, 32], BF16)
                hc = A2("nhc", [128, 2], F32)
                gx = A2("ngx", [128, 128], F32)
                gw = A2("ngw", [128, 128], F32)
                gh = A2("ngh", [128, 2, 128], BF16)
                w1_r, w2_r, hc_r, gx_r, gw_r, gh_r = Reg(), Reg(), Reg(), Reg(), Reg(), Reg()
                kb.op("pool", lambda g: g.memset(posz[:], 0.0), writes=[w2_r])
                kb.op("pool", lambda g: g.memset(gh[:], 0.0), writes=[gh_r])
                for kv in range(2):
                    kb.dma("pool", posz[0:64, kv, :], nsa_posT.ap()[l, kv], reads=[w2_r], writes=[w2_r])
                for half in range(2):
                    kb.dma("pool", w2d[:, :, half * 64:(half + 1) * 64], nsa_wk2.ap()[l].rearrange("(t p) n -> p t n", p=128), writes=[w2_r])
                kb.dma("pool", wv2[:, :, :], nsa_wv2.ap()[l].rearrange("(t p) n -> p t n", p=128), writes=[w2_r])
                kb.dma("pool", vcmp[:, 0, 65:97], c_ovl.ap(), reads=[vcm_r], writes=[vcm_r])
                kb.dma("pool", vcmp[:, 1, 65:97], c_ovl.ap(), reads=[vcm_r], writes=[vcm_r])
                kb.op("pool", lambda g: g.memset(vcmp[0:127, :, 64:65], 1.0), reads=[vcm_r], writes=[vcm_r])
                for kv, w1src in ((0, wk1_16), (1, wv1_16)):
                    src3 = w1src.ap()[l].rearrange("(l d) n -> d l n", d=64)
                    for half in range(2):
                        for l4 in range(8):
                            kb.dma("sp", w1d[half * 64:(half + 1) * 64, l4 * 4:(l4 + 1) * 4, :], src3[:, l4 * 4:(l4 + 1) * 4, :],
                                   reads=[w16_r], writes=[w1_r])
                    srcz = (lambda g: KTz[:, 0, g, :]) if kv == 0 else (lambda g: VCz[:, g, :])
                    src_regs = KT_r if kv == 0 else VC_r
                    for hf in range(2):
                        def fn(pe):
                            inst = None
                            for ll in range(32):
                                inst = pe.matmul(PS[0][:, 0:1], lhsT=w1d[:, ll, hf * 128:(hf + 1) * 128], rhs=posz[:, kv, ll:ll + 1],
                                                 start=(ll == 0), stop=(ll == 31))
                            return inst
                        kb.op("pe", fn, reads=[w1_r, w2_r], writes=[PS_r[0]])
                        kb.op("act", lambda a: a.activation(out=hc[:, hf:hf + 1], in_=PS[0][:, 0:1], func=AF.Copy), reads=[PS_r[0]], writes=[hc_r])
                    for g in range(2):
                        for hf in range(2):
                            def fn(pe):
                                inst = None
                                for ll in range(32):
                                    rhs = bass.AP(srcz(g).tensor, srcz(g).offset + ll, [srcz(g).ap[0], [16, 127]])
                                    inst = pe.matmul(PS[1][:, 0:127], lhsT=w1d[:, ll, hf * 128:(hf + 1) * 128], rhs=rhs,
                                                     start=(ll == 0), stop=(ll == 31))
                                return inst
                            kb.op("pe", fn, reads=[w1_r] + src_regs, writes=[PS_r[1]])
                            kb.op("act", lambda a: a.activation(out=gx[:, 0:127], in_=PS[1][:, 0:127], func=AF.Identity, bias=hc[:, hf:hf + 1], scale=1.0),
                                  reads=[PS_r[1], hc_r], writes=[gx_r])
                            kb.op("dve", lambda v: v.tensor_tensor(out=gw[:, 0:127], in0=gx[:, 0:127], in1=gx[:, 0:127], op=ALU.mult),
                                  reads=[gx_r], writes=[gw_r])
                            kb.op("dve", lambda v: v.tensor_scalar(out=gw[:, 0:127], in0=gw[:, 0:127], scalar1=0.044715, scalar2=1.0,
                                                                   op0=ALU.mult, op1=ALU.add), reads=[gw_r], writes=[gw_r])
                            kb.op("dve", lambda v: v.tensor_tensor(out=gw[:, 0:127], in0=gw[:, 0:127], in1=gx[:, 0:127], op=ALU.mult),
                                  reads=[gw_r, gx_r], writes=[gw_r])
                            kb.op("act", lambda a: a.activation(out=gw[:, 0:127], in_=gw[:, 0:127], func=AF.Sigmoid, scale=GK), reads=[gw_r], writes=[gw_r])
                            kb.op("dve", lambda v: v.tensor_tensor(out=gh[:, hf, 0:127], in0=gx[:, 0:127], in1=gw[:, 0:127], op=ALU.mult),
                                  reads=[gw_r, gx_r], writes=[gh_r])
                        if kv == 0:
                            def fn(pe):
                                pe.matmul(PS[2][:, 0:127], lhsT=w2d[:, 0, :], rhs=gh[:, 0, 0:127], start=True, stop=False)
                                return pe.matmul(PS[2][:, 0:127], lhsT=w2d[:, 1, :], rhs=gh[:, 1, 0:127], start=False, stop=True)
                            kb.op("pe", fn, reads=[w2_r, gh_r], writes=[PS_r[2]])
                            pr = slice(g * 64, g * 64 + 64)
                            kb.op("act", lambda a: a.activation(out=kcmpTz[pr, g, 0:127], in_=PS[2][pr, 0:127], func=AF.Copy),
                                  reads=[PS_r[2], kc_r], writes=[kc_r])
                        else:
                            def fn(pe):
                                pe.matmul(PS[2][:, 0:64], lhsT=gh[:, 0, :], rhs=wv2[:, 0, :], start=True, stop=False)
                                return pe.matmul(PS[2][:, 0:64], lhsT=gh[:, 1, :], rhs=wv2[:, 1, :], start=False, stop=True)
                            kb.op("pe", fn, reads=[w2_r, gh_r], writes=[PS_r[2]])
                            kb.op("act", lambda a: a.activation(out=vcmp[0:127, g, 0:64], in_=PS[2][0:127, 0:64], func=AF.Copy),
                                  reads=[PS_r[2], vcm_r], writes=[vcm_r])
                kb.barrier()
            if ns_ <= 2:
                kb.barrier()
                return
            selbT = A("nselbT", [128, 2, S], BF16)
            onehot = A("nonehot", [128, S], BF16)
            cmask = A("ncmask", [128, S], BF16)
            selm = A("nselm", [128, 2, NT, 32], F32)
            wmask = A("nwmask", [128, 128], F32)
            PT = [A("nPT%d" % i, [128, 640], BF16) for i in range(2)]
            ex = A("nex", [128, 512], F32)
            ocs = A("nocs", [128, 4, 97], F32)
            ONSA2 = [A("nONSA%d" % i_, [128, 4, 512], F32) for i_ in range(2)]
            impb2 = [A("nimp%d" % i_, [128, 4, 2, 32], F32) for i_ in range(2)]
            sc = A("nsc", [128, 32], F32)
            cm3 = A("ncm3", [128, 32, 32], F32)
            sb16 = A("nsb16", [128, 32], BF16)
            dn = A("ndn", [128, 16], F32)
            obf = A("nobf", [128, 512], BF16)
            k_r, sel_r, PT_r, ex_r, ocs_r, sc_r, cm3_r, sb16_r, dn_r, obf_r = \
                Reg(), regs(4), regs(2), Reg(), Reg(), Reg(), Reg(), Reg(), Reg(), Reg()
            on_r2, imp_r2 = [regs(4), regs(4)], regs(2)
            kb.op("pool", lambda g: g.memset(selbT[:], 0.0), writes=sel_r)
            kb.op("pool", lambda g: g.memset(onehot[:], 0.0), writes=[k_r])
            kb.dma("pool", onehot[0:32, :], c_onehot.ap(), reads=[k_r], writes=[k_r])
            kb.dma("pool", cmask[:, :], c_cmpmask.ap(), writes=[k_r])
            for m_ in range(2):
                kb.dma("sp", selm[:, m_, :, :], c_selm.ap()[m_], writes=[k_r])
            kb.op("dve", lambda v: v.tensor_scalar(out=wmask[:, :], in0=caus[:, 0, :], scalar1=-1.0, scalar2=1.0, op0=ALU.mult, op1=ALU.add),
                  reads=[c_r], writes=[k_r])

            def branch_finish(acc_ps, acc_r, W, nq, h, b, tts, first, ONSA, on_r):
                kb.op("act", lambda a: a.activation(out=ocs[:, 0:nq, 0:W], in_=acc_ps.rearrange("p (q w) -> p q w", q=nq), func=AF.Copy),
                      reads=[acc_r], writes=[ocs_r, acc_r])
                kb.op("dve", lambda v: v.tensor_scalar(out=dn[:, 0:nq], in0=ocs[:, 0:nq, 64], scalar1=1e-30, scalar2=None, op0=ALU.max),
                      reads=[ocs_r], writes=[dn_r])
                kb.op("dve", lambda v: v.reciprocal(out=dn[:, 0:nq], in_=dn[:, 0:nq]), reads=[dn_r], writes=[dn_r])
                kb.op("dve", lambda v: v.tensor_tensor(out=dn[:, 8:8 + nq], in0=dn[:, 0:nq], in1=gsig[:, tts[0]:tts[0] + nq, h * 3 + b], op=ALU.mult),
                      reads=[dn_r] + gs_r[tts[0]:tts[0] + nq], writes=[dn_r])
                for qi in range(nq):
                    tl = tts[qi] % 4
                    if first:
                        kb.op("dve", lambda v: v.tensor_scalar(out=ONSA[:, tl, h * 64:(h + 1) * 64], in0=ocs[:, qi, 0:64], scalar1=dn[:, 8 + qi:9 + qi],
                                                               scalar2=None, op0=ALU.mult), reads=[ocs_r, dn_r], writes=[on_r[tl]])
                    else:
                        kb.op("dve", lambda v: v.scalar_tensor_tensor(out=ONSA[:, tl, h * 64:(h + 1) * 64], in0=ocs[:, qi, 0:64], scalar=dn[:, 8 + qi:9 + qi],
                                                                      in1=ONSA[:, tl, h * 64:(h + 1) * 64], op0=ALU.mult, op1=ALU.add),
                              reads=[ocs_r, dn_r, on_r[tl]], writes=[on_r[tl]])

            def nsa_a(qb):
                qc = slice(qb * 512, (qb + 1) * 512)
                tts = list(range(qb * 4, qb * 4 + 4))
                ONSA, on_r, impb, imp_r = ONSA2[qb % 2], on_r2[qb % 2], impb2[qb % 2], imp_r2[qb % 2]
                for h in range(8):
                    g, i = h // 4, h % 4
                    pi = h % 2
                    kb.op("pe", lambda pe: pe.matmul(PS[pi][:, :], lhsT=kcmpTz[:, g, :], rhs=QT[:, i, qc], start=True, stop=True),
                          reads=[kc_r, QT_r[qb]], writes=[PS_r[pi]])
                    kb.op("act", lambda a: a.activation(out=ex[:, :], in_=PS[pi][:, :], func=AF.Exp), reads=[PS_r[pi]], writes=[ex_r])
                    kb.op("dve", lambda v: v.tensor_tensor(out=PT[pi][:, 0:512], in0=ex[:, :], in1=cmask[:, qc], op=ALU.mult),
                          reads=[ex_r, k_r], writes=[PT_r[pi]])
                    ai = 2 + (h % 2)

                    def fn(pe):
                        inst = None
                        for q in range(4):
                            inst = pe.matmul(PS[ai][:, q * 97:(q + 1) * 97], lhsT=PT[pi][:, q * 128:(q + 1) * 128], rhs=vcmp[:, g, :], start=True, stop=True)
                        return inst
                    kb.op("pe", fn, reads=[PT_r[pi], vcm_r], writes=[PS_r[ai]])
                    branch_finish(PS[ai][:, 0:388], PS_r[ai], 97, 4, h, 0, tts, True, ONSA, on_r)
                    for q in range(4):
                        if i == 0:
                            kb.op("dve", lambda v: v.tensor_scalar(out=impb[:, q, g, :], in0=ocs[:, q, 65:97], scalar1=dn[:, q:q + 1], scalar2=None, op0=ALU.mult),
                                  reads=[ocs_r, dn_r], writes=[imp_r])
                        else:
                            kb.op("dve", lambda v: v.scalar_tensor_tensor(out=impb[:, q, g, :], in0=ocs[:, q, 65:97], scalar=dn[:, q:q + 1], in1=impb[:, q, g, :],
                                                                          op0=ALU.mult, op1=ALU.add), reads=[ocs_r, dn_r, imp_r], writes=[imp_r])
            def nsa_b(qb):
                qc = slice(qb * 512, (qb + 1) * 512)
                tts = list(range(qb * 4, qb * 4 + 4))
                ONSA, on_r, impb, imp_r = ONSA2[qb % 2], on_r2[qb % 2], impb2[qb % 2], imp_r2[qb % 2]
                for q in range(4):
                    tt = tts[q]
                    for g in range(2):
                        kb.op("dve", lambda v: v.tensor_tensor(out=sc[:, :], in0=impb[:, q, g, :], in1=selm[:, 0, tt, :], op=ALU.mult),
                              reads=[imp_r, k_r], writes=[sc_r])
                        kb.op("dve", lambda v: v.tensor_tensor(out=sc[:, :], in0=sc[:, :], in1=selm[:, 1, tt, :], op=ALU.add),
                              reads=[sc_r, k_r], writes=[sc_r])
                        kb.op("dve", lambda v: v.tensor_tensor(out=cm3[:, :, :], in0=bass.AP(sc, 0, [[32, 128], [0, 32], [1, 32]]),
                                                              in1=bass.AP(sc, 0, [[32, 128], [1, 32], [0, 32]]), op=ALU.is_gt),
                              reads=[sc_r], writes=[cm3_r])
                        kb.op("dve", lambda v: v.reduce_sum(out=sc[:, :], in_=cm3[:, :, :], axis=AX.X), reads=[cm3_r, sc_r], writes=[sc_r])
                        kb.op("dve", lambda v: v.tensor_scalar(out=sb16[:, :], in0=sc[:, :], scalar1=15.5, scalar2=-30000.0, op0=ALU.is_gt, op1=ALU.mult),
                              reads=[sc_r], writes=[sb16_r])
                        kb.op("pe", lambda pe: pe.transpose(out=PB[0][0:32, 0:128], in_=sb16[:, :], identity=ident[:, :]),
                              reads=[sb16_r, c_r], writes=[PB_r[0]])
                        kb.op("act", lambda a: a.activation(out=selbT[0:32, g, tt * 128:(tt + 1) * 128], in_=PB[0][0:32, 0:128], func=AF.Copy),
                              reads=[PB_r[0]], writes=[sel_r[qb]])
            def nsa_c(qb):
                qc = slice(qb * 512, (qb + 1) * 512)
                tts = list(range(qb * 4, qb * 4 + 4))
                ONSA, on_r, impb, imp_r = ONSA2[qb % 2], on_r2[qb % 2], impb2[qb % 2], imp_r2[qb % 2]
                for h in range(8):
                    g, i = h // 4, h % 4
                    nkt = 4 * qb + 4
                    for kt in range(nkt):
                        kc_ = slice(kt * 128, (kt + 1) * 128)
                        pi = kt % 2

                        def fn(pe):
                            pe.matmul(PS[pi][:, :], lhsT=KTz[:, 1, g, kc_], rhs=QT[:, i, qc], start=True, stop=False)
                            return pe.matmul(PS[pi][:, :], lhsT=onehot[:, kc_], rhs=selbT[:, g, qc], start=False, stop=True)
                        kb.op("pe", fn, reads=[KT_r[kt // 4], QT_r[qb], k_r, sel_r[qb]], writes=[PS_r[pi]])
                        kb.op("act", lambda a: a.activation(out=PT[pi][:, 0:512], in_=PS[pi][:, :], func=AF.Exp), reads=[PS_r[pi]], writes=[PT_r[pi]])
                        if kt >= 4 * qb:
                            ql = kt - 4 * qb
                            kb.op("dve", lambda v: v.tensor_tensor(out=PT[pi][:, ql * 128:(ql + 1) * 128], in0=PT[pi][:, ql * 128:(ql + 1) * 128],
                                                                  in1=caus[:, 0, :], op=ALU.mult), reads=[PT_r[pi], c_r], writes=[PT_r[pi]])
                        for q in range(4):
                            qt = 4 * qb + q
                            if qt < kt:
                                continue
                            kb.op("pe", lambda pe: pe.matmul(PS[2 + q][:, 0:65], lhsT=PT[pi][:, q * 128:(q + 1) * 128], rhs=VS[:, kt, g, :],
                                                             start=(kt == 0), stop=(kt == qt)), reads=[PT_r[pi], VS_r[kt]], writes=[PS_r[2 + q]])
                    for q in range(4):
                        branch_finish(PS[2 + q][:, 0:65], PS_r[2 + q], 65, 1, h, 1, [tts[q]], False, ONSA, on_r)
            def nsa_d(qb):
                qc = slice(qb * 512, (qb + 1) * 512)
                tts = list(range(qb * 4, qb * 4 + 4))
                ONSA, on_r, impb, imp_r = ONSA2[qb % 2], on_r2[qb % 2], impb2[qb % 2], imp_r2[qb % 2]
                for h in range(8):
                    g, i = h // 4, h % 4
                    for q in range(4):
                        qt = 4 * qb + q
                        qcs = slice(qt * 128, (qt + 1) * 128)
                        kts = [kt for kt in range(qt - 4, qt + 1) if kt >= 0]
                        pi = q % 2
                        main = [kt for kt in kts if kt >= qt - 3]

                        def fn(pe):
                            inst = None
                            for n_, kt in enumerate(main):
                                inst = pe.matmul(PS[pi][:, n_ * 128:(n_ + 1) * 128], lhsT=KTz[:, 2, g, kt * 128:(kt + 1) * 128], rhs=QT[:, i, qcs],
                                                 start=True, stop=True)
                            return inst
                        kb.op("pe", fn, reads=KT_r + [QT_r[qb]], writes=[PS_r[pi]])
                        nm = len(main)
                        kb.op("act", lambda a: a.activation(out=PT[pi][:, 0:nm * 128], in_=PS[pi][:, 0:nm * 128], func=AF.Exp), reads=[PS_r[pi]], writes=[PT_r[pi]])
                        kb.op("dve", lambda v: v.tensor_tensor(out=PT[pi][:, (nm - 1) * 128:nm * 128], in0=PT[pi][:, (nm - 1) * 128:nm * 128],
                                                              in1=caus[:, 0, :], op=ALU.mult), reads=[PT_r[pi], c_r], writes=[PT_r[pi]])
                        tail = (qt - 4 >= 0)
                        if tail:
                            kt = qt - 4
                            kb.op("pe", lambda pe: pe.matmul(PS[2 + pi][:, 0:128], lhsT=KTz[:, 2, g, kt * 128:(kt + 1) * 128], rhs=QT[:, i, qcs],
                                                             start=True, stop=True), reads=KT_r + [QT_r[qb]], writes=[PS_r[2 + pi]])
                            kb.op("act", lambda a: a.activation(out=ex[:, 0:128], in_=PS[2 + pi][:, 0:128], func=AF.Exp), reads=[PS_r[2 + pi]], writes=[ex_r])
                            kb.op("dve", lambda v: v.tensor_tensor(out=PT[pi][:, 512:640], in0=ex[:, 0:128], in1=wmask[:, :], op=ALU.mult),
                                  reads=[ex_r, k_r, PT_r[pi]], writes=[PT_r[pi]])

                        def fn(pe):
                            inst = None
                            seq_ = [(n_, kt) for n_, kt in enumerate(main)] + ([(4, qt - 4)] if tail else [])
                            for idx, (n_, kt) in enumerate(seq_):
                                inst = pe.matmul(PS[4][:, q * 65:(q + 1) * 65], lhsT=PT[pi][:, n_ * 128:(n_ + 1) * 128], rhs=VW[:, kt, g, :],
                                                 start=(idx == 0), stop=(idx == len(seq_) - 1))
                            return inst
                        kb.op("pe", fn, reads=[PT_r[pi]] + VW_r[max(0, qt - 4):qt + 1], writes=[PS_r[4]])
                    branch_finish(PS[4][:, 0:260], PS_r[4], 65, 4, h, 2, tts, False, ONSA, on_r)
            def nsa_e(qb):
                qc = slice(qb * 512, (qb + 1) * 512)
                tts = list(range(qb * 4, qb * 4 + 4))
                ONSA, on_r, impb, imp_r = ONSA2[qb % 2], on_r2[qb % 2], impb2[qb % 2], imp_r2[qb % 2]
                for q in range(4):
                    tt = tts[q]
                    kb.op("act", lambda a: a.activation(out=obf[:, :], in_=ONSA[:, q, :], func=AF.Copy), reads=[on_r[q]], writes=[obf_r])
                    for half in range(2):
                        out_to_OT(obf[:, half * 256:(half + 1) * 256], obf_r, 128, OT, OT_r, 4 + 2 * half, tt * 128)
            for qb in range(4):
                if qb == 0:
                    nsa_a(0)
                if qb + 1 < 4:
                    nsa_a(qb + 1)
                if ns_ > 3:
                    nsa_b(qb)
                if ns_ > 5:
                    nsa_d(qb)
                if ns_ > 4:
                    nsa_c(qb)
                if ns_ > 6:
                    nsa_e(qb)
            kb.barrier()

    for sq in range(NSEQ):
        for l in range(NLAY):
            if l == 0:
                with sb("xstage", [128, 2, D], F32) as xst:
                    xst_r = regs(2)
                    for tt in range(NT):
                        i = tt % 2
                        kb.dma("sp", xst[:, i, :], x_d.ap()[sq, tt * 128:(tt + 1) * 128, :], writes=[xst_r[i]])
                        make_xT(xst[:, i, :], xst_r[i], tt)
                    kb.barrier()
            if cfg.get("stop") == "xt":
                continue
            res_src = (lambda tt: x_d.ap()[sq, tt * 128:(tt + 1) * 128, :]) if l == 0 else \
                      (lambda tt: xres[1].ap()[tt * 128:(tt + 1) * 128, :])
            res_regs = None if l == 0 else xres_r[1]

            with sb("OT", [128, 8, S], BF16) as OT:
                OT_r = regs(NT)
                if inject_O:
                    for k in range(8):
                        kb.dma("pool", OT[:, k, :], dbg_OT.ap()[:, k, :], writes=OT_r)
                if "gla" in mixers:
                    gla_phase(sq, l, OT, OT_r)
                if "rwkv" in mixers:
                    rwkv_phase(sq, l, OT, OT_r)
                if "nsa" in mixers:
                    nsa_phase(sq, l, OT, OT_r)
                if "OT" in dump_d:
                    for k in range(8):
                        with sb("otd", [128, S], F32) as otd:
                            r_ = Reg()
                            kb.op("act", lambda a: a.activation(out=otd[:], in_=OT[:, k, :], func=AF.Copy),
                                  reads=OT_r, writes=[r_])
                            dump("OT", otd[:], r_, idx=k)
                            kb.barrier()
                if cfg.get("stop") == "outproj0":
                    continue
                with sb("Wo", [128, 8, D], BF16) as Wo, \
                        sb("ln1", [128, 2, D], F32) as ln1, \
                        sb("xs1", [128, 2, D], F32) as xs1, \
                        sb("st1", [128, 2, 32], F32) as st1:
                    Wo_r = Reg()
                    ln_r = Reg()
                    xs_r = regs(2)
                    st_r = regs(2)
                    load_w(Wo, w_out16.ap()[l], D, Wo_r)
                    kb.dma("sp", ln1[:, 0, :], bcast_rows(rowtab, l * 5120 + 1024, D), writes=[ln_r])
                    kb.dma("sp", ln1[:, 1, :], bcast_rows(rowtab, l * 5120 + 2048, D), writes=[ln_r])
                    for tt in range(NT):
                        i = tt % 2
                        kb.dma("sp", xs1[:, i, :], res_src(tt), reads=([res_regs[tt]] if res_regs else []), writes=[xs_r[i]])
                        for hf in range(2):
                            def fn(pe, hf=hf):
                                inst = None
                                for k in range(8):
                                    inst = pe.matmul(PS[hf][:, :], lhsT=OT[:, k, tt * 128:(tt + 1) * 128],
                                                     rhs=Wo[:, k, hf * 512:(hf + 1) * 512], start=(k == 0), stop=(k == 7))
                                return inst
                            kb.op("pe", fn, reads=[OT_r[tt], Wo_r], writes=[PS_r[hf]])
                            kb.op("dve", lambda v, hf=hf: v.scalar_tensor_tensor(
                                out=xs1[:, i, hf * 512:(hf + 1) * 512], in0=xs1[:, i, hf * 512:(hf + 1) * 512], scalar=ALPHA,
                                in1=PS[hf][:, :], op0=ALU.mult, op1=ALU.add), reads=[PS_r[hf], xs_r[i]], writes=[xs_r[i]])
                        layer_norm(xs1[:, i, :], xs_r[i], ln1[:, 0, :], ln1[:, 1, :], ln_r, st1[:, i, :], st_r[i])
                        kb.dma("sp", xres[0].ap()[tt * 128:(tt + 1) * 128, :], xs1[:, i, :], reads=[xs_r[i]], writes=[xres_r[0][tt]])
                        make_xT(xs1[:, i, :], xs_r[i], tt)
                        if "x1" in dump_d and sq == 0 and l == cfg.get("dump_layer", 0):
                            dump("x1", xs1[:, i, :], xs_r[i], idx=tt)
                    kb.barrier()
            if cfg.get("stop") in ("outproj", "outproj0"):
                continue
            with sb("aT", [128, NFC, 1024], BF16) as aT, \
                    sb("Wd", [128, NFC, D], BF16) as Wd, \
                    sb("Wgu", [128, 2, 2, 8, 512], BF16) as Wgu, \
                    sb("sg", [128, 2, 512], F32) as sg, \
                    sb("ln2", [128, 2, D], F32) as ln2, \
                    sb("xs2", [128, 2, D], F32) as xs2, \
                    sb("st2", [128, 2, 32], F32) as st2:
                Wd_r = Reg()
                ln_r = Reg()
                aT_r = regs(NFC)
                Wgu_r = regs(2)
                sg_r = regs(2)
                xs_r = regs(2)
                st_r = regs(2)
                kb.dma("sp", ln2[:, 0, :], bcast_rows(rowtab, l * 5120 + 3072, D), writes=[ln_r])
                kb.dma("sp", ln2[:, 1, :], bcast_rows(rowtab, l * 5120 + 4096, D), writes=[ln_r])
                for k in range(NFC):
                    for c0 in (0, 512):
                        kb.dma("sp", Wd[:, k, c0:c0 + 512], w_down16.ap()[l, k * 128:(k + 1) * 128, c0:c0 + 512], reads=[w16_r], writes=[Wd_r])
                last = (l == NLAY - 1)
                for mt in range(2):
                    tok0 = mt * 1024
                    for hc in range(NFC):
                        cg, ci = hc // 4, hc % 4
                        wi = cg % 2
                        if ci == 0:
                            ncol = min(512, FF - cg * 512)
                            for gu, wsrc in ((0, w_gate16), (1, w_up16)):
                                for k in range(8):
                                    kb.dma("sp", Wgu[:, wi, gu, k, 0:ncol],
                                           wsrc.ap()[l, k * 128:(k + 1) * 128, cg * 512:cg * 512 + ncol],
                                           reads=[w16_r], writes=[Wgu_r[wi]])
                        for blk in range(2):
                            t0 = tok0 + blk * 512
                            pg, pu = (0, 1) if blk == 0 else (2, 3)
                            for gu, pi in ((0, pg), (1, pu)):
                                def fn(pe, gu=gu, pi=pi):
                                    inst = None
                                    for k in range(8):
                                        inst = pe.matmul(PS[pi][:, :], lhsT=Wgu[:, wi, gu, k, ci * 128:(ci + 1) * 128],
                                                         rhs=XT[:, k, t0:t0 + 512], start=(k == 0), stop=(k == 7))
                                    return inst
                                kb.op("pe", fn, reads=[Wgu_r[wi]] + XT_r[t0 // 128:t0 // 128 + 4], writes=[PS_r[pi]])
                            kb.op("act", lambda a: a.activation(out=sg[:, blk, :], in_=PS[pg][:, :], func=AF.Silu),
                                  reads=[PS_r[pg]], writes=[sg_r[blk]])
                            kb.op("dve", lambda v: v.tensor_tensor(out=aT[:, hc, blk * 512:(blk + 1) * 512], in0=sg[:, blk, :],
                                                                  in1=PS[pu][:, :], op=ALU.mult),
                                  reads=[sg_r[blk], PS_r[pu]], writes=[aT_r[hc]])
                    for t8 in range(8):
                        tt = mt * 8 + t8
                        i = tt % 2
                        kb.dma("sp", xs2[:, i, :], xres[0].ap()[tt * 128:(tt + 1) * 128, :], reads=[xres_r[0][tt]], writes=[xs_r[i]])
                        for hf in range(2):
                            pi = 4 + hf

                            def fn(pe, hf=hf, pi=pi):
                                inst = None
                                for k in range(NFC):
                                    inst = pe.matmul(PS[pi][:, :], lhsT=aT[:, k, t8 * 128:(t8 + 1) * 128],
                                                     rhs=Wd[:, k, hf * 512:(hf + 1) * 512], start=(k == 0), stop=(k == NFC - 1))
                                return inst
                            kb.op("pe", fn, reads=aT_r + [Wd_r], writes=[PS_r[pi]])
                            kb.op("dve", lambda v, hf=hf, pi=pi: v.scalar_tensor_tensor(
                                out=xs2[:, i, hf * 512:(hf + 1) * 512], in0=xs2[:, i, hf * 512:(hf + 1) * 512], scalar=ALPHA,
                                in1=PS[pi][:, :], op0=ALU.mult, op1=ALU.add), reads=[PS_r[pi], xs_r[i]], writes=[xs_r[i]])
                        layer_norm(xs2[:, i, :], xs_r[i], ln2[:, 0, :], ln2[:, 1, :], ln_r, st2[:, i, :], st_r[i])
                        if last:
                            kb.dma("sp", out_d.ap()[sq, tt * 128:(tt + 1) * 128, :], xs2[:, i, :], reads=[xs_r[i]])
                        else:
                            kb.dma("sp", xres[1].ap()[tt * 128:(tt + 1) * 128, :], xs2[:, i, :], reads=[xs_r[i]], writes=[xres_r[1][tt]])
                            make_xT(xs2[:, i, :], xs_r[i], tt)
                        if "x2" in dump_d and sq == 0 and l == cfg.get("dump_layer", 0):
                            dump("x2", xs2[:, i, :], xs_r[i], idx=tt)
                    if not last:
                        pass
                kb.barrier()
    kb.finish()
    return kb


def host_consts():
    c = {}
    c["c_ident"] = np.eye(128, dtype=np.float32)
    s = np.arange(128)
    c["c_caus"] = (s[:, None] <= s[None, :]).astype(np.float32)
    half = 32
    inv = (10000.0 ** (-np.arange(half, dtype=np.float32) / half)).astype(np.float32)
    ang = (np.arange(S, dtype=np.float32)[:, None] * inv[None, :]).astype(np.float32)
    cos = np.cos(ang).astype(np.float32).T
    sin = np.sin(ang).astype(np.float32).T
    cosT = np.concatenate([cos, cos, cos, cos], 0)
    sinT = np.concatenate([-sin, sin, -sin, sin], 0)
    c["c_rope"] = np.stack([cosT, sinT]).astype(np.float32)
    cc = np.arange(128)
    t = np.arange(S)
    c["c_cmpmask"] = ((16 * cc[:, None] + 31 <= t[None, :]) & (cc[:, None] < 127)).astype(np.float32)
    j = np.arange(32)
    c["c_onehot"] = ((t[None, :] // 64) == j[:, None]).astype(np.float32)
    cur = t // 64
    forced = (j[None, :] == 0) | (j[None, :] == cur[:, None]) | (j[None, :] == cur[:, None] - 1)
    future = j[None, :] > cur[:, None]
    m1 = (~forced & ~future).astype(np.float32)
    m2 = np.where(forced, 1e9, np.where(future, -1e9, 0.0)).astype(np.float32)
    selm = np.stack([m1, m2])
    c["c_selm"] = np.ascontiguousarray(selm.reshape(2, NT, 128, 32).transpose(0, 2, 1, 3))
    c0 = np.arange(127) * 16
    s0 = np.arange(32) * 64
    lo = np.maximum(c0[:, None], s0[None, :])
    hi = np.minimum(c0[:, None] + 32, s0[None, :] + 64)
    ov = np.zeros((128, 32), np.float32)
    ov[:127] = np.maximum(hi - lo, 0) / 16
    c["c_ovl"] = ov
    blk = np.zeros((128, 128), np.float32)
    blk[:64, :64] = 1
    blk[64:, 64:] = 1
    c["c_blk"] = blk
    hs = np.zeros((128, 2), np.float32)
    hs[:64, 0] = 1
    hs[64:, 1] = 1
    c["c_hsel"] = hs
    i64 = np.arange(64)
    lo_strict = (i64[None, :] < i64[:, None]).astype(np.float32)
    up_strict = (i64[:, None] < i64[None, :]).astype(np.float32)
    up_incl = (i64[:, None] <= i64[None, :]).astype(np.float32)
    c["c_rwmask"] = np.ascontiguousarray(np.stack([lo_strict, up_strict, up_incl, up_strict, up_incl], axis=1))
    return c


def host_layout(inp):
    d = {}
    f = lambda a: np.ascontiguousarray(np.asarray(a, dtype=np.float32))
    for k in ("w_in", "w_in_vres", "gla_w_a2", "rwkv_w2", "rwkv_a2", "rwkv_v2", "rwkv_g2", "nsa_wk1", "nsa_wk2",
              "nsa_wv1", "nsa_wv2", "w_out", "ffn_w_gate", "ffn_w_up", "ffn_w_down"):
        d[k] = f(inp[k])
    base = 2096
    sw = lambda b: list(range(b + 32, b + 64)) + list(range(b, b + 32))
    pl = lambda b: list(range(b, b + 64))
    cols = []
    for i in range(4):
        cols += pl(base + i * 64) + pl(base + (4 + i) * 64)
    for i in range(4):
        cols += sw(base + i * 64) + sw(base + (4 + i) * 64)
    for c0 in (512, 768, 1024):
        cols += pl(base + c0) + pl(base + c0 + 64)
        cols += sw(base + c0) + sw(base + c0 + 64)
    cols += list(range(base + 640, base + 768)) + list(range(base + 896, base + 1024)) + list(range(base + 1152, base + 1280))
    cols += list(range(base + 1280, base + 1304))
    assert len(cols) == 2200
    d["w_nsa"] = f(np.asarray(inp["w_in"])[:, :, cols])
    d["nsa_posT"] = f(np.stack([np.asarray(inp["nsa_pos_k"]).transpose(0, 2, 1),
                                np.asarray(inp["nsa_pos_v"]).transpose(0, 2, 1)], axis=1))
    pt = np.zeros((L, 128, 32), np.float32)
    mu = np.asarray(inp["rwkv_mu"])
    rwt = [(0, 128), (128, 128), (256, 128), (384, 128), (512, 128), (640, 128), (768, 64), (832, 64), (896, 128), (1024, 32)]
    for l in range(L):
        pt[l, :, 0:2] = np.asarray(inp["gla_b_a"])[l].reshape(2, 128).T
        for i, (c0, n) in enumerate(rwt):
            pt[l, :n, 2 + i] = mu[l, c0:c0 + n]
        if l >= 1:
            pt[l, :32, 12] = np.asarray(inp["rwkv_mu_vres"])[l - 1]
            pt[l, :, 17:19] = np.asarray(inp["rwkv_v0"])[l - 1].reshape(2, 128).T
        pt[l, :, 13:15] = np.asarray(inp["rwkv_w0"])[l].reshape(2, 128).T
        pt[l, :, 15:17] = np.asarray(inp["rwkv_a0"])[l].reshape(2, 128).T
        pt[l, :, 19:21] = np.asarray(inp["rwkv_k_k"])[l].reshape(2, 128).T
        pt[l, :, 21:23] = np.asarray(inp["rwkv_k_a"])[l].reshape(2, 128).T
        pt[l, :, 23:25] = np.asarray(inp["rwkv_r_k"])[l].reshape(2, 128).T
    d["ptab"] = pt
    rt = np.zeros((L, 5120), np.float32)
    for l in range(L):
        rt[l, 0:256] = np.asarray(inp["gla_ln_w"])[l]
        rt[l, 256:512] = np.asarray(inp["gla_ln_b"])[l]
        rt[l, 512:768] = np.asarray(inp["rwkv_ln_w"])[l]
        rt[l, 768:1024] = np.asarray(inp["rwkv_ln_b"])[l]
        rt[l, 1024:2048] = np.asarray(inp["ln1_w"])[l]
        rt[l, 2048:3072] = np.asarray(inp["ln1_b"])[l]
        rt[l, 3072:4096] = np.asarray(inp["ln2_w"])[l]
        rt[l, 4096:5120] = np.asarray(inp["ln2_b"])[l]
    d["rowtab"] = rt
    return d


_CACHE = {}


def kernel(**inputs):
    cfg = {}
    if "full" not in _CACHE:
        _CACHE["full"] = build(cfg)
    kb = _CACHE["full"]
    shared = host_layout(inputs)
    shared.update(host_consts())
    x = np.ascontiguousarray(np.asarray(inputs["x"], dtype=np.float32))
    in_maps = []
    for c in range(8):
        m = dict(shared)
        m["x"] = x[2 * c:2 * c + 2]
        in_maps.append(m)
    res = run_bass_kernel_spmd(kb.nc, in_maps, core_ids=list(range(8)))
    return np.concatenate([r["out"] for r in res.results], axis=0).astype(np.float32)
```

```python
import math
from contextlib import ExitStack
import numpy as np
import concourse.bass as bass
import concourse.mybir as mybir
from concourse.bass_utils import run_bass_kernel_spmd

F32 = mybir.dt.float32
BF16 = mybir.dt.bfloat16
AF = mybir.ActivationFunctionType
ALU = mybir.AluOpType
AX = mybir.AxisListType

S = 2048
D = 1024
NT = S // 128
L = 2
FF = 2816
NFC = FF // 128
ALPHA = float((2 * L) ** 0.25)
LN_EPS = 1e-5
RW_EPS = 64e-5
NDS = 6


class Reg:
    __slots__ = ("lw", "rd")

    def __init__(self):
        self.lw = None
        self.rd = {}


def regs(n):
    return [Reg() for _ in range(n)]


class _Rec:
    def __init__(self):
        self.calls = []

    def __getattr__(self, name):
        def f(*a, **kw):
            self.calls.append((name, a, kw))
            return self
        return f


class _Node:
    __slots__ = ("e", "kind", "calls", "reads", "writes", "cost", "deps")

    def __init__(self, e, kind, calls, reads, writes, cost):
        self.e, self.kind, self.calls, self.reads, self.writes, self.cost = e, kind, calls, reads, writes, cost
        self.deps = ()


def _free_elems(ap):
    try:
        n = 1
        for s_ in list(ap.shape)[1:]:
            n *= int(s_)
        return n
    except Exception:
        return 256


def _ap_bytes(ap):
    try:
        n = 1
        for s_ in list(ap.shape):
            n *= int(s_)
        return n * 4
    except Exception:
        return 65536


def _est_cost(e, calls):
    t = 0.0
    for name, a, kw in calls:
        out = kw.get("out", a[0] if a else None)
        n = _free_elems(out) if out is not None else 256
        if e == "pe":
            mul = 4.0 if (name == "matmul" and getattr(kw.get("lhsT"), "dtype", None) == F32) else 1.0
            t += mul * max(n, 64) / 1.6 + 25.0
        elif e == "act":
            t += n / 1.0 + 220.0
        elif e == "dve":
            t += n / 0.9 + 80.0
        else:
            t += n * 2.0 + 300.0
    return t


class KB:
    def __init__(self, sched=True, W=32):
        nc = bass.Bass("TRN2", target_bir_lowering=False)
        self.nc = nc
        self.E = {"pe": nc.tensor, "act": nc.scalar, "dve": nc.vector, "pool": nc.gpsimd, "sp": nc.sync}
        self.sems = {}
        self.cnt = {}
        for e in ("pe", "act", "dve", "pool"):
            self.sems[e] = nc.alloc_semaphore("s_" + e)
            self.cnt[e] = 0
        self.dq = {}
        for q, nds in (("sp", 12), ("pool", NDS), ("act", 6)):
            keys = []
            for i in range(nds):
                k = "d_%s%d" % (q, i)
                self.sems[k] = nc.alloc_semaphore(k)
                self.cnt[k] = 0
                keys.append(k)
            self.dq[q] = [keys, 0]
        self.seen = {e: {} for e in self.E}
        self.nops = 0
        self.sched = sched
        self.W = W
        self.pending = []

    def _waits(self, e, reads, writes, extra=()):
        need = {}

        def add(rec):
            if rec is None:
                return
            k, c = rec
            if need.get(k, 0) < c:
                need[k] = c

        for r in reads:
            add(r.lw)
        for w in writes:
            add(w.lw)
            for k, c in w.rd.items():
                add((k, c))
        for rec in extra:
            add(rec)
        eng = self.E[e]
        seen = self.seen[e]
        for k, c in need.items():
            if e == "pe" and k == "pe":
                continue
            if seen.get(k, 0) >= c:
                continue
            eng.wait_ge(self.sems[k], c)
            seen[k] = c

    def _mark(self, rec, reads, writes):
        k, c = rec
        for r in reads:
            if r.rd.get(k, 0) < c:
                r.rd[k] = c
        for w in writes:
            w.lw = rec
            w.rd = {}

    def op(self, e, fn, reads=(), writes=()):
        rec = _Rec()
        fn(rec)
        node = _Node(e, "op", rec.calls, list(reads), list(writes), _est_cost(e, rec.calls))
        if self.sched:
            self.pending.append(node)
        else:
            self._emit(node)

    def dma(self, q, out, in_, reads=(), writes=(), **kw):
        node = _Node(q, "dma", (out, in_, kw), list(reads), list(writes), 2000.0 + _ap_bytes(out) / 100.0)
        if self.sched:
            self.pending.append(node)
        else:
            self._emit(node)

    def _emit(self, n):
        e = n.e
        if n.kind == "op":
            self._waits(e, n.reads, n.writes)
            eng = self.E[e]
            inst = None
            for name, a, kw in n.calls:
                inst = getattr(eng, name)(*a, **kw)
            self.cnt[e] += 1
            inst.then_inc(self.sems[e], 1)
            self._mark((e, self.cnt[e]), n.reads, n.writes)
        else:
            out, in_, kw = n.calls
            keys, i = self.dq[e]
            k = keys[i]
            self.dq[e][1] = (i + 1) % len(keys)
            extra = [(k, self.cnt[k])] if self.cnt[k] else []
            self._waits(e, n.reads, n.writes, extra)
            self.E[e].dma_start(out=out, in_=in_, **kw).then_inc(self.sems[k], 16)
            self.cnt[k] += 16
            self._mark((k, self.cnt[k]), n.reads, n.writes)
        self.nops += 1

    def flush(self):
        nodes = self.pending
        self.pending = []
        if not nodes:
            return
        lastw, readers = {}, {}
        for i, n in enumerate(nodes):
            d = set()
            for r in n.reads:
                if id(r) in lastw:
                    d.add(lastw[id(r)])
            for w in n.writes:
                if id(w) in lastw:
                    d.add(lastw[id(w)])
                d.update(readers.get(id(w), ()))
            d.discard(i)
            n.deps = d
            for r in n.reads:
                readers.setdefault(id(r), []).append(i)
            for w in n.writes:
                lastw[id(w)] = i
                readers[id(w)] = []
        queues = {}
        for i, n in enumerate(nodes):
            queues.setdefault(n.e, []).append(i)
        fin = [None] * len(nodes)
        efree = {e: 0.0 for e in queues}
        W, LAT = self.W, 120.0
        remaining = len(nodes)
        while remaining:
            best = None
            for e, q in queues.items():
                for i in q[:W]:
                    n = nodes[i]
                    ready = 0.0
                    ok = True
                    for d in n.deps:
                        f = fin[d]
                        if f is None:
                            ok = False
                            break
                        if f + LAT > ready:
                            ready = f + LAT
                    if not ok:
                        continue
                    start = ready if ready > efree[e] else efree[e]
                    key = (start, i)
                    if best is None or key < best[0]:
                        best = (key, e, i)
            assert best is not None
            (start, _), e, i = best
            n = nodes[i]
            fin[i] = start + n.cost
            efree[e] = (start + 60.0) if n.kind == "dma" else fin[i]
            queues[e].remove(i)
            self._emit(n)
            remaining -= 1

    def barrier(self):
        self.flush()
        for e in self.E:
            for k, c in self.cnt.items():
                if c == 0 or (e == "pe" and k == "pe"):
                    continue
                if self.seen[e].get(k, 0) >= c:
                    continue
                self.E[e].wait_ge(self.sems[k], c)
                self.seen[e][k] = c

    def finish(self):
        self.barrier()


def bcast_rows(t, off, n, parts=128):
    return bass.AP(t, off, [[0, parts], [1, n]])


def build(cfg):
    kb = KB(sched=cfg.get("sched", True), W=cfg.get("W", 128))
    nc = kb.nc
    NSEQ = cfg.get("nseq", 2)
    NLAY = cfg.get("nlay", 2)
    mixers = cfg.get("mixers", ("gla", "rwkv", "nsa"))
    dumps = cfg.get("dumps", ())
    inject_O = cfg.get("inject_O", False)

    def dram_in(name, shape):
        return nc.dram_tensor(name, list(shape), F32, kind="ExternalInput")

    x_d = dram_in("x", [2, S, D])
    w_in = dram_in("w_in", [L, D, 3400])
    w_vres = dram_in("w_in_vres", [1, D, 32])
    w_nsa = dram_in("w_nsa", [L, D, 2200])
    gla_w_a2 = dram_in("gla_w_a2", [L, 16, 256])
    rwkv_w2 = dram_in("rwkv_w2", [L, 64, 256])
    rwkv_a2 = dram_in("rwkv_a2", [L, 64, 256])
    rwkv_v2 = dram_in("rwkv_v2", [1, 32, 256])
    rwkv_g2 = dram_in("rwkv_g2", [L, 160, 256])
    nsa_wk1 = dram_in("nsa_wk1", [L, 2048, 256])
    nsa_wk2 = dram_in("nsa_wk2", [L, 256, 64])
    nsa_wv1 = dram_in("nsa_wv1", [L, 2048, 256])
    nsa_wv2 = dram_in("nsa_wv2", [L, 256, 64])
    nsa_posT = dram_in("nsa_posT", [L, 2, 64, 32])
    w_out = dram_in("w_out", [L, D, D])
    w_gate = dram_in("ffn_w_gate", [L, D, FF])
    w_up = dram_in("ffn_w_up", [L, D, FF])
    w_down = dram_in("ffn_w_down", [L, FF, D])
    ptab = dram_in("ptab", [L, 128, 32])
    rowtab = dram_in("rowtab", [L, 5120])
    c_ident = dram_in("c_ident", [128, 128])
    c_caus = dram_in("c_caus", [128, 128])
    c_rope = dram_in("c_rope", [2, 128, S])
    c_cmpmask = dram_in("c_cmpmask", [128, S])
    c_onehot = dram_in("c_onehot", [32, S])
    c_selm = dram_in("c_selm", [2, 128, NT, 32])
    c_ovl = dram_in("c_ovl", [128, 32])
    c_blk = dram_in("c_blk", [128, 128])
    c_hsel = dram_in("c_hsel", [128, 2])
    c_rwmask2 = dram_in("c_rwmask2", [128, 3, 128])
    c_istk = dram_in("c_istk", [128, 64])
    if inject_O:
        dbg_OT = dram_in("dbg_OT", [128, 8, S])
    out_d = nc.dram_tensor("out", [2, S, D], F32, kind="ExternalOutput")
    xres = [nc.dram_tensor("xres%d" % i, [S, D], F32, kind="Internal") for i in range(2)]
    xres_r = [regs(NT) for _ in range(2)]
    vfirst_d = nc.dram_tensor("vfirst", [2, 128, S], F32, kind="Internal")
    vfirst_r = regs(2)
    dump_d = {}
    for name, shape in dumps:
        dump_d[name] = nc.dram_tensor("dump_" + name, list(shape), F32, kind="ExternalOutput")

    XT = nc.alloc_sbuf_tensor("XT", [128, 8, S], BF16)
    XT_r = regs(NT)
    ident = nc.alloc_sbuf_tensor("ident", [128, 128], BF16)
    identf = nc.alloc_sbuf_tensor("identf", [128, 128], F32)
    caus = nc.alloc_sbuf_tensor("caus", [128, 4, 128], F32)
    ptb = nc.alloc_sbuf_tensor("ptb", [128, L, 32], F32)
    ptn = nc.alloc_sbuf_tensor("ptn", [128, L, 32], F32)
    c_r = Reg()
    for l in range(L):
        kb.dma("sp", ptb[:, l, :], ptab.ap()[l], writes=[c_r])
    kb.dma("pool", ident[:], c_ident.ap(), writes=[c_r])
    kb.dma("sp", identf[:], c_ident.ap(), writes=[c_r])
    for i in range(4):
        kb.dma("sp", caus[:, i, :], c_caus.ap(), writes=[c_r])

    def dram16(name, shape):
        return nc.dram_tensor(name, list(shape), BF16, kind="Internal")

    conv_list = [(w_in, [L, D, 3400]), (w_nsa, [L, D, 2200]), (w_out, [L, D, D]), (w_gate, [L, D, FF]), (w_up, [L, D, FF]),
                 (w_down, [L, FF, D]), (nsa_wk1, [L, 2048, 256]), (nsa_wv1, [L, 2048, 256])]
    w16 = {}
    w16_r = Reg()
    CH = 2816
    with nc.sbuf_tensor("cv_f", [128, 3, CH], F32) as cvf, nc.sbuf_tensor("cv_b", [128, 3, CH], BF16) as cvb:
        cvf_r, cvb_r = regs(3), regs(3)
        job = 0
        for src, shape in conv_list:
            dst = dram16(src.name + "_16", shape)
            w16[src.name] = dst
            tot = 1
            for d_ in shape:
                tot *= d_
            per = tot // 128
            assert per * 128 == tot
            for c0 in range(0, per, CH):
                n = min(CH, per - c0)
                i = job % 3
                kb.dma("sp", cvf[:, i, 0:n], bass.AP(src, c0, [[per, 128], [1, n]]), writes=[cvf_r[i]])
                eng = "act"
                if eng == "act":
                    kb.op("act", lambda a: a.activation(out=cvb[:, i, 0:n], in_=cvf[:, i, 0:n], func=AF.Copy),
                          reads=[cvf_r[i]], writes=[cvb_r[i]])
                else:
                    kb.op(eng, lambda v: v.tensor_scalar(out=cvb[:, i, 0:n], in0=cvf[:, i, 0:n], scalar1=1.0, scalar2=None, op0=ALU.mult),
                          reads=[cvf_r[i]], writes=[cvb_r[i]])
                kb.dma("act", bass.AP(dst, c0, [[per, 128], [1, n]]), cvb[:, i, 0:n], reads=[cvb_r[i]], writes=[w16_r])
                job += 1
        kb.barrier()
    w_in16, w_nsa16, w_out16 = w16["w_in"], w16["w_nsa"], w16["w_out"]
    w_gate16, w_up16, w_down16 = w16["ffn_w_gate"], w16["ffn_w_up"], w16["ffn_w_down"]
    wk1_16, wv1_16 = w16["nsa_wk1"], w16["nsa_wv1"]

    PS = [nc.alloc_psum_tensor("ps%d" % i, [128, 512], F32) for i in range(6)]
    PS_r = regs(6)
    PB = [nc.alloc_psum_tensor("pb%d" % i, [128, 1024], BF16) for i in range(2)]
    PB_r = regs(2)

    _uid = [0]

    def sb(name, shape, dt):
        _uid[0] += 1
        return nc.sbuf_tensor("%s_%d" % (name, _uid[0]), list(shape), dt)

    def proj_fm(ps_ap, Wt, c0, M, tok0, N, wreads, treads, pw, kparts=8):
        def fn(pe):
            inst = None
            for k in range(kparts):
                inst = pe.matmul(ps_ap, lhsT=Wt[:, k, c0:c0 + M], rhs=XT[:, k, tok0:tok0 + N],
                                 start=(k == 0), stop=(k == kparts - 1))
            return inst
        kb.op("pe", fn, reads=list(wreads) + list(treads), writes=[pw])

    def proj_tm(ps_ap, Wt, c0, N, tt, wreads, pw):
        def fn(pe):
            inst = None
            for k in range(8):
                inst = pe.matmul(ps_ap, lhsT=XT[:, k, tt * 128:(tt + 1) * 128], rhs=Wt[:, k, c0:c0 + N],
                                 start=(k == 0), stop=(k == 7))
            return inst
        kb.op("pe", fn, reads=list(wreads) + [XT_r[tt]], writes=[pw])

    def load_w(Wt, src3, ncols, wreg, q="sp", chunk=None):
        for k in range(8):
            kb.dma(q, Wt[:, k, 0:ncols], src3[k * 128:(k + 1) * 128, 0:ncols], reads=[w16_r], writes=[wreg])

    xb = [nc.alloc_sbuf_tensor("xb%d" % i, [128, D], BF16) for i in range(2)]
    xb_r = regs(2)
    xb_i = [0]

    def make_xT(src_ap, src_reg, tt):
        i = xb_i[0]
        xb_i[0] ^= 1
        kb.op("act", lambda a: a.activation(out=xb[i][:], in_=src_ap, func=AF.Copy),
              reads=[src_reg], writes=[xb_r[i]])
        pb = PB[i]

        def fn(pe):
            inst = None
            for k in range(8):
                inst = pe.transpose(out=pb[:, k * 128:(k + 1) * 128], in_=xb[i][:, k * 128:(k + 1) * 128],
                                    identity=ident[:])
            return inst
        kb.op("pe", fn, reads=[xb_r[i], c_r], writes=[PB_r[i]])
        kb.op("dve", lambda v: v.tensor_copy(out=XT[:, :, tt * 128:(tt + 1) * 128],
                                             in_=pb[:, :].rearrange("p (k t) -> p k t", k=8)),
              reads=[PB_r[i]], writes=[XT_r[tt]])

    def layer_norm(xt_ap, xreg, lnw_ap, lnb_ap, lnreg, st, st_r):
        kb.op("dve", lambda v: v.bn_stats(out=st[:, 0:6], in_=xt_ap[:, 0:512]), reads=[xreg], writes=[st_r])
        kb.op("dve", lambda v: v.bn_stats(out=st[:, 6:12], in_=xt_ap[:, 512:1024]), reads=[xreg, st_r], writes=[st_r])
        kb.op("dve", lambda v: v.bn_aggr(out=st[:, 12:14], in_=st[:, 0:12]), reads=[st_r], writes=[st_r])
        kb.op("act", lambda a: a.activation(out=st[:, 14:15], in_=st[:, 13:14], func=AF.Sqrt, bias=LN_EPS, scale=1.0),
              reads=[st_r], writes=[st_r])
        kb.op("dve", lambda v: v.reciprocal(out=st[:, 15:16], in_=st[:, 14:15]), reads=[st_r], writes=[st_r])
        kb.op("dve", lambda v: v.scalar_tensor_tensor(out=st[:, 16:17], in0=st[:, 12:13], scalar=-1.0, in1=st[:, 15:16],
                                                      op0=ALU.mult, op1=ALU.mult), reads=[st_r], writes=[st_r])
        kb.op("act", lambda a: a.activation(out=xt_ap, in_=xt_ap, func=AF.Identity, bias=st[:, 16:17], scale=st[:, 15:16]),
              reads=[st_r, xreg], writes=[xreg])
        kb.op("dve", lambda v: v.tensor_tensor(out=xt_ap, in0=xt_ap, in1=lnw_ap, op=ALU.mult), reads=[xreg, lnreg], writes=[xreg])
        kb.op("dve", lambda v: v.tensor_tensor(out=xt_ap, in0=xt_ap, in1=lnb_ap, op=ALU.add), reads=[xreg, lnreg], writes=[xreg])

    def dump(name, sb_ap, reg, idx=None):
        if name in dump_d:
            dst = dump_d[name].ap() if idx is None else dump_d[name].ap()[idx]
            kb.dma("sp", dst, sb_ap, reads=[reg])

    kb.op("dve", lambda v: v.tensor_scalar(out=ptn[:, :, :], in0=ptb[:, :, :], scalar1=-1.0, scalar2=None, op0=ALU.mult),
          reads=[c_r], writes=[c_r])
    kb.op("dve", lambda v: v.tensor_scalar(out=ptn[:, :, 2:13], in0=ptb[:, :, 2:13], scalar1=-1.0, scalar2=1.0,
                                           op0=ALU.mult, op1=ALU.add), reads=[c_r], writes=[c_r])

    def gla_phase(sq, l, OT, OT_r):
        with ExitStack() as es:
            A = lambda n_, s_, d_: es.enter_context(sb(n_, s_, d_))
            Wg = A("Wg", [128, 8, 1152], BF16)
            wa2 = A("wa2", [16, 256], BF16)
            gln = A("gln", [128, 2, 256], F32)
            alrT = A("alrT", [16, 512], BF16)
            t1 = A("gt1", [128, 512], F32)
            t2 = A("gt2", [128, 512], F32)
            EB = A("EB", [128, 2, 512], F32)
            QTl = A("gQT", [128, 2, 2, 512], BF16)
            KTl = A("gKT", [128, 2, 512], BF16)
            Ktok = A("gKtok", [128, 4, 256], BF16)
            V = A("gV", [128, 4, 256], BF16)
            SG = A("gSG", [128, 4, 256], F32)
            AT = A("gAT", [128, 4, 128], BF16)
            St = A("gSt", [128, 2, 128], F32)
            tmpS = A("gtmpS", [128, 128], F32)
            blkm = A("gblk", [128, 128], F32)
            Sb = A("gSb", [128, 2, 128], BF16)
            rmask = A("grm", [128, 512], F32)
            ow = A("gow", [128, 256], F32)
            ow2 = A("gow2", [128, 256], F32)
            ob = A("gob", [128, 256], BF16)
            st = A("gst", [128, 32], F32)
            W_r, p_r, alr_r, t1_r, t2_r = Reg(), Reg(), Reg(), Reg(), Reg()
            EB_r, QT_r, KT_r = regs(2), regs(2), regs(2)
            Ktok_r, V_r, SG_r, AT_r, St_r, Sb_r = Reg(), regs(4), regs(4), Reg(), Reg(), Reg()
            ow_r, ow2_r, ob_r, st_r = Reg(), Reg(), Reg(), Reg()
            gs = cfg.get("gla_stop", 99)
            load_w(Wg, w_in16.ap()[l][:, 0:1040], 1040, W_r)
            kb.dma("pool", wa2[:, :], gla_w_a2.ap()[l], writes=[p_r])
            kb.dma("sp", gln[:, 0, :], bcast_rows(rowtab, l * 5120 + 0, 256), writes=[p_r])
            kb.dma("sp", gln[:, 1, :], bcast_rows(rowtab, l * 5120 + 256, 256), writes=[p_r])
            kb.op("pool", lambda g: g.memset(rmask[:, :], 1.0), writes=[p_r])
            kb.op("pool", lambda g: g.memset(rmask[:, :].rearrange("p (c t) -> p c t", c=4)[:, :, 0:1], 0.0), writes=[p_r])
            kb.op("pool", lambda g: g.memset(St[:, :, :], 0.0), writes=[St_r])
            kb.op("pool", lambda g: g.memset(QTl[:, :, :, :], 0.0), writes=QT_r)
            kb.dma("sp", blkm[:, :], c_blk.ap(), writes=[p_r])
            tmpS_r = Reg()
            kb.op("pool", lambda g: g.memset(Sb[:, :, :], 0.0), writes=[Sb_r])
            for mt in range(4 if gs > 0 else 0):
                tok0 = mt * 512
                xr = XT_r[mt * 4:mt * 4 + 4]
                proj_fm(PS[0][0:16, :], Wg, 1024, 16, tok0, 512, [W_r], xr, PS_r[0])
                kb.op("act", lambda a: a.activation(out=alrT[:, :], in_=PS[0][0:16, :], func=AF.Copy),
                      reads=[PS_r[0]], writes=[alr_r])
                for hp in range(2):
                    kb.op("pe", lambda pe: pe.matmul(PS[1][:, :], lhsT=wa2[0:16, hp * 128:(hp + 1) * 128], rhs=alrT[0:16, :],
                                                     start=True, stop=True), reads=[p_r, alr_r], writes=[PS_r[1]])
                    kb.op("act", lambda a: a.activation(out=t1[:, :], in_=PS[1][:, :], func=AF.Exp, scale=-1.0,
                                                        bias=ptn[:, l, hp:hp + 1]), reads=[PS_r[1], c_r], writes=[t1_r])
                    kb.op("act", lambda a: a.activation(out=t1[:, :], in_=t1[:, :], func=AF.Ln, bias=1.0, scale=1.0),
                          reads=[t1_r], writes=[t1_r])
                    kb.op("dve", lambda v: v.tensor_tensor_scan(out=t2[:, :], data0=rmask[:, :], data1=t1[:, :], initial=0.0,
                                                                op0=ALU.mult, op1=ALU.add), reads=[t1_r, p_r], writes=[t2_r])
                    kb.op("act", lambda a: a.activation(out=EB[:, hp, :], in_=t2[:, :], func=AF.Exp, scale=-1.0 / 16.0),
                          reads=[t2_r], writes=[EB_r[hp]])
                    kb.op("act", lambda a: a.activation(out=t1[:, :], in_=t2[:, :], func=AF.Exp, scale=1.0 / 16.0),
                          reads=[t2_r], writes=[t1_r])
                    proj_fm(PS[2][:, :], Wg, hp * 128, 128, tok0, 512, [W_r], xr, PS_r[2])
                    for hh in range(2):
                        pr = slice(hh * 64, hh * 64 + 64)
                        kb.op("dve", lambda v: v.scalar_tensor_tensor(out=QTl[pr, hp, hh, :], in0=PS[2][pr, :], scalar=0.125,
                                                                      in1=EB[pr, hp, :], op0=ALU.mult, op1=ALU.mult),
                              reads=[PS_r[2], EB_r[hp]], writes=[QT_r[hp]])
                    proj_fm(PS[3][:, :], Wg, 256 + hp * 128, 128, tok0, 512, [W_r], xr, PS_r[3])
                    kb.op("dve", lambda v: v.tensor_tensor(out=KTl[:, hp, :], in0=PS[3][:, :], in1=t1[:, :], op=ALU.mult),
                          reads=[PS_r[3], t1_r], writes=[KT_r[hp]])

                    def fn(pe):
                        inst = None
                        for j in range(4):
                            inst = pe.transpose(out=PB[0][:, j * 128:(j + 1) * 128], in_=KTl[:, hp, j * 128:(j + 1) * 128],
                                                identity=ident[:])
                        return inst
                    kb.op("pe", fn, reads=[KT_r[hp], c_r], writes=[PB_r[0]])
                    kb.op("dve", lambda v: v.tensor_copy(out=Ktok[:, :, hp * 128:(hp + 1) * 128],
                                                         in_=PB[0][:, 0:512].rearrange("p (j c) -> p j c", j=4)),
                          reads=[PB_r[0]], writes=[Ktok_r])
                for j in range(4 if gs > 1 else 0):
                    tt = mt * 4 + j
                    proj_tm(PS[4][:, 0:256], Wg, 512, 256, tt, [W_r], PS_r[4])
                    proj_tm(PS[5][:, 0:256], Wg, 768, 256, tt, [W_r], PS_r[5])
                    if cfg.get("gv", 3) >= 2:
                        kb.op("dve", lambda v: v.tensor_scalar(out=V[:, j, :], in0=PS[4][:, 0:256], scalar1=1.0, scalar2=None, op0=ALU.mult),
                              reads=[PS_r[4]], writes=[V_r[j]])
                    if cfg.get("gv", 3) >= 3:
                        kb.op("act", lambda a: a.activation(out=SG[:, j, :], in_=PS[5][:, 0:256], func=AF.Silu),
                              reads=[PS_r[5]], writes=[SG_r[j]])
                for j in range(4 if gs > 2 else 0):
                    tt = mt * 4 + j
                    cc = slice(j * 128, (j + 1) * 128)

                    def fn(pe):
                        inst = None
                        for h in range(4):
                            hp, hh = h // 2, h % 2
                            inst = pe.matmul(PS[0][:, h * 128:(h + 1) * 128], lhsT=KTl[:, hp, cc], rhs=QTl[:, hp, hh, cc],
                                             start=True, stop=True)
                        return inst
                    kb.op("pe", fn, reads=QT_r + KT_r, writes=[PS_r[0]])
                    kb.op("dve", lambda v: v.tensor_tensor(out=AT[:, :, :], in0=PS[0][:, :].rearrange("p (h t) -> p h t", h=4),
                                                          in1=caus[:, :, :], op=ALU.mult), reads=[PS_r[0], c_r], writes=[AT_r])

                    def fn(pe):
                        inst = None
                        for h in range(4):
                            hp, hh = h // 2, h % 2
                            pe.matmul(PS[1][:, h * 64:(h + 1) * 64], lhsT=AT[:, h, :], rhs=V[:, j, h * 64:(h + 1) * 64],
                                      start=True, stop=False)
                            inst = pe.matmul(PS[1][:, h * 64:(h + 1) * 64], lhsT=QTl[:, hp, hh, cc], rhs=Sb[:, hp, hh * 64:(hh + 1) * 64],
                                             start=False, stop=True)
                        return inst
                    if gs <= 3:
                        continue
                    kb.op("pe", fn, reads=[AT_r, V_r[j], Sb_r] + QT_r, writes=[PS_r[1]])

                    def fn(pe):
                        inst = None
                        for hp in range(2):
                            inst = pe.matmul(PS[2][:, hp * 128:(hp + 1) * 128], lhsT=Ktok[:, j, hp * 128:(hp + 1) * 128],
                                             rhs=V[:, j, hp * 128:(hp + 1) * 128], start=True, stop=True)
                        return inst
                    if gs <= 4:
                        continue
                    kb.op("pe", fn, reads=[Ktok_r, V_r[j]], writes=[PS_r[2]])
                    for hp in range(2):
                        ee = EB[:, hp, j * 128 + 127:j * 128 + 128]
                        kb.op("dve", lambda v: v.scalar_tensor_tensor(out=tmpS[:, :], in0=PS[2][:, hp * 128:(hp + 1) * 128], scalar=ee,
                                                                      in1=blkm[:, :], op0=ALU.mult, op1=ALU.mult),
                              reads=[PS_r[2], EB_r[hp], p_r], writes=[tmpS_r])
                        kb.op("dve", lambda v: v.scalar_tensor_tensor(out=St[:, hp, :], in0=St[:, hp, :], scalar=ee, in1=tmpS[:, :],
                                                                      op0=ALU.mult, op1=ALU.add),
                              reads=[St_r, tmpS_r, EB_r[hp]], writes=[St_r])
                    kb.op("act", lambda a: a.activation(out=Sb[:, :, :], in_=St[:, :, :], func=AF.Copy), reads=[St_r], writes=[Sb_r])
                    if gs <= 5:
                        continue
                    head_norm_gate(PS[1][:, 0:256], PS_r[1], ow, ow_r, ow2, ow2_r, st, st_r, gln, p_r, LN_EPS)
                    kb.op("dve", lambda v: v.tensor_tensor(out=ob[:, :], in0=ow[:, :], in1=SG[:, j, :], op=ALU.mult),
                          reads=[ow_r, SG_r[j]], writes=[ob_r])
                    if gs <= 6:
                        continue
                    out_to_OT(ob, ob_r, 128, OT, OT_r, 0, tt * 128)
            kb.barrier()

    def head_norm_gate(ps_ap, ps_r, ow, ow_r, ow2, ow2_r, st, st_r, gln, gln_r, eps, P=128):
        v4 = lambda ap: ap.rearrange("p (h d) -> p h d", h=4)
        kb.op("act", lambda a: a.activation(out=ow[0:P, :], in_=ps_ap, func=AF.Copy), reads=[ps_r], writes=[ow_r])
        kb.op("act", lambda a: a.activation(out=ow2[0:P, :], in_=ow[0:P, :], func=AF.Square), reads=[ow_r], writes=[ow2_r])
        kb.op("dve", lambda v: v.reduce_sum(out=st[0:P, 0:4], in_=v4(ow[0:P, :]), axis=AX.X), reads=[ow_r], writes=[st_r])
        kb.op("dve", lambda v: v.reduce_sum(out=st[0:P, 4:8], in_=v4(ow2[0:P, :]), axis=AX.X), reads=[ow2_r, st_r], writes=[st_r])
        kb.op("dve", lambda v: v.tensor_scalar(out=st[0:P, 8:16], in0=st[0:P, 0:8], scalar1=1.0 / 64.0, scalar2=None, op0=ALU.mult),
              reads=[st_r], writes=[st_r])
        kb.op("dve", lambda v: v.tensor_tensor(out=st[0:P, 16:20], in0=st[0:P, 8:12], in1=st[0:P, 8:12], op=ALU.mult),
              reads=[st_r], writes=[st_r])
        kb.op("dve", lambda v: v.tensor_tensor(out=st[0:P, 20:24], in0=st[0:P, 12:16], in1=st[0:P, 16:20], op=ALU.subtract),
              reads=[st_r], writes=[st_r])
        kb.op("act", lambda a: a.activation(out=st[0:P, 24:28], in_=st[0:P, 20:24], func=AF.Sqrt, bias=eps, scale=1.0),
              reads=[st_r], writes=[st_r])
        kb.op("dve", lambda v: v.reciprocal(out=st[0:P, 28:32], in_=st[0:P, 24:28]), reads=[st_r], writes=[st_r])
        for h in range(4):
            kb.op("dve", lambda v: v.tensor_scalar(out=ow[0:P, h * 64:(h + 1) * 64], in0=ow[0:P, h * 64:(h + 1) * 64],
                                                   scalar1=st[0:P, 8 + h:9 + h], scalar2=st[0:P, 28 + h:29 + h],
                                                   op0=ALU.subtract, op1=ALU.mult), reads=[ow_r, st_r], writes=[ow_r])
        kb.op("dve", lambda v: v.tensor_tensor(out=ow[0:P, :], in0=ow[0:P, :], in1=gln[0:P, 0, :], op=ALU.mult),
              reads=[ow_r, gln_r], writes=[ow_r])
        kb.op("dve", lambda v: v.tensor_tensor(out=ow[0:P, :], in0=ow[0:P, :], in1=gln[0:P, 1, :], op=ALU.add),
              reads=[ow_r, gln_r], writes=[ow_r])

    def out_to_OT(ob, ob_r, P, OT, OT_r, k0, tokc0, ncols=256):
        nk = ncols // 128

        def fn(pe):
            inst = None
            for kk in range(nk):
                inst = pe.transpose(out=PB[1][:, kk * 128:kk * 128 + P], in_=ob[0:P, kk * 128:(kk + 1) * 128],
                                    identity=ident[0:P, 0:P])
            return inst
        kb.op("pe", fn, reads=[ob_r, c_r], writes=[PB_r[1]])
        tr = OT_r[tokc0 // 128]
        kb.op("act", lambda a: a.activation(out=OT[:, k0:k0 + nk, tokc0:tokc0 + P],
                                            in_=PB[1][:, 0:nk * 128].rearrange("p (k t) -> p k t", k=nk)[:, :, 0:P], func=AF.Copy),
              reads=[PB_r[1]], writes=[tr])

    RWT = [(0, 128), (128, 128), (256, 128), (384, 128), (512, 128), (640, 128), (768, 64), (832, 64),
           (896, 128), (1024, 32), (1056, 32)]
    C0 = float(math.exp(-0.5))

    def rwkv_phase(sq, l, OT, OT_r):
        MT = 256
        NCH = MT // 64
        PU = NCH * 2
        with ExitStack() as es:
            A = lambda n_, s_, d_: es.enter_context(sb(n_, s_, d_))
            Wr = A("Wr", [128, 8, 1088], BF16)
            w2 = A("rw2", [64, 256], BF16)
            a2 = A("ra2", [64, 256], BF16)
            v2 = A("rv2", [32, 256], BF16)
            g2a = A("rg2a", [128, 256], BF16)
            g2b = A("rg2b", [128, 256], BF16)
            rln = A("rln", [128, 2, 2, 64], F32)
            blkf = A("rblkf", [128, 128], F32)
            m3 = A("rm3", [128, 3, 128], F32)
            istk = A("ristk", [128, 64], BF16)
            onec = A("ronec", [128, 1], BF16)
            rmask = A("rrm", [128, MT], F32)
            xs = [A("rxs%d" % i, [128, MT], F32) for i in range(6)]
            tt_ = [A("rt%d" % i, [128, MT], F32) for i in range(8)]
            ttb_ = [A("rtb%d" % i, [128, MT], F32) for i in range(9)]
            tb_r = regs(9)
            Gam2 = [A("rGam%d" % i_, [128, 2, MT], F32) for i_ in range(2)]
            AR2 = [A("rAR%d" % i_, [128, 2, 2, NCH, 2, 64], BF16) for i_ in range(2)]
            BK = A("rBK", [128, 2, 2, NCH, 2, 64], BF16)
            VTz = A("rVTz", [128, 2, NCH, 2, 64], BF16)
            rkrz = A("rrkrz", [128, 2, NCH, 2, 64], BF16)
            TW = A("rTW", [64, MT], BF16)
            AL = A("rAL", [64, MT], BF16)
            SGLd = A("rSGLd", [128, NCH, 2, 64], BF16)
            SGL2d = A("rSGL2d", [128, NCH, 2, 64], BF16)
            VR = A("rVR", [32, MT], BF16)
            Vstk2 = [A("rVstk%d" % i_, [128, NCH, 2, 64], BF16) for i_ in range(2)]
            Bblk2 = [A("rBblk%d" % i_, [128, NCH, 2, 128], BF16) for i_ in range(2)]
            Kblk2 = [A("rKblk%d" % i_, [128, NCH, 2, 128], BF16) for i_ in range(2)]
            gtok2 = [A("rgtok%d" % i_, [128, NCH, 2, 64], F32) for i_ in range(2)]
            cb2 = [A("rcb%d" % i_, [128, NCH, 2], F32) for i_ in range(2)]
            MN = [A("rMN%d" % i, [128, PU, 2, 128], F32) for i in range(2)]
            Pm2 = [A("rP%d" % i_, [128, PU, 128], F32) for i_ in range(2)]
            A32 = [A("rA3%d" % i_, [128, PU, 3, 128], BF16) for i_ in range(2)]
            Rs = A("rRs", [128, 128], F32)
            Ub = A("rUb", [128, 128], BF16)
            Tst = A("rTst", [128, 2, 64], F32)
            Tb = A("rTb", [128, 2, 64], BF16)
            tmpS = A("rtmpS", [128, 2, 64], F32)
            ow = A("row", [128, 128], F32)
            ow2 = A("row2", [128, 128], F32)
            obblk = A("robblk", [128, 2, 128], BF16)
            st = A("rst", [128, 32], F32)
            W_r, p_r = Reg(), Reg()
            xs_r, t_r = regs(6), regs(8)
            t_r0 = t_r
            BK_r, VTz_r, rkrz_r = regs(2), regs(2), regs(2)
            Gam_r2, AR_r2 = [regs(2), regs(2)], [regs(2), regs(2)]
            TW_r, AL_r, SGL_r, SGL2_r, VR_r = Reg(), Reg(), Reg(), Reg(), Reg()
            Vstk_r2, Bblk_r2, Kblk_r2, gtok_r2, cb_r2 = regs(2), regs(2), regs(2), regs(2), regs(2)
            MN_r, Rs_r, Ub_r, T_r, Tb_r, tmpS_r = regs(2), Reg(), Reg(), Reg(), Reg(), Reg()
            P_r2, A3_r2 = regs(2), regs(2)
            ow_r, ow2_r, ob_r, st_r = Reg(), Reg(), Reg(), Reg()
            rs_ = cfg.get("rw_stop", 99)
            load_w(Wr, w_in16.ap()[l][:, 1040:2096], 1056, W_r)
            if l >= 1:
                for k in range(8):
                    kb.dma("pool", Wr[:, k, 1056:1088], w_vres.ap()[l - 1][k * 128:(k + 1) * 128, :], writes=[W_r])
            kb.dma("pool", w2[:, :], rwkv_w2.ap()[l], writes=[p_r])
            kb.dma("pool", a2[:, :], rwkv_a2.ap()[l], writes=[p_r])
            if l >= 1:
                kb.dma("pool", v2[:, :], rwkv_v2.ap()[l - 1], writes=[p_r])
            kb.op("pool", lambda g: g.memset(g2b[:, :], 0.0), writes=[p_r])
            kb.dma("pool", g2a[:, :], rwkv_g2.ap()[l][0:128, :], writes=[p_r])
            kb.dma("pool", g2b[0:32, :], rwkv_g2.ap()[l][128:160, :], reads=[p_r], writes=[p_r])
            for wb in range(2):
                for hp in range(2):
                    for hh in range(2):
                        kb.dma("sp", rln[hh * 64:(hh + 1) * 64, wb, hp, :],
                               bcast_rows(rowtab, l * 5120 + 512 + wb * 256 + (2 * hp + hh) * 64, 64, parts=64), writes=[p_r])
            kb.dma("sp", blkf[:, :], c_blk.ap(), writes=[p_r])
            kb.dma("sp", m3[:, :, :], c_rwmask2.ap(), writes=[p_r])
            kb.dma("pool", istk[:, :], c_istk.ap(), writes=[p_r])
            kb.op("pool", lambda g: g.memset(onec[:, :], 1.0), writes=[p_r])
            kb.op("pool", lambda g: g.memset(rmask[:, :], 1.0), writes=[p_r])
            kb.op("pool", lambda g: g.memset(rmask[:, :].rearrange("p (c t) -> p c t", c=NCH)[:, :, 0:1], 0.0), writes=[p_r])
            zl = [(Tst, [T_r]), (Tb, [Tb_r]), (SGL2d, [SGL2_r]), (BK, BK_r), (VTz, VTz_r), (rkrz, rkrz_r), (obblk, [ob_r])]
            for i_ in range(2):
                zl += [(AR2[i_], AR_r2[i_])]
            for tz, rr in zl:
                kb.op("pool", lambda g, tz=tz: g.memset(tz[:], 0.0), writes=rr)

            carry = A("rcarry", [128, 12], F32)
            carry_r = regs(11)
            kb.op("pool", lambda g: g.memset(carry[:, :], 0.0), writes=carry_r)

            def shift_proj(i, tok0, dst_fn):
                c0, n = RWT[i]
                mucol = 2 + i
                pi = i % 2
                tp = ttb_[7] if pi == 0 else ttb_[8]
                tpr = tb_r[7] if pi == 0 else tb_r[8]
                proj_fm(PS[pi][0:n, 0:MT], Wr, c0, n, tok0, MT, [W_r], XT_r[tok0 // 128:tok0 // 128 + MT // 128], PS_r[pi])
                kb.op("act", lambda a: a.activation(out=tp[0:n, 1:MT], in_=PS[pi][0:n, 0:MT - 1], func=AF.Copy,
                                                    scale=ptb[0:n, l, mucol:mucol + 1]), reads=[PS_r[pi], c_r], writes=[tpr, PS_r[pi]])
                kb.op("act", lambda a: a.activation(out=tp[0:n, 0:1], in_=carry[0:n, i:i + 1], func=AF.Copy,
                                                    scale=ptb[0:n, l, mucol:mucol + 1]), reads=[carry_r[i], c_r, tpr], writes=[tpr])
                kb.op("act", lambda a: a.activation(out=carry[0:n, i:i + 1], in_=PS[pi][0:n, MT - 1:MT], func=AF.Copy),
                      reads=[PS_r[pi], carry_r[i]], writes=[carry_r[i], PS_r[pi]])
                dst_ap, dst_regs = dst_fn()
                kb.op("dve", lambda v: v.scalar_tensor_tensor(out=dst_ap, in0=PS[pi][0:n, 0:MT], scalar=ptn[0:n, l, mucol:mucol + 1],
                                                              in1=tp[0:n, 0:MT], op0=ALU.mult, op1=ALU.add),
                      reads=[PS_r[pi], tpr, c_r], writes=dst_regs + [PS_r[pi]])

            def prep(mt):
                tok0 = mt * MT
                par = mt % 2
                t_r = t_r0
                AR, Vstk, Bblk, Kblk, Gam, gtok, cb, Pm, A3 = AR2[par], Vstk2[par], Bblk2[par], Kblk2[par], Gam2[par], gtok2[par], cb2[par], Pm2[par], A32[par]
                AR_r, Vstk_r, Bblk_r, Kblk_r, Gam_r, gtok_r, cb_r, P_r, A3_r = AR_r2[par], Vstk_r2[par], Bblk_r2[par], Kblk_r2[par], Gam_r2[par], gtok_r2[par], cb_r2[par], P_r2[par], A3_r2[par]
                for i in range(6):
                    shift_proj(i, tok0, lambda i=i: (xs[i][:, :], [xs_r[i]]))
                shift_proj(6, tok0, lambda: (tt_[0][0:64, :], [t_r[0]]))
                kb.op("act", lambda a: a.activation(out=TW[:, :], in_=tt_[0][0:64, :], func=AF.Tanh), reads=[t_r[0]], writes=[TW_r])
                shift_proj(7, tok0, lambda: (AL[:, :], [AL_r]))
                shift_proj(8, tok0, lambda: (tt_[0][:, :], [t_r[0]]))
                for hh_ in range(2):
                    kb.op("act", lambda a: a.activation(out=SGLd[:, :, hh_, :], in_=tt_[0][:, :].rearrange("p (c i) -> p c i", c=NCH), func=AF.Sigmoid),
                          reads=[t_r[0]], writes=[SGL_r])
                shift_proj(9, tok0, lambda: (tt_[0][0:32, :], [t_r[0]]))
                for hh_ in range(2):
                    kb.op("act", lambda a: a.activation(out=SGL2d[0:32, :, hh_, :], in_=tt_[0][0:32, :].rearrange("p (c i) -> p c i", c=NCH), func=AF.Sigmoid),
                          reads=[t_r[0]], writes=[SGL2_r])
                if l >= 1:
                    shift_proj(10, tok0, lambda: (VR[:, :], [VR_r]))
                if rs_ <= 1:
                    return
                for hp in range(2):
                    rT, kT, vT = xs[hp], xs[2 + hp], xs[4 + hp]
                    rT_r, kT_r, vT_r = xs_r[hp], xs_r[2 + hp], xs_r[4 + hp]
                    t1, t2, t3, t4, t5, t6, t7 = tt_[0:7] if hp == 0 else ttb_[0:7]
                    t_r = t_r0 if hp == 0 else tb_r
                    kb.op("pe", lambda pe: pe.matmul(PS[2][:, 0:MT], lhsT=w2[:, hp * 128:(hp + 1) * 128], rhs=TW[:, :], start=True, stop=True),
                          reads=[p_r, TW_r], writes=[PS_r[2]])
                    kb.op("act", lambda a: a.activation(out=t1[:, :], in_=PS[2][:, 0:MT], func=AF.Sigmoid, bias=ptb[:, l, 13 + hp:14 + hp]),
                          reads=[PS_r[2], c_r], writes=[t_r[0]])
                    kb.op("dve", lambda v: v.tensor_tensor_scan(out=t2[:, :], data0=rmask[:, :], data1=t1[:, :], initial=0.0,
                                                                op0=ALU.mult, op1=ALU.add), reads=[t_r[0], p_r], writes=[t_r[1]])
                    kb.op("act", lambda a: a.activation(out=Gam[:, hp, :], in_=t2[:, :], func=AF.Exp, scale=-C0), reads=[t_r[1]], writes=[Gam_r[hp]])
                    kb.op("act", lambda a: a.activation(out=t3[:, :], in_=t2[:, :], func=AF.Exp, scale=C0), reads=[t_r[1]], writes=[t_r[2]])
                    kb.op("dve", lambda v: v.tensor_tensor(out=t4[:, :], in0=t2[:, :], in1=t1[:, :], op=ALU.subtract),
                          reads=[t_r[0], t_r[1]], writes=[t_r[3]])
                    kb.op("act", lambda a: a.activation(out=t4[:, :], in_=t4[:, :], func=AF.Exp, scale=-C0), reads=[t_r[3]], writes=[t_r[3]])
                    kb.op("pe", lambda pe: pe.matmul(PS[3][:, 0:MT], lhsT=a2[:, hp * 128:(hp + 1) * 128], rhs=AL[:, :], start=True, stop=True),
                          reads=[p_r, AL_r], writes=[PS_r[3]])
                    kb.op("act", lambda a: a.activation(out=t5[:, :], in_=PS[3][:, 0:MT], func=AF.Sigmoid, bias=ptb[:, l, 15 + hp:16 + hp]),
                          reads=[PS_r[3], c_r], writes=[t_r[4]])
                    kb.op("dve", lambda v: v.tensor_scalar(out=t6[:, :], in0=kT[:, :], scalar1=ptb[:, l, 19 + hp:20 + hp], scalar2=None, op0=ALU.mult),
                          reads=[kT_r, c_r], writes=[t_r[5]])
                    kb.op("act", lambda a: a.activation(out=t7[:, :], in_=t6[:, :], func=AF.Square), reads=[t_r[5]], writes=[t_r[6]])
                    kb.op("pe", lambda pe: pe.matmul(PS[4][:, 0:MT], lhsT=blkf[:, :], rhs=t7[:, :], start=True, stop=True),
                          reads=[p_r, t_r[6]], writes=[PS_r[4]])
                    kb.op("act", lambda a: a.activation(out=t7[:, :], in_=PS[4][:, 0:MT], func=AF.Sqrt), reads=[PS_r[4], t_r[6]], writes=[t_r[6]])
                    kb.op("dve", lambda v: v.tensor_scalar(out=t7[:, :], in0=t7[:, :], scalar1=1e-12, scalar2=None, op0=ALU.max),
                          reads=[t_r[6]], writes=[t_r[6]])
                    kb.op("dve", lambda v: v.reciprocal(out=t7[:, :], in_=t7[:, :]), reads=[t_r[6]], writes=[t_r[6]])
                    kb.op("dve", lambda v: v.tensor_tensor(out=t6[:, :], in0=t6[:, :], in1=t7[:, :], op=ALU.mult),
                          reads=[t_r[5], t_r[6]], writes=[t_r[5]])
                    kb.op("dve", lambda v: v.tensor_scalar(out=t7[:, :], in0=t5[:, :], scalar1=-1.0, scalar2=ptb[:, l, 21 + hp:22 + hp],
                                                           op0=ALU.add, op1=ALU.mult), reads=[t_r[4], t_r[6], c_r], writes=[t_r[6]])
                    kb.op("dve", lambda v: v.scalar_tensor_tensor(out=t7[:, :], in0=t7[:, :], scalar=1.0, in1=kT[:, :], op0=ALU.add, op1=ALU.mult),
                          reads=[t_r[6], kT_r], writes=[t_r[6]])
                    v3 = lambda ap: ap.rearrange("p (c i) -> p c i", c=NCH)
                    for hh in range(2):
                        pr = slice(hh * 64, hh * 64 + 64)
                        kb.op("dve", lambda v: v.scalar_tensor_tensor(out=AR[pr, hp, 0, :, hh, :], in0=v3(t6[pr, :]), scalar=-1.0, in1=v3(t4[pr, :]),
                                                                      op0=ALU.mult, op1=ALU.mult), reads=[t_r[5], t_r[3]], writes=[AR_r[hp]])
                        kb.op("dve", lambda v: v.tensor_tensor(out=AR[pr, hp, 1, :, hh, :], in0=v3(rT[pr, :]), in1=v3(Gam[pr, hp, :]), op=ALU.mult),
                              reads=[rT_r, Gam_r[hp]], writes=[AR_r[hp]])
                    kb.op("dve", lambda v: v.tensor_tensor(out=t1[:, :], in0=t6[:, :], in1=t5[:, :], op=ALU.mult),
                          reads=[t_r[5], t_r[4], t_r[0]], writes=[t_r[0]])
                    for hh in range(2):
                        pr = slice(hh * 64, hh * 64 + 64)
                        kb.op("dve", lambda v: v.tensor_tensor(out=BK[pr, hp, 0, :, hh, :], in0=v3(t1[pr, :]), in1=v3(t3[pr, :]), op=ALU.mult),
                              reads=[t_r[0], t_r[2]], writes=[BK_r[hp]])
                        kb.op("dve", lambda v: v.tensor_tensor(out=BK[pr, hp, 1, :, hh, :], in0=v3(t7[pr, :]), in1=v3(t3[pr, :]), op=ALU.mult),
                              reads=[t_r[6], t_r[2]], writes=[BK_r[hp]])
                        kb.op("dve", lambda v: v.scalar_tensor_tensor(out=rkrz[pr, hp, :, hh, :], in0=v3(rT[pr, :]), scalar=ptb[pr, l, 23 + hp:24 + hp],
                                                                      in1=v3(t7[pr, :]), op0=ALU.mult, op1=ALU.mult),
                              reads=[rT_r, t_r[6], c_r], writes=[rkrz_r[hp]])
                    if l == 0:
                        kb.dma("sp", vfirst_d.ap()[hp, :, sq * 0 + tok0:tok0 + MT], vT[:, :], reads=[vT_r], writes=[vfirst_r[hp]])
                    else:
                        kb.op("pe", lambda pe: pe.matmul(PS[5][:, 0:MT], lhsT=v2[:, hp * 128:(hp + 1) * 128], rhs=VR[:, :], start=True, stop=True),
                              reads=[p_r, VR_r], writes=[PS_r[5]])
                        kb.op("act", lambda a: a.activation(out=t1[:, :], in_=PS[5][:, 0:MT], func=AF.Sigmoid, bias=ptb[:, l, 17 + hp:18 + hp]),
                              reads=[PS_r[5], c_r, t_r[0]], writes=[t_r[0]])
                        kb.dma("sp", t2[:, :], vfirst_d.ap()[hp, :, tok0:tok0 + MT], reads=[vfirst_r[hp], t_r[1]], writes=[t_r[1]])
                        kb.op("dve", lambda v: v.tensor_tensor(out=t2[:, :], in0=t2[:, :], in1=vT[:, :], op=ALU.subtract),
                              reads=[t_r[1], vT_r], writes=[t_r[1]])
                        kb.op("dve", lambda v: v.tensor_tensor(out=t2[:, :], in0=t2[:, :], in1=t1[:, :], op=ALU.mult),
                              reads=[t_r[1], t_r[0]], writes=[t_r[1]])
                        kb.op("dve", lambda v: v.tensor_tensor(out=vT[:, :], in0=vT[:, :], in1=t2[:, :], op=ALU.add),
                              reads=[t_r[1], vT_r], writes=[vT_r])
                    for hh in range(2):
                        pr = slice(hh * 64, hh * 64 + 64)
                        kb.op("act", lambda a: a.activation(out=VTz[pr, hp, :, hh, :], in_=vT[pr, :].rearrange("p (c i) -> p c i", c=NCH), func=AF.Copy),
                              reads=[vT_r], writes=[VTz_r[hp]])
                if rs_ <= 2:
                    return
                fl = lambda ap: ap.rearrange("p a b -> p (a b)")
                def fn(pe):
                    inst = None
                    for c in range(NCH):
                        for hp in range(2):
                            inst = pe.matmul(PS[2][:, (c * 2 + hp) * 64:(c * 2 + hp + 1) * 64], lhsT=fl(VTz[:, hp, c, :, :]), rhs=istk[:, :], start=True, stop=True)
                    return inst
                kb.op("pe", fn, reads=VTz_r + [p_r], writes=[PS_r[2]])
                kb.op("act", lambda a: a.activation(out=Vstk[:, :, :, :], in_=PS[2][:, :].rearrange("p (c h d) -> p c h d", c=NCH, h=2), func=AF.Copy),
                      reads=[PS_r[2]], writes=[Vstk_r, PS_r[2]])

                def fn(pe):
                    inst = None
                    for c in range(NCH):
                        for hp in range(2):
                            inst = pe.matmul(PS[3][:, c * 2 + hp:c * 2 + hp + 1], lhsT=fl(rkrz[:, hp, c, :, :]), rhs=onec[:, :], start=True, stop=True)
                    return inst
                kb.op("pe", fn, reads=rkrz_r + [p_r], writes=[PS_r[3]])
                kb.op("act", lambda a: a.activation(out=cb[:, :, :], in_=PS[3][:, 0:NCH * 2].rearrange("p (c h) -> p c h", c=NCH), func=AF.Copy),
                      reads=[PS_r[3]], writes=[cb_r, PS_r[3]])
                for bk, dst, dst_r, pbi in ((0, Bblk, Bblk_r, 0), (1, Kblk, Kblk_r, 1)):
                    def fn(pe, bk=bk, pbi=pbi):
                        inst = None
                        for c in range(NCH):
                            for hp in range(2):
                                inst = pe.transpose(out=PB[pbi][:, (c * 2 + hp) * 128:(c * 2 + hp + 1) * 128], in_=fl(BK[:, hp, bk, c, :, :]), identity=ident[:, :])
                        return inst
                    kb.op("pe", fn, reads=BK_r + [c_r], writes=[PB_r[pbi]])
                    kb.op("dve", lambda v, dst=dst, pbi=pbi: v.tensor_copy(out=dst[:, :, :, :], in_=PB[pbi][:, :].rearrange("p (c h d) -> p c h d", c=NCH, h=2)),
                          reads=[PB_r[pbi]], writes=[dst_r, PB_r[pbi]])
                for c2 in range(NCH // 2):
                    def fn(pe):
                        inst = None
                        for cc_ in range(2):
                            c = c2 * 2 + cc_
                            pe.matmul(PS[3][:, cc_ * 256:(cc_ + 1) * 256], lhsT=fl(SGLd[:, c, :, :]), rhs=g2a[:, :], start=True, stop=False)
                            inst = pe.matmul(PS[3][:, cc_ * 256:(cc_ + 1) * 256], lhsT=fl(SGL2d[:, c, :, :]), rhs=g2b[:, :], start=False, stop=True)
                        return inst
                    kb.op("pe", fn, reads=[SGL_r, SGL2_r, p_r], writes=[PS_r[3]])
                    for hh in range(2):
                        pr = slice(hh * 64, hh * 64 + 64)
                        kb.op("act", lambda a: a.activation(out=gtok[pr, c2 * 2:c2 * 2 + 2, :, :],
                                                            in_=PS[3][pr, :].rearrange("p (c h g d) -> p c h g d", c=2, h=2, g=2)[:, :, :, hh, :], func=AF.Copy),
                              reads=[PS_r[3]], writes=[gtok_r, PS_r[3]])
                for c in range(NCH):
                    for hp in range(2):
                        pu = c * 2 + hp
                        px, py = (4, 5) if pu % 2 == 0 else (2, 3)

                        def fn(pe, px=px, py=py):
                            pe.matmul(PS[px][:, 0:128], lhsT=fl(AR[:, hp, 0, c, :, :]), rhs=fl(BK[:, hp, 0, c, :, :]), start=True, stop=True)
                            pe.matmul(PS[px][:, 128:384], lhsT=fl(BK[:, hp, 0, c, :, :]), rhs=AR[:, hp, :, c, :, :].rearrange("p a h i -> p a (h i)"),
                                      start=True, stop=True)
                            return pe.matmul(PS[py][:, 0:256], lhsT=fl(BK[:, hp, 1, c, :, :]), rhs=AR[:, hp, :, c, :, :].rearrange("p a h i -> p a (h i)"),
                                             start=True, stop=True)
                        kb.op("pe", fn, reads=AR_r + BK_r, writes=[PS_r[px], PS_r[py]])
                        kb.op("dve", lambda v, px=px: v.tensor_tensor(out=MN[0][:, pu, :, :], in0=PS[px][:, 0:256].rearrange("p (a b) -> p a b", a=2),
                                                                      in1=m3[:, 0:2, :], op=ALU.mult), reads=[PS_r[px], p_r], writes=[MN_r[0], PS_r[px]])
                        kb.op("dve", lambda v, px=px: v.tensor_tensor(out=A3[:, pu, 0, :], in0=PS[px][:, 256:384], in1=m3[:, 2, :], op=ALU.mult),
                              reads=[PS_r[px], p_r], writes=[A3_r, PS_r[px]])
                        kb.op("dve", lambda v, py=py: v.tensor_tensor(out=A3[:, pu, 1:3, :], in0=PS[py][:, 0:256].rearrange("p (a b) -> p a b", a=2),
                                                                      in1=m3[:, 1:3, :], op=ALU.mult), reads=[PS_r[py], p_r], writes=[A3_r, PS_r[py]])
                if rs_ <= 3:
                    return
                kb.op("dve", lambda v: v.tensor_tensor(out=Pm[:, :, :], in0=MN[0][:, :, 1, :],
                                                      in1=bass.AP(identf, 0, [[128, 128], [0, PU], [1, 128]]), op=ALU.add),
                      reads=[MN_r[0], c_r], writes=[P_r])
                cur = 0
                for lev in range(5):
                    nxt = 1 - cur
                    lastlev = (lev == 4)
                    for g2_ in range(PU // 2):
                        pi = 4 + (g2_ % 2)

                        def fn(pe, pi=pi):
                            inst = None
                            for q in range(2):
                                u = g2_ * 2 + q
                                inst = pe.matmul(PS[pi][:, q * 256:q * 256 + 128], lhsT=MN[cur][:, u, 1, :], rhs=MN[cur][:, u, 0, :], start=True, stop=True)
                                if not lastlev:
                                    inst = pe.matmul(PS[pi][:, q * 256 + 128:q * 256 + 256], lhsT=MN[cur][:, u, 0, :], rhs=MN[cur][:, u, 1, :],
                                                     start=True, stop=True)
                            return inst
                        kb.op("pe", fn, reads=[MN_r[cur]], writes=[PS_r[pi]])
                        kb.op("act", lambda a, pi=pi: a.activation(out=MN[nxt][:, g2_ * 2:g2_ * 2 + 2, :, :],
                                                                   in_=PS[pi][:, :].rearrange("p (u a b) -> p u a b", u=2, a=2), func=AF.Copy),
                              reads=[PS_r[pi]], writes=[MN_r[nxt], PS_r[pi]])
                    for g4_ in range(PU // 4):
                        pi = 2 + (g4_ % 2)

                        def fn(pe, pi=pi):
                            inst = None
                            for q in range(4):
                                u = g4_ * 4 + q
                                inst = pe.matmul(PS[pi][:, q * 128:(q + 1) * 128], lhsT=MN[nxt][:, u, 0, :], rhs=Pm[:, u, :], start=True, stop=True)
                            return inst
                        kb.op("pe", fn, reads=[MN_r[nxt], P_r], writes=[PS_r[pi]])
                        kb.op("dve", lambda v, pi=pi: v.tensor_tensor(out=Pm[:, g4_ * 4:g4_ * 4 + 4, :], in0=PS[pi][:, :].rearrange("p (u b) -> p u b", u=4),
                                                                      in1=Pm[:, g4_ * 4:g4_ * 4 + 4, :], op=ALU.add), reads=[PS_r[pi], P_r], writes=[P_r, PS_r[pi]])
                    cur = nxt

            def chain(mt):
                tok0 = mt * MT
                par = mt % 2
                AR, Vstk, Bblk, Kblk, Gam, gtok, cb, Pm, A3 = AR2[par], Vstk2[par], Bblk2[par], Kblk2[par], Gam2[par], gtok2[par], cb2[par], Pm2[par], A32[par]
                AR_r, Vstk_r, Bblk_r, Kblk_r, Gam_r, gtok_r, cb_r, P_r, A3_r = AR_r2[par], Vstk_r2[par], Bblk_r2[par], Kblk_r2[par], Gam_r2[par], gtok_r2[par], cb_r2[par], P_r2[par], A3_r2[par]
                fl = lambda ap: ap.rearrange("p a b -> p (a b)")
                if rs_ <= 4:
                    return
                for c in range(NCH):
                    def fn(pe):
                        inst = None
                        for hp in range(2):
                            pu = c * 2 + hp
                            pe.matmul(PS[0][:, hp * 64:(hp + 1) * 64], lhsT=A3[:, pu, 1, :], rhs=Vstk[:, c, hp, :], start=True, stop=False)
                            inst = pe.matmul(PS[0][:, hp * 64:(hp + 1) * 64], lhsT=fl(AR[:, hp, 0, c, :, :]), rhs=Tb[:, hp, :], start=False, stop=True)
                        return inst
                    kb.op("pe", fn, reads=[A3_r, Vstk_r, Tb_r] + AR_r, writes=[PS_r[0]])
                    kb.op("act", lambda a: a.activation(out=Rs[:, :], in_=PS[0][:, 0:128], func=AF.Copy), reads=[PS_r[0]], writes=[Rs_r, PS_r[0]])

                    def fn(pe):
                        inst = None
                        for hp in range(2):
                            pu = c * 2 + hp
                            inst = pe.matmul(PS[1][:, hp * 64:(hp + 1) * 64], lhsT=Pm[:, pu, :], rhs=Rs[:, hp * 64:(hp + 1) * 64], start=True, stop=True)
                        return inst
                    kb.op("pe", fn, reads=[P_r, Rs_r], writes=[PS_r[1]])
                    kb.op("act", lambda a: a.activation(out=Ub[:, :], in_=PS[1][:, 0:128], func=AF.Copy), reads=[PS_r[1]], writes=[Ub_r, PS_r[1]])

                    def fn(pe):
                        inst = None
                        for hp in range(2):
                            pu = c * 2 + hp
                            pe.matmul(PS[0][:, hp * 64:(hp + 1) * 64], lhsT=fl(AR[:, hp, 1, c, :, :]), rhs=Tb[:, hp, :], start=True, stop=False)
                            pe.matmul(PS[0][:, hp * 64:(hp + 1) * 64], lhsT=A3[:, pu, 0, :], rhs=Ub[:, hp * 64:(hp + 1) * 64], start=False, stop=False)
                            inst = pe.matmul(PS[0][:, hp * 64:(hp + 1) * 64], lhsT=A3[:, pu, 2, :], rhs=Vstk[:, c, hp, :], start=False, stop=True)
                        return inst
                    kb.op("pe", fn, reads=[A3_r, Vstk_r, Tb_r, Ub_r] + AR_r, writes=[PS_r[0]])

                    def fn(pe):
                        inst = None
                        for hp in range(2):
                            pe.matmul(PS[1][:, hp * 64:(hp + 1) * 64], lhsT=Bblk[:, c, hp, :], rhs=Ub[:, hp * 64:(hp + 1) * 64], start=True, stop=False)
                            inst = pe.matmul(PS[1][:, hp * 64:(hp + 1) * 64], lhsT=Kblk[:, c, hp, :], rhs=Vstk[:, c, hp, :], start=False, stop=True)
                        return inst
                    kb.op("pe", fn, reads=[Bblk_r, Kblk_r, Vstk_r, Ub_r], writes=[PS_r[1]])
                    kb.op("dve", lambda v: v.tensor_tensor(out=tmpS[:, :, :], in0=PS[1][:, 0:128].rearrange("p (h d) -> p h d", h=2), in1=Tst[:, :, :], op=ALU.add),
                          reads=[PS_r[1], T_r], writes=[tmpS_r, PS_r[1]])
                    kb.op("dve", lambda v: v.tensor_tensor(out=Tst[:, :, :], in0=tmpS[:, :, :],
                                                          in1=bass.AP(Gam, c * 64 + 63, [[2 * MT, 128], [MT, 2], [0, 64]]), op=ALU.mult),
                          reads=[tmpS_r, Gam_r[0], Gam_r[1]], writes=[T_r])
                    kb.op("act", lambda a: a.activation(out=Tb[:, :, :], in_=Tst[:, :, :], func=AF.Copy), reads=[T_r], writes=[Tb_r])
                    if rs_ <= 5:
                        continue
                    v2_ = lambda ap: ap.rearrange("p (h d) -> p h d", h=2)
                    kb.op("act", lambda a: a.activation(out=ow[:, :], in_=PS[0][:, 0:128], func=AF.Copy), reads=[PS_r[0]], writes=[ow_r, PS_r[0]])
                    kb.op("act", lambda a: a.activation(out=ow2[:, :], in_=ow[:, :], func=AF.Square), reads=[ow_r], writes=[ow2_r])
                    kb.op("dve", lambda v: v.reduce_sum(out=st[:, 0:2], in_=v2_(ow[:, :]), axis=AX.X), reads=[ow_r], writes=[st_r])
                    kb.op("dve", lambda v: v.reduce_sum(out=st[:, 2:4], in_=v2_(ow2[:, :]), axis=AX.X), reads=[ow2_r, st_r], writes=[st_r])
                    kb.op("dve", lambda v: v.tensor_scalar(out=st[:, 4:8], in0=st[:, 0:4], scalar1=1.0 / 64.0, scalar2=None, op0=ALU.mult), reads=[st_r], writes=[st_r])
                    kb.op("dve", lambda v: v.tensor_tensor(out=st[:, 8:10], in0=st[:, 4:6], in1=st[:, 4:6], op=ALU.mult), reads=[st_r], writes=[st_r])
                    kb.op("dve", lambda v: v.tensor_tensor(out=st[:, 10:12], in0=st[:, 6:8], in1=st[:, 8:10], op=ALU.subtract), reads=[st_r], writes=[st_r])
                    kb.op("act", lambda a: a.activation(out=st[:, 12:14], in_=st[:, 10:12], func=AF.Sqrt, bias=RW_EPS, scale=1.0), reads=[st_r], writes=[st_r])
                    kb.op("dve", lambda v: v.reciprocal(out=st[:, 14:16], in_=st[:, 12:14]), reads=[st_r], writes=[st_r])
                    for hp in range(2):
                        kb.op("dve", lambda v: v.tensor_scalar(out=ow[:, hp * 64:(hp + 1) * 64], in0=ow[:, hp * 64:(hp + 1) * 64],
                                                               scalar1=st[:, 4 + hp:5 + hp], scalar2=st[:, 14 + hp:15 + hp],
                                                               op0=ALU.subtract, op1=ALU.mult), reads=[ow_r, st_r], writes=[ow_r])
                    kb.op("dve", lambda v: v.tensor_tensor(out=v2_(ow[:, :]), in0=v2_(ow[:, :]), in1=rln[:, 0, :, :], op=ALU.mult), reads=[ow_r, p_r], writes=[ow_r])
                    kb.op("dve", lambda v: v.tensor_tensor(out=v2_(ow[:, :]), in0=v2_(ow[:, :]), in1=rln[:, 1, :, :], op=ALU.add), reads=[ow_r, p_r], writes=[ow_r])
                    if rs_ <= 6:
                        continue
                    for hp in range(2):
                        kb.op("dve", lambda v: v.scalar_tensor_tensor(out=ow[:, hp * 64:(hp + 1) * 64], in0=Vstk[:, c, hp, :], scalar=cb[:, c, hp:hp + 1],
                                                                      in1=ow[:, hp * 64:(hp + 1) * 64], op0=ALU.mult, op1=ALU.add),
                              reads=[Vstk_r, cb_r, ow_r], writes=[ow_r])
                    for hh in range(2):
                        pr = slice(hh * 64, hh * 64 + 64)
                        kb.op("dve", lambda v: v.tensor_tensor(out=obblk[pr, :, hh * 64:(hh + 1) * 64], in0=v2_(ow[pr, :]), in1=gtok[pr, c, :, :], op=ALU.mult),
                              reads=[ow_r, gtok_r], writes=[ob_r])

                    if rs_ <= 7:
                        continue

                    def fn(pe):
                        inst = None
                        for hp in range(2):
                            inst = pe.matmul(PS[1][:, 256 + hp * 64:256 + (hp + 1) * 64], lhsT=obblk[:, hp, :], rhs=istk[:, :], start=True, stop=True)
                        return inst
                    kb.op("pe", fn, reads=[ob_r, p_r], writes=[PS_r[1]])
                    tokc = tok0 + c * 64
                    kb.op("act", lambda a: a.activation(out=OT[:, 2:4, tokc:tokc + 64], in_=PS[1][:, 256:384].rearrange("p (h t) -> p h t", h=2), func=AF.Copy),
                          reads=[PS_r[1]], writes=[OT_r[tokc // 128], PS_r[1]])
            nmt = S // MT if rs_ > 0 else 0
            if nmt:
                prep(0)
            for mt in range(nmt):
                if mt + 1 < nmt:
                    prep(mt + 1)
                chain(mt)
            kb.barrier()

    NQ, NQS, NKC, NKS, NKW, NVC, NVS, NGT = 0, 512, 1024, 1280, 1536, 1792, 1920, 2176
    GK = 1.5957691216057308

    def nsa_phase(sq, l, OT, OT_r):
        ns_ = cfg.get("nsa_stop", 99)
        with ExitStack() as es:
            A = lambda n_, s_, d_: es.enter_context(sb(n_, s_, d_))
            QT = A("nQT", [128, 4, S], BF16)
            KTz = A("nKTz", [128, 3, 2, S], BF16)
            VCz = A("nVCz", [128, 2, S], BF16)
            VS = A("nVS", [128, NT, 2, 65], BF16)
            VW = A("nVW", [128, NT, 2, 65], BF16)
            gsig = A("ngsig", [128, NT, 24], F32)
            kcmpTz = A("nkcmp", [128, 2, 128], BF16)
            vcmp = A("nvcmp", [128, 2, 97], BF16)
            QT_r, KT_r, VC_r, VS_r, VW_r, gs_r = regs(4), regs(4), regs(4), regs(NT), regs(NT), regs(NT)
            kc_r, vcm_r = Reg(), Reg()
            kb.op("pool", lambda g: g.memset(KTz[:], 0.0), writes=KT_r)
            kb.op("pool", lambda g: g.memset(VCz[:], 0.0), writes=VC_r)
            kb.op("pool", lambda g: g.memset(VS[:, :, :, 64:65], 1.0), writes=VS_r)
            kb.op("pool", lambda g: g.memset(VW[:, :, :, 64:65], 1.0), writes=VW_r)
            kb.op("pool", lambda g: g.memset(kcmpTz[:], 0.0), writes=[kc_r])
            kb.op("pool", lambda g: g.memset(vcmp[:], 0.0), writes=[vcm_r])
            with ExitStack() as es1:
                A1 = lambda n_, s_, d_: es1.enter_context(sb(n_, s_, d_))
                Wn = A1("nWn", [128, 8, 2200], BF16)
                rp = A1("nrp", [128, 2, 512], F32)
                t1 = A1("nt1", [128, 512], F32)
                t2 = A1("nt2", [128, 512], F32)
                W_r, rp_r, t1_r, t2_r = Reg(), Reg(), Reg(), Reg()
                load_w(Wn, w_nsa16.ap()[l], 2200, W_r)
                for mt in range(4):
                    tok0 = mt * 512
                    bl = slice(tok0, tok0 + 512)
                    xr = XT_r[mt * 4:mt * 4 + 4]
                    kb.dma("sp", rp[:, 0, :], c_rope.ap()[0][:, bl], writes=[rp_r])
                    kb.dma("sp", rp[:, 1, :], c_rope.ap()[1][:, bl], writes=[rp_r])
                    for i in range(4):
                        proj_fm(PS[0][:, :], Wn, NQ + i * 128, 128, tok0, 512, [W_r], xr, PS_r[0])
                        proj_fm(PS[1][:, :], Wn, NQS + i * 128, 128, tok0, 512, [W_r], xr, PS_r[1])
                        kb.op("dve", lambda v: v.tensor_tensor(out=t1[:, :], in0=PS[0][:, :], in1=rp[:, 0, :], op=ALU.mult),
                              reads=[PS_r[0], rp_r], writes=[t1_r])
                        kb.op("dve", lambda v: v.scalar_tensor_tensor(out=t2[:, :], in0=PS[1][:, :], scalar=0.125, in1=rp[:, 1, :],
                                                                      op0=ALU.mult, op1=ALU.mult), reads=[PS_r[1], rp_r], writes=[t2_r])
                        kb.op("dve", lambda v: v.scalar_tensor_tensor(out=QT[:, i, bl], in0=t1[:, :], scalar=0.125, in1=t2[:, :],
                                                                      op0=ALU.mult, op1=ALU.add), reads=[t1_r, t2_r], writes=[QT_r[mt]])
                    for ty, c0 in ((0, NKC), (1, NKS), (2, NKW)):
                        proj_fm(PS[0][:, :], Wn, c0, 128, tok0, 512, [W_r], xr, PS_r[0])
                        proj_fm(PS[1][:, :], Wn, c0 + 128, 128, tok0, 512, [W_r], xr, PS_r[1])
                        kb.op("dve", lambda v: v.tensor_tensor(out=t1[:, :], in0=PS[0][:, :], in1=rp[:, 0, :], op=ALU.mult),
                              reads=[PS_r[0], rp_r], writes=[t1_r])
                        kb.op("dve", lambda v: v.tensor_tensor(out=t2[:, :], in0=PS[1][:, :], in1=rp[:, 1, :], op=ALU.mult),
                              reads=[PS_r[1], rp_r], writes=[t2_r])
                        for g in range(2):
                            pr = slice(g * 64, g * 64 + 64)
                            kb.op("dve", lambda v: v.tensor_tensor(out=KTz[pr, ty, g, bl], in0=t1[pr, :], in1=t2[pr, :], op=ALU.add),
                                  reads=[t1_r, t2_r], writes=[KT_r[mt]])
                    proj_fm(PS[2][:, :], Wn, NVC, 128, tok0, 512, [W_r], xr, PS_r[2])
                    for g in range(2):
                        pr = slice(g * 64, g * 64 + 64)
                        kb.op("act", lambda a: a.activation(out=VCz[pr, g, bl], in_=PS[2][pr, :], func=AF.Copy),
                              reads=[PS_r[2]], writes=[VC_r[mt]])
                    for j in range(4):
                        tt = mt * 4 + j
                        proj_tm(PS[3][:, 0:256], Wn, NVS, 256, tt, [W_r], PS_r[3])
                        kb.op("act", lambda a: a.activation(out=VS[:, tt, :, 0:64], in_=PS[3][:, 0:128].rearrange("p (g d) -> p g d", g=2), func=AF.Copy),
                              reads=[PS_r[3]], writes=[VS_r[tt], PS_r[3]])
                        kb.op("act", lambda a: a.activation(out=VW[:, tt, :, 0:64], in_=PS[3][:, 128:256].rearrange("p (g d) -> p g d", g=2), func=AF.Copy),
                              reads=[PS_r[3]], writes=[VW_r[tt], PS_r[3]])
                        proj_tm(PS[4][:, 0:24], Wn, NGT, 24, tt, [W_r], PS_r[4])
                        kb.op("act", lambda a: a.activation(out=gsig[:, tt, :], in_=PS[4][:, 0:24], func=AF.Sigmoid),
                              reads=[PS_r[4]], writes=[gs_r[tt]])
                kb.barrier()
            if ns_ <= 1:
                kb.barrier()
                return
            with ExitStack() as es2:
                A2 = lambda n_, s_, d_: es2.enter_context(sb(n_, s_, d_))
                w1d = A2("nw1d", [128, 32, 256], BF16)
                w2d = A2("nw2d", [128, 2, 128], BF16)
                wv2 = A2("nwv2", [128, 2, 64], BF16)
                posz = A2("nposz", [128, 2, 32], BF16)
                hc = A2("nhc", [128, 2], F32)
                gx = A2("ngx", [128, 128], F32)
                gw = A2("ngw", [128, 128], F32)
                gh = A2("ngh", [128, 2, 128], BF16)
                w1_r, w2_r, hc_r, gx_r, gw_r, gh_r = Reg(), Reg(), Reg(), Reg(), Reg(), Reg()
                kb.op("pool", lambda g: g.memset(posz[:], 0.0), writes=[w2_r])
                kb.op("pool", lambda g: g.memset(gh[:], 0.0), writes=[gh_r])
                for kv in range(2):
                    kb.dma("pool", posz[0:64, kv, :], nsa_posT.ap()[l, kv], reads=[w2_r], writes=[w2_r])
                for half in range(2):
                    kb.dma("pool", w2d[:, :, half * 64:(half + 1) * 64], nsa_wk2.ap()[l].rearrange("(t p) n -> p t n", p=128), writes=[w2_r])
                kb.dma("pool", wv2[:, :, :], nsa_wv2.ap()[l].rearrange("(t p) n -> p t n", p=128), writes=[w2_r])
                kb.dma("pool", vcmp[:, 0, 65:97], c_ovl.ap(), reads=[vcm_r], writes=[vcm_r])
                kb.dma("pool", vcmp[:, 1, 65:97], c_ovl.ap(), reads=[vcm_r], writes=[vcm_r])
                kb.op("pool", lambda g: g.memset(vcmp[0:127, :, 64:65], 1.0), reads=[vcm_r], writes=[vcm_r])
                for kv, w1src in ((0, wk1_16), (1, wv1_16)):
                    src3 = w1src.ap()[l].rearrange("(l d) n -> d l n", d=64)
                    for half in range(2):
                        for l4 in range(8):
                            kb.dma("sp", w1d[half * 64:(half + 1) * 64, l4 * 4:(l4 + 1) * 4, :], src3[:, l4 * 4:(l4 + 1) * 4, :],
                                   reads=[w16_r], writes=[w1_r])
                    srcz = (lambda g: KTz[:, 0, g, :]) if kv == 0 else (lambda g: VCz[:, g, :])
                    src_regs = KT_r if kv == 0 else VC_r
                    for hf in range(2):
                        def fn(pe):
                            inst = None
                            for ll in range(32):
                                inst = pe.matmul(PS[0][:, 0:1], lhsT=w1d[:, ll, hf * 128:(hf + 1) * 128], rhs=posz[:, kv, ll:ll + 1],
                                                 start=(ll == 0), stop=(ll == 31))
                            return inst
                        kb.op("pe", fn, reads=[w1_r, w2_r], writes=[PS_r[0]])
                        kb.op("act", lambda a: a.activation(out=hc[:, hf:hf + 1], in_=PS[0][:, 0:1], func=AF.Copy), reads=[PS_r[0]], writes=[hc_r])
                    for g in range(2):
                        for hf in range(2):
                            def fn(pe):
                                inst = None
                                for ll in range(32):
                                    rhs = bass.AP(srcz(g).tensor, srcz(g).offset + ll, [srcz(g).ap[0], [16, 127]])
                                    inst = pe.matmul(PS[1][:, 0:127], lhsT=w1d[:, ll, hf * 128:(hf + 1) * 128], rhs=rhs,
                                                     start=(ll == 0), stop=(ll == 31))
                                return inst
                            kb.op("pe", fn, reads=[w1_r] + src_regs, writes=[PS_r[1]])
                            kb.op("act", lambda a: a.activation(out=gx[:, 0:127], in_=PS[1][:, 0:127], func=AF.Identity, bias=hc[:, hf:hf + 1], scale=1.0),
                                  reads=[PS_r[1], hc_r], writes=[gx_r])
                            kb.op("dve", lambda v: v.tensor_tensor(out=gw[:, 0:127], in0=gx[:, 0:127], in1=gx[:, 0:127], op=ALU.mult),
                                  reads=[gx_r], writes=[gw_r])
                            kb.op("dve", lambda v: v.tensor_scalar(out=gw[:, 0:127], in0=gw[:, 0:127], scalar1=0.044715, scalar2=1.0,
                                                                   op0=ALU.mult, op1=ALU.add), reads=[gw_r], writes=[gw_r])
                            kb.op("dve", lambda v: v.tensor_tensor(out=gw[:, 0:127], in0=gw[:, 0:127], in1=gx[:, 0:127], op=ALU.mult),
                                  reads=[gw_r, gx_r], writes=[gw_r])
                            kb.op("act", lambda a: a.activation(out=gw[:, 0:127], in_=gw[:, 0:127], func=AF.Sigmoid, scale=GK), reads=[gw_r], writes=[gw_r])
                            kb.op("dve", lambda v: v.tensor_tensor(out=gh[:, hf, 0:127], in0=gx[:, 0:127], in1=gw[:, 0:127], op=ALU.mult),
                                  reads=[gw_r, gx_r], writes=[gh_r])
                        if kv == 0:
                            def fn(pe):
                                pe.matmul(PS[2][:, 0:127], lhsT=w2d[:, 0, :], rhs=gh[:, 0, 0:127], start=True, stop=False)
                                return pe.matmul(PS[2][:, 0:127], lhsT=w2d[:, 1, :], rhs=gh[:, 1, 0:127], start=False, stop=True)
                            kb.op("pe", fn, reads=[w2_r, gh_r], writes=[PS_r[2]])
                            pr = slice(g * 64, g * 64 + 64)
                            kb.op("act", lambda a: a.activation(out=kcmpTz[pr, g, 0:127], in_=PS[2][pr, 0:127], func=AF.Copy),
                                  reads=[PS_r[2], kc_r], writes=[kc_r])
                        else:
                            def fn(pe):
                                pe.matmul(PS[2][:, 0:64], lhsT=gh[:, 0, :], rhs=wv2[:, 0, :], start=True, stop=False)
                                return pe.matmul(PS[2][:, 0:64], lhsT=gh[:, 1, :], rhs=wv2[:, 1, :], start=False, stop=True)
                            kb.op("pe", fn, reads=[w2_r, gh_r], writes=[PS_r[2]])
                            kb.op("act", lambda a: a.activation(out=vcmp[0:127, g, 0:64], in_=PS[2][0:127, 0:64], func=AF.Copy),
                                  reads=[PS_r[2], vcm_r], writes=[vcm_r])
                kb.barrier()
            if ns_ <= 2:
                kb.barrier()
                return
            selbT = A("nselbT", [128, 2, S], BF16)
            onehot = A("nonehot", [128, S], BF16)
            cmask = A("ncmask", [128, S], BF16)
            selm = A("nselm", [128, 2, NT, 32], F32)
            wmask = A("nwmask", [128, 128], F32)
            PT = [A("nPT%d" % i, [128, 640], BF16) for i in range(2)]
            ex = A("nex", [128, 512], F32)
            ocs = A("nocs", [128, 4, 97], F32)
            ONSA2 = [A("nONSA%d" % i_, [128, 4, 512], F32) for i_ in range(2)]
            impb2 = [A("nimp%d" % i_, [128, 4, 2, 32], F32) for i_ in range(2)]
            sc = A("nsc", [128, 32], F32)
            cm3 = A("ncm3", [128, 32, 32], F32)
            sb16 = A("nsb16", [128, 32], BF16)
            dn = A("ndn", [128, 16], F32)
            obf = A("nobf", [128, 512], BF16)
            k_r, sel_r, PT_r, ex_r, ocs_r, sc_r, cm3_r, sb16_r, dn_r, obf_r = \
                Reg(), regs(4), regs(2), Reg(), Reg(), Reg(), Reg(), Reg(), Reg(), Reg()
            on_r2, imp_r2 = [regs(4), regs(4)], regs(2)
            kb.op("pool", lambda g: g.memset(selbT[:], 0.0), writes=sel_r)
            kb.op("pool", lambda g: g.memset(onehot[:], 0.0), writes=[k_r])
            kb.dma("pool", onehot[0:32, :], c_onehot.ap(), reads=[k_r], writes=[k_r])
            kb.dma("pool", cmask[:, :], c_cmpmask.ap(), writes=[k_r])
            for m_ in range(2):
                kb.dma("sp", selm[:, m_, :, :], c_selm.ap()[m_], writes=[k_r])
            kb.op("dve", lambda v: v.tensor_scalar(out=wmask[:, :], in0=caus[:, 0, :], scalar1=-1.0, scalar2=1.0, op0=ALU.mult, op1=ALU.add),
                  reads=[c_r], writes=[k_r])

            def branch_finish(acc_ps, acc_r, W, nq, h, b, tts, first, ONSA, on_r):
                kb.op("act", lambda a: a.activation(out=ocs[:, 0:nq, 0:W], in_=acc_ps.rearrange("p (q w) -> p q w", q=nq), func=AF.Copy),
                      reads=[acc_r], writes=[ocs_r, acc_r])
                kb.op("dve", lambda v: v.tensor_scalar(out=dn[:, 0:nq], in0=ocs[:, 0:nq, 64], scalar1=1e-30, scalar2=None, op0=ALU.max),
                      reads=[ocs_r], writes=[dn_r])
                kb.op("dve", lambda v: v.reciprocal(out=dn[:, 0:nq], in_=dn[:, 0:nq]), reads=[dn_r], writes=[dn_r])
                kb.op("dve", lambda v: v.tensor_tensor(out=dn[:, 8:8 + nq], in0=dn[:, 0:nq], in1=gsig[:, tts[0]:tts[0] + nq, h * 3 + b], op=ALU.mult),
                      reads=[dn_r] + gs_r[tts[0]:tts[0] + nq], writes=[dn_r])
                for qi in range(nq):
                    tl = tts[qi] % 4
                    if first:
                        kb.op("dve", lambda v: v.tensor_scalar(out=ONSA[:, tl, h * 64:(h + 1) * 64], in0=ocs[:, qi, 0:64], scalar1=dn[:, 8 + qi:9 + qi],
                                                               scalar2=None, op0=ALU.mult), reads=[ocs_r, dn_r], writes=[on_r[tl]])
                    else:
                        kb.op("dve", lambda v: v.scalar_tensor_tensor(out=ONSA[:, tl, h * 64:(h + 1) * 64], in0=ocs[:, qi, 0:64], scalar=dn[:, 8 + qi:9 + qi],
                                                                      in1=ONSA[:, tl, h * 64:(h + 1) * 64], op0=ALU.mult, op1=ALU.add),
                              reads=[ocs_r, dn_r, on_r[tl]], writes=[on_r[tl]])

            def nsa_a(qb):
                qc = slice(qb * 512, (qb + 1) * 512)
                tts = list(range(qb * 4, qb * 4 + 4))
                ONSA, on_r, impb, imp_r = ONSA2[qb % 2], on_r2[qb % 2], impb2[qb % 2], imp_r2[qb % 2]
                for h in range(8):
                    g, i = h // 4, h % 4
                    pi = h % 2
                    kb.op("pe", lambda pe: pe.matmul(PS[pi][:, :], lhsT=kcmpTz[:, g, :], rhs=QT[:, i, qc], start=True, stop=True),
                          reads=[kc_r, QT_r[qb]], writes=[PS_r[pi]])
                    kb.op("act", lambda a: a.activation(out=ex[:, :], in_=PS[pi][:, :], func=AF.Exp), reads=[PS_r[pi]], writes=[ex_r])
                    kb.op("dve", lambda v: v.tensor_tensor(out=PT[pi][:, 0:512], in0=ex[:, :], in1=cmask[:, qc], op=ALU.mult),
                          reads=[ex_r, k_r], writes=[PT_r[pi]])
                    ai = 2 + (h % 2)

                    def fn(pe):
                        inst = None
                        for q in range(4):
                            inst = pe.matmul(PS[ai][:, q * 97:(q + 1) * 97], lhsT=PT[pi][:, q * 128:(q + 1) * 128], rhs=vcmp[:, g, :], start=True, stop=True)
                        return inst
                    kb.op("pe", fn, reads=[PT_r[pi], vcm_r], writes=[PS_r[ai]])
                    branch_finish(PS[ai][:, 0:388], PS_r[ai], 97, 4, h, 0, tts, True, ONSA, on_r)
                    for q in range(4):
                        if i == 0:
                            kb.op("dve", lambda v: v.tensor_scalar(out=impb[:, q, g, :], in0=ocs[:, q, 65:97], scalar1=dn[:, q:q + 1], scalar2=None, op0=ALU.mult),
                                  reads=[ocs_r, dn_r], writes=[imp_r])
                        else:
                            kb.op("dve", lambda v: v.scalar_tensor_tensor(out=impb[:, q, g, :], in0=ocs[:, q, 65:97], scalar=dn[:, q:q + 1], in1=impb[:, q, g, :],
                                                                          op0=ALU.mult, op1=ALU.add), reads=[ocs_r, dn_r, imp_r], writes=[imp_r])
            def nsa_b(qb):
                qc = slice(qb * 512, (qb + 1) * 512)
                tts = list(range(qb * 4, qb * 4 + 4))
                ONSA, on_r, impb, imp_r = ONSA2[qb % 2], on_r2[qb % 2], impb2[qb % 2], imp_r2[qb % 2]
                for q in range(4):
                    tt = tts[q]
                    for g in range(2):
                        kb.op("dve", lambda v: v.tensor_tensor(out=sc[:, :], in0=impb[:, q, g, :], in1=selm[:, 0, tt, :], op=ALU.mult),
                              reads=[imp_r, k_r], writes=[sc_r])
                        kb.op("dve", lambda v: v.tensor_tensor(out=sc[:, :], in0=sc[:, :], in1=selm[:, 1, tt, :], op=ALU.add),
                              reads=[sc_r, k_r], writes=[sc_r])
                        kb.op("dve", lambda v: v.tensor_tensor(out=cm3[:, :, :], in0=bass.AP(sc, 0, [[32, 128], [0, 32], [1, 32]]),
                                                              in1=bass.AP(sc, 0, [[32, 128], [1, 32], [0, 32]]), op=ALU.is_gt),
                              reads=[sc_r], writes=[cm3_r])
                        kb.op("dve", lambda v: v.reduce_sum(out=sc[:, :], in_=cm3[:, :, :], axis=AX.X), reads=[cm3_r, sc_r], writes=[sc_r])
                        kb.op("dve", lambda v: v.tensor_scalar(out=sb16[:, :], in0=sc[:, :], scalar1=15.5, scalar2=-30000.0, op0=ALU.is_gt, op1=ALU.mult),
                              reads=[sc_r], writes=[sb16_r])
                        kb.op("pe", lambda pe: pe.transpose(out=PB[0][0:32, 0:128], in_=sb16[:, :], identity=ident[:, :]),
                              reads=[sb16_r, c_r], writes=[PB_r[0]])
                        kb.op("act", lambda a: a.activation(out=selbT[0:32, g, tt * 128:(tt + 1) * 128], in_=PB[0][0:32, 0:128], func=AF.Copy),
                              reads=[PB_r[0]], writes=[sel_r[qb]])
            def nsa_c(qb):
                qc = slice(qb * 512, (qb + 1) * 512)
                tts = list(range(qb * 4, qb * 4 + 4))
                ONSA, on_r, impb, imp_r = ONSA2[qb % 2], on_r2[qb % 2], impb2[qb % 2], imp_r2[qb % 2]
                for h in range(8):
                    g, i = h // 4, h % 4
                    nkt = 4 * qb + 4
                    for kt in range(nkt):
                        kc_ = slice(kt * 128, (kt + 1) * 128)
                        pi = kt % 2

                        def fn(pe):
                            pe.matmul(PS[pi][:, :], lhsT=KTz[:, 1, g, kc_], rhs=QT[:, i, qc], start=True, stop=False)
                            return pe.matmul(PS[pi][:, :], lhsT=onehot[:, kc_], rhs=selbT[:, g, qc], start=False, stop=True)
                        kb.op("pe", fn, reads=[KT_r[kt // 4], QT_r[qb], k_r, sel_r[qb]], writes=[PS_r[pi]])
                        kb.op("act", lambda a: a.activation(out=PT[pi][:, 0:512], in_=PS[pi][:, :], func=AF.Exp), reads=[PS_r[pi]], writes=[PT_r[pi]])
                        if kt >= 4 * qb:
                            ql = kt - 4 * qb
                            kb.op("dve", lambda v: v.tensor_tensor(out=PT[pi][:, ql * 128:(ql + 1) * 128], in0=PT[pi][:, ql * 128:(ql + 1) * 128],
                                                                  in1=caus[:, 0, :], op=ALU.mult), reads=[PT_r[pi], c_r], writes=[PT_r[pi]])
                        for q in range(4):
                            qt = 4 * qb + q
                            if qt < kt:
                                continue
                            kb.op("pe", lambda pe: pe.matmul(PS[2 + q][:, 0:65], lhsT=PT[pi][:, q * 128:(q + 1) * 128], rhs=VS[:, kt, g, :],
                                                             start=(kt == 0), stop=(kt == qt)), reads=[PT_r[pi], VS_r[kt]], writes=[PS_r[2 + q]])
                    for q in range(4):
                        branch_finish(PS[2 + q][:, 0:65], PS_r[2 + q], 65, 1, h, 1, [tts[q]], False, ONSA, on_r)
            def nsa_d(qb):
                qc = slice(qb * 512, (qb + 1) * 512)
                tts = list(range(qb * 4, qb * 4 + 4))
                ONSA, on_r, impb, imp_r = ONSA2[qb % 2], on_r2[qb % 2], impb2[qb % 2], imp_r2[qb % 2]
                for h in range(8):
                    g, i = h // 4, h % 4
                    for q in range(4):
                        qt = 4 * qb + q
                        qcs = slice(qt * 128, (qt + 1) * 128)
                        kts = [kt for kt in range(qt - 4, qt + 1) if kt >= 0]
                        pi = q % 2
                        main = [kt for kt in kts if kt >= qt - 3]

                        def fn(pe):
                            inst = None
                            for n_, kt in enumerate(main):
                                inst = pe.matmul(PS[pi][:, n_ * 128:(n_ + 1) * 128], lhsT=KTz[:, 2, g, kt * 128:(kt + 1) * 128], rhs=QT[:, i, qcs],
                                                 start=True, stop=True)
                            return inst
                        kb.op("pe", fn, reads=KT_r + [QT_r[qb]], writes=[PS_r[pi]])
                        nm = len(main)
                        kb.op("act", lambda a: a.activation(out=PT[pi][:, 0:nm * 128], in_=PS[pi][:, 0:nm * 128], func=AF.Exp), reads=[PS_r[pi]], writes=[PT_r[pi]])
                        kb.op("dve", lambda v: v.tensor_tensor(out=PT[pi][:, (nm - 1) * 128:nm * 128], in0=PT[pi][:, (nm - 1) * 128:nm * 128],
                                                              in1=caus[:, 0, :], op=ALU.mult), reads=[PT_r[pi], c_r], writes=[PT_r[pi]])
                        tail = (qt - 4 >= 0)
                        if tail:
                            kt = qt - 4
                            kb.op("pe", lambda pe: pe.matmul(PS[2 + pi][:, 0:128], lhsT=KTz[:, 2, g, kt * 128:(kt + 1) * 128], rhs=QT[:, i, qcs],
                                                             start=True, stop=True), reads=KT_r + [QT_r[qb]], writes=[PS_r[2 + pi]])
                            kb.op("act", lambda a: a.activation(out=ex[:, 0:128], in_=PS[2 + pi][:, 0:128], func=AF.Exp), reads=[PS_r[2 + pi]], writes=[ex_r])
                            kb.op("dve", lambda v: v.tensor_tensor(out=PT[pi][:, 512:640], in0=ex[:, 0:128], in1=wmask[:, :], op=ALU.mult),
                                  reads=[ex_r, k_r, PT_r[pi]], writes=[PT_r[pi]])

                        def fn(pe):
                            inst = None
                            seq_ = [(n_, kt) for n_, kt in enumerate(main)] + ([(4, qt - 4)] if tail else [])
                            for idx, (n_, kt) in enumerate(seq_):
                                inst = pe.matmul(PS[4][:, q * 65:(q + 1) * 65], lhsT=PT[pi][:, n_ * 128:(n_ + 1) * 128], rhs=VW[:, kt, g, :],
                                                 start=(idx == 0), stop=(idx == len(seq_) - 1))
                            return inst
                        kb.op("pe", fn, reads=[PT_r[pi]] + VW_r[max(0, qt - 4):qt + 1], writes=[PS_r[4]])
                    branch_finish(PS[4][:, 0:260], PS_r[4], 65, 4, h, 2, tts, False, ONSA, on_r)
            def nsa_e(qb):
                qc = slice(qb * 512, (qb + 1) * 512)
                tts = list(range(qb * 4, qb * 4 + 4))
                ONSA, on_r, impb, imp_r = ONSA2[qb % 2], on_r2[qb % 2], impb2[qb % 2], imp_r2[qb % 2]
                for q in range(4):
                    tt = tts[q]
                    kb.op("act", lambda a: a.activation(out=obf[:, :], in_=ONSA[:, q, :], func=AF.Copy), reads=[on_r[q]], writes=[obf_r])
                    for half in range(2):
                        out_to_OT(obf[:, half * 256:(half + 1) * 256], obf_r, 128, OT, OT_r, 4 + 2 * half, tt * 128)
            for qb in range(4):
                if qb == 0:
                    nsa_a(0)
                if qb + 1 < 4:
                    nsa_a(qb + 1)
                if ns_ > 3:
                    nsa_b(qb)
                if ns_ > 5:
                    nsa_d(qb)
                if ns_ > 4:
                    nsa_c(qb)
                if ns_ > 6:
                    nsa_e(qb)
            kb.barrier()

    for sq in range(NSEQ):
        for l in range(NLAY):
            if l == 0:
                with sb("xstage", [128, 2, D], F32) as xst:
                    xst_r = regs(2)
                    for tt in range(NT):
                        i = tt % 2
                        kb.dma("sp", xst[:, i, :], x_d.ap()[sq, tt * 128:(tt + 1) * 128, :], writes=[xst_r[i]])
                        make_xT(xst[:, i, :], xst_r[i], tt)
                    kb.barrier()
            if cfg.get("stop") == "xt":
                continue
            res_src = (lambda tt: x_d.ap()[sq, tt * 128:(tt + 1) * 128, :]) if l == 0 else \
                      (lambda tt: xres[1].ap()[tt * 128:(tt + 1) * 128, :])
            res_regs = None if l == 0 else xres_r[1]

            with sb("OT", [128, 8, S], BF16) as OT:
                OT_r = regs(NT)
                if inject_O:
                    for k in range(8):
                        kb.dma("pool", OT[:, k, :], dbg_OT.ap()[:, k, :], writes=OT_r)
                if "gla" in mixers:
                    gla_phase(sq, l, OT, OT_r)
                if "rwkv" in mixers:
                    rwkv_phase(sq, l, OT, OT_r)
                if "nsa" in mixers:
                    nsa_phase(sq, l, OT, OT_r)
                if "OT" in dump_d:
                    for k in range(8):
                        with sb("otd", [128, S], F32) as otd:
                            r_ = Reg()
                            kb.op("act", lambda a: a.activation(out=otd[:], in_=OT[:, k, :], func=AF.Copy),
                                  reads=OT_r, writes=[r_])
                            dump("OT", otd[:], r_, idx=k)
                            kb.barrier()
                if cfg.get("stop") == "outproj0":
                    continue
                with sb("Wo", [128, 8, D], BF16) as Wo, \
                        sb("ln1", [128, 2, D], F32) as ln1, \
                        sb("xs1", [128, 2, D], F32) as xs1, \
                        sb("st1", [128, 2, 32], F32) as st1:
                    Wo_r = Reg()
                    ln_r = Reg()
                    xs_r = regs(2)
                    st_r = regs(2)
                    load_w(Wo, w_out16.ap()[l], D, Wo_r)
                    kb.dma("sp", ln1[:, 0, :], bcast_rows(rowtab, l * 5120 + 1024, D), writes=[ln_r])
                    kb.dma("sp", ln1[:, 1, :], bcast_rows(rowtab, l * 5120 + 2048, D), writes=[ln_r])
                    for tt in range(NT):
                        i = tt % 2
                        kb.dma("sp", xs1[:, i, :], res_src(tt), reads=([res_regs[tt]] if res_regs else []), writes=[xs_r[i]])
                        for hf in range(2):
                            def fn(pe, hf=hf):
                                inst = None
                                for k in range(8):
                                    inst = pe.matmul(PS[hf][:, :], lhsT=OT[:, k, tt * 128:(tt + 1) * 128],
                                                     rhs=Wo[:, k, hf * 512:(hf + 1) * 512], start=(k == 0), stop=(k == 7))
                                return inst
                            kb.op("pe", fn, reads=[OT_r[tt], Wo_r], writes=[PS_r[hf]])
                            kb.op("dve", lambda v, hf=hf: v.scalar_tensor_tensor(
                                out=xs1[:, i, hf * 512:(hf + 1) * 512], in0=xs1[:, i, hf * 512:(hf + 1) * 512], scalar=ALPHA,
                                in1=PS[hf][:, :], op0=ALU.mult, op1=ALU.add), reads=[PS_r[hf], xs_r[i]], writes=[xs_r[i]])
                        layer_norm(xs1[:, i, :], xs_r[i], ln1[:, 0, :], ln1[:, 1, :], ln_r, st1[:, i, :], st_r[i])
                        kb.dma("sp", xres[0].ap()[tt * 128:(tt + 1) * 128, :], xs1[:, i, :], reads=[xs_r[i]], writes=[xres_r[0][tt]])
                        make_xT(xs1[:, i, :], xs_r[i], tt)
                        if "x1" in dump_d and sq == 0 and l == cfg.get("dump_layer", 0):
                            dump("x1", xs1[:, i, :], xs_r[i], idx=tt)
                    kb.barrier()
            if cfg.get("stop") in ("outproj", "outproj0"):
                continue
            with sb("aT", [128, NFC, 1024], BF16) as aT, \
                    sb("Wd", [128, NFC, D], BF16) as Wd, \
                    sb("Wgu", [128, 2, 2, 8, 512], BF16) as Wgu, \
                    sb("sg", [128, 2, 512], F32) as sg, \
                    sb("ln2", [128, 2, D], F32) as ln2, \
                    sb("xs2", [128, 2, D], F32) as xs2, \
                    sb("st2", [128, 2, 32], F32) as st2:
                Wd_r = Reg()
                ln_r = Reg()
                aT_r = regs(NFC)
                Wgu_r = regs(2)
                sg_r = regs(2)
                xs_r = regs(2)
                st_r = regs(2)
                kb.dma("sp", ln2[:, 0, :], bcast_rows(rowtab, l * 5120 + 3072, D), writes=[ln_r])
                kb.dma("sp", ln2[:, 1, :], bcast_rows(rowtab, l * 5120 + 4096, D), writes=[ln_r])
                for k in range(NFC):
                    for c0 in (0, 512):
                        kb.dma("sp", Wd[:, k, c0:c0 + 512], w_down16.ap()[l, k * 128:(k + 1) * 128, c0:c0 + 512], reads=[w16_r], writes=[Wd_r])
                last = (l == NLAY - 1)
                for mt in range(2):
                    tok0 = mt * 1024
                    for hc in range(NFC):
                        cg, ci = hc // 4, hc % 4
                        wi = cg % 2
                        if ci == 0:
                            ncol = min(512, FF - cg * 512)
                            for gu, wsrc in ((0, w_gate16), (1, w_up16)):
                                for k in range(8):
                                    kb.dma("sp", Wgu[:, wi, gu, k, 0:ncol],
                                           wsrc.ap()[l, k * 128:(k + 1) * 128, cg * 512:cg * 512 + ncol],
                                           reads=[w16_r], writes=[Wgu_r[wi]])
                        for blk in range(2):
                            t0 = tok0 + blk * 512
                            pg, pu = (0, 1) if blk == 0 else (2, 3)
                            for gu, pi in ((0, pg), (1, pu)):
                                def fn(pe, gu=gu, pi=pi):
                                    inst = None
                                    for k in range(8):
                                        inst = pe.matmul(PS[pi][:, :], lhsT=Wgu[:, wi, gu, k, ci * 128:(ci + 1) * 128],
                                                         rhs=XT[:, k, t0:t0 + 512], start=(k == 0), stop=(k == 7))
                                    return inst
                                kb.op("pe", fn, reads=[Wgu_r[wi]] + XT_r[t0 // 128:t0 // 128 + 4], writes=[PS_r[pi]])
                            kb.op("act", lambda a: a.activation(out=sg[:, blk, :], in_=PS[pg][:, :], func=AF.Silu),
                                  reads=[PS_r[pg]], writes=[sg_r[blk]])
                            kb.op("dve", lambda v: v.tensor_tensor(out=aT[:, hc, blk * 512:(blk + 1) * 512], in0=sg[:, blk, :],
                                                                  in1=PS[pu][:, :], op=ALU.mult),
                                  reads=[sg_r[blk], PS_r[pu]], writes=[aT_r[hc]])
                    for t8 in range(8):
                        tt = mt * 8 + t8
                        i = tt % 2
                        kb.dma("sp", xs2[:, i, :], xres[0].ap()[tt * 128:(tt + 1) * 128, :], reads=[xres_r[0][tt]], writes=[xs_r[i]])
                        for hf in range(2):
                            pi = 4 + hf

                            def fn(pe, hf=hf, pi=pi):
                                inst = None
                                for k in range(NFC):
                                    inst = pe.matmul(PS[pi][:, :], lhsT=aT[:, k, t8 * 128:(t8 + 1) * 128],
                                                     rhs=Wd[:, k, hf * 512:(hf + 1) * 512], start=(k == 0), stop=(k == NFC - 1))
                                return inst
                            kb.op("pe", fn, reads=aT_r + [Wd_r], writes=[PS_r[pi]])
                            kb.op("dve", lambda v, hf=hf, pi=pi: v.scalar_tensor_tensor(
                                out=xs2[:, i, hf * 512:(hf + 1) * 512], in0=xs2[:, i, hf * 512:(hf + 1) * 512], scalar=ALPHA,
                                in1=PS[pi][:, :], op0=ALU.mult, op1=ALU.add), reads=[PS_r[pi], xs_r[i]], writes=[xs_r[i]])
                        layer_norm(xs2[:, i, :], xs_r[i], ln2[:, 0, :], ln2[:, 1, :], ln_r, st2[:, i, :], st_r[i])
                        if last:
                            kb.dma("sp", out_d.ap()[sq, tt * 128:(tt + 1) * 128, :], xs2[:, i, :], reads=[xs_r[i]])
                        else:
                            kb.dma("sp", xres[1].ap()[tt * 128:(tt + 1) * 128, :], xs2[:, i, :], reads=[xs_r[i]], writes=[xres_r[1][tt]])
                            make_xT(xs2[:, i, :], xs_r[i], tt)
                        if "x2" in dump_d and sq == 0 and l == cfg.get("dump_layer", 0):
                            dump("x2", xs2[:, i, :], xs_r[i], idx=tt)
                    if not last:
                        pass
                kb.barrier()
    kb.finish()
    return kb


def host_consts():
    c = {}
    c["c_ident"] = np.eye(128, dtype=np.float32)
    s = np.arange(128)
    c["c_caus"] = (s[:, None] <= s[None, :]).astype(np.float32)
    half = 32
    inv = (10000.0 ** (-np.arange(half, dtype=np.float32) / half)).astype(np.float32)
    ang = (np.arange(S, dtype=np.float32)[:, None] * inv[None, :]).astype(np.float32)
    cos = np.cos(ang).astype(np.float32).T
    sin = np.sin(ang).astype(np.float32).T
    cosT = np.concatenate([cos, cos, cos, cos], 0)
    sinT = np.concatenate([-sin, sin, -sin, sin], 0)
    c["c_rope"] = np.stack([cosT, sinT]).astype(np.float32)
    cc = np.arange(128)
    t = np.arange(S)
    c["c_cmpmask"] = ((16 * cc[:, None] + 31 <= t[None, :]) & (cc[:, None] < 127)).astype(np.float32)
    j = np.arange(32)
    c["c_onehot"] = ((t[None, :] // 64) == j[:, None]).astype(np.float32)
    cur = t // 64
    forced = (j[None, :] == 0) | (j[None, :] == cur[:, None]) | (j[None, :] == cur[:, None] - 1)
    future = j[None, :] > cur[:, None]
    m1 = (~forced & ~future).astype(np.float32)
    m2 = np.where(forced, 1e9, np.where(future, -1e9, 0.0)).astype(np.float32)
    selm = np.stack([m1, m2])
    c["c_selm"] = np.ascontiguousarray(selm.reshape(2, NT, 128, 32).transpose(0, 2, 1, 3))
    c0 = np.arange(127) * 16
    s0 = np.arange(32) * 64
    lo = np.maximum(c0[:, None], s0[None, :])
    hi = np.minimum(c0[:, None] + 32, s0[None, :] + 64)
    ov = np.zeros((128, 32), np.float32)
    ov[:127] = np.maximum(hi - lo, 0) / 16
    c["c_ovl"] = ov
    blk = np.zeros((128, 128), np.float32)
    blk[:64, :64] = 1
    blk[64:, 64:] = 1
    c["c_blk"] = blk
    hs = np.zeros((128, 2), np.float32)
    hs[:64, 0] = 1
    hs[64:, 1] = 1
    c["c_hsel"] = hs
    i64 = np.arange(64)
    lo_strict = (i64[None, :] < i64[:, None]).astype(np.float32)
    up_strict = (i64[:, None] < i64[None, :]).astype(np.float32)
    up_incl = (i64[:, None] <= i64[None, :]).astype(np.float32)
    def bd(m):
        o = np.zeros((128, 128), np.float32)
        o[:64, :64] = m
        o[64:, 64:] = m
        return o
    c["c_rwmask2"] = np.ascontiguousarray(np.stack([bd(lo_strict), bd(up_strict), bd(up_incl)], axis=1))
    c["c_istk"] = np.concatenate([np.eye(64, dtype=np.float32), np.eye(64, dtype=np.float32)], axis=0)
    return c


def host_layout(inp):
    d = {}
    f = lambda a: np.ascontiguousarray(np.asarray(a, dtype=np.float32))
    for k in ("w_in", "w_in_vres", "gla_w_a2", "rwkv_w2", "rwkv_a2", "rwkv_v2", "rwkv_g2", "nsa_wk1", "nsa_wk2",
              "nsa_wv1", "nsa_wv2", "w_out", "ffn_w_gate", "ffn_w_up", "ffn_w_down"):
        d[k] = f(inp[k])
    base = 2096
    sw = lambda b: list(range(b + 32, b + 64)) + list(range(b, b + 32))
    pl = lambda b: list(range(b, b + 64))
    cols = []
    for i in range(4):
        cols += pl(base + i * 64) + pl(base + (4 + i) * 64)
    for i in range(4):
        cols += sw(base + i * 64) + sw(base + (4 + i) * 64)
    for c0 in (512, 768, 1024):
        cols += pl(base + c0) + pl(base + c0 + 64)
        cols += sw(base + c0) + sw(base + c0 + 64)
    cols += list(range(base + 640, base + 768)) + list(range(base + 896, base + 1024)) + list(range(base + 1152, base + 1280))
    cols += list(range(base + 1280, base + 1304))
    assert len(cols) == 2200
    d["w_nsa"] = f(np.asarray(inp["w_in"])[:, :, cols])
    d["nsa_posT"] = f(np.stack([np.asarray(inp["nsa_pos_k"]).transpose(0, 2, 1),
                                np.asarray(inp["nsa_pos_v"]).transpose(0, 2, 1)], axis=1))
    pt = np.zeros((L, 128, 32), np.float32)
    mu = np.asarray(inp["rwkv_mu"])
    rwt = [(0, 128), (128, 128), (256, 128), (384, 128), (512, 128), (640, 128), (768, 64), (832, 64), (896, 128), (1024, 32)]
    for l in range(L):
        pt[l, :, 0:2] = np.asarray(inp["gla_b_a"])[l].reshape(2, 128).T
        for i, (c0, n) in enumerate(rwt):
            pt[l, :n, 2 + i] = mu[l, c0:c0 + n]
        if l >= 1:
            pt[l, :32, 12] = np.asarray(inp["rwkv_mu_vres"])[l - 1]
            pt[l, :, 17:19] = np.asarray(inp["rwkv_v0"])[l - 1].reshape(2, 128).T
        pt[l, :, 13:15] = np.asarray(inp["rwkv_w0"])[l].reshape(2, 128).T
        pt[l, :, 15:17] = np.asarray(inp["rwkv_a0"])[l].reshape(2, 128).T
        pt[l, :, 19:21] = np.asarray(inp["rwkv_k_k"])[l].reshape(2, 128).T
        pt[l, :, 21:23] = np.asarray(inp["rwkv_k_a"])[l].reshape(2, 128).T
        pt[l, :, 23:25] = np.asarray(inp["rwkv_r_k"])[l].reshape(2, 128).T
    d["ptab"] = pt
    rt = np.zeros((L, 5120), np.float32)
    for l in range(L):
        rt[l, 0:256] = np.asarray(inp["gla_ln_w"])[l]
        rt[l, 256:512] = np.asarray(inp["gla_ln_b"])[l]
        rt[l, 512:768] = np.asarray(inp["rwkv_ln_w"])[l]
        rt[l, 768:1024] = np.asarray(inp["rwkv_ln_b"])[l]
        rt[l, 1024:2048] = np.asarray(inp["ln1_w"])[l]
        rt[l, 2048:3072] = np.asarray(inp["ln1_b"])[l]
        rt[l, 3072:4096] = np.asarray(inp["ln2_w"])[l]
        rt[l, 4096:5120] = np.asarray(inp["ln2_b"])[l]
    d["rowtab"] = rt
    return d


_CACHE = {}


def kernel(**inputs):
    cfg = {}
    if "full" not in _CACHE:
        _CACHE["full"] = build(cfg)
    kb = _CACHE["full"]
    shared = host_layout(inputs)
    shared.update(host_consts())
    x = np.ascontiguousarray(np.asarray(inputs["x"], dtype=np.float32))
    in_maps = []
    for c in range(8):
        m = dict(shared)
        m["x"] = x[2 * c:2 * c + 2]
        in_maps.append(m)
    res = run_bass_kernel_spmd(kb.nc, in_maps, core_ids=list(range(8)))
    return np.concatenate([r["out"] for r in res.results], axis=0).astype(np.float32)
```

```python
import math
from contextlib import ExitStack
import numpy as np
import concourse.bass as bass
import concourse.mybir as mybir
from concourse.bass_utils import run_bass_kernel_spmd

F32 = mybir.dt.float32
BF16 = mybir.dt.bfloat16
AF = mybir.ActivationFunctionType
ALU = mybir.AluOpType
AX = mybir.AxisListType

S = 2048
D = 1024
NT = S // 128
L = 2
FF = 2816
NFC = FF // 128
ALPHA = float((2 * L) ** 0.25)
LN_EPS = 1e-5
RW_EPS = 64e-5
NDS = 6


class Reg:
    __slots__ = ("lw", "rd")

    def __init__(self):
        self.lw = None
        self.rd = {}


def regs(n):
    return [Reg() for _ in range(n)]


class _Rec:
    def __init__(self):
        self.calls = []

    def __getattr__(self, name):
        def f(*a, **kw):
            self.calls.append((name, a, kw))
            return self
        return f


class _Node:
    __slots__ = ("e", "kind", "calls", "reads", "writes", "cost", "deps")

    def __init__(self, e, kind, calls, reads, writes, cost):
        self.e, self.kind, self.calls, self.reads, self.writes, self.cost = e, kind, calls, reads, writes, cost
        self.deps = ()


def _free_elems(ap):
    try:
        n = 1
        for s_ in list(ap.shape)[1:]:
            n *= int(s_)
        return n
    except Exception:
        return 256


def _ap_bytes(ap):
    try:
        n = 1
        for s_ in list(ap.shape):
            n *= int(s_)
        return n * 4
    except Exception:
        return 65536


def _est_cost(e, calls):
    t = 0.0
    for name, a, kw in calls:
        out = kw.get("out", a[0] if a else None)
        n = _free_elems(out) if out is not None else 256
        if e == "pe":
            mul = 4.0 if (name == "matmul" and getattr(kw.get("lhsT"), "dtype", None) == F32) else 1.0
            t += mul * max(n, 64) / 1.6 + 25.0
        elif e == "act":
            t += n / 1.0 + 220.0
        elif e == "dve":
            t += n / 0.9 + 80.0
        else:
            t += n * 2.0 + 300.0
    return t


class KB:
    def __init__(self, sched=True, W=32):
        nc = bass.Bass("TRN2", target_bir_lowering=False)
        self.nc = nc
        self.E = {"pe": nc.tensor, "act": nc.scalar, "dve": nc.vector, "pool": nc.gpsimd, "sp": nc.sync}
        self.sems = {}
        self.cnt = {}
        for e in ("pe", "act", "dve", "pool"):
            self.sems[e] = nc.alloc_semaphore("s_" + e)
            self.cnt[e] = 0
        self.dq = {}
        for q, nds in (("sp", 12), ("pool", NDS), ("act", 6)):
            keys = []
            for i in range(nds):
                k = "d_%s%d" % (q, i)
                self.sems[k] = nc.alloc_semaphore(k)
                self.cnt[k] = 0
                keys.append(k)
            self.dq[q] = [keys, 0]
        self.seen = {e: {} for e in self.E}
        self.nops = 0
        self.sched = sched
        self.W = W
        self.pending = []

    def _waits(self, e, reads, writes, extra=()):
        need = {}

        def add(rec):
            if rec is None:
                return
            k, c = rec
            if need.get(k, 0) < c:
                need[k] = c

        for r in reads:
            add(r.lw)
        for w in writes:
            add(w.lw)
            for k, c in w.rd.items():
                add((k, c))
        for rec in extra:
            add(rec)
        eng = self.E[e]
        seen = self.seen[e]
        for k, c in need.items():
            if e == "pe" and k == "pe":
                continue
            if seen.get(k, 0) >= c:
                continue
            eng.wait_ge(self.sems[k], c)
            seen[k] = c

    def _mark(self, rec, reads, writes):
        k, c = rec
        for r in reads:
            if r.rd.get(k, 0) < c:
                r.rd[k] = c
        for w in writes:
            w.lw = rec
            w.rd = {}

    def op(self, e, fn, reads=(), writes=()):
        rec = _Rec()
        fn(rec)
        node = _Node(e, "op", rec.calls, list(reads), list(writes), _est_cost(e, rec.calls))
        if self.sched:
            self.pending.append(node)
        else:
            self._emit(node)

    def dma(self, q, out, in_, reads=(), writes=(), **kw):
        node = _Node(q, "dma", (out, in_, kw), list(reads), list(writes), 2000.0 + _ap_bytes(out) / 100.0)
        if self.sched:
            self.pending.append(node)
        else:
            self._emit(node)

    def _emit(self, n):
        e = n.e
        if n.kind == "op":
            self._waits(e, n.reads, n.writes)
            eng = self.E[e]
            inst = None
            for name, a, kw in n.calls:
                inst = getattr(eng, name)(*a, **kw)
            self.cnt[e] += 1
            inst.then_inc(self.sems[e], 1)
            self._mark((e, self.cnt[e]), n.reads, n.writes)
        else:
            out, in_, kw = n.calls
            keys, i = self.dq[e]
            k = keys[i]
            self.dq[e][1] = (i + 1) % len(keys)
            extra = [(k, self.cnt[k])] if self.cnt[k] else []
            self._waits(e, n.reads, n.writes, extra)
            self.E[e].dma_start(out=out, in_=in_, **kw).then_inc(self.sems[k], 16)
            self.cnt[k] += 16
            self._mark((k, self.cnt[k]), n.reads, n.writes)
        self.nops += 1

    def flush(self):
        nodes = self.pending
        self.pending = []
        if not nodes:
            return
        lastw, readers = {}, {}
        for i, n in enumerate(nodes):
            d = set()
            for r in n.reads:
                if id(r) in lastw:
                    d.add(lastw[id(r)])
            for w in n.writes:
                if id(w) in lastw:
                    d.add(lastw[id(w)])
                d.update(readers.get(id(w), ()))
            d.discard(i)
            n.deps = d
            for r in n.reads:
                readers.setdefault(id(r), []).append(i)
            for w in n.writes:
                lastw[id(w)] = i
                readers[id(w)] = []
        queues = {}
        for i, n in enumerate(nodes):
            queues.setdefault(n.e, []).append(i)
        fin = [None] * len(nodes)
        efree = {e: 0.0 for e in queues}
        W, LAT = self.W, 120.0
        remaining = len(nodes)
        while remaining:
            best = None
            for e, q in queues.items():
                for i in q[:W]:
                    n = nodes[i]
                    ready = 0.0
                    ok = True
                    for d in n.deps:
                        f = fin[d]
                        if f is None:
                            ok = False
                            break
                        if f + LAT > ready:
                            ready = f + LAT
                    if not ok:
                        continue
                    start = ready if ready > efree[e] else efree[e]
                    key = (start, i)
                    if best is None or key < best[0]:
                        best = (key, e, i)
            assert best is not None
            (start, _), e, i = best
            n = nodes[i]
            fin[i] = start + n.cost
            efree[e] = (start + 60.0) if n.kind == "dma" else fin[i]
            queues[e].remove(i)
            self._emit(n)
            remaining -= 1

    def barrier(self):
        self.flush()
        for e in self.E:
            for k, c in self.cnt.items():
                if c == 0 or (e == "pe" and k == "pe"):
                    continue
                if self.seen[e].get(k, 0) >= c:
                    continue
                self.E[e].wait_ge(self.sems[k], c)
                self.seen[e][k] = c

    def finish(self):
        self.barrier()


def bcast_rows(t, off, n, parts=128):
    return bass.AP(t, off, [[0, parts], [1, n]])


def build(cfg):
    kb = KB(sched=cfg.get("sched", True), W=cfg.get("W", 128))
    nc = kb.nc
    NSEQ = cfg.get("nseq", 2)
    NLAY = cfg.get("nlay", 2)
    mixers = cfg.get("mixers", ("gla", "rwkv", "nsa"))
    dumps = cfg.get("dumps", ())
    inject_O = cfg.get("inject_O", False)

    def dram_in(name, shape):
        return nc.dram_tensor(name, list(shape), F32, kind="ExternalInput")

    x_d = dram_in("x", [2, S, D])
    w_in = dram_in("w_in", [L, D, 3400])
    w_vres = dram_in("w_in_vres", [1, D, 32])
    w_nsa = dram_in("w_nsa", [L, D, 2200])
    gla_w_a2 = dram_in("gla_w_a2", [L, 16, 256])
    rwkv_w2 = dram_in("rwkv_w2", [L, 64, 256])
    rwkv_a2 = dram_in("rwkv_a2", [L, 64, 256])
    rwkv_v2 = dram_in("rwkv_v2", [1, 32, 256])
    rwkv_g2 = dram_in("rwkv_g2", [L, 160, 256])
    nsa_wk1 = dram_in("nsa_wk1", [L, 2048, 256])
    nsa_wk2 = dram_in("nsa_wk2", [L, 256, 64])
    nsa_wv1 = dram_in("nsa_wv1", [L, 2048, 256])
    nsa_wv2 = dram_in("nsa_wv2", [L, 256, 64])
    nsa_posT = dram_in("nsa_posT", [L, 2, 64, 32])
    w_out = dram_in("w_out", [L, D, D])
    w_gate = dram_in("ffn_w_gate", [L, D, FF])
    w_up = dram_in("ffn_w_up", [L, D, FF])
    w_down = dram_in("ffn_w_down", [L, FF, D])
    ptab = dram_in("ptab", [L, 128, 32])
    rowtab = dram_in("rowtab", [L, 5120])
    c_ident = dram_in("c_ident", [128, 128])
    c_caus = dram_in("c_caus", [128, 128])
    c_rope = dram_in("c_rope", [2, 128, S])
    c_cmpmask = dram_in("c_cmpmask", [128, S])
    c_onehot = dram_in("c_onehot", [32, S])
    c_selm = dram_in("c_selm", [2, 128, NT, 32])
    c_ovl = dram_in("c_ovl", [128, 32])
    c_blk = dram_in("c_blk", [128, 128])
    c_hsel = dram_in("c_hsel", [128, 2])
    c_rwmask2 = dram_in("c_rwmask2", [128, 3, 128])
    c_istk = dram_in("c_istk", [128, 64])
    if inject_O:
        dbg_OT = dram_in("dbg_OT", [128, 8, S])
    out_d = nc.dram_tensor("out", [2, S, D], F32, kind="ExternalOutput")
    xres = [nc.dram_tensor("xres%d" % i, [S, D], F32, kind="Internal") for i in range(2)]
    xres_r = [regs(NT) for _ in range(2)]
    vfirst_d = nc.dram_tensor("vfirst", [2, 128, S], F32, kind="Internal")
    vfirst_r = regs(2)
    dump_d = {}
    for name, shape in dumps:
        dump_d[name] = nc.dram_tensor("dump_" + name, list(shape), F32, kind="ExternalOutput")

    XT = nc.alloc_sbuf_tensor("XT", [128, 8, S], BF16)
    XT_r = regs(NT)
    ident = nc.alloc_sbuf_tensor("ident", [128, 128], BF16)
    identf = nc.alloc_sbuf_tensor("identf", [128, 128], F32)
    caus = nc.alloc_sbuf_tensor("caus", [128, 4, 128], F32)
    ptb = nc.alloc_sbuf_tensor("ptb", [128, L, 32], F32)
    ptn = nc.alloc_sbuf_tensor("ptn", [128, L, 32], F32)
    c_r = Reg()
    for l in range(L):
        kb.dma("sp", ptb[:, l, :], ptab.ap()[l], writes=[c_r])
    kb.dma("pool", ident[:], c_ident.ap(), writes=[c_r])
    kb.dma("sp", identf[:], c_ident.ap(), writes=[c_r])
    for i in range(4):
        kb.dma("sp", caus[:, i, :], c_caus.ap(), writes=[c_r])

    def dram16(name, shape):
        return nc.dram_tensor(name, list(shape), BF16, kind="Internal")

    conv_list = [(w_in, [L, D, 3400]), (w_nsa, [L, D, 2200]), (w_out, [L, D, D]), (w_gate, [L, D, FF]), (w_up, [L, D, FF]),
                 (w_down, [L, FF, D]), (nsa_wk1, [L, 2048, 256]), (nsa_wv1, [L, 2048, 256])]
    w16 = {}
    w16_r = Reg()
    CH = 2816
    with nc.sbuf_tensor("cv_f", [128, 3, CH], F32) as cvf, nc.sbuf_tensor("cv_b", [128, 3, CH], BF16) as cvb:
        cvf_r, cvb_r = regs(3), regs(3)
        job = 0
        for src, shape in conv_list:
            dst = dram16(src.name + "_16", shape)
            w16[src.name] = dst
            tot = 1
            for d_ in shape:
                tot *= d_
            per = tot // 128
            assert per * 128 == tot
            for c0 in range(0, per, CH):
                n = min(CH, per - c0)
                i = job % 3
                kb.dma("sp", cvf[:, i, 0:n], bass.AP(src, c0, [[per, 128], [1, n]]), writes=[cvf_r[i]])
                eng = "act"
                if eng == "act":
                    kb.op("act", lambda a: a.activation(out=cvb[:, i, 0:n], in_=cvf[:, i, 0:n], func=AF.Copy),
                          reads=[cvf_r[i]], writes=[cvb_r[i]])
                else:
                    kb.op(eng, lambda v: v.tensor_scalar(out=cvb[:, i, 0:n], in0=cvf[:, i, 0:n], scalar1=1.0, scalar2=None, op0=ALU.mult),
                          reads=[cvf_r[i]], writes=[cvb_r[i]])
                kb.dma("act", bass.AP(dst, c0, [[per, 128], [1, n]]), cvb[:, i, 0:n], reads=[cvb_r[i]], writes=[w16_r])
                job += 1
        kb.barrier()
    w_in16, w_nsa16, w_out16 = w16["w_in"], w16["w_nsa"], w16["w_out"]
    w_gate16, w_up16, w_down16 = w16["ffn_w_gate"], w16["ffn_w_up"], w16["ffn_w_down"]
    wk1_16, wv1_16 = w16["nsa_wk1"], w16["nsa_wv1"]

    PS = [nc.alloc_psum_tensor("ps%d" % i, [128, 512], F32) for i in range(6)]
    PS_r = regs(6)
    PB = [nc.alloc_psum_tensor("pb%d" % i, [128, 1024], BF16) for i in range(2)]
    PB_r = regs(2)

    _uid = [0]

    def sb(name, shape, dt):
        _uid[0] += 1
        return nc.sbuf_tensor("%s_%d" % (name, _uid[0]), list(shape), dt)

    def proj_fm(ps_ap, Wt, c0, M, tok0, N, wreads, treads, pw, kparts=8):
        def fn(pe):
            inst = None
            for k in range(kparts):
                inst = pe.matmul(ps_ap, lhsT=Wt[:, k, c0:c0 + M], rhs=XT[:, k, tok0:tok0 + N],
                                 start=(k == 0), stop=(k == kparts - 1))
            return inst
        kb.op("pe", fn, reads=list(wreads) + list(treads), writes=[pw])

    def proj_tm(ps_ap, Wt, c0, N, tt, wreads, pw):
        def fn(pe):
            inst = None
            for k in range(8):
                inst = pe.matmul(ps_ap, lhsT=XT[:, k, tt * 128:(tt + 1) * 128], rhs=Wt[:, k, c0:c0 + N],
                                 start=(k == 0), stop=(k == 7))
            return inst
        kb.op("pe", fn, reads=list(wreads) + [XT_r[tt]], writes=[pw])

    def load_w(Wt, src3, ncols, wreg, q="sp", chunk=None):
        for k in range(8):
            kb.dma(q, Wt[:, k, 0:ncols], src3[k * 128:(k + 1) * 128, 0:ncols], reads=[w16_r], writes=[wreg])

    xb = [nc.alloc_sbuf_tensor("xb%d" % i, [128, D], BF16) for i in range(2)]
    xb_r = regs(2)
    xb_i = [0]

    def make_xT(src_ap, src_reg, tt):
        i = xb_i[0]
        xb_i[0] ^= 1
        kb.op("act", lambda a: a.activation(out=xb[i][:], in_=src_ap, func=AF.Copy),
              reads=[src_reg], writes=[xb_r[i]])
        pb = PB[i]

        def fn(pe):
            inst = None
            for k in range(8):
                inst = pe.transpose(out=pb[:, k * 128:(k + 1) * 128], in_=xb[i][:, k * 128:(k + 1) * 128],
                                    identity=ident[:])
            return inst
        kb.op("pe", fn, reads=[xb_r[i], c_r], writes=[PB_r[i]])
        kb.op("dve", lambda v: v.tensor_copy(out=XT[:, :, tt * 128:(tt + 1) * 128],
                                             in_=pb[:, :].rearrange("p (k t) -> p k t", k=8)),
              reads=[PB_r[i]], writes=[XT_r[tt]])

    def layer_norm(xt_ap, xreg, lnw_ap, lnb_ap, lnreg, st, st_r):
        kb.op("dve", lambda v: v.bn_stats(out=st[:, 0:6], in_=xt_ap[:, 0:512]), reads=[xreg], writes=[st_r])
        kb.op("dve", lambda v: v.bn_stats(out=st[:, 6:12], in_=xt_ap[:, 512:1024]), reads=[xreg, st_r], writes=[st_r])
        kb.op("dve", lambda v: v.bn_aggr(out=st[:, 12:14], in_=st[:, 0:12]), reads=[st_r], writes=[st_r])
        kb.op("act", lambda a: a.activation(out=st[:, 14:15], in_=st[:, 13:14], func=AF.Sqrt, bias=LN_EPS, scale=1.0),
              reads=[st_r], writes=[st_r])
        kb.op("dve", lambda v: v.reciprocal(out=st[:, 15:16], in_=st[:, 14:15]), reads=[st_r], writes=[st_r])
        kb.op("dve", lambda v: v.scalar_tensor_tensor(out=st[:, 16:17], in0=st[:, 12:13], scalar=-1.0, in1=st[:, 15:16],
                                                      op0=ALU.mult, op1=ALU.mult), reads=[st_r], writes=[st_r])
        kb.op("act", lambda a: a.activation(out=xt_ap, in_=xt_ap, func=AF.Identity, bias=st[:, 16:17], scale=st[:, 15:16]),
              reads=[st_r, xreg], writes=[xreg])
        kb.op("dve", lambda v: v.tensor_tensor(out=xt_ap, in0=xt_ap, in1=lnw_ap, op=ALU.mult), reads=[xreg, lnreg], writes=[xreg])
        kb.op("dve", lambda v: v.tensor_tensor(out=xt_ap, in0=xt_ap, in1=lnb_ap, op=ALU.add), reads=[xreg, lnreg], writes=[xreg])

    def dump(name, sb_ap, reg, idx=None):
        if name in dump_d:
            dst = dump_d[name].ap() if idx is None else dump_d[name].ap()[idx]
            kb.dma("sp", dst, sb_ap, reads=[reg])

    kb.op("dve", lambda v: v.tensor_scalar(out=ptn[:, :, :], in0=ptb[:, :, :], scalar1=-1.0, scalar2=None, op0=ALU.mult),
          reads=[c_r], writes=[c_r])
    kb.op("dve", lambda v: v.tensor_scalar(out=ptn[:, :, 2:13], in0=ptb[:, :, 2:13], scalar1=-1.0, scalar2=1.0,
                                           op0=ALU.mult, op1=ALU.add), reads=[c_r], writes=[c_r])

    def gla_phase(sq, l, OT, OT_r):
        with ExitStack() as es:
            A = lambda n_, s_, d_: es.enter_context(sb(n_, s_, d_))
            Wg = A("Wg", [128, 8, 1152], BF16)
            wa2 = A("wa2", [16, 256], BF16)
            gln = A("gln", [128, 2, 256], F32)
            alrT = A("alrT", [16, 512], BF16)
            t1 = A("gt1", [128, 512], F32)
            t2 = A("gt2", [128, 512], F32)
            EB = A("EB", [128, 2, 512], F32)
            QTl = A("gQT", [128, 2, 2, 512], BF16)
            KTl = A("gKT", [128, 2, 512], BF16)
            Ktok = A("gKtok", [128, 4, 256], BF16)
            V = A("gV", [128, 4, 256], BF16)
            SG = A("gSG", [128, 4, 256], F32)
            AT = A("gAT", [128, 4, 128], BF16)
            St = A("gSt", [128, 2, 128], F32)
            tmpS = A("gtmpS", [128, 128], F32)
            blkm = A("gblk", [128, 128], F32)
            Sb = A("gSb", [128, 2, 128], BF16)
            rmask = A("grm", [128, 512], F32)
            ow = A("gow", [128, 256], F32)
            ow2 = A("gow2", [128, 256], F32)
            ob = A("gob", [128, 256], BF16)
            st = A("gst", [128, 32], F32)
            W_r, p_r, alr_r, t1_r, t2_r = Reg(), Reg(), Reg(), Reg(), Reg()
            EB_r, QT_r, KT_r = regs(2), regs(2), regs(2)
            Ktok_r, V_r, SG_r, AT_r, St_r, Sb_r = Reg(), regs(4), regs(4), Reg(), Reg(), Reg()
            ow_r, ow2_r, ob_r, st_r = Reg(), Reg(), Reg(), Reg()
            gs = cfg.get("gla_stop", 99)
            load_w(Wg, w_in16.ap()[l][:, 0:1040], 1040, W_r)
            kb.dma("pool", wa2[:, :], gla_w_a2.ap()[l], writes=[p_r])
            kb.dma("sp", gln[:, 0, :], bcast_rows(rowtab, l * 5120 + 0, 256), writes=[p_r])
            kb.dma("sp", gln[:, 1, :], bcast_rows(rowtab, l * 5120 + 256, 256), writes=[p_r])
            kb.op("pool", lambda g: g.memset(rmask[:, :], 1.0), writes=[p_r])
            kb.op("pool", lambda g: g.memset(rmask[:, :].rearrange("p (c t) -> p c t", c=4)[:, :, 0:1], 0.0), writes=[p_r])
            kb.op("pool", lambda g: g.memset(St[:, :, :], 0.0), writes=[St_r])
            kb.op("pool", lambda g: g.memset(QTl[:, :, :, :], 0.0), writes=QT_r)
            kb.dma("sp", blkm[:, :], c_blk.ap(), writes=[p_r])
            tmpS_r = Reg()
            kb.op("pool", lambda g: g.memset(Sb[:, :, :], 0.0), writes=[Sb_r])
            for mt in range(4 if gs > 0 else 0):
                tok0 = mt * 512
                xr = XT_r[mt * 4:mt * 4 + 4]
                proj_fm(PS[0][0:16, :], Wg, 1024, 16, tok0, 512, [W_r], xr, PS_r[0])
                kb.op("act", lambda a: a.activation(out=alrT[:, :], in_=PS[0][0:16, :], func=AF.Copy),
                      reads=[PS_r[0]], writes=[alr_r])
                for hp in range(2):
                    kb.op("pe", lambda pe: pe.matmul(PS[1][:, :], lhsT=wa2[0:16, hp * 128:(hp + 1) * 128], rhs=alrT[0:16, :],
                                                     start=True, stop=True), reads=[p_r, alr_r], writes=[PS_r[1]])
                    kb.op("act", lambda a: a.activation(out=t1[:, :], in_=PS[1][:, :], func=AF.Exp, scale=-1.0,
                                                        bias=ptn[:, l, hp:hp + 1]), reads=[PS_r[1], c_r], writes=[t1_r])
                    kb.op("act", lambda a: a.activation(out=t1[:, :], in_=t1[:, :], func=AF.Ln, bias=1.0, scale=1.0),
                          reads=[t1_r], writes=[t1_r])
                    kb.op("dve", lambda v: v.tensor_tensor_scan(out=t2[:, :], data0=rmask[:, :], data1=t1[:, :], initial=0.0,
                                                                op0=ALU.mult, op1=ALU.add), reads=[t1_r, p_r], writes=[t2_r])
                    kb.op("act", lambda a: a.activation(out=EB[:, hp, :], in_=t2[:, :], func=AF.Exp, scale=-1.0 / 16.0),
                          reads=[t2_r], writes=[EB_r[hp]])
                    kb.op("act", lambda a: a.activation(out=t1[:, :], in_=t2[:, :], func=AF.Exp, scale=1.0 / 16.0),
                          reads=[t2_r], writes=[t1_r])
                    proj_fm(PS[2][:, :], Wg, hp * 128, 128, tok0, 512, [W_r], xr, PS_r[2])
                    for hh in range(2):
                        pr = slice(hh * 64, hh * 64 + 64)
                        kb.op("dve", lambda v: v.scalar_tensor_tensor(out=QTl[pr, hp, hh, :], in0=PS[2][pr, :], scalar=0.125,
                                                                      in1=EB[pr, hp, :], op0=ALU.mult, op1=ALU.mult),
                              reads=[PS_r[2], EB_r[hp]], writes=[QT_r[hp]])
                    proj_fm(PS[3][:, :], Wg, 256 + hp * 128, 128, tok0, 512, [W_r], xr, PS_r[3])
                    kb.op("dve", lambda v: v.tensor_tensor(out=KTl[:, hp, :], in0=PS[3][:, :], in1=t1[:, :], op=ALU.mult),
                          reads=[PS_r[3], t1_r], writes=[KT_r[hp]])

                    def fn(pe):
                        inst = None
                        for j in range(4):
                            inst = pe.transpose(out=PB[0][:, j * 128:(j + 1) * 128], in_=KTl[:, hp, j * 128:(j + 1) * 128],
                                                identity=ident[:])
                        return inst
                    kb.op("pe", fn, reads=[KT_r[hp], c_r], writes=[PB_r[0]])
                    kb.op("dve", lambda v: v.tensor_copy(out=Ktok[:, :, hp * 128:(hp + 1) * 128],
                                                         in_=PB[0][:, 0:512].rearrange("p (j c) -> p j c", j=4)),
                          reads=[PB_r[0]], writes=[Ktok_r])
                for j in range(4 if gs > 1 else 0):
                    tt = mt * 4 + j
                    proj_tm(PS[4][:, 0:256], Wg, 512, 256, tt, [W_r], PS_r[4])
                    proj_tm(PS[5][:, 0:256], Wg, 768, 256, tt, [W_r], PS_r[5])
                    if cfg.get("gv", 3) >= 2:
                        kb.op("dve", lambda v: v.tensor_scalar(out=V[:, j, :], in0=PS[4][:, 0:256], scalar1=1.0, scalar2=None, op0=ALU.mult),
                              reads=[PS_r[4]], writes=[V_r[j]])
                    if cfg.get("gv", 3) >= 3:
                        kb.op("act", lambda a: a.activation(out=SG[:, j, :], in_=PS[5][:, 0:256], func=AF.Silu),
                              reads=[PS_r[5]], writes=[SG_r[j]])
                for j in range(4 if gs > 2 else 0):
                    tt = mt * 4 + j
                    cc = slice(j * 128, (j + 1) * 128)

                    def fn(pe):
                        inst = None
                        for h in range(4):
                            hp, hh = h // 2, h % 2
                            inst = pe.matmul(PS[0][:, h * 128:(h + 1) * 128], lhsT=KTl[:, hp, cc], rhs=QTl[:, hp, hh, cc],
                                             start=True, stop=True)
                        return inst
                    kb.op("pe", fn, reads=QT_r + KT_r, writes=[PS_r[0]])
                    kb.op("dve", lambda v: v.tensor_tensor(out=AT[:, :, :], in0=PS[0][:, :].rearrange("p (h t) -> p h t", h=4),
                                                          in1=caus[:, :, :], op=ALU.mult), reads=[PS_r[0], c_r], writes=[AT_r])

                    def fn(pe):
                        inst = None
                        for h in range(4):
                            hp, hh = h // 2, h % 2
                            pe.matmul(PS[1][:, h * 64:(h + 1) * 64], lhsT=AT[:, h, :], rhs=V[:, j, h * 64:(h + 1) * 64],
                                      start=True, stop=False)
                            inst = pe.matmul(PS[1][:, h * 64:(h + 1) * 64], lhsT=QTl[:, hp, hh, cc], rhs=Sb[:, hp, hh * 64:(hh + 1) * 64],
                                             start=False, stop=True)
                        return inst
                    if gs <= 3:
                        continue
                    kb.op("pe", fn, reads=[AT_r, V_r[j], Sb_r] + QT_r, writes=[PS_r[1]])

                    def fn(pe):
                        inst = None
                        for hp in range(2):
                            inst = pe.matmul(PS[2][:, hp * 128:(hp + 1) * 128], lhsT=Ktok[:, j, hp * 128:(hp + 1) * 128],
                                             rhs=V[:, j, hp * 128:(hp + 1) * 128], start=True, stop=True)
                        return inst
                    if gs <= 4:
                        continue
                    kb.op("pe", fn, reads=[Ktok_r, V_r[j]], writes=[PS_r[2]])
                    for hp in range(2):
                        ee = EB[:, hp, j * 128 + 127:j * 128 + 128]
                        kb.op("dve", lambda v: v.scalar_tensor_tensor(out=tmpS[:, :], in0=PS[2][:, hp * 128:(hp + 1) * 128], scalar=ee,
                                                                      in1=blkm[:, :], op0=ALU.mult, op1=ALU.mult),
                              reads=[PS_r[2], EB_r[hp], p_r], writes=[tmpS_r])
                        kb.op("dve", lambda v: v.scalar_tensor_tensor(out=St[:, hp, :], in0=St[:, hp, :], scalar=ee, in1=tmpS[:, :],
                                                                      op0=ALU.mult, op1=ALU.add),
                              reads=[St_r, tmpS_r, EB_r[hp]], writes=[St_r])
                    kb.op("act", lambda a: a.activation(out=Sb[:, :, :], in_=St[:, :, :], func=AF.Copy), reads=[St_r], writes=[Sb_r])
                    if gs <= 5:
                        continue
                    head_norm_gate(PS[1][:, 0:256], PS_r[1], ow, ow_r, ow2, ow2_r, st, st_r, gln, p_r, LN_EPS)
                    kb.op("dve", lambda v: v.tensor_tensor(out=ob[:, :], in0=ow[:, :], in1=SG[:, j, :], op=ALU.mult),
                          reads=[ow_r, SG_r[j]], writes=[ob_r])
                    if gs <= 6:
                        continue
                    out_to_OT(ob, ob_r, 128, OT, OT_r, 0, tt * 128)
            kb.barrier()

    def head_norm_gate(ps_ap, ps_r, ow, ow_r, ow2, ow2_r, st, st_r, gln, gln_r, eps, P=128):
        v4 = lambda ap: ap.rearrange("p (h d) -> p h d", h=4)
        kb.op("act", lambda a: a.activation(out=ow[0:P, :], in_=ps_ap, func=AF.Copy), reads=[ps_r], writes=[ow_r])
        kb.op("act", lambda a: a.activation(out=ow2[0:P, :], in_=ow[0:P, :], func=AF.Square), reads=[ow_r], writes=[ow2_r])
        kb.op("dve", lambda v: v.reduce_sum(out=st[0:P, 0:4], in_=v4(ow[0:P, :]), axis=AX.X), reads=[ow_r], writes=[st_r])
        kb.op("dve", lambda v: v.reduce_sum(out=st[0:P, 4:8], in_=v4(ow2[0:P, :]), axis=AX.X), reads=[ow2_r, st_r], writes=[st_r])
        kb.op("dve", lambda v: v.tensor_scalar(out=st[0:P, 8:16], in0=st[0:P, 0:8], scalar1=1.0 / 64.0, scalar2=None, op0=ALU.mult),
              reads=[st_r], writes=[st_r])
        kb.op("dve", lambda v: v.tensor_tensor(out=st[0:P, 16:20], in0=st[0:P, 8:12], in1=st[0:P, 8:12], op=ALU.mult),
              reads=[st_r], writes=[st_r])
        kb.op("dve", lambda v: v.tensor_tensor(out=st[0:P, 20:24], in0=st[0:P, 12:16], in1=st[0:P, 16:20], op=ALU.subtract),
              reads=[st_r], writes=[st_r])
        kb.op("act", lambda a: a.activation(out=st[0:P, 24:28], in_=st[0:P, 20:24], func=AF.Sqrt, bias=eps, scale=1.0),
              reads=[st_r], writes=[st_r])
        kb.op("dve", lambda v: v.reciprocal(out=st[0:P, 28:32], in_=st[0:P, 24:28]), reads=[st_r], writes=[st_r])
        for h in range(4):
            kb.op("dve", lambda v: v.tensor_scalar(out=ow[0:P, h * 64:(h + 1) * 64], in0=ow[0:P, h * 64:(h + 1) * 64],
                                                   scalar1=st[0:P, 8 + h:9 + h], scalar2=st[0:P, 28 + h:29 + h],
                                                   op0=ALU.subtract, op1=ALU.mult), reads=[ow_r, st_r], writes=[ow_r])
        kb.op("dve", lambda v: v.tensor_tensor(out=ow[0:P, :], in0=ow[0:P, :], in1=gln[0:P, 0, :], op=ALU.mult),
              reads=[ow_r, gln_r], writes=[ow_r])
        kb.op("dve", lambda v: v.tensor_tensor(out=ow[0:P, :], in0=ow[0:P, :], in1=gln[0:P, 1, :], op=ALU.add),
              reads=[ow_r, gln_r], writes=[ow_r])

    def out_to_OT(ob, ob_r, P, OT, OT_r, k0, tokc0, ncols=256):
        nk = ncols // 128

        def fn(pe):
            inst = None
            for kk in range(nk):
                inst = pe.transpose(out=PB[1][:, kk * 128:kk * 128 + P], in_=ob[0:P, kk * 128:(kk + 1) * 128],
                                    identity=ident[0:P, 0:P])
            return inst
        kb.op("pe", fn, reads=[ob_r, c_r], writes=[PB_r[1]])
        tr = OT_r[tokc0 // 128]
        kb.op("act", lambda a: a.activation(out=OT[:, k0:k0 + nk, tokc0:tokc0 + P],
                                            in_=PB[1][:, 0:nk * 128].rearrange("p (k t) -> p k t", k=nk)[:, :, 0:P], func=AF.Copy),
              reads=[PB_r[1]], writes=[tr])

    RWT = [(0, 128), (128, 128), (256, 128), (384, 128), (512, 128), (640, 128), (768, 64), (832, 64),
           (896, 128), (1024, 32), (1056, 32)]
    C0 = float(math.exp(-0.5))

    def rwkv_phase(sq, l, OT, OT_r):
        MT = 256
        NCH = MT // 64
        PU = NCH * 2
        with ExitStack() as es:
            A = lambda n_, s_, d_: es.enter_context(sb(n_, s_, d_))
            Wr = A("Wr", [128, 8, 1088], BF16)
            w2 = A("rw2", [64, 256], BF16)
            a2 = A("ra2", [64, 256], BF16)
            v2 = A("rv2", [32, 256], BF16)
            g2a = A("rg2a", [128, 256], BF16)
            g2b = A("rg2b", [128, 256], BF16)
            rln = A("rln", [128, 2, 2, 64], F32)
            blkf = A("rblkf", [128, 128], F32)
            m3 = A("rm3", [128, 3, 128], F32)
            istk = A("ristk", [128, 64], BF16)
            onec = A("ronec", [128, 1], BF16)
            rmask = A("rrm", [128, MT], F32)
            xs = [A("rxs%d" % i, [128, MT], F32) for i in range(6)]
            tt_ = [A("rt%d" % i, [128, MT], F32) for i in range(8)]
            ttb_ = [A("rtb%d" % i, [128, MT], F32) for i in range(9)]
            tb_r = regs(9)
            Gam2 = [A("rGam%d" % i_, [128, 2, MT], F32) for i_ in range(2)]
            AR2 = [A("rAR%d" % i_, [128, 2, 2, NCH, 2, 64], BF16) for i_ in range(2)]
            BK = A("rBK", [128, 2, 2, NCH, 2, 64], BF16)
            VTz = A("rVTz", [128, 2, NCH, 2, 64], BF16)
            rkrz = A("rrkrz", [128, 2, NCH, 2, 64], BF16)
            TW = A("rTW", [64, MT], BF16)
            AL = A("rAL", [64, MT], BF16)
            SGLd = A("rSGLd", [128, NCH, 2, 64], BF16)
            SGL2d = A("rSGL2d", [128, NCH, 2, 64], BF16)
            VR = A("rVR", [32, MT], BF16)
            Vstk2 = [A("rVstk%d" % i_, [128, NCH, 2, 64], BF16) for i_ in range(2)]
            Bblk2 = [A("rBblk%d" % i_, [128, NCH, 2, 128], BF16) for i_ in range(2)]
            Kblk2 = [A("rKblk%d" % i_, [128, NCH, 2, 128], BF16) for i_ in range(2)]
            gtok2 = [A("rgtok%d" % i_, [128, NCH, 2, 64], F32) for i_ in range(2)]
            cb2 = [A("rcb%d" % i_, [128, NCH, 2], F32) for i_ in range(2)]
            MN = [A("rMN%d" % i, [128, PU, 2, 128], F32) for i in range(2)]
            Pm2 = [A("rP%d" % i_, [128, PU, 128], F32) for i_ in range(2)]
            A32 = [A("rA3%d" % i_, [128, PU, 3, 128], BF16) for i_ in range(2)]
            Rs = A("rRs", [128, 128], F32)
            Ub = A("rUb", [128, 128], BF16)
            Tst = A("rTst", [128, 2, 64], F32)
            Tb = A("rTb", [128, 2, 64], BF16)
            tmpS = A("rtmpS", [128, 2, 64], F32)
            ow = A("row", [128, 128], F32)
            ow2 = A("row2", [128, 128], F32)
            obblk = A("robblk", [128, 2, 128], BF16)
            st = A("rst", [128, 32], F32)
            W_r, p_r = Reg(), Reg()
            xs_r, t_r = regs(6), regs(8)
            t_r0 = t_r
            BK_r, VTz_r, rkrz_r = regs(2), regs(2), regs(2)
            Gam_r2, AR_r2 = [regs(2), regs(2)], [regs(2), regs(2)]
            TW_r, AL_r, SGL_r, SGL2_r, VR_r = Reg(), Reg(), Reg(), Reg(), Reg()
            Vstk_r2, Bblk_r2, Kblk_r2, gtok_r2, cb_r2 = regs(2), regs(2), regs(2), regs(2), regs(2)
            MN_r, Rs_r, Ub_r, T_r, Tb_r, tmpS_r = regs(2), Reg(), Reg(), Reg(), Reg(), Reg()
            P_r2, A3_r2 = regs(2), regs(2)
            ow_r, ow2_r, ob_r, st_r = Reg(), Reg(), Reg(), Reg()
            rs_ = cfg.get("rw_stop", 99)
            load_w(Wr, w_in16.ap()[l][:, 1040:2096], 1056, W_r)
            if l >= 1:
                for k in range(8):
                    kb.dma("pool", Wr[:, k, 1056:1088], w_vres.ap()[l - 1][k * 128:(k + 1) * 128, :], writes=[W_r])
            kb.dma("pool", w2[:, :], rwkv_w2.ap()[l], writes=[p_r])
            kb.dma("pool", a2[:, :], rwkv_a2.ap()[l], writes=[p_r])
            if l >= 1:
                kb.dma("pool", v2[:, :], rwkv_v2.ap()[l - 1], writes=[p_r])
            kb.op("pool", lambda g: g.memset(g2b[:, :], 0.0), writes=[p_r])
            kb.dma("pool", g2a[:, :], rwkv_g2.ap()[l][0:128, :], writes=[p_r])
            kb.dma("pool", g2b[0:32, :], rwkv_g2.ap()[l][128:160, :], reads=[p_r], writes=[p_r])
            for wb in range(2):
                for hp in range(2):
                    for hh in range(2):
                        kb.dma("sp", rln[hh * 64:(hh + 1) * 64, wb, hp, :],
                               bcast_rows(rowtab, l * 5120 + 512 + wb * 256 + (2 * hp + hh) * 64, 64, parts=64), writes=[p_r])
            kb.dma("sp", blkf[:, :], c_blk.ap(), writes=[p_r])
            kb.dma("sp", m3[:, :, :], c_rwmask2.ap(), writes=[p_r])
            kb.dma("pool", istk[:, :], c_istk.ap(), writes=[p_r])
            kb.op("pool", lambda g: g.memset(onec[:, :], 1.0), writes=[p_r])
            kb.op("pool", lambda g: g.memset(rmask[:, :], 1.0), writes=[p_r])
            kb.op("pool", lambda g: g.memset(rmask[:, :].rearrange("p (c t) -> p c t", c=NCH)[:, :, 0:1], 0.0), writes=[p_r])
            zl = [(Tst, [T_r]), (Tb, [Tb_r]), (SGL2d, [SGL2_r]), (BK, BK_r), (VTz, VTz_r), (rkrz, rkrz_r), (obblk, [ob_r])]
            for i_ in range(2):
                zl += [(AR2[i_], AR_r2[i_])]
            for tz, rr in zl:
                kb.op("pool", lambda g, tz=tz: g.memset(tz[:], 0.0), writes=rr)

            carry = A("rcarry", [128, 12], F32)
            carry_r = regs(11)
            kb.op("pool", lambda g: g.memset(carry[:, :], 0.0), writes=carry_r)

            def shift_proj(i, tok0, dst_fn):
                c0, n = RWT[i]
                mucol = 2 + i
                pi = i % 2
                tp = ttb_[7] if pi == 0 else ttb_[8]
                tpr = tb_r[7] if pi == 0 else tb_r[8]
                proj_fm(PS[pi][0:n, 0:MT], Wr, c0, n, tok0, MT, [W_r], XT_r[tok0 // 128:tok0 // 128 + MT // 128], PS_r[pi])
                kb.op("act", lambda a: a.activation(out=tp[0:n, 1:MT], in_=PS[pi][0:n, 0:MT - 1], func=AF.Copy,
                                                    scale=ptb[0:n, l, mucol:mucol + 1]), reads=[PS_r[pi], c_r], writes=[tpr, PS_r[pi]])
                kb.op("act", lambda a: a.activation(out=tp[0:n, 0:1], in_=carry[0:n, i:i + 1], func=AF.Copy,
                                                    scale=ptb[0:n, l, mucol:mucol + 1]), reads=[carry_r[i], c_r, tpr], writes=[tpr])
                kb.op("act", lambda a: a.activation(out=carry[0:n, i:i + 1], in_=PS[pi][0:n, MT - 1:MT], func=AF.Copy),
                      reads=[PS_r[pi], carry_r[i]], writes=[carry_r[i], PS_r[pi]])
                dst_ap, dst_regs = dst_fn()
                kb.op("dve", lambda v: v.scalar_tensor_tensor(out=dst_ap, in0=PS[pi][0:n, 0:MT], scalar=ptn[0:n, l, mucol:mucol + 1],
                                                              in1=tp[0:n, 0:MT], op0=ALU.mult, op1=ALU.add),
                      reads=[PS_r[pi], tpr, c_r], writes=dst_regs + [PS_r[pi]])

            def prep(mt):
                tok0 = mt * MT
                par = mt % 2
                t_r = t_r0
                AR, Vstk, Bblk, Kblk, Gam, gtok, cb, Pm, A3 = AR2[par], Vstk2[par], Bblk2[par], Kblk2[par], Gam2[par], gtok2[par], cb2[par], Pm2[par], A32[par]
                AR_r, Vstk_r, Bblk_r, Kblk_r, Gam_r, gtok_r, cb_r, P_r, A3_r = AR_r2[par], Vstk_r2[par], Bblk_r2[par], Kblk_r2[par], Gam_r2[par], gtok_r2[par], cb_r2[par], P_r2[par], A3_r2[par]
                for i in range(6):
                    shift_proj(i, tok0, lambda i=i: (xs[i][:, :], [xs_r[i]]))
                shift_proj(6, tok0, lambda: (tt_[0][0:64, :], [t_r[0]]))
                kb.op("act", lambda a: a.activation(out=TW[:, :], in_=tt_[0][0:64, :], func=AF.Tanh), reads=[t_r[0]], writes=[TW_r])
                shift_proj(7, tok0, lambda: (AL[:, :], [AL_r]))
                shift_proj(8, tok0, lambda: (tt_[0][:, :], [t_r[0]]))
                for hh_ in range(2):
                    kb.op("act", lambda a: a.activation(out=SGLd[:, :, hh_, :], in_=tt_[0][:, :].rearrange("p (c i) -> p c i", c=NCH), func=AF.Sigmoid),
                          reads=[t_r[0]], writes=[SGL_r])
                shift_proj(9, tok0, lambda: (tt_[0][0:32, :], [t_r[0]]))
                for hh_ in range(2):
                    kb.op("act", lambda a: a.activation(out=SGL2d[0:32, :, hh_, :], in_=tt_[0][0:32, :].rearrange("p (c i) -> p c i", c=NCH), func=AF.Sigmoid),
                          reads=[t_r[0]], writes=[SGL2_r])
                if l >= 1:
                    shift_proj(10, tok0, lambda: (VR[:, :], [VR_r]))
                if rs_ <= 1:
                    return
                for hp in range(2):
                    rT, kT, vT = xs[hp], xs[2 + hp], xs[4 + hp]
                    rT_r, kT_r, vT_r = xs_r[hp], xs_r[2 + hp], xs_r[4 + hp]
                    t1, t2, t3, t4, t5, t6, t7 = tt_[0:7] if hp == 0 else ttb_[0:7]
                    t_r = t_r0 if hp == 0 else tb_r
                    kb.op("pe", lambda pe: pe.matmul(PS[2][:, 0:MT], lhsT=w2[:, hp * 128:(hp + 1) * 128], rhs=TW[:, :], start=True, stop=True),
                          reads=[p_r, TW_r], writes=[PS_r[2]])
                    kb.op("act", lambda a: a.activation(out=t1[:, :], in_=PS[2][:, 0:MT], func=AF.Sigmoid, bias=ptb[:, l, 13 + hp:14 + hp]),
                          reads=[PS_r[2], c_r], writes=[t_r[0]])
                    kb.op("dve", lambda v: v.tensor_tensor_scan(out=t2[:, :], data0=rmask[:, :], data1=t1[:, :], initial=0.0,
                                                                op0=ALU.mult, op1=ALU.add), reads=[t_r[0], p_r], writes=[t_r[1]])
                    kb.op("act", lambda a: a.activation(out=Gam[:, hp, :], in_=t2[:, :], func=AF.Exp, scale=-C0), reads=[t_r[1]], writes=[Gam_r[hp]])
                    kb.op("act", lambda a: a.activation(out=t3[:, :], in_=t2[:, :], func=AF.Exp, scale=C0), reads=[t_r[1]], writes=[t_r[2]])
                    kb.op("dve", lambda v: v.tensor_tensor(out=t4[:, :], in0=t2[:, :], in1=t1[:, :], op=ALU.subtract),
                          reads=[t_r[0], t_r[1]], writes=[t_r[3]])
                    kb.op("act", lambda a: a.activation(out=t4[:, :], in_=t4[:, :], func=AF.Exp, scale=-C0), reads=[t_r[3]], writes=[t_r[3]])
                    kb.op("pe", lambda pe: pe.matmul(PS[3][:, 0:MT], lhsT=a2[:, hp * 128:(hp + 1) * 128], rhs=AL[:, :], start=True, stop=True),
                          reads=[p_r, AL_r], writes=[PS_r[3]])
                    kb.op("act", lambda a: a.activation(out=t5[:, :], in_=PS[3][:, 0:MT], func=AF.Sigmoid, bias=ptb[:, l, 15 + hp:16 + hp]),
                          reads=[PS_r[3], c_r], writes=[t_r[4]])
                    kb.op("dve", lambda v: v.tensor_scalar(out=t6[:, :], in0=kT[:, :], scalar1=ptb[:, l, 19 + hp:20 + hp], scalar2=None, op0=ALU.mult),
                          reads=[kT_r, c_r], writes=[t_r[5]])
                    kb.op("act", lambda a: a.activation(out=t7[:, :], in_=t6[:, :], func=AF.Square), reads=[t_r[5]], writes=[t_r[6]])
                    kb.op("pe", lambda pe: pe.matmul(PS[4][:, 0:MT], lhsT=blkf[:, :], rhs=t7[:, :], start=True, stop=True),
                          reads=[p_r, t_r[6]], writes=[PS_r[4]])
                    kb.op("act", lambda a: a.activation(out=t7[:, :], in_=PS[4][:, 0:MT], func=AF.Sqrt), reads=[PS_r[4], t_r[6]], writes=[t_r[6]])
                    kb.op("dve", lambda v: v.tensor_scalar(out=t7[:, :], in0=t7[:, :], scalar1=1e-12, scalar2=None, op0=ALU.max),
                          reads=[t_r[6]], writes=[t_r[6]])
                    kb.op("dve", lambda v: v.reciprocal(out=t7[:, :], in_=t7[:, :]), reads=[t_r[6]], writes=[t_r[6]])
                    kb.op("dve", lambda v: v.tensor_tensor(out=t6[:, :], in0=t6[:, :], in1=t7[:, :], op=ALU.mult),
                          reads=[t_r[5], t_r[6]], writes=[t_r[5]])
                    kb.op("dve", lambda v: v.tensor_scalar(out=t7[:, :], in0=t5[:, :], scalar1=-1.0, scalar2=ptb[:, l, 21 + hp:22 + hp],
                                                           op0=ALU.add, op1=ALU.mult), reads=[t_r[4], t_r[6], c_r], writes=[t_r[6]])
                    kb.op("dve", lambda v: v.scalar_tensor_tensor(out=t7[:, :], in0=t7[:, :], scalar=1.0, in1=kT[:, :], op0=ALU.add, op1=ALU.mult),
                          reads=[t_r[6], kT_r], writes=[t_r[6]])
                    v3 = lambda ap: ap.rearrange("p (c i) -> p c i", c=NCH)
                    for hh in range(2):
                        pr = slice(hh * 64, hh * 64 + 64)
                        kb.op("dve", lambda v: v.scalar_tensor_tensor(out=AR[pr, hp, 0, :, hh, :], in0=v3(t6[pr, :]), scalar=-1.0, in1=v3(t4[pr, :]),
                                                                      op0=ALU.mult, op1=ALU.mult), reads=[t_r[5], t_r[3]], writes=[AR_r[hp]])
                        kb.op("dve", lambda v: v.tensor_tensor(out=AR[pr, hp, 1, :, hh, :], in0=v3(rT[pr, :]), in1=v3(Gam[pr, hp, :]), op=ALU.mult),
                              reads=[rT_r, Gam_r[hp]], writes=[AR_r[hp]])
                    kb.op("dve", lambda v: v.tensor_tensor(out=t1[:, :], in0=t6[:, :], in1=t5[:, :], op=ALU.mult),
                          reads=[t_r[5], t_r[4], t_r[0]], writes=[t_r[0]])
                    for hh in range(2):
                        pr = slice(hh * 64, hh * 64 + 64)
                        kb.op("dve", lambda v: v.tensor_tensor(out=BK[pr, hp, 0, :, hh, :], in0=v3(t1[pr, :]), in1=v3(t3[pr, :]), op=ALU.mult),
                              reads=[t_r[0], t_r[2]], writes=[BK_r[hp]])
                        kb.op("dve", lambda v: v.tensor_tensor(out=BK[pr, hp, 1, :, hh, :], in0=v3(t7[pr, :]), in1=v3(t3[pr, :]), op=ALU.mult),
                              reads=[t_r[6], t_r[2]], writes=[BK_r[hp]])
                        kb.op("dve", lambda v: v.scalar_tensor_tensor(out=rkrz[pr, hp, :, hh, :], in0=v3(rT[pr, :]), scalar=ptb[pr, l, 23 + hp:24 + hp],
                                                                      in1=v3(t7[pr, :]), op0=ALU.mult, op1=ALU.mult),
                              reads=[rT_r, t_r[6], c_r], writes=[rkrz_r[hp]])
                    if l == 0:
                        kb.dma("sp", vfirst_d.ap()[hp, :, sq * 0 + tok0:tok0 + MT], vT[:, :], reads=[vT_r], writes=[vfirst_r[hp]])
                    else:
                        kb.op("pe", lambda pe: pe.matmul(PS[5][:, 0:MT], lhsT=v2[:, hp * 128:(hp + 1) * 128], rhs=VR[:, :], start=True, stop=True),
                              reads=[p_r, VR_r], writes=[PS_r[5]])
                        kb.op("act", lambda a: a.activation(out=t1[:, :], in_=PS[5][:, 0:MT], func=AF.Sigmoid, bias=ptb[:, l, 17 + hp:18 + hp]),
                              reads=[PS_r[5], c_r, t_r[0]], writes=[t_r[0]])
                        kb.dma("sp", t2[:, :], vfirst_d.ap()[hp, :, tok0:tok0 + MT], reads=[vfirst_r[hp], t_r[1]], writes=[t_r[1]])
                        kb.op("dve", lambda v: v.tensor_tensor(out=t2[:, :], in0=t2[:, :], in1=vT[:, :], op=ALU.subtract),
                              reads=[t_r[1], vT_r], writes=[t_r[1]])
                        kb.op("dve", lambda v: v.tensor_tensor(out=t2[:, :], in0=t2[:, :], in1=t1[:, :], op=ALU.mult),
                              reads=[t_r[1], t_r[0]], writes=[t_r[1]])
                        kb.op("dve", lambda v: v.tensor_tensor(out=vT[:, :], in0=vT[:, :], in1=t2[:, :], op=ALU.add),
                              reads=[t_r[1], vT_r], writes=[vT_r])
                    for hh in range(2):
                        pr = slice(hh * 64, hh * 64 + 64)
                        kb.op("act", lambda a: a.activation(out=VTz[pr, hp, :, hh, :], in_=vT[pr, :].rearrange("p (c i) -> p c i", c=NCH), func=AF.Copy),
                              reads=[vT_r], writes=[VTz_r[hp]])
                if rs_ <= 2:
                    return
                fl = lambda ap: ap.rearrange("p a b -> p (a b)")
                def fn(pe):
                    inst = None
                    for c in range(NCH):
                        for hp in range(2):
                            inst = pe.matmul(PS[2][:, (c * 2 + hp) * 64:(c * 2 + hp + 1) * 64], lhsT=fl(VTz[:, hp, c, :, :]), rhs=istk[:, :], start=True, stop=True)
                    return inst
                kb.op("pe", fn, reads=VTz_r + [p_r], writes=[PS_r[2]])
                kb.op("act", lambda a: a.activation(out=Vstk[:, :, :, :], in_=PS[2][:, :].rearrange("p (c h d) -> p c h d", c=NCH, h=2), func=AF.Copy),
                      reads=[PS_r[2]], writes=[Vstk_r, PS_r[2]])

                def fn(pe):
                    inst = None
                    for c in range(NCH):
                        for hp in range(2):
                            inst = pe.matmul(PS[3][:, c * 2 + hp:c * 2 + hp + 1], lhsT=fl(rkrz[:, hp, c, :, :]), rhs=onec[:, :], start=True, stop=True)
                    return inst
                kb.op("pe", fn, reads=rkrz_r + [p_r], writes=[PS_r[3]])
                kb.op("act", lambda a: a.activation(out=cb[:, :, :], in_=PS[3][:, 0:NCH * 2].rearrange("p (c h) -> p c h", c=NCH), func=AF.Copy),
                      reads=[PS_r[3]], writes=[cb_r, PS_r[3]])
                for bk, dst, dst_r, pbi in ((0, Bblk, Bblk_r, 0), (1, Kblk, Kblk_r, 1)):
                    def fn(pe, bk=bk, pbi=pbi):
                        inst = None
                        for c in range(NCH):
                            for hp in range(2):
                                inst = pe.transpose(out=PB[pbi][:, (c * 2 + hp) * 128:(c * 2 + hp + 1) * 128], in_=fl(BK[:, hp, bk, c, :, :]), identity=ident[:, :])
                        return inst
                    kb.op("pe", fn, reads=BK_r + [c_r], writes=[PB_r[pbi]])
                    kb.op("dve", lambda v, dst=dst, pbi=pbi: v.tensor_copy(out=dst[:, :, :, :], in_=PB[pbi][:, :].rearrange("p (c h d) -> p c h d", c=NCH, h=2)),
                          reads=[PB_r[pbi]], writes=[dst_r, PB_r[pbi]])
                for c2 in range(NCH // 2):
                    def fn(pe):
                        inst = None
                        for cc_ in range(2):
                            c = c2 * 2 + cc_
                            pe.matmul(PS[3][:, cc_ * 256:(cc_ + 1) * 256], lhsT=fl(SGLd[:, c, :, :]), rhs=g2a[:, :], start=True, stop=False)
                            inst = pe.matmul(PS[3][:, cc_ * 256:(cc_ + 1) * 256], lhsT=fl(SGL2d[:, c, :, :]), rhs=g2b[:, :], start=False, stop=True)
                        return inst
                    kb.op("pe", fn, reads=[SGL_r, SGL2_r, p_r], writes=[PS_r[3]])
                    for hh in range(2):
                        pr = slice(hh * 64, hh * 64 + 64)
                        kb.op("act", lambda a: a.activation(out=gtok[pr, c2 * 2:c2 * 2 + 2, :, :],
                                                            in_=PS[3][pr, :].rearrange("p (c h g d) -> p c h g d", c=2, h=2, g=2)[:, :, :, hh, :], func=AF.Copy),
                              reads=[PS_r[3]], writes=[gtok_r, PS_r[3]])
                for c in range(NCH):
                    for hp in range(2):
                        pu = c * 2 + hp
                        px, py = (4, 5) if pu % 2 == 0 else (2, 3)

                        def fn(pe, px=px, py=py):
                            pe.matmul(PS[px][:, 0:128], lhsT=fl(AR[:, hp, 0, c, :, :]), rhs=fl(BK[:, hp, 0, c, :, :]), start=True, stop=True)
                            pe.matmul(PS[px][:, 128:384], lhsT=fl(BK[:, hp, 0, c, :, :]), rhs=AR[:, hp, :, c, :, :].rearrange("p a h i -> p a (h i)"),
                                      start=True, stop=True)
                            return pe.matmul(PS[py][:, 0:256], lhsT=fl(BK[:, hp, 1, c, :, :]), rhs=AR[:, hp, :, c, :, :].rearrange("p a h i -> p a (h i)"),
                                             start=True, stop=True)
                        kb.op("pe", fn, reads=AR_r + BK_r, writes=[PS_r[px], PS_r[py]])
                        kb.op("dve", lambda v, px=px: v.tensor_tensor(out=MN[0][:, pu, :, :], in0=PS[px][:, 0:256].rearrange("p (a b) -> p a b", a=2),
                                                                      in1=m3[:, 0:2, :], op=ALU.mult), reads=[PS_r[px], p_r], writes=[MN_r[0], PS_r[px]])
                        kb.op("dve", lambda v, px=px: v.tensor_tensor(out=A3[:, pu, 0, :], in0=PS[px][:, 256:384], in1=m3[:, 2, :], op=ALU.mult),
                              reads=[PS_r[px], p_r], writes=[A3_r, PS_r[px]])
                        kb.op("dve", lambda v, py=py: v.tensor_tensor(out=A3[:, pu, 1:3, :], in0=PS[py][:, 0:256].rearrange("p (a b) -> p a b", a=2),
                                                                      in1=m3[:, 1:3, :], op=ALU.mult), reads=[PS_r[py], p_r], writes=[A3_r, PS_r[py]])
                if rs_ <= 3:
                    return
                kb.op("dve", lambda v: v.tensor_tensor(out=Pm[:, :, :], in0=MN[0][:, :, 1, :],
                                                      in1=bass.AP(identf, 0, [[128, 128], [0, PU], [1, 128]]), op=ALU.add),
                      reads=[MN_r[0], c_r], writes=[P_r])
                cur = 0
                for lev in range(5):
                    nxt = 1 - cur
                    lastlev = (lev == 4)
                    for g2_ in range(PU // 2):
                        pi = 4 + (g2_ % 2)

                        def fn(pe, pi=pi):
                            inst = None
                            for q in range(2):
                                u = g2_ * 2 + q
                                inst = pe.matmul(PS[pi][:, q * 256:q * 256 + 128], lhsT=MN[cur][:, u, 1, :], rhs=MN[cur][:, u, 0, :], start=True, stop=True)
                                if not lastlev:
                                    inst = pe.matmul(PS[pi][:, q * 256 + 128:q * 256 + 256], lhsT=MN[cur][:, u, 0, :], rhs=MN[cur][:, u, 1, :],
                                                     start=True, stop=True)
                            return inst
                        kb.op("pe", fn, reads=[MN_r[cur]], writes=[PS_r[pi]])
                        kb.op("act", lambda a, pi=pi: a.activation(out=MN[nxt][:, g2_ * 2:g2_ * 2 + 2, :, :],
                                                                   in_=PS[pi][:, :].rearrange("p (u a b) -> p u a b", u=2, a=2), func=AF.Copy),
                              reads=[PS_r[pi]], writes=[MN_r[nxt], PS_r[pi]])
                    for g4_ in range(PU // 4):
                        pi = 2 + (g4_ % 2)

                        def fn(pe, pi=pi):
                            inst = None
                            for q in range(4):
                                u = g4_ * 4 + q
                                inst = pe.matmul(PS[pi][:, q * 128:(q + 1) * 128], lhsT=MN[nxt][:, u, 0, :], rhs=Pm[:, u, :], start=True, stop=True)
                            return inst
                        kb.op("pe", fn, reads=[MN_r[nxt], P_r], writes=[PS_r[pi]])
                        kb.op("dve", lambda v, pi=pi: v.tensor_tensor(out=Pm[:, g4_ * 4:g4_ * 4 + 4, :], in0=PS[pi][:, :].rearrange("p (u b) -> p u b", u=4),
                                                                      in1=Pm[:, g4_ * 4:g4_ * 4 + 4, :], op=ALU.add), reads=[PS_r[pi], P_r], writes=[P_r, PS_r[pi]])
                    cur = nxt

            def chain(mt):
                tok0 = mt * MT
                par = mt % 2
                AR, Vstk, Bblk, Kblk, Gam, gtok, cb, Pm, A3 = AR2[par], Vstk2[par], Bblk2[par], Kblk2[par], Gam2[par], gtok2[par], cb2[par], Pm2[par], A32[par]
                AR_r, Vstk_r, Bblk_r, Kblk_r, Gam_r, gtok_r, cb_r, P_r, A3_r = AR_r2[par], Vstk_r2[par], Bblk_r2[par], Kblk_r2[par], Gam_r2[par], gtok_r2[par], cb_r2[par], P_r2[par], A3_r2[par]
                fl = lambda ap: ap.rearrange("p a b -> p (a b)")
                if rs_ <= 4:
                    return
                for c in range(NCH):
                    def fn(pe):
                        inst = None
                        for hp in range(2):
                            pu = c * 2 + hp
                            pe.matmul(PS[0][:, hp * 64:(hp + 1) * 64], lhsT=A3[:, pu, 1, :], rhs=Vstk[:, c, hp, :], start=True, stop=False)
                            inst = pe.matmul(PS[0][:, hp * 64:(hp + 1) * 64], lhsT=fl(AR[:, hp, 0, c, :, :]), rhs=Tb[:, hp, :], start=False, stop=True)
                        return inst
                    kb.op("pe", fn, reads=[A3_r, Vstk_r, Tb_r] + AR_r, writes=[PS_r[0]])
                    kb.op("act", lambda a: a.activation(out=Rs[:, :], in_=PS[0][:, 0:128], func=AF.Copy), reads=[PS_r[0]], writes=[Rs_r, PS_r[0]])

                    def fn(pe):
                        inst = None
                        for hp in range(2):
                            pu = c * 2 + hp
                            inst = pe.matmul(PS[1][:, hp * 64:(hp + 1) * 64], lhsT=Pm[:, pu, :], rhs=Rs[:, hp * 64:(hp + 1) * 64], start=True, stop=True)
                        return inst
                    kb.op("pe", fn, reads=[P_r, Rs_r], writes=[PS_r[1]])
                    kb.op("act", lambda a: a.activation(out=Ub[:, :], in_=PS[1][:, 0:128], func=AF.Copy), reads=[PS_r[1]], writes=[Ub_r, PS_r[1]])

                    def fn(pe):
                        inst = None
                        for hp in range(2):
                            pu = c * 2 + hp
                            pe.matmul(PS[0][:, hp * 64:(hp + 1) * 64], lhsT=fl(AR[:, hp, 1, c, :, :]), rhs=Tb[:, hp, :], start=True, stop=False)
                            pe.matmul(PS[0][:, hp * 64:(hp + 1) * 64], lhsT=A3[:, pu, 0, :], rhs=Ub[:, hp * 64:(hp + 1) * 64], start=False, stop=False)
                            inst = pe.matmul(PS[0][:, hp * 64:(hp + 1) * 64], lhsT=A3[:, pu, 2, :], rhs=Vstk[:, c, hp, :], start=False, stop=True)
                        return inst
                    kb.op("pe", fn, reads=[A3_r, Vstk_r, Tb_r, Ub_r] + AR_r, writes=[PS_r[0]])

                    def fn(pe):
                        inst = None
                        for hp in range(2):
                            pe.matmul(PS[1][:, hp * 64:(hp + 1) * 64], lhsT=Bblk[:, c, hp, :], rhs=Ub[:, hp * 64:(hp + 1) * 64], start=True, stop=False)
                            inst = pe.matmul(PS[1][:, hp * 64:(hp + 1) * 64], lhsT=Kblk[:, c, hp, :], rhs=Vstk[:, c, hp, :], start=False, stop=True)
                        return inst
                    kb.op("pe", fn, reads=[Bblk_r, Kblk_r, Vstk_r, Ub_r], writes=[PS_r[1]])
                    kb.op("dve", lambda v: v.tensor_tensor(out=tmpS[:, :, :], in0=PS[1][:, 0:128].rearrange("p (h d) -> p h d", h=2), in1=Tst[:, :, :], op=ALU.add),
                          reads=[PS_r[1], T_r], writes=[tmpS_r, PS_r[1]])
                    kb.op("dve", lambda v: v.tensor_tensor(out=Tst[:, :, :], in0=tmpS[:, :, :],
                                                          in1=bass.AP(Gam, c * 64 + 63, [[2 * MT, 128], [MT, 2], [0, 64]]), op=ALU.mult),
                          reads=[tmpS_r, Gam_r[0], Gam_r[1]], writes=[T_r])
                    kb.op("act", lambda a: a.activation(out=Tb[:, :, :], in_=Tst[:, :, :], func=AF.Copy), reads=[T_r], writes=[Tb_r])
                    if rs_ <= 5:
                        continue
                    v2_ = lambda ap: ap.rearrange("p (h d) -> p h d", h=2)
                    kb.op("act", lambda a: a.activation(out=ow[:, :], in_=PS[0][:, 0:128], func=AF.Copy), reads=[PS_r[0]], writes=[ow_r, PS_r[0]])
                    kb.op("act", lambda a: a.activation(out=ow2[:, :], in_=ow[:, :], func=AF.Square), reads=[ow_r], writes=[ow2_r])
                    kb.op("dve", lambda v: v.reduce_sum(out=st[:, 0:2], in_=v2_(ow[:, :]), axis=AX.X), reads=[ow_r], writes=[st_r])
                    kb.op("dve", lambda v: v.reduce_sum(out=st[:, 2:4], in_=v2_(ow2[:, :]), axis=AX.X), reads=[ow2_r, st_r], writes=[st_r])
                    kb.op("dve", lambda v: v.tensor_scalar(out=st[:, 4:8], in0=st[:, 0:4], scalar1=1.0 / 64.0, scalar2=None, op0=ALU.mult), reads=[st_r], writes=[st_r])
                    kb.op("dve", lambda v: v.tensor_tensor(out=st[:, 8:10], in0=st[:, 4:6], in1=st[:, 4:6], op=ALU.mult), reads=[st_r], writes=[st_r])
                    kb.op("dve", lambda v: v.tensor_tensor(out=st[:, 10:12], in0=st[:, 6:8], in1=st[:, 8:10], op=ALU.subtract), reads=[st_r], writes=[st_r])
                    kb.op("act", lambda a: a.activation(out=st[:, 12:14], in_=st[:, 10:12], func=AF.Sqrt, bias=RW_EPS, scale=1.0), reads=[st_r], writes=[st_r])
                    kb.op("dve", lambda v: v.reciprocal(out=st[:, 14:16], in_=st[:, 12:14]), reads=[st_r], writes=[st_r])
                    for hp in range(2):
                        kb.op("dve", lambda v: v.tensor_scalar(out=ow[:, hp * 64:(hp + 1) * 64], in0=ow[:, hp * 64:(hp + 1) * 64],
                                                               scalar1=st[:, 4 + hp:5 + hp], scalar2=st[:, 14 + hp:15 + hp],
                                                               op0=ALU.subtract, op1=ALU.mult), reads=[ow_r, st_r], writes=[ow_r])
                    kb.op("dve", lambda v: v.tensor_tensor(out=v2_(ow[:, :]), in0=v2_(ow[:, :]), in1=rln[:, 0, :, :], op=ALU.mult), reads=[ow_r, p_r], writes=[ow_r])
                    kb.op("dve", lambda v: v.tensor_tensor(out=v2_(ow[:, :]), in0=v2_(ow[:, :]), in1=rln[:, 1, :, :], op=ALU.add), reads=[ow_r, p_r], writes=[ow_r])
                    if rs_ <= 6:
                        continue
                    for hp in range(2):
                        kb.op("dve", lambda v: v.scalar_tensor_tensor(out=ow[:, hp * 64:(hp + 1) * 64], in0=Vstk[:, c, hp, :], scalar=cb[:, c, hp:hp + 1],
                                                                      in1=ow[:, hp * 64:(hp + 1) * 64], op0=ALU.mult, op1=ALU.add),
                              reads=[Vstk_r, cb_r, ow_r], writes=[ow_r])
                    for hh in range(2):
                        pr = slice(hh * 64, hh * 64 + 64)
                        kb.op("dve", lambda v: v.tensor_tensor(out=obblk[pr, :, hh * 64:(hh + 1) * 64], in0=v2_(ow[pr, :]), in1=gtok[pr, c, :, :], op=ALU.mult),
                              reads=[ow_r, gtok_r], writes=[ob_r])

                    if rs_ <= 7:
                        continue

                    def fn(pe):
                        inst = None
                        for hp in range(2):
                            inst = pe.matmul(PS[1][:, 256 + hp * 64:256 + (hp + 1) * 64], lhsT=obblk[:, hp, :], rhs=istk[:, :], start=True, stop=True)
                        return inst
                    kb.op("pe", fn, reads=[ob_r, p_r], writes=[PS_r[1]])
                    tokc = tok0 + c * 64
                    kb.op("act", lambda a: a.activation(out=OT[:, 2:4, tokc:tokc + 64], in_=PS[1][:, 256:384].rearrange("p (h t) -> p h t", h=2), func=AF.Copy),
                          reads=[PS_r[1]], writes=[OT_r[tokc // 128], PS_r[1]])
            nmt = S // MT if rs_ > 0 else 0
            if nmt:
                prep(0)
            for mt in range(nmt):
                if mt + 1 < nmt:
                    prep(mt + 1)
                chain(mt)
            kb.barrier()

    NQ, NQS, NKC, NKS, NKW, NVC, NVS, NGT = 0, 512, 1024, 1280, 1536, 1792, 1920, 2176
    GK = 1.5957691216057308

    def nsa_phase(sq, l, OT, OT_r):
        ns_ = cfg.get("nsa_stop", 99)
        with ExitStack() as es:
            A = lambda n_, s_, d_: es.enter_context(sb(n_, s_, d_))
            QT = A("nQT", [128, 4, S], BF16)
            KTz = A("nKTz", [128, 3, 2, S], BF16)
            VCz = A("nVCz", [128, 2, S], BF16)
            VS = A("nVS", [128, NT, 2, 65], BF16)
            VW = A("nVW", [128, NT, 2, 65], BF16)
            gsig = A("ngsig", [128, NT, 24], F32)
            kcmpTz = A("nkcmp", [128, 2, 128], BF16)
            vcmp = A("nvcmp", [128, 2, 97], BF16)
            QT_r, KT_r, VC_r, VS_r, VW_r, gs_r = regs(4), regs(4), regs(4), regs(NT), regs(NT), regs(NT)
            kc_r, vcm_r = Reg(), Reg()
            kb.op("pool", lambda g: g.memset(KTz[:], 0.0), writes=KT_r)
            kb.op("pool", lambda g: g.memset(VCz[:], 0.0), writes=VC_r)
            kb.op("pool", lambda g: g.memset(VS[:, :, :, 64:65], 1.0), writes=VS_r)
            kb.op("pool", lambda g: g.memset(VW[:, :, :, 64:65], 1.0), writes=VW_r)
            kb.op("pool", lambda g: g.memset(kcmpTz[:], 0.0), writes=[kc_r])
            kb.op("pool", lambda g: g.memset(vcmp[:], 0.0), writes=[vcm_r])
            with ExitStack() as es1:
                A1 = lambda n_, s_, d_: es1.enter_context(sb(n_, s_, d_))
                Wn = A1("nWn", [128, 8, 2200], BF16)
                rp = A1("nrp", [128, 2, 512], F32)
                t1 = A1("nt1", [128, 512], F32)
                t2 = A1("nt2", [128, 512], F32)
                W_r, rp_r, t1_r, t2_r = Reg(), Reg(), Reg(), Reg()
                load_w(Wn, w_nsa16.ap()[l], 2200, W_r)
                for mt in range(4):
                    tok0 = mt * 512
                    bl = slice(tok0, tok0 + 512)
                    xr = XT_r[mt * 4:mt * 4 + 4]
                    kb.dma("sp", rp[:, 0, :], c_rope.ap()[0][:, bl], writes=[rp_r])
                    kb.dma("sp", rp[:, 1, :], c_rope.ap()[1][:, bl], writes=[rp_r])
                    for i in range(4):
                        proj_fm(PS[0][:, :], Wn, NQ + i * 128, 128, tok0, 512, [W_r], xr, PS_r[0])
                        proj_fm(PS[1][:, :], Wn, NQS + i * 128, 128, tok0, 512, [W_r], xr, PS_r[1])
                        kb.op("dve", lambda v: v.tensor_tensor(out=t1[:, :], in0=PS[0][:, :], in1=rp[:, 0, :], op=ALU.mult),
                              reads=[PS_r[0], rp_r], writes=[t1_r])
                        kb.op("dve", lambda v: v.scalar_tensor_tensor(out=t2[:, :], in0=PS[1][:, :], scalar=0.125, in1=rp[:, 1, :],
                                                                      op0=ALU.mult, op1=ALU.mult), reads=[PS_r[1], rp_r], writes=[t2_r])
                        kb.op("dve", lambda v: v.scalar_tensor_tensor(out=QT[:, i, bl], in0=t1[:, :], scalar=0.125, in1=t2[:, :],
                                                                      op0=ALU.mult, op1=ALU.add), reads=[t1_r, t2_r], writes=[QT_r[mt]])
                    for ty, c0 in ((0, NKC), (1, NKS), (2, NKW)):
                        proj_fm(PS[0][:, :], Wn, c0, 128, tok0, 512, [W_r], xr, PS_r[0])
                        proj_fm(PS[1][:, :], Wn, c0 + 128, 128, tok0, 512, [W_r], xr, PS_r[1])
                        kb.op("dve", lambda v: v.tensor_tensor(out=t1[:, :], in0=PS[0][:, :], in1=rp[:, 0, :], op=ALU.mult),
                              reads=[PS_r[0], rp_r], writes=[t1_r])
                        kb.op("dve", lambda v: v.tensor_tensor(out=t2[:, :], in0=PS[1][:, :], in1=rp[:, 1, :], op=ALU.mult),
                              reads=[PS_r[1], rp_r], writes=[t2_r])
                        for g in range(2):
                            pr = slice(g * 64, g * 64 + 64)
                            kb.op("dve", lambda v: v.tensor_tensor(out=KTz[pr, ty, g, bl], in0=t1[pr, :], in1=t2[pr, :], op=ALU.add),
                                  reads=[t1_r, t2_r], writes=[KT_r[mt]])
                    proj_fm(PS[2][:, :], Wn, NVC, 128, tok0, 512, [W_r], xr, PS_r[2])
                    for g in range(2):
                        pr = slice(g * 64, g * 64 + 64)
                        kb.op("act", lambda a: a.activation(out=VCz[pr, g, bl], in_=PS[2][pr, :], func=AF.Copy),
                              reads=[PS_r[2]], writes=[VC_r[mt]])
                    for j in range(4):
                        tt = mt * 4 + j
                        proj_tm(PS[3][:, 0:256], Wn, NVS, 256, tt, [W_r], PS_r[3])
                        kb.op("act", lambda a: a.activation(out=VS[:, tt, :, 0:64], in_=PS[3][:, 0:128].rearrange("p (g d) -> p g d", g=2), func=AF.Copy),
                              reads=[PS_r[3]], writes=[VS_r[tt], PS_r[3]])
                        kb.op("act", lambda a: a.activation(out=VW[:, tt, :, 0:64], in_=PS[3][:, 128:256].rearrange("p (g d) -> p g d", g=2), func=AF.Copy),
                              reads=[PS_r[3]], writes=[VW_r[tt], PS_r[3]])
                        proj_tm(PS[4][:, 0:24], Wn, NGT, 24, tt, [W_r], PS_r[4])
                        kb.op("act", lambda a: a.activation(out=gsig[:, tt, :], in_=PS[4][:, 0:24], func=AF.Sigmoid),
                              reads=[PS_r[4]], writes=[gs_r[tt]])
                kb.barrier()
            if ns_ <= 1:
                kb.barrier()
                return
            with ExitStack() as es2:
                A2 = lambda n_, s_, d_: es2.enter_context(sb(n_, s_, d_))
                w1d = A2("nw1d", [128, 32, 256], BF16)
                w2d = A2("nw2d", [128, 2, 128], BF16)
                wv2 = A2("nwv2", [128, 2, 64], BF16)
                posz = A2("nposz", [128, 2, 32], BF16)
                hc = A2("nhc", [128, 2], F32)
                gx = A2("ngx", [128, 128], F32)
                gw = A2("ngw", [128, 128], F32)
                gh = A2("ngh", [128, 2, 128], BF16)
                w1_r, w2_r, hc_r, gx_r, gw_r, gh_r = Reg(), Reg(), Reg(), Reg(), Reg(), Reg()
                kb.op("pool", lambda g: g.memset(posz[:], 0.0), writes=[w2_r])
                kb.op("pool", lambda g: g.memset(gh[:], 0.0), writes=[gh_r])
                for kv in range(2):
                    kb.dma("pool", posz[0:64, kv, :], nsa_posT.ap()[l, kv], reads=[w2_r], writes=[w2_r])
                for half in range(2):
                    kb.dma("pool", w2d[:, :, half * 64:(half + 1) * 64], nsa_wk2.ap()[l].rearrange("(t p) n -> p t n", p=128), writes=[w2_r])
                kb.dma("pool", wv2[:, :, :], nsa_wv2.ap()[l].rearrange("(t p) n -> p t n", p=128), writes=[w2_r])
                kb.dma("pool", vcmp[:, 0, 65:97], c_ovl.ap(), reads=[vcm_r], writes=[vcm_r])
                kb.dma("pool", vcmp[:, 1, 65:97], c_ovl.ap(), reads=[vcm_r], writes=[vcm_r])
                kb.op("pool", lambda g: g.memset(vcmp[0:127, :, 64:65], 1.0), reads=[vcm_r], writes=[vcm_r])
                for kv, w1src in ((0, wk1_16), (1, wv1_16)):
                    src3 = w1src.ap()[l].rearrange("(l d) n -> d l n", d=64)
                    for half in range(2):
                        for l4 in range(8):
                            kb.dma("sp", w1d[half * 64:(half + 1) * 64, l4 * 4:(l4 + 1) * 4, :], src3[:, l4 * 4:(l4 + 1) * 4, :],
                                   reads=[w16_r], writes=[w1_r])
                    srcz = (lambda g: KTz[:, 0, g, :]) if kv == 0 else (lambda g: VCz[:, g, :])
                    src_regs = KT_r if kv == 0 else VC_r
                    for hf in range(2):
                        def fn(pe):
                            inst = None
                            for ll in range(32):
                                inst = pe.matmul(PS[0][:, 0:1], lhsT=w1d[:, ll, hf * 128:(hf + 1) * 128], rhs=posz[:, kv, ll:ll + 1],
                                                 start=(ll == 0), stop=(ll == 31))
                            return inst
                        kb.op("pe", fn, reads=[w1_r, w2_r], writes=[PS_r[0]])
                        kb.op("act", lambda a: a.activation(out=hc[:, hf:hf + 1], in_=PS[0][:, 0:1], func=AF.Copy), reads=[PS_r[0]], writes=[hc_r])
                    for g in range(2):
                        for hf in range(2):
                            def fn(pe):
                                inst = None
                                for ll in range(32):
                                    rhs = bass.AP(srcz(g).tensor, srcz(g).offset + ll, [srcz(g).ap[0], [16, 127]])
                                    inst = pe.matmul(PS[1][:, 0:127], lhsT=w1d[:, ll, hf * 128:(hf + 1) * 128], rhs=rhs,
                                                     start=(ll == 0), stop=(ll == 31))
                                return inst
                            kb.op("pe", fn, reads=[w1_r] + src_regs, writes=[PS_r[1]])
                            kb.op("act", lambda a: a.activation(out=gx[:, 0:127], in_=PS[1][:, 0:127], func=AF.Identity, bias=hc[:, hf:hf + 1], scale=1.0),
                                  reads=[PS_r[1], hc_r], writes=[gx_r])
                            kb.op("dve", lambda v: v.tensor_tensor(out=gw[:, 0:127], in0=gx[:, 0:127], in1=gx[:, 0:127], op=ALU.mult),
                                  reads=[gx_r], writes=[gw_r])
                            kb.op("dve", lambda v: v.tensor_scalar(out=gw[:, 0:127], in0=gw[:, 0:127], scalar1=0.044715, scalar2=1.0,
                                                                   op0=ALU.mult, op1=ALU.add), reads=[gw_r], writes=[gw_r])
                            kb.op("dve", lambda v: v.tensor_tensor(out=gw[:, 0:127], in0=gw[:, 0:127], in1=gx[:, 0:127], op=ALU.mult),
                                  reads=[gw_r, gx_r], writes=[gw_r])
                            kb.op("act", lambda a: a.activation(out=gw[:, 0:127], in_=gw[:, 0:127], func=AF.Sigmoid, scale=GK), reads=[gw_r], writes=[gw_r])
                            kb.op("dve", lambda v: v.tensor_tensor(out=gh[:, hf, 0:127], in0=gx[:, 0:127], in1=gw[:, 0:127], op=ALU.mult),
                                  reads=[gw_r, gx_r], writes=[gh_r])
                        if kv == 0:
                            def fn(pe):
                                pe.matmul(PS[2][:, 0:127], lhsT=w2d[:, 0, :], rhs=gh[:, 0, 0:127], start=True, stop=False)
                                return pe.matmul(PS[2][:, 0:127], lhsT=w2d[:, 1, :], rhs=gh[:, 1, 0:127], start=False, stop=True)
                            kb.op("pe", fn, reads=[w2_r, gh_r], writes=[PS_r[2]])
                            pr = slice(g * 64, g * 64 + 64)
                            kb.op("act", lambda a: a.activation(out=kcmpTz[pr, g, 0:127], in_=PS[2][pr, 0:127], func=AF.Copy),
                                  reads=[PS_r[2], kc_r], writes=[kc_r])
                        else:
                            def fn(pe):
                                pe.matmul(PS[2][:, 0:64], lhsT=gh[:, 0, :], rhs=wv2[:, 0, :], start=True, stop=False)
                                return pe.matmul(PS[2][:, 0:64], lhsT=gh[:, 1, :], rhs=wv2[:, 1, :], start=False, stop=True)
                            kb.op("pe", fn, reads=[w2_r, gh_r], writes=[PS_r[2]])
                            kb.op("act", lambda a: a.activation(out=vcmp[0:127, g, 0:64], in_=PS[2][0:127, 0:64], func=AF.Copy),
                                  reads=[PS_r[2], vcm_r], writes=[vcm_r])
                kb.barrier()
            if ns_ <= 2:
                kb.barrier()
                return
            selbT = A("nselbT", [128, 2, S], BF16)
            onehot = A("nonehot", [128, S], BF16)
            cmask = A("ncmask", [128, S], BF16)
            selm = A("nselm", [128, 2, NT, 32], F32)
            wmask = A("nwmask", [128, 128], F32)
            PT = [A("nPT%d" % i, [128, 640], BF16) for i in range(2)]
            ex = A("nex", [128, 512], F32)
            PTs = [A("nPTs%d" % i, [128, 512], BF16) for i in range(4)]
            PTw = [A("nPTw%d" % i, [128, 640], BF16) for i in range(4)]
            PTw_r = regs(4)
            exw = [A("nexw%d" % i, [128, 128], F32) for i in range(2)]
            exw_r = regs(2)
            ocsw = [A("nocsw%d" % i, [128, 4, 65], F32) for i in range(2)]
            ocsw_r = regs(2)
            dnw = [A("ndnw%d" % i, [128, 16], F32) for i in range(2)]
            dnw_r = regs(2)
            PTs_r = regs(4)
            ocss = [A("nocss%d" % i, [128, 4, 65], F32) for i in range(2)]
            ocss_r = regs(2)
            dns = [A("ndns%d" % i, [128, 16], F32) for i in range(2)]
            dns_r = regs(2)
            ocs = A("nocs", [128, 4, 97], F32)
            ONSA2 = [A("nONSA%d" % i_, [128, 4, 512], F32) for i_ in range(2)]
            impb2 = [A("nimp%d" % i_, [128, 4, 2, 32], F32) for i_ in range(2)]
            sc = A("nsc", [128, 32], F32)
            cm3 = A("ncm3", [128, 32, 32], F32)
            sb16 = A("nsb16", [128, 32], BF16)
            dn = A("ndn", [128, 16], F32)
            obf = A("nobf", [128, 512], BF16)
            k_r, sel_r, PT_r, ex_r, ocs_r, sc_r, cm3_r, sb16_r, dn_r, obf_r = \
                Reg(), regs(4), regs(2), Reg(), Reg(), Reg(), Reg(), Reg(), Reg(), Reg()
            on_r2, imp_r2 = [regs(4), regs(4)], regs(2)
            kb.op("pool", lambda g: g.memset(selbT[:], 0.0), writes=sel_r)
            kb.op("pool", lambda g: g.memset(onehot[:], 0.0), writes=[k_r])
            kb.dma("pool", onehot[0:32, :], c_onehot.ap(), reads=[k_r], writes=[k_r])
            kb.dma("pool", cmask[:, :], c_cmpmask.ap(), writes=[k_r])
            for m_ in range(2):
                kb.dma("sp", selm[:, m_, :, :], c_selm.ap()[m_], writes=[k_r])
            kb.op("dve", lambda v: v.tensor_scalar(out=wmask[:, :], in0=caus[:, 0, :], scalar1=-1.0, scalar2=1.0, op0=ALU.mult, op1=ALU.add),
                  reads=[c_r], writes=[k_r])

            def bf_math(buf, buf_r, dnv, dnv_r, nq, h, b, tts, first, ONSA, on_r):
                kb.op("dve", lambda v: v.tensor_scalar(out=dnv[:, 0:nq], in0=buf[:, 0:nq, 64], scalar1=1e-30, scalar2=None, op0=ALU.max),
                      reads=[buf_r], writes=[dnv_r])
                kb.op("dve", lambda v: v.reciprocal(out=dnv[:, 0:nq], in_=dnv[:, 0:nq]), reads=[dnv_r], writes=[dnv_r])
                kb.op("dve", lambda v: v.tensor_tensor(out=dnv[:, 8:8 + nq], in0=dnv[:, 0:nq], in1=gsig[:, tts[0]:tts[0] + nq, h * 3 + b], op=ALU.mult),
                      reads=[dnv_r] + gs_r[tts[0]:tts[0] + nq], writes=[dnv_r])
                for qi in range(nq):
                    tl = tts[qi] % 4
                    if first:
                        kb.op("dve", lambda v: v.tensor_scalar(out=ONSA[:, tl, h * 64:(h + 1) * 64], in0=buf[:, qi, 0:64], scalar1=dnv[:, 8 + qi:9 + qi],
                                                               scalar2=None, op0=ALU.mult), reads=[buf_r, dnv_r], writes=[on_r[tl]])
                    else:
                        kb.op("dve", lambda v: v.scalar_tensor_tensor(out=ONSA[:, tl, h * 64:(h + 1) * 64], in0=buf[:, qi, 0:64], scalar=dnv[:, 8 + qi:9 + qi],
                                                                      in1=ONSA[:, tl, h * 64:(h + 1) * 64], op0=ALU.mult, op1=ALU.add),
                              reads=[buf_r, dnv_r, on_r[tl]], writes=[on_r[tl]])

            def branch_finish(acc_ps, acc_r, W, nq, h, b, tts, first, ONSA, on_r):
                kb.op("act", lambda a: a.activation(out=ocs[:, 0:nq, 0:W], in_=acc_ps.rearrange("p (q w) -> p q w", q=nq), func=AF.Copy),
                      reads=[acc_r], writes=[ocs_r, acc_r])
                bf_math(ocs, ocs_r, dn, dn_r, nq, h, b, tts, first, ONSA, on_r)

            def nsa_a(qb):
                qc = slice(qb * 512, (qb + 1) * 512)
                tts = list(range(qb * 4, qb * 4 + 4))
                ONSA, on_r, impb, imp_r = ONSA2[qb % 2], on_r2[qb % 2], impb2[qb % 2], imp_r2[qb % 2]
                for h in range(8):
                    g, i = h // 4, h % 4
                    pi = h % 2
                    kb.op("pe", lambda pe: pe.matmul(PS[pi][:, :], lhsT=kcmpTz[:, g, :], rhs=QT[:, i, qc], start=True, stop=True),
                          reads=[kc_r, QT_r[qb]], writes=[PS_r[pi]])
                    kb.op("act", lambda a: a.activation(out=ex[:, :], in_=PS[pi][:, :], func=AF.Exp), reads=[PS_r[pi]], writes=[ex_r])
                    kb.op("dve", lambda v: v.tensor_tensor(out=PT[pi][:, 0:512], in0=ex[:, :], in1=cmask[:, qc], op=ALU.mult),
                          reads=[ex_r, k_r], writes=[PT_r[pi]])
                    ai = 2 + (h % 2)

                    def fn(pe):
                        inst = None
                        for q in range(4):
                            inst = pe.matmul(PS[ai][:, q * 97:(q + 1) * 97], lhsT=PT[pi][:, q * 128:(q + 1) * 128], rhs=vcmp[:, g, :], start=True, stop=True)
                        return inst
                    kb.op("pe", fn, reads=[PT_r[pi], vcm_r], writes=[PS_r[ai]])
                    branch_finish(PS[ai][:, 0:388], PS_r[ai], 97, 4, h, 0, tts, True, ONSA, on_r)
                    for q in range(4):
                        if i == 0:
                            kb.op("dve", lambda v: v.tensor_scalar(out=impb[:, q, g, :], in0=ocs[:, q, 65:97], scalar1=dn[:, q:q + 1], scalar2=None, op0=ALU.mult),
                                  reads=[ocs_r, dn_r], writes=[imp_r])
                        else:
                            kb.op("dve", lambda v: v.scalar_tensor_tensor(out=impb[:, q, g, :], in0=ocs[:, q, 65:97], scalar=dn[:, q:q + 1], in1=impb[:, q, g, :],
                                                                          op0=ALU.mult, op1=ALU.add), reads=[ocs_r, dn_r, imp_r], writes=[imp_r])
            def nsa_b(qb):
                qc = slice(qb * 512, (qb + 1) * 512)
                tts = list(range(qb * 4, qb * 4 + 4))
                ONSA, on_r, impb, imp_r = ONSA2[qb % 2], on_r2[qb % 2], impb2[qb % 2], imp_r2[qb % 2]
                for q in range(4):
                    tt = tts[q]
                    for g in range(2):
                        kb.op("dve", lambda v: v.tensor_tensor(out=sc[:, :], in0=impb[:, q, g, :], in1=selm[:, 0, tt, :], op=ALU.mult),
                              reads=[imp_r, k_r], writes=[sc_r])
                        kb.op("dve", lambda v: v.tensor_tensor(out=sc[:, :], in0=sc[:, :], in1=selm[:, 1, tt, :], op=ALU.add),
                              reads=[sc_r, k_r], writes=[sc_r])
                        kb.op("dve", lambda v: v.tensor_tensor(out=cm3[:, :, :], in0=bass.AP(sc, 0, [[32, 128], [0, 32], [1, 32]]),
                                                              in1=bass.AP(sc, 0, [[32, 128], [1, 32], [0, 32]]), op=ALU.is_gt),
                              reads=[sc_r], writes=[cm3_r])
                        kb.op("dve", lambda v: v.reduce_sum(out=sc[:, :], in_=cm3[:, :, :], axis=AX.X), reads=[cm3_r, sc_r], writes=[sc_r])
                        kb.op("dve", lambda v: v.tensor_scalar(out=sb16[:, :], in0=sc[:, :], scalar1=15.5, scalar2=-30000.0, op0=ALU.is_gt, op1=ALU.mult),
                              reads=[sc_r], writes=[sb16_r])
                        kb.op("pe", lambda pe: pe.transpose(out=PB[0][0:32, 0:128], in_=sb16[:, :], identity=ident[:, :]),
                              reads=[sb16_r, c_r], writes=[PB_r[0]])
                        kb.op("act", lambda a: a.activation(out=selbT[0:32, g, tt * 128:(tt + 1) * 128], in_=PB[0][0:32, 0:128], func=AF.Copy),
                              reads=[PB_r[0]], writes=[sel_r[qb]])
            def nsa_c(qb):
                qc = slice(qb * 512, (qb + 1) * 512)
                tts = list(range(qb * 4, qb * 4 + 4))
                ONSA, on_r, impb, imp_r = ONSA2[qb % 2], on_r2[qb % 2], impb2[qb % 2], imp_r2[qb % 2]
                for h in range(8):
                    g, i = h // 4, h % 4
                    nkt = 4 * qb + 4
                    for kt in range(nkt):
                        kc_ = slice(kt * 128, (kt + 1) * 128)
                        pi = kt % 2

                        def fn(pe):
                            pe.matmul(PS[pi][:, :], lhsT=KTz[:, 1, g, kc_], rhs=QT[:, i, qc], start=True, stop=False)
                            return pe.matmul(PS[pi][:, :], lhsT=onehot[:, kc_], rhs=selbT[:, g, qc], start=False, stop=True)
                        kb.op("pe", fn, reads=[KT_r[kt // 4], QT_r[qb], k_r, sel_r[qb]], writes=[PS_r[pi]])
                        pt = kt % 4
                        kb.op("act", lambda a: a.activation(out=PTs[pt][:, 0:512], in_=PS[pi][:, :], func=AF.Exp), reads=[PS_r[pi]], writes=[PTs_r[pt], PS_r[pi]])
                        if kt >= 4 * qb:
                            ql = kt - 4 * qb
                            kb.op("dve", lambda v: v.tensor_tensor(out=PTs[pt][:, ql * 128:(ql + 1) * 128], in0=PTs[pt][:, ql * 128:(ql + 1) * 128],
                                                                  in1=caus[:, 0, :], op=ALU.mult), reads=[PTs_r[pt], c_r], writes=[PTs_r[pt]])
                        for q in range(4):
                            qt = 4 * qb + q
                            if qt < kt:
                                continue
                            kb.op("pe", lambda pe: pe.matmul(PS[2 + q][:, 0:65], lhsT=PTs[pt][:, q * 128:(q + 1) * 128], rhs=VS[:, kt, g, :],
                                                             start=(kt == 0), stop=(kt == qt)), reads=[PTs_r[pt], VS_r[kt]], writes=[PS_r[2 + q]])
                    hb = h % 2
                    for q in range(4):
                        kb.op("act", lambda a: a.activation(out=ocss[hb][:, q, :], in_=PS[2 + q][:, 0:65], func=AF.Copy),
                              reads=[PS_r[2 + q]], writes=[ocss_r[hb], PS_r[2 + q]])
                    bf_math(ocss[hb], ocss_r[hb], dns[hb], dns_r[hb], 4, h, 1, tts, False, ONSA, on_r)
            def nsa_d(qb):
                qc = slice(qb * 512, (qb + 1) * 512)
                tts = list(range(qb * 4, qb * 4 + 4))
                ONSA, on_r, impb, imp_r = ONSA2[qb % 2], on_r2[qb % 2], impb2[qb % 2], imp_r2[qb % 2]
                for h in range(8):
                    g, i = h // 4, h % 4
                    for q in range(4):
                        qt = 4 * qb + q
                        qcs = slice(qt * 128, (qt + 1) * 128)
                        kts = [kt for kt in range(qt - 4, qt + 1) if kt >= 0]
                        pi = q % 2
                        pw = q % 4
                        main = [kt for kt in kts if kt >= qt - 3]

                        def fn(pe):
                            inst = None
                            for n_, kt in enumerate(main):
                                inst = pe.matmul(PS[pi][:, n_ * 128:(n_ + 1) * 128], lhsT=KTz[:, 2, g, kt * 128:(kt + 1) * 128], rhs=QT[:, i, qcs],
                                                 start=True, stop=True)
                            return inst
                        kb.op("pe", fn, reads=KT_r + [QT_r[qb]], writes=[PS_r[pi]])
                        nm = len(main)
                        kb.op("act", lambda a: a.activation(out=PTw[pw][:, 0:nm * 128], in_=PS[pi][:, 0:nm * 128], func=AF.Exp), reads=[PS_r[pi]], writes=[PTw_r[pw]])
                        kb.op("dve", lambda v: v.tensor_tensor(out=PTw[pw][:, (nm - 1) * 128:nm * 128], in0=PTw[pw][:, (nm - 1) * 128:nm * 128],
                                                              in1=caus[:, 0, :], op=ALU.mult), reads=[PTw_r[pw], c_r], writes=[PTw_r[pw]])
                        tail = (qt - 4 >= 0)
                        if tail:
                            kt = qt - 4
                            kb.op("pe", lambda pe: pe.matmul(PS[2 + pi][:, 0:128], lhsT=KTz[:, 2, g, kt * 128:(kt + 1) * 128], rhs=QT[:, i, qcs],
                                                             start=True, stop=True), reads=KT_r + [QT_r[qb]], writes=[PS_r[2 + pi]])
                            kb.op("act", lambda a: a.activation(out=exw[pi][:, 0:128], in_=PS[2 + pi][:, 0:128], func=AF.Exp), reads=[PS_r[2 + pi]], writes=[exw_r[pi]])
                            kb.op("dve", lambda v: v.tensor_tensor(out=PTw[pw][:, 512:640], in0=exw[pi][:, 0:128], in1=wmask[:, :], op=ALU.mult),
                                  reads=[exw_r[pi], k_r, PTw_r[pw]], writes=[PTw_r[pw]])

                        def fn(pe):
                            inst = None
                            seq_ = [(n_, kt) for n_, kt in enumerate(main)] + ([(4, qt - 4)] if tail else [])
                            for idx, (n_, kt) in enumerate(seq_):
                                inst = pe.matmul(PS[4][:, q * 65:(q + 1) * 65], lhsT=PTw[pw][:, n_ * 128:(n_ + 1) * 128], rhs=VW[:, kt, g, :],
                                                 start=(idx == 0), stop=(idx == len(seq_) - 1))
                            return inst
                        kb.op("pe", fn, reads=[PTw_r[pw]] + VW_r[max(0, qt - 4):qt + 1], writes=[PS_r[4]])
                    hb = h % 2
                    kb.op("act", lambda a: a.activation(out=ocsw[hb][:, :, :], in_=PS[4][:, 0:260].rearrange("p (q w) -> p q w", q=4), func=AF.Copy),
                          reads=[PS_r[4]], writes=[ocsw_r[hb], PS_r[4]])
                    bf_math(ocsw[hb], ocsw_r[hb], dnw[hb], dnw_r[hb], 4, h, 2, tts, False, ONSA, on_r)
            def nsa_e(qb):
                qc = slice(qb * 512, (qb + 1) * 512)
                tts = list(range(qb * 4, qb * 4 + 4))
                ONSA, on_r, impb, imp_r = ONSA2[qb % 2], on_r2[qb % 2], impb2[qb % 2], imp_r2[qb % 2]
                for q in range(4):
                    tt = tts[q]
                    kb.op("act", lambda a: a.activation(out=obf[:, :], in_=ONSA[:, q, :], func=AF.Copy), reads=[on_r[q]], writes=[obf_r])
                    for half in range(2):
                        out_to_OT(obf[:, half * 256:(half + 1) * 256], obf_r, 128, OT, OT_r, 4 + 2 * half, tt * 128)
            for qb in range(4):
                if qb == 0:
                    nsa_a(0)
                if qb + 1 < 4:
                    nsa_a(qb + 1)
                if ns_ > 3:
                    nsa_b(qb)
                if ns_ > 5:
                    nsa_d(qb)
                if ns_ > 4:
                    nsa_c(qb)
                if ns_ > 6:
                    nsa_e(qb)
            kb.barrier()

    for sq in range(NSEQ):
        for l in range(NLAY):
            if l == 0:
                with sb("xstage", [128, 2, D], F32) as xst:
                    xst_r = regs(2)
                    for tt in range(NT):
                        i = tt % 2
                        kb.dma("sp", xst[:, i, :], x_d.ap()[sq, tt * 128:(tt + 1) * 128, :], writes=[xst_r[i]])
                        make_xT(xst[:, i, :], xst_r[i], tt)
                    kb.barrier()
            if cfg.get("stop") == "xt":
                continue
            res_src = (lambda tt: x_d.ap()[sq, tt * 128:(tt + 1) * 128, :]) if l == 0 else \
                      (lambda tt: xres[1].ap()[tt * 128:(tt + 1) * 128, :])
            res_regs = None if l == 0 else xres_r[1]

            with sb("OT", [128, 8, S], BF16) as OT:
                OT_r = regs(NT)
                if inject_O:
                    for k in range(8):
                        kb.dma("pool", OT[:, k, :], dbg_OT.ap()[:, k, :], writes=OT_r)
                if "gla" in mixers:
                    gla_phase(sq, l, OT, OT_r)
                if "rwkv" in mixers:
                    rwkv_phase(sq, l, OT, OT_r)
                if "nsa" in mixers:
                    nsa_phase(sq, l, OT, OT_r)
                if "OT" in dump_d:
                    for k in range(8):
                        with sb("otd", [128, S], F32) as otd:
                            r_ = Reg()
                            kb.op("act", lambda a: a.activation(out=otd[:], in_=OT[:, k, :], func=AF.Copy),
                                  reads=OT_r, writes=[r_])
                            dump("OT", otd[:], r_, idx=k)
                            kb.barrier()
                if cfg.get("stop") == "outproj0":
                    continue
                with sb("Wo", [128, 8, D], BF16) as Wo, \
                        sb("ln1", [128, 2, D], F32) as ln1, \
                        sb("xs1", [128, 2, D], F32) as xs1, \
                        sb("st1", [128, 2, 32], F32) as st1:
                    Wo_r = Reg()
                    ln_r = Reg()
                    xs_r = regs(2)
                    st_r = regs(2)
                    load_w(Wo, w_out16.ap()[l], D, Wo_r)
                    kb.dma("sp", ln1[:, 0, :], bcast_rows(rowtab, l * 5120 + 1024, D), writes=[ln_r])
                    kb.dma("sp", ln1[:, 1, :], bcast_rows(rowtab, l * 5120 + 2048, D), writes=[ln_r])
                    for tt in range(NT):
                        i = tt % 2
                        kb.dma("sp", xs1[:, i, :], res_src(tt), reads=([res_regs[tt]] if res_regs else []), writes=[xs_r[i]])
                        for hf in range(2):
                            def fn(pe, hf=hf):
                                inst = None
                                for k in range(8):
                                    inst = pe.matmul(PS[hf][:, :], lhsT=OT[:, k, tt * 128:(tt + 1) * 128],
                                                     rhs=Wo[:, k, hf * 512:(hf + 1) * 512], start=(k == 0), stop=(k == 7))
                                return inst
                            kb.op("pe", fn, reads=[OT_r[tt], Wo_r], writes=[PS_r[hf]])
                            kb.op("dve", lambda v, hf=hf: v.scalar_tensor_tensor(
                                out=xs1[:, i, hf * 512:(hf + 1) * 512], in0=xs1[:, i, hf * 512:(hf + 1) * 512], scalar=ALPHA,
                                in1=PS[hf][:, :], op0=ALU.mult, op1=ALU.add), reads=[PS_r[hf], xs_r[i]], writes=[xs_r[i]])
                        layer_norm(xs1[:, i, :], xs_r[i], ln1[:, 0, :], ln1[:, 1, :], ln_r, st1[:, i, :], st_r[i])
                        kb.dma("sp", xres[0].ap()[tt * 128:(tt + 1) * 128, :], xs1[:, i, :], reads=[xs_r[i]], writes=[xres_r[0][tt]])
                        make_xT(xs1[:, i, :], xs_r[i], tt)
                        if "x1" in dump_d and sq == 0 and l == cfg.get("dump_layer", 0):
                            dump("x1", xs1[:, i, :], xs_r[i], idx=tt)
                    kb.barrier()
            if cfg.get("stop") in ("outproj", "outproj0"):
                continue
            with sb("aT", [128, NFC, 1024], BF16) as aT, \
                    sb("Wd", [128, NFC, D], BF16) as Wd, \
                    sb("Wgu", [128, 2, 2, 8, 512], BF16) as Wgu, \
                    sb("sg", [128, 2, 512], F32) as sg, \
                    sb("ln2", [128, 2, D], F32) as ln2, \
                    sb("xs2", [128, 2, D], F32) as xs2, \
                    sb("st2", [128, 2, 32], F32) as st2:
                Wd_r = Reg()
                ln_r = Reg()
                aT_r = regs(NFC)
                Wgu_r = regs(2)
                sg_r = regs(2)
                xs_r = regs(2)
                st_r = regs(2)
                kb.dma("sp", ln2[:, 0, :], bcast_rows(rowtab, l * 5120 + 3072, D), writes=[ln_r])
                kb.dma("sp", ln2[:, 1, :], bcast_rows(rowtab, l * 5120 + 4096, D), writes=[ln_r])
                for k in range(NFC):
                    for c0 in (0, 512):
                        kb.dma("sp", Wd[:, k, c0:c0 + 512], w_down16.ap()[l, k * 128:(k + 1) * 128, c0:c0 + 512], reads=[w16_r], writes=[Wd_r])
                last = (l == NLAY - 1)
                for mt in range(2):
                    tok0 = mt * 1024
                    for hc in range(NFC):
                        cg, ci = hc // 4, hc % 4
                        wi = cg % 2
                        if ci == 0:
                            ncol = min(512, FF - cg * 512)
                            for gu, wsrc in ((0, w_gate16), (1, w_up16)):
                                for k in range(8):
                                    kb.dma("sp", Wgu[:, wi, gu, k, 0:ncol],
                                           wsrc.ap()[l, k * 128:(k + 1) * 128, cg * 512:cg * 512 + ncol],
                                           reads=[w16_r], writes=[Wgu_r[wi]])
                        for blk in range(2):
                            t0 = tok0 + blk * 512
                            pg, pu = (0, 1) if blk == 0 else (2, 3)
                            for gu, pi in ((0, pg), (1, pu)):
                                def fn(pe, gu=gu, pi=pi):
                                    inst = None
                                    for k in range(8):
                                        inst = pe.matmul(PS[pi][:, :], lhsT=Wgu[:, wi, gu, k, ci * 128:(ci + 1) * 128],
                                                         rhs=XT[:, k, t0:t0 + 512], start=(k == 0), stop=(k == 7))
                                    return inst
                                kb.op("pe", fn, reads=[Wgu_r[wi]] + XT_r[t0 // 128:t0 // 128 + 4], writes=[PS_r[pi]])
                            kb.op("act", lambda a: a.activation(out=sg[:, blk, :], in_=PS[pg][:, :], func=AF.Silu),
                                  reads=[PS_r[pg]], writes=[sg_r[blk]])
                            kb.op("dve", lambda v: v.tensor_tensor(out=aT[:, hc, blk * 512:(blk + 1) * 512], in0=sg[:, blk, :],
                                                                  in1=PS[pu][:, :], op=ALU.mult),
                                  reads=[sg_r[blk], PS_r[pu]], writes=[aT_r[hc]])
                    for t8 in range(8):
                        tt = mt * 8 + t8
                        i = tt % 2
                        kb.dma("sp", xs2[:, i, :], xres[0].ap()[tt * 128:(tt + 1) * 128, :], reads=[xres_r[0][tt]], writes=[xs_r[i]])
                        for hf in range(2):
                            pi = 4 + hf

                            def fn(pe, hf=hf, pi=pi):
                                inst = None
                                for k in range(NFC):
                                    inst = pe.matmul(PS[pi][:, :], lhsT=aT[:, k, t8 * 128:(t8 + 1) * 128],
                                                     rhs=Wd[:, k, hf * 512:(hf + 1) * 512], start=(k == 0), stop=(k == NFC - 1))
                                return inst
                            kb.op("pe", fn, reads=aT_r + [Wd_r], writes=[PS_r[pi]])
                            kb.op("dve", lambda v, hf=hf, pi=pi: v.scalar_tensor_tensor(
                                out=xs2[:, i, hf * 512:(hf + 1) * 512], in0=xs2[:, i, hf * 512:(hf + 1) * 512], scalar=ALPHA,
                                in1=PS[pi][:, :], op0=ALU.mult, op1=ALU.add), reads=[PS_r[pi], xs_r[i]], writes=[xs_r[i]])
                        layer_norm(xs2[:, i, :], xs_r[i], ln2[:, 0, :], ln2[:, 1, :], ln_r, st2[:, i, :], st_r[i])
                        if last:
                            kb.dma("sp", out_d.ap()[sq, tt * 128:(tt + 1) * 128, :], xs2[:, i, :], reads=[xs_r[i]])
                        else:
                            kb.dma("sp", xres[1].ap()[tt * 128:(tt + 1) * 128, :], xs2[:, i, :], reads=[xs_r[i]], writes=[xres_r[1][tt]])
                            make_xT(xs2[:, i, :], xs_r[i], tt)
                        if "x2" in dump_d and sq == 0 and l == cfg.get("dump_layer", 0):
                            dump("x2", xs2[:, i, :], xs_r[i], idx=tt)
                    if not last:
                        pass
                kb.barrier()
    kb.finish()
    return kb


def host_consts():
    c = {}
    c["c_ident"] = np.eye(128, dtype=np.float32)
    s = np.arange(128)
    c["c_caus"] = (s[:, None] <= s[None, :]).astype(np.float32)
    half = 32
    inv = (10000.0 ** (-np.arange(half, dtype=np.float32) / half)).astype(np.float32)
    ang = (np.arange(S, dtype=np.float32)[:, None] * inv[None, :]).astype(np.float32)
    cos = np.cos(ang).astype(np.float32).T
    sin = np.sin(ang).astype(np.float32).T
    cosT = np.concatenate([cos, cos, cos, cos], 0)
    sinT = np.concatenate([-sin, sin, -sin, sin], 0)
    c["c_rope"] = np.stack([cosT, sinT]).astype(np.float32)
    cc = np.arange(128)
    t = np.arange(S)
    c["c_cmpmask"] = ((16 * cc[:, None] + 31 <= t[None, :]) & (cc[:, None] < 127)).astype(np.float32)
    j = np.arange(32)
    c["c_onehot"] = ((t[None, :] // 64) == j[:, None]).astype(np.float32)
    cur = t // 64
    forced = (j[None, :] == 0) | (j[None, :] == cur[:, None]) | (j[None, :] == cur[:, None] - 1)
    future = j[None, :] > cur[:, None]
    m1 = (~forced & ~future).astype(np.float32)
    m2 = np.where(forced, 1e9, np.where(future, -1e9, 0.0)).astype(np.float32)
    selm = np.stack([m1, m2])
    c["c_selm"] = np.ascontiguousarray(selm.reshape(2, NT, 128, 32).transpose(0, 2, 1, 3))
    c0 = np.arange(127) * 16
    s0 = np.arange(32) * 64
    lo = np.maximum(c0[:, None], s0[None, :])
    hi = np.minimum(c0[:, None] + 32, s0[None, :] + 64)
    ov = np.zeros((128, 32), np.float32)
    ov[:127] = np.maximum(hi - lo, 0) / 16
    c["c_ovl"] = ov
    blk = np.zeros((128, 128), np.float32)
    blk[:64, :64] = 1
    blk[64:, 64:] = 1
    c["c_blk"] = blk
    hs = np.zeros((128, 2), np.float32)
    hs[:64, 0] = 1
    hs[64:, 1] = 1
    c["c_hsel"] = hs
    i64 = np.arange(64)
    lo_strict = (i64[None, :] < i64[:, None]).astype(np.float32)
    up_strict = (i64[:, None] < i64[None, :]).astype(np.float32)
    up_incl = (i64[:, None] <= i64[None, :]).astype(np.float32)
    def bd(m):
        o = np.zeros((128, 128), np.float32)
        o[:64, :64] = m
        o[64:, 64:] = m
        return o
    c["c_rwmask2"] = np.ascontiguousarray(np.stack([bd(lo_strict), bd(up_strict), bd(up_incl)], axis=1))
    c["c_istk"] = np.concatenate([np.eye(64, dtype=np.float32), np.eye(64, dtype=np.float32)], axis=0)
    return c


def host_layout(inp):
    d = {}
    f = lambda a: np.ascontiguousarray(np.asarray(a, dtype=np.float32))
    for k in ("w_in", "w_in_vres", "gla_w_a2", "rwkv_w2", "rwkv_a2", "rwkv_v2", "rwkv_g2", "nsa_wk1", "nsa_wk2",
              "nsa_wv1", "nsa_wv2", "w_out", "ffn_w_gate", "ffn_w_up", "ffn_w_down"):
        d[k] = f(inp[k])
    base = 2096
    sw = lambda b: list(range(b + 32, b + 64)) + list(range(b, b + 32))
    pl = lambda b: list(range(b, b + 64))
    cols = []
    for i in range(4):
        cols += pl(base + i * 64) + pl(base + (4 + i) * 64)
    for i in range(4):
        cols += sw(base + i * 64) + sw(base + (4 + i) * 64)
    for c0 in (512, 768, 1024):
        cols += pl(base + c0) + pl(base + c0 + 64)
        cols += sw(base + c0) + sw(base + c0 + 64)
    cols += list(range(base + 640, base + 768)) + list(range(base + 896, base + 1024)) + list(range(base + 1152, base + 1280))
    cols += list(range(base + 1280, base + 1304))
    assert len(cols) == 2200
    d["w_nsa"] = f(np.asarray(inp["w_in"])[:, :, cols])
    d["nsa_posT"] = f(np.stack([np.asarray(inp["nsa_pos_k"]).transpose(0, 2, 1),
                                np.asarray(inp["nsa_pos_v"]).transpose(0, 2, 1)], axis=1))
    pt = np.zeros((L, 128, 32), np.float32)
    mu = np.asarray(inp["rwkv_mu"])
    rwt = [(0, 128), (128, 128), (256, 128), (384, 128), (512, 128), (640, 128), (768, 64), (832, 64), (896, 128), (1024, 32)]
    for l in range(L):
        pt[l, :, 0:2] = np.asarray(inp["gla_b_a"])[l].reshape(2, 128).T
        for i, (c0, n) in enumerate(rwt):
            pt[l, :n, 2 + i] = mu[l, c0:c0 + n]
        if l >= 1:
            pt[l, :32, 12] = np.asarray(inp["rwkv_mu_vres"])[l - 1]
            pt[l, :, 17:19] = np.asarray(inp["rwkv_v0"])[l - 1].reshape(2, 128).T
        pt[l, :, 13:15] = np.asarray(inp["rwkv_w0"])[l].reshape(2, 128).T
        pt[l, :, 15:17] = np.asarray(inp["rwkv_a0"])[l].reshape(2, 128).T
        pt[l, :, 19:21] = np.asarray(inp["rwkv_k_k"])[l].reshape(2, 128).T
        pt[l, :, 21:23] = np.asarray(inp["rwkv_k_a"])[l].reshape(2, 128).T
        pt[l, :, 23:25] = np.asarray(inp["rwkv_r_k"])[l].reshape(2, 128).T
    d["ptab"] = pt
    rt = np.zeros((L, 5120), np.float32)
    for l in range(L):
        rt[l, 0:256] = np.asarray(inp["gla_ln_w"])[l]
        rt[l, 256:512] = np.asarray(inp["gla_ln_b"])[l]
        rt[l, 512:768] = np.asarray(inp["rwkv_ln_w"])[l]
        rt[l, 768:1024] = np.asarray(inp["rwkv_ln_b"])[l]
        rt[l, 1024:2048] = np.asarray(inp["ln1_w"])[l]
        rt[l, 2048:3072] = np.asarray(inp["ln1_b"])[l]
        rt[l, 3072:4096] = np.asarray(inp["ln2_w"])[l]
        rt[l, 4096:5120] = np.asarray(inp["ln2_b"])[l]
    d["rowtab"] = rt
    return d


_CACHE = {}


def kernel(**inputs):
    cfg = {}
    if "full" not in _CACHE:
        _CACHE["full"] = build(cfg)
    kb = _CACHE["full"]
    shared = host_layout(inputs)
    shared.update(host_consts())
    x = np.ascontiguousarray(np.asarray(inputs["x"], dtype=np.float32))
    in_maps = []
    for c in range(8):
        m = dict(shared)
        m["x"] = x[2 * c:2 * c + 2]
        in_maps.append(m)
    res = run_bass_kernel_spmd(kb.nc, in_maps, core_ids=list(range(8)))
    return np.concatenate([r["out"] for r in res.results], axis=0).astype(np.float32)
```

```python
import math
from contextlib import ExitStack
import numpy as np
import concourse.bass as bass
import concourse.mybir as mybir
from concourse.bass_utils import run_bass_kernel_spmd

F32 = mybir.dt.float32
BF16 = mybir.dt.bfloat16
AF = mybir.ActivationFunctionType
ALU = mybir.AluOpType
AX = mybir.AxisListType

S = 2048
D = 1024
NT = S // 128
L = 2
FF = 2816
NFC = FF // 128
ALPHA = float((2 * L) ** 0.25)
LN_EPS = 1e-5
RW_EPS = 64e-5
NDS = 6


class Reg:
    __slots__ = ("lw", "rd")

    def __init__(self):
        self.lw = None
        self.rd = {}


def regs(n):
    return [Reg() for _ in range(n)]


class _Rec:
    def __init__(self):
        self.calls = []

    def __getattr__(self, name):
        def f(*a, **kw):
            self.calls.append((name, a, kw))
            return self
        return f


class _Node:
    __slots__ = ("e", "kind", "calls", "reads", "writes", "cost", "deps")

    def __init__(self, e, kind, calls, reads, writes, cost):
        self.e, self.kind, self.calls, self.reads, self.writes, self.cost = e, kind, calls, reads, writes, cost
        self.deps = ()


def _free_elems(ap):
    try:
        n = 1
        for s_ in list(ap.shape)[1:]:
            n *= int(s_)
        return n
    except Exception:
        return 256


def _ap_bytes(ap):
    try:
        n = 1
        for s_ in list(ap.shape):
            n *= int(s_)
        return n * 4
    except Exception:
        return 65536


def _est_cost(e, calls):
    t = 0.0
    for name, a, kw in calls:
        out = kw.get("out", a[0] if a else None)
        n = _free_elems(out) if out is not None else 256
        if e == "pe":
            mul = 4.0 if (name == "matmul" and getattr(kw.get("lhsT"), "dtype", None) == F32) else 1.0
            t += mul * max(n, 64) / 1.6 + 25.0
        elif e == "act":
            t += n / 1.0 + 220.0
        elif e == "dve":
            t += n / 0.9 + 80.0
        else:
            t += n * 2.0 + 300.0
    return t


class KB:
    def __init__(self, sched=True, W=32, LAT=120.0):
        nc = bass.Bass("TRN2", target_bir_lowering=False)
        self.nc = nc
        self.E = {"pe": nc.tensor, "act": nc.scalar, "dve": nc.vector, "pool": nc.gpsimd, "sp": nc.sync}
        self.sems = {}
        self.cnt = {}
        for e in ("pe", "act", "dve", "pool"):
            self.sems[e] = nc.alloc_semaphore("s_" + e)
            self.cnt[e] = 0
        self.dq = {}
        for q, nds in (("sp", 12), ("pool", NDS), ("act", 6)):
            keys = []
            for i in range(nds):
                k = "d_%s%d" % (q, i)
                self.sems[k] = nc.alloc_semaphore(k)
                self.cnt[k] = 0
                keys.append(k)
            self.dq[q] = [keys, 0]
        self.seen = {e: {} for e in self.E}
        self.nops = 0
        self.sched = sched
        self.W = W
        self.LAT = LAT
        self.pending = []

    def _waits(self, e, reads, writes, extra=()):
        need = {}

        def add(rec):
            if rec is None:
                return
            k, c = rec
            if need.get(k, 0) < c:
                need[k] = c

        for r in reads:
            add(r.lw)
        for w in writes:
            add(w.lw)
            for k, c in w.rd.items():
                add((k, c))
        for rec in extra:
            add(rec)
        eng = self.E[e]
        seen = self.seen[e]
        for k, c in need.items():
            if e == "pe" and k == "pe":
                continue
            if seen.get(k, 0) >= c:
                continue
            eng.wait_ge(self.sems[k], c)
            seen[k] = c

    def _mark(self, rec, reads, writes):
        k, c = rec
        for r in reads:
            if r.rd.get(k, 0) < c:
                r.rd[k] = c
        for w in writes:
            w.lw = rec
            w.rd = {}

    def op(self, e, fn, reads=(), writes=()):
        rec = _Rec()
        fn(rec)
        node = _Node(e, "op", rec.calls, list(reads), list(writes), _est_cost(e, rec.calls))
        if self.sched:
            self.pending.append(node)
        else:
            self._emit(node)

    def dma(self, q, out, in_, reads=(), writes=(), **kw):
        node = _Node(q, "dma", (out, in_, kw), list(reads), list(writes), 2000.0 + _ap_bytes(out) / 100.0)
        if self.sched:
            self.pending.append(node)
        else:
            self._emit(node)

    def _emit(self, n):
        e = n.e
        if n.kind == "op":
            self._waits(e, n.reads, n.writes)
            eng = self.E[e]
            inst = None
            for name, a, kw in n.calls:
                inst = getattr(eng, name)(*a, **kw)
            self.cnt[e] += 1
            inst.then_inc(self.sems[e], 1)
            self._mark((e, self.cnt[e]), n.reads, n.writes)
        else:
            out, in_, kw = n.calls
            keys, i = self.dq[e]
            k = keys[i]
            self.dq[e][1] = (i + 1) % len(keys)
            extra = [(k, self.cnt[k])] if self.cnt[k] else []
            self._waits(e, n.reads, n.writes, extra)
            self.E[e].dma_start(out=out, in_=in_, **kw).then_inc(self.sems[k], 16)
            self.cnt[k] += 16
            self._mark((k, self.cnt[k]), n.reads, n.writes)
        self.nops += 1

    def flush(self):
        nodes = self.pending
        self.pending = []
        if not nodes:
            return
        lastw, readers = {}, {}
        for i, n in enumerate(nodes):
            d = set()
            for r in n.reads:
                if id(r) in lastw:
                    d.add(lastw[id(r)])
            for w in n.writes:
                if id(w) in lastw:
                    d.add(lastw[id(w)])
                d.update(readers.get(id(w), ()))
            d.discard(i)
            n.deps = d
            for r in n.reads:
                readers.setdefault(id(r), []).append(i)
            for w in n.writes:
                lastw[id(w)] = i
                readers[id(w)] = []
        queues = {}
        for i, n in enumerate(nodes):
            queues.setdefault(n.e, []).append(i)
        fin = [None] * len(nodes)
        efree = {e: 0.0 for e in queues}
        W, LAT = self.W, self.LAT
        remaining = len(nodes)
        while remaining:
            best = None
            for e, q in queues.items():
                for i in q[:W]:
                    n = nodes[i]
                    ready = 0.0
                    ok = True
                    for d in n.deps:
                        f = fin[d]
                        if f is None:
                            ok = False
                            break
                        if f + LAT > ready:
                            ready = f + LAT
                    if not ok:
                        continue
                    start = ready if ready > efree[e] else efree[e]
                    key = (start, i)
                    if best is None or key < best[0]:
                        best = (key, e, i)
            assert best is not None
            (start, _), e, i = best
            n = nodes[i]
            fin[i] = start + n.cost
            efree[e] = (start + 60.0) if n.kind == "dma" else fin[i]
            queues[e].remove(i)
            self._emit(n)
            remaining -= 1

    def barrier(self):
        self.flush()
        for e in self.E:
            for k, c in self.cnt.items():
                if c == 0 or (e == "pe" and k == "pe"):
                    continue
                if self.seen[e].get(k, 0) >= c:
                    continue
                self.E[e].wait_ge(self.sems[k], c)
                self.seen[e][k] = c

    def finish(self):
        self.barrier()


def bcast_rows(t, off, n, parts=128):
    return bass.AP(t, off, [[0, parts], [1, n]])


def build(cfg):
    kb = KB(sched=cfg.get("sched", True), W=cfg.get("W", 128), LAT=cfg.get("LAT", 1000.0))
    nc = kb.nc
    NSEQ = cfg.get("nseq", 2)
    NLAY = cfg.get("nlay", 2)
    mixers = cfg.get("mixers", ("gla", "rwkv", "nsa"))
    dumps = cfg.get("dumps", ())
    inject_O = cfg.get("inject_O", False)

    def dram_in(name, shape):
        return nc.dram_tensor(name, list(shape), F32, kind="ExternalInput")

    x_d = dram_in("x", [2, S, D])
    w_in = dram_in("w_in", [L, D, 3400])
    w_vres = dram_in("w_in_vres", [1, D, 32])
    w_nsa = dram_in("w_nsa", [L, D, 2200])
    gla_w_a2 = dram_in("gla_w_a2", [L, 16, 256])
    rwkv_w2 = dram_in("rwkv_w2", [L, 64, 256])
    rwkv_a2 = dram_in("rwkv_a2", [L, 64, 256])
    rwkv_v2 = dram_in("rwkv_v2", [1, 32, 256])
    rwkv_g2 = dram_in("rwkv_g2", [L, 160, 256])
    nsa_wk1 = dram_in("nsa_wk1", [L, 2048, 256])
    nsa_wk2 = dram_in("nsa_wk2", [L, 256, 64])
    nsa_wv1 = dram_in("nsa_wv1", [L, 2048, 256])
    nsa_wv2 = dram_in("nsa_wv2", [L, 256, 64])
    nsa_posT = dram_in("nsa_posT", [L, 2, 64, 32])
    w_out = dram_in("w_out", [L, D, D])
    w_gate = dram_in("ffn_w_gate", [L, D, FF])
    w_up = dram_in("ffn_w_up", [L, D, FF])
    w_down = dram_in("ffn_w_down", [L, FF, D])
    ptab = dram_in("ptab", [L, 128, 32])
    rowtab = dram_in("rowtab", [L, 5120])
    c_ident = dram_in("c_ident", [128, 128])
    c_caus = dram_in("c_caus", [128, 128])
    c_rope = dram_in("c_rope", [2, 128, S])
    c_cmpmask = dram_in("c_cmpmask", [128, S])
    c_onehot = dram_in("c_onehot", [32, S])
    c_selm = dram_in("c_selm", [2, 128, NT, 32])
    c_ovl = dram_in("c_ovl", [128, 32])
    c_blk = dram_in("c_blk", [128, 128])
    c_hsel = dram_in("c_hsel", [128, 2])
    c_rwmask2 = dram_in("c_rwmask2", [128, 3, 128])
    c_istk = dram_in("c_istk", [128, 64])
    if inject_O:
        dbg_OT = dram_in("dbg_OT", [128, 8, S])
    out_d = nc.dram_tensor("out", [2, S, D], F32, kind="ExternalOutput")
    xres = [nc.dram_tensor("xres%d" % i, [S, D], F32, kind="Internal") for i in range(2)]
    xres_r = [regs(NT) for _ in range(2)]
    vfirst_d = nc.dram_tensor("vfirst", [2, 128, S], F32, kind="Internal")
    vfirst_r = regs(2)
    dump_d = {}
    for name, shape in dumps:
        dump_d[name] = nc.dram_tensor("dump_" + name, list(shape), F32, kind="ExternalOutput")

    XT = nc.alloc_sbuf_tensor("XT", [128, 8, S], BF16)
    XT_r = regs(NT)
    ident = nc.alloc_sbuf_tensor("ident", [128, 128], BF16)
    identf = nc.alloc_sbuf_tensor("identf", [128, 128], F32)
    caus = nc.alloc_sbuf_tensor("caus", [128, 4, 128], F32)
    ptb = nc.alloc_sbuf_tensor("ptb", [128, L, 32], F32)
    ptn = nc.alloc_sbuf_tensor("ptn", [128, L, 32], F32)
    c_r = Reg()
    for l in range(L):
        kb.dma("sp", ptb[:, l, :], ptab.ap()[l], writes=[c_r])
    kb.dma("pool", ident[:], c_ident.ap(), writes=[c_r])
    kb.dma("sp", identf[:], c_ident.ap(), writes=[c_r])
    for i in range(4):
        kb.dma("sp", caus[:, i, :], c_caus.ap(), writes=[c_r])

    def dram16(name, shape):
        return nc.dram_tensor(name, list(shape), BF16, kind="Internal")

    conv_list = [(w_in, [L, D, 3400]), (w_nsa, [L, D, 2200]), (w_out, [L, D, D]), (w_gate, [L, D, FF]), (w_up, [L, D, FF]),
                 (w_down, [L, FF, D]), (nsa_wk1, [L, 2048, 256]), (nsa_wv1, [L, 2048, 256])]
    w16 = {}
    w16_r = Reg()
    CH = 2816
    with nc.sbuf_tensor("cv_f", [128, 3, CH], F32) as cvf, nc.sbuf_tensor("cv_b", [128, 3, CH], BF16) as cvb:
        cvf_r, cvb_r = regs(3), regs(3)
        job = 0
        for src, shape in conv_list:
            dst = dram16(src.name + "_16", shape)
            w16[src.name] = dst
            tot = 1
            for d_ in shape:
                tot *= d_
            per = tot // 128
            assert per * 128 == tot
            for c0 in range(0, per, CH):
                n = min(CH, per - c0)
                i = job % 3
                kb.dma("sp", cvf[:, i, 0:n], bass.AP(src, c0, [[per, 128], [1, n]]), writes=[cvf_r[i]])
                eng = "act"
                if eng == "act":
                    kb.op("act", lambda a: a.activation(out=cvb[:, i, 0:n], in_=cvf[:, i, 0:n], func=AF.Copy),
                          reads=[cvf_r[i]], writes=[cvb_r[i]])
                else:
                    kb.op(eng, lambda v: v.tensor_scalar(out=cvb[:, i, 0:n], in0=cvf[:, i, 0:n], scalar1=1.0, scalar2=None, op0=ALU.mult),
                          reads=[cvf_r[i]], writes=[cvb_r[i]])
                kb.dma("act", bass.AP(dst, c0, [[per, 128], [1, n]]), cvb[:, i, 0:n], reads=[cvb_r[i]], writes=[w16_r])
                job += 1
        kb.barrier()
    w_in16, w_nsa16, w_out16 = w16["w_in"], w16["w_nsa"], w16["w_out"]
    w_gate16, w_up16, w_down16 = w16["ffn_w_gate"], w16["ffn_w_up"], w16["ffn_w_down"]
    wk1_16, wv1_16 = w16["nsa_wk1"], w16["nsa_wv1"]

    PS = [nc.alloc_psum_tensor("ps%d" % i, [128, 512], F32) for i in range(6)]
    PS_r = regs(6)
    PB = [nc.alloc_psum_tensor("pb%d" % i, [128, 1024], BF16) for i in range(2)]
    PB_r = regs(2)

    _uid = [0]

    def sb(name, shape, dt):
        _uid[0] += 1
        return nc.sbuf_tensor("%s_%d" % (name, _uid[0]), list(shape), dt)

    def proj_fm(ps_ap, Wt, c0, M, tok0, N, wreads, treads, pw, kparts=8):
        def fn(pe):
            inst = None
            for k in range(kparts):
                inst = pe.matmul(ps_ap, lhsT=Wt[:, k, c0:c0 + M], rhs=XT[:, k, tok0:tok0 + N],
                                 start=(k == 0), stop=(k == kparts - 1))
            return inst
        kb.op("pe", fn, reads=list(wreads) + list(treads), writes=[pw])

    def proj_tm(ps_ap, Wt, c0, N, tt, wreads, pw):
        def fn(pe):
            inst = None
            for k in range(8):
                inst = pe.matmul(ps_ap, lhsT=XT[:, k, tt * 128:(tt + 1) * 128], rhs=Wt[:, k, c0:c0 + N],
                                 start=(k == 0), stop=(k == 7))
            return inst
        kb.op("pe", fn, reads=list(wreads) + [XT_r[tt]], writes=[pw])

    def load_w(Wt, src3, ncols, wreg, q="sp", chunk=None):
        for k in range(8):
            kb.dma(q, Wt[:, k, 0:ncols], src3[k * 128:(k + 1) * 128, 0:ncols], reads=[w16_r], writes=[wreg])

    xb = [nc.alloc_sbuf_tensor("xb%d" % i, [128, D], BF16) for i in range(2)]
    xb_r = regs(2)
    xb_i = [0]

    def make_xT(src_ap, src_reg, tt):
        i = xb_i[0]
        xb_i[0] ^= 1
        kb.op("act", lambda a: a.activation(out=xb[i][:], in_=src_ap, func=AF.Copy),
              reads=[src_reg], writes=[xb_r[i]])
        pb = PB[i]

        def fn(pe):
            inst = None
            for k in range(8):
                inst = pe.transpose(out=pb[:, k * 128:(k + 1) * 128], in_=xb[i][:, k * 128:(k + 1) * 128],
                                    identity=ident[:])
            return inst
        kb.op("pe", fn, reads=[xb_r[i], c_r], writes=[PB_r[i]])
        kb.op("dve", lambda v: v.tensor_copy(out=XT[:, :, tt * 128:(tt + 1) * 128],
                                             in_=pb[:, :].rearrange("p (k t) -> p k t", k=8)),
              reads=[PB_r[i]], writes=[XT_r[tt]])

    def layer_norm(xt_ap, xreg, lnw_ap, lnb_ap, lnreg, st, st_r):
        kb.op("dve", lambda v: v.bn_stats(out=st[:, 0:6], in_=xt_ap[:, 0:512]), reads=[xreg], writes=[st_r])
        kb.op("dve", lambda v: v.bn_stats(out=st[:, 6:12], in_=xt_ap[:, 512:1024]), reads=[xreg, st_r], writes=[st_r])
        kb.op("dve", lambda v: v.bn_aggr(out=st[:, 12:14], in_=st[:, 0:12]), reads=[st_r], writes=[st_r])
        kb.op("act", lambda a: a.activation(out=st[:, 14:15], in_=st[:, 13:14], func=AF.Sqrt, bias=LN_EPS, scale=1.0),
              reads=[st_r], writes=[st_r])
        kb.op("dve", lambda v: v.reciprocal(out=st[:, 15:16], in_=st[:, 14:15]), reads=[st_r], writes=[st_r])
        kb.op("dve", lambda v: v.scalar_tensor_tensor(out=st[:, 16:17], in0=st[:, 12:13], scalar=-1.0, in1=st[:, 15:16],
                                                      op0=ALU.mult, op1=ALU.mult), reads=[st_r], writes=[st_r])
        kb.op("act", lambda a: a.activation(out=xt_ap, in_=xt_ap, func=AF.Identity, bias=st[:, 16:17], scale=st[:, 15:16]),
              reads=[st_r, xreg], writes=[xreg])
        kb.op("dve", lambda v: v.tensor_tensor(out=xt_ap, in0=xt_ap, in1=lnw_ap, op=ALU.mult), reads=[xreg, lnreg], writes=[xreg])
        kb.op("dve", lambda v: v.tensor_tensor(out=xt_ap, in0=xt_ap, in1=lnb_ap, op=ALU.add), reads=[xreg, lnreg], writes=[xreg])

    def dump(name, sb_ap, reg, idx=None):
        if name in dump_d:
            dst = dump_d[name].ap() if idx is None else dump_d[name].ap()[idx]
            kb.dma("sp", dst, sb_ap, reads=[reg])

    kb.op("dve", lambda v: v.tensor_scalar(out=ptn[:, :, :], in0=ptb[:, :, :], scalar1=-1.0, scalar2=None, op0=ALU.mult),
          reads=[c_r], writes=[c_r])
    kb.op("dve", lambda v: v.tensor_scalar(out=ptn[:, :, 2:13], in0=ptb[:, :, 2:13], scalar1=-1.0, scalar2=1.0,
                                           op0=ALU.mult, op1=ALU.add), reads=[c_r], writes=[c_r])

    def gla_phase(sq, l, OT, OT_r):
        with ExitStack() as es:
            A = lambda n_, s_, d_: es.enter_context(sb(n_, s_, d_))
            Wg = A("Wg", [128, 8, 1152], BF16)
            wa2 = A("wa2", [16, 256], BF16)
            gln = A("gln", [128, 2, 256], F32)
            alrT = A("alrT", [16, 512], BF16)
            t1 = A("gt1", [128, 512], F32)
            t2 = A("gt2", [128, 512], F32)
            EB = A("EB", [128, 2, 512], F32)
            QTl = A("gQT", [128, 2, 2, 512], BF16)
            KTl = A("gKT", [128, 2, 512], BF16)
            Ktok = A("gKtok", [128, 4, 256], BF16)
            V = A("gV", [128, 4, 256], BF16)
            SG = A("gSG", [128, 4, 256], F32)
            AT = A("gAT", [128, 4, 128], BF16)
            St = A("gSt", [128, 2, 128], F32)
            tmpS = A("gtmpS", [128, 128], F32)
            blkm = A("gblk", [128, 128], F32)
            Sb = A("gSb", [128, 2, 128], BF16)
            rmask = A("grm", [128, 512], F32)
            ow = A("gow", [128, 256], F32)
            ow2 = A("gow2", [128, 256], F32)
            ob = A("gob", [128, 256], BF16)
            st = A("gst", [128, 32], F32)
            W_r, p_r, alr_r, t1_r, t2_r = Reg(), Reg(), Reg(), Reg(), Reg()
            EB_r, QT_r, KT_r = regs(2), regs(2), regs(2)
            Ktok_r, V_r, SG_r, AT_r, St_r, Sb_r = Reg(), regs(4), regs(4), Reg(), Reg(), Reg()
            ow_r, ow2_r, ob_r, st_r = Reg(), Reg(), Reg(), Reg()
            gs = cfg.get("gla_stop", 99)
            load_w(Wg, w_in16.ap()[l][:, 0:1040], 1040, W_r)
            kb.dma("pool", wa2[:, :], gla_w_a2.ap()[l], writes=[p_r])
            kb.dma("sp", gln[:, 0, :], bcast_rows(rowtab, l * 5120 + 0, 256), writes=[p_r])
            kb.dma("sp", gln[:, 1, :], bcast_rows(rowtab, l * 5120 + 256, 256), writes=[p_r])
            kb.op("pool", lambda g: g.memset(rmask[:, :], 1.0), writes=[p_r])
            kb.op("pool", lambda g: g.memset(rmask[:, :].rearrange("p (c t) -> p c t", c=4)[:, :, 0:1], 0.0), writes=[p_r])
            kb.op("pool", lambda g: g.memset(St[:, :, :], 0.0), writes=[St_r])
            kb.op("pool", lambda g: g.memset(QTl[:, :, :, :], 0.0), writes=QT_r)
            kb.dma("sp", blkm[:, :], c_blk.ap(), writes=[p_r])
            tmpS_r = Reg()
            kb.op("pool", lambda g: g.memset(Sb[:, :, :], 0.0), writes=[Sb_r])
            for mt in range(4 if gs > 0 else 0):
                tok0 = mt * 512
                xr = XT_r[mt * 4:mt * 4 + 4]
                proj_fm(PS[0][0:16, :], Wg, 1024, 16, tok0, 512, [W_r], xr, PS_r[0])
                kb.op("act", lambda a: a.activation(out=alrT[:, :], in_=PS[0][0:16, :], func=AF.Copy),
                      reads=[PS_r[0]], writes=[alr_r])
                for hp in range(2):
                    kb.op("pe", lambda pe: pe.matmul(PS[1][:, :], lhsT=wa2[0:16, hp * 128:(hp + 1) * 128], rhs=alrT[0:16, :],
                                                     start=True, stop=True), reads=[p_r, alr_r], writes=[PS_r[1]])
                    kb.op("act", lambda a: a.activation(out=t1[:, :], in_=PS[1][:, :], func=AF.Exp, scale=-1.0,
                                                        bias=ptn[:, l, hp:hp + 1]), reads=[PS_r[1], c_r], writes=[t1_r])
                    kb.op("act", lambda a: a.activation(out=t1[:, :], in_=t1[:, :], func=AF.Ln, bias=1.0, scale=1.0),
                          reads=[t1_r], writes=[t1_r])
                    kb.op("dve", lambda v: v.tensor_tensor_scan(out=t2[:, :], data0=rmask[:, :], data1=t1[:, :], initial=0.0,
                                                                op0=ALU.mult, op1=ALU.add), reads=[t1_r, p_r], writes=[t2_r])
                    kb.op("act", lambda a: a.activation(out=EB[:, hp, :], in_=t2[:, :], func=AF.Exp, scale=-1.0 / 16.0),
                          reads=[t2_r], writes=[EB_r[hp]])
                    kb.op("act", lambda a: a.activation(out=t1[:, :], in_=t2[:, :], func=AF.Exp, scale=1.0 / 16.0),
                          reads=[t2_r], writes=[t1_r])
                    proj_fm(PS[2][:, :], Wg, hp * 128, 128, tok0, 512, [W_r], xr, PS_r[2])
                    for hh in range(2):
                        pr = slice(hh * 64, hh * 64 + 64)
                        kb.op("dve", lambda v: v.scalar_tensor_tensor(out=QTl[pr, hp, hh, :], in0=PS[2][pr, :], scalar=0.125,
                                                                      in1=EB[pr, hp, :], op0=ALU.mult, op1=ALU.mult),
                              reads=[PS_r[2], EB_r[hp]], writes=[QT_r[hp]])
                    proj_fm(PS[3][:, :], Wg, 256 + hp * 128, 128, tok0, 512, [W_r], xr, PS_r[3])
                    kb.op("dve", lambda v: v.tensor_tensor(out=KTl[:, hp, :], in0=PS[3][:, :], in1=t1[:, :], op=ALU.mult),
                          reads=[PS_r[3], t1_r], writes=[KT_r[hp]])

                    def fn(pe):
                        inst = None
                        for j in range(4):
                            inst = pe.transpose(out=PB[0][:, j * 128:(j + 1) * 128], in_=KTl[:, hp, j * 128:(j + 1) * 128],
                                                identity=ident[:])
                        return inst
                    kb.op("pe", fn, reads=[KT_r[hp], c_r], writes=[PB_r[0]])
                    kb.op("dve", lambda v: v.tensor_copy(out=Ktok[:, :, hp * 128:(hp + 1) * 128],
                                                         in_=PB[0][:, 0:512].rearrange("p (j c) -> p j c", j=4)),
                          reads=[PB_r[0]], writes=[Ktok_r])
                for j in range(4 if gs > 1 else 0):
                    tt = mt * 4 + j
                    proj_tm(PS[4][:, 0:256], Wg, 512, 256, tt, [W_r], PS_r[4])
                    proj_tm(PS[5][:, 0:256], Wg, 768, 256, tt, [W_r], PS_r[5])
                    if cfg.get("gv", 3) >= 2:
                        kb.op("dve", lambda v: v.tensor_scalar(out=V[:, j, :], in0=PS[4][:, 0:256], scalar1=1.0, scalar2=None, op0=ALU.mult),
                              reads=[PS_r[4]], writes=[V_r[j]])
                    if cfg.get("gv", 3) >= 3:
                        kb.op("act", lambda a: a.activation(out=SG[:, j, :], in_=PS[5][:, 0:256], func=AF.Silu),
                              reads=[PS_r[5]], writes=[SG_r[j]])
                for j in range(4 if gs > 2 else 0):
                    tt = mt * 4 + j
                    cc = slice(j * 128, (j + 1) * 128)

                    def fn(pe):
                        inst = None
                        for h in range(4):
                            hp, hh = h // 2, h % 2
                            inst = pe.matmul(PS[0][:, h * 128:(h + 1) * 128], lhsT=KTl[:, hp, cc], rhs=QTl[:, hp, hh, cc],
                                             start=True, stop=True)
                        return inst
                    kb.op("pe", fn, reads=QT_r + KT_r, writes=[PS_r[0]])
                    kb.op("dve", lambda v: v.tensor_tensor(out=AT[:, :, :], in0=PS[0][:, :].rearrange("p (h t) -> p h t", h=4),
                                                          in1=caus[:, :, :], op=ALU.mult), reads=[PS_r[0], c_r], writes=[AT_r])

                    def fn(pe):
                        inst = None
                        for h in range(4):
                            hp, hh = h // 2, h % 2
                            pe.matmul(PS[1][:, h * 64:(h + 1) * 64], lhsT=AT[:, h, :], rhs=V[:, j, h * 64:(h + 1) * 64],
                                      start=True, stop=False)
                            inst = pe.matmul(PS[1][:, h * 64:(h + 1) * 64], lhsT=QTl[:, hp, hh, cc], rhs=Sb[:, hp, hh * 64:(hh + 1) * 64],
                                             start=False, stop=True)
                        return inst
                    if gs <= 3:
                        continue
                    kb.op("pe", fn, reads=[AT_r, V_r[j], Sb_r] + QT_r, writes=[PS_r[1]])

                    def fn(pe):
                        inst = None
                        for hp in range(2):
                            inst = pe.matmul(PS[2][:, hp * 128:(hp + 1) * 128], lhsT=Ktok[:, j, hp * 128:(hp + 1) * 128],
                                             rhs=V[:, j, hp * 128:(hp + 1) * 128], start=True, stop=True)
                        return inst
                    if gs <= 4:
                        continue
                    kb.op("pe", fn, reads=[Ktok_r, V_r[j]], writes=[PS_r[2]])
                    for hp in range(2):
                        ee = EB[:, hp, j * 128 + 127:j * 128 + 128]
                        kb.op("dve", lambda v: v.scalar_tensor_tensor(out=tmpS[:, :], in0=PS[2][:, hp * 128:(hp + 1) * 128], scalar=ee,
                                                                      in1=blkm[:, :], op0=ALU.mult, op1=ALU.mult),
                              reads=[PS_r[2], EB_r[hp], p_r], writes=[tmpS_r])
                        kb.op("dve", lambda v: v.scalar_tensor_tensor(out=St[:, hp, :], in0=St[:, hp, :], scalar=ee, in1=tmpS[:, :],
                                                                      op0=ALU.mult, op1=ALU.add),
                              reads=[St_r, tmpS_r, EB_r[hp]], writes=[St_r])
                    kb.op("act", lambda a: a.activation(out=Sb[:, :, :], in_=St[:, :, :], func=AF.Copy), reads=[St_r], writes=[Sb_r])
                    if gs <= 5:
                        continue
                    head_norm_gate(PS[1][:, 0:256], PS_r[1], ow, ow_r, ow2, ow2_r, st, st_r, gln, p_r, LN_EPS)
                    kb.op("dve", lambda v: v.tensor_tensor(out=ob[:, :], in0=ow[:, :], in1=SG[:, j, :], op=ALU.mult),
                          reads=[ow_r, SG_r[j]], writes=[ob_r])
                    if gs <= 6:
                        continue
                    out_to_OT(ob, ob_r, 128, OT, OT_r, 0, tt * 128)
            kb.barrier()

    def head_norm_gate(ps_ap, ps_r, ow, ow_r, ow2, ow2_r, st, st_r, gln, gln_r, eps, P=128):
        v4 = lambda ap: ap.rearrange("p (h d) -> p h d", h=4)
        kb.op("act", lambda a: a.activation(out=ow[0:P, :], in_=ps_ap, func=AF.Copy), reads=[ps_r], writes=[ow_r])
        kb.op("act", lambda a: a.activation(out=ow2[0:P, :], in_=ow[0:P, :], func=AF.Square), reads=[ow_r], writes=[ow2_r])
        kb.op("dve", lambda v: v.reduce_sum(out=st[0:P, 0:4], in_=v4(ow[0:P, :]), axis=AX.X), reads=[ow_r], writes=[st_r])
        kb.op("dve", lambda v: v.reduce_sum(out=st[0:P, 4:8], in_=v4(ow2[0:P, :]), axis=AX.X), reads=[ow2_r, st_r], writes=[st_r])
        kb.op("dve", lambda v: v.tensor_scalar(out=st[0:P, 8:16], in0=st[0:P, 0:8], scalar1=1.0 / 64.0, scalar2=None, op0=ALU.mult),
              reads=[st_r], writes=[st_r])
        kb.op("dve", lambda v: v.tensor_tensor(out=st[0:P, 16:20], in0=st[0:P, 8:12], in1=st[0:P, 8:12], op=ALU.mult),
              reads=[st_r], writes=[st_r])
        kb.op("dve", lambda v: v.tensor_tensor(out=st[0:P, 20:24], in0=st[0:P, 12:16], in1=st[0:P, 16:20], op=ALU.subtract),
              reads=[st_r], writes=[st_r])
        kb.op("act", lambda a: a.activation(out=st[0:P, 24:28], in_=st[0:P, 20:24], func=AF.Sqrt, bias=eps, scale=1.0),
              reads=[st_r], writes=[st_r])
        kb.op("dve", lambda v: v.reciprocal(out=st[0:P, 28:32], in_=st[0:P, 24:28]), reads=[st_r], writes=[st_r])
        for h in range(4):
            kb.op("dve", lambda v: v.tensor_scalar(out=ow[0:P, h * 64:(h + 1) * 64], in0=ow[0:P, h * 64:(h + 1) * 64],
                                                   scalar1=st[0:P, 8 + h:9 + h], scalar2=st[0:P, 28 + h:29 + h],
                                                   op0=ALU.subtract, op1=ALU.mult), reads=[ow_r, st_r], writes=[ow_r])
        kb.op("dve", lambda v: v.tensor_tensor(out=ow[0:P, :], in0=ow[0:P, :], in1=gln[0:P, 0, :], op=ALU.mult),
              reads=[ow_r, gln_r], writes=[ow_r])
        kb.op("dve", lambda v: v.tensor_tensor(out=ow[0:P, :], in0=ow[0:P, :], in1=gln[0:P, 1, :], op=ALU.add),
              reads=[ow_r, gln_r], writes=[ow_r])

    def out_to_OT(ob, ob_r, P, OT, OT_r, k0, tokc0, ncols=256):
        nk = ncols // 128

        def fn(pe):
            inst = None
            for kk in range(nk):
                inst = pe.transpose(out=PB[1][:, kk * 128:kk * 128 + P], in_=ob[0:P, kk * 128:(kk + 1) * 128],
                                    identity=ident[0:P, 0:P])
            return inst
        kb.op("pe", fn, reads=[ob_r, c_r], writes=[PB_r[1]])
        tr = OT_r[tokc0 // 128]
        kb.op("act", lambda a: a.activation(out=OT[:, k0:k0 + nk, tokc0:tokc0 + P],
                                            in_=PB[1][:, 0:nk * 128].rearrange("p (k t) -> p k t", k=nk)[:, :, 0:P], func=AF.Copy),
              reads=[PB_r[1]], writes=[tr])

    RWT = [(0, 128), (128, 128), (256, 128), (384, 128), (512, 128), (640, 128), (768, 64), (832, 64),
           (896, 128), (1024, 32), (1056, 32)]
    C0 = float(math.exp(-0.5))

    def rwkv_phase(sq, l, OT, OT_r):
        MT = 256
        NCH = MT // 64
        PU = NCH * 2
        with ExitStack() as es:
            A = lambda n_, s_, d_: es.enter_context(sb(n_, s_, d_))
            Wr = A("Wr", [128, 8, 1088], BF16)
            w2 = A("rw2", [64, 256], BF16)
            a2 = A("ra2", [64, 256], BF16)
            v2 = A("rv2", [32, 256], BF16)
            g2a = A("rg2a", [128, 256], BF16)
            g2b = A("rg2b", [128, 256], BF16)
            rln = A("rln", [128, 2, 2, 64], F32)
            blkf = A("rblkf", [128, 128], F32)
            m3 = A("rm3", [128, 3, 128], F32)
            istk = A("ristk", [128, 64], BF16)
            onec = A("ronec", [128, 1], BF16)
            rmask = A("rrm", [128, MT], F32)
            xs = [A("rxs%d" % i, [128, MT], F32) for i in range(6)]
            tt_ = [A("rt%d" % i, [128, MT], F32) for i in range(8)]
            ttb_ = [A("rtb%d" % i, [128, MT], F32) for i in range(9)]
            tb_r = regs(9)
            Gam2 = [A("rGam%d" % i_, [128, 2, MT], F32) for i_ in range(2)]
            AR2 = [A("rAR%d" % i_, [128, 2, 2, NCH, 2, 64], BF16) for i_ in range(2)]
            BK = A("rBK", [128, 2, 2, NCH, 2, 64], BF16)
            VTz = A("rVTz", [128, 2, NCH, 2, 64], BF16)
            rkrz = A("rrkrz", [128, 2, NCH, 2, 64], BF16)
            TW = A("rTW", [64, MT], BF16)
            AL = A("rAL", [64, MT], BF16)
            SGLd = A("rSGLd", [128, NCH, 2, 64], BF16)
            SGL2d = A("rSGL2d", [128, NCH, 2, 64], BF16)
            VR = A("rVR", [32, MT], BF16)
            Vstk2 = [A("rVstk%d" % i_, [128, NCH, 2, 64], BF16) for i_ in range(2)]
            Bblk2 = [A("rBblk%d" % i_, [128, NCH, 2, 128], BF16) for i_ in range(2)]
            Kblk2 = [A("rKblk%d" % i_, [128, NCH, 2, 128], BF16) for i_ in range(2)]
            gtok2 = [A("rgtok%d" % i_, [128, NCH, 2, 64], F32) for i_ in range(2)]
            cb2 = [A("rcb%d" % i_, [128, NCH, 2], F32) for i_ in range(2)]
            MN = [A("rMN%d" % i, [128, PU, 2, 128], F32) for i in range(2)]
            Pm2 = [A("rP%d" % i_, [128, PU, 128], F32) for i_ in range(2)]
            A32 = [A("rA3%d" % i_, [128, PU, 3, 128], BF16) for i_ in range(2)]
            Rs = A("rRs", [128, 128], F32)
            Ub = A("rUb", [128, 128], BF16)
            Tst = A("rTst", [128, 2, 64], F32)
            Tb = A("rTb", [128, 2, 64], BF16)
            tmpS = A("rtmpS", [128, 2, 64], F32)
            ow = A("row", [128, 128], F32)
            ow2 = A("row2", [128, 128], F32)
            obblk = A("robblk", [128, 2, 128], BF16)
            st = A("rst", [128, 32], F32)
            W_r, p_r = Reg(), Reg()
            xs_r, t_r = regs(6), regs(8)
            t_r0 = t_r
            BK_r, VTz_r, rkrz_r = regs(2), regs(2), regs(2)
            Gam_r2, AR_r2 = [regs(2), regs(2)], [regs(2), regs(2)]
            TW_r, AL_r, SGL_r, SGL2_r, VR_r = Reg(), Reg(), Reg(), Reg(), Reg()
            Vstk_r2, Bblk_r2, Kblk_r2, gtok_r2, cb_r2 = regs(2), regs(2), regs(2), regs(2), regs(2)
            MN_r, Rs_r, Ub_r, T_r, Tb_r, tmpS_r = regs(2), Reg(), Reg(), Reg(), Reg(), Reg()
            P_r2, A3_r2 = regs(2), regs(2)
            ow_r, ow2_r, ob_r, st_r = Reg(), Reg(), Reg(), Reg()
            rs_ = cfg.get("rw_stop", 99)
            load_w(Wr, w_in16.ap()[l][:, 1040:2096], 1056, W_r)
            if l >= 1:
                for k in range(8):
                    kb.dma("pool", Wr[:, k, 1056:1088], w_vres.ap()[l - 1][k * 128:(k + 1) * 128, :], writes=[W_r])
            kb.dma("pool", w2[:, :], rwkv_w2.ap()[l], writes=[p_r])
            kb.dma("pool", a2[:, :], rwkv_a2.ap()[l], writes=[p_r])
            if l >= 1:
                kb.dma("pool", v2[:, :], rwkv_v2.ap()[l - 1], writes=[p_r])
            kb.op("pool", lambda g: g.memset(g2b[:, :], 0.0), writes=[p_r])
            kb.dma("pool", g2a[:, :], rwkv_g2.ap()[l][0:128, :], writes=[p_r])
            kb.dma("pool", g2b[0:32, :], rwkv_g2.ap()[l][128:160, :], reads=[p_r], writes=[p_r])
            for wb in range(2):
                for hp in range(2):
                    for hh in range(2):
                        kb.dma("sp", rln[hh * 64:(hh + 1) * 64, wb, hp, :],
                               bcast_rows(rowtab, l * 5120 + 512 + wb * 256 + (2 * hp + hh) * 64, 64, parts=64), writes=[p_r])
            kb.dma("sp", blkf[:, :], c_blk.ap(), writes=[p_r])
            kb.dma("sp", m3[:, :, :], c_rwmask2.ap(), writes=[p_r])
            kb.dma("pool", istk[:, :], c_istk.ap(), writes=[p_r])
            kb.op("pool", lambda g: g.memset(onec[:, :], 1.0), writes=[p_r])
            kb.op("pool", lambda g: g.memset(rmask[:, :], 1.0), writes=[p_r])
            kb.op("pool", lambda g: g.memset(rmask[:, :].rearrange("p (c t) -> p c t", c=NCH)[:, :, 0:1], 0.0), writes=[p_r])
            zl = [(Tst, [T_r]), (Tb, [Tb_r]), (SGL2d, [SGL2_r]), (BK, BK_r), (VTz, VTz_r), (rkrz, rkrz_r), (obblk, [ob_r])]
            for i_ in range(2):
                zl += [(AR2[i_], AR_r2[i_])]
            for tz, rr in zl:
                kb.op("pool", lambda g, tz=tz: g.memset(tz[:], 0.0), writes=rr)

            carry = A("rcarry", [128, 12], F32)
            carry_r = regs(11)
            kb.op("pool", lambda g: g.memset(carry[:, :], 0.0), writes=carry_r)

            def shift_proj(i, tok0, dst_fn):
                c0, n = RWT[i]
                mucol = 2 + i
                pi = i % 2
                tp = ttb_[7] if pi == 0 else ttb_[8]
                tpr = tb_r[7] if pi == 0 else tb_r[8]
                proj_fm(PS[pi][0:n, 0:MT], Wr, c0, n, tok0, MT, [W_r], XT_r[tok0 // 128:tok0 // 128 + MT // 128], PS_r[pi])
                kb.op("act", lambda a: a.activation(out=tp[0:n, 1:MT], in_=PS[pi][0:n, 0:MT - 1], func=AF.Copy,
                                                    scale=ptb[0:n, l, mucol:mucol + 1]), reads=[PS_r[pi], c_r], writes=[tpr, PS_r[pi]])
                kb.op("act", lambda a: a.activation(out=tp[0:n, 0:1], in_=carry[0:n, i:i + 1], func=AF.Copy,
                                                    scale=ptb[0:n, l, mucol:mucol + 1]), reads=[carry_r[i], c_r, tpr], writes=[tpr])
                kb.op("act", lambda a: a.activation(out=carry[0:n, i:i + 1], in_=PS[pi][0:n, MT - 1:MT], func=AF.Copy),
                      reads=[PS_r[pi], carry_r[i]], writes=[carry_r[i], PS_r[pi]])
                dst_ap, dst_regs = dst_fn()
                kb.op("dve", lambda v: v.scalar_tensor_tensor(out=dst_ap, in0=PS[pi][0:n, 0:MT], scalar=ptn[0:n, l, mucol:mucol + 1],
                                                              in1=tp[0:n, 0:MT], op0=ALU.mult, op1=ALU.add),
                      reads=[PS_r[pi], tpr, c_r], writes=dst_regs + [PS_r[pi]])

            def prep(mt):
                tok0 = mt * MT
                par = mt % 2
                t_r = t_r0
                AR, Vstk, Bblk, Kblk, Gam, gtok, cb, Pm, A3 = AR2[par], Vstk2[par], Bblk2[par], Kblk2[par], Gam2[par], gtok2[par], cb2[par], Pm2[par], A32[par]
                AR_r, Vstk_r, Bblk_r, Kblk_r, Gam_r, gtok_r, cb_r, P_r, A3_r = AR_r2[par], Vstk_r2[par], Bblk_r2[par], Kblk_r2[par], Gam_r2[par], gtok_r2[par], cb_r2[par], P_r2[par], A3_r2[par]
                for i in range(6):
                    shift_proj(i, tok0, lambda i=i: (xs[i][:, :], [xs_r[i]]))
                shift_proj(6, tok0, lambda: (tt_[0][0:64, :], [t_r[0]]))
                kb.op("act", lambda a: a.activation(out=TW[:, :], in_=tt_[0][0:64, :], func=AF.Tanh), reads=[t_r[0]], writes=[TW_r])
                shift_proj(7, tok0, lambda: (AL[:, :], [AL_r]))
                shift_proj(8, tok0, lambda: (tt_[0][:, :], [t_r[0]]))
                for hh_ in range(2):
                    kb.op("act", lambda a: a.activation(out=SGLd[:, :, hh_, :], in_=tt_[0][:, :].rearrange("p (c i) -> p c i", c=NCH), func=AF.Sigmoid),
                          reads=[t_r[0]], writes=[SGL_r])
                shift_proj(9, tok0, lambda: (tt_[0][0:32, :], [t_r[0]]))
                for hh_ in range(2):
                    kb.op("act", lambda a: a.activation(out=SGL2d[0:32, :, hh_, :], in_=tt_[0][0:32, :].rearrange("p (c i) -> p c i", c=NCH), func=AF.Sigmoid),
                          reads=[t_r[0]], writes=[SGL2_r])
                if l >= 1:
                    shift_proj(10, tok0, lambda: (VR[:, :], [VR_r]))
                if rs_ <= 1:
                    return
                for hp in range(2):
                    rT, kT, vT = xs[hp], xs[2 + hp], xs[4 + hp]
                    rT_r, kT_r, vT_r = xs_r[hp], xs_r[2 + hp], xs_r[4 + hp]
                    t1, t2, t3, t4, t5, t6, t7 = tt_[0:7] if hp == 0 else ttb_[0:7]
                    t_r = t_r0 if hp == 0 else tb_r
                    kb.op("pe", lambda pe: pe.matmul(PS[2][:, 0:MT], lhsT=w2[:, hp * 128:(hp + 1) * 128], rhs=TW[:, :], start=True, stop=True),
                          reads=[p_r, TW_r], writes=[PS_r[2]])
                    kb.op("act", lambda a: a.activation(out=t1[:, :], in_=PS[2][:, 0:MT], func=AF.Sigmoid, bias=ptb[:, l, 13 + hp:14 + hp]),
                          reads=[PS_r[2], c_r], writes=[t_r[0]])
                    kb.op("dve", lambda v: v.tensor_tensor_scan(out=t2[:, :], data0=rmask[:, :], data1=t1[:, :], initial=0.0,
                                                                op0=ALU.mult, op1=ALU.add), reads=[t_r[0], p_r], writes=[t_r[1]])
                    kb.op("act", lambda a: a.activation(out=Gam[:, hp, :], in_=t2[:, :], func=AF.Exp, scale=-C0), reads=[t_r[1]], writes=[Gam_r[hp]])
                    kb.op("act", lambda a: a.activation(out=t3[:, :], in_=t2[:, :], func=AF.Exp, scale=C0), reads=[t_r[1]], writes=[t_r[2]])
                    kb.op("dve", lambda v: v.tensor_tensor(out=t4[:, :], in0=t2[:, :], in1=t1[:, :], op=ALU.subtract),
                          reads=[t_r[0], t_r[1]], writes=[t_r[3]])
                    kb.op("act", lambda a: a.activation(out=t4[:, :], in_=t4[:, :], func=AF.Exp, scale=-C0), reads=[t_r[3]], writes=[t_r[3]])
                    kb.op("pe", lambda pe: pe.matmul(PS[3][:, 0:MT], lhsT=a2[:, hp * 128:(hp + 1) * 128], rhs=AL[:, :], start=True, stop=True),
                          reads=[p_r, AL_r], writes=[PS_r[3]])
                    kb.op("act", lambda a: a.activation(out=t5[:, :], in_=PS[3][:, 0:MT], func=AF.Sigmoid, bias=ptb[:, l, 15 + hp:16 + hp]),
                          reads=[PS_r[3], c_r], writes=[t_r[4]])
                    kb.op("dve", lambda v: v.tensor_scalar(out=t6[:, :], in0=kT[:, :], scalar1=ptb[:, l, 19 + hp:20 + hp], scalar2=None, op0=ALU.mult),
                          reads=[kT_r, c_r], writes=[t_r[5]])
                    kb.op("act", lambda a: a.activation(out=t7[:, :], in_=t6[:, :], func=AF.Square), reads=[t_r[5]], writes=[t_r[6]])
                    kb.op("pe", lambda pe: pe.matmul(PS[4][:, 0:MT], lhsT=blkf[:, :], rhs=t7[:, :], start=True, stop=True),
                          reads=[p_r, t_r[6]], writes=[PS_r[4]])
                    kb.op("act", lambda a: a.activation(out=t7[:, :], in_=PS[4][:, 0:MT], func=AF.Sqrt), reads=[PS_r[4], t_r[6]], writes=[t_r[6]])
                    kb.op("dve", lambda v: v.tensor_scalar(out=t7[:, :], in0=t7[:, :], scalar1=1e-12, scalar2=None, op0=ALU.max),
                          reads=[t_r[6]], writes=[t_r[6]])
                    kb.op("dve", lambda v: v.reciprocal(out=t7[:, :], in_=t7[:, :]), reads=[t_r[6]], writes=[t_r[6]])
                    kb.op("dve", lambda v: v.tensor_tensor(out=t6[:, :], in0=t6[:, :], in1=t7[:, :], op=ALU.mult),
                          reads=[t_r[5], t_r[6]], writes=[t_r[5]])
                    kb.op("dve", lambda v: v.tensor_scalar(out=t7[:, :], in0=t5[:, :], scalar1=-1.0, scalar2=ptb[:, l, 21 + hp:22 + hp],
                                                           op0=ALU.add, op1=ALU.mult), reads=[t_r[4], t_r[6], c_r], writes=[t_r[6]])
                    kb.op("dve", lambda v: v.scalar_tensor_tensor(out=t7[:, :], in0=t7[:, :], scalar=1.0, in1=kT[:, :], op0=ALU.add, op1=ALU.mult),
                          reads=[t_r[6], kT_r], writes=[t_r[6]])
                    v3 = lambda ap: ap.rearrange("p (c i) -> p c i", c=NCH)
                    for hh in range(2):
                        pr = slice(hh * 64, hh * 64 + 64)
                        kb.op("dve", lambda v: v.scalar_tensor_tensor(out=AR[pr, hp, 0, :, hh, :], in0=v3(t6[pr, :]), scalar=-1.0, in1=v3(t4[pr, :]),
                                                                      op0=ALU.mult, op1=ALU.mult), reads=[t_r[5], t_r[3]], writes=[AR_r[hp]])
                        kb.op("dve", lambda v: v.tensor_tensor(out=AR[pr, hp, 1, :, hh, :], in0=v3(rT[pr, :]), in1=v3(Gam[pr, hp, :]), op=ALU.mult),
                              reads=[rT_r, Gam_r[hp]], writes=[AR_r[hp]])
                    kb.op("dve", lambda v: v.tensor_tensor(out=t1[:, :], in0=t6[:, :], in1=t5[:, :], op=ALU.mult),
                          reads=[t_r[5], t_r[4], t_r[0]], writes=[t_r[0]])
                    for hh in range(2):
                        pr = slice(hh * 64, hh * 64 + 64)
                        kb.op("dve", lambda v: v.tensor_tensor(out=BK[pr, hp, 0, :, hh, :], in0=v3(t1[pr, :]), in1=v3(t3[pr, :]), op=ALU.mult),
                              reads=[t_r[0], t_r[2]], writes=[BK_r[hp]])
                        kb.op("dve", lambda v: v.tensor_tensor(out=BK[pr, hp, 1, :, hh, :], in0=v3(t7[pr, :]), in1=v3(t3[pr, :]), op=ALU.mult),
                              reads=[t_r[6], t_r[2]], writes=[BK_r[hp]])
                        kb.op("dve", lambda v: v.scalar_tensor_tensor(out=rkrz[pr, hp, :, hh, :], in0=v3(rT[pr, :]), scalar=ptb[pr, l, 23 + hp:24 + hp],
                                                                      in1=v3(t7[pr, :]), op0=ALU.mult, op1=ALU.mult),
                              reads=[rT_r, t_r[6], c_r], writes=[rkrz_r[hp]])
                    if l == 0:
                        kb.dma("sp", vfirst_d.ap()[hp, :, sq * 0 + tok0:tok0 + MT], vT[:, :], reads=[vT_r], writes=[vfirst_r[hp]])
                    else:
                        kb.op("pe", lambda pe: pe.matmul(PS[5][:, 0:MT], lhsT=v2[:, hp * 128:(hp + 1) * 128], rhs=VR[:, :], start=True, stop=True),
                              reads=[p_r, VR_r], writes=[PS_r[5]])
                        kb.op("act", lambda a: a.activation(out=t1[:, :], in_=PS[5][:, 0:MT], func=AF.Sigmoid, bias=ptb[:, l, 17 + hp:18 + hp]),
                              reads=[PS_r[5], c_r, t_r[0]], writes=[t_r[0]])
                        kb.dma("sp", t2[:, :], vfirst_d.ap()[hp, :, tok0:tok0 + MT], reads=[vfirst_r[hp], t_r[1]], writes=[t_r[1]])
                        kb.op("dve", lambda v: v.tensor_tensor(out=t2[:, :], in0=t2[:, :], in1=vT[:, :], op=ALU.subtract),
                              reads=[t_r[1], vT_r], writes=[t_r[1]])
                        kb.op("dve", lambda v: v.tensor_tensor(out=t2[:, :], in0=t2[:, :], in1=t1[:, :], op=ALU.mult),
                              reads=[t_r[1], t_r[0]], writes=[t_r[1]])
                        kb.op("dve", lambda v: v.tensor_tensor(out=vT[:, :], in0=vT[:, :], in1=t2[:, :], op=ALU.add),
                              reads=[t_r[1], vT_r], writes=[vT_r])
                    for hh in range(2):
                        pr = slice(hh * 64, hh * 64 + 64)
                        kb.op("act", lambda a: a.activation(out=VTz[pr, hp, :, hh, :], in_=vT[pr, :].rearrange("p (c i) -> p c i", c=NCH), func=AF.Copy),
                              reads=[vT_r], writes=[VTz_r[hp]])
                if rs_ <= 2:
                    return
                fl = lambda ap: ap.rearrange("p a b -> p (a b)")
                def fn(pe):
                    inst = None
                    for c in range(NCH):
                        for hp in range(2):
                            inst = pe.matmul(PS[2][:, (c * 2 + hp) * 64:(c * 2 + hp + 1) * 64], lhsT=fl(VTz[:, hp, c, :, :]), rhs=istk[:, :], start=True, stop=True)
                    return inst
                kb.op("pe", fn, reads=VTz_r + [p_r], writes=[PS_r[2]])
                kb.op("act", lambda a: a.activation(out=Vstk[:, :, :, :], in_=PS[2][:, :].rearrange("p (c h d) -> p c h d", c=NCH, h=2), func=AF.Copy),
                      reads=[PS_r[2]], writes=[Vstk_r, PS_r[2]])

                def fn(pe):
                    inst = None
                    for c in range(NCH):
                        for hp in range(2):
                            inst = pe.matmul(PS[3][:, c * 2 + hp:c * 2 + hp + 1], lhsT=fl(rkrz[:, hp, c, :, :]), rhs=onec[:, :], start=True, stop=True)
                    return inst
                kb.op("pe", fn, reads=rkrz_r + [p_r], writes=[PS_r[3]])
                kb.op("act", lambda a: a.activation(out=cb[:, :, :], in_=PS[3][:, 0:NCH * 2].rearrange("p (c h) -> p c h", c=NCH), func=AF.Copy),
                      reads=[PS_r[3]], writes=[cb_r, PS_r[3]])
                for bk, dst, dst_r, pbi in ((0, Bblk, Bblk_r, 0), (1, Kblk, Kblk_r, 1)):
                    def fn(pe, bk=bk, pbi=pbi):
                        inst = None
                        for c in range(NCH):
                            for hp in range(2):
                                inst = pe.transpose(out=PB[pbi][:, (c * 2 + hp) * 128:(c * 2 + hp + 1) * 128], in_=fl(BK[:, hp, bk, c, :, :]), identity=ident[:, :])
                        return inst
                    kb.op("pe", fn, reads=BK_r + [c_r], writes=[PB_r[pbi]])
                    kb.op("dve", lambda v, dst=dst, pbi=pbi: v.tensor_copy(out=dst[:, :, :, :], in_=PB[pbi][:, :].rearrange("p (c h d) -> p c h d", c=NCH, h=2)),
                          reads=[PB_r[pbi]], writes=[dst_r, PB_r[pbi]])
                for c2 in range(NCH // 2):
                    def fn(pe):
                        inst = None
                        for cc_ in range(2):
                            c = c2 * 2 + cc_
                            pe.matmul(PS[3][:, cc_ * 256:(cc_ + 1) * 256], lhsT=fl(SGLd[:, c, :, :]), rhs=g2a[:, :], start=True, stop=False)
                            inst = pe.matmul(PS[3][:, cc_ * 256:(cc_ + 1) * 256], lhsT=fl(SGL2d[:, c, :, :]), rhs=g2b[:, :], start=False, stop=True)
                        return inst
                    kb.op("pe", fn, reads=[SGL_r, SGL2_r, p_r], writes=[PS_r[3]])
                    for hh in range(2):
                        pr = slice(hh * 64, hh * 64 + 64)
                        kb.op("act", lambda a: a.activation(out=gtok[pr, c2 * 2:c2 * 2 + 2, :, :],
                                                            in_=PS[3][pr, :].rearrange("p (c h g d) -> p c h g d", c=2, h=2, g=2)[:, :, :, hh, :], func=AF.Copy),
                              reads=[PS_r[3]], writes=[gtok_r, PS_r[3]])
                for c in range(NCH):
                    for hp in range(2):
                        pu = c * 2 + hp
                        px, py = (4, 5) if pu % 2 == 0 else (2, 3)

                        def fn(pe, px=px, py=py):
                            pe.matmul(PS[px][:, 0:128], lhsT=fl(AR[:, hp, 0, c, :, :]), rhs=fl(BK[:, hp, 0, c, :, :]), start=True, stop=True)
                            pe.matmul(PS[px][:, 128:384], lhsT=fl(BK[:, hp, 0, c, :, :]), rhs=AR[:, hp, :, c, :, :].rearrange("p a h i -> p a (h i)"),
                                      start=True, stop=True)
                            return pe.matmul(PS[py][:, 0:256], lhsT=fl(BK[:, hp, 1, c, :, :]), rhs=AR[:, hp, :, c, :, :].rearrange("p a h i -> p a (h i)"),
                                             start=True, stop=True)
                        kb.op("pe", fn, reads=AR_r + BK_r, writes=[PS_r[px], PS_r[py]])
                        kb.op("dve", lambda v, px=px: v.tensor_tensor(out=MN[0][:, pu, :, :], in0=PS[px][:, 0:256].rearrange("p (a b) -> p a b", a=2),
                                                                      in1=m3[:, 0:2, :], op=ALU.mult), reads=[PS_r[px], p_r], writes=[MN_r[0], PS_r[px]])
                        kb.op("dve", lambda v, px=px: v.tensor_tensor(out=A3[:, pu, 0, :], in0=PS[px][:, 256:384], in1=m3[:, 2, :], op=ALU.mult),
                              reads=[PS_r[px], p_r], writes=[A3_r, PS_r[px]])
                        kb.op("dve", lambda v, py=py: v.tensor_tensor(out=A3[:, pu, 1:3, :], in0=PS[py][:, 0:256].rearrange("p (a b) -> p a b", a=2),
                                                                      in1=m3[:, 1:3, :], op=ALU.mult), reads=[PS_r[py], p_r], writes=[A3_r, PS_r[py]])
                if rs_ <= 3:
                    return
                kb.op("dve", lambda v: v.tensor_tensor(out=Pm[:, :, :], in0=MN[0][:, :, 1, :],
                                                      in1=bass.AP(identf, 0, [[128, 128], [0, PU], [1, 128]]), op=ALU.add),
                      reads=[MN_r[0], c_r], writes=[P_r])
                cur = 0
                for lev in range(5):
                    nxt = 1 - cur
                    lastlev = (lev == 4)
                    for g2_ in range(PU // 2):
                        pi = 4 + (g2_ % 2)

                        def fn(pe, pi=pi):
                            inst = None
                            for q in range(2):
                                u = g2_ * 2 + q
                                inst = pe.matmul(PS[pi][:, q * 256:q * 256 + 128], lhsT=MN[cur][:, u, 1, :], rhs=MN[cur][:, u, 0, :], start=True, stop=True)
                                if not lastlev:
                                    inst = pe.matmul(PS[pi][:, q * 256 + 128:q * 256 + 256], lhsT=MN[cur][:, u, 0, :], rhs=MN[cur][:, u, 1, :],
                                                     start=True, stop=True)
                            return inst
                        kb.op("pe", fn, reads=[MN_r[cur]], writes=[PS_r[pi]])
                        kb.op("act", lambda a, pi=pi: a.activation(out=MN[nxt][:, g2_ * 2:g2_ * 2 + 2, :, :],
                                                                   in_=PS[pi][:, :].rearrange("p (u a b) -> p u a b", u=2, a=2), func=AF.Copy),
                              reads=[PS_r[pi]], writes=[MN_r[nxt], PS_r[pi]])
                    for g4_ in range(PU // 4):
                        pi = 2 + (g4_ % 2)

                        def fn(pe, pi=pi):
                            inst = None
                            for q in range(4):
                                u = g4_ * 4 + q
                                inst = pe.matmul(PS[pi][:, q * 128:(q + 1) * 128], lhsT=MN[nxt][:, u, 0, :], rhs=Pm[:, u, :], start=True, stop=True)
                            return inst
                        kb.op("pe", fn, reads=[MN_r[nxt], P_r], writes=[PS_r[pi]])
                        kb.op("dve", lambda v, pi=pi: v.tensor_tensor(out=Pm[:, g4_ * 4:g4_ * 4 + 4, :], in0=PS[pi][:, :].rearrange("p (u b) -> p u b", u=4),
                                                                      in1=Pm[:, g4_ * 4:g4_ * 4 + 4, :], op=ALU.add), reads=[PS_r[pi], P_r], writes=[P_r, PS_r[pi]])
                    cur = nxt

            def chain(mt):
                tok0 = mt * MT
                par = mt % 2
                AR, Vstk, Bblk, Kblk, Gam, gtok, cb, Pm, A3 = AR2[par], Vstk2[par], Bblk2[par], Kblk2[par], Gam2[par], gtok2[par], cb2[par], Pm2[par], A32[par]
                AR_r, Vstk_r, Bblk_r, Kblk_r, Gam_r, gtok_r, cb_r, P_r, A3_r = AR_r2[par], Vstk_r2[par], Bblk_r2[par], Kblk_r2[par], Gam_r2[par], gtok_r2[par], cb_r2[par], P_r2[par], A3_r2[par]
                fl = lambda ap: ap.rearrange("p a b -> p (a b)")
                if rs_ <= 4:
                    return
                for c in range(NCH):
                    def fn(pe):
                        inst = None
                        for hp in range(2):
                            pu = c * 2 + hp
                            pe.matmul(PS[0][:, hp * 64:(hp + 1) * 64], lhsT=A3[:, pu, 1, :], rhs=Vstk[:, c, hp, :], start=True, stop=False)
                            inst = pe.matmul(PS[0][:, hp * 64:(hp + 1) * 64], lhsT=fl(AR[:, hp, 0, c, :, :]), rhs=Tb[:, hp, :], start=False, stop=True)
                        return inst
                    kb.op("pe", fn, reads=[A3_r, Vstk_r, Tb_r] + AR_r, writes=[PS_r[0]])
                    kb.op("act", lambda a: a.activation(out=Rs[:, :], in_=PS[0][:, 0:128], func=AF.Copy), reads=[PS_r[0]], writes=[Rs_r, PS_r[0]])

                    def fn(pe):
                        inst = None
                        for hp in range(2):
                            pu = c * 2 + hp
                            inst = pe.matmul(PS[1][:, hp * 64:(hp + 1) * 64], lhsT=Pm[:, pu, :], rhs=Rs[:, hp * 64:(hp + 1) * 64], start=True, stop=True)
                        return inst
                    kb.op("pe", fn, reads=[P_r, Rs_r], writes=[PS_r[1]])
                    kb.op("act", lambda a: a.activation(out=Ub[:, :], in_=PS[1][:, 0:128], func=AF.Copy), reads=[PS_r[1]], writes=[Ub_r, PS_r[1]])

                    def fn(pe):
                        inst = None
                        for hp in range(2):
                            pu = c * 2 + hp
                            pe.matmul(PS[0][:, hp * 64:(hp + 1) * 64], lhsT=fl(AR[:, hp, 1, c, :, :]), rhs=Tb[:, hp, :], start=True, stop=False)
                            pe.matmul(PS[0][:, hp * 64:(hp + 1) * 64], lhsT=A3[:, pu, 0, :], rhs=Ub[:, hp * 64:(hp + 1) * 64], start=False, stop=False)
                            inst = pe.matmul(PS[0][:, hp * 64:(hp + 1) * 64], lhsT=A3[:, pu, 2, :], rhs=Vstk[:, c, hp, :], start=False, stop=True)
                        return inst
                    kb.op("pe", fn, reads=[A3_r, Vstk_r, Tb_r, Ub_r] + AR_r, writes=[PS_r[0]])

                    def fn(pe):
                        inst = None
                        for hp in range(2):
                            pe.matmul(PS[1][:, hp * 64:(hp + 1) * 64], lhsT=Bblk[:, c, hp, :], rhs=Ub[:, hp * 64:(hp + 1) * 64], start=True, stop=False)
                            inst = pe.matmul(PS[1][:, hp * 64:(hp + 1) * 64], lhsT=Kblk[:, c, hp, :], rhs=Vstk[:, c, hp, :], start=False, stop=True)
                        return inst
                    kb.op("pe", fn, reads=[Bblk_r, Kblk_r, Vstk_r, Ub_r], writes=[PS_r[1]])
                    kb.op("dve", lambda v: v.tensor_tensor(out=tmpS[:, :, :], in0=PS[1][:, 0:128].rearrange("p (h d) -> p h d", h=2), in1=Tst[:, :, :], op=ALU.add),
                          reads=[PS_r[1], T_r], writes=[tmpS_r, PS_r[1]])
                    kb.op("dve", lambda v: v.tensor_tensor(out=Tst[:, :, :], in0=tmpS[:, :, :],
                                                          in1=bass.AP(Gam, c * 64 + 63, [[2 * MT, 128], [MT, 2], [0, 64]]), op=ALU.mult),
                          reads=[tmpS_r, Gam_r[0], Gam_r[1]], writes=[T_r])
                    kb.op("act", lambda a: a.activation(out=Tb[:, :, :], in_=Tst[:, :, :], func=AF.Copy), reads=[T_r], writes=[Tb_r])
                    if rs_ <= 5:
                        continue
                    v2_ = lambda ap: ap.rearrange("p (h d) -> p h d", h=2)
                    kb.op("act", lambda a: a.activation(out=ow[:, :], in_=PS[0][:, 0:128], func=AF.Copy), reads=[PS_r[0]], writes=[ow_r, PS_r[0]])
                    kb.op("act", lambda a: a.activation(out=ow2[:, :], in_=ow[:, :], func=AF.Square), reads=[ow_r], writes=[ow2_r])
                    kb.op("dve", lambda v: v.reduce_sum(out=st[:, 0:2], in_=v2_(ow[:, :]), axis=AX.X), reads=[ow_r], writes=[st_r])
                    kb.op("dve", lambda v: v.reduce_sum(out=st[:, 2:4], in_=v2_(ow2[:, :]), axis=AX.X), reads=[ow2_r, st_r], writes=[st_r])
                    kb.op("dve", lambda v: v.tensor_scalar(out=st[:, 4:8], in0=st[:, 0:4], scalar1=1.0 / 64.0, scalar2=None, op0=ALU.mult), reads=[st_r], writes=[st_r])
                    kb.op("dve", lambda v: v.tensor_tensor(out=st[:, 8:10], in0=st[:, 4:6], in1=st[:, 4:6], op=ALU.mult), reads=[st_r], writes=[st_r])
                    kb.op("dve", lambda v: v.tensor_tensor(out=st[:, 10:12], in0=st[:, 6:8], in1=st[:, 8:10], op=ALU.subtract), reads=[st_r], writes=[st_r])
                    kb.op("act", lambda a: a.activation(out=st[:, 12:14], in_=st[:, 10:12], func=AF.Sqrt, bias=RW_EPS, scale=1.0), reads=[st_r], writes=[st_r])
                    kb.op("dve", lambda v: v.reciprocal(out=st[:, 14:16], in_=st[:, 12:14]), reads=[st_r], writes=[st_r])
                    for hp in range(2):
                        kb.op("dve", lambda v: v.tensor_scalar(out=ow[:, hp * 64:(hp + 1) * 64], in0=ow[:, hp * 64:(hp + 1) * 64],
                                                               scalar1=st[:, 4 + hp:5 + hp], scalar2=st[:, 14 + hp:15 + hp],
                                                               op0=ALU.subtract, op1=ALU.mult), reads=[ow_r, st_r], writes=[ow_r])
                    kb.op("dve", lambda v: v.tensor_tensor(out=v2_(ow[:, :]), in0=v2_(ow[:, :]), in1=rln[:, 0, :, :], op=ALU.mult), reads=[ow_r, p_r], writes=[ow_r])
                    kb.op("dve", lambda v: v.tensor_tensor(out=v2_(ow[:, :]), in0=v2_(ow[:, :]), in1=rln[:, 1, :, :], op=ALU.add), reads=[ow_r, p_r], writes=[ow_r])
                    if rs_ <= 6:
                        continue
                    for hp in range(2):
                        kb.op("dve", lambda v: v.scalar_tensor_tensor(out=ow[:, hp * 64:(hp + 1) * 64], in0=Vstk[:, c, hp, :], scalar=cb[:, c, hp:hp + 1],
                                                                      in1=ow[:, hp * 64:(hp + 1) * 64], op0=ALU.mult, op1=ALU.add),
                              reads=[Vstk_r, cb_r, ow_r], writes=[ow_r])
                    for hh in range(2):
                        pr = slice(hh * 64, hh * 64 + 64)
                        kb.op("dve", lambda v: v.tensor_tensor(out=obblk[pr, :, hh * 64:(hh + 1) * 64], in0=v2_(ow[pr, :]), in1=gtok[pr, c, :, :], op=ALU.mult),
                              reads=[ow_r, gtok_r], writes=[ob_r])

                    if rs_ <= 7:
                        continue

                    def fn(pe):
                        inst = None
                        for hp in range(2):
                            inst = pe.matmul(PS[1][:, 256 + hp * 64:256 + (hp + 1) * 64], lhsT=obblk[:, hp, :], rhs=istk[:, :], start=True, stop=True)
                        return inst
                    kb.op("pe", fn, reads=[ob_r, p_r], writes=[PS_r[1]])
                    tokc = tok0 + c * 64
                    kb.op("act", lambda a: a.activation(out=OT[:, 2:4, tokc:tokc + 64], in_=PS[1][:, 256:384].rearrange("p (h t) -> p h t", h=2), func=AF.Copy),
                          reads=[PS_r[1]], writes=[OT_r[tokc // 128], PS_r[1]])
            nmt = S // MT if rs_ > 0 else 0
            if nmt:
                prep(0)
            for mt in range(nmt):
                if mt + 1 < nmt:
                    prep(mt + 1)
                chain(mt)
            kb.barrier()

    NQ, NQS, NKC, NKS, NKW, NVC, NVS, NGT = 0, 512, 1024, 1280, 1536, 1792, 1920, 2176
    GK = 1.5957691216057308

    def nsa_phase(sq, l, OT, OT_r):
        ns_ = cfg.get("nsa_stop", 99)
        with ExitStack() as es:
            A = lambda n_, s_, d_: es.enter_context(sb(n_, s_, d_))
            QT = A("nQT", [128, 4, S], BF16)
            KTz = A("nKTz", [128, 3, 2, S], BF16)
            VCz = A("nVCz", [128, 2, S], BF16)
            VS = A("nVS", [128, NT, 2, 65], BF16)
            VW = A("nVW", [128, NT, 2, 65], BF16)
            gsig = A("ngsig", [128, NT, 24], F32)
            kcmpTz = A("nkcmp", [128, 2, 128], BF16)
            vcmp = A("nvcmp", [128, 2, 97], BF16)
            QT_r, KT_r, VC_r, VS_r, VW_r, gs_r = regs(4), regs(4), regs(4), regs(NT), regs(NT), regs(NT)
            kc_r, vcm_r = Reg(), Reg()
            kb.op("pool", lambda g: g.memset(KTz[:], 0.0), writes=KT_r)
            kb.op("pool", lambda g: g.memset(VCz[:], 0.0), writes=VC_r)
            kb.op("pool", lambda g: g.memset(VS[:, :, :, 64:65], 1.0), writes=VS_r)
            kb.op("pool", lambda g: g.memset(VW[:, :, :, 64:65], 1.0), writes=VW_r)
            kb.op("pool", lambda g: g.memset(kcmpTz[:], 0.0), writes=[kc_r])
            kb.op("pool", lambda g: g.memset(vcmp[:], 0.0), writes=[vcm_r])
            with ExitStack() as es1:
                A1 = lambda n_, s_, d_: es1.enter_context(sb(n_, s_, d_))
                Wn = A1("nWn", [128, 8, 2200], BF16)
                rp = A1("nrp", [128, 2, 512], F32)
                t1 = A1("nt1", [128, 512], F32)
                t2 = A1("nt2", [128, 512], F32)
                W_r, rp_r, t1_r, t2_r = Reg(), Reg(), Reg(), Reg()
                load_w(Wn, w_nsa16.ap()[l], 2200, W_r)
                for mt in range(4):
                    tok0 = mt * 512
                    bl = slice(tok0, tok0 + 512)
                    xr = XT_r[mt * 4:mt * 4 + 4]
                    kb.dma("sp", rp[:, 0, :], c_rope.ap()[0][:, bl], writes=[rp_r])
                    kb.dma("sp", rp[:, 1, :], c_rope.ap()[1][:, bl], writes=[rp_r])
                    for i in range(4):
                        proj_fm(PS[0][:, :], Wn, NQ + i * 128, 128, tok0, 512, [W_r], xr, PS_r[0])
                        proj_fm(PS[1][:, :], Wn, NQS + i * 128, 128, tok0, 512, [W_r], xr, PS_r[1])
                        kb.op("dve", lambda v: v.tensor_tensor(out=t1[:, :], in0=PS[0][:, :], in1=rp[:, 0, :], op=ALU.mult),
                              reads=[PS_r[0], rp_r], writes=[t1_r])
                        kb.op("dve", lambda v: v.scalar_tensor_tensor(out=t2[:, :], in0=PS[1][:, :], scalar=0.125, in1=rp[:, 1, :],
                                                                      op0=ALU.mult, op1=ALU.mult), reads=[PS_r[1], rp_r], writes=[t2_r])
                        kb.op("dve", lambda v: v.scalar_tensor_tensor(out=QT[:, i, bl], in0=t1[:, :], scalar=0.125, in1=t2[:, :],
                                                                      op0=ALU.mult, op1=ALU.add), reads=[t1_r, t2_r], writes=[QT_r[mt]])
                    for ty, c0 in ((0, NKC), (1, NKS), (2, NKW)):
                        proj_fm(PS[0][:, :], Wn, c0, 128, tok0, 512, [W_r], xr, PS_r[0])
                        proj_fm(PS[1][:, :], Wn, c0 + 128, 128, tok0, 512, [W_r], xr, PS_r[1])
                        kb.op("dve", lambda v: v.tensor_tensor(out=t1[:, :], in0=PS[0][:, :], in1=rp[:, 0, :], op=ALU.mult),
                              reads=[PS_r[0], rp_r], writes=[t1_r])
                        kb.op("dve", lambda v: v.tensor_tensor(out=t2[:, :], in0=PS[1][:, :], in1=rp[:, 1, :], op=ALU.mult),
                              reads=[PS_r[1], rp_r], writes=[t2_r])
                        for g in range(2):
                            pr = slice(g * 64, g * 64 + 64)
                            kb.op("dve", lambda v: v.tensor_tensor(out=KTz[pr, ty, g, bl], in0=t1[pr, :], in1=t2[pr, :], op=ALU.add),
                                  reads=[t1_r, t2_r], writes=[KT_r[mt]])
                    proj_fm(PS[2][:, :], Wn, NVC, 128, tok0, 512, [W_r], xr, PS_r[2])
                    for g in range(2):
                        pr = slice(g * 64, g * 64 + 64)
                        kb.op("act", lambda a: a.activation(out=VCz[pr, g, bl], in_=PS[2][pr, :], func=AF.Copy),
                              reads=[PS_r[2]], writes=[VC_r[mt]])
                    for j in range(4):
                        tt = mt * 4 + j
                        proj_tm(PS[3][:, 0:256], Wn, NVS, 256, tt, [W_r], PS_r[3])
                        kb.op("act", lambda a: a.activation(out=VS[:, tt, :, 0:64], in_=PS[3][:, 0:128].rearrange("p (g d) -> p g d", g=2), func=AF.Copy),
                              reads=[PS_r[3]], writes=[VS_r[tt], PS_r[3]])
                        kb.op("act", lambda a: a.activation(out=VW[:, tt, :, 0:64], in_=PS[3][:, 128:256].rearrange("p (g d) -> p g d", g=2), func=AF.Copy),
                              reads=[PS_r[3]], writes=[VW_r[tt], PS_r[3]])
                        proj_tm(PS[4][:, 0:24], Wn, NGT, 24, tt, [W_r], PS_r[4])
                        kb.op("act", lambda a: a.activation(out=gsig[:, tt, :], in_=PS[4][:, 0:24], func=AF.Sigmoid),
                              reads=[PS_r[4]], writes=[gs_r[tt]])
                kb.barrier()
            if ns_ <= 1:
                kb.barrier()
                return
            with ExitStack() as es2:
                A2 = lambda n_, s_, d_: es2.enter_context(sb(n_, s_, d_))
                w1d = A2("nw1d", [128, 32, 256], BF16)
                w2d = A2("nw2d", [128, 2, 128], BF16)
                wv2 = A2("nwv2", [128, 2, 64], BF16)
                posz = A2("nposz", [128, 2, 32], BF16)
                hc = A2("nhc", [128, 2], F32)
                gx = A2("ngx", [128, 128], F32)
                gw = A2("ngw", [128, 128], F32)
                gh = A2("ngh", [128, 2, 128], BF16)
                w1_r, w2_r, hc_r, gx_r, gw_r, gh_r = Reg(), Reg(), Reg(), Reg(), Reg(), Reg()
                kb.op("pool", lambda g: g.memset(posz[:], 0.0), writes=[w2_r])
                kb.op("pool", lambda g: g.memset(gh[:], 0.0), writes=[gh_r])
                for kv in range(2):
                    kb.dma("pool", posz[0:64, kv, :], nsa_posT.ap()[l, kv], reads=[w2_r], writes=[w2_r])
                for half in range(2):
                    kb.dma("pool", w2d[:, :, half * 64:(half + 1) * 64], nsa_wk2.ap()[l].rearrange("(t p) n -> p t n", p=128), writes=[w2_r])
                kb.dma("pool", wv2[:, :, :], nsa_wv2.ap()[l].rearrange("(t p) n -> p t n", p=128), writes=[w2_r])
                kb.dma("pool", vcmp[:, 0, 65:97], c_ovl.ap(), reads=[vcm_r], writes=[vcm_r])
                kb.dma("pool", vcmp[:, 1, 65:97], c_ovl.ap(), reads=[vcm_r], writes=[vcm_r])
                kb.op("pool", lambda g: g.memset(vcmp[0:127, :, 64:65], 1.0), reads=[vcm_r], writes=[vcm_r])
                for kv, w1src in ((0, wk1_16), (1, wv1_16)):
                    src3 = w1src.ap()[l].rearrange("(l d) n -> d l n", d=64)
                    for half in range(2):
                        for l4 in range(8):
                            kb.dma("sp", w1d[half * 64:(half + 1) * 64, l4 * 4:(l4 + 1) * 4, :], src3[:, l4 * 4:(l4 + 1) * 4, :],
                                   reads=[w16_r], writes=[w1_r])
                    srcz = (lambda g: KTz[:, 0, g, :]) if kv == 0 else (lambda g: VCz[:, g, :])
                    src_regs = KT_r if kv == 0 else VC_r
                    for hf in range(2):
                        def fn(pe):
                            inst = None
                            for ll in range(32):
                                inst = pe.matmul(PS[0][:, 0:1], lhsT=w1d[:, ll, hf * 128:(hf + 1) * 128], rhs=posz[:, kv, ll:ll + 1],
                                                 start=(ll == 0), stop=(ll == 31))
                            return inst
                        kb.op("pe", fn, reads=[w1_r, w2_r], writes=[PS_r[0]])
                        kb.op("act", lambda a: a.activation(out=hc[:, hf:hf + 1], in_=PS[0][:, 0:1], func=AF.Copy), reads=[PS_r[0]], writes=[hc_r])
                    for g in range(2):
                        for hf in range(2):
                            def fn(pe):
                                inst = None
                                for ll in range(32):
                                    rhs = bass.AP(srcz(g).tensor, srcz(g).offset + ll, [srcz(g).ap[0], [16, 127]])
                                    inst = pe.matmul(PS[1][:, 0:127], lhsT=w1d[:, ll, hf * 128:(hf + 1) * 128], rhs=rhs,
                                                     start=(ll == 0), stop=(ll == 31))
                                return inst
                            kb.op("pe", fn, reads=[w1_r] + src_regs, writes=[PS_r[1]])
                            kb.op("act", lambda a: a.activation(out=gx[:, 0:127], in_=PS[1][:, 0:127], func=AF.Identity, bias=hc[:, hf:hf + 1], scale=1.0),
                                  reads=[PS_r[1], hc_r], writes=[gx_r])
                            kb.op("dve", lambda v: v.tensor_tensor(out=gw[:, 0:127], in0=gx[:, 0:127], in1=gx[:, 0:127], op=ALU.mult),
                                  reads=[gx_r], writes=[gw_r])
                            kb.op("dve", lambda v: v.tensor_scalar(out=gw[:, 0:127], in0=gw[:, 0:127], scalar1=0.044715, scalar2=1.0,
                                                                   op0=ALU.mult, op1=ALU.add), reads=[gw_r], writes=[gw_r])
                            kb.op("dve", lambda v: v.tensor_tensor(out=gw[:, 0:127], in0=gw[:, 0:127], in1=gx[:, 0:127], op=ALU.mult),
                                  reads=[gw_r, gx_r], writes=[gw_r])
                            kb.op("act", lambda a: a.activation(out=gw[:, 0:127], in_=gw[:, 0:127], func=AF.Sigmoid, scale=GK), reads=[gw_r], writes=[gw_r])
                            kb.op("dve", lambda v: v.tensor_tensor(out=gh[:, hf, 0:127], in0=gx[:, 0:127], in1=gw[:, 0:127], op=ALU.mult),
                                  reads=[gw_r, gx_r], writes=[gh_r])
                        if kv == 0:
                            def fn(pe):
                                pe.matmul(PS[2][:, 0:127], lhsT=w2d[:, 0, :], rhs=gh[:, 0, 0:127], start=True, stop=False)
                                return pe.matmul(PS[2][:, 0:127], lhsT=w2d[:, 1, :], rhs=gh[:, 1, 0:127], start=False, stop=True)
                            kb.op("pe", fn, reads=[w2_r, gh_r], writes=[PS_r[2]])
                            pr = slice(g * 64, g * 64 + 64)
                            kb.op("act", lambda a: a.activation(out=kcmpTz[pr, g, 0:127], in_=PS[2][pr, 0:127], func=AF.Copy),
                                  reads=[PS_r[2], kc_r], writes=[kc_r])
                        else:
                            def fn(pe):
                                pe.matmul(PS[2][:, 0:64], lhsT=gh[:, 0, :], rhs=wv2[:, 0, :], start=True, stop=False)
                                return pe.matmul(PS[2][:, 0:64], lhsT=gh[:, 1, :], rhs=wv2[:, 1, :], start=False, stop=True)
                            kb.op("pe", fn, reads=[w2_r, gh_r], writes=[PS_r[2]])
                            kb.op("act", lambda a: a.activation(out=vcmp[0:127, g, 0:64], in_=PS[2][0:127, 0:64], func=AF.Copy),
                                  reads=[PS_r[2], vcm_r], writes=[vcm_r])
                kb.barrier()
            if ns_ <= 2:
                kb.barrier()
                return
            selbT = A("nselbT", [128, 2, S], BF16)
            onehot = A("nonehot", [128, S], BF16)
            cmask = A("ncmask", [128, S], BF16)
            selm = A("nselm", [128, 2, NT, 32], F32)
            wmask = A("nwmask", [128, 128], F32)
            PT = [A("nPT%d" % i, [128, 512], BF16) for i in range(2)]
            PTs = [A("nPTs%d" % i, [128, 512], BF16) for i in range(4)]
            exa = [A("nexa%d" % i, [128, 512], F32) for i in range(2)]
            exa_r = regs(2)
            ocsa = [A("nocsa%d" % i, [128, 4, 97], F32) for i in range(2)]
            ocsa_r = regs(2)
            dna = [A("ndna%d" % i, [128, 16], F32) for i in range(2)]
            dna_r = regs(2)
            scl = [A("nscl%d" % i, [128, 32], F32) for i in range(2)]
            cm3l = [A("ncm3l%d" % i, [128, 32, 32], F32) for i in range(2)]
            sb16l = [A("nsb16l%d" % i, [128, 32], BF16) for i in range(2)]
            scl_r, cm3l_r, sb16l_r = regs(2), regs(2), regs(2)
            PTw = [A("nPTw%d" % i, [128, 640], BF16) for i in range(4)]
            PTw_r = regs(4)
            exw = [A("nexw%d" % i, [128, 128], F32) for i in range(2)]
            exw_r = regs(2)
            ocsw = [A("nocsw%d" % i, [128, 4, 65], F32) for i in range(2)]
            ocsw_r = regs(2)
            dnw = [A("ndnw%d" % i, [128, 16], F32) for i in range(2)]
            dnw_r = regs(2)
            PTs_r = regs(4)
            ocss = [A("nocss%d" % i, [128, 4, 65], F32) for i in range(2)]
            ocss_r = regs(2)
            dns = [A("ndns%d" % i, [128, 16], F32) for i in range(2)]
            dns_r = regs(2)
            ONSA2 = [A("nONSA%d" % i_, [128, 4, 512], F32) for i_ in range(2)]
            impb2 = [A("nimp%d" % i_, [128, 4, 2, 32], F32) for i_ in range(2)]
            obf = A("nobf", [128, 512], BF16)
            k_r, sel_r, PT_r, ex_r, ocs_r, sc_r, cm3_r, sb16_r, dn_r, obf_r = \
                Reg(), regs(4), regs(2), Reg(), Reg(), Reg(), Reg(), Reg(), Reg(), Reg()
            on_r2, imp_r2 = [regs(4), regs(4)], regs(2)
            kb.op("pool", lambda g: g.memset(selbT[:], 0.0), writes=sel_r)
            kb.op("pool", lambda g: g.memset(onehot[:], 0.0), writes=[k_r])
            kb.dma("pool", onehot[0:32, :], c_onehot.ap(), reads=[k_r], writes=[k_r])
            kb.dma("pool", cmask[:, :], c_cmpmask.ap(), writes=[k_r])
            for m_ in range(2):
                kb.dma("sp", selm[:, m_, :, :], c_selm.ap()[m_], writes=[k_r])
            kb.op("dve", lambda v: v.tensor_scalar(out=wmask[:, :], in0=caus[:, 0, :], scalar1=-1.0, scalar2=1.0, op0=ALU.mult, op1=ALU.add),
                  reads=[c_r], writes=[k_r])

            def bf_math(buf, buf_r, dnv, dnv_r, nq, h, b, tts, first, ONSA, on_r):
                kb.op("dve", lambda v: v.tensor_scalar(out=dnv[:, 0:nq], in0=buf[:, 0:nq, 64], scalar1=1e-30, scalar2=None, op0=ALU.max),
                      reads=[buf_r], writes=[dnv_r])
                kb.op("dve", lambda v: v.reciprocal(out=dnv[:, 0:nq], in_=dnv[:, 0:nq]), reads=[dnv_r], writes=[dnv_r])
                kb.op("dve", lambda v: v.tensor_tensor(out=dnv[:, 8:8 + nq], in0=dnv[:, 0:nq], in1=gsig[:, tts[0]:tts[0] + nq, h * 3 + b], op=ALU.mult),
                      reads=[dnv_r] + gs_r[tts[0]:tts[0] + nq], writes=[dnv_r])
                for qi in range(nq):
                    tl = tts[qi] % 4
                    if first:
                        kb.op("dve", lambda v: v.tensor_scalar(out=ONSA[:, tl, h * 64:(h + 1) * 64], in0=buf[:, qi, 0:64], scalar1=dnv[:, 8 + qi:9 + qi],
                                                               scalar2=None, op0=ALU.mult), reads=[buf_r, dnv_r], writes=[on_r[tl]])
                    else:
                        kb.op("dve", lambda v: v.scalar_tensor_tensor(out=ONSA[:, tl, h * 64:(h + 1) * 64], in0=buf[:, qi, 0:64], scalar=dnv[:, 8 + qi:9 + qi],
                                                                      in1=ONSA[:, tl, h * 64:(h + 1) * 64], op0=ALU.mult, op1=ALU.add),
                              reads=[buf_r, dnv_r, on_r[tl]], writes=[on_r[tl]])

            def nsa_a(qb):
                qc = slice(qb * 512, (qb + 1) * 512)
                tts = list(range(qb * 4, qb * 4 + 4))
                ONSA, on_r, impb, imp_r = ONSA2[qb % 2], on_r2[qb % 2], impb2[qb % 2], imp_r2[qb % 2]
                for h in range(8):
                    g, i = h // 4, h % 4
                    pi = h % 2
                    kb.op("pe", lambda pe: pe.matmul(PS[pi][:, :], lhsT=kcmpTz[:, g, :], rhs=QT[:, i, qc], start=True, stop=True),
                          reads=[kc_r, QT_r[qb]], writes=[PS_r[pi]])
                    kb.op("act", lambda a: a.activation(out=exa[pi][:, :], in_=PS[pi][:, :], func=AF.Exp), reads=[PS_r[pi]], writes=[exa_r[pi]])
                    kb.op("dve", lambda v: v.tensor_tensor(out=PT[pi][:, 0:512], in0=exa[pi][:, :], in1=cmask[:, qc], op=ALU.mult),
                          reads=[exa_r[pi], k_r], writes=[PT_r[pi]])
                    ai = 2 + (h % 2)

                    def fn(pe):
                        inst = None
                        for q in range(4):
                            inst = pe.matmul(PS[ai][:, q * 97:(q + 1) * 97], lhsT=PT[pi][:, q * 128:(q + 1) * 128], rhs=vcmp[:, g, :], start=True, stop=True)
                        return inst
                    kb.op("pe", fn, reads=[PT_r[pi], vcm_r], writes=[PS_r[ai]])
                    kb.op("act", lambda a: a.activation(out=ocsa[pi][:, :, :], in_=PS[ai][:, 0:388].rearrange("p (q w) -> p q w", q=4), func=AF.Copy),
                          reads=[PS_r[ai]], writes=[ocsa_r[pi], PS_r[ai]])
                    bf_math(ocsa[pi], ocsa_r[pi], dna[pi], dna_r[pi], 4, h, 0, tts, True, ONSA, on_r)
                    for q in range(4):
                        if i == 0:
                            kb.op("dve", lambda v: v.tensor_scalar(out=impb[:, q, g, :], in0=ocsa[pi][:, q, 65:97], scalar1=dna[pi][:, q:q + 1], scalar2=None, op0=ALU.mult),
                                  reads=[ocsa_r[pi], dna_r[pi]], writes=[imp_r])
                        else:
                            kb.op("dve", lambda v: v.scalar_tensor_tensor(out=impb[:, q, g, :], in0=ocsa[pi][:, q, 65:97], scalar=dna[pi][:, q:q + 1], in1=impb[:, q, g, :],
                                                                          op0=ALU.mult, op1=ALU.add), reads=[ocsa_r[pi], dna_r[pi], imp_r], writes=[imp_r])
            def nsa_b(qb):
                qc = slice(qb * 512, (qb + 1) * 512)
                tts = list(range(qb * 4, qb * 4 + 4))
                ONSA, on_r, impb, imp_r = ONSA2[qb % 2], on_r2[qb % 2], impb2[qb % 2], imp_r2[qb % 2]
                for q in range(4):
                    tt = tts[q]
                    for g in range(2):
                        sc, cm3, sb16, sc_r, cm3_r, sb16_r = scl[g], cm3l[g], sb16l[g], scl_r[g], cm3l_r[g], sb16l_r[g]
                        pbi = g
                        kb.op("dve", lambda v: v.tensor_tensor(out=sc[:, :], in0=impb[:, q, g, :], in1=selm[:, 0, tt, :], op=ALU.mult),
                              reads=[imp_r, k_r], writes=[sc_r])
                        kb.op("dve", lambda v: v.tensor_tensor(out=sc[:, :], in0=sc[:, :], in1=selm[:, 1, tt, :], op=ALU.add),
                              reads=[sc_r, k_r], writes=[sc_r])
                        kb.op("dve", lambda v: v.tensor_tensor(out=cm3[:, :, :], in0=bass.AP(sc, 0, [[32, 128], [0, 32], [1, 32]]),
                                                              in1=bass.AP(sc, 0, [[32, 128], [1, 32], [0, 32]]), op=ALU.is_gt),
                              reads=[sc_r], writes=[cm3_r])
                        kb.op("dve", lambda v: v.reduce_sum(out=sc[:, :], in_=cm3[:, :, :], axis=AX.X), reads=[cm3_r, sc_r], writes=[sc_r])
                        kb.op("dve", lambda v: v.tensor_scalar(out=sb16[:, :], in0=sc[:, :], scalar1=15.5, scalar2=-30000.0, op0=ALU.is_gt, op1=ALU.mult),
                              reads=[sc_r], writes=[sb16_r])
                        kb.op("pe", lambda pe: pe.transpose(out=PB[pbi][0:32, 0:128], in_=sb16[:, :], identity=ident[:, :]),
                              reads=[sb16_r, c_r], writes=[PB_r[pbi]])
                        kb.op("act", lambda a: a.activation(out=selbT[0:32, g, tt * 128:(tt + 1) * 128], in_=PB[pbi][0:32, 0:128], func=AF.Copy),
                              reads=[PB_r[pbi]], writes=[sel_r[qb], PB_r[pbi]])
            def nsa_c(qb):
                qc = slice(qb * 512, (qb + 1) * 512)
                tts = list(range(qb * 4, qb * 4 + 4))
                ONSA, on_r, impb, imp_r = ONSA2[qb % 2], on_r2[qb % 2], impb2[qb % 2], imp_r2[qb % 2]
                for h in range(8):
                    g, i = h // 4, h % 4
                    nkt = 4 * qb + 4
                    for kt in range(nkt):
                        kc_ = slice(kt * 128, (kt + 1) * 128)
                        pi = kt % 2

                        def fn(pe):
                            pe.matmul(PS[pi][:, :], lhsT=KTz[:, 1, g, kc_], rhs=QT[:, i, qc], start=True, stop=False)
                            return pe.matmul(PS[pi][:, :], lhsT=onehot[:, kc_], rhs=selbT[:, g, qc], start=False, stop=True)
                        kb.op("pe", fn, reads=[KT_r[kt // 4], QT_r[qb], k_r, sel_r[qb]], writes=[PS_r[pi]])
                        pt = kt % 4
                        kb.op("act", lambda a: a.activation(out=PTs[pt][:, 0:512], in_=PS[pi][:, :], func=AF.Exp), reads=[PS_r[pi]], writes=[PTs_r[pt], PS_r[pi]])
                        if kt >= 4 * qb:
                            ql = kt - 4 * qb
                            kb.op("dve", lambda v: v.tensor_tensor(out=PTs[pt][:, ql * 128:(ql + 1) * 128], in0=PTs[pt][:, ql * 128:(ql + 1) * 128],
                                                                  in1=caus[:, 0, :], op=ALU.mult), reads=[PTs_r[pt], c_r], writes=[PTs_r[pt]])
                        for q in range(4):
                            qt = 4 * qb + q
                            if qt < kt:
                                continue
                            kb.op("pe", lambda pe: pe.matmul(PS[2 + q][:, 0:65], lhsT=PTs[pt][:, q * 128:(q + 1) * 128], rhs=VS[:, kt, g, :],
                                                             start=(kt == 0), stop=(kt == qt)), reads=[PTs_r[pt], VS_r[kt]], writes=[PS_r[2 + q]])
                    hb = h % 2
                    for q in range(4):
                        kb.op("act", lambda a: a.activation(out=ocss[hb][:, q, :], in_=PS[2 + q][:, 0:65], func=AF.Copy),
                              reads=[PS_r[2 + q]], writes=[ocss_r[hb], PS_r[2 + q]])
                    bf_math(ocss[hb], ocss_r[hb], dns[hb], dns_r[hb], 4, h, 1, tts, False, ONSA, on_r)
            def nsa_d(qb):
                qc = slice(qb * 512, (qb + 1) * 512)
                tts = list(range(qb * 4, qb * 4 + 4))
                ONSA, on_r, impb, imp_r = ONSA2[qb % 2], on_r2[qb % 2], impb2[qb % 2], imp_r2[qb % 2]
                for h in range(8):
                    g, i = h // 4, h % 4
                    for q in range(4):
                        qt = 4 * qb + q
                        qcs = slice(qt * 128, (qt + 1) * 128)
                        kts = [kt for kt in range(qt - 4, qt + 1) if kt >= 0]
                        pi = q % 2
                        pw = q % 4
                        main = [kt for kt in kts if kt >= qt - 3]

                        def fn(pe):
                            inst = None
                            for n_, kt in enumerate(main):
                                inst = pe.matmul(PS[pi][:, n_ * 128:(n_ + 1) * 128], lhsT=KTz[:, 2, g, kt * 128:(kt + 1) * 128], rhs=QT[:, i, qcs],
                                                 start=True, stop=True)
                            return inst
                        kb.op("pe", fn, reads=KT_r + [QT_r[qb]], writes=[PS_r[pi]])
                        nm = len(main)
                        kb.op("act", lambda a: a.activation(out=PTw[pw][:, 0:nm * 128], in_=PS[pi][:, 0:nm * 128], func=AF.Exp), reads=[PS_r[pi]], writes=[PTw_r[pw]])
                        kb.op("dve", lambda v: v.tensor_tensor(out=PTw[pw][:, (nm - 1) * 128:nm * 128], in0=PTw[pw][:, (nm - 1) * 128:nm * 128],
                                                              in1=caus[:, 0, :], op=ALU.mult), reads=[PTw_r[pw], c_r], writes=[PTw_r[pw]])
                        tail = (qt - 4 >= 0)
                        if tail:
                            kt = qt - 4
                            kb.op("pe", lambda pe: pe.matmul(PS[2 + pi][:, 0:128], lhsT=KTz[:, 2, g, kt * 128:(kt + 1) * 128], rhs=QT[:, i, qcs],
                                                             start=True, stop=True), reads=KT_r + [QT_r[qb]], writes=[PS_r[2 + pi]])
                            kb.op("act", lambda a: a.activation(out=exw[pi][:, 0:128], in_=PS[2 + pi][:, 0:128], func=AF.Exp), reads=[PS_r[2 + pi]], writes=[exw_r[pi]])
                            kb.op("dve", lambda v: v.tensor_tensor(out=PTw[pw][:, 512:640], in0=exw[pi][:, 0:128], in1=wmask[:, :], op=ALU.mult),
                                  reads=[exw_r[pi], k_r, PTw_r[pw]], writes=[PTw_r[pw]])

                        def fn(pe):
                            inst = None
                            seq_ = [(n_, kt) for n_, kt in enumerate(main)] + ([(4, qt - 4)] if tail else [])
                            for idx, (n_, kt) in enumerate(seq_):
                                inst = pe.matmul(PS[4][:, q * 65:(q + 1) * 65], lhsT=PTw[pw][:, n_ * 128:(n_ + 1) * 128], rhs=VW[:, kt, g, :],
                                                 start=(idx == 0), stop=(idx == len(seq_) - 1))
                            return inst
                        kb.op("pe", fn, reads=[PTw_r[pw]] + VW_r[max(0, qt - 4):qt + 1], writes=[PS_r[4]])
                    hb = h % 2
                    kb.op("act", lambda a: a.activation(out=ocsw[hb][:, :, :], in_=PS[4][:, 0:260].rearrange("p (q w) -> p q w", q=4), func=AF.Copy),
                          reads=[PS_r[4]], writes=[ocsw_r[hb], PS_r[4]])
                    bf_math(ocsw[hb], ocsw_r[hb], dnw[hb], dnw_r[hb], 4, h, 2, tts, False, ONSA, on_r)
            def nsa_e(qb):
                qc = slice(qb * 512, (qb + 1) * 512)
                tts = list(range(qb * 4, qb * 4 + 4))
                ONSA, on_r, impb, imp_r = ONSA2[qb % 2], on_r2[qb % 2], impb2[qb % 2], imp_r2[qb % 2]
                for q in range(4):
                    tt = tts[q]
                    kb.op("act", lambda a: a.activation(out=obf[:, :], in_=ONSA[:, q, :], func=AF.Copy), reads=[on_r[q]], writes=[obf_r])
                    for half in range(2):
                        out_to_OT(obf[:, half * 256:(half + 1) * 256], obf_r, 128, OT, OT_r, 4 + 2 * half, tt * 128)
            for qb in range(4):
                if qb == 0:
                    nsa_a(0)
                if qb + 1 < 4:
                    nsa_a(qb + 1)
                if ns_ > 3:
                    nsa_b(qb)
                if ns_ > 5:
                    nsa_d(qb)
                if ns_ > 4:
                    nsa_c(qb)
                if ns_ > 6:
                    nsa_e(qb)
            kb.barrier()

    for sq in range(NSEQ):
        for l in range(NLAY):
            if l == 0:
                with sb("xstage", [128, 2, D], F32) as xst:
                    xst_r = regs(2)
                    for tt in range(NT):
                        i = tt % 2
                        kb.dma("sp", xst[:, i, :], x_d.ap()[sq, tt * 128:(tt + 1) * 128, :], writes=[xst_r[i]])
                        make_xT(xst[:, i, :], xst_r[i], tt)
                    kb.barrier()
            if cfg.get("stop") == "xt":
                continue
            res_src = (lambda tt: x_d.ap()[sq, tt * 128:(tt + 1) * 128, :]) if l == 0 else \
                      (lambda tt: xres[1].ap()[tt * 128:(tt + 1) * 128, :])
            res_regs = None if l == 0 else xres_r[1]

            with sb("OT", [128, 8, S], BF16) as OT:
                OT_r = regs(NT)
                if inject_O:
                    for k in range(8):
                        kb.dma("pool", OT[:, k, :], dbg_OT.ap()[:, k, :], writes=OT_r)
                if "gla" in mixers:
                    gla_phase(sq, l, OT, OT_r)
                if "rwkv" in mixers:
                    rwkv_phase(sq, l, OT, OT_r)
                if "nsa" in mixers:
                    nsa_phase(sq, l, OT, OT_r)
                if "OT" in dump_d:
                    for k in range(8):
                        with sb("otd", [128, S], F32) as otd:
                            r_ = Reg()
                            kb.op("act", lambda a: a.activation(out=otd[:], in_=OT[:, k, :], func=AF.Copy),
                                  reads=OT_r, writes=[r_])
                            dump("OT", otd[:], r_, idx=k)
                            kb.barrier()
                if cfg.get("stop") == "outproj0":
                    continue
                with sb("Wo", [128, 8, D], BF16) as Wo, \
                        sb("ln1", [128, 2, D], F32) as ln1, \
                        sb("xs1", [128, 2, D], F32) as xs1, \
                        sb("st1", [128, 2, 32], F32) as st1:
                    Wo_r = Reg()
                    ln_r = Reg()
                    xs_r = regs(2)
                    st_r = regs(2)
                    load_w(Wo, w_out16.ap()[l], D, Wo_r)
                    kb.dma("sp", ln1[:, 0, :], bcast_rows(rowtab, l * 5120 + 1024, D), writes=[ln_r])
                    kb.dma("sp", ln1[:, 1, :], bcast_rows(rowtab, l * 5120 + 2048, D), writes=[ln_r])
                    for tt in range(NT):
                        i = tt % 2
                        kb.dma("sp", xs1[:, i, :], res_src(tt), reads=([res_regs[tt]] if res_regs else []), writes=[xs_r[i]])
                        for hf in range(2):
                            def fn(pe, hf=hf):
                                inst = None
                                for k in range(8):
                                    inst = pe.matmul(PS[hf][:, :], lhsT=OT[:, k, tt * 128:(tt + 1) * 128],
                                                     rhs=Wo[:, k, hf * 512:(hf + 1) * 512], start=(k == 0), stop=(k == 7))
                                return inst
                            kb.op("pe", fn, reads=[OT_r[tt], Wo_r], writes=[PS_r[hf]])
                            kb.op("dve", lambda v, hf=hf: v.scalar_tensor_tensor(
                                out=xs1[:, i, hf * 512:(hf + 1) * 512], in0=xs1[:, i, hf * 512:(hf + 1) * 512], scalar=ALPHA,
                                in1=PS[hf][:, :], op0=ALU.mult, op1=ALU.add), reads=[PS_r[hf], xs_r[i]], writes=[xs_r[i]])
                        layer_norm(xs1[:, i, :], xs_r[i], ln1[:, 0, :], ln1[:, 1, :], ln_r, st1[:, i, :], st_r[i])
                        kb.dma("sp", xres[0].ap()[tt * 128:(tt + 1) * 128, :], xs1[:, i, :], reads=[xs_r[i]], writes=[xres_r[0][tt]])
                        make_xT(xs1[:, i, :], xs_r[i], tt)
                        if "x1" in dump_d and sq == 0 and l == cfg.get("dump_layer", 0):
                            dump("x1", xs1[:, i, :], xs_r[i], idx=tt)
                    kb.barrier()
            if cfg.get("stop") in ("outproj", "outproj0"):
                continue
            with sb("aT", [128, NFC, 1024], BF16) as aT, \
                    sb("Wd", [128, NFC, D], BF16) as Wd, \
                    sb("Wgu", [128, 2, 2, 8, 512], BF16) as Wgu, \
                    sb("sg", [128, 2, 512], F32) as sg, \
                    sb("ln2", [128, 2, D], F32) as ln2, \
                    sb("xs2", [128, 2, D], F32) as xs2, \
                    sb("st2", [128, 2, 32], F32) as st2:
                Wd_r = Reg()
                ln_r = Reg()
                aT_r = regs(NFC)
                Wgu_r = regs(2)
                sg_r = regs(2)
                xs_r = regs(2)
                st_r = regs(2)
                kb.dma("sp", ln2[:, 0, :], bcast_rows(rowtab, l * 5120 + 3072, D), writes=[ln_r])
                kb.dma("sp", ln2[:, 1, :], bcast_rows(rowtab, l * 5120 + 4096, D), writes=[ln_r])
                for k in range(NFC):
                    for c0 in (0, 512):
                        kb.dma("sp", Wd[:, k, c0:c0 + 512], w_down16.ap()[l, k * 128:(k + 1) * 128, c0:c0 + 512], reads=[w16_r], writes=[Wd_r])
                last = (l == NLAY - 1)
                for mt in range(2):
                    tok0 = mt * 1024
                    for hc in range(NFC):
                        cg, ci = hc // 4, hc % 4
                        wi = cg % 2
                        if ci == 0:
                            ncol = min(512, FF - cg * 512)
                            for gu, wsrc in ((0, w_gate16), (1, w_up16)):
                                for k in range(8):
                                    kb.dma("sp", Wgu[:, wi, gu, k, 0:ncol],
                                           wsrc.ap()[l, k * 128:(k + 1) * 128, cg * 512:cg * 512 + ncol],
                                           reads=[w16_r], writes=[Wgu_r[wi]])
                        for blk in range(2):
                            t0 = tok0 + blk * 512
                            pg, pu = (0, 1) if blk == 0 else (2, 3)
                            for gu, pi in ((0, pg), (1, pu)):
                                def fn(pe, gu=gu, pi=pi):
                                    inst = None
                                    for k in range(8):
                                        inst = pe.matmul(PS[pi][:, :], lhsT=Wgu[:, wi, gu, k, ci * 128:(ci + 1) * 128],
                                                         rhs=XT[:, k, t0:t0 + 512], start=(k == 0), stop=(k == 7))
                                    return inst
                                kb.op("pe", fn, reads=[Wgu_r[wi]] + XT_r[t0 // 128:t0 // 128 + 4], writes=[PS_r[pi]])
                            kb.op("act", lambda a: a.activation(out=sg[:, blk, :], in_=PS[pg][:, :], func=AF.Silu),
                                  reads=[PS_r[pg]], writes=[sg_r[blk]])
                            kb.op("dve", lambda v: v.tensor_tensor(out=aT[:, hc, blk * 512:(blk + 1) * 512], in0=sg[:, blk, :],
                                                                  in1=PS[pu][:, :], op=ALU.mult),
                                  reads=[sg_r[blk], PS_r[pu]], writes=[aT_r[hc]])
                    for t8 in range(8):
                        tt = mt * 8 + t8
                        i = tt % 2
                        kb.dma("sp", xs2[:, i, :], xres[0].ap()[tt * 128:(tt + 1) * 128, :], reads=[xres_r[0][tt]], writes=[xs_r[i]])
                        for hf in range(2):
                            pi = 4 + hf

                            def fn(pe, hf=hf, pi=pi):
                                inst = None
                                for k in range(NFC):
                                    inst = pe.matmul(PS[pi][:, :], lhsT=aT[:, k, t8 * 128:(t8 + 1) * 128],
                                                     rhs=Wd[:, k, hf * 512:(hf + 1) * 512], start=(k == 0), stop=(k == NFC - 1))
                                return inst
                            kb.op("pe", fn, reads=aT_r + [Wd_r], writes=[PS_r[pi]])
                            kb.op("dve", lambda v, hf=hf, pi=pi: v.scalar_tensor_tensor(
                                out=xs2[:, i, hf * 512:(hf + 1) * 512], in0=xs2[:, i, hf * 512:(hf + 1) * 512], scalar=ALPHA,
                                in1=PS[pi][:, :], op0=ALU.mult, op1=ALU.add), reads=[PS_r[pi], xs_r[i]], writes=[xs_r[i]])
                        layer_norm(xs2[:, i, :], xs_r[i], ln2[:, 0, :], ln2[:, 1, :], ln_r, st2[:, i, :], st_r[i])
                        if last:
                            kb.dma("sp", out_d.ap()[sq, tt * 128:(tt + 1) * 128, :], xs2[:, i, :], reads=[xs_r[i]])
                        else:
                            kb.dma("sp", xres[1].ap()[tt * 128:(tt + 1) * 128, :], xs2[:, i, :], reads=[xs_r[i]], writes=[xres_r[1][tt]])
                            make_xT(xs2[:, i, :], xs_r[i], tt)
                        if "x2" in dump_d and sq == 0 and l == cfg.get("dump_layer", 0):
                            dump("x2", xs2[:, i, :], xs_r[i], idx=tt)
                    if not last:
                        pass
                kb.barrier()
    kb.finish()
    return kb


def host_consts():
    c = {}
    c["c_ident"] = np.eye(128, dtype=np.float32)
    s = np.arange(128)
    c["c_caus"] = (s[:, None] <= s[None, :]).astype(np.float32)
    half = 32
    inv = (10000.0 ** (-np.arange(half, dtype=np.float32) / half)).astype(np.float32)
    ang = (np.arange(S, dtype=np.float32)[:, None] * inv[None, :]).astype(np.float32)
    cos = np.cos(ang).astype(np.float32).T
    sin = np.sin(ang).astype(np.float32).T
    cosT = np.concatenate([cos, cos, cos, cos], 0)
    sinT = np.concatenate([-sin, sin, -sin, sin], 0)
    c["c_rope"] = np.stack([cosT, sinT]).astype(np.float32)
    cc = np.arange(128)
    t = np.arange(S)
    c["c_cmpmask"] = ((16 * cc[:, None] + 31 <= t[None, :]) & (cc[:, None] < 127)).astype(np.float32)
    j = np.arange(32)
    c["c_onehot"] = ((t[None, :] // 64) == j[:, None]).astype(np.float32)
    cur = t // 64
    forced = (j[None, :] == 0) | (j[None, :] == cur[:, None]) | (j[None, :] == cur[:, None] - 1)
    future = j[None, :] > cur[:, None]
    m1 = (~forced & ~future).astype(np.float32)
    m2 = np.where(forced, 1e9, np.where(future, -1e9, 0.0)).astype(np.float32)
    selm = np.stack([m1, m2])
    c["c_selm"] = np.ascontiguousarray(selm.reshape(2, NT, 128, 32).transpose(0, 2, 1, 3))
    c0 = np.arange(127) * 16
    s0 = np.arange(32) * 64
    lo = np.maximum(c0[:, None], s0[None, :])
    hi = np.minimum(c0[:, None] + 32, s0[None, :] + 64)
    ov = np.zeros((128, 32), np.float32)
    ov[:127] = np.maximum(hi - lo, 0) / 16
    c["c_ovl"] = ov
    blk = np.zeros((128, 128), np.float32)
    blk[:64, :64] = 1
    blk[64:, 64:] = 1
    c["c_blk"] = blk
    hs = np.zeros((128, 2), np.float32)
    hs[:64, 0] = 1
    hs[64:, 1] = 1
    c["c_hsel"] = hs
    i64 = np.arange(64)
    lo_strict = (i64[None, :] < i64[:, None]).astype(np.float32)
    up_strict = (i64[:, None] < i64[None, :]).astype(np.float32)
    up_incl = (i64[:, None] <= i64[None, :]).astype(np.float32)
    def bd(m):
        o = np.zeros((128, 128), np.float32)
        o[:64, :64] = m
        o[64:, 64:] = m
        return o
    c["c_rwmask2"] = np.ascontiguousarray(np.stack([bd(lo_strict), bd(up_strict), bd(up_incl)], axis=1))
    c["c_istk"] = np.concatenate([np.eye(64, dtype=np.float32), np.eye(64, dtype=np.float32)], axis=0)
    return c


def host_layout(inp):
    d = {}
    f = lambda a: np.ascontiguousarray(np.asarray(a, dtype=np.float32))
    for k in ("w_in", "w_in_vres", "gla_w_a2", "rwkv_w2", "rwkv_a2", "rwkv_v2", "rwkv_g2", "nsa_wk1", "nsa_wk2",
              "nsa_wv1", "nsa_wv2", "w_out", "ffn_w_gate", "ffn_w_up", "ffn_w_down"):
        d[k] = f(inp[k])
    base = 2096
    sw = lambda b: list(range(b + 32, b + 64)) + list(range(b, b + 32))
    pl = lambda b: list(range(b, b + 64))
    cols = []
    for i in range(4):
        cols += pl(base + i * 64) + pl(base + (4 + i) * 64)
    for i in range(4):
        cols += sw(base + i * 64) + sw(base + (4 + i) * 64)
    for c0 in (512, 768, 1024):
        cols += pl(base + c0) + pl(base + c0 + 64)
        cols += sw(base + c0) + sw(base + c0 + 64)
    cols += list(range(base + 640, base + 768)) + list(range(base + 896, base + 1024)) + list(range(base + 1152, base + 1280))
    cols += list(range(base + 1280, base + 1304))
    assert len(cols) == 2200
    d["w_nsa"] = f(np.asarray(inp["w_in"])[:, :, cols])
    d["nsa_posT"] = f(np.stack([np.asarray(inp["nsa_pos_k"]).transpose(0, 2, 1),
                                np.asarray(inp["nsa_pos_v"]).transpose(0, 2, 1)], axis=1))
    pt = np.zeros((L, 128, 32), np.float32)
    mu = np.asarray(inp["rwkv_mu"])
    rwt = [(0, 128), (128, 128), (256, 128), (384, 128), (512, 128), (640, 128), (768, 64), (832, 64), (896, 128), (1024, 32)]
    for l in range(L):
        pt[l, :, 0:2] = np.asarray(inp["gla_b_a"])[l].reshape(2, 128).T
        for i, (c0, n) in enumerate(rwt):
            pt[l, :n, 2 + i] = mu[l, c0:c0 + n]
        if l >= 1:
            pt[l, :32, 12] = np.asarray(inp["rwkv_mu_vres"])[l - 1]
            pt[l, :, 17:19] = np.asarray(inp["rwkv_v0"])[l - 1].reshape(2, 128).T
        pt[l, :, 13:15] = np.asarray(inp["rwkv_w0"])[l].reshape(2, 128).T
        pt[l, :, 15:17] = np.asarray(inp["rwkv_a0"])[l].reshape(2, 128).T
        pt[l, :, 19:21] = np.asarray(inp["rwkv_k_k"])[l].reshape(2, 128).T
        pt[l, :, 21:23] = np.asarray(inp["rwkv_k_a"])[l].reshape(2, 128).T
        pt[l, :, 23:25] = np.asarray(inp["rwkv_r_k"])[l].reshape(2, 128).T
    d["ptab"] = pt
    rt = np.zeros((L, 5120), np.float32)
    for l in range(L):
        rt[l, 0:256] = np.asarray(inp["gla_ln_w"])[l]
        rt[l, 256:512] = np.asarray(inp["gla_ln_b"])[l]
        rt[l, 512:768] = np.asarray(inp["rwkv_ln_w"])[l]
        rt[l, 768:1024] = np.asarray(inp["rwkv_ln_b"])[l]
        rt[l, 1024:2048] = np.asarray(inp["ln1_w"])[l]
        rt[l, 2048:3072] = np.asarray(inp["ln1_b"])[l]
        rt[l, 3072:4096] = np.asarray(inp["ln2_w"])[l]
        rt[l, 4096:5120] = np.asarray(inp["ln2_b"])[l]
    d["rowtab"] = rt
    return d


_CACHE = {}


def kernel(**inputs):
    cfg = {}
    if "full" not in _CACHE:
        _CACHE["full"] = build(cfg)
    kb = _CACHE["full"]
    shared = host_layout(inputs)
    shared.update(host_consts())
    x = np.ascontiguousarray(np.asarray(inputs["x"], dtype=np.float32))
    in_maps = []
    for c in range(8):
        m = dict(shared)
        m["x"] = x[2 * c:2 * c + 2]
        in_maps.append(m)
    res = run_bass_kernel_spmd(kb.nc, in_maps, core_ids=list(range(8)))
    return np.concatenate([r["out"] for r in res.results], axis=0).astype(np.float32)
```

```python
import math
from contextlib import ExitStack
import numpy as np
import concourse.bass as bass
import concourse.mybir as mybir
from concourse.bass_utils import run_bass_kernel_spmd

F32 = mybir.dt.float32
BF16 = mybir.dt.bfloat16
AF = mybir.ActivationFunctionType
ALU = mybir.AluOpType
AX = mybir.AxisListType
Q_DEF = 300.0

S = 2048
D = 1024
NT = S // 128
L = 2
FF = 2816
NFC = FF // 128
ALPHA = float((2 * L) ** 0.25)
LN_EPS = 1e-5
RW_EPS = 64e-5
NDS = 6


class Reg:
    __slots__ = ("lw", "rd")

    def __init__(self):
        self.lw = None
        self.rd = {}


def regs(n):
    return [Reg() for _ in range(n)]


class _Rec:
    def __init__(self):
        self.calls = []

    def __getattr__(self, name):
        def f(*a, **kw):
            self.calls.append((name, a, kw))
            return self
        return f


class _Node:
    __slots__ = ("e", "kind", "calls", "reads", "writes", "cost", "deps")

    def __init__(self, e, kind, calls, reads, writes, cost):
        self.e, self.kind, self.calls, self.reads, self.writes, self.cost = e, kind, calls, reads, writes, cost
        self.deps = ()


def _free_elems(ap):
    try:
        n = 1
        for s_ in list(ap.shape)[1:]:
            n *= int(s_)
        return n
    except Exception:
        return 256


def _ap_bytes(ap):
    try:
        n = 1
        for s_ in list(ap.shape):
            n *= int(s_)
        return n * 4
    except Exception:
        return 65536


def _est_cost(e, calls):
    t = 0.0
    for name, a, kw in calls:
        out = kw.get("out", a[0] if a else None)
        n = _free_elems(out) if out is not None else 256
        if e == "pe":
            mul = 4.0 if (name == "matmul" and getattr(kw.get("lhsT"), "dtype", None) == F32) else 1.0
            t += mul * max(n, 64) / 1.6 + 25.0
        elif e == "act":
            t += n / 1.0 + 220.0
        elif e == "dve":
            t += n / 0.9 + 80.0
        else:
            t += n * 2.0 + 300.0
    return t


class KB:
    def __init__(self, sched=True, W=32, LAT=120.0):
        nc = bass.Bass("TRN2", target_bir_lowering=False)
        self.nc = nc
        self.E = {"pe": nc.tensor, "act": nc.scalar, "dve": nc.vector, "pool": nc.gpsimd, "sp": nc.sync}
        self.sems = {}
        self.cnt = {}
        for e in ("pe", "act", "dve", "pool"):
            self.sems[e] = nc.alloc_semaphore("s_" + e)
            self.cnt[e] = 0
        self.dq = {}
        for q, nds in (("sp", 12), ("pool", NDS), ("act", 6)):
            keys = []
            for i in range(nds):
                k = "d_%s%d" % (q, i)
                self.sems[k] = nc.alloc_semaphore(k)
                self.cnt[k] = 0
                keys.append(k)
            self.dq[q] = [keys, 0]
        self.seen = {e: {} for e in self.E}
        self.nops = 0
        self.sched = sched
        self.W = W
        self.LAT = LAT
        self.use_prio = True
        self.Q = Q_DEF
        self.pending = []

    def _waits(self, e, reads, writes, extra=()):
        need = {}

        def add(rec):
            if rec is None:
                return
            k, c = rec
            if need.get(k, 0) < c:
                need[k] = c

        for r in reads:
            add(r.lw)
        for w in writes:
            add(w.lw)
            for k, c in w.rd.items():
                add((k, c))
        for rec in extra:
            add(rec)
        eng = self.E[e]
        seen = self.seen[e]
        for k, c in need.items():
            if e == "pe" and k == "pe":
                continue
            if seen.get(k, 0) >= c:
                continue
            eng.wait_ge(self.sems[k], c)
            seen[k] = c

    def _mark(self, rec, reads, writes):
        k, c = rec
        for r in reads:
            if r.rd.get(k, 0) < c:
                r.rd[k] = c
        for w in writes:
            w.lw = rec
            w.rd = {}

    def op(self, e, fn, reads=(), writes=()):
        rec = _Rec()
        fn(rec)
        node = _Node(e, "op", rec.calls, list(reads), list(writes), _est_cost(e, rec.calls))
        if self.sched:
            self.pending.append(node)
        else:
            self._emit(node)

    def dma(self, q, out, in_, reads=(), writes=(), **kw):
        node = _Node(q, "dma", (out, in_, kw), list(reads), list(writes), 2000.0 + _ap_bytes(out) / 100.0)
        if self.sched:
            self.pending.append(node)
        else:
            self._emit(node)

    def _emit(self, n):
        e = n.e
        if n.kind == "op":
            self._waits(e, n.reads, n.writes)
            eng = self.E[e]
            inst = None
            for name, a, kw in n.calls:
                inst = getattr(eng, name)(*a, **kw)
            self.cnt[e] += 1
            inst.then_inc(self.sems[e], 1)
            self._mark((e, self.cnt[e]), n.reads, n.writes)
        else:
            out, in_, kw = n.calls
            keys, i = self.dq[e]
            k = keys[i]
            self.dq[e][1] = (i + 1) % len(keys)
            extra = [(k, self.cnt[k])] if self.cnt[k] else []
            self._waits(e, n.reads, n.writes, extra)
            self.E[e].dma_start(out=out, in_=in_, **kw).then_inc(self.sems[k], 16)
            self.cnt[k] += 16
            self._mark((k, self.cnt[k]), n.reads, n.writes)
        self.nops += 1

    def flush(self):
        nodes = self.pending
        self.pending = []
        if not nodes:
            return
        lastw, readers = {}, {}
        for i, n in enumerate(nodes):
            d = set()
            for r in n.reads:
                if id(r) in lastw:
                    d.add(lastw[id(r)])
            for w in n.writes:
                if id(w) in lastw:
                    d.add(lastw[id(w)])
                d.update(readers.get(id(w), ()))
            d.discard(i)
            n.deps = d
            for r in n.reads:
                readers.setdefault(id(r), []).append(i)
            for w in n.writes:
                lastw[id(w)] = i
                readers[id(w)] = []
        prio = [n.cost for n in nodes]
        for i in range(len(nodes) - 1, -1, -1):
            pi_ = prio[i]
            for d in nodes[i].deps:
                v_ = nodes[d].cost + pi_
                if v_ > prio[d]:
                    prio[d] = v_
        queues = {}
        for i, n in enumerate(nodes):
            queues.setdefault(n.e, []).append(i)
        fin = [None] * len(nodes)
        efree = {e: 0.0 for e in queues}
        W, LAT = self.W, self.LAT
        remaining = len(nodes)
        starts_ = {}
        while remaining:
            best = None
            for e, q in queues.items():
                for i in q[:W]:
                    n = nodes[i]
                    ready = 0.0
                    ok = True
                    for d in n.deps:
                        f = fin[d]
                        if f is None:
                            ok = False
                            break
                        if f + LAT > ready:
                            ready = f + LAT
                    if not ok:
                        continue
                    start = ready if ready > efree[e] else efree[e]
                    key = (int(start / self.Q), -prio[i], i) if self.use_prio else (start, i)
                    starts_[i] = start
                    if best is None or key < best[0]:
                        best = (key, e, i)
            assert best is not None
            e, i = best[1], best[2]
            start = starts_[i]
            n = nodes[i]
            fin[i] = start + n.cost
            efree[e] = (start + 60.0) if n.kind == "dma" else fin[i]
            queues[e].remove(i)
            self._emit(n)
            remaining -= 1

    def barrier(self):
        self.flush()
        for e in self.E:
            for k, c in self.cnt.items():
                if c == 0 or (e == "pe" and k == "pe"):
                    continue
                if self.seen[e].get(k, 0) >= c:
                    continue
                self.E[e].wait_ge(self.sems[k], c)
                self.seen[e][k] = c

    def finish(self):
        self.barrier()


def bcast_rows(t, off, n, parts=128):
    return bass.AP(t, off, [[0, parts], [1, n]])


def build(cfg):
    kb = KB(sched=cfg.get("sched", True), W=cfg.get("W", 128), LAT=cfg.get("LAT", 1000.0))
    nc = kb.nc
    NSEQ = cfg.get("nseq", 2)
    NLAY = cfg.get("nlay", 2)
    mixers = cfg.get("mixers", ("gla", "rwkv", "nsa"))
    dumps = cfg.get("dumps", ())
    inject_O = cfg.get("inject_O", False)

    def dram_in(name, shape):
        return nc.dram_tensor(name, list(shape), F32, kind="ExternalInput")

    x_d = dram_in("x", [2, S, D])
    w_in = dram_in("w_in", [L, D, 3400])
    w_vres = dram_in("w_in_vres", [1, D, 32])
    w_nsa = dram_in("w_nsa", [L, D, 2200])
    gla_w_a2 = dram_in("gla_w_a2", [L, 16, 256])
    rwkv_w2 = dram_in("rwkv_w2", [L, 64, 256])
    rwkv_a2 = dram_in("rwkv_a2", [L, 64, 256])
    rwkv_v2 = dram_in("rwkv_v2", [1, 32, 256])
    rwkv_g2 = dram_in("rwkv_g2", [L, 160, 256])
    nsa_wk1 = dram_in("nsa_wk1", [L, 2048, 256])
    nsa_wk2 = dram_in("nsa_wk2", [L, 256, 64])
    nsa_wv1 = dram_in("nsa_wv1", [L, 2048, 256])
    nsa_wv2 = dram_in("nsa_wv2", [L, 256, 64])
    nsa_posT = dram_in("nsa_posT", [L, 2, 64, 32])
    w_out = dram_in("w_out", [L, D, D])
    w_gate = dram_in("ffn_w_gate", [L, D, FF])
    w_up = dram_in("ffn_w_up", [L, D, FF])
    w_down = dram_in("ffn_w_down", [L, FF, D])
    ptab = dram_in("ptab", [L, 128, 32])
    rowtab = dram_in("rowtab", [L, 5120])
    c_ident = dram_in("c_ident", [128, 128])
    c_caus = dram_in("c_caus", [128, 128])
    c_rope = dram_in("c_rope", [2, 128, S])
    c_cmpmask = dram_in("c_cmpmask", [128, S])
    c_onehot = dram_in("c_onehot", [32, S])
    c_selm = dram_in("c_selm", [2, 128, NT, 32])
    c_ovl = dram_in("c_ovl", [128, 32])
    c_blk = dram_in("c_blk", [128, 128])
    c_hsel = dram_in("c_hsel", [128, 2])
    c_rwmask2 = dram_in("c_rwmask2", [128, 3, 128])
    c_istk = dram_in("c_istk", [128, 64])
    if inject_O:
        dbg_OT = dram_in("dbg_OT", [128, 8, S])
    out_d = nc.dram_tensor("out", [2, S, D], F32, kind="ExternalOutput")
    xres = [nc.dram_tensor("xres%d" % i, [S, D], F32, kind="Internal") for i in range(2)]
    xres_r = [regs(NT) for _ in range(2)]
    vfirst_d = nc.dram_tensor("vfirst", [2, 128, S], F32, kind="Internal")
    vfirst_r = regs(2)
    dump_d = {}
    for name, shape in dumps:
        dump_d[name] = nc.dram_tensor("dump_" + name, list(shape), F32, kind="ExternalOutput")

    XT = nc.alloc_sbuf_tensor("XT", [128, 8, S], BF16)
    XT_r = regs(NT)
    ident = nc.alloc_sbuf_tensor("ident", [128, 128], BF16)
    identf = nc.alloc_sbuf_tensor("identf", [128, 128], F32)
    caus = nc.alloc_sbuf_tensor("caus", [128, 4, 128], F32)
    ptb = nc.alloc_sbuf_tensor("ptb", [128, L, 32], F32)
    ptn = nc.alloc_sbuf_tensor("ptn", [128, L, 32], F32)
    c_r = Reg()
    for l in range(L):
        kb.dma("sp", ptb[:, l, :], ptab.ap()[l], writes=[c_r])
    kb.dma("pool", ident[:], c_ident.ap(), writes=[c_r])
    kb.dma("sp", identf[:], c_ident.ap(), writes=[c_r])
    for i in range(4):
        kb.dma("sp", caus[:, i, :], c_caus.ap(), writes=[c_r])

    def dram16(name, shape):
        return nc.dram_tensor(name, list(shape), BF16, kind="Internal")

    conv_list = [(w_in, [L, D, 3400]), (w_nsa, [L, D, 2200]), (w_out, [L, D, D]), (w_gate, [L, D, FF]), (w_up, [L, D, FF]),
                 (w_down, [L, FF, D]), (nsa_wk1, [L, 2048, 256]), (nsa_wv1, [L, 2048, 256])]
    w16 = {}
    w16_r = Reg()
    CH = 2816
    with nc.sbuf_tensor("cv_f", [128, 3, CH], F32) as cvf, nc.sbuf_tensor("cv_b", [128, 3, CH], BF16) as cvb:
        cvf_r, cvb_r = regs(3), regs(3)
        job = 0
        for src, shape in conv_list:
            dst = dram16(src.name + "_16", shape)
            w16[src.name] = dst
            tot = 1
            for d_ in shape:
                tot *= d_
            per = tot // 128
            assert per * 128 == tot
            for c0 in range(0, per, CH):
                n = min(CH, per - c0)
                i = job % 3
                kb.dma("sp", cvf[:, i, 0:n], bass.AP(src, c0, [[per, 128], [1, n]]), writes=[cvf_r[i]])
                eng = "act"
                if eng == "act":
                    kb.op("act", lambda a: a.activation(out=cvb[:, i, 0:n], in_=cvf[:, i, 0:n], func=AF.Copy),
                          reads=[cvf_r[i]], writes=[cvb_r[i]])
                else:
                    kb.op(eng, lambda v: v.tensor_scalar(out=cvb[:, i, 0:n], in0=cvf[:, i, 0:n], scalar1=1.0, scalar2=None, op0=ALU.mult),
                          reads=[cvf_r[i]], writes=[cvb_r[i]])
                kb.dma("act", bass.AP(dst, c0, [[per, 128], [1, n]]), cvb[:, i, 0:n], reads=[cvb_r[i]], writes=[w16_r])
                job += 1
        kb.barrier()
    w_in16, w_nsa16, w_out16 = w16["w_in"], w16["w_nsa"], w16["w_out"]
    w_gate16, w_up16, w_down16 = w16["ffn_w_gate"], w16["ffn_w_up"], w16["ffn_w_down"]
    wk1_16, wv1_16 = w16["nsa_wk1"], w16["nsa_wv1"]

    PS = [nc.alloc_psum_tensor("ps%d" % i, [128, 512], F32) for i in range(6)]
    PS_r = regs(6)
    PB = [nc.alloc_psum_tensor("pb%d" % i, [128, 1024], BF16) for i in range(2)]
    PB_r = regs(2)

    _uid = [0]

    def sb(name, shape, dt):
        _uid[0] += 1
        return nc.sbuf_tensor("%s_%d" % (name, _uid[0]), list(shape), dt)

    def proj_fm(ps_ap, Wt, c0, M, tok0, N, wreads, treads, pw, kparts=8):
        def fn(pe):
            inst = None
            for k in range(kparts):
                inst = pe.matmul(ps_ap, lhsT=Wt[:, k, c0:c0 + M], rhs=XT[:, k, tok0:tok0 + N],
                                 start=(k == 0), stop=(k == kparts - 1))
            return inst
        kb.op("pe", fn, reads=list(wreads) + list(treads), writes=[pw])

    def proj_tm(ps_ap, Wt, c0, N, tt, wreads, pw):
        def fn(pe):
            inst = None
            for k in range(8):
                inst = pe.matmul(ps_ap, lhsT=XT[:, k, tt * 128:(tt + 1) * 128], rhs=Wt[:, k, c0:c0 + N],
                                 start=(k == 0), stop=(k == 7))
            return inst
        kb.op("pe", fn, reads=list(wreads) + [XT_r[tt]], writes=[pw])

    def load_w(Wt, src3, ncols, wreg, q="sp", chunk=None):
        for k in range(8):
            kb.dma(q, Wt[:, k, 0:ncols], src3[k * 128:(k + 1) * 128, 0:ncols], reads=[w16_r], writes=[wreg])

    xb = [nc.alloc_sbuf_tensor("xb%d" % i, [128, D], BF16) for i in range(2)]
    xb_r = regs(2)
    xb_i = [0]

    def make_xT(src_ap, src_reg, tt):
        i = xb_i[0]
        xb_i[0] ^= 1
        kb.op("act", lambda a: a.activation(out=xb[i][:], in_=src_ap, func=AF.Copy),
              reads=[src_reg], writes=[xb_r[i]])
        pb = PB[i]

        def fn(pe):
            inst = None
            for k in range(8):
                inst = pe.transpose(out=pb[:, k * 128:(k + 1) * 128], in_=xb[i][:, k * 128:(k + 1) * 128],
                                    identity=ident[:])
            return inst
        kb.op("pe", fn, reads=[xb_r[i], c_r], writes=[PB_r[i]])
        kb.op("dve", lambda v: v.tensor_copy(out=XT[:, :, tt * 128:(tt + 1) * 128],
                                             in_=pb[:, :].rearrange("p (k t) -> p k t", k=8)),
              reads=[PB_r[i]], writes=[XT_r[tt]])

    def layer_norm(xt_ap, xreg, lnw_ap, lnb_ap, lnreg, st, st_r):
        kb.op("dve", lambda v: v.bn_stats(out=st[:, 0:6], in_=xt_ap[:, 0:512]), reads=[xreg], writes=[st_r])
        kb.op("dve", lambda v: v.bn_stats(out=st[:, 6:12], in_=xt_ap[:, 512:1024]), reads=[xreg, st_r], writes=[st_r])
        kb.op("dve", lambda v: v.bn_aggr(out=st[:, 12:14], in_=st[:, 0:12]), reads=[st_r], writes=[st_r])
        kb.op("act", lambda a: a.activation(out=st[:, 14:15], in_=st[:, 13:14], func=AF.Sqrt, bias=LN_EPS, scale=1.0),
              reads=[st_r], writes=[st_r])
        kb.op("dve", lambda v: v.reciprocal(out=st[:, 15:16], in_=st[:, 14:15]), reads=[st_r], writes=[st_r])
        kb.op("dve", lambda v: v.scalar_tensor_tensor(out=st[:, 16:17], in0=st[:, 12:13], scalar=-1.0, in1=st[:, 15:16],
                                                      op0=ALU.mult, op1=ALU.mult), reads=[st_r], writes=[st_r])
        kb.op("act", lambda a: a.activation(out=xt_ap, in_=xt_ap, func=AF.Identity, bias=st[:, 16:17], scale=st[:, 15:16]),
              reads=[st_r, xreg], writes=[xreg])
        kb.op("dve", lambda v: v.tensor_tensor(out=xt_ap, in0=xt_ap, in1=lnw_ap, op=ALU.mult), reads=[xreg, lnreg], writes=[xreg])
        kb.op("dve", lambda v: v.tensor_tensor(out=xt_ap, in0=xt_ap, in1=lnb_ap, op=ALU.add), reads=[xreg, lnreg], writes=[xreg])

    def dump(name, sb_ap, reg, idx=None):
        if name in dump_d:
            dst = dump_d[name].ap() if idx is None else dump_d[name].ap()[idx]
            kb.dma("sp", dst, sb_ap, reads=[reg])

    kb.op("dve", lambda v: v.tensor_scalar(out=ptn[:, :, :], in0=ptb[:, :, :], scalar1=-1.0, scalar2=None, op0=ALU.mult),
          reads=[c_r], writes=[c_r])
    kb.op("dve", lambda v: v.tensor_scalar(out=ptn[:, :, 2:13], in0=ptb[:, :, 2:13], scalar1=-1.0, scalar2=1.0,
                                           op0=ALU.mult, op1=ALU.add), reads=[c_r], writes=[c_r])

    def gla_phase(sq, l, OT, OT_r):
        with ExitStack() as es:
            A = lambda n_, s_, d_: es.enter_context(sb(n_, s_, d_))
            Wg = A("Wg", [128, 8, 1152], BF16)
            wa2 = A("wa2", [16, 256], BF16)
            gln = A("gln", [128, 2, 256], F32)
            alrT = A("alrT", [16, 512], BF16)
            t1 = A("gt1", [128, 512], F32)
            t2 = A("gt2", [128, 512], F32)
            EB = A("EB", [128, 2, 512], F32)
            QTl = A("gQT", [128, 2, 2, 512], BF16)
            KTl = A("gKT", [128, 2, 512], BF16)
            Ktok = A("gKtok", [128, 4, 256], BF16)
            V = A("gV", [128, 4, 256], BF16)
            SG = A("gSG", [128, 4, 256], F32)
            AT = A("gAT", [128, 4, 128], BF16)
            St = A("gSt", [128, 2, 128], F32)
            tmpS = A("gtmpS", [128, 128], F32)
            blkm = A("gblk", [128, 128], F32)
            Sb = A("gSb", [128, 2, 128], BF16)
            rmask = A("grm", [128, 512], F32)
            ow = A("gow", [128, 256], F32)
            ow2 = A("gow2", [128, 256], F32)
            ob = A("gob", [128, 256], BF16)
            st = A("gst", [128, 32], F32)
            W_r, p_r, alr_r, t1_r, t2_r = Reg(), Reg(), Reg(), Reg(), Reg()
            EB_r, QT_r, KT_r = regs(2), regs(2), regs(2)
            Ktok_r, V_r, SG_r, AT_r, St_r, Sb_r = Reg(), regs(4), regs(4), Reg(), Reg(), Reg()
            ow_r, ow2_r, ob_r, st_r = Reg(), Reg(), Reg(), Reg()
            gs = cfg.get("gla_stop", 99)
            load_w(Wg, w_in16.ap()[l][:, 0:1040], 1040, W_r)
            kb.dma("pool", wa2[:, :], gla_w_a2.ap()[l], writes=[p_r])
            kb.dma("sp", gln[:, 0, :], bcast_rows(rowtab, l * 5120 + 0, 256), writes=[p_r])
            kb.dma("sp", gln[:, 1, :], bcast_rows(rowtab, l * 5120 + 256, 256), writes=[p_r])
            kb.op("pool", lambda g: g.memset(rmask[:, :], 1.0), writes=[p_r])
            kb.op("pool", lambda g: g.memset(rmask[:, :].rearrange("p (c t) -> p c t", c=4)[:, :, 0:1], 0.0), writes=[p_r])
            kb.op("pool", lambda g: g.memset(St[:, :, :], 0.0), writes=[St_r])
            kb.op("pool", lambda g: g.memset(QTl[:, :, :, :], 0.0), writes=QT_r)
            kb.dma("sp", blkm[:, :], c_blk.ap(), writes=[p_r])
            tmpS_r = Reg()
            kb.op("pool", lambda g: g.memset(Sb[:, :, :], 0.0), writes=[Sb_r])
            for mt in range(4 if gs > 0 else 0):
                tok0 = mt * 512
                xr = XT_r[mt * 4:mt * 4 + 4]
                proj_fm(PS[0][0:16, :], Wg, 1024, 16, tok0, 512, [W_r], xr, PS_r[0])
                kb.op("act", lambda a: a.activation(out=alrT[:, :], in_=PS[0][0:16, :], func=AF.Copy),
                      reads=[PS_r[0]], writes=[alr_r])
                for hp in range(2):
                    kb.op("pe", lambda pe: pe.matmul(PS[1][:, :], lhsT=wa2[0:16, hp * 128:(hp + 1) * 128], rhs=alrT[0:16, :],
                                                     start=True, stop=True), reads=[p_r, alr_r], writes=[PS_r[1]])
                    kb.op("act", lambda a: a.activation(out=t1[:, :], in_=PS[1][:, :], func=AF.Exp, scale=-1.0,
                                                        bias=ptn[:, l, hp:hp + 1]), reads=[PS_r[1], c_r], writes=[t1_r])
                    kb.op("act", lambda a: a.activation(out=t1[:, :], in_=t1[:, :], func=AF.Ln, bias=1.0, scale=1.0),
                          reads=[t1_r], writes=[t1_r])
                    kb.op("dve", lambda v: v.tensor_tensor_scan(out=t2[:, :], data0=rmask[:, :], data1=t1[:, :], initial=0.0,
                                                                op0=ALU.mult, op1=ALU.add), reads=[t1_r, p_r], writes=[t2_r])
                    kb.op("act", lambda a: a.activation(out=EB[:, hp, :], in_=t2[:, :], func=AF.Exp, scale=-1.0 / 16.0),
                          reads=[t2_r], writes=[EB_r[hp]])
                    kb.op("act", lambda a: a.activation(out=t1[:, :], in_=t2[:, :], func=AF.Exp, scale=1.0 / 16.0),
                          reads=[t2_r], writes=[t1_r])
                    proj_fm(PS[2][:, :], Wg, hp * 128, 128, tok0, 512, [W_r], xr, PS_r[2])
                    for hh in range(2):
                        pr = slice(hh * 64, hh * 64 + 64)
                        kb.op("dve", lambda v: v.scalar_tensor_tensor(out=QTl[pr, hp, hh, :], in0=PS[2][pr, :], scalar=0.125,
                                                                      in1=EB[pr, hp, :], op0=ALU.mult, op1=ALU.mult),
                              reads=[PS_r[2], EB_r[hp]], writes=[QT_r[hp]])
                    proj_fm(PS[3][:, :], Wg, 256 + hp * 128, 128, tok0, 512, [W_r], xr, PS_r[3])
                    kb.op("dve", lambda v: v.tensor_tensor(out=KTl[:, hp, :], in0=PS[3][:, :], in1=t1[:, :], op=ALU.mult),
                          reads=[PS_r[3], t1_r], writes=[KT_r[hp]])

                    def fn(pe):
                        inst = None
                        for j in range(4):
                            inst = pe.transpose(out=PB[0][:, j * 128:(j + 1) * 128], in_=KTl[:, hp, j * 128:(j + 1) * 128],
                                                identity=ident[:])
                        return inst
                    kb.op("pe", fn, reads=[KT_r[hp], c_r], writes=[PB_r[0]])
                    kb.op("dve", lambda v: v.tensor_copy(out=Ktok[:, :, hp * 128:(hp + 1) * 128],
                                                         in_=PB[0][:, 0:512].rearrange("p (j c) -> p j c", j=4)),
                          reads=[PB_r[0]], writes=[Ktok_r])
                for j in range(4 if gs > 1 else 0):
                    tt = mt * 4 + j
                    proj_tm(PS[4][:, 0:256], Wg, 512, 256, tt, [W_r], PS_r[4])
                    proj_tm(PS[5][:, 0:256], Wg, 768, 256, tt, [W_r], PS_r[5])
                    if cfg.get("gv", 3) >= 2:
                        kb.op("dve", lambda v: v.tensor_scalar(out=V[:, j, :], in0=PS[4][:, 0:256], scalar1=1.0, scalar2=None, op0=ALU.mult),
                              reads=[PS_r[4]], writes=[V_r[j]])
                    if cfg.get("gv", 3) >= 3:
                        kb.op("act", lambda a: a.activation(out=SG[:, j, :], in_=PS[5][:, 0:256], func=AF.Silu),
                              reads=[PS_r[5]], writes=[SG_r[j]])
                for j in range(4 if gs > 2 else 0):
                    tt = mt * 4 + j
                    cc = slice(j * 128, (j + 1) * 128)

                    def fn(pe):
                        inst = None
                        for h in range(4):
                            hp, hh = h // 2, h % 2
                            inst = pe.matmul(PS[0][:, h * 128:(h + 1) * 128], lhsT=KTl[:, hp, cc], rhs=QTl[:, hp, hh, cc],
                                             start=True, stop=True)
                        return inst
                    kb.op("pe", fn, reads=QT_r + KT_r, writes=[PS_r[0]])
                    kb.op("dve", lambda v: v.tensor_tensor(out=AT[:, :, :], in0=PS[0][:, :].rearrange("p (h t) -> p h t", h=4),
                                                          in1=caus[:, :, :], op=ALU.mult), reads=[PS_r[0], c_r], writes=[AT_r])

                    def fn(pe):
                        inst = None
                        for h in range(4):
                            hp, hh = h // 2, h % 2
                            pe.matmul(PS[1][:, h * 64:(h + 1) * 64], lhsT=AT[:, h, :], rhs=V[:, j, h * 64:(h + 1) * 64],
                                      start=True, stop=False)
                            inst = pe.matmul(PS[1][:, h * 64:(h + 1) * 64], lhsT=QTl[:, hp, hh, cc], rhs=Sb[:, hp, hh * 64:(hh + 1) * 64],
                                             start=False, stop=True)
                        return inst
                    if gs <= 3:
                        continue
                    kb.op("pe", fn, reads=[AT_r, V_r[j], Sb_r] + QT_r, writes=[PS_r[1]])

                    def fn(pe):
                        inst = None
                        for hp in range(2):
                            inst = pe.matmul(PS[2][:, hp * 128:(hp + 1) * 128], lhsT=Ktok[:, j, hp * 128:(hp + 1) * 128],
                                             rhs=V[:, j, hp * 128:(hp + 1) * 128], start=True, stop=True)
                        return inst
                    if gs <= 4:
                        continue
                    kb.op("pe", fn, reads=[Ktok_r, V_r[j]], writes=[PS_r[2]])
                    for hp in range(2):
                        ee = EB[:, hp, j * 128 + 127:j * 128 + 128]
                        kb.op("dve", lambda v: v.scalar_tensor_tensor(out=tmpS[:, :], in0=PS[2][:, hp * 128:(hp + 1) * 128], scalar=ee,
                                                                      in1=blkm[:, :], op0=ALU.mult, op1=ALU.mult),
                              reads=[PS_r[2], EB_r[hp], p_r], writes=[tmpS_r])
                        kb.op("dve", lambda v: v.scalar_tensor_tensor(out=St[:, hp, :], in0=St[:, hp, :], scalar=ee, in1=tmpS[:, :],
                                                                      op0=ALU.mult, op1=ALU.add),
                              reads=[St_r, tmpS_r, EB_r[hp]], writes=[St_r])
                    kb.op("act", lambda a: a.activation(out=Sb[:, :, :], in_=St[:, :, :], func=AF.Copy), reads=[St_r], writes=[Sb_r])
                    if gs <= 5:
                        continue
                    head_norm_gate(PS[1][:, 0:256], PS_r[1], ow, ow_r, ow2, ow2_r, st, st_r, gln, p_r, LN_EPS)
                    kb.op("dve", lambda v: v.tensor_tensor(out=ob[:, :], in0=ow[:, :], in1=SG[:, j, :], op=ALU.mult),
                          reads=[ow_r, SG_r[j]], writes=[ob_r])
                    if gs <= 6:
                        continue
                    out_to_OT(ob, ob_r, 128, OT, OT_r, 0, tt * 128)
            kb.barrier()

    def head_norm_gate(ps_ap, ps_r, ow, ow_r, ow2, ow2_r, st, st_r, gln, gln_r, eps, P=128):
        v4 = lambda ap: ap.rearrange("p (h d) -> p h d", h=4)
        kb.op("act", lambda a: a.activation(out=ow[0:P, :], in_=ps_ap, func=AF.Copy), reads=[ps_r], writes=[ow_r])
        kb.op("act", lambda a: a.activation(out=ow2[0:P, :], in_=ow[0:P, :], func=AF.Square), reads=[ow_r], writes=[ow2_r])
        kb.op("dve", lambda v: v.reduce_sum(out=st[0:P, 0:4], in_=v4(ow[0:P, :]), axis=AX.X), reads=[ow_r], writes=[st_r])
        kb.op("dve", lambda v: v.reduce_sum(out=st[0:P, 4:8], in_=v4(ow2[0:P, :]), axis=AX.X), reads=[ow2_r, st_r], writes=[st_r])
        kb.op("dve", lambda v: v.tensor_scalar(out=st[0:P, 8:16], in0=st[0:P, 0:8], scalar1=1.0 / 64.0, scalar2=None, op0=ALU.mult),
              reads=[st_r], writes=[st_r])
        kb.op("dve", lambda v: v.tensor_tensor(out=st[0:P, 16:20], in0=st[0:P, 8:12], in1=st[0:P, 8:12], op=ALU.mult),
              reads=[st_r], writes=[st_r])
        kb.op("dve", lambda v: v.tensor_tensor(out=st[0:P, 20:24], in0=st[0:P, 12:16], in1=st[0:P, 16:20], op=ALU.subtract),
              reads=[st_r], writes=[st_r])
        kb.op("act", lambda a: a.activation(out=st[0:P, 24:28], in_=st[0:P, 20:24], func=AF.Sqrt, bias=eps, scale=1.0),
              reads=[st_r], writes=[st_r])
        kb.op("dve", lambda v: v.reciprocal(out=st[0:P, 28:32], in_=st[0:P, 24:28]), reads=[st_r], writes=[st_r])
        for h in range(4):
            kb.op("dve", lambda v: v.tensor_scalar(out=ow[0:P, h * 64:(h + 1) * 64], in0=ow[0:P, h * 64:(h + 1) * 64],
                                                   scalar1=st[0:P, 8 + h:9 + h], scalar2=st[0:P, 28 + h:29 + h],
                                                   op0=ALU.subtract, op1=ALU.mult), reads=[ow_r, st_r], writes=[ow_r])
        kb.op("dve", lambda v: v.tensor_tensor(out=ow[0:P, :], in0=ow[0:P, :], in1=gln[0:P, 0, :], op=ALU.mult),
              reads=[ow_r, gln_r], writes=[ow_r])
        kb.op("dve", lambda v: v.tensor_tensor(out=ow[0:P, :], in0=ow[0:P, :], in1=gln[0:P, 1, :], op=ALU.add),
              reads=[ow_r, gln_r], writes=[ow_r])

    def out_to_OT(ob, ob_r, P, OT, OT_r, k0, tokc0, ncols=256):
        nk = ncols // 128

        def fn(pe):
            inst = None
            for kk in range(nk):
                inst = pe.transpose(out=PB[1][:, kk * 128:kk * 128 + P], in_=ob[0:P, kk * 128:(kk + 1) * 128],
                                    identity=ident[0:P, 0:P])
            return inst
        kb.op("pe", fn, reads=[ob_r, c_r], writes=[PB_r[1]])
        tr = OT_r[tokc0 // 128]
        kb.op("act", lambda a: a.activation(out=OT[:, k0:k0 + nk, tokc0:tokc0 + P],
                                            in_=PB[1][:, 0:nk * 128].rearrange("p (k t) -> p k t", k=nk)[:, :, 0:P], func=AF.Copy),
              reads=[PB_r[1]], writes=[tr])

    RWT = [(0, 128), (128, 128), (256, 128), (384, 128), (512, 128), (640, 128), (768, 64), (832, 64),
           (896, 128), (1024, 32), (1056, 32)]
    C0 = float(math.exp(-0.5))

    def rwkv_phase(sq, l, OT, OT_r):
        MT = 256
        NCH = MT // 64
        PU = NCH * 2
        with ExitStack() as es:
            A = lambda n_, s_, d_: es.enter_context(sb(n_, s_, d_))
            Wr = A("Wr", [128, 8, 1088], BF16)
            w2 = A("rw2", [64, 256], BF16)
            a2 = A("ra2", [64, 256], BF16)
            v2 = A("rv2", [32, 256], BF16)
            g2a = A("rg2a", [128, 256], BF16)
            g2b = A("rg2b", [128, 256], BF16)
            rln = A("rln", [128, 2, 2, 64], F32)
            blkf = A("rblkf", [128, 128], F32)
            m3 = A("rm3", [128, 3, 128], F32)
            istk = A("ristk", [128, 64], BF16)
            onec = A("ronec", [128, 1], BF16)
            rmask = A("rrm", [128, MT], F32)
            xs = [A("rxs%d" % i, [128, MT], F32) for i in range(6)]
            tt_ = [A("rt%d" % i, [128, MT], F32) for i in range(8)]
            ttb_ = [A("rtb%d" % i, [128, MT], F32) for i in range(9)]
            tb_r = regs(9)
            Gam2 = [A("rGam%d" % i_, [128, 2, MT], F32) for i_ in range(2)]
            AR2 = [A("rAR%d" % i_, [128, 2, 2, NCH, 2, 64], BF16) for i_ in range(2)]
            BK = A("rBK", [128, 2, 2, NCH, 2, 64], BF16)
            VTz = A("rVTz", [128, 2, NCH, 2, 64], BF16)
            rkrz = A("rrkrz", [128, 2, NCH, 2, 64], BF16)
            TW = A("rTW", [64, MT], BF16)
            AL = A("rAL", [64, MT], BF16)
            SGLd = A("rSGLd", [128, NCH, 2, 64], BF16)
            SGL2d = A("rSGL2d", [128, NCH, 2, 64], BF16)
            VR = A("rVR", [32, MT], BF16)
            Vstk2 = [A("rVstk%d" % i_, [128, NCH, 2, 64], BF16) for i_ in range(2)]
            Bblk2 = [A("rBblk%d" % i_, [128, NCH, 2, 128], BF16) for i_ in range(2)]
            Kblk2 = [A("rKblk%d" % i_, [128, NCH, 2, 128], BF16) for i_ in range(2)]
            gtok2 = [A("rgtok%d" % i_, [128, NCH, 2, 64], F32) for i_ in range(2)]
            cb2 = [A("rcb%d" % i_, [128, NCH, 2], F32) for i_ in range(2)]
            MN = [A("rMN%d" % i, [128, PU, 2, 128], F32) for i in range(2)]
            Pm2 = [A("rP%d" % i_, [128, PU, 128], F32) for i_ in range(2)]
            A32 = [A("rA3%d" % i_, [128, PU, 3, 128], BF16) for i_ in range(2)]
            Rs = A("rRs", [128, 128], F32)
            Ub = A("rUb", [128, 128], BF16)
            Tst = A("rTst", [128, 2, 64], F32)
            Tb = A("rTb", [128, 2, 64], BF16)
            tmpS = A("rtmpS", [128, 2, 64], F32)
            ow = A("row", [128, 128], F32)
            ow2 = A("row2", [128, 128], F32)
            obblk = A("robblk", [128, 2, 128], BF16)
            st = A("rst", [128, 32], F32)
            W_r, p_r = Reg(), Reg()
            xs_r, t_r = regs(6), regs(8)
            t_r0 = t_r
            BK_r, VTz_r, rkrz_r = regs(2), regs(2), regs(2)
            Gam_r2, AR_r2 = [regs(2), regs(2)], [regs(2), regs(2)]
            TW_r, AL_r, SGL_r, SGL2_r, VR_r = Reg(), Reg(), Reg(), Reg(), Reg()
            Vstk_r2, Bblk_r2, Kblk_r2, gtok_r2, cb_r2 = regs(2), regs(2), regs(2), regs(2), regs(2)
            MN_r, Rs_r, Ub_r, T_r, Tb_r, tmpS_r = regs(2), Reg(), Reg(), Reg(), Reg(), Reg()
            P_r2, A3_r2 = regs(2), regs(2)
            ow_r, ow2_r, ob_r, st_r = Reg(), Reg(), Reg(), Reg()
            rs_ = cfg.get("rw_stop", 99)
            load_w(Wr, w_in16.ap()[l][:, 1040:2096], 1056, W_r)
            if l >= 1:
                for k in range(8):
                    kb.dma("pool", Wr[:, k, 1056:1088], w_vres.ap()[l - 1][k * 128:(k + 1) * 128, :], writes=[W_r])
            kb.dma("pool", w2[:, :], rwkv_w2.ap()[l], writes=[p_r])
            kb.dma("pool", a2[:, :], rwkv_a2.ap()[l], writes=[p_r])
            if l >= 1:
                kb.dma("pool", v2[:, :], rwkv_v2.ap()[l - 1], writes=[p_r])
            kb.op("pool", lambda g: g.memset(g2b[:, :], 0.0), writes=[p_r])
            kb.dma("pool", g2a[:, :], rwkv_g2.ap()[l][0:128, :], writes=[p_r])
            kb.dma("pool", g2b[0:32, :], rwkv_g2.ap()[l][128:160, :], reads=[p_r], writes=[p_r])
            for wb in range(2):
                for hp in range(2):
                    for hh in range(2):
                        kb.dma("sp", rln[hh * 64:(hh + 1) * 64, wb, hp, :],
                               bcast_rows(rowtab, l * 5120 + 512 + wb * 256 + (2 * hp + hh) * 64, 64, parts=64), writes=[p_r])
            kb.dma("sp", blkf[:, :], c_blk.ap(), writes=[p_r])
            kb.dma("sp", m3[:, :, :], c_rwmask2.ap(), writes=[p_r])
            kb.dma("pool", istk[:, :], c_istk.ap(), writes=[p_r])
            kb.op("pool", lambda g: g.memset(onec[:, :], 1.0), writes=[p_r])
            kb.op("pool", lambda g: g.memset(rmask[:, :], 1.0), writes=[p_r])
            kb.op("pool", lambda g: g.memset(rmask[:, :].rearrange("p (c t) -> p c t", c=NCH)[:, :, 0:1], 0.0), writes=[p_r])
            zl = [(Tst, [T_r]), (Tb, [Tb_r]), (SGL2d, [SGL2_r]), (BK, BK_r), (VTz, VTz_r), (rkrz, rkrz_r), (obblk, [ob_r])]
            for i_ in range(2):
                zl += [(AR2[i_], AR_r2[i_])]
            for tz, rr in zl:
                kb.op("pool", lambda g, tz=tz: g.memset(tz[:], 0.0), writes=rr)

            carry = A("rcarry", [128, 12], F32)
            carry_r = regs(11)
            kb.op("pool", lambda g: g.memset(carry[:, :], 0.0), writes=carry_r)

            def shift_proj(i, tok0, dst_fn):
                c0, n = RWT[i]
                mucol = 2 + i
                pi = i % 2
                tp = ttb_[7] if pi == 0 else ttb_[8]
                tpr = tb_r[7] if pi == 0 else tb_r[8]
                proj_fm(PS[pi][0:n, 0:MT], Wr, c0, n, tok0, MT, [W_r], XT_r[tok0 // 128:tok0 // 128 + MT // 128], PS_r[pi])
                kb.op("act", lambda a: a.activation(out=tp[0:n, 1:MT], in_=PS[pi][0:n, 0:MT - 1], func=AF.Copy,
                                                    scale=ptb[0:n, l, mucol:mucol + 1]), reads=[PS_r[pi], c_r], writes=[tpr, PS_r[pi]])
                kb.op("act", lambda a: a.activation(out=tp[0:n, 0:1], in_=carry[0:n, i:i + 1], func=AF.Copy,
                                                    scale=ptb[0:n, l, mucol:mucol + 1]), reads=[carry_r[i], c_r, tpr], writes=[tpr])
                kb.op("act", lambda a: a.activation(out=carry[0:n, i:i + 1], in_=PS[pi][0:n, MT - 1:MT], func=AF.Copy),
                      reads=[PS_r[pi], carry_r[i]], writes=[carry_r[i], PS_r[pi]])
                dst_ap, dst_regs = dst_fn()
                kb.op("dve", lambda v: v.scalar_tensor_tensor(out=dst_ap, in0=PS[pi][0:n, 0:MT], scalar=ptn[0:n, l, mucol:mucol + 1],
                                                              in1=tp[0:n, 0:MT], op0=ALU.mult, op1=ALU.add),
                      reads=[PS_r[pi], tpr, c_r], writes=dst_regs + [PS_r[pi]])

            def prep(mt):
                tok0 = mt * MT
                par = mt % 2
                t_r = t_r0
                AR, Vstk, Bblk, Kblk, Gam, gtok, cb, Pm, A3 = AR2[par], Vstk2[par], Bblk2[par], Kblk2[par], Gam2[par], gtok2[par], cb2[par], Pm2[par], A32[par]
                AR_r, Vstk_r, Bblk_r, Kblk_r, Gam_r, gtok_r, cb_r, P_r, A3_r = AR_r2[par], Vstk_r2[par], Bblk_r2[par], Kblk_r2[par], Gam_r2[par], gtok_r2[par], cb_r2[par], P_r2[par], A3_r2[par]
                for i in range(6):
                    shift_proj(i, tok0, lambda i=i: (xs[i][:, :], [xs_r[i]]))
                shift_proj(6, tok0, lambda: (tt_[0][0:64, :], [t_r[0]]))
                kb.op("act", lambda a: a.activation(out=TW[:, :], in_=tt_[0][0:64, :], func=AF.Tanh), reads=[t_r[0]], writes=[TW_r])
                shift_proj(7, tok0, lambda: (AL[:, :], [AL_r]))
                shift_proj(8, tok0, lambda: (tt_[0][:, :], [t_r[0]]))
                for hh_ in range(2):
                    kb.op("act", lambda a: a.activation(out=SGLd[:, :, hh_, :], in_=tt_[0][:, :].rearrange("p (c i) -> p c i", c=NCH), func=AF.Sigmoid),
                          reads=[t_r[0]], writes=[SGL_r])
                shift_proj(9, tok0, lambda: (tt_[0][0:32, :], [t_r[0]]))
                for hh_ in range(2):
                    kb.op("act", lambda a: a.activation(out=SGL2d[0:32, :, hh_, :], in_=tt_[0][0:32, :].rearrange("p (c i) -> p c i", c=NCH), func=AF.Sigmoid),
                          reads=[t_r[0]], writes=[SGL2_r])
                if l >= 1:
                    shift_proj(10, tok0, lambda: (VR[:, :], [VR_r]))
                if rs_ <= 1:
                    return
                for hp in range(2):
                    rT, kT, vT = xs[hp], xs[2 + hp], xs[4 + hp]
                    rT_r, kT_r, vT_r = xs_r[hp], xs_r[2 + hp], xs_r[4 + hp]
                    t1, t2, t3, t4, t5, t6, t7 = tt_[0:7] if hp == 0 else ttb_[0:7]
                    t_r = t_r0 if hp == 0 else tb_r
                    kb.op("pe", lambda pe: pe.matmul(PS[2][:, 0:MT], lhsT=w2[:, hp * 128:(hp + 1) * 128], rhs=TW[:, :], start=True, stop=True),
                          reads=[p_r, TW_r], writes=[PS_r[2]])
                    kb.op("act", lambda a: a.activation(out=t1[:, :], in_=PS[2][:, 0:MT], func=AF.Sigmoid, bias=ptb[:, l, 13 + hp:14 + hp]),
                          reads=[PS_r[2], c_r], writes=[t_r[0]])
                    kb.op("dve", lambda v: v.tensor_tensor_scan(out=t2[:, :], data0=rmask[:, :], data1=t1[:, :], initial=0.0,
                                                                op0=ALU.mult, op1=ALU.add), reads=[t_r[0], p_r], writes=[t_r[1]])
                    kb.op("act", lambda a: a.activation(out=Gam[:, hp, :], in_=t2[:, :], func=AF.Exp, scale=-C0), reads=[t_r[1]], writes=[Gam_r[hp]])
                    kb.op("act", lambda a: a.activation(out=t3[:, :], in_=t2[:, :], func=AF.Exp, scale=C0), reads=[t_r[1]], writes=[t_r[2]])
                    kb.op("dve", lambda v: v.tensor_tensor(out=t4[:, :], in0=t2[:, :], in1=t1[:, :], op=ALU.subtract),
                          reads=[t_r[0], t_r[1]], writes=[t_r[3]])
                    kb.op("act", lambda a: a.activation(out=t4[:, :], in_=t4[:, :], func=AF.Exp, scale=-C0), reads=[t_r[3]], writes=[t_r[3]])
                    kb.op("pe", lambda pe: pe.matmul(PS[3][:, 0:MT], lhsT=a2[:, hp * 128:(hp + 1) * 128], rhs=AL[:, :], start=True, stop=True),
                          reads=[p_r, AL_r], writes=[PS_r[3]])
                    kb.op("act", lambda a: a.activation(out=t5[:, :], in_=PS[3][:, 0:MT], func=AF.Sigmoid, bias=ptb[:, l, 15 + hp:16 + hp]),
                          reads=[PS_r[3], c_r], writes=[t_r[4]])
                    kb.op("dve", lambda v: v.tensor_scalar(out=t6[:, :], in0=kT[:, :], scalar1=ptb[:, l, 19 + hp:20 + hp], scalar2=None, op0=ALU.mult),
                          reads=[kT_r, c_r], writes=[t_r[5]])
                    kb.op("act", lambda a: a.activation(out=t7[:, :], in_=t6[:, :], func=AF.Square), reads=[t_r[5]], writes=[t_r[6]])
                    kb.op("pe", lambda pe: pe.matmul(PS[4][:, 0:MT], lhsT=blkf[:, :], rhs=t7[:, :], start=True, stop=True),
                          reads=[p_r, t_r[6]], writes=[PS_r[4]])
                    kb.op("act", lambda a: a.activation(out=t7[:, :], in_=PS[4][:, 0:MT], func=AF.Sqrt), reads=[PS_r[4], t_r[6]], writes=[t_r[6]])
                    kb.op("dve", lambda v: v.tensor_scalar(out=t7[:, :], in0=t7[:, :], scalar1=1e-12, scalar2=None, op0=ALU.max),
                          reads=[t_r[6]], writes=[t_r[6]])
                    kb.op("dve", lambda v: v.reciprocal(out=t7[:, :], in_=t7[:, :]), reads=[t_r[6]], writes=[t_r[6]])
                    kb.op("dve", lambda v: v.tensor_tensor(out=t6[:, :], in0=t6[:, :], in1=t7[:, :], op=ALU.mult),
                          reads=[t_r[5], t_r[6]], writes=[t_r[5]])
                    kb.op("dve", lambda v: v.tensor_scalar(out=t7[:, :], in0=t5[:, :], scalar1=-1.0, scalar2=ptb[:, l, 21 + hp:22 + hp],
                                                           op0=ALU.add, op1=ALU.mult), reads=[t_r[4], t_r[6], c_r], writes=[t_r[6]])
                    kb.op("dve", lambda v: v.scalar_tensor_tensor(out=t7[:, :], in0=t7[:, :], scalar=1.0, in1=kT[:, :], op0=ALU.add, op1=ALU.mult),
                          reads=[t_r[6], kT_r], writes=[t_r[6]])
                    v3 = lambda ap: ap.rearrange("p (c i) -> p c i", c=NCH)
                    for hh in range(2):
                        pr = slice(hh * 64, hh * 64 + 64)
                        kb.op("dve", lambda v: v.scalar_tensor_tensor(out=AR[pr, hp, 0, :, hh, :], in0=v3(t6[pr, :]), scalar=-1.0, in1=v3(t4[pr, :]),
                                                                      op0=ALU.mult, op1=ALU.mult), reads=[t_r[5], t_r[3]], writes=[AR_r[hp]])
                        kb.op("dve", lambda v: v.tensor_tensor(out=AR[pr, hp, 1, :, hh, :], in0=v3(rT[pr, :]), in1=v3(Gam[pr, hp, :]), op=ALU.mult),
                              reads=[rT_r, Gam_r[hp]], writes=[AR_r[hp]])
                    kb.op("dve", lambda v: v.tensor_tensor(out=t1[:, :], in0=t6[:, :], in1=t5[:, :], op=ALU.mult),
                          reads=[t_r[5], t_r[4], t_r[0]], writes=[t_r[0]])
                    for hh in range(2):
                        pr = slice(hh * 64, hh * 64 + 64)
                        kb.op("dve", lambda v: v.tensor_tensor(out=BK[pr, hp, 0, :, hh, :], in0=v3(t1[pr, :]), in1=v3(t3[pr, :]), op=ALU.mult),
                              reads=[t_r[0], t_r[2]], writes=[BK_r[hp]])
                        kb.op("dve", lambda v: v.tensor_tensor(out=BK[pr, hp, 1, :, hh, :], in0=v3(t7[pr, :]), in1=v3(t3[pr, :]), op=ALU.mult),
                              reads=[t_r[6], t_r[2]], writes=[BK_r[hp]])
                        kb.op("dve", lambda v: v.scalar_tensor_tensor(out=rkrz[pr, hp, :, hh, :], in0=v3(rT[pr, :]), scalar=ptb[pr, l, 23 + hp:24 + hp],
                                                                      in1=v3(t7[pr, :]), op0=ALU.mult, op1=ALU.mult),
                              reads=[rT_r, t_r[6], c_r], writes=[rkrz_r[hp]])
                    if l == 0:
                        kb.dma("sp", vfirst_d.ap()[hp, :, sq * 0 + tok0:tok0 + MT], vT[:, :], reads=[vT_r], writes=[vfirst_r[hp]])
                    else:
                        kb.op("pe", lambda pe: pe.matmul(PS[5][:, 0:MT], lhsT=v2[:, hp * 128:(hp + 1) * 128], rhs=VR[:, :], start=True, stop=True),
                              reads=[p_r, VR_r], writes=[PS_r[5]])
                        kb.op("act", lambda a: a.activation(out=t1[:, :], in_=PS[5][:, 0:MT], func=AF.Sigmoid, bias=ptb[:, l, 17 + hp:18 + hp]),
                              reads=[PS_r[5], c_r, t_r[0]], writes=[t_r[0]])
                        kb.dma("sp", t2[:, :], vfirst_d.ap()[hp, :, tok0:tok0 + MT], reads=[vfirst_r[hp], t_r[1]], writes=[t_r[1]])
                        kb.op("dve", lambda v: v.tensor_tensor(out=t2[:, :], in0=t2[:, :], in1=vT[:, :], op=ALU.subtract),
                              reads=[t_r[1], vT_r], writes=[t_r[1]])
                        kb.op("dve", lambda v: v.tensor_tensor(out=t2[:, :], in0=t2[:, :], in1=t1[:, :], op=ALU.mult),
                              reads=[t_r[1], t_r[0]], writes=[t_r[1]])
                        kb.op("dve", lambda v: v.tensor_tensor(out=vT[:, :], in0=vT[:, :], in1=t2[:, :], op=ALU.add),
                              reads=[t_r[1], vT_r], writes=[vT_r])
                    for hh in range(2):
                        pr = slice(hh * 64, hh * 64 + 64)
                        kb.op("act", lambda a: a.activation(out=VTz[pr, hp, :, hh, :], in_=vT[pr, :].rearrange("p (c i) -> p c i", c=NCH), func=AF.Copy),
                              reads=[vT_r], writes=[VTz_r[hp]])
                if rs_ <= 2:
                    return
                fl = lambda ap: ap.rearrange("p a b -> p (a b)")
                def fn(pe):
                    inst = None
                    for c in range(NCH):
                        for hp in range(2):
                            inst = pe.matmul(PS[2][:, (c * 2 + hp) * 64:(c * 2 + hp + 1) * 64], lhsT=fl(VTz[:, hp, c, :, :]), rhs=istk[:, :], start=True, stop=True)
                    return inst
                kb.op("pe", fn, reads=VTz_r + [p_r], writes=[PS_r[2]])
                kb.op("act", lambda a: a.activation(out=Vstk[:, :, :, :], in_=PS[2][:, :].rearrange("p (c h d) -> p c h d", c=NCH, h=2), func=AF.Copy),
                      reads=[PS_r[2]], writes=[Vstk_r, PS_r[2]])

                def fn(pe):
                    inst = None
                    for c in range(NCH):
                        for hp in range(2):
                            inst = pe.matmul(PS[3][:, c * 2 + hp:c * 2 + hp + 1], lhsT=fl(rkrz[:, hp, c, :, :]), rhs=onec[:, :], start=True, stop=True)
                    return inst
                kb.op("pe", fn, reads=rkrz_r + [p_r], writes=[PS_r[3]])
                kb.op("act", lambda a: a.activation(out=cb[:, :, :], in_=PS[3][:, 0:NCH * 2].rearrange("p (c h) -> p c h", c=NCH), func=AF.Copy),
                      reads=[PS_r[3]], writes=[cb_r, PS_r[3]])
                for bk, dst, dst_r, pbi in ((0, Bblk, Bblk_r, 0), (1, Kblk, Kblk_r, 1)):
                    def fn(pe, bk=bk, pbi=pbi):
                        inst = None
                        for c in range(NCH):
                            for hp in range(2):
                                inst = pe.transpose(out=PB[pbi][:, (c * 2 + hp) * 128:(c * 2 + hp + 1) * 128], in_=fl(BK[:, hp, bk, c, :, :]), identity=ident[:, :])
                        return inst
                    kb.op("pe", fn, reads=BK_r + [c_r], writes=[PB_r[pbi]])
                    kb.op("dve", lambda v, dst=dst, pbi=pbi: v.tensor_copy(out=dst[:, :, :, :], in_=PB[pbi][:, :].rearrange("p (c h d) -> p c h d", c=NCH, h=2)),
                          reads=[PB_r[pbi]], writes=[dst_r, PB_r[pbi]])
                for c2 in range(NCH // 2):
                    def fn(pe):
                        inst = None
                        for cc_ in range(2):
                            c = c2 * 2 + cc_
                            pe.matmul(PS[3][:, cc_ * 256:(cc_ + 1) * 256], lhsT=fl(SGLd[:, c, :, :]), rhs=g2a[:, :], start=True, stop=False)
                            inst = pe.matmul(PS[3][:, cc_ * 256:(cc_ + 1) * 256], lhsT=fl(SGL2d[:, c, :, :]), rhs=g2b[:, :], start=False, stop=True)
                        return inst
                    kb.op("pe", fn, reads=[SGL_r, SGL2_r, p_r], writes=[PS_r[3]])
                    for hh in range(2):
                        pr = slice(hh * 64, hh * 64 + 64)
                        kb.op("act", lambda a: a.activation(out=gtok[pr, c2 * 2:c2 * 2 + 2, :, :],
                                                            in_=PS[3][pr, :].rearrange("p (c h g d) -> p c h g d", c=2, h=2, g=2)[:, :, :, hh, :], func=AF.Copy),
                              reads=[PS_r[3]], writes=[gtok_r, PS_r[3]])
                for c in range(NCH):
                    for hp in range(2):
                        pu = c * 2 + hp
                        px, py = (4, 5) if pu % 2 == 0 else (2, 3)

                        def fn(pe, px=px, py=py):
                            pe.matmul(PS[px][:, 0:128], lhsT=fl(AR[:, hp, 0, c, :, :]), rhs=fl(BK[:, hp, 0, c, :, :]), start=True, stop=True)
                            pe.matmul(PS[px][:, 128:384], lhsT=fl(BK[:, hp, 0, c, :, :]), rhs=AR[:, hp, :, c, :, :].rearrange("p a h i -> p a (h i)"),
                                      start=True, stop=True)
                            return pe.matmul(PS[py][:, 0:256], lhsT=fl(BK[:, hp, 1, c, :, :]), rhs=AR[:, hp, :, c, :, :].rearrange("p a h i -> p a (h i)"),
                                             start=True, stop=True)
                        kb.op("pe", fn, reads=AR_r + BK_r, writes=[PS_r[px], PS_r[py]])
                        kb.op("dve", lambda v, px=px: v.tensor_tensor(out=MN[0][:, pu, :, :], in0=PS[px][:, 0:256].rearrange("p (a b) -> p a b", a=2),
                                                                      in1=m3[:, 0:2, :], op=ALU.mult), reads=[PS_r[px], p_r], writes=[MN_r[0], PS_r[px]])
                        kb.op("dve", lambda v, px=px: v.tensor_tensor(out=A3[:, pu, 0, :], in0=PS[px][:, 256:384], in1=m3[:, 2, :], op=ALU.mult),
                              reads=[PS_r[px], p_r], writes=[A3_r, PS_r[px]])
                        kb.op("dve", lambda v, py=py: v.tensor_tensor(out=A3[:, pu, 1:3, :], in0=PS[py][:, 0:256].rearrange("p (a b) -> p a b", a=2),
                                                                      in1=m3[:, 1:3, :], op=ALU.mult), reads=[PS_r[py], p_r], writes=[A3_r, PS_r[py]])
                if rs_ <= 3:
                    return
                kb.op("dve", lambda v: v.tensor_tensor(out=Pm[:, :, :], in0=MN[0][:, :, 1, :],
                                                      in1=bass.AP(identf, 0, [[128, 128], [0, PU], [1, 128]]), op=ALU.add),
                      reads=[MN_r[0], c_r], writes=[P_r])
                cur = 0
                for lev in range(5):
                    nxt = 1 - cur
                    lastlev = (lev == 4)
                    for g2_ in range(PU // 2):
                        pi = 4 + (g2_ % 2)

                        def fn(pe, pi=pi):
                            inst = None
                            for q in range(2):
                                u = g2_ * 2 + q
                                inst = pe.matmul(PS[pi][:, q * 256:q * 256 + 128], lhsT=MN[cur][:, u, 1, :], rhs=MN[cur][:, u, 0, :], start=True, stop=True)
                                if not lastlev:
                                    inst = pe.matmul(PS[pi][:, q * 256 + 128:q * 256 + 256], lhsT=MN[cur][:, u, 0, :], rhs=MN[cur][:, u, 1, :],
                                                     start=True, stop=True)
                            return inst
                        kb.op("pe", fn, reads=[MN_r[cur]], writes=[PS_r[pi]])
                        kb.op("act", lambda a, pi=pi: a.activation(out=MN[nxt][:, g2_ * 2:g2_ * 2 + 2, :, :],
                                                                   in_=PS[pi][:, :].rearrange("p (u a b) -> p u a b", u=2, a=2), func=AF.Copy),
                              reads=[PS_r[pi]], writes=[MN_r[nxt], PS_r[pi]])
                    for g4_ in range(PU // 4):
                        pi = 2 + (g4_ % 2)

                        def fn(pe, pi=pi):
                            inst = None
                            for q in range(4):
                                u = g4_ * 4 + q
                                inst = pe.matmul(PS[pi][:, q * 128:(q + 1) * 128], lhsT=MN[nxt][:, u, 0, :], rhs=Pm[:, u, :], start=True, stop=True)
                            return inst
                        kb.op("pe", fn, reads=[MN_r[nxt], P_r], writes=[PS_r[pi]])
                        kb.op("dve", lambda v, pi=pi: v.tensor_tensor(out=Pm[:, g4_ * 4:g4_ * 4 + 4, :], in0=PS[pi][:, :].rearrange("p (u b) -> p u b", u=4),
                                                                      in1=Pm[:, g4_ * 4:g4_ * 4 + 4, :], op=ALU.add), reads=[PS_r[pi], P_r], writes=[P_r, PS_r[pi]])
                    cur = nxt

            def chain(mt):
                tok0 = mt * MT
                par = mt % 2
                AR, Vstk, Bblk, Kblk, Gam, gtok, cb, Pm, A3 = AR2[par], Vstk2[par], Bblk2[par], Kblk2[par], Gam2[par], gtok2[par], cb2[par], Pm2[par], A32[par]
                AR_r, Vstk_r, Bblk_r, Kblk_r, Gam_r, gtok_r, cb_r, P_r, A3_r = AR_r2[par], Vstk_r2[par], Bblk_r2[par], Kblk_r2[par], Gam_r2[par], gtok_r2[par], cb_r2[par], P_r2[par], A3_r2[par]
                fl = lambda ap: ap.rearrange("p a b -> p (a b)")
                if rs_ <= 4:
                    return
                for c in range(NCH):
                    def fn(pe):
                        inst = None
                        for hp in range(2):
                            pu = c * 2 + hp
                            pe.matmul(PS[0][:, hp * 64:(hp + 1) * 64], lhsT=A3[:, pu, 1, :], rhs=Vstk[:, c, hp, :], start=True, stop=False)
                            inst = pe.matmul(PS[0][:, hp * 64:(hp + 1) * 64], lhsT=fl(AR[:, hp, 0, c, :, :]), rhs=Tb[:, hp, :], start=False, stop=True)
                        return inst
                    kb.op("pe", fn, reads=[A3_r, Vstk_r, Tb_r] + AR_r, writes=[PS_r[0]])
                    kb.op("act", lambda a: a.activation(out=Rs[:, :], in_=PS[0][:, 0:128], func=AF.Copy), reads=[PS_r[0]], writes=[Rs_r, PS_r[0]])

                    def fn(pe):
                        inst = None
                        for hp in range(2):
                            pu = c * 2 + hp
                            inst = pe.matmul(PS[1][:, hp * 64:(hp + 1) * 64], lhsT=Pm[:, pu, :], rhs=Rs[:, hp * 64:(hp + 1) * 64], start=True, stop=True)
                        return inst
                    kb.op("pe", fn, reads=[P_r, Rs_r], writes=[PS_r[1]])
                    kb.op("act", lambda a: a.activation(out=Ub[:, :], in_=PS[1][:, 0:128], func=AF.Copy), reads=[PS_r[1]], writes=[Ub_r, PS_r[1]])

                    def fn(pe):
                        inst = None
                        for hp in range(2):
                            pu = c * 2 + hp
                            pe.matmul(PS[0][:, hp * 64:(hp + 1) * 64], lhsT=fl(AR[:, hp, 1, c, :, :]), rhs=Tb[:, hp, :], start=True, stop=False)
                            pe.matmul(PS[0][:, hp * 64:(hp + 1) * 64], lhsT=A3[:, pu, 0, :], rhs=Ub[:, hp * 64:(hp + 1) * 64], start=False, stop=False)
                            inst = pe.matmul(PS[0][:, hp * 64:(hp + 1) * 64], lhsT=A3[:, pu, 2, :], rhs=Vstk[:, c, hp, :], start=False, stop=True)
                        return inst
                    kb.op("pe", fn, reads=[A3_r, Vstk_r, Tb_r, Ub_r] + AR_r, writes=[PS_r[0]])

                    def fn(pe):
                        inst = None
                        for hp in range(2):
                            pe.matmul(PS[1][:, hp * 64:(hp + 1) * 64], lhsT=Bblk[:, c, hp, :], rhs=Ub[:, hp * 64:(hp + 1) * 64], start=True, stop=False)
                            inst = pe.matmul(PS[1][:, hp * 64:(hp + 1) * 64], lhsT=Kblk[:, c, hp, :], rhs=Vstk[:, c, hp, :], start=False, stop=True)
                        return inst
                    kb.op("pe", fn, reads=[Bblk_r, Kblk_r, Vstk_r, Ub_r], writes=[PS_r[1]])
                    kb.op("dve", lambda v: v.tensor_tensor(out=tmpS[:, :, :], in0=PS[1][:, 0:128].rearrange("p (h d) -> p h d", h=2), in1=Tst[:, :, :], op=ALU.add),
                          reads=[PS_r[1], T_r], writes=[tmpS_r, PS_r[1]])
                    kb.op("dve", lambda v: v.tensor_tensor(out=Tst[:, :, :], in0=tmpS[:, :, :],
                                                          in1=bass.AP(Gam, c * 64 + 63, [[2 * MT, 128], [MT, 2], [0, 64]]), op=ALU.mult),
                          reads=[tmpS_r, Gam_r[0], Gam_r[1]], writes=[T_r])
                    kb.op("act", lambda a: a.activation(out=Tb[:, :, :], in_=Tst[:, :, :], func=AF.Copy), reads=[T_r], writes=[Tb_r])
                    if rs_ <= 5:
                        continue
                    v2_ = lambda ap: ap.rearrange("p (h d) -> p h d", h=2)
                    kb.op("act", lambda a: a.activation(out=ow[:, :], in_=PS[0][:, 0:128], func=AF.Copy), reads=[PS_r[0]], writes=[ow_r, PS_r[0]])
                    kb.op("act", lambda a: a.activation(out=ow2[:, :], in_=ow[:, :], func=AF.Square), reads=[ow_r], writes=[ow2_r])
                    kb.op("dve", lambda v: v.reduce_sum(out=st[:, 0:2], in_=v2_(ow[:, :]), axis=AX.X), reads=[ow_r], writes=[st_r])
                    kb.op("dve", lambda v: v.reduce_sum(out=st[:, 2:4], in_=v2_(ow2[:, :]), axis=AX.X), reads=[ow2_r, st_r], writes=[st_r])
                    kb.op("dve", lambda v: v.tensor_scalar(out=st[:, 4:8], in0=st[:, 0:4], scalar1=1.0 / 64.0, scalar2=None, op0=ALU.mult), reads=[st_r], writes=[st_r])
                    kb.op("dve", lambda v: v.tensor_tensor(out=st[:, 8:10], in0=st[:, 4:6], in1=st[:, 4:6], op=ALU.mult), reads=[st_r], writes=[st_r])
                    kb.op("dve", lambda v: v.tensor_tensor(out=st[:, 10:12], in0=st[:, 6:8], in1=st[:, 8:10], op=ALU.subtract), reads=[st_r], writes=[st_r])
                    kb.op("act", lambda a: a.activation(out=st[:, 12:14], in_=st[:, 10:12], func=AF.Sqrt, bias=RW_EPS, scale=1.0), reads=[st_r], writes=[st_r])
                    kb.op("dve", lambda v: v.reciprocal(out=st[:, 14:16], in_=st[:, 12:14]), reads=[st_r], writes=[st_r])
                    for hp in range(2):
                        kb.op("dve", lambda v: v.tensor_scalar(out=ow[:, hp * 64:(hp + 1) * 64], in0=ow[:, hp * 64:(hp + 1) * 64],
                                                               scalar1=st[:, 4 + hp:5 + hp], scalar2=st[:, 14 + hp:15 + hp],
                                                               op0=ALU.subtract, op1=ALU.mult), reads=[ow_r, st_r], writes=[ow_r])
                    kb.op("dve", lambda v: v.tensor_tensor(out=v2_(ow[:, :]), in0=v2_(ow[:, :]), in1=rln[:, 0, :, :], op=ALU.mult), reads=[ow_r, p_r], writes=[ow_r])
                    kb.op("dve", lambda v: v.tensor_tensor(out=v2_(ow[:, :]), in0=v2_(ow[:, :]), in1=rln[:, 1, :, :], op=ALU.add), reads=[ow_r, p_r], writes=[ow_r])
                    if rs_ <= 6:
                        continue
                    for hp in range(2):
                        kb.op("dve", lambda v: v.scalar_tensor_tensor(out=ow[:, hp * 64:(hp + 1) * 64], in0=Vstk[:, c, hp, :], scalar=cb[:, c, hp:hp + 1],
                                                                      in1=ow[:, hp * 64:(hp + 1) * 64], op0=ALU.mult, op1=ALU.add),
                              reads=[Vstk_r, cb_r, ow_r], writes=[ow_r])
                    for hh in range(2):
                        pr = slice(hh * 64, hh * 64 + 64)
                        kb.op("dve", lambda v: v.tensor_tensor(out=obblk[pr, :, hh * 64:(hh + 1) * 64], in0=v2_(ow[pr, :]), in1=gtok[pr, c, :, :], op=ALU.mult),
                              reads=[ow_r, gtok_r], writes=[ob_r])

                    if rs_ <= 7:
                        continue

                    def fn(pe):
                        inst = None
                        for hp in range(2):
                            inst = pe.matmul(PS[1][:, 256 + hp * 64:256 + (hp + 1) * 64], lhsT=obblk[:, hp, :], rhs=istk[:, :], start=True, stop=True)
                        return inst
                    kb.op("pe", fn, reads=[ob_r, p_r], writes=[PS_r[1]])
                    tokc = tok0 + c * 64
                    kb.op("act", lambda a: a.activation(out=OT[:, 2:4, tokc:tokc + 64], in_=PS[1][:, 256:384].rearrange("p (h t) -> p h t", h=2), func=AF.Copy),
                          reads=[PS_r[1]], writes=[OT_r[tokc // 128], PS_r[1]])
            nmt = S // MT if rs_ > 0 else 0
            if nmt:
                prep(0)
            for mt in range(nmt):
                if mt + 1 < nmt:
                    prep(mt + 1)
                chain(mt)
            kb.barrier()

    NQ, NQS, NKC, NKS, NKW, NVC, NVS, NGT = 0, 512, 1024, 1280, 1536, 1792, 1920, 2176
    GK = 1.5957691216057308

    def nsa_phase(sq, l, OT, OT_r):
        ns_ = cfg.get("nsa_stop", 99)
        with ExitStack() as es:
            A = lambda n_, s_, d_: es.enter_context(sb(n_, s_, d_))
            QT = A("nQT", [128, 4, S], BF16)
            KTz = A("nKTz", [128, 3, 2, S], BF16)
            VCz = A("nVCz", [128, 2, S], BF16)
            VS = A("nVS", [128, NT, 2, 65], BF16)
            VW = A("nVW", [128, NT, 2, 65], BF16)
            gsig = A("ngsig", [128, NT, 24], F32)
            kcmpTz = A("nkcmp", [128, 2, 128], BF16)
            vcmp = A("nvcmp", [128, 2, 97], BF16)
            QT_r, KT_r, VC_r, VS_r, VW_r, gs_r = regs(4), regs(4), regs(4), regs(NT), regs(NT), regs(NT)
            kc_r, vcm_r = Reg(), Reg()
            kb.op("pool", lambda g: g.memset(KTz[:], 0.0), writes=KT_r)
            kb.op("pool", lambda g: g.memset(VCz[:], 0.0), writes=VC_r)
            kb.op("pool", lambda g: g.memset(VS[:, :, :, 64:65], 1.0), writes=VS_r)
            kb.op("pool", lambda g: g.memset(VW[:, :, :, 64:65], 1.0), writes=VW_r)
            kb.op("pool", lambda g: g.memset(kcmpTz[:], 0.0), writes=[kc_r])
            kb.op("pool", lambda g: g.memset(vcmp[:], 0.0), writes=[vcm_r])
            with ExitStack() as es1:
                A1 = lambda n_, s_, d_: es1.enter_context(sb(n_, s_, d_))
                Wn = A1("nWn", [128, 8, 2200], BF16)
                rp = A1("nrp", [128, 2, 512], F32)
                t1 = A1("nt1", [128, 512], F32)
                t2 = A1("nt2", [128, 512], F32)
                W_r, rp_r, t1_r, t2_r = Reg(), Reg(), Reg(), Reg()
                load_w(Wn, w_nsa16.ap()[l], 2200, W_r)
                for mt in range(4):
                    tok0 = mt * 512
                    bl = slice(tok0, tok0 + 512)
                    xr = XT_r[mt * 4:mt * 4 + 4]
                    kb.dma("sp", rp[:, 0, :], c_rope.ap()[0][:, bl], writes=[rp_r])
                    kb.dma("sp", rp[:, 1, :], c_rope.ap()[1][:, bl], writes=[rp_r])
                    for i in range(4):
                        proj_fm(PS[0][:, :], Wn, NQ + i * 128, 128, tok0, 512, [W_r], xr, PS_r[0])
                        proj_fm(PS[1][:, :], Wn, NQS + i * 128, 128, tok0, 512, [W_r], xr, PS_r[1])
                        kb.op("dve", lambda v: v.tensor_tensor(out=t1[:, :], in0=PS[0][:, :], in1=rp[:, 0, :], op=ALU.mult),
                              reads=[PS_r[0], rp_r], writes=[t1_r])
                        kb.op("dve", lambda v: v.scalar_tensor_tensor(out=t2[:, :], in0=PS[1][:, :], scalar=0.125, in1=rp[:, 1, :],
                                                                      op0=ALU.mult, op1=ALU.mult), reads=[PS_r[1], rp_r], writes=[t2_r])
                        kb.op("dve", lambda v: v.scalar_tensor_tensor(out=QT[:, i, bl], in0=t1[:, :], scalar=0.125, in1=t2[:, :],
                                                                      op0=ALU.mult, op1=ALU.add), reads=[t1_r, t2_r], writes=[QT_r[mt]])
                    for ty, c0 in ((0, NKC), (1, NKS), (2, NKW)):
                        proj_fm(PS[0][:, :], Wn, c0, 128, tok0, 512, [W_r], xr, PS_r[0])
                        proj_fm(PS[1][:, :], Wn, c0 + 128, 128, tok0, 512, [W_r], xr, PS_r[1])
                        kb.op("dve", lambda v: v.tensor_tensor(out=t1[:, :], in0=PS[0][:, :], in1=rp[:, 0, :], op=ALU.mult),
                              reads=[PS_r[0], rp_r], writes=[t1_r])
                        kb.op("dve", lambda v: v.tensor_tensor(out=t2[:, :], in0=PS[1][:, :], in1=rp[:, 1, :], op=ALU.mult),
                              reads=[PS_r[1], rp_r], writes=[t2_r])
                        for g in range(2):
                            pr = slice(g * 64, g * 64 + 64)
                            kb.op("dve", lambda v: v.tensor_tensor(out=KTz[pr, ty, g, bl], in0=t1[pr, :], in1=t2[pr, :], op=ALU.add),
                                  reads=[t1_r, t2_r], writes=[KT_r[mt]])
                    proj_fm(PS[2][:, :], Wn, NVC, 128, tok0, 512, [W_r], xr, PS_r[2])
                    for g in range(2):
                        pr = slice(g * 64, g * 64 + 64)
                        kb.op("act", lambda a: a.activation(out=VCz[pr, g, bl], in_=PS[2][pr, :], func=AF.Copy),
                              reads=[PS_r[2]], writes=[VC_r[mt]])
                    for j in range(4):
                        tt = mt * 4 + j
                        proj_tm(PS[3][:, 0:256], Wn, NVS, 256, tt, [W_r], PS_r[3])
                        kb.op("act", lambda a: a.activation(out=VS[:, tt, :, 0:64], in_=PS[3][:, 0:128].rearrange("p (g d) -> p g d", g=2), func=AF.Copy),
                              reads=[PS_r[3]], writes=[VS_r[tt], PS_r[3]])
                        kb.op("act", lambda a: a.activation(out=VW[:, tt, :, 0:64], in_=PS[3][:, 128:256].rearrange("p (g d) -> p g d", g=2), func=AF.Copy),
                              reads=[PS_r[3]], writes=[VW_r[tt], PS_r[3]])
                        proj_tm(PS[4][:, 0:24], Wn, NGT, 24, tt, [W_r], PS_r[4])
                        kb.op("act", lambda a: a.activation(out=gsig[:, tt, :], in_=PS[4][:, 0:24], func=AF.Sigmoid),
                              reads=[PS_r[4]], writes=[gs_r[tt]])
                kb.barrier()
            if ns_ <= 1:
                kb.barrier()
                return
            with ExitStack() as es2:
                A2 = lambda n_, s_, d_: es2.enter_context(sb(n_, s_, d_))
                w1d = A2("nw1d", [128, 32, 256], BF16)
                w2d = A2("nw2d", [128, 2, 128], BF16)
                wv2 = A2("nwv2", [128, 2, 64], BF16)
                posz = A2("nposz", [128, 2, 32], BF16)
                hc = A2("nhc", [128, 2], F32)
                gx = A2("ngx", [128, 128], F32)
                gw = A2("ngw", [128, 128], F32)
                gh = A2("ngh", [128, 2, 128], BF16)
                w1_r, w2_r, hc_r, gx_r, gw_r, gh_r = Reg(), Reg(), Reg(), Reg(), Reg(), Reg()
                kb.op("pool", lambda g: g.memset(posz[:], 0.0), writes=[w2_r])
                kb.op("pool", lambda g: g.memset(gh[:], 0.0), writes=[gh_r])
                for kv in range(2):
                    kb.dma("pool", posz[0:64, kv, :], nsa_posT.ap()[l, kv], reads=[w2_r], writes=[w2_r])
                for half in range(2):
                    kb.dma("pool", w2d[:, :, half * 64:(half + 1) * 64], nsa_wk2.ap()[l].rearrange("(t p) n -> p t n", p=128), writes=[w2_r])
                kb.dma("pool", wv2[:, :, :], nsa_wv2.ap()[l].rearrange("(t p) n -> p t n", p=128), writes=[w2_r])
                kb.dma("pool", vcmp[:, 0, 65:97], c_ovl.ap(), reads=[vcm_r], writes=[vcm_r])
                kb.dma("pool", vcmp[:, 1, 65:97], c_ovl.ap(), reads=[vcm_r], writes=[vcm_r])
                kb.op("pool", lambda g: g.memset(vcmp[0:127, :, 64:65], 1.0), reads=[vcm_r], writes=[vcm_r])
                for kv, w1src in ((0, wk1_16), (1, wv1_16)):
                    src3 = w1src.ap()[l].rearrange("(l d) n -> d l n", d=64)
                    for half in range(2):
                        for l4 in range(8):
                            kb.dma("sp", w1d[half * 64:(half + 1) * 64, l4 * 4:(l4 + 1) * 4, :], src3[:, l4 * 4:(l4 + 1) * 4, :],
                                   reads=[w16_r], writes=[w1_r])
                    srcz = (lambda g: KTz[:, 0, g, :]) if kv == 0 else (lambda g: VCz[:, g, :])
                    src_regs = KT_r if kv == 0 else VC_r
                    for hf in range(2):
                        def fn(pe):
                            inst = None
                            for ll in range(32):
                                inst = pe.matmul(PS[0][:, 0:1], lhsT=w1d[:, ll, hf * 128:(hf + 1) * 128], rhs=posz[:, kv, ll:ll + 1],
                                                 start=(ll == 0), stop=(ll == 31))
                            return inst
                        kb.op("pe", fn, reads=[w1_r, w2_r], writes=[PS_r[0]])
                        kb.op("act", lambda a: a.activation(out=hc[:, hf:hf + 1], in_=PS[0][:, 0:1], func=AF.Copy), reads=[PS_r[0]], writes=[hc_r])
                    for g in range(2):
                        for hf in range(2):
                            def fn(pe):
                                inst = None
                                for ll in range(32):
                                    rhs = bass.AP(srcz(g).tensor, srcz(g).offset + ll, [srcz(g).ap[0], [16, 127]])
                                    inst = pe.matmul(PS[1][:, 0:127], lhsT=w1d[:, ll, hf * 128:(hf + 1) * 128], rhs=rhs,
                                                     start=(ll == 0), stop=(ll == 31))
                                return inst
                            kb.op("pe", fn, reads=[w1_r] + src_regs, writes=[PS_r[1]])
                            kb.op("act", lambda a: a.activation(out=gx[:, 0:127], in_=PS[1][:, 0:127], func=AF.Identity, bias=hc[:, hf:hf + 1], scale=1.0),
                                  reads=[PS_r[1], hc_r], writes=[gx_r])
                            kb.op("dve", lambda v: v.tensor_tensor(out=gw[:, 0:127], in0=gx[:, 0:127], in1=gx[:, 0:127], op=ALU.mult),
                                  reads=[gx_r], writes=[gw_r])
                            kb.op("dve", lambda v: v.tensor_scalar(out=gw[:, 0:127], in0=gw[:, 0:127], scalar1=0.044715, scalar2=1.0,
                                                                   op0=ALU.mult, op1=ALU.add), reads=[gw_r], writes=[gw_r])
                            kb.op("dve", lambda v: v.tensor_tensor(out=gw[:, 0:127], in0=gw[:, 0:127], in1=gx[:, 0:127], op=ALU.mult),
                                  reads=[gw_r, gx_r], writes=[gw_r])
                            kb.op("act", lambda a: a.activation(out=gw[:, 0:127], in_=gw[:, 0:127], func=AF.Sigmoid, scale=GK), reads=[gw_r], writes=[gw_r])
                            kb.op("dve", lambda v: v.tensor_tensor(out=gh[:, hf, 0:127], in0=gx[:, 0:127], in1=gw[:, 0:127], op=ALU.mult),
                                  reads=[gw_r, gx_r], writes=[gh_r])
                        if kv == 0:
                            def fn(pe):
                                pe.matmul(PS[2][:, 0:127], lhsT=w2d[:, 0, :], rhs=gh[:, 0, 0:127], start=True, stop=False)
                                return pe.matmul(PS[2][:, 0:127], lhsT=w2d[:, 1, :], rhs=gh[:, 1, 0:127], start=False, stop=True)
                            kb.op("pe", fn, reads=[w2_r, gh_r], writes=[PS_r[2]])
                            pr = slice(g * 64, g * 64 + 64)
                            kb.op("act", lambda a: a.activation(out=kcmpTz[pr, g, 0:127], in_=PS[2][pr, 0:127], func=AF.Copy),
                                  reads=[PS_r[2], kc_r], writes=[kc_r])
                        else:
                            def fn(pe):
                                pe.matmul(PS[2][:, 0:64], lhsT=gh[:, 0, :], rhs=wv2[:, 0, :], start=True, stop=False)
                                return pe.matmul(PS[2][:, 0:64], lhsT=gh[:, 1, :], rhs=wv2[:, 1, :], start=False, stop=True)
                            kb.op("pe", fn, reads=[w2_r, gh_r], writes=[PS_r[2]])
                            kb.op("act", lambda a: a.activation(out=vcmp[0:127, g, 0:64], in_=PS[2][0:127, 0:64], func=AF.Copy),
                                  reads=[PS_r[2], vcm_r], writes=[vcm_r])
                kb.barrier()
            if ns_ <= 2:
                kb.barrier()
                return
            selbT = A("nselbT", [128, 2, S], BF16)
            onehot = A("nonehot", [128, S], BF16)
            cmask = A("ncmask", [128, S], BF16)
            selm = A("nselm", [128, 2, NT, 32], F32)
            wmask = A("nwmask", [128, 128], F32)
            PT = [A("nPT%d" % i, [128, 512], BF16) for i in range(2)]
            PTs = [A("nPTs%d" % i, [128, 512], BF16) for i in range(4)]
            exa = [A("nexa%d" % i, [128, 512], F32) for i in range(2)]
            exa_r = regs(2)
            ocsa = [A("nocsa%d" % i, [128, 4, 97], F32) for i in range(2)]
            ocsa_r = regs(2)
            dna = [A("ndna%d" % i, [128, 16], F32) for i in range(2)]
            dna_r = regs(2)
            scl = [A("nscl%d" % i, [128, 32], F32) for i in range(2)]
            cm3l = [A("ncm3l%d" % i, [128, 32, 32], F32) for i in range(2)]
            sb16l = [A("nsb16l%d" % i, [128, 32], BF16) for i in range(2)]
            scl_r, cm3l_r, sb16l_r = regs(2), regs(2), regs(2)
            PTw = [A("nPTw%d" % i, [128, 640], BF16) for i in range(4)]
            PTw_r = regs(4)
            exw = [A("nexw%d" % i, [128, 128], F32) for i in range(2)]
            exw_r = regs(2)
            ocsw = [A("nocsw%d" % i, [128, 4, 65], F32) for i in range(2)]
            ocsw_r = regs(2)
            dnw = [A("ndnw%d" % i, [128, 16], F32) for i in range(2)]
            dnw_r = regs(2)
            PTs_r = regs(4)
            ocss = [A("nocss%d" % i, [128, 4, 65], F32) for i in range(2)]
            ocss_r = regs(2)
            dns = [A("ndns%d" % i, [128, 16], F32) for i in range(2)]
            dns_r = regs(2)
            ONSA2 = [A("nONSA%d" % i_, [128, 4, 512], F32) for i_ in range(2)]
            impb2 = [A("nimp%d" % i_, [128, 4, 2, 32], F32) for i_ in range(2)]
            obf = A("nobf", [128, 512], BF16)
            k_r, sel_r, PT_r, ex_r, ocs_r, sc_r, cm3_r, sb16_r, dn_r, obf_r = \
                Reg(), regs(4), regs(2), Reg(), Reg(), Reg(), Reg(), Reg(), Reg(), Reg()
            on_r2, imp_r2 = [regs(4), regs(4)], regs(2)
            kb.op("pool", lambda g: g.memset(selbT[:], 0.0), writes=sel_r)
            kb.op("pool", lambda g: g.memset(onehot[:], 0.0), writes=[k_r])
            kb.dma("pool", onehot[0:32, :], c_onehot.ap(), reads=[k_r], writes=[k_r])
            kb.dma("pool", cmask[:, :], c_cmpmask.ap(), writes=[k_r])
            for m_ in range(2):
                kb.dma("sp", selm[:, m_, :, :], c_selm.ap()[m_], writes=[k_r])
            kb.op("dve", lambda v: v.tensor_scalar(out=wmask[:, :], in0=caus[:, 0, :], scalar1=-1.0, scalar2=1.0, op0=ALU.mult, op1=ALU.add),
                  reads=[c_r], writes=[k_r])

            def bf_math(buf, buf_r, dnv, dnv_r, nq, h, b, tts, first, ONSA, on_r):
                kb.op("dve", lambda v: v.tensor_scalar(out=dnv[:, 0:nq], in0=buf[:, 0:nq, 64], scalar1=1e-30, scalar2=None, op0=ALU.max),
                      reads=[buf_r], writes=[dnv_r])
                kb.op("dve", lambda v: v.reciprocal(out=dnv[:, 0:nq], in_=dnv[:, 0:nq]), reads=[dnv_r], writes=[dnv_r])
                kb.op("dve", lambda v: v.tensor_tensor(out=dnv[:, 8:8 + nq], in0=dnv[:, 0:nq], in1=gsig[:, tts[0]:tts[0] + nq, h * 3 + b], op=ALU.mult),
                      reads=[dnv_r] + gs_r[tts[0]:tts[0] + nq], writes=[dnv_r])
                for qi in range(nq):
                    tl = tts[qi] % 4
                    if first:
                        kb.op("dve", lambda v: v.tensor_scalar(out=ONSA[:, tl, h * 64:(h + 1) * 64], in0=buf[:, qi, 0:64], scalar1=dnv[:, 8 + qi:9 + qi],
                                                               scalar2=None, op0=ALU.mult), reads=[buf_r, dnv_r], writes=[on_r[tl]])
                    else:
                        kb.op("dve", lambda v: v.scalar_tensor_tensor(out=ONSA[:, tl, h * 64:(h + 1) * 64], in0=buf[:, qi, 0:64], scalar=dnv[:, 8 + qi:9 + qi],
                                                                      in1=ONSA[:, tl, h * 64:(h + 1) * 64], op0=ALU.mult, op1=ALU.add),
                              reads=[buf_r, dnv_r, on_r[tl]], writes=[on_r[tl]])

            def nsa_a(qb):
                qc = slice(qb * 512, (qb + 1) * 512)
                tts = list(range(qb * 4, qb * 4 + 4))
                ONSA, on_r, impb, imp_r = ONSA2[qb % 2], on_r2[qb % 2], impb2[qb % 2], imp_r2[qb % 2]
                for h in range(8):
                    g, i = h // 4, h % 4
                    pi = h % 2
                    kb.op("pe", lambda pe: pe.matmul(PS[pi][:, :], lhsT=kcmpTz[:, g, :], rhs=QT[:, i, qc], start=True, stop=True),
                          reads=[kc_r, QT_r[qb]], writes=[PS_r[pi]])
                    kb.op("act", lambda a: a.activation(out=exa[pi][:, :], in_=PS[pi][:, :], func=AF.Exp), reads=[PS_r[pi]], writes=[exa_r[pi]])
                    kb.op("dve", lambda v: v.tensor_tensor(out=PT[pi][:, 0:512], in0=exa[pi][:, :], in1=cmask[:, qc], op=ALU.mult),
                          reads=[exa_r[pi], k_r], writes=[PT_r[pi]])
                    ai = 2 + (h % 2)

                    def fn(pe):
                        inst = None
                        for q in range(4):
                            inst = pe.matmul(PS[ai][:, q * 97:(q + 1) * 97], lhsT=PT[pi][:, q * 128:(q + 1) * 128], rhs=vcmp[:, g, :], start=True, stop=True)
                        return inst
                    kb.op("pe", fn, reads=[PT_r[pi], vcm_r], writes=[PS_r[ai]])
                    kb.op("act", lambda a: a.activation(out=ocsa[pi][:, :, :], in_=PS[ai][:, 0:388].rearrange("p (q w) -> p q w", q=4), func=AF.Copy),
                          reads=[PS_r[ai]], writes=[ocsa_r[pi], PS_r[ai]])
                    bf_math(ocsa[pi], ocsa_r[pi], dna[pi], dna_r[pi], 4, h, 0, tts, True, ONSA, on_r)
                    for q in range(4):
                        if i == 0:
                            kb.op("dve", lambda v: v.tensor_scalar(out=impb[:, q, g, :], in0=ocsa[pi][:, q, 65:97], scalar1=dna[pi][:, q:q + 1], scalar2=None, op0=ALU.mult),
                                  reads=[ocsa_r[pi], dna_r[pi]], writes=[imp_r])
                        else:
                            kb.op("dve", lambda v: v.scalar_tensor_tensor(out=impb[:, q, g, :], in0=ocsa[pi][:, q, 65:97], scalar=dna[pi][:, q:q + 1], in1=impb[:, q, g, :],
                                                                          op0=ALU.mult, op1=ALU.add), reads=[ocsa_r[pi], dna_r[pi], imp_r], writes=[imp_r])
            def nsa_b(qb):
                qc = slice(qb * 512, (qb + 1) * 512)
                tts = list(range(qb * 4, qb * 4 + 4))
                ONSA, on_r, impb, imp_r = ONSA2[qb % 2], on_r2[qb % 2], impb2[qb % 2], imp_r2[qb % 2]
                for q in range(4):
                    tt = tts[q]
                    for g in range(2):
                        sc, cm3, sb16, sc_r, cm3_r, sb16_r = scl[g], cm3l[g], sb16l[g], scl_r[g], cm3l_r[g], sb16l_r[g]
                        pbi = g
                        kb.op("dve", lambda v: v.tensor_tensor(out=sc[:, :], in0=impb[:, q, g, :], in1=selm[:, 0, tt, :], op=ALU.mult),
                              reads=[imp_r, k_r], writes=[sc_r])
                        kb.op("dve", lambda v: v.tensor_tensor(out=sc[:, :], in0=sc[:, :], in1=selm[:, 1, tt, :], op=ALU.add),
                              reads=[sc_r, k_r], writes=[sc_r])
                        kb.op("dve", lambda v: v.tensor_tensor(out=cm3[:, :, :], in0=bass.AP(sc, 0, [[32, 128], [0, 32], [1, 32]]),
                                                              in1=bass.AP(sc, 0, [[32, 128], [1, 32], [0, 32]]), op=ALU.is_gt),
                              reads=[sc_r], writes=[cm3_r])
                        kb.op("dve", lambda v: v.reduce_sum(out=sc[:, :], in_=cm3[:, :, :], axis=AX.X), reads=[cm3_r, sc_r], writes=[sc_r])
                        kb.op("dve", lambda v: v.tensor_scalar(out=sb16[:, :], in0=sc[:, :], scalar1=15.5, scalar2=-30000.0, op0=ALU.is_gt, op1=ALU.mult),
                              reads=[sc_r], writes=[sb16_r])
                        kb.op("pe", lambda pe: pe.transpose(out=PB[pbi][0:32, 0:128], in_=sb16[:, :], identity=ident[:, :]),
                              reads=[sb16_r, c_r], writes=[PB_r[pbi]])
                        kb.op("act", lambda a: a.activation(out=selbT[0:32, g, tt * 128:(tt + 1) * 128], in_=PB[pbi][0:32, 0:128], func=AF.Copy),
                              reads=[PB_r[pbi]], writes=[sel_r[qb], PB_r[pbi]])
            def nsa_c(qb):
                qc = slice(qb * 512, (qb + 1) * 512)
                tts = list(range(qb * 4, qb * 4 + 4))
                ONSA, on_r, impb, imp_r = ONSA2[qb % 2], on_r2[qb % 2], impb2[qb % 2], imp_r2[qb % 2]
                for h in range(8):
                    g, i = h // 4, h % 4
                    nkt = 4 * qb + 4
                    for kt in range(nkt):
                        kc_ = slice(kt * 128, (kt + 1) * 128)
                        pi = kt % 2

                        def fn(pe):
                            pe.matmul(PS[pi][:, :], lhsT=KTz[:, 1, g, kc_], rhs=QT[:, i, qc], start=True, stop=False)
                            return pe.matmul(PS[pi][:, :], lhsT=onehot[:, kc_], rhs=selbT[:, g, qc], start=False, stop=True)
                        kb.op("pe", fn, reads=[KT_r[kt // 4], QT_r[qb], k_r, sel_r[qb]], writes=[PS_r[pi]])
                        pt = kt % 4
                        kb.op("act", lambda a: a.activation(out=PTs[pt][:, 0:512], in_=PS[pi][:, :], func=AF.Exp), reads=[PS_r[pi]], writes=[PTs_r[pt], PS_r[pi]])
                        if kt >= 4 * qb:
                            ql = kt - 4 * qb
                            kb.op("dve", lambda v: v.tensor_tensor(out=PTs[pt][:, ql * 128:(ql + 1) * 128], in0=PTs[pt][:, ql * 128:(ql + 1) * 128],
                                                                  in1=caus[:, 0, :], op=ALU.mult), reads=[PTs_r[pt], c_r], writes=[PTs_r[pt]])
                        for q in range(4):
                            qt = 4 * qb + q
                            if qt < kt:
                                continue
                            kb.op("pe", lambda pe: pe.matmul(PS[2 + q][:, 0:65], lhsT=PTs[pt][:, q * 128:(q + 1) * 128], rhs=VS[:, kt, g, :],
                                                             start=(kt == 0), stop=(kt == qt)), reads=[PTs_r[pt], VS_r[kt]], writes=[PS_r[2 + q]])
                    hb = h % 2
                    for q in range(4):
                        kb.op("act", lambda a: a.activation(out=ocss[hb][:, q, :], in_=PS[2 + q][:, 0:65], func=AF.Copy),
                              reads=[PS_r[2 + q]], writes=[ocss_r[hb], PS_r[2 + q]])
                    bf_math(ocss[hb], ocss_r[hb], dns[hb], dns_r[hb], 4, h, 1, tts, False, ONSA, on_r)
            def nsa_d(qb):
                qc = slice(qb * 512, (qb + 1) * 512)
                tts = list(range(qb * 4, qb * 4 + 4))
                ONSA, on_r, impb, imp_r = ONSA2[qb % 2], on_r2[qb % 2], impb2[qb % 2], imp_r2[qb % 2]
                for h in range(8):
                    g, i = h // 4, h % 4
                    for q in range(4):
                        qt = 4 * qb + q
                        qcs = slice(qt * 128, (qt + 1) * 128)
                        kts = [kt for kt in range(qt - 4, qt + 1) if kt >= 0]
                        pi = q % 2
                        pw = q % 4
                        main = [kt for kt in kts if kt >= qt - 3]

                        def fn(pe):
                            inst = None
                            for n_, kt in enumerate(main):
                                inst = pe.matmul(PS[pi][:, n_ * 128:(n_ + 1) * 128], lhsT=KTz[:, 2, g, kt * 128:(kt + 1) * 128], rhs=QT[:, i, qcs],
                                                 start=True, stop=True)
                            return inst
                        kb.op("pe", fn, reads=KT_r + [QT_r[qb]], writes=[PS_r[pi]])
                        nm = len(main)
                        kb.op("act", lambda a: a.activation(out=PTw[pw][:, 0:nm * 128], in_=PS[pi][:, 0:nm * 128], func=AF.Exp), reads=[PS_r[pi]], writes=[PTw_r[pw]])
                        kb.op("dve", lambda v: v.tensor_tensor(out=PTw[pw][:, (nm - 1) * 128:nm * 128], in0=PTw[pw][:, (nm - 1) * 128:nm * 128],
                                                              in1=caus[:, 0, :], op=ALU.mult), reads=[PTw_r[pw], c_r], writes=[PTw_r[pw]])
                        tail = (qt - 4 >= 0)
                        if tail:
                            kt = qt - 4
                            kb.op("pe", lambda pe: pe.matmul(PS[2 + pi][:, 0:128], lhsT=KTz[:, 2, g, kt * 128:(kt + 1) * 128], rhs=QT[:, i, qcs],
                                                             start=True, stop=True), reads=KT_r + [QT_r[qb]], writes=[PS_r[2 + pi]])
                            kb.op("act", lambda a: a.activation(out=exw[pi][:, 0:128], in_=PS[2 + pi][:, 0:128], func=AF.Exp), reads=[PS_r[2 + pi]], writes=[exw_r[pi]])
                            kb.op("dve", lambda v: v.tensor_tensor(out=PTw[pw][:, 512:640], in0=exw[pi][:, 0:128], in1=wmask[:, :], op=ALU.mult),
                                  reads=[exw_r[pi], k_r, PTw_r[pw]], writes=[PTw_r[pw]])

                        def fn(pe):
                            inst = None
                            seq_ = [(n_, kt) for n_, kt in enumerate(main)] + ([(4, qt - 4)] if tail else [])
                            for idx, (n_, kt) in enumerate(seq_):
                                inst = pe.matmul(PS[4][:, q * 65:(q + 1) * 65], lhsT=PTw[pw][:, n_ * 128:(n_ + 1) * 128], rhs=VW[:, kt, g, :],
                                                 start=(idx == 0), stop=(idx == len(seq_) - 1))
                            return inst
                        kb.op("pe", fn, reads=[PTw_r[pw]] + VW_r[max(0, qt - 4):qt + 1], writes=[PS_r[4]])
                    hb = h % 2
                    kb.op("act", lambda a: a.activation(out=ocsw[hb][:, :, :], in_=PS[4][:, 0:260].rearrange("p (q w) -> p q w", q=4), func=AF.Copy),
                          reads=[PS_r[4]], writes=[ocsw_r[hb], PS_r[4]])
                    bf_math(ocsw[hb], ocsw_r[hb], dnw[hb], dnw_r[hb], 4, h, 2, tts, False, ONSA, on_r)
            def nsa_e(qb):
                qc = slice(qb * 512, (qb + 1) * 512)
                tts = list(range(qb * 4, qb * 4 + 4))
                ONSA, on_r, impb, imp_r = ONSA2[qb % 2], on_r2[qb % 2], impb2[qb % 2], imp_r2[qb % 2]
                for q in range(4):
                    tt = tts[q]
                    kb.op("act", lambda a: a.activation(out=obf[:, :], in_=ONSA[:, q, :], func=AF.Copy), reads=[on_r[q]], writes=[obf_r])
                    for half in range(2):
                        out_to_OT(obf[:, half * 256:(half + 1) * 256], obf_r, 128, OT, OT_r, 4 + 2 * half, tt * 128)
            for qb in range(4):
                if qb == 0:
                    nsa_a(0)
                if qb + 1 < 4:
                    nsa_a(qb + 1)
                if ns_ > 3:
                    nsa_b(qb)
                if ns_ > 5:
                    nsa_d(qb)
                if ns_ > 4:
                    nsa_c(qb)
                if ns_ > 6:
                    nsa_e(qb)
            kb.barrier()

    for sq in range(NSEQ):
        for l in range(NLAY):
            if l == 0:
                with sb("xstage", [128, 2, D], F32) as xst:
                    xst_r = regs(2)
                    for tt in range(NT):
                        i = tt % 2
                        kb.dma("sp", xst[:, i, :], x_d.ap()[sq, tt * 128:(tt + 1) * 128, :], writes=[xst_r[i]])
                        make_xT(xst[:, i, :], xst_r[i], tt)
                    kb.barrier()
            if cfg.get("stop") == "xt":
                continue
            res_src = (lambda tt: x_d.ap()[sq, tt * 128:(tt + 1) * 128, :]) if l == 0 else \
                      (lambda tt: xres[1].ap()[tt * 128:(tt + 1) * 128, :])
            res_regs = None if l == 0 else xres_r[1]

            with sb("OT", [128, 8, S], BF16) as OT:
                OT_r = regs(NT)
                if inject_O:
                    for k in range(8):
                        kb.dma("pool", OT[:, k, :], dbg_OT.ap()[:, k, :], writes=OT_r)
                if "gla" in mixers:
                    gla_phase(sq, l, OT, OT_r)
                if "rwkv" in mixers:
                    rwkv_phase(sq, l, OT, OT_r)
                if "nsa" in mixers:
                    nsa_phase(sq, l, OT, OT_r)
                if "OT" in dump_d:
                    for k in range(8):
                        with sb("otd", [128, S], F32) as otd:
                            r_ = Reg()
                            kb.op("act", lambda a: a.activation(out=otd[:], in_=OT[:, k, :], func=AF.Copy),
                                  reads=OT_r, writes=[r_])
                            dump("OT", otd[:], r_, idx=k)
                            kb.barrier()
                if cfg.get("stop") == "outproj0":
                    continue
                with sb("Wo", [128, 8, D], BF16) as Wo, \
                        sb("ln1", [128, 2, D], F32) as ln1, \
                        sb("xs1", [128, 2, D], F32) as xs1, \
                        sb("st1", [128, 2, 32], F32) as st1:
                    Wo_r = Reg()
                    ln_r = Reg()
                    xs_r = regs(2)
                    st_r = regs(2)
                    load_w(Wo, w_out16.ap()[l], D, Wo_r)
                    kb.dma("sp", ln1[:, 0, :], bcast_rows(rowtab, l * 5120 + 1024, D), writes=[ln_r])
                    kb.dma("sp", ln1[:, 1, :], bcast_rows(rowtab, l * 5120 + 2048, D), writes=[ln_r])
                    for tt in range(NT):
                        i = tt % 2
                        kb.dma("sp", xs1[:, i, :], res_src(tt), reads=([res_regs[tt]] if res_regs else []), writes=[xs_r[i]])
                        for hf in range(2):
                            def fn(pe, hf=hf):
                                inst = None
                                for k in range(8):
                                    inst = pe.matmul(PS[hf][:, :], lhsT=OT[:, k, tt * 128:(tt + 1) * 128],
                                                     rhs=Wo[:, k, hf * 512:(hf + 1) * 512], start=(k == 0), stop=(k == 7))
                                return inst
                            kb.op("pe", fn, reads=[OT_r[tt], Wo_r], writes=[PS_r[hf]])
                            kb.op("dve", lambda v, hf=hf: v.scalar_tensor_tensor(
                                out=xs1[:, i, hf * 512:(hf + 1) * 512], in0=xs1[:, i, hf * 512:(hf + 1) * 512], scalar=ALPHA,
                                in1=PS[hf][:, :], op0=ALU.mult, op1=ALU.add), reads=[PS_r[hf], xs_r[i]], writes=[xs_r[i]])
                        layer_norm(xs1[:, i, :], xs_r[i], ln1[:, 0, :], ln1[:, 1, :], ln_r, st1[:, i, :], st_r[i])
                        kb.dma("sp", xres[0].ap()[tt * 128:(tt + 1) * 128, :], xs1[:, i, :], reads=[xs_r[i]], writes=[xres_r[0][tt]])
                        make_xT(xs1[:, i, :], xs_r[i], tt)
                        if "x1" in dump_d and sq == 0 and l == cfg.get("dump_layer", 0):
                            dump("x1", xs1[:, i, :], xs_r[i], idx=tt)
                    kb.barrier()
            if cfg.get("stop") in ("outproj", "outproj0"):
                continue
            with sb("aT", [128, NFC, 1024], BF16) as aT, \
                    sb("Wd", [128, NFC, D], BF16) as Wd, \
                    sb("Wgu", [128, 2, 2, 8, 512], BF16) as Wgu, \
                    sb("sg", [128, 2, 512], F32) as sg, \
                    sb("ln2", [128, 2, D], F32) as ln2, \
                    sb("xs2", [128, 2, D], F32) as xs2, \
                    sb("st2", [128, 2, 32], F32) as st2:
                Wd_r = Reg()
                ln_r = Reg()
                aT_r = regs(NFC)
                Wgu_r = regs(2)
                sg_r = regs(2)
                xs_r = regs(2)
                st_r = regs(2)
                kb.dma("sp", ln2[:, 0, :], bcast_rows(rowtab, l * 5120 + 3072, D), writes=[ln_r])
                kb.dma("sp", ln2[:, 1, :], bcast_rows(rowtab, l * 5120 + 4096, D), writes=[ln_r])
                for k in range(NFC):
                    for c0 in (0, 512):
                        kb.dma("sp", Wd[:, k, c0:c0 + 512], w_down16.ap()[l, k * 128:(k + 1) * 128, c0:c0 + 512], reads=[w16_r], writes=[Wd_r])
                last = (l == NLAY - 1)
                for mt in range(2):
                    tok0 = mt * 1024
                    for hc in range(NFC):
                        cg, ci = hc // 4, hc % 4
                        wi = cg % 2
                        if ci == 0:
                            ncol = min(512, FF - cg * 512)
                            for gu, wsrc in ((0, w_gate16), (1, w_up16)):
                                for k in range(8):
                                    kb.dma("sp", Wgu[:, wi, gu, k, 0:ncol],
                                           wsrc.ap()[l, k * 128:(k + 1) * 128, cg * 512:cg * 512 + ncol],
                                           reads=[w16_r], writes=[Wgu_r[wi]])
                        for blk in range(2):
                            t0 = tok0 + blk * 512
                            pg, pu = (0, 1) if blk == 0 else (2, 3)
                            for gu, pi in ((0, pg), (1, pu)):
                                def fn(pe, gu=gu, pi=pi):
                                    inst = None
                                    for k in range(8):
                                        inst = pe.matmul(PS[pi][:, :], lhsT=Wgu[:, wi, gu, k, ci * 128:(ci + 1) * 128],
                                                         rhs=XT[:, k, t0:t0 + 512], start=(k == 0), stop=(k == 7))
                                    return inst
                                kb.op("pe", fn, reads=[Wgu_r[wi]] + XT_r[t0 // 128:t0 // 128 + 4], writes=[PS_r[pi]])
                            kb.op("act", lambda a: a.activation(out=sg[:, blk, :], in_=PS[pg][:, :], func=AF.Silu),
                                  reads=[PS_r[pg]], writes=[sg_r[blk]])
                            kb.op("dve", lambda v: v.tensor_tensor(out=aT[:, hc, blk * 512:(blk + 1) * 512], in0=sg[:, blk, :],
                                                                  in1=PS[pu][:, :], op=ALU.mult),
                                  reads=[sg_r[blk], PS_r[pu]], writes=[aT_r[hc]])
                    for t8 in range(8):
                        tt = mt * 8 + t8
                        i = tt % 2
                        kb.dma("sp", xs2[:, i, :], xres[0].ap()[tt * 128:(tt + 1) * 128, :], reads=[xres_r[0][tt]], writes=[xs_r[i]])
                        for hf in range(2):
                            pi = 4 + hf

                            def fn(pe, hf=hf, pi=pi):
                                inst = None
                                for k in range(NFC):
                                    inst = pe.matmul(PS[pi][:, :], lhsT=aT[:, k, t8 * 128:(t8 + 1) * 128],
                                                     rhs=Wd[:, k, hf * 512:(hf + 1) * 512], start=(k == 0), stop=(k == NFC - 1))
                                return inst
                            kb.op("pe", fn, reads=aT_r + [Wd_r], writes=[PS_r[pi]])
                            kb.op("dve", lambda v, hf=hf, pi=pi: v.scalar_tensor_tensor(
                                out=xs2[:, i, hf * 512:(hf + 1) * 512], in0=xs2[:, i, hf * 512:(hf + 1) * 512], scalar=ALPHA,
                                in1=PS[pi][:, :], op0=ALU.mult, op1=ALU.add), reads=[PS_r[pi], xs_r[i]], writes=[xs_r[i]])
                        layer_norm(xs2[:, i, :], xs_r[i], ln2[:, 0, :], ln2[:, 1, :], ln_r, st2[:, i, :], st_r[i])
                        if last:
                            kb.dma("sp", out_d.ap()[sq, tt * 128:(tt + 1) * 128, :], xs2[:, i, :], reads=[xs_r[i]])
                        else:
                            kb.dma("sp", xres[1].ap()[tt * 128:(tt + 1) * 128, :], xs2[:, i, :], reads=[xs_r[i]], writes=[xres_r[1][tt]])
                            make_xT(xs2[:, i, :], xs_r[i], tt)
                        if "x2" in dump_d and sq == 0 and l == cfg.get("dump_layer", 0):
                            dump("x2", xs2[:, i, :], xs_r[i], idx=tt)
                    if not last:
                        pass
                kb.barrier()
    kb.finish()
    return kb


def host_consts():
    c = {}
    c["c_ident"] = np.eye(128, dtype=np.float32)
    s = np.arange(128)
    c["c_caus"] = (s[:, None] <= s[None, :]).astype(np.float32)
    half = 32
    inv = (10000.0 ** (-np.arange(half, dtype=np.float32) / half)).astype(np.float32)
    ang = (np.arange(S, dtype=np.float32)[:, None] * inv[None, :]).astype(np.float32)
    cos = np.cos(ang).astype(np.float32).T
    sin = np.sin(ang).astype(np.float32).T
    cosT = np.concatenate([cos, cos, cos, cos], 0)
    sinT = np.concatenate([-sin, sin, -sin, sin], 0)
    c["c_rope"] = np.stack([cosT, sinT]).astype(np.float32)
    cc = np.arange(128)
    t = np.arange(S)
    c["c_cmpmask"] = ((16 * cc[:, None] + 31 <= t[None, :]) & (cc[:, None] < 127)).astype(np.float32)
    j = np.arange(32)
    c["c_onehot"] = ((t[None, :] // 64) == j[:, None]).astype(np.float32)
    cur = t // 64
    forced = (j[None, :] == 0) | (j[None, :] == cur[:, None]) | (j[None, :] == cur[:, None] - 1)
    future = j[None, :] > cur[:, None]
    m1 = (~forced & ~future).astype(np.float32)
    m2 = np.where(forced, 1e9, np.where(future, -1e9, 0.0)).astype(np.float32)
    selm = np.stack([m1, m2])
    c["c_selm"] = np.ascontiguousarray(selm.reshape(2, NT, 128, 32).transpose(0, 2, 1, 3))
    c0 = np.arange(127) * 16
    s0 = np.arange(32) * 64
    lo = np.maximum(c0[:, None], s0[None, :])
    hi = np.minimum(c0[:, None] + 32, s0[None, :] + 64)
    ov = np.zeros((128, 32), np.float32)
    ov[:127] = np.maximum(hi - lo, 0) / 16
    c["c_ovl"] = ov
    blk = np.zeros((128, 128), np.float32)
    blk[:64, :64] = 1
    blk[64:, 64:] = 1
    c["c_blk"] = blk
    hs = np.zeros((128, 2), np.float32)
    hs[:64, 0] = 1
    hs[64:, 1] = 1
    c["c_hsel"] = hs
    i64 = np.arange(64)
    lo_strict = (i64[None, :] < i64[:, None]).astype(np.float32)
    up_strict = (i64[:, None] < i64[None, :]).astype(np.float32)
    up_incl = (i64[:, None] <= i64[None, :]).astype(np.float32)
    def bd(m):
        o = np.zeros((128, 128), np.float32)
        o[:64, :64] = m
        o[64:, 64:] = m
        return o
    c["c_rwmask2"] = np.ascontiguousarray(np.stack([bd(lo_strict), bd(up_strict), bd(up_incl)], axis=1))
    c["c_istk"] = np.concatenate([np.eye(64, dtype=np.float32), np.eye(64, dtype=np.float32)], axis=0)
    return c


def host_layout(inp):
    d = {}
    f = lambda a: np.ascontiguousarray(np.asarray(a, dtype=np.float32))
    for k in ("w_in", "w_in_vres", "gla_w_a2", "rwkv_w2", "rwkv_a2", "rwkv_v2", "rwkv_g2", "nsa_wk1", "nsa_wk2",
              "nsa_wv1", "nsa_wv2", "w_out", "ffn_w_gate", "ffn_w_up", "ffn_w_down"):
        d[k] = f(inp[k])
    base = 2096
    sw = lambda b: list(range(b + 32, b + 64)) + list(range(b, b + 32))
    pl = lambda b: list(range(b, b + 64))
    cols = []
    for i in range(4):
        cols += pl(base + i * 64) + pl(base + (4 + i) * 64)
    for i in range(4):
        cols += sw(base + i * 64) + sw(base + (4 + i) * 64)
    for c0 in (512, 768, 1024):
        cols += pl(base + c0) + pl(base + c0 + 64)
        cols += sw(base + c0) + sw(base + c0 + 64)
    cols += list(range(base + 640, base + 768)) + list(range(base + 896, base + 1024)) + list(range(base + 1152, base + 1280))
    cols += list(range(base + 1280, base + 1304))
    assert len(cols) == 2200
    d["w_nsa"] = f(np.asarray(inp["w_in"])[:, :, cols])
    d["nsa_posT"] = f(np.stack([np.asarray(inp["nsa_pos_k"]).transpose(0, 2, 1),
                                np.asarray(inp["nsa_pos_v"]).transpose(0, 2, 1)], axis=1))
    pt = np.zeros((L, 128, 32), np.float32)
    mu = np.asarray(inp["rwkv_mu"])
    rwt = [(0, 128), (128, 128), (256, 128), (384, 128), (512, 128), (640, 128), (768, 64), (832, 64), (896, 128), (1024, 32)]
    for l in range(L):
        pt[l, :, 0:2] = np.asarray(inp["gla_b_a"])[l].reshape(2, 128).T
        for i, (c0, n) in enumerate(rwt):
            pt[l, :n, 2 + i] = mu[l, c0:c0 + n]
        if l >= 1:
            pt[l, :32, 12] = np.asarray(inp["rwkv_mu_vres"])[l - 1]
            pt[l, :, 17:19] = np.asarray(inp["rwkv_v0"])[l - 1].reshape(2, 128).T
        pt[l, :, 13:15] = np.asarray(inp["rwkv_w0"])[l].reshape(2, 128).T
        pt[l, :, 15:17] = np.asarray(inp["rwkv_a0"])[l].reshape(2, 128).T
        pt[l, :, 19:21] = np.asarray(inp["rwkv_k_k"])[l].reshape(2, 128).T
        pt[l, :, 21:23] = np.asarray(inp["rwkv_k_a"])[l].reshape(2, 128).T
        pt[l, :, 23:25] = np.asarray(inp["rwkv_r_k"])[l].reshape(2, 128).T
    d["ptab"] = pt
    rt = np.zeros((L, 5120), np.float32)
    for l in range(L):
        rt[l, 0:256] = np.asarray(inp["gla_ln_w"])[l]
        rt[l, 256:512] = np.asarray(inp["gla_ln_b"])[l]
        rt[l, 512:768] = np.asarray(inp["rwkv_ln_w"])[l]
        rt[l, 768:1024] = np.asarray(inp["rwkv_ln_b"])[l]
        rt[l, 1024:2048] = np.asarray(inp["ln1_w"])[l]
        rt[l, 2048:3072] = np.asarray(inp["ln1_b"])[l]
        rt[l, 3072:4096] = np.asarray(inp["ln2_w"])[l]
        rt[l, 4096:5120] = np.asarray(inp["ln2_b"])[l]
    d["rowtab"] = rt
    return d


_CACHE = {}


def kernel(**inputs):
    cfg = {}
    if "full" not in _CACHE:
        _CACHE["full"] = build(cfg)
    kb = _CACHE["full"]
    shared = host_layout(inputs)
    shared.update(host_consts())
    x = np.ascontiguousarray(np.asarray(inputs["x"], dtype=np.float32))
    in_maps = []
    for c in range(8):
        m = dict(shared)
        m["x"] = x[2 * c:2 * c + 2]
        in_maps.append(m)
    res = run_bass_kernel_spmd(kb.nc, in_maps, core_ids=list(range(8)))
    return np.concatenate([r["out"] for r in res.results], axis=0).astype(np.float32)
```

```python
import math
from contextlib import ExitStack
import numpy as np
import concourse.bass as bass
import concourse.mybir as mybir
from concourse.bass_utils import run_bass_kernel_spmd

F32 = mybir.dt.float32
BF16 = mybir.dt.bfloat16
AF = mybir.ActivationFunctionType
ALU = mybir.AluOpType
AX = mybir.AxisListType
Q_DEF = 300.0

S = 2048
D = 1024
NT = S // 128
L = 2
FF = 2816
NFC = FF // 128
ALPHA = float((2 * L) ** 0.25)
LN_EPS = 1e-5
RW_EPS = 64e-5
NDS = 6


class Reg:
    __slots__ = ("lw", "rd")

    def __init__(self):
        self.lw = None
        self.rd = {}


def regs(n):
    return [Reg() for _ in range(n)]


class _Rec:
    def __init__(self):
        self.calls = []

    def __getattr__(self, name):
        def f(*a, **kw):
            self.calls.append((name, a, kw))
            return self
        return f


class _Node:
    __slots__ = ("e", "kind", "calls", "reads", "writes", "cost", "deps")

    def __init__(self, e, kind, calls, reads, writes, cost):
        self.e, self.kind, self.calls, self.reads, self.writes, self.cost = e, kind, calls, reads, writes, cost
        self.deps = ()


def _free_elems(ap):
    try:
        n = 1
        for s_ in list(ap.shape)[1:]:
            n *= int(s_)
        return n
    except Exception:
        return 256


def _ap_bytes(ap):
    try:
        n = 1
        for s_ in list(ap.shape):
            n *= int(s_)
        return n * 4
    except Exception:
        return 65536


def _est_cost(e, calls):
    t = 0.0
    for name, a, kw in calls:
        out = kw.get("out", a[0] if a else None)
        n = _free_elems(out) if out is not None else 256
        if e == "pe":
            mul = 4.0 if (name == "matmul" and getattr(kw.get("lhsT"), "dtype", None) == F32) else 1.0
            t += mul * max(n, 64) / 1.6 + 25.0
        elif e == "act":
            t += n / 1.0 + 220.0
        elif e == "dve":
            t += n / 0.9 + 80.0
        else:
            t += n * 2.0 + 300.0
    return t


class KB:
    def __init__(self, sched=True, W=32, LAT=120.0):
        nc = bass.Bass("TRN2", target_bir_lowering=False)
        self.nc = nc
        self.E = {"pe": nc.tensor, "act": nc.scalar, "dve": nc.vector, "pool": nc.gpsimd, "sp": nc.sync}
        self.sems = {}
        self.cnt = {}
        for e in ("pe", "act", "dve", "pool"):
            self.sems[e] = nc.alloc_semaphore("s_" + e)
            self.cnt[e] = 0
        self.dq = {}
        for q, nds in (("sp", 12), ("pool", NDS), ("act", 6)):
            keys = []
            for i in range(nds):
                k = "d_%s%d" % (q, i)
                self.sems[k] = nc.alloc_semaphore(k)
                self.cnt[k] = 0
                keys.append(k)
            self.dq[q] = [keys, 0]
        self.seen = {e: {} for e in self.E}
        self.nops = 0
        self.sched = sched
        self.W = W
        self.LAT = LAT
        self.use_prio = True
        self.Q = Q_DEF
        self.pending = []

    def _waits(self, e, reads, writes, extra=()):
        need = {}

        def add(rec):
            if rec is None:
                return
            k, c = rec
            if need.get(k, 0) < c:
                need[k] = c

        for r in reads:
            add(r.lw)
        for w in writes:
            add(w.lw)
            for k, c in w.rd.items():
                add((k, c))
        for rec in extra:
            add(rec)
        eng = self.E[e]
        seen = self.seen[e]
        for k, c in need.items():
            if e == "pe" and k == "pe":
                continue
            if seen.get(k, 0) >= c:
                continue
            eng.wait_ge(self.sems[k], c)
            seen[k] = c

    def _mark(self, rec, reads, writes):
        k, c = rec
        for r in reads:
            if r.rd.get(k, 0) < c:
                r.rd[k] = c
        for w in writes:
            w.lw = rec
            w.rd = {}

    def op(self, e, fn, reads=(), writes=()):
        rec = _Rec()
        fn(rec)
        node = _Node(e, "op", rec.calls, list(reads), list(writes), _est_cost(e, rec.calls))
        if self.sched:
            self.pending.append(node)
        else:
            self._emit(node)

    def dma(self, q, out, in_, reads=(), writes=(), **kw):
        node = _Node(q, "dma", (out, in_, kw), list(reads), list(writes), 2000.0 + _ap_bytes(out) / 100.0)
        if self.sched:
            self.pending.append(node)
        else:
            self._emit(node)

    def _emit(self, n):
        e = n.e
        if n.kind == "op":
            self._waits(e, n.reads, n.writes)
            eng = self.E[e]
            inst = None
            for name, a, kw in n.calls:
                inst = getattr(eng, name)(*a, **kw)
            self.cnt[e] += 1
            inst.then_inc(self.sems[e], 1)
            self._mark((e, self.cnt[e]), n.reads, n.writes)
        else:
            out, in_, kw = n.calls
            keys, i = self.dq[e]
            k = keys[i]
            self.dq[e][1] = (i + 1) % len(keys)
            extra = [(k, self.cnt[k])] if self.cnt[k] else []
            self._waits(e, n.reads, n.writes, extra)
            self.E[e].dma_start(out=out, in_=in_, **kw).then_inc(self.sems[k], 16)
            self.cnt[k] += 16
            self._mark((k, self.cnt[k]), n.reads, n.writes)
        self.nops += 1

    def flush(self):
        nodes = self.pending
        self.pending = []
        if not nodes:
            return
        lastw, readers = {}, {}
        for i, n in enumerate(nodes):
            d = set()
            for r in n.reads:
                if id(r) in lastw:
                    d.add(lastw[id(r)])
            for w in n.writes:
                if id(w) in lastw:
                    d.add(lastw[id(w)])
                d.update(readers.get(id(w), ()))
            d.discard(i)
            n.deps = d
            for r in n.reads:
                readers.setdefault(id(r), []).append(i)
            for w in n.writes:
                lastw[id(w)] = i
                readers[id(w)] = []
        prio = [n.cost for n in nodes]
        for i in range(len(nodes) - 1, -1, -1):
            pi_ = prio[i]
            for d in nodes[i].deps:
                v_ = nodes[d].cost + pi_
                if v_ > prio[d]:
                    prio[d] = v_
        queues = {}
        for i, n in enumerate(nodes):
            queues.setdefault(n.e, []).append(i)
        fin = [None] * len(nodes)
        efree = {e: 0.0 for e in queues}
        W, LAT = self.W, self.LAT
        remaining = len(nodes)
        starts_ = {}
        while remaining:
            best = None
            for e, q in queues.items():
                for i in q[:W]:
                    n = nodes[i]
                    ready = 0.0
                    ok = True
                    for d in n.deps:
                        f = fin[d]
                        if f is None:
                            ok = False
                            break
                        if f + LAT > ready:
                            ready = f + LAT
                    if not ok:
                        continue
                    start = ready if ready > efree[e] else efree[e]
                    key = (int(start / self.Q), -prio[i], i) if self.use_prio else (start, i)
                    starts_[i] = start
                    if best is None or key < best[0]:
                        best = (key, e, i)
            assert best is not None
            e, i = best[1], best[2]
            start = starts_[i]
            n = nodes[i]
            fin[i] = start + n.cost
            efree[e] = (start + 60.0) if n.kind == "dma" else fin[i]
            queues[e].remove(i)
            self._emit(n)
            remaining -= 1

    def barrier(self):
        self.flush()
        for e in self.E:
            for k, c in self.cnt.items():
                if c == 0 or (e == "pe" and k == "pe"):
                    continue
                if self.seen[e].get(k, 0) >= c:
                    continue
                self.E[e].wait_ge(self.sems[k], c)
                self.seen[e][k] = c

    def finish(self):
        self.barrier()


def bcast_rows(t, off, n, parts=128):
    return bass.AP(t, off, [[0, parts], [1, n]])


def build(cfg):
    kb = KB(sched=cfg.get("sched", True), W=cfg.get("W", 128), LAT=cfg.get("LAT", 1000.0))
    nc = kb.nc
    NSEQ = cfg.get("nseq", 2)
    NLAY = cfg.get("nlay", 2)
    mixers = cfg.get("mixers", ("gla", "rwkv", "nsa"))
    dumps = cfg.get("dumps", ())
    inject_O = cfg.get("inject_O", False)

    def dram_in(name, shape):
        return nc.dram_tensor(name, list(shape), F32, kind="ExternalInput")

    x_d = dram_in("x", [2, S, D])
    w_in = dram_in("w_in", [L, D, 3400])
    w_vres = dram_in("w_in_vres", [1, D, 32])
    w_nsa = dram_in("w_nsa", [L, D, 2200])
    gla_w_a2 = dram_in("gla_w_a2", [L, 16, 256])
    rwkv_w2 = dram_in("rwkv_w2", [L, 64, 256])
    rwkv_a2 = dram_in("rwkv_a2", [L, 64, 256])
    rwkv_v2 = dram_in("rwkv_v2", [1, 32, 256])
    rwkv_g2 = dram_in("rwkv_g2", [L, 160, 256])
    nsa_wk1 = dram_in("nsa_wk1", [L, 2048, 256])
    nsa_wk2 = dram_in("nsa_wk2", [L, 256, 64])
    nsa_wv1 = dram_in("nsa_wv1", [L, 2048, 256])
    nsa_wv2 = dram_in("nsa_wv2", [L, 256, 64])
    nsa_posT = dram_in("nsa_posT", [L, 2, 64, 32])
    w_out = dram_in("w_out", [L, D, D])
    w_gate = dram_in("ffn_w_gate", [L, D, FF])
    w_up = dram_in("ffn_w_up", [L, D, FF])
    w_down = dram_in("ffn_w_down", [L, FF, D])
    ptab = dram_in("ptab", [L, 128, 32])
    rowtab = dram_in("rowtab", [L, 5120])
    c_ident = dram_in("c_ident", [128, 128])
    c_caus = dram_in("c_caus", [128, 128])
    c_rope = dram_in("c_rope", [2, 128, S])
    c_cmpmask = dram_in("c_cmpmask", [128, S])
    c_onehot = dram_in("c_onehot", [32, S])
    c_selm = dram_in("c_selm", [2, 128, NT, 32])
    c_ovl = dram_in("c_ovl", [128, 32])
    c_blk = dram_in("c_blk", [128, 128])
    c_hsel = dram_in("c_hsel", [128, 2])
    c_rwmask2 = dram_in("c_rwmask2", [128, 3, 128])
    c_istk = dram_in("c_istk", [128, 64])
    if inject_O:
        dbg_OT = dram_in("dbg_OT", [128, 8, S])
    out_d = nc.dram_tensor("out", [2, S, D], F32, kind="ExternalOutput")
    xres = [nc.dram_tensor("xres%d" % i, [S, D], F32, kind="Internal") for i in range(2)]
    xres_r = [regs(NT) for _ in range(2)]
    vfirst_d = nc.dram_tensor("vfirst", [2, 128, S], F32, kind="Internal")
    vfirst_r = regs(2)
    dump_d = {}
    for name, shape in dumps:
        dump_d[name] = nc.dram_tensor("dump_" + name, list(shape), F32, kind="ExternalOutput")

    XT = nc.alloc_sbuf_tensor("XT", [128, 8, S], BF16)
    XT_r = regs(NT)
    ident = nc.alloc_sbuf_tensor("ident", [128, 128], BF16)
    identf = nc.alloc_sbuf_tensor("identf", [128, 128], F32)
    caus = nc.alloc_sbuf_tensor("caus", [128, 4, 128], F32)
    ptb = nc.alloc_sbuf_tensor("ptb", [128, L, 32], F32)
    ptn = nc.alloc_sbuf_tensor("ptn", [128, L, 32], F32)
    c_r = Reg()
    for l in range(L):
        kb.dma("sp", ptb[:, l, :], ptab.ap()[l], writes=[c_r])
    kb.dma("pool", ident[:], c_ident.ap(), writes=[c_r])
    kb.dma("sp", identf[:], c_ident.ap(), writes=[c_r])
    for i in range(4):
        kb.dma("sp", caus[:, i, :], c_caus.ap(), writes=[c_r])

    def dram16(name, shape):
        return nc.dram_tensor(name, list(shape), BF16, kind="Internal")

    conv_list = [(w_in, [L, D, 3400]), (w_nsa, [L, D, 2200]), (w_out, [L, D, D]), (w_gate, [L, D, FF]), (w_up, [L, D, FF]),
                 (w_down, [L, FF, D]), (nsa_wk1, [L, 2048, 256]), (nsa_wv1, [L, 2048, 256])]
    w16 = {}
    w16_r = Reg()
    CH = 2816
    NCV = 6
    with nc.sbuf_tensor("cv_f", [128, NCV, CH], F32) as cvf, nc.sbuf_tensor("cv_b", [128, NCV, CH], BF16) as cvb:
        cvf_r, cvb_r = regs(NCV), regs(NCV)
        job = 0

        def conv_job(src_ap, dst_ap, n):
            nonlocal job
            i = job % NCV
            kb.dma("sp", cvf[:, i, 0:n], src_ap, writes=[cvf_r[i]])
            kb.op("act", lambda a: a.activation(out=cvb[:, i, 0:n], in_=cvf[:, i, 0:n], func=AF.Copy),
                  reads=[cvf_r[i]], writes=[cvb_r[i]])
            kb.dma("act", dst_ap, cvb[:, i, 0:n], reads=[cvb_r[i]], writes=[w16_r])
            job += 1

        for src, shape in conv_list:
            dst = dram16(src.name + "_16", shape)
            w16[src.name] = dst
            if src is w_in:
                for l_ in range(L):
                    for k_ in range(8):
                        conv_job(src.ap()[l_, k_ * 128:(k_ + 1) * 128, 0:2096], dst.ap()[l_, k_ * 128:(k_ + 1) * 128, 0:2096], 2096)
                continue
            tot = 1
            for d_ in shape:
                tot *= d_
            per = tot // 128
            assert per * 128 == tot
            for c0 in range(0, per, CH):
                n = min(CH, per - c0)
                conv_job(bass.AP(src, c0, [[per, 128], [1, n]]), bass.AP(dst, c0, [[per, 128], [1, n]]), n)
        kb.barrier()
    w_in16, w_nsa16, w_out16 = w16["w_in"], w16["w_nsa"], w16["w_out"]
    w_gate16, w_up16, w_down16 = w16["ffn_w_gate"], w16["ffn_w_up"], w16["ffn_w_down"]
    wk1_16, wv1_16 = w16["nsa_wk1"], w16["nsa_wv1"]

    PS = [nc.alloc_psum_tensor("ps%d" % i, [128, 512], F32) for i in range(6)]
    PS_r = regs(6)
    PB = [nc.alloc_psum_tensor("pb%d" % i, [128, 1024], BF16) for i in range(2)]
    PB_r = regs(2)

    _uid = [0]

    def sb(name, shape, dt):
        _uid[0] += 1
        return nc.sbuf_tensor("%s_%d" % (name, _uid[0]), list(shape), dt)

    def proj_fm(ps_ap, Wt, c0, M, tok0, N, wreads, treads, pw, kparts=8):
        def fn(pe):
            inst = None
            for k in range(kparts):
                inst = pe.matmul(ps_ap, lhsT=Wt[:, k, c0:c0 + M], rhs=XT[:, k, tok0:tok0 + N],
                                 start=(k == 0), stop=(k == kparts - 1))
            return inst
        kb.op("pe", fn, reads=list(wreads) + list(treads), writes=[pw])

    def proj_tm(ps_ap, Wt, c0, N, tt, wreads, pw):
        def fn(pe):
            inst = None
            for k in range(8):
                inst = pe.matmul(ps_ap, lhsT=XT[:, k, tt * 128:(tt + 1) * 128], rhs=Wt[:, k, c0:c0 + N],
                                 start=(k == 0), stop=(k == 7))
            return inst
        kb.op("pe", fn, reads=list(wreads) + [XT_r[tt]], writes=[pw])

    def load_w(Wt, src3, ncols, wreg, q="sp", chunk=None):
        for k in range(8):
            kb.dma(q, Wt[:, k, 0:ncols], src3[k * 128:(k + 1) * 128, 0:ncols], reads=[w16_r], writes=[wreg])

    xb = [nc.alloc_sbuf_tensor("xb%d" % i, [128, D], BF16) for i in range(2)]
    xb_r = regs(2)
    xb_i = [0]

    def make_xT(src_ap, src_reg, tt):
        i = xb_i[0]
        xb_i[0] ^= 1
        kb.op("act", lambda a: a.activation(out=xb[i][:], in_=src_ap, func=AF.Copy),
              reads=[src_reg], writes=[xb_r[i]])
        pb = PB[i]

        def fn(pe):
            inst = None
            for k in range(8):
                inst = pe.transpose(out=pb[:, k * 128:(k + 1) * 128], in_=xb[i][:, k * 128:(k + 1) * 128],
                                    identity=ident[:])
            return inst
        kb.op("pe", fn, reads=[xb_r[i], c_r], writes=[PB_r[i]])
        kb.op("dve", lambda v: v.tensor_copy(out=XT[:, :, tt * 128:(tt + 1) * 128],
                                             in_=pb[:, :].rearrange("p (k t) -> p k t", k=8)),
              reads=[PB_r[i]], writes=[XT_r[tt]])

    def layer_norm(xt_ap, xreg, lnw_ap, lnb_ap, lnreg, st, st_r):
        kb.op("dve", lambda v: v.bn_stats(out=st[:, 0:6], in_=xt_ap[:, 0:512]), reads=[xreg], writes=[st_r])
        kb.op("dve", lambda v: v.bn_stats(out=st[:, 6:12], in_=xt_ap[:, 512:1024]), reads=[xreg, st_r], writes=[st_r])
        kb.op("dve", lambda v: v.bn_aggr(out=st[:, 12:14], in_=st[:, 0:12]), reads=[st_r], writes=[st_r])
        kb.op("act", lambda a: a.activation(out=st[:, 14:15], in_=st[:, 13:14], func=AF.Sqrt, bias=LN_EPS, scale=1.0),
              reads=[st_r], writes=[st_r])
        kb.op("dve", lambda v: v.reciprocal(out=st[:, 15:16], in_=st[:, 14:15]), reads=[st_r], writes=[st_r])
        kb.op("dve", lambda v: v.scalar_tensor_tensor(out=st[:, 16:17], in0=st[:, 12:13], scalar=-1.0, in1=st[:, 15:16],
                                                      op0=ALU.mult, op1=ALU.mult), reads=[st_r], writes=[st_r])
        kb.op("act", lambda a: a.activation(out=xt_ap, in_=xt_ap, func=AF.Identity, bias=st[:, 16:17], scale=st[:, 15:16]),
              reads=[st_r, xreg], writes=[xreg])
        kb.op("dve", lambda v: v.tensor_tensor(out=xt_ap, in0=xt_ap, in1=lnw_ap, op=ALU.mult), reads=[xreg, lnreg], writes=[xreg])
        kb.op("dve", lambda v: v.tensor_tensor(out=xt_ap, in0=xt_ap, in1=lnb_ap, op=ALU.add), reads=[xreg, lnreg], writes=[xreg])

    def dump(name, sb_ap, reg, idx=None):
        if name in dump_d:
            dst = dump_d[name].ap() if idx is None else dump_d[name].ap()[idx]
            kb.dma("sp", dst, sb_ap, reads=[reg])

    kb.op("dve", lambda v: v.tensor_scalar(out=ptn[:, :, :], in0=ptb[:, :, :], scalar1=-1.0, scalar2=None, op0=ALU.mult),
          reads=[c_r], writes=[c_r])
    kb.op("dve", lambda v: v.tensor_scalar(out=ptn[:, :, 2:13], in0=ptb[:, :, 2:13], scalar1=-1.0, scalar2=1.0,
                                           op0=ALU.mult, op1=ALU.add), reads=[c_r], writes=[c_r])

    def gla_phase(sq, l, OT, OT_r):
        with ExitStack() as es:
            A = lambda n_, s_, d_: es.enter_context(sb(n_, s_, d_))
            Wg = A("Wg", [128, 8, 1152], BF16)
            wa2 = A("wa2", [16, 256], BF16)
            gln = A("gln", [128, 2, 256], F32)
            alrT = A("alrT", [16, 512], BF16)
            t1 = A("gt1", [128, 512], F32)
            t2 = A("gt2", [128, 512], F32)
            EB = A("EB", [128, 2, 512], F32)
            QTl = A("gQT", [128, 2, 2, 512], BF16)
            KTl = A("gKT", [128, 2, 512], BF16)
            Ktok = A("gKtok", [128, 4, 256], BF16)
            V = A("gV", [128, 4, 256], BF16)
            SG = A("gSG", [128, 4, 256], F32)
            AT = A("gAT", [128, 4, 128], BF16)
            St = A("gSt", [128, 2, 128], F32)
            tmpS = A("gtmpS", [128, 128], F32)
            blkm = A("gblk", [128, 128], F32)
            Sb = A("gSb", [128, 2, 128], BF16)
            rmask = A("grm", [128, 512], F32)
            ow = A("gow", [128, 256], F32)
            ow2 = A("gow2", [128, 256], F32)
            ob = A("gob", [128, 256], BF16)
            st = A("gst", [128, 32], F32)
            W_r, p_r, alr_r, t1_r, t2_r = Reg(), Reg(), Reg(), Reg(), Reg()
            EB_r, QT_r, KT_r = regs(2), regs(2), regs(2)
            Ktok_r, V_r, SG_r, AT_r, St_r, Sb_r = Reg(), regs(4), regs(4), Reg(), Reg(), Reg()
            ow_r, ow2_r, ob_r, st_r = Reg(), Reg(), Reg(), Reg()
            gs = cfg.get("gla_stop", 99)
            load_w(Wg, w_in16.ap()[l][:, 0:1040], 1040, W_r)
            kb.dma("pool", wa2[:, :], gla_w_a2.ap()[l], writes=[p_r])
            kb.dma("sp", gln[:, 0, :], bcast_rows(rowtab, l * 5120 + 0, 256), writes=[p_r])
            kb.dma("sp", gln[:, 1, :], bcast_rows(rowtab, l * 5120 + 256, 256), writes=[p_r])
            kb.op("pool", lambda g: g.memset(rmask[:, :], 1.0), writes=[p_r])
            kb.op("pool", lambda g: g.memset(rmask[:, :].rearrange("p (c t) -> p c t", c=4)[:, :, 0:1], 0.0), writes=[p_r])
            kb.op("pool", lambda g: g.memset(St[:, :, :], 0.0), writes=[St_r])
            kb.op("pool", lambda g: g.memset(QTl[:, :, :, :], 0.0), writes=QT_r)
            kb.dma("sp", blkm[:, :], c_blk.ap(), writes=[p_r])
            tmpS_r = Reg()
            kb.op("pool", lambda g: g.memset(Sb[:, :, :], 0.0), writes=[Sb_r])
            for mt in range(4 if gs > 0 else 0):
                tok0 = mt * 512
                xr = XT_r[mt * 4:mt * 4 + 4]
                proj_fm(PS[0][0:16, :], Wg, 1024, 16, tok0, 512, [W_r], xr, PS_r[0])
                kb.op("act", lambda a: a.activation(out=alrT[:, :], in_=PS[0][0:16, :], func=AF.Copy),
                      reads=[PS_r[0]], writes=[alr_r])
                for hp in range(2):
                    kb.op("pe", lambda pe: pe.matmul(PS[1][:, :], lhsT=wa2[0:16, hp * 128:(hp + 1) * 128], rhs=alrT[0:16, :],
                                                     start=True, stop=True), reads=[p_r, alr_r], writes=[PS_r[1]])
                    kb.op("act", lambda a: a.activation(out=t1[:, :], in_=PS[1][:, :], func=AF.Exp, scale=-1.0,
                                                        bias=ptn[:, l, hp:hp + 1]), reads=[PS_r[1], c_r], writes=[t1_r])
                    kb.op("act", lambda a: a.activation(out=t1[:, :], in_=t1[:, :], func=AF.Ln, bias=1.0, scale=1.0),
                          reads=[t1_r], writes=[t1_r])
                    kb.op("dve", lambda v: v.tensor_tensor_scan(out=t2[:, :], data0=rmask[:, :], data1=t1[:, :], initial=0.0,
                                                                op0=ALU.mult, op1=ALU.add), reads=[t1_r, p_r], writes=[t2_r])
                    kb.op("act", lambda a: a.activation(out=EB[:, hp, :], in_=t2[:, :], func=AF.Exp, scale=-1.0 / 16.0),
                          reads=[t2_r], writes=[EB_r[hp]])
                    kb.op("act", lambda a: a.activation(out=t1[:, :], in_=t2[:, :], func=AF.Exp, scale=1.0 / 16.0),
                          reads=[t2_r], writes=[t1_r])
                    proj_fm(PS[2][:, :], Wg, hp * 128, 128, tok0, 512, [W_r], xr, PS_r[2])
                    for hh in range(2):
                        pr = slice(hh * 64, hh * 64 + 64)
                        kb.op("dve", lambda v: v.scalar_tensor_tensor(out=QTl[pr, hp, hh, :], in0=PS[2][pr, :], scalar=0.125,
                                                                      in1=EB[pr, hp, :], op0=ALU.mult, op1=ALU.mult),
                              reads=[PS_r[2], EB_r[hp]], writes=[QT_r[hp]])
                    proj_fm(PS[3][:, :], Wg, 256 + hp * 128, 128, tok0, 512, [W_r], xr, PS_r[3])
                    kb.op("dve", lambda v: v.tensor_tensor(out=KTl[:, hp, :], in0=PS[3][:, :], in1=t1[:, :], op=ALU.mult),
                          reads=[PS_r[3], t1_r], writes=[KT_r[hp]])

                    def fn(pe):
                        inst = None
                        for j in range(4):
                            inst = pe.transpose(out=PB[0][:, j * 128:(j + 1) * 128], in_=KTl[:, hp, j * 128:(j + 1) * 128],
                                                identity=ident[:])
                        return inst
                    kb.op("pe", fn, reads=[KT_r[hp], c_r], writes=[PB_r[0]])
                    kb.op("dve", lambda v: v.tensor_copy(out=Ktok[:, :, hp * 128:(hp + 1) * 128],
                                                         in_=PB[0][:, 0:512].rearrange("p (j c) -> p j c", j=4)),
                          reads=[PB_r[0]], writes=[Ktok_r])
                for j in range(4 if gs > 1 else 0):
                    tt = mt * 4 + j
                    proj_tm(PS[4][:, 0:256], Wg, 512, 256, tt, [W_r], PS_r[4])
                    proj_tm(PS[5][:, 0:256], Wg, 768, 256, tt, [W_r], PS_r[5])
                    if cfg.get("gv", 3) >= 2:
                        kb.op("dve", lambda v: v.tensor_scalar(out=V[:, j, :], in0=PS[4][:, 0:256], scalar1=1.0, scalar2=None, op0=ALU.mult),
                              reads=[PS_r[4]], writes=[V_r[j]])
                    if cfg.get("gv", 3) >= 3:
                        kb.op("act", lambda a: a.activation(out=SG[:, j, :], in_=PS[5][:, 0:256], func=AF.Silu),
                              reads=[PS_r[5]], writes=[SG_r[j]])
                for j in range(4 if gs > 2 else 0):
                    tt = mt * 4 + j
                    cc = slice(j * 128, (j + 1) * 128)

                    def fn(pe):
                        inst = None
                        for h in range(4):
                            hp, hh = h // 2, h % 2
                            inst = pe.matmul(PS[0][:, h * 128:(h + 1) * 128], lhsT=KTl[:, hp, cc], rhs=QTl[:, hp, hh, cc],
                                             start=True, stop=True)
                        return inst
                    kb.op("pe", fn, reads=QT_r + KT_r, writes=[PS_r[0]])
                    kb.op("dve", lambda v: v.tensor_tensor(out=AT[:, :, :], in0=PS[0][:, :].rearrange("p (h t) -> p h t", h=4),
                                                          in1=caus[:, :, :], op=ALU.mult), reads=[PS_r[0], c_r], writes=[AT_r])

                    def fn(pe):
                        inst = None
                        for h in range(4):
                            hp, hh = h // 2, h % 2
                            pe.matmul(PS[1][:, h * 64:(h + 1) * 64], lhsT=AT[:, h, :], rhs=V[:, j, h * 64:(h + 1) * 64],
                                      start=True, stop=False)
                            inst = pe.matmul(PS[1][:, h * 64:(h + 1) * 64], lhsT=QTl[:, hp, hh, cc], rhs=Sb[:, hp, hh * 64:(hh + 1) * 64],
                                             start=False, stop=True)
                        return inst
                    if gs <= 3:
                        continue
                    kb.op("pe", fn, reads=[AT_r, V_r[j], Sb_r] + QT_r, writes=[PS_r[1]])

                    def fn(pe):
                        inst = None
                        for hp in range(2):
                            inst = pe.matmul(PS[2][:, hp * 128:(hp + 1) * 128], lhsT=Ktok[:, j, hp * 128:(hp + 1) * 128],
                                             rhs=V[:, j, hp * 128:(hp + 1) * 128], start=True, stop=True)
                        return inst
                    if gs <= 4:
                        continue
                    kb.op("pe", fn, reads=[Ktok_r, V_r[j]], writes=[PS_r[2]])
                    for hp in range(2):
                        ee = EB[:, hp, j * 128 + 127:j * 128 + 128]
                        kb.op("dve", lambda v: v.scalar_tensor_tensor(out=tmpS[:, :], in0=PS[2][:, hp * 128:(hp + 1) * 128], scalar=ee,
                                                                      in1=blkm[:, :], op0=ALU.mult, op1=ALU.mult),
                              reads=[PS_r[2], EB_r[hp], p_r], writes=[tmpS_r])
                        kb.op("dve", lambda v: v.scalar_tensor_tensor(out=St[:, hp, :], in0=St[:, hp, :], scalar=ee, in1=tmpS[:, :],
                                                                      op0=ALU.mult, op1=ALU.add),
                              reads=[St_r, tmpS_r, EB_r[hp]], writes=[St_r])
                    kb.op("act", lambda a: a.activation(out=Sb[:, :, :], in_=St[:, :, :], func=AF.Copy), reads=[St_r], writes=[Sb_r])
                    if gs <= 5:
                        continue
                    head_norm_gate(PS[1][:, 0:256], PS_r[1], ow, ow_r, ow2, ow2_r, st, st_r, gln, p_r, LN_EPS)
                    kb.op("dve", lambda v: v.tensor_tensor(out=ob[:, :], in0=ow[:, :], in1=SG[:, j, :], op=ALU.mult),
                          reads=[ow_r, SG_r[j]], writes=[ob_r])
                    if gs <= 6:
                        continue
                    out_to_OT(ob, ob_r, 128, OT, OT_r, 0, tt * 128)
            kb.barrier()

    def head_norm_gate(ps_ap, ps_r, ow, ow_r, ow2, ow2_r, st, st_r, gln, gln_r, eps, P=128):
        v4 = lambda ap: ap.rearrange("p (h d) -> p h d", h=4)
        kb.op("act", lambda a: a.activation(out=ow[0:P, :], in_=ps_ap, func=AF.Copy), reads=[ps_r], writes=[ow_r])
        kb.op("act", lambda a: a.activation(out=ow2[0:P, :], in_=ow[0:P, :], func=AF.Square), reads=[ow_r], writes=[ow2_r])
        kb.op("dve", lambda v: v.reduce_sum(out=st[0:P, 0:4], in_=v4(ow[0:P, :]), axis=AX.X), reads=[ow_r], writes=[st_r])
        kb.op("dve", lambda v: v.reduce_sum(out=st[0:P, 4:8], in_=v4(ow2[0:P, :]), axis=AX.X), reads=[ow2_r, st_r], writes=[st_r])
        kb.op("dve", lambda v: v.tensor_scalar(out=st[0:P, 8:16], in0=st[0:P, 0:8], scalar1=1.0 / 64.0, scalar2=None, op0=ALU.mult),
              reads=[st_r], writes=[st_r])
        kb.op("dve", lambda v: v.tensor_tensor(out=st[0:P, 16:20], in0=st[0:P, 8:12], in1=st[0:P, 8:12], op=ALU.mult),
              reads=[st_r], writes=[st_r])
        kb.op("dve", lambda v: v.tensor_tensor(out=st[0:P, 20:24], in0=st[0:P, 12:16], in1=st[0:P, 16:20], op=ALU.subtract),
              reads=[st_r], writes=[st_r])
        kb.op("act", lambda a: a.activation(out=st[0:P, 24:28], in_=st[0:P, 20:24], func=AF.Sqrt, bias=eps, scale=1.0),
              reads=[st_r], writes=[st_r])
        kb.op("dve", lambda v: v.reciprocal(out=st[0:P, 28:32], in_=st[0:P, 24:28]), reads=[st_r], writes=[st_r])
        for h in range(4):
            kb.op("dve", lambda v: v.tensor_scalar(out=ow[0:P, h * 64:(h + 1) * 64], in0=ow[0:P, h * 64:(h + 1) * 64],
                                                   scalar1=st[0:P, 8 + h:9 + h], scalar2=st[0:P, 28 + h:29 + h],
                                                   op0=ALU.subtract, op1=ALU.mult), reads=[ow_r, st_r], writes=[ow_r])
        kb.op("dve", lambda v: v.tensor_tensor(out=ow[0:P, :], in0=ow[0:P, :], in1=gln[0:P, 0, :], op=ALU.mult),
              reads=[ow_r, gln_r], writes=[ow_r])
        kb.op("dve", lambda v: v.tensor_tensor(out=ow[0:P, :], in0=ow[0:P, :], in1=gln[0:P, 1, :], op=ALU.add),
              reads=[ow_r, gln_r], writes=[ow_r])

    def out_to_OT(ob, ob_r, P, OT, OT_r, k0, tokc0, ncols=256):
        nk = ncols // 128

        def fn(pe):
            inst = None
            for kk in range(nk):
                inst = pe.transpose(out=PB[1][:, kk * 128:kk * 128 + P], in_=ob[0:P, kk * 128:(kk + 1) * 128],
                                    identity=ident[0:P, 0:P])
            return inst
        kb.op("pe", fn, reads=[ob_r, c_r], writes=[PB_r[1]])
        tr = OT_r[tokc0 // 128]
        kb.op("act", lambda a: a.activation(out=OT[:, k0:k0 + nk, tokc0:tokc0 + P],
                                            in_=PB[1][:, 0:nk * 128].rearrange("p (k t) -> p k t", k=nk)[:, :, 0:P], func=AF.Copy),
              reads=[PB_r[1]], writes=[tr])

    RWT = [(0, 128), (128, 128), (256, 128), (384, 128), (512, 128), (640, 128), (768, 64), (832, 64),
           (896, 128), (1024, 32), (1056, 32)]
    C0 = float(math.exp(-0.5))

    def rwkv_phase(sq, l, OT, OT_r):
        MT = 256
        NCH = MT // 64
        PU = NCH * 2
        with ExitStack() as es:
            A = lambda n_, s_, d_: es.enter_context(sb(n_, s_, d_))
            Wr = A("Wr", [128, 8, 1088], BF16)
            w2 = A("rw2", [64, 256], BF16)
            a2 = A("ra2", [64, 256], BF16)
            v2 = A("rv2", [32, 256], BF16)
            g2a = A("rg2a", [128, 256], BF16)
            g2b = A("rg2b", [128, 256], BF16)
            rln = A("rln", [128, 2, 2, 64], F32)
            blkf = A("rblkf", [128, 128], F32)
            m3 = A("rm3", [128, 3, 128], F32)
            istk = A("ristk", [128, 64], BF16)
            onec = A("ronec", [128, 1], BF16)
            rmask = A("rrm", [128, MT], F32)
            xs = [A("rxs%d" % i, [128, MT], F32) for i in range(6)]
            tt_ = [A("rt%d" % i, [128, MT], F32) for i in range(8)]
            ttb_ = [A("rtb%d" % i, [128, MT], F32) for i in range(9)]
            tb_r = regs(9)
            Gam2 = [A("rGam%d" % i_, [128, 2, MT], F32) for i_ in range(2)]
            AR2 = [A("rAR%d" % i_, [128, 2, 2, NCH, 2, 64], BF16) for i_ in range(2)]
            BK = A("rBK", [128, 2, 2, NCH, 2, 64], BF16)
            VTz = A("rVTz", [128, 2, NCH, 2, 64], BF16)
            rkrz = A("rrkrz", [128, 2, NCH, 2, 64], BF16)
            TW = A("rTW", [64, MT], BF16)
            AL = A("rAL", [64, MT], BF16)
            SGLd = A("rSGLd", [128, NCH, 2, 64], BF16)
            SGL2d = A("rSGL2d", [128, NCH, 2, 64], BF16)
            VR = A("rVR", [32, MT], BF16)
            Vstk2 = [A("rVstk%d" % i_, [128, NCH, 2, 64], BF16) for i_ in range(2)]
            Bblk2 = [A("rBblk%d" % i_, [128, NCH, 2, 128], BF16) for i_ in range(2)]
            Kblk2 = [A("rKblk%d" % i_, [128, NCH, 2, 128], BF16) for i_ in range(2)]
            gtok2 = [A("rgtok%d" % i_, [128, NCH, 2, 64], F32) for i_ in range(2)]
            cb2 = [A("rcb%d" % i_, [128, NCH, 2], F32) for i_ in range(2)]
            MN = [A("rMN%d" % i, [128, PU, 2, 128], F32) for i in range(2)]
            Pm2 = [A("rP%d" % i_, [128, PU, 128], F32) for i_ in range(2)]
            A32 = [A("rA3%d" % i_, [128, PU, 3, 128], BF16) for i_ in range(2)]
            Rs = A("rRs", [128, 128], F32)
            Ub = A("rUb", [128, 128], BF16)
            Tst = A("rTst", [128, 2, 64], F32)
            Tb = A("rTb", [128, 2, 64], BF16)
            tmpS = A("rtmpS", [128, 2, 64], F32)
            ow = A("row", [128, 128], F32)
            ow2 = A("row2", [128, 128], F32)
            obblk = A("robblk", [128, 2, 128], BF16)
            st = A("rst", [128, 32], F32)
            W_r, p_r = Reg(), Reg()
            xs_r, t_r = regs(6), regs(8)
            t_r0 = t_r
            BK_r, VTz_r, rkrz_r = regs(2), regs(2), regs(2)
            Gam_r2, AR_r2 = [regs(2), regs(2)], [regs(2), regs(2)]
            TW_r, AL_r, SGL_r, SGL2_r, VR_r = Reg(), Reg(), Reg(), Reg(), Reg()
            Vstk_r2, Bblk_r2, Kblk_r2, gtok_r2, cb_r2 = regs(2), regs(2), regs(2), regs(2), regs(2)
            MN_r, Rs_r, Ub_r, T_r, Tb_r, tmpS_r = regs(2), Reg(), Reg(), Reg(), Reg(), Reg()
            P_r2, A3_r2 = regs(2), regs(2)
            ow_r, ow2_r, ob_r, st_r = Reg(), Reg(), Reg(), Reg()
            rs_ = cfg.get("rw_stop", 99)
            load_w(Wr, w_in16.ap()[l][:, 1040:2096], 1056, W_r)
            if l >= 1:
                for k in range(8):
                    kb.dma("pool", Wr[:, k, 1056:1088], w_vres.ap()[l - 1][k * 128:(k + 1) * 128, :], writes=[W_r])
            kb.dma("pool", w2[:, :], rwkv_w2.ap()[l], writes=[p_r])
            kb.dma("pool", a2[:, :], rwkv_a2.ap()[l], writes=[p_r])
            if l >= 1:
                kb.dma("pool", v2[:, :], rwkv_v2.ap()[l - 1], writes=[p_r])
            kb.op("pool", lambda g: g.memset(g2b[:, :], 0.0), writes=[p_r])
            kb.dma("pool", g2a[:, :], rwkv_g2.ap()[l][0:128, :], writes=[p_r])
            kb.dma("pool", g2b[0:32, :], rwkv_g2.ap()[l][128:160, :], reads=[p_r], writes=[p_r])
            for wb in range(2):
                for hp in range(2):
                    for hh in range(2):
                        kb.dma("sp", rln[hh * 64:(hh + 1) * 64, wb, hp, :],
                               bcast_rows(rowtab, l * 5120 + 512 + wb * 256 + (2 * hp + hh) * 64, 64, parts=64), writes=[p_r])
            kb.dma("sp", blkf[:, :], c_blk.ap(), writes=[p_r])
            kb.dma("sp", m3[:, :, :], c_rwmask2.ap(), writes=[p_r])
            kb.dma("pool", istk[:, :], c_istk.ap(), writes=[p_r])
            kb.op("pool", lambda g: g.memset(onec[:, :], 1.0), writes=[p_r])
            kb.op("pool", lambda g: g.memset(rmask[:, :], 1.0), writes=[p_r])
            kb.op("pool", lambda g: g.memset(rmask[:, :].rearrange("p (c t) -> p c t", c=NCH)[:, :, 0:1], 0.0), writes=[p_r])
            zl = [(Tst, [T_r]), (Tb, [Tb_r]), (SGL2d, [SGL2_r]), (BK, BK_r), (VTz, VTz_r), (rkrz, rkrz_r), (obblk, [ob_r])]
            for i_ in range(2):
                zl += [(AR2[i_], AR_r2[i_])]
            for tz, rr in zl:
                kb.op("pool", lambda g, tz=tz: g.memset(tz[:], 0.0), writes=rr)

            carry = A("rcarry", [128, 12], F32)
            carry_r = regs(11)
            kb.op("pool", lambda g: g.memset(carry[:, :], 0.0), writes=carry_r)

            def shift_proj(i, tok0, dst_fn):
                c0, n = RWT[i]
                mucol = 2 + i
                pi = i % 2
                tp = ttb_[7] if pi == 0 else ttb_[8]
                tpr = tb_r[7] if pi == 0 else tb_r[8]
                proj_fm(PS[pi][0:n, 0:MT], Wr, c0, n, tok0, MT, [W_r], XT_r[tok0 // 128:tok0 // 128 + MT // 128], PS_r[pi])
                kb.op("act", lambda a: a.activation(out=tp[0:n, 1:MT], in_=PS[pi][0:n, 0:MT - 1], func=AF.Copy,
                                                    scale=ptb[0:n, l, mucol:mucol + 1]), reads=[PS_r[pi], c_r], writes=[tpr, PS_r[pi]])
                kb.op("act", lambda a: a.activation(out=tp[0:n, 0:1], in_=carry[0:n, i:i + 1], func=AF.Copy,
                                                    scale=ptb[0:n, l, mucol:mucol + 1]), reads=[carry_r[i], c_r, tpr], writes=[tpr])
                kb.op("act", lambda a: a.activation(out=carry[0:n, i:i + 1], in_=PS[pi][0:n, MT - 1:MT], func=AF.Copy),
                      reads=[PS_r[pi], carry_r[i]], writes=[carry_r[i], PS_r[pi]])
                dst_ap, dst_regs = dst_fn()
                kb.op("dve", lambda v: v.scalar_tensor_tensor(out=dst_ap, in0=PS[pi][0:n, 0:MT], scalar=ptn[0:n, l, mucol:mucol + 1],
                                                              in1=tp[0:n, 0:MT], op0=ALU.mult, op1=ALU.add),
                      reads=[PS_r[pi], tpr, c_r], writes=dst_regs + [PS_r[pi]])

            def prep(mt):
                tok0 = mt * MT
                par = mt % 2
                t_r = t_r0
                AR, Vstk, Bblk, Kblk, Gam, gtok, cb, Pm, A3 = AR2[par], Vstk2[par], Bblk2[par], Kblk2[par], Gam2[par], gtok2[par], cb2[par], Pm2[par], A32[par]
                AR_r, Vstk_r, Bblk_r, Kblk_r, Gam_r, gtok_r, cb_r, P_r, A3_r = AR_r2[par], Vstk_r2[par], Bblk_r2[par], Kblk_r2[par], Gam_r2[par], gtok_r2[par], cb_r2[par], P_r2[par], A3_r2[par]
                for i in range(6):
                    shift_proj(i, tok0, lambda i=i: (xs[i][:, :], [xs_r[i]]))
                shift_proj(6, tok0, lambda: (tt_[0][0:64, :], [t_r[0]]))
                kb.op("act", lambda a: a.activation(out=TW[:, :], in_=tt_[0][0:64, :], func=AF.Tanh), reads=[t_r[0]], writes=[TW_r])
                shift_proj(7, tok0, lambda: (AL[:, :], [AL_r]))
                shift_proj(8, tok0, lambda: (tt_[0][:, :], [t_r[0]]))
                for hh_ in range(2):
                    kb.op("act", lambda a: a.activation(out=SGLd[:, :, hh_, :], in_=tt_[0][:, :].rearrange("p (c i) -> p c i", c=NCH), func=AF.Sigmoid),
                          reads=[t_r[0]], writes=[SGL_r])
                shift_proj(9, tok0, lambda: (tt_[0][0:32, :], [t_r[0]]))
                for hh_ in range(2):
                    kb.op("act", lambda a: a.activation(out=SGL2d[0:32, :, hh_, :], in_=tt_[0][0:32, :].rearrange("p (c i) -> p c i", c=NCH), func=AF.Sigmoid),
                          reads=[t_r[0]], writes=[SGL2_r])
                if l >= 1:
                    shift_proj(10, tok0, lambda: (VR[:, :], [VR_r]))
                if rs_ <= 1:
                    return
                for hp in range(2):
                    rT, kT, vT = xs[hp], xs[2 + hp], xs[4 + hp]
                    rT_r, kT_r, vT_r = xs_r[hp], xs_r[2 + hp], xs_r[4 + hp]
                    t1, t2, t3, t4, t5, t6, t7 = tt_[0:7] if hp == 0 else ttb_[0:7]
                    t_r = t_r0 if hp == 0 else tb_r
                    kb.op("pe", lambda pe: pe.matmul(PS[2][:, 0:MT], lhsT=w2[:, hp * 128:(hp + 1) * 128], rhs=TW[:, :], start=True, stop=True),
                          reads=[p_r, TW_r], writes=[PS_r[2]])
                    kb.op("act", lambda a: a.activation(out=t1[:, :], in_=PS[2][:, 0:MT], func=AF.Sigmoid, bias=ptb[:, l, 13 + hp:14 + hp]),
                          reads=[PS_r[2], c_r], writes=[t_r[0]])
                    kb.op("dve", lambda v: v.tensor_tensor_scan(out=t2[:, :], data0=rmask[:, :], data1=t1[:, :], initial=0.0,
                                                                op0=ALU.mult, op1=ALU.add), reads=[t_r[0], p_r], writes=[t_r[1]])
                    kb.op("act", lambda a: a.activation(out=Gam[:, hp, :], in_=t2[:, :], func=AF.Exp, scale=-C0), reads=[t_r[1]], writes=[Gam_r[hp]])
                    kb.op("act", lambda a: a.activation(out=t3[:, :], in_=t2[:, :], func=AF.Exp, scale=C0), reads=[t_r[1]], writes=[t_r[2]])
                    kb.op("dve", lambda v: v.tensor_tensor(out=t4[:, :], in0=t2[:, :], in1=t1[:, :], op=ALU.subtract),
                          reads=[t_r[0], t_r[1]], writes=[t_r[3]])
                    kb.op("act", lambda a: a.activation(out=t4[:, :], in_=t4[:, :], func=AF.Exp, scale=-C0), reads=[t_r[3]], writes=[t_r[3]])
                    kb.op("pe", lambda pe: pe.matmul(PS[3][:, 0:MT], lhsT=a2[:, hp * 128:(hp + 1) * 128], rhs=AL[:, :], start=True, stop=True),
                          reads=[p_r, AL_r], writes=[PS_r[3]])
                    kb.op("act", lambda a: a.activation(out=t5[:, :], in_=PS[3][:, 0:MT], func=AF.Sigmoid, bias=ptb[:, l, 15 + hp:16 + hp]),
                          reads=[PS_r[3], c_r], writes=[t_r[4]])
                    kb.op("dve", lambda v: v.tensor_scalar(out=t6[:, :], in0=kT[:, :], scalar1=ptb[:, l, 19 + hp:20 + hp], scalar2=None, op0=ALU.mult),
                          reads=[kT_r, c_r], writes=[t_r[5]])
                    kb.op("act", lambda a: a.activation(out=t7[:, :], in_=t6[:, :], func=AF.Square), reads=[t_r[5]], writes=[t_r[6]])
                    kb.op("pe", lambda pe: pe.matmul(PS[4][:, 0:MT], lhsT=blkf[:, :], rhs=t7[:, :], start=True, stop=True),
                          reads=[p_r, t_r[6]], writes=[PS_r[4]])
                    kb.op("act", lambda a: a.activation(out=t7[:, :], in_=PS[4][:, 0:MT], func=AF.Sqrt), reads=[PS_r[4], t_r[6]], writes=[t_r[6]])
                    kb.op("dve", lambda v: v.tensor_scalar(out=t7[:, :], in0=t7[:, :], scalar1=1e-12, scalar2=None, op0=ALU.max),
                          reads=[t_r[6]], writes=[t_r[6]])
                    kb.op("dve", lambda v: v.reciprocal(out=t7[:, :], in_=t7[:, :]), reads=[t_r[6]], writes=[t_r[6]])
                    kb.op("dve", lambda v: v.tensor_tensor(out=t6[:, :], in0=t6[:, :], in1=t7[:, :], op=ALU.mult),
                          reads=[t_r[5], t_r[6]], writes=[t_r[5]])
                    kb.op("dve", lambda v: v.tensor_scalar(out=t7[:, :], in0=t5[:, :], scalar1=-1.0, scalar2=ptb[:, l, 21 + hp:22 + hp],
                                                           op0=ALU.add, op1=ALU.mult), reads=[t_r[4], t_r[6], c_r], writes=[t_r[6]])
                    kb.op("dve", lambda v: v.scalar_tensor_tensor(out=t7[:, :], in0=t7[:, :], scalar=1.0, in1=kT[:, :], op0=ALU.add, op1=ALU.mult),
                          reads=[t_r[6], kT_r], writes=[t_r[6]])
                    v3 = lambda ap: ap.rearrange("p (c i) -> p c i", c=NCH)
                    for hh in range(2):
                        pr = slice(hh * 64, hh * 64 + 64)
                        kb.op("dve", lambda v: v.scalar_tensor_tensor(out=AR[pr, hp, 0, :, hh, :], in0=v3(t6[pr, :]), scalar=-1.0, in1=v3(t4[pr, :]),
                                                                      op0=ALU.mult, op1=ALU.mult), reads=[t_r[5], t_r[3]], writes=[AR_r[hp]])
                        kb.op("dve", lambda v: v.tensor_tensor(out=AR[pr, hp, 1, :, hh, :], in0=v3(rT[pr, :]), in1=v3(Gam[pr, hp, :]), op=ALU.mult),
                              reads=[rT_r, Gam_r[hp]], writes=[AR_r[hp]])
                    kb.op("dve", lambda v: v.tensor_tensor(out=t1[:, :], in0=t6[:, :], in1=t5[:, :], op=ALU.mult),
                          reads=[t_r[5], t_r[4], t_r[0]], writes=[t_r[0]])
                    for hh in range(2):
                        pr = slice(hh * 64, hh * 64 + 64)
                        kb.op("dve", lambda v: v.tensor_tensor(out=BK[pr, hp, 0, :, hh, :], in0=v3(t1[pr, :]), in1=v3(t3[pr, :]), op=ALU.mult),
                              reads=[t_r[0], t_r[2]], writes=[BK_r[hp]])
                        kb.op("dve", lambda v: v.tensor_tensor(out=BK[pr, hp, 1, :, hh, :], in0=v3(t7[pr, :]), in1=v3(t3[pr, :]), op=ALU.mult),
                              reads=[t_r[6], t_r[2]], writes=[BK_r[hp]])
                        kb.op("dve", lambda v: v.scalar_tensor_tensor(out=rkrz[pr, hp, :, hh, :], in0=v3(rT[pr, :]), scalar=ptb[pr, l, 23 + hp:24 + hp],
                                                                      in1=v3(t7[pr, :]), op0=ALU.mult, op1=ALU.mult),
                              reads=[rT_r, t_r[6], c_r], writes=[rkrz_r[hp]])
                    if l == 0:
                        kb.dma("sp", vfirst_d.ap()[hp, :, sq * 0 + tok0:tok0 + MT], vT[:, :], reads=[vT_r], writes=[vfirst_r[hp]])
                    else:
                        kb.op("pe", lambda pe: pe.matmul(PS[5][:, 0:MT], lhsT=v2[:, hp * 128:(hp + 1) * 128], rhs=VR[:, :], start=True, stop=True),
                              reads=[p_r, VR_r], writes=[PS_r[5]])
                        kb.op("act", lambda a: a.activation(out=t1[:, :], in_=PS[5][:, 0:MT], func=AF.Sigmoid, bias=ptb[:, l, 17 + hp:18 + hp]),
                              reads=[PS_r[5], c_r, t_r[0]], writes=[t_r[0]])
                        kb.dma("sp", t2[:, :], vfirst_d.ap()[hp, :, tok0:tok0 + MT], reads=[vfirst_r[hp], t_r[1]], writes=[t_r[1]])
                        kb.op("dve", lambda v: v.tensor_tensor(out=t2[:, :], in0=t2[:, :], in1=vT[:, :], op=ALU.subtract),
                              reads=[t_r[1], vT_r], writes=[t_r[1]])
                        kb.op("dve", lambda v: v.tensor_tensor(out=t2[:, :], in0=t2[:, :], in1=t1[:, :], op=ALU.mult),
                              reads=[t_r[1], t_r[0]], writes=[t_r[1]])
                        kb.op("dve", lambda v: v.tensor_tensor(out=vT[:, :], in0=vT[:, :], in1=t2[:, :], op=ALU.add),
                              reads=[t_r[1], vT_r], writes=[vT_r])
                    for hh in range(2):
                        pr = slice(hh * 64, hh * 64 + 64)
                        kb.op("act", lambda a: a.activation(out=VTz[pr, hp, :, hh, :], in_=vT[pr, :].rearrange("p (c i) -> p c i", c=NCH), func=AF.Copy),
                              reads=[vT_r], writes=[VTz_r[hp]])
                if rs_ <= 2:
                    return
                fl = lambda ap: ap.rearrange("p a b -> p (a b)")
                def fn(pe):
                    inst = None
                    for c in range(NCH):
                        for hp in range(2):
                            inst = pe.matmul(PS[2][:, (c * 2 + hp) * 64:(c * 2 + hp + 1) * 64], lhsT=fl(VTz[:, hp, c, :, :]), rhs=istk[:, :], start=True, stop=True)
                    return inst
                kb.op("pe", fn, reads=VTz_r + [p_r], writes=[PS_r[2]])
                kb.op("act", lambda a: a.activation(out=Vstk[:, :, :, :], in_=PS[2][:, :].rearrange("p (c h d) -> p c h d", c=NCH, h=2), func=AF.Copy),
                      reads=[PS_r[2]], writes=[Vstk_r, PS_r[2]])

                def fn(pe):
                    inst = None
                    for c in range(NCH):
                        for hp in range(2):
                            inst = pe.matmul(PS[3][:, c * 2 + hp:c * 2 + hp + 1], lhsT=fl(rkrz[:, hp, c, :, :]), rhs=onec[:, :], start=True, stop=True)
                    return inst
                kb.op("pe", fn, reads=rkrz_r + [p_r], writes=[PS_r[3]])
                kb.op("act", lambda a: a.activation(out=cb[:, :, :], in_=PS[3][:, 0:NCH * 2].rearrange("p (c h) -> p c h", c=NCH), func=AF.Copy),
                      reads=[PS_r[3]], writes=[cb_r, PS_r[3]])
                for bk, dst, dst_r, pbi in ((0, Bblk, Bblk_r, 0), (1, Kblk, Kblk_r, 1)):
                    def fn(pe, bk=bk, pbi=pbi):
                        inst = None
                        for c in range(NCH):
                            for hp in range(2):
                                inst = pe.transpose(out=PB[pbi][:, (c * 2 + hp) * 128:(c * 2 + hp + 1) * 128], in_=fl(BK[:, hp, bk, c, :, :]), identity=ident[:, :])
                        return inst
                    kb.op("pe", fn, reads=BK_r + [c_r], writes=[PB_r[pbi]])
                    kb.op("dve", lambda v, dst=dst, pbi=pbi: v.tensor_copy(out=dst[:, :, :, :], in_=PB[pbi][:, :].rearrange("p (c h d) -> p c h d", c=NCH, h=2)),
                          reads=[PB_r[pbi]], writes=[dst_r, PB_r[pbi]])
                for c2 in range(NCH // 2):
                    def fn(pe):
                        inst = None
                        for cc_ in range(2):
                            c = c2 * 2 + cc_
                            pe.matmul(PS[3][:, cc_ * 256:(cc_ + 1) * 256], lhsT=fl(SGLd[:, c, :, :]), rhs=g2a[:, :], start=True, stop=False)
                            inst = pe.matmul(PS[3][:, cc_ * 256:(cc_ + 1) * 256], lhsT=fl(SGL2d[:, c, :, :]), rhs=g2b[:, :], start=False, stop=True)
                        return inst
                    kb.op("pe", fn, reads=[SGL_r, SGL2_r, p_r], writes=[PS_r[3]])
                    for hh in range(2):
                        pr = slice(hh * 64, hh * 64 + 64)
                        kb.op("act", lambda a: a.activation(out=gtok[pr, c2 * 2:c2 * 2 + 2, :, :],
                                                            in_=PS[3][pr, :].rearrange("p (c h g d) -> p c h g d", c=2, h=2, g=2)[:, :, :, hh, :], func=AF.Copy),
                              reads=[PS_r[3]], writes=[gtok_r, PS_r[3]])
                for c in range(NCH):
                    for hp in range(2):
                        pu = c * 2 + hp
                        px, py = (4, 5) if pu % 2 == 0 else (2, 3)

                        def fn(pe, px=px, py=py):
                            pe.matmul(PS[px][:, 0:128], lhsT=fl(AR[:, hp, 0, c, :, :]), rhs=fl(BK[:, hp, 0, c, :, :]), start=True, stop=True)
                            pe.matmul(PS[px][:, 128:384], lhsT=fl(BK[:, hp, 0, c, :, :]), rhs=AR[:, hp, :, c, :, :].rearrange("p a h i -> p a (h i)"),
                                      start=True, stop=True)
                            return pe.matmul(PS[py][:, 0:256], lhsT=fl(BK[:, hp, 1, c, :, :]), rhs=AR[:, hp, :, c, :, :].rearrange("p a h i -> p a (h i)"),
                                             start=True, stop=True)
                        kb.op("pe", fn, reads=AR_r + BK_r, writes=[PS_r[px], PS_r[py]])
                        kb.op("dve", lambda v, px=px: v.tensor_tensor(out=MN[0][:, pu, :, :], in0=PS[px][:, 0:256].rearrange("p (a b) -> p a b", a=2),
                                                                      in1=m3[:, 0:2, :], op=ALU.mult), reads=[PS_r[px], p_r], writes=[MN_r[0], PS_r[px]])
                        kb.op("dve", lambda v, px=px: v.tensor_tensor(out=A3[:, pu, 0, :], in0=PS[px][:, 256:384], in1=m3[:, 2, :], op=ALU.mult),
                              reads=[PS_r[px], p_r], writes=[A3_r, PS_r[px]])
                        kb.op("dve", lambda v, py=py: v.tensor_tensor(out=A3[:, pu, 1:3, :], in0=PS[py][:, 0:256].rearrange("p (a b) -> p a b", a=2),
                                                                      in1=m3[:, 1:3, :], op=ALU.mult), reads=[PS_r[py], p_r], writes=[A3_r, PS_r[py]])
                if rs_ <= 3:
                    return
                kb.op("dve", lambda v: v.tensor_tensor(out=Pm[:, :, :], in0=MN[0][:, :, 1, :],
                                                      in1=bass.AP(identf, 0, [[128, 128], [0, PU], [1, 128]]), op=ALU.add),
                      reads=[MN_r[0], c_r], writes=[P_r])
                cur = 0
                for lev in range(5):
                    nxt = 1 - cur
                    lastlev = (lev == 4)
                    for g2_ in range(PU // 2):
                        pi = 4 + (g2_ % 2)

                        def fn(pe, pi=pi):
                            inst = None
                            for q in range(2):
                                u = g2_ * 2 + q
                                inst = pe.matmul(PS[pi][:, q * 256:q * 256 + 128], lhsT=MN[cur][:, u, 1, :], rhs=MN[cur][:, u, 0, :], start=True, stop=True)
                                if not lastlev:
                                    inst = pe.matmul(PS[pi][:, q * 256 + 128:q * 256 + 256], lhsT=MN[cur][:, u, 0, :], rhs=MN[cur][:, u, 1, :],
                                                     start=True, stop=True)
                            return inst
                        kb.op("pe", fn, reads=[MN_r[cur]], writes=[PS_r[pi]])
                        kb.op("act", lambda a, pi=pi: a.activation(out=MN[nxt][:, g2_ * 2:g2_ * 2 + 2, :, :],
                                                                   in_=PS[pi][:, :].rearrange("p (u a b) -> p u a b", u=2, a=2), func=AF.Copy),
                              reads=[PS_r[pi]], writes=[MN_r[nxt], PS_r[pi]])
                    for g4_ in range(PU // 4):
                        pi = 2 + (g4_ % 2)

                        def fn(pe, pi=pi):
                            inst = None
                            for q in range(4):
                                u = g4_ * 4 + q
                                inst = pe.matmul(PS[pi][:, q * 128:(q + 1) * 128], lhsT=MN[nxt][:, u, 0, :], rhs=Pm[:, u, :], start=True, stop=True)
                            return inst
                        kb.op("pe", fn, reads=[MN_r[nxt], P_r], writes=[PS_r[pi]])
                        kb.op("dve", lambda v, pi=pi: v.tensor_tensor(out=Pm[:, g4_ * 4:g4_ * 4 + 4, :], in0=PS[pi][:, :].rearrange("p (u b) -> p u b", u=4),
                                                                      in1=Pm[:, g4_ * 4:g4_ * 4 + 4, :], op=ALU.add), reads=[PS_r[pi], P_r], writes=[P_r, PS_r[pi]])
                    cur = nxt

            def chain(mt):
                tok0 = mt * MT
                par = mt % 2
                AR, Vstk, Bblk, Kblk, Gam, gtok, cb, Pm, A3 = AR2[par], Vstk2[par], Bblk2[par], Kblk2[par], Gam2[par], gtok2[par], cb2[par], Pm2[par], A32[par]
                AR_r, Vstk_r, Bblk_r, Kblk_r, Gam_r, gtok_r, cb_r, P_r, A3_r = AR_r2[par], Vstk_r2[par], Bblk_r2[par], Kblk_r2[par], Gam_r2[par], gtok_r2[par], cb_r2[par], P_r2[par], A3_r2[par]
                fl = lambda ap: ap.rearrange("p a b -> p (a b)")
                if rs_ <= 4:
                    return
                for c in range(NCH):
                    def fn(pe):
                        inst = None
                        for hp in range(2):
                            pu = c * 2 + hp
                            pe.matmul(PS[0][:, hp * 64:(hp + 1) * 64], lhsT=A3[:, pu, 1, :], rhs=Vstk[:, c, hp, :], start=True, stop=False)
                            inst = pe.matmul(PS[0][:, hp * 64:(hp + 1) * 64], lhsT=fl(AR[:, hp, 0, c, :, :]), rhs=Tb[:, hp, :], start=False, stop=True)
                        return inst
                    kb.op("pe", fn, reads=[A3_r, Vstk_r, Tb_r] + AR_r, writes=[PS_r[0]])
                    kb.op("act", lambda a: a.activation(out=Rs[:, :], in_=PS[0][:, 0:128], func=AF.Copy), reads=[PS_r[0]], writes=[Rs_r, PS_r[0]])

                    def fn(pe):
                        inst = None
                        for hp in range(2):
                            pu = c * 2 + hp
                            inst = pe.matmul(PS[1][:, hp * 64:(hp + 1) * 64], lhsT=Pm[:, pu, :], rhs=Rs[:, hp * 64:(hp + 1) * 64], start=True, stop=True)
                        return inst
                    kb.op("pe", fn, reads=[P_r, Rs_r], writes=[PS_r[1]])
                    kb.op("act", lambda a: a.activation(out=Ub[:, :], in_=PS[1][:, 0:128], func=AF.Copy), reads=[PS_r[1]], writes=[Ub_r, PS_r[1]])

                    def fn(pe):
                        inst = None
                        for hp in range(2):
                            pu = c * 2 + hp
                            pe.matmul(PS[0][:, hp * 64:(hp + 1) * 64], lhsT=fl(AR[:, hp, 1, c, :, :]), rhs=Tb[:, hp, :], start=True, stop=False)
                            pe.matmul(PS[0][:, hp * 64:(hp + 1) * 64], lhsT=A3[:, pu, 0, :], rhs=Ub[:, hp * 64:(hp + 1) * 64], start=False, stop=False)
                            inst = pe.matmul(PS[0][:, hp * 64:(hp + 1) * 64], lhsT=A3[:, pu, 2, :], rhs=Vstk[:, c, hp, :], start=False, stop=True)
                        return inst
                    kb.op("pe", fn, reads=[A3_r, Vstk_r, Tb_r, Ub_r] + AR_r, writes=[PS_r[0]])

                    def fn(pe):
                        inst = None
                        for hp in range(2):
                            pe.matmul(PS[1][:, hp * 64:(hp + 1) * 64], lhsT=Bblk[:, c, hp, :], rhs=Ub[:, hp * 64:(hp + 1) * 64], start=True, stop=False)
                            inst = pe.matmul(PS[1][:, hp * 64:(hp + 1) * 64], lhsT=Kblk[:, c, hp, :], rhs=Vstk[:, c, hp, :], start=False, stop=True)
                        return inst
                    kb.op("pe", fn, reads=[Bblk_r, Kblk_r, Vstk_r, Ub_r], writes=[PS_r[1]])
                    kb.op("dve", lambda v: v.tensor_tensor(out=tmpS[:, :, :], in0=PS[1][:, 0:128].rearrange("p (h d) -> p h d", h=2), in1=Tst[:, :, :], op=ALU.add),
                          reads=[PS_r[1], T_r], writes=[tmpS_r, PS_r[1]])
                    kb.op("dve", lambda v: v.tensor_tensor(out=Tst[:, :, :], in0=tmpS[:, :, :],
                                                          in1=bass.AP(Gam, c * 64 + 63, [[2 * MT, 128], [MT, 2], [0, 64]]), op=ALU.mult),
                          reads=[tmpS_r, Gam_r[0], Gam_r[1]], writes=[T_r])
                    kb.op("act", lambda a: a.activation(out=Tb[:, :, :], in_=Tst[:, :, :], func=AF.Copy), reads=[T_r], writes=[Tb_r])
                    if rs_ <= 5:
                        continue
                    v2_ = lambda ap: ap.rearrange("p (h d) -> p h d", h=2)
                    kb.op("act", lambda a: a.activation(out=ow[:, :], in_=PS[0][:, 0:128], func=AF.Copy), reads=[PS_r[0]], writes=[ow_r, PS_r[0]])
                    kb.op("act", lambda a: a.activation(out=ow2[:, :], in_=ow[:, :], func=AF.Square), reads=[ow_r], writes=[ow2_r])
                    kb.op("dve", lambda v: v.reduce_sum(out=st[:, 0:2], in_=v2_(ow[:, :]), axis=AX.X), reads=[ow_r], writes=[st_r])
                    kb.op("dve", lambda v: v.reduce_sum(out=st[:, 2:4], in_=v2_(ow2[:, :]), axis=AX.X), reads=[ow2_r, st_r], writes=[st_r])
                    kb.op("dve", lambda v: v.tensor_scalar(out=st[:, 4:8], in0=st[:, 0:4], scalar1=1.0 / 64.0, scalar2=None, op0=ALU.mult), reads=[st_r], writes=[st_r])
                    kb.op("dve", lambda v: v.tensor_tensor(out=st[:, 8:10], in0=st[:, 4:6], in1=st[:, 4:6], op=ALU.mult), reads=[st_r], writes=[st_r])
                    kb.op("dve", lambda v: v.tensor_tensor(out=st[:, 10:12], in0=st[:, 6:8], in1=st[:, 8:10], op=ALU.subtract), reads=[st_r], writes=[st_r])
                    kb.op("act", lambda a: a.activation(out=st[:, 12:14], in_=st[:, 10:12], func=AF.Sqrt, bias=RW_EPS, scale=1.0), reads=[st_r], writes=[st_r])
                    kb.op("dve", lambda v: v.reciprocal(out=st[:, 14:16], in_=st[:, 12:14]), reads=[st_r], writes=[st_r])
                    for hp in range(2):
                        kb.op("dve", lambda v: v.tensor_scalar(out=ow[:, hp * 64:(hp + 1) * 64], in0=ow[:, hp * 64:(hp + 1) * 64],
                                                               scalar1=st[:, 4 + hp:5 + hp], scalar2=st[:, 14 + hp:15 + hp],
                                                               op0=ALU.subtract, op1=ALU.mult), reads=[ow_r, st_r], writes=[ow_r])
                    kb.op("dve", lambda v: v.tensor_tensor(out=v2_(ow[:, :]), in0=v2_(ow[:, :]), in1=rln[:, 0, :, :], op=ALU.mult), reads=[ow_r, p_r], writes=[ow_r])
                    kb.op("dve", lambda v: v.tensor_tensor(out=v2_(ow[:, :]), in0=v2_(ow[:, :]), in1=rln[:, 1, :, :], op=ALU.add), reads=[ow_r, p_r], writes=[ow_r])
                    if rs_ <= 6:
                        continue
                    for hp in range(2):
                        kb.op("dve", lambda v: v.scalar_tensor_tensor(out=ow[:, hp * 64:(hp + 1) * 64], in0=Vstk[:, c, hp, :], scalar=cb[:, c, hp:hp + 1],
                                                                      in1=ow[:, hp * 64:(hp + 1) * 64], op0=ALU.mult, op1=ALU.add),
                              reads=[Vstk_r, cb_r, ow_r], writes=[ow_r])
                    for hh in range(2):
                        pr = slice(hh * 64, hh * 64 + 64)
                        kb.op("dve", lambda v: v.tensor_tensor(out=obblk[pr, :, hh * 64:(hh + 1) * 64], in0=v2_(ow[pr, :]), in1=gtok[pr, c, :, :], op=ALU.mult),
                              reads=[ow_r, gtok_r], writes=[ob_r])

                    if rs_ <= 7:
                        continue

                    def fn(pe):
                        inst = None
                        for hp in range(2):
                            inst = pe.matmul(PS[1][:, 256 + hp * 64:256 + (hp + 1) * 64], lhsT=obblk[:, hp, :], rhs=istk[:, :], start=True, stop=True)
                        return inst
                    kb.op("pe", fn, reads=[ob_r, p_r], writes=[PS_r[1]])
                    tokc = tok0 + c * 64
                    kb.op("act", lambda a: a.activation(out=OT[:, 2:4, tokc:tokc + 64], in_=PS[1][:, 256:384].rearrange("p (h t) -> p h t", h=2), func=AF.Copy),
                          reads=[PS_r[1]], writes=[OT_r[tokc // 128], PS_r[1]])
            nmt = S // MT if rs_ > 0 else 0
            if nmt:
                prep(0)
            for mt in range(nmt):
                if mt + 1 < nmt:
                    prep(mt + 1)
                chain(mt)
            kb.barrier()

    NQ, NQS, NKC, NKS, NKW, NVC, NVS, NGT = 0, 512, 1024, 1280, 1536, 1792, 1920, 2176
    GK = 1.5957691216057308

    def nsa_phase(sq, l, OT, OT_r):
        ns_ = cfg.get("nsa_stop", 99)
        with ExitStack() as es:
            A = lambda n_, s_, d_: es.enter_context(sb(n_, s_, d_))
            QT = A("nQT", [128, 4, S], BF16)
            KTz = A("nKTz", [128, 3, 2, S], BF16)
            VCz = A("nVCz", [128, 2, S], BF16)
            VS = A("nVS", [128, NT, 2, 65], BF16)
            VW = A("nVW", [128, NT, 2, 65], BF16)
            gsig = A("ngsig", [128, NT, 24], F32)
            kcmpTz = A("nkcmp", [128, 2, 128], BF16)
            vcmp = A("nvcmp", [128, 2, 97], BF16)
            QT_r, KT_r, VC_r, VS_r, VW_r, gs_r = regs(4), regs(4), regs(4), regs(NT), regs(NT), regs(NT)
            kc_r, vcm_r = Reg(), Reg()
            kb.op("pool", lambda g: g.memset(KTz[:], 0.0), writes=KT_r)
            kb.op("pool", lambda g: g.memset(VCz[:], 0.0), writes=VC_r)
            kb.op("pool", lambda g: g.memset(VS[:, :, :, 64:65], 1.0), writes=VS_r)
            kb.op("pool", lambda g: g.memset(VW[:, :, :, 64:65], 1.0), writes=VW_r)
            kb.op("pool", lambda g: g.memset(kcmpTz[:], 0.0), writes=[kc_r])
            kb.op("pool", lambda g: g.memset(vcmp[:], 0.0), writes=[vcm_r])
            with ExitStack() as es1:
                A1 = lambda n_, s_, d_: es1.enter_context(sb(n_, s_, d_))
                Wn = A1("nWn", [128, 8, 2200], BF16)
                rp = A1("nrp", [128, 2, 512], F32)
                t1 = A1("nt1", [128, 512], F32)
                t2 = A1("nt2", [128, 512], F32)
                W_r, rp_r, t1_r, t2_r = Reg(), Reg(), Reg(), Reg()
                load_w(Wn, w_nsa16.ap()[l], 2200, W_r)
                for mt in range(4):
                    tok0 = mt * 512
                    bl = slice(tok0, tok0 + 512)
                    xr = XT_r[mt * 4:mt * 4 + 4]
                    kb.dma("sp", rp[:, 0, :], c_rope.ap()[0][:, bl], writes=[rp_r])
                    kb.dma("sp", rp[:, 1, :], c_rope.ap()[1][:, bl], writes=[rp_r])
                    for i in range(4):
                        proj_fm(PS[0][:, :], Wn, NQ + i * 128, 128, tok0, 512, [W_r], xr, PS_r[0])
                        proj_fm(PS[1][:, :], Wn, NQS + i * 128, 128, tok0, 512, [W_r], xr, PS_r[1])
                        kb.op("dve", lambda v: v.tensor_tensor(out=t1[:, :], in0=PS[0][:, :], in1=rp[:, 0, :], op=ALU.mult),
                              reads=[PS_r[0], rp_r], writes=[t1_r])
                        kb.op("dve", lambda v: v.scalar_tensor_tensor(out=t2[:, :], in0=PS[1][:, :], scalar=0.125, in1=rp[:, 1, :],
                                                                      op0=ALU.mult, op1=ALU.mult), reads=[PS_r[1], rp_r], writes=[t2_r])
                        kb.op("dve", lambda v: v.scalar_tensor_tensor(out=QT[:, i, bl], in0=t1[:, :], scalar=0.125, in1=t2[:, :],
                                                                      op0=ALU.mult, op1=ALU.add), reads=[t1_r, t2_r], writes=[QT_r[mt]])
                    for ty, c0 in ((0, NKC), (1, NKS), (2, NKW)):
                        proj_fm(PS[0][:, :], Wn, c0, 128, tok0, 512, [W_r], xr, PS_r[0])
                        proj_fm(PS[1][:, :], Wn, c0 + 128, 128, tok0, 512, [W_r], xr, PS_r[1])
                        kb.op("dve", lambda v: v.tensor_tensor(out=t1[:, :], in0=PS[0][:, :], in1=rp[:, 0, :], op=ALU.mult),
                              reads=[PS_r[0], rp_r], writes=[t1_r])
                        kb.op("dve", lambda v: v.tensor_tensor(out=t2[:, :], in0=PS[1][:, :], in1=rp[:, 1, :], op=ALU.mult),
                              reads=[PS_r[1], rp_r], writes=[t2_r])
                        for g in range(2):
                            pr = slice(g * 64, g * 64 + 64)
                            kb.op("dve", lambda v: v.tensor_tensor(out=KTz[pr, ty, g, bl], in0=t1[pr, :], in1=t2[pr, :], op=ALU.add),
                                  reads=[t1_r, t2_r], writes=[KT_r[mt]])
                    proj_fm(PS[2][:, :], Wn, NVC, 128, tok0, 512, [W_r], xr, PS_r[2])
                    for g in range(2):
                        pr = slice(g * 64, g * 64 + 64)
                        kb.op("act", lambda a: a.activation(out=VCz[pr, g, bl], in_=PS[2][pr, :], func=AF.Copy),
                              reads=[PS_r[2]], writes=[VC_r[mt]])
                    for j in range(4):
                        tt = mt * 4 + j
                        proj_tm(PS[3][:, 0:256], Wn, NVS, 256, tt, [W_r], PS_r[3])
                        kb.op("act", lambda a: a.activation(out=VS[:, tt, :, 0:64], in_=PS[3][:, 0:128].rearrange("p (g d) -> p g d", g=2), func=AF.Copy),
                              reads=[PS_r[3]], writes=[VS_r[tt], PS_r[3]])
                        kb.op("act", lambda a: a.activation(out=VW[:, tt, :, 0:64], in_=PS[3][:, 128:256].rearrange("p (g d) -> p g d", g=2), func=AF.Copy),
                              reads=[PS_r[3]], writes=[VW_r[tt], PS_r[3]])
                        proj_tm(PS[4][:, 0:24], Wn, NGT, 24, tt, [W_r], PS_r[4])
                        kb.op("act", lambda a: a.activation(out=gsig[:, tt, :], in_=PS[4][:, 0:24], func=AF.Sigmoid),
                              reads=[PS_r[4]], writes=[gs_r[tt]])
                kb.barrier()
            if ns_ <= 1:
                kb.barrier()
                return
            with ExitStack() as es2:
                A2 = lambda n_, s_, d_: es2.enter_context(sb(n_, s_, d_))
                w1d = A2("nw1d", [128, 32, 256], BF16)
                w2d = A2("nw2d", [128, 2, 128], BF16)
                wv2 = A2("nwv2", [128, 2, 64], BF16)
                posz = A2("nposz", [128, 2, 32], BF16)
                hc = A2("nhc", [128, 2], F32)
                gx = A2("ngx", [128, 128], F32)
                gw = A2("ngw", [128, 128], F32)
                gh = A2("ngh", [128, 2, 128], BF16)
                w1_r, w2_r, hc_r, gx_r, gw_r, gh_r = Reg(), Reg(), Reg(), Reg(), Reg(), Reg()
                kb.op("pool", lambda g: g.memset(posz[:], 0.0), writes=[w2_r])
                kb.op("pool", lambda g: g.memset(gh[:], 0.0), writes=[gh_r])
                for kv in range(2):
                    kb.dma("pool", posz[0:64, kv, :], nsa_posT.ap()[l, kv], reads=[w2_r], writes=[w2_r])
                for half in range(2):
                    kb.dma("pool", w2d[:, :, half * 64:(half + 1) * 64], nsa_wk2.ap()[l].rearrange("(t p) n -> p t n", p=128), writes=[w2_r])
                kb.dma("pool", wv2[:, :, :], nsa_wv2.ap()[l].rearrange("(t p) n -> p t n", p=128), writes=[w2_r])
                kb.dma("pool", vcmp[:, 0, 65:97], c_ovl.ap(), reads=[vcm_r], writes=[vcm_r])
                kb.dma("pool", vcmp[:, 1, 65:97], c_ovl.ap(), reads=[vcm_r], writes=[vcm_r])
                kb.op("pool", lambda g: g.memset(vcmp[0:127, :, 64:65], 1.0), reads=[vcm_r], writes=[vcm_r])
                for kv, w1src in ((0, wk1_16), (1, wv1_16)):
                    src3 = w1src.ap()[l].rearrange("(l d) n -> d l n", d=64)
                    for half in range(2):
                        for l4 in range(8):
                            kb.dma("sp", w1d[half * 64:(half + 1) * 64, l4 * 4:(l4 + 1) * 4, :], src3[:, l4 * 4:(l4 + 1) * 4, :],
                                   reads=[w16_r], writes=[w1_r])
                    srcz = (lambda g: KTz[:, 0, g, :]) if kv == 0 else (lambda g: VCz[:, g, :])
                    src_regs = KT_r if kv == 0 else VC_r
                    for hf in range(2):
                        def fn(pe):
                            inst = None
                            for ll in range(32):
                                inst = pe.matmul(PS[0][:, 0:1], lhsT=w1d[:, ll, hf * 128:(hf + 1) * 128], rhs=posz[:, kv, ll:ll + 1],
                                                 start=(ll == 0), stop=(ll == 31))
                            return inst
                        kb.op("pe", fn, reads=[w1_r, w2_r], writes=[PS_r[0]])
                        kb.op("act", lambda a: a.activation(out=hc[:, hf:hf + 1], in_=PS[0][:, 0:1], func=AF.Copy), reads=[PS_r[0]], writes=[hc_r])
                    for g in range(2):
                        for hf in range(2):
                            def fn(pe):
                                inst = None
                                for ll in range(32):
                                    rhs = bass.AP(srcz(g).tensor, srcz(g).offset + ll, [srcz(g).ap[0], [16, 127]])
                                    inst = pe.matmul(PS[1][:, 0:127], lhsT=w1d[:, ll, hf * 128:(hf + 1) * 128], rhs=rhs,
                                                     start=(ll == 0), stop=(ll == 31))
                                return inst
                            kb.op("pe", fn, reads=[w1_r] + src_regs, writes=[PS_r[1]])
                            kb.op("act", lambda a: a.activation(out=gx[:, 0:127], in_=PS[1][:, 0:127], func=AF.Identity, bias=hc[:, hf:hf + 1], scale=1.0),
                                  reads=[PS_r[1], hc_r], writes=[gx_r])
                            kb.op("dve", lambda v: v.tensor_tensor(out=gw[:, 0:127], in0=gx[:, 0:127], in1=gx[:, 0:127], op=ALU.mult),
                                  reads=[gx_r], writes=[gw_r])
                            kb.op("dve", lambda v: v.tensor_scalar(out=gw[:, 0:127], in0=gw[:, 0:127], scalar1=0.044715, scalar2=1.0,
                                                                   op0=ALU.mult, op1=ALU.add), reads=[gw_r], writes=[gw_r])
                            kb.op("dve", lambda v: v.tensor_tensor(out=gw[:, 0:127], in0=gw[:, 0:127], in1=gx[:, 0:127], op=ALU.mult),
                                  reads=[gw_r, gx_r], writes=[gw_r])
                            kb.op("act", lambda a: a.activation(out=gw[:, 0:127], in_=gw[:, 0:127], func=AF.Sigmoid, scale=GK), reads=[gw_r], writes=[gw_r])
                            kb.op("dve", lambda v: v.tensor_tensor(out=gh[:, hf, 0:127], in0=gx[:, 0:127], in1=gw[:, 0:127], op=ALU.mult),
                                  reads=[gw_r, gx_r], writes=[gh_r])
                        if kv == 0:
                            def fn(pe):
                                pe.matmul(PS[2][:, 0:127], lhsT=w2d[:, 0, :], rhs=gh[:, 0, 0:127], start=True, stop=False)
                                return pe.matmul(PS[2][:, 0:127], lhsT=w2d[:, 1, :], rhs=gh[:, 1, 0:127], start=False, stop=True)
                            kb.op("pe", fn, reads=[w2_r, gh_r], writes=[PS_r[2]])
                            pr = slice(g * 64, g * 64 + 64)
                            kb.op("act", lambda a: a.activation(out=kcmpTz[pr, g, 0:127], in_=PS[2][pr, 0:127], func=AF.Copy),
                                  reads=[PS_r[2], kc_r], writes=[kc_r])
                        else:
                            def fn(pe):
                                pe.matmul(PS[2][:, 0:64], lhsT=gh[:, 0, :], rhs=wv2[:, 0, :], start=True, stop=False)
                                return pe.matmul(PS[2][:, 0:64], lhsT=gh[:, 1, :], rhs=wv2[:, 1, :], start=False, stop=True)
                            kb.op("pe", fn, reads=[w2_r, gh_r], writes=[PS_r[2]])
                            kb.op("act", lambda a: a.activation(out=vcmp[0:127, g, 0:64], in_=PS[2][0:127, 0:64], func=AF.Copy),
                                  reads=[PS_r[2], vcm_r], writes=[vcm_r])
                kb.barrier()
            if ns_ <= 2:
                kb.barrier()
                return
            selbT = A("nselbT", [128, 2, S], BF16)
            onehot = A("nonehot", [128, S], BF16)
            cmask = A("ncmask", [128, S], BF16)
            selm = A("nselm", [128, 2, NT, 32], F32)
            wmask = A("nwmask", [128, 128], F32)
            PT = [A("nPT%d" % i, [128, 512], BF16) for i in range(2)]
            PTs = [A("nPTs%d" % i, [128, 512], BF16) for i in range(4)]
            exa = [A("nexa%d" % i, [128, 512], F32) for i in range(2)]
            exa_r = regs(2)
            ocsa = [A("nocsa%d" % i, [128, 4, 97], F32) for i in range(2)]
            ocsa_r = regs(2)
            dna = [A("ndna%d" % i, [128, 16], F32) for i in range(2)]
            dna_r = regs(2)
            scl = [A("nscl%d" % i, [128, 32], F32) for i in range(2)]
            cm3l = [A("ncm3l%d" % i, [128, 32, 32], F32) for i in range(2)]
            sb16l = [A("nsb16l%d" % i, [128, 32], BF16) for i in range(2)]
            scl_r, cm3l_r, sb16l_r = regs(2), regs(2), regs(2)
            PTw = [A("nPTw%d" % i, [128, 640], BF16) for i in range(4)]
            PTw_r = regs(4)
            exw = [A("nexw%d" % i, [128, 128], F32) for i in range(2)]
            exw_r = regs(2)
            ocsw = [A("nocsw%d" % i, [128, 4, 65], F32) for i in range(2)]
            ocsw_r = regs(2)
            dnw = [A("ndnw%d" % i, [128, 16], F32) for i in range(2)]
            dnw_r = regs(2)
            PTs_r = regs(4)
            ocss = [A("nocss%d" % i, [128, 4, 65], F32) for i in range(2)]
            ocss_r = regs(2)
            dns = [A("ndns%d" % i, [128, 16], F32) for i in range(2)]
            dns_r = regs(2)
            ONSA2 = [A("nONSA%d" % i_, [128, 4, 512], F32) for i_ in range(2)]
            impb2 = [A("nimp%d" % i_, [128, 4, 2, 32], F32) for i_ in range(2)]
            obf = A("nobf", [128, 512], BF16)
            k_r, sel_r, PT_r, ex_r, ocs_r, sc_r, cm3_r, sb16_r, dn_r, obf_r = \
                Reg(), regs(4), regs(2), Reg(), Reg(), Reg(), Reg(), Reg(), Reg(), Reg()
            on_r2, imp_r2 = [regs(4), regs(4)], regs(2)
            kb.op("pool", lambda g: g.memset(selbT[:], 0.0), writes=sel_r)
            kb.op("pool", lambda g: g.memset(onehot[:], 0.0), writes=[k_r])
            kb.dma("pool", onehot[0:32, :], c_onehot.ap(), reads=[k_r], writes=[k_r])
            kb.dma("pool", cmask[:, :], c_cmpmask.ap(), writes=[k_r])
            for m_ in range(2):
                kb.dma("sp", selm[:, m_, :, :], c_selm.ap()[m_], writes=[k_r])
            kb.op("dve", lambda v: v.tensor_scalar(out=wmask[:, :], in0=caus[:, 0, :], scalar1=-1.0, scalar2=1.0, op0=ALU.mult, op1=ALU.add),
                  reads=[c_r], writes=[k_r])

            def bf_math(buf, buf_r, dnv, dnv_r, nq, h, b, tts, first, ONSA, on_r):
                kb.op("dve", lambda v: v.tensor_scalar(out=dnv[:, 0:nq], in0=buf[:, 0:nq, 64], scalar1=1e-30, scalar2=None, op0=ALU.max),
                      reads=[buf_r], writes=[dnv_r])
                kb.op("dve", lambda v: v.reciprocal(out=dnv[:, 0:nq], in_=dnv[:, 0:nq]), reads=[dnv_r], writes=[dnv_r])
                kb.op("dve", lambda v: v.tensor_tensor(out=dnv[:, 8:8 + nq], in0=dnv[:, 0:nq], in1=gsig[:, tts[0]:tts[0] + nq, h * 3 + b], op=ALU.mult),
                      reads=[dnv_r] + gs_r[tts[0]:tts[0] + nq], writes=[dnv_r])
                for qi in range(nq):
                    tl = tts[qi] % 4
                    if first:
                        kb.op("dve", lambda v: v.tensor_scalar(out=ONSA[:, tl, h * 64:(h + 1) * 64], in0=buf[:, qi, 0:64], scalar1=dnv[:, 8 + qi:9 + qi],
                                                               scalar2=None, op0=ALU.mult), reads=[buf_r, dnv_r], writes=[on_r[tl]])
                    else:
                        kb.op("dve", lambda v: v.scalar_tensor_tensor(out=ONSA[:, tl, h * 64:(h + 1) * 64], in0=buf[:, qi, 0:64], scalar=dnv[:, 8 + qi:9 + qi],
                                                                      in1=ONSA[:, tl, h * 64:(h + 1) * 64], op0=ALU.mult, op1=ALU.add),
                              reads=[buf_r, dnv_r, on_r[tl]], writes=[on_r[tl]])

            def nsa_a(qb):
                qc = slice(qb * 512, (qb + 1) * 512)
                tts = list(range(qb * 4, qb * 4 + 4))
                ONSA, on_r, impb, imp_r = ONSA2[qb % 2], on_r2[qb % 2], impb2[qb % 2], imp_r2[qb % 2]
                for h in range(8):
                    g, i = h // 4, h % 4
                    pi = h % 2
                    kb.op("pe", lambda pe: pe.matmul(PS[pi][:, :], lhsT=kcmpTz[:, g, :], rhs=QT[:, i, qc], start=True, stop=True),
                          reads=[kc_r, QT_r[qb]], writes=[PS_r[pi]])
                    kb.op("act", lambda a: a.activation(out=exa[pi][:, :], in_=PS[pi][:, :], func=AF.Exp), reads=[PS_r[pi]], writes=[exa_r[pi]])
                    kb.op("dve", lambda v: v.tensor_tensor(out=PT[pi][:, 0:512], in0=exa[pi][:, :], in1=cmask[:, qc], op=ALU.mult),
                          reads=[exa_r[pi], k_r], writes=[PT_r[pi]])
                    ai = 2 + (h % 2)

                    def fn(pe):
                        inst = None
                        for q in range(4):
                            inst = pe.matmul(PS[ai][:, q * 97:(q + 1) * 97], lhsT=PT[pi][:, q * 128:(q + 1) * 128], rhs=vcmp[:, g, :], start=True, stop=True)
                        return inst
                    kb.op("pe", fn, reads=[PT_r[pi], vcm_r], writes=[PS_r[ai]])
                    kb.op("act", lambda a: a.activation(out=ocsa[pi][:, :, :], in_=PS[ai][:, 0:388].rearrange("p (q w) -> p q w", q=4), func=AF.Copy),
                          reads=[PS_r[ai]], writes=[ocsa_r[pi], PS_r[ai]])
                    bf_math(ocsa[pi], ocsa_r[pi], dna[pi], dna_r[pi], 4, h, 0, tts, True, ONSA, on_r)
                    for q in range(4):
                        if i == 0:
                            kb.op("dve", lambda v: v.tensor_scalar(out=impb[:, q, g, :], in0=ocsa[pi][:, q, 65:97], scalar1=dna[pi][:, q:q + 1], scalar2=None, op0=ALU.mult),
                                  reads=[ocsa_r[pi], dna_r[pi]], writes=[imp_r])
                        else:
                            kb.op("dve", lambda v: v.scalar_tensor_tensor(out=impb[:, q, g, :], in0=ocsa[pi][:, q, 65:97], scalar=dna[pi][:, q:q + 1], in1=impb[:, q, g, :],
                                                                          op0=ALU.mult, op1=ALU.add), reads=[ocsa_r[pi], dna_r[pi], imp_r], writes=[imp_r])
            def nsa_b(qb):
                qc = slice(qb * 512, (qb + 1) * 512)
                tts = list(range(qb * 4, qb * 4 + 4))
                ONSA, on_r, impb, imp_r = ONSA2[qb % 2], on_r2[qb % 2], impb2[qb % 2], imp_r2[qb % 2]
                for q in range(4):
                    tt = tts[q]
                    for g in range(2):
                        sc, cm3, sb16, sc_r, cm3_r, sb16_r = scl[g], cm3l[g], sb16l[g], scl_r[g], cm3l_r[g], sb16l_r[g]
                        pbi = g
                        kb.op("dve", lambda v: v.tensor_tensor(out=sc[:, :], in0=impb[:, q, g, :], in1=selm[:, 0, tt, :], op=ALU.mult),
                              reads=[imp_r, k_r], writes=[sc_r])
                        kb.op("dve", lambda v: v.tensor_tensor(out=sc[:, :], in0=sc[:, :], in1=selm[:, 1, tt, :], op=ALU.add),
                              reads=[sc_r, k_r], writes=[sc_r])
                        kb.op("dve", lambda v: v.tensor_tensor(out=cm3[:, :, :], in0=bass.AP(sc, 0, [[32, 128], [0, 32], [1, 32]]),
                                                              in1=bass.AP(sc, 0, [[32, 128], [1, 32], [0, 32]]), op=ALU.is_gt),
                              reads=[sc_r], writes=[cm3_r])
                        kb.op("dve", lambda v: v.reduce_sum(out=sc[:, :], in_=cm3[:, :, :], axis=AX.X), reads=[cm3_r, sc_r], writes=[sc_r])
                        kb.op("dve", lambda v: v.tensor_scalar(out=sb16[:, :], in0=sc[:, :], scalar1=15.5, scalar2=-30000.0, op0=ALU.is_gt, op1=ALU.mult),
                              reads=[sc_r], writes=[sb16_r])
                        kb.op("pe", lambda pe: pe.transpose(out=PB[pbi][0:32, 0:128], in_=sb16[:, :], identity=ident[:, :]),
                              reads=[sb16_r, c_r], writes=[PB_r[pbi]])
                        kb.op("act", lambda a: a.activation(out=selbT[0:32, g, tt * 128:(tt + 1) * 128], in_=PB[pbi][0:32, 0:128], func=AF.Copy),
                              reads=[PB_r[pbi]], writes=[sel_r[qb], PB_r[pbi]])
            def nsa_c(qb):
                qc = slice(qb * 512, (qb + 1) * 512)
                tts = list(range(qb * 4, qb * 4 + 4))
                ONSA, on_r, impb, imp_r = ONSA2[qb % 2], on_r2[qb % 2], impb2[qb % 2], imp_r2[qb % 2]
                for h in range(8):
                    g, i = h // 4, h % 4
                    nkt = 4 * qb + 4
                    for kt in range(nkt):
                        kc_ = slice(kt * 128, (kt + 1) * 128)
                        pi = kt % 2

                        def fn(pe):
                            pe.matmul(PS[pi][:, :], lhsT=KTz[:, 1, g, kc_], rhs=QT[:, i, qc], start=True, stop=False)
                            return pe.matmul(PS[pi][:, :], lhsT=onehot[:, kc_], rhs=selbT[:, g, qc], start=False, stop=True)
                        kb.op("pe", fn, reads=[KT_r[kt // 4], QT_r[qb], k_r, sel_r[qb]], writes=[PS_r[pi]])
                        pt = kt % 4
                        kb.op("act", lambda a: a.activation(out=PTs[pt][:, 0:512], in_=PS[pi][:, :], func=AF.Exp), reads=[PS_r[pi]], writes=[PTs_r[pt], PS_r[pi]])
                        if kt >= 4 * qb:
                            ql = kt - 4 * qb
                            kb.op("dve", lambda v: v.tensor_tensor(out=PTs[pt][:, ql * 128:(ql + 1) * 128], in0=PTs[pt][:, ql * 128:(ql + 1) * 128],
                                                                  in1=caus[:, 0, :], op=ALU.mult), reads=[PTs_r[pt], c_r], writes=[PTs_r[pt]])
                        for q in range(4):
                            qt = 4 * qb + q
                            if qt < kt:
                                continue
                            kb.op("pe", lambda pe: pe.matmul(PS[2 + q][:, 0:65], lhsT=PTs[pt][:, q * 128:(q + 1) * 128], rhs=VS[:, kt, g, :],
                                                             start=(kt == 0), stop=(kt == qt)), reads=[PTs_r[pt], VS_r[kt]], writes=[PS_r[2 + q]])
                    hb = h % 2
                    for q in range(4):
                        kb.op("act", lambda a: a.activation(out=ocss[hb][:, q, :], in_=PS[2 + q][:, 0:65], func=AF.Copy),
                              reads=[PS_r[2 + q]], writes=[ocss_r[hb], PS_r[2 + q]])
                    bf_math(ocss[hb], ocss_r[hb], dns[hb], dns_r[hb], 4, h, 1, tts, False, ONSA, on_r)
            def nsa_d(qb):
                qc = slice(qb * 512, (qb + 1) * 512)
                tts = list(range(qb * 4, qb * 4 + 4))
                ONSA, on_r, impb, imp_r = ONSA2[qb % 2], on_r2[qb % 2], impb2[qb % 2], imp_r2[qb % 2]
                for h in range(8):
                    g, i = h // 4, h % 4
                    for q in range(4):
                        qt = 4 * qb + q
                        qcs = slice(qt * 128, (qt + 1) * 128)
                        kts = [kt for kt in range(qt - 4, qt + 1) if kt >= 0]
                        pi = q % 2
                        pw = q % 4
                        main = [kt for kt in kts if kt >= qt - 3]

                        def fn(pe):
                            inst = None
                            for n_, kt in enumerate(main):
                                inst = pe.matmul(PS[pi][:, n_ * 128:(n_ + 1) * 128], lhsT=KTz[:, 2, g, kt * 128:(kt + 1) * 128], rhs=QT[:, i, qcs],
                                                 start=True, stop=True)
                            return inst
                        kb.op("pe", fn, reads=KT_r + [QT_r[qb]], writes=[PS_r[pi]])
                        nm = len(main)
                        kb.op("act", lambda a: a.activation(out=PTw[pw][:, 0:nm * 128], in_=PS[pi][:, 0:nm * 128], func=AF.Exp), reads=[PS_r[pi]], writes=[PTw_r[pw]])
                        kb.op("dve", lambda v: v.tensor_tensor(out=PTw[pw][:, (nm - 1) * 128:nm * 128], in0=PTw[pw][:, (nm - 1) * 128:nm * 128],
                                                              in1=caus[:, 0, :], op=ALU.mult), reads=[PTw_r[pw], c_r], writes=[PTw_r[pw]])
                        tail = (qt - 4 >= 0)
                        if tail:
                            kt = qt - 4
                            kb.op("pe", lambda pe: pe.matmul(PS[2 + pi][:, 0:128], lhsT=KTz[:, 2, g, kt * 128:(kt + 1) * 128], rhs=QT[:, i, qcs],
                                                             start=True, stop=True), reads=KT_r + [QT_r[qb]], writes=[PS_r[2 + pi]])
                            kb.op("act", lambda a: a.activation(out=exw[pi][:, 0:128], in_=PS[2 + pi][:, 0:128], func=AF.Exp), reads=[PS_r[2 + pi]], writes=[exw_r[pi]])
                            kb.op("dve", lambda v: v.tensor_tensor(out=PTw[pw][:, 512:640], in0=exw[pi][:, 0:128], in1=wmask[:, :], op=ALU.mult),
                                  reads=[exw_r[pi], k_r, PTw_r[pw]], writes=[PTw_r[pw]])

                        def fn(pe):
                            inst = None
                            seq_ = [(n_, kt) for n_, kt in enumerate(main)] + ([(4, qt - 4)] if tail else [])
                            for idx, (n_, kt) in enumerate(seq_):
                                inst = pe.matmul(PS[4][:, q * 65:(q + 1) * 65], lhsT=PTw[pw][:, n_ * 128:(n_ + 1) * 128], rhs=VW[:, kt, g, :],
                                                 start=(idx == 0), stop=(idx == len(seq_) - 1))
                            return inst
                        kb.op("pe", fn, reads=[PTw_r[pw]] + VW_r[max(0, qt - 4):qt + 1], writes=[PS_r[4]])
                    hb = h % 2
                    kb.op("act", lambda a: a.activation(out=ocsw[hb][:, :, :], in_=PS[4][:, 0:260].rearrange("p (q w) -> p q w", q=4), func=AF.Copy),
                          reads=[PS_r[4]], writes=[ocsw_r[hb], PS_r[4]])
                    bf_math(ocsw[hb], ocsw_r[hb], dnw[hb], dnw_r[hb], 4, h, 2, tts, False, ONSA, on_r)
            def nsa_e(qb):
                qc = slice(qb * 512, (qb + 1) * 512)
                tts = list(range(qb * 4, qb * 4 + 4))
                ONSA, on_r, impb, imp_r = ONSA2[qb % 2], on_r2[qb % 2], impb2[qb % 2], imp_r2[qb % 2]
                for q in range(4):
                    tt = tts[q]
                    kb.op("act", lambda a: a.activation(out=obf[:, :], in_=ONSA[:, q, :], func=AF.Copy), reads=[on_r[q]], writes=[obf_r])
                    for half in range(2):
                        out_to_OT(obf[:, half * 256:(half + 1) * 256], obf_r, 128, OT, OT_r, 4 + 2 * half, tt * 128)
            for qb in range(4):
                if qb == 0:
                    nsa_a(0)
                if qb + 1 < 4:
                    nsa_a(qb + 1)
                if ns_ > 3:
                    nsa_b(qb)
                if ns_ > 5:
                    nsa_d(qb)
                if ns_ > 4:
                    nsa_c(qb)
                if ns_ > 6:
                    nsa_e(qb)
            kb.barrier()

    for sq in range(NSEQ):
        for l in range(NLAY):
            if l == 0:
                with sb("xstage", [128, 2, D], F32) as xst:
                    xst_r = regs(2)
                    for tt in range(NT):
                        i = tt % 2
                        kb.dma("sp", xst[:, i, :], x_d.ap()[sq, tt * 128:(tt + 1) * 128, :], writes=[xst_r[i]])
                        make_xT(xst[:, i, :], xst_r[i], tt)
                    kb.barrier()
            if cfg.get("stop") == "xt":
                continue
            res_src = (lambda tt: x_d.ap()[sq, tt * 128:(tt + 1) * 128, :]) if l == 0 else \
                      (lambda tt: xres[1].ap()[tt * 128:(tt + 1) * 128, :])
            res_regs = None if l == 0 else xres_r[1]

            with sb("OT", [128, 8, S], BF16) as OT:
                OT_r = regs(NT)
                if inject_O:
                    for k in range(8):
                        kb.dma("pool", OT[:, k, :], dbg_OT.ap()[:, k, :], writes=OT_r)
                if "gla" in mixers:
                    gla_phase(sq, l, OT, OT_r)
                if "rwkv" in mixers:
                    rwkv_phase(sq, l, OT, OT_r)
                if "nsa" in mixers:
                    nsa_phase(sq, l, OT, OT_r)
                if "OT" in dump_d:
                    for k in range(8):
                        with sb("otd", [128, S], F32) as otd:
                            r_ = Reg()
                            kb.op("act", lambda a: a.activation(out=otd[:], in_=OT[:, k, :], func=AF.Copy),
                                  reads=OT_r, writes=[r_])
                            dump("OT", otd[:], r_, idx=k)
                            kb.barrier()
                if cfg.get("stop") == "outproj0":
                    continue
                with sb("Wo", [128, 8, D], BF16) as Wo, \
                        sb("ln1", [128, 2, D], F32) as ln1, \
                        sb("xs1", [128, 2, D], F32) as xs1, \
                        sb("st1", [128, 2, 32], F32) as st1:
                    Wo_r = Reg()
                    ln_r = Reg()
                    xs_r = regs(2)
                    st_r = regs(2)
                    load_w(Wo, w_out16.ap()[l], D, Wo_r)
                    kb.dma("sp", ln1[:, 0, :], bcast_rows(rowtab, l * 5120 + 1024, D), writes=[ln_r])
                    kb.dma("sp", ln1[:, 1, :], bcast_rows(rowtab, l * 5120 + 2048, D), writes=[ln_r])
                    for tt in range(NT):
                        i = tt % 2
                        kb.dma("sp", xs1[:, i, :], res_src(tt), reads=([res_regs[tt]] if res_regs else []), writes=[xs_r[i]])
                        for hf in range(2):
                            def fn(pe, hf=hf):
                                inst = None
                                for k in range(8):
                                    inst = pe.matmul(PS[hf][:, :], lhsT=OT[:, k, tt * 128:(tt + 1) * 128],
                                                     rhs=Wo[:, k, hf * 512:(hf + 1) * 512], start=(k == 0), stop=(k == 7))
                                return inst
                            kb.op("pe", fn, reads=[OT_r[tt], Wo_r], writes=[PS_r[hf]])
                            kb.op("dve", lambda v, hf=hf: v.scalar_tensor_tensor(
                                out=xs1[:, i, hf * 512:(hf + 1) * 512], in0=xs1[:, i, hf * 512:(hf + 1) * 512], scalar=ALPHA,
                                in1=PS[hf][:, :], op0=ALU.mult, op1=ALU.add), reads=[PS_r[hf], xs_r[i]], writes=[xs_r[i]])
                        layer_norm(xs1[:, i, :], xs_r[i], ln1[:, 0, :], ln1[:, 1, :], ln_r, st1[:, i, :], st_r[i])
                        kb.dma("sp", xres[0].ap()[tt * 128:(tt + 1) * 128, :], xs1[:, i, :], reads=[xs_r[i]], writes=[xres_r[0][tt]])
                        make_xT(xs1[:, i, :], xs_r[i], tt)
                        if "x1" in dump_d and sq == 0 and l == cfg.get("dump_layer", 0):
                            dump("x1", xs1[:, i, :], xs_r[i], idx=tt)
                    kb.barrier()
            if cfg.get("stop") in ("outproj", "outproj0"):
                continue
            with sb("aT", [128, NFC, 1024], BF16) as aT, \
                    sb("Wd", [128, NFC, D], BF16) as Wd, \
                    sb("Wgu", [128, 2, 2, 8, 512], BF16) as Wgu, \
                    sb("sg", [128, 2, 512], F32) as sg, \
                    sb("ln2", [128, 2, D], F32) as ln2, \
                    sb("xs2", [128, 2, D], F32) as xs2, \
                    sb("st2", [128, 2, 32], F32) as st2:
                Wd_r = Reg()
                ln_r = Reg()
                aT_r = regs(NFC)
                Wgu_r = regs(2)
                sg_r = regs(2)
                xs_r = regs(2)
                st_r = regs(2)
                kb.dma("sp", ln2[:, 0, :], bcast_rows(rowtab, l * 5120 + 3072, D), writes=[ln_r])
                kb.dma("sp", ln2[:, 1, :], bcast_rows(rowtab, l * 5120 + 4096, D), writes=[ln_r])
                for k in range(NFC):
                    for c0 in (0, 512):
                        kb.dma("sp", Wd[:, k, c0:c0 + 512], w_down16.ap()[l, k * 128:(k + 1) * 128, c0:c0 + 512], reads=[w16_r], writes=[Wd_r])
                last = (l == NLAY - 1)
                for mt in range(2):
                    tok0 = mt * 1024
                    for hc in range(NFC):
                        cg, ci = hc // 4, hc % 4
                        wi = cg % 2
                        if ci == 0:
                            ncol = min(512, FF - cg * 512)
                            for gu, wsrc in ((0, w_gate16), (1, w_up16)):
                                for k in range(8):
                                    kb.dma("sp", Wgu[:, wi, gu, k, 0:ncol],
                                           wsrc.ap()[l, k * 128:(k + 1) * 128, cg * 512:cg * 512 + ncol],
                                           reads=[w16_r], writes=[Wgu_r[wi]])
                        for blk in range(2):
                            t0 = tok0 + blk * 512
                            pg, pu = (0, 1) if blk == 0 else (2, 3)
                            for gu, pi in ((0, pg), (1, pu)):
                                def fn(pe, gu=gu, pi=pi):
                                    inst = None
                                    for k in range(8):
                                        inst = pe.matmul(PS[pi][:, :], lhsT=Wgu[:, wi, gu, k, ci * 128:(ci + 1) * 128],
                                                         rhs=XT[:, k, t0:t0 + 512], start=(k == 0), stop=(k == 7))
                                    return inst
                                kb.op("pe", fn, reads=[Wgu_r[wi]] + XT_r[t0 // 128:t0 // 128 + 4], writes=[PS_r[pi]])
                            kb.op("act", lambda a: a.activation(out=sg[:, blk, :], in_=PS[pg][:, :], func=AF.Silu),
                                  reads=[PS_r[pg]], writes=[sg_r[blk]])
                            kb.op("dve", lambda v: v.tensor_tensor(out=aT[:, hc, blk * 512:(blk + 1) * 512], in0=sg[:, blk, :],
                                                                  in1=PS[pu][:, :], op=ALU.mult),
                                  reads=[sg_r[blk], PS_r[pu]], writes=[aT_r[hc]])
                    for t8 in range(8):
                        tt = mt * 8 + t8
                        i = tt % 2
                        kb.dma("sp", xs2[:, i, :], xres[0].ap()[tt * 128:(tt + 1) * 128, :], reads=[xres_r[0][tt]], writes=[xs_r[i]])
                        for hf in range(2):
                            pi = 4 + hf

                            def fn(pe, hf=hf, pi=pi):
                                inst = None
                                for k in range(NFC):
                                    inst = pe.matmul(PS[pi][:, :], lhsT=aT[:, k, t8 * 128:(t8 + 1) * 128],
                                                     rhs=Wd[:, k, hf * 512:(hf + 1) * 512], start=(k == 0), stop=(k == NFC - 1))
                                return inst
                            kb.op("pe", fn, reads=aT_r + [Wd_r], writes=[PS_r[pi]])
                            kb.op("dve", lambda v, hf=hf, pi=pi: v.scalar_tensor_tensor(
                                out=xs2[:, i, hf * 512:(hf + 1) * 512], in0=xs2[:, i, hf * 512:(hf + 1) * 512], scalar=ALPHA,
                                in1=PS[pi][:, :], op0=ALU.mult, op1=ALU.add), reads=[PS_r[pi], xs_r[i]], writes=[xs_r[i]])
                        layer_norm(xs2[:, i, :], xs_r[i], ln2[:, 0, :], ln2[:, 1, :], ln_r, st2[:, i, :], st_r[i])
                        if last:
                            kb.dma("sp", out_d.ap()[sq, tt * 128:(tt + 1) * 128, :], xs2[:, i, :], reads=[xs_r[i]])
                        else:
                            kb.dma("sp", xres[1].ap()[tt * 128:(tt + 1) * 128, :], xs2[:, i, :], reads=[xs_r[i]], writes=[xres_r[1][tt]])
                            make_xT(xs2[:, i, :], xs_r[i], tt)
                        if "x2" in dump_d and sq == 0 and l == cfg.get("dump_layer", 0):
                            dump("x2", xs2[:, i, :], xs_r[i], idx=tt)
                    if not last:
                        pass
                kb.barrier()
    kb.finish()
    return kb


def host_consts():
    c = {}
    c["c_ident"] = np.eye(128, dtype=np.float32)
    s = np.arange(128)
    c["c_caus"] = (s[:, None] <= s[None, :]).astype(np.float32)
    half = 32
    inv = (10000.0 ** (-np.arange(half, dtype=np.float32) / half)).astype(np.float32)
    ang = (np.arange(S, dtype=np.float32)[:, None] * inv[None, :]).astype(np.float32)
    cos = np.cos(ang).astype(np.float32).T
    sin = np.sin(ang).astype(np.float32).T
    cosT = np.concatenate([cos, cos, cos, cos], 0)
    sinT = np.concatenate([-sin, sin, -sin, sin], 0)
    c["c_rope"] = np.stack([cosT, sinT]).astype(np.float32)
    cc = np.arange(128)
    t = np.arange(S)
    c["c_cmpmask"] = ((16 * cc[:, None] + 31 <= t[None, :]) & (cc[:, None] < 127)).astype(np.float32)
    j = np.arange(32)
    c["c_onehot"] = ((t[None, :] // 64) == j[:, None]).astype(np.float32)
    cur = t // 64
    forced = (j[None, :] == 0) | (j[None, :] == cur[:, None]) | (j[None, :] == cur[:, None] - 1)
    future = j[None, :] > cur[:, None]
    m1 = (~forced & ~future).astype(np.float32)
    m2 = np.where(forced, 1e9, np.where(future, -1e9, 0.0)).astype(np.float32)
    selm = np.stack([m1, m2])
    c["c_selm"] = np.ascontiguousarray(selm.reshape(2, NT, 128, 32).transpose(0, 2, 1, 3))
    c0 = np.arange(127) * 16
    s0 = np.arange(32) * 64
    lo = np.maximum(c0[:, None], s0[None, :])
    hi = np.minimum(c0[:, None] + 32, s0[None, :] + 64)
    ov = np.zeros((128, 32), np.float32)
    ov[:127] = np.maximum(hi - lo, 0) / 16
    c["c_ovl"] = ov
    blk = np.zeros((128, 128), np.float32)
    blk[:64, :64] = 1
    blk[64:, 64:] = 1
    c["c_blk"] = blk
    hs = np.zeros((128, 2), np.float32)
    hs[:64, 0] = 1
    hs[64:, 1] = 1
    c["c_hsel"] = hs
    i64 = np.arange(64)
    lo_strict = (i64[None, :] < i64[:, None]).astype(np.float32)
    up_strict = (i64[:, None] < i64[None, :]).astype(np.float32)
    up_incl = (i64[:, None] <= i64[None, :]).astype(np.float32)
    def bd(m):
        o = np.zeros((128, 128), np.float32)
        o[:64, :64] = m
        o[64:, 64:] = m
        return o
    c["c_rwmask2"] = np.ascontiguousarray(np.stack([bd(lo_strict), bd(up_strict), bd(up_incl)], axis=1))
    c["c_istk"] = np.concatenate([np.eye(64, dtype=np.float32), np.eye(64, dtype=np.float32)], axis=0)
    return c


def host_layout(inp):
    d = {}
    f = lambda a: np.ascontiguousarray(np.asarray(a, dtype=np.float32))
    for k in ("w_in", "w_in_vres", "gla_w_a2", "rwkv_w2", "rwkv_a2", "rwkv_v2", "rwkv_g2", "nsa_wk1", "nsa_wk2",
              "nsa_wv1", "nsa_wv2", "w_out", "ffn_w_gate", "ffn_w_up", "ffn_w_down"):
        d[k] = f(inp[k])
    base = 2096
    sw = lambda b: list(range(b + 32, b + 64)) + list(range(b, b + 32))
    pl = lambda b: list(range(b, b + 64))
    cols = []
    for i in range(4):
        cols += pl(base + i * 64) + pl(base + (4 + i) * 64)
    for i in range(4):
        cols += sw(base + i * 64) + sw(base + (4 + i) * 64)
    for c0 in (512, 768, 1024):
        cols += pl(base + c0) + pl(base + c0 + 64)
        cols += sw(base + c0) + sw(base + c0 + 64)
    cols += list(range(base + 640, base + 768)) + list(range(base + 896, base + 1024)) + list(range(base + 1152, base + 1280))
    cols += list(range(base + 1280, base + 1304))
    assert len(cols) == 2200
    d["w_nsa"] = f(np.asarray(inp["w_in"])[:, :, cols])
    d["nsa_posT"] = f(np.stack([np.asarray(inp["nsa_pos_k"]).transpose(0, 2, 1),
                                np.asarray(inp["nsa_pos_v"]).transpose(0, 2, 1)], axis=1))
    pt = np.zeros((L, 128, 32), np.float32)
    mu = np.asarray(inp["rwkv_mu"])
    rwt = [(0, 128), (128, 128), (256, 128), (384, 128), (512, 128), (640, 128), (768, 64), (832, 64), (896, 128), (1024, 32)]
    for l in range(L):
        pt[l, :, 0:2] = np.asarray(inp["gla_b_a"])[l].reshape(2, 128).T
        for i, (c0, n) in enumerate(rwt):
            pt[l, :n, 2 + i] = mu[l, c0:c0 + n]
        if l >= 1:
            pt[l, :32, 12] = np.asarray(inp["rwkv_mu_vres"])[l - 1]
            pt[l, :, 17:19] = np.asarray(inp["rwkv_v0"])[l - 1].reshape(2, 128).T
        pt[l, :, 13:15] = np.asarray(inp["rwkv_w0"])[l].reshape(2, 128).T
        pt[l, :, 15:17] = np.asarray(inp["rwkv_a0"])[l].reshape(2, 128).T
        pt[l, :, 19:21] = np.asarray(inp["rwkv_k_k"])[l].reshape(2, 128).T
        pt[l, :, 21:23] = np.asarray(inp["rwkv_k_a"])[l].reshape(2, 128).T
        pt[l, :, 23:25] = np.asarray(inp["rwkv_r_k"])[l].reshape(2, 128).T
    d["ptab"] = pt
    rt = np.zeros((L, 5120), np.float32)
    for l in range(L):
        rt[l, 0:256] = np.asarray(inp["gla_ln_w"])[l]
        rt[l, 256:512] = np.asarray(inp["gla_ln_b"])[l]
        rt[l, 512:768] = np.asarray(inp["rwkv_ln_w"])[l]
        rt[l, 768:1024] = np.asarray(inp["rwkv_ln_b"])[l]
        rt[l, 1024:2048] = np.asarray(inp["ln1_w"])[l]
        rt[l, 2048:3072] = np.asarray(inp["ln1_b"])[l]
        rt[l, 3072:4096] = np.asarray(inp["ln2_w"])[l]
        rt[l, 4096:5120] = np.asarray(inp["ln2_b"])[l]
    d["rowtab"] = rt
    return d


_CACHE = {}


def kernel(**inputs):
    cfg = {}
    if "full" not in _CACHE:
        _CACHE["full"] = build(cfg)
    kb = _CACHE["full"]
    shared = host_layout(inputs)
    shared.update(host_consts())
    x = np.ascontiguousarray(np.asarray(inputs["x"], dtype=np.float32))
    in_maps = []
    for c in range(8):
        m = dict(shared)
        m["x"] = x[2 * c:2 * c + 2]
        in_maps.append(m)
    res = run_bass_kernel_spmd(kb.nc, in_maps, core_ids=list(range(8)))
    return np.concatenate([r["out"] for r in res.results], axis=0).astype(np.float32)
```
